# Optimizing a Trainium2 kernel written in Bass

```python
import jax, jax.numpy as jnp
from jax import lax
import numpy as np

D_MODEL = 1024
BATCH = 8
SEQ = 2048
DEPTH = 2
DEC_BATCH = 16
DEC_SEQ = 16
PAST_LEN = 1024

CHUNK = 64
MIX_WIDTH = D_MODEL
HEAD_DIM = 64
FOX_WIDTH = MIX_WIDTH // 2
FOX_HEADS = FOX_WIDTH // HEAD_DIM
SGU_WIDTH = MIX_WIDTH // 4
SGU_GROUPS = 4
SGU_GROUP_DIM = SGU_WIDTH // SGU_GROUPS
GMLP_CHUNK = 128
MEM_WIDTH = MIX_WIDTH // 4
MEM_HEADS = 4
MEM_HEAD_DIM = MEM_WIDTH // MEM_HEADS
N_MEM = 256
FFN_DIM = 2816
CONV_WIDTH = 3
Q_BLOCK = 128
RMS_EPS = 1e-6
NEG_INF = -1e30
IN_COLS = 3 * FOX_WIDTH + FOX_HEADS + 2 * SGU_WIDTH + MEM_WIDTH
SPLITS = (FOX_WIDTH, 2 * FOX_WIDTH, 3 * FOX_WIDTH, 3 * FOX_WIDTH + FOX_HEADS,
          3 * FOX_WIDTH + FOX_HEADS + 2 * SGU_WIDTH)

kernel_name = 'hybrid_fox_gmlp_memory_stream'


def rmsnorm(x, g):
    xf = x.astype(jnp.float32)
    y = xf * lax.rsqrt(jnp.mean(xf * xf, axis=-1, keepdims=True) + RMS_EPS)
    return (y * g.astype(jnp.float32)).astype(x.dtype)


def fox_attend(q, k, v, c_q, c_k, pos_q, pos_k):
    s = jnp.einsum('bqhd,bkhd->bhqk', q, k, preferred_element_type=jnp.float32) * (HEAD_DIM ** -0.5)
    s = s + jnp.swapaxes(c_q, 1, 2)[:, :, :, None] - jnp.swapaxes(c_k, 1, 2)[:, :, None, :]
    mask = pos_k[None, :] <= pos_q[:, None]
    s = jnp.where(mask[None, None], s, NEG_INF)
    p = jax.nn.softmax(s, axis=-1)
    return jnp.einsum('bhqk,bkhd->bqhd', p.astype(v.dtype), v)


def fox_prompt(q, k, v, c):
    B, S, H, Dh = q.shape
    nb = S // Q_BLOCK
    pos = jnp.arange(S)
    qb = q.reshape(B, nb, Q_BLOCK, H, Dh).transpose(1, 0, 2, 3, 4)
    cb = c.reshape(B, nb, Q_BLOCK, H).transpose(1, 0, 2, 3)
    pb = pos.reshape(nb, Q_BLOCK)
    out = lax.map(lambda blk: fox_attend(blk[0], k, v, blk[1], c, blk[2], pos), (qb, cb, pb))
    return out.transpose(1, 0, 2, 3, 4).reshape(B, S, H * Dh)


def spatial_gating(gm, w_s, b_s, g_sgu, L):
    z = jax.nn.gelu(gm)
    u, vv = jnp.split(z, 2, axis=-1)
    vv = rmsnorm(vv, g_sgu)
    B, T, _ = vv.shape
    idx = jnp.arange(GMLP_CHUNK)
    mask = (idx[None, :] // CHUNK) <= (idx[:, None] // CHUNK)
    w = jnp.where(mask[None], w_s, 0.0)[:, :L, :L]
    vc = vv.reshape(B, T // L, L, SGU_GROUPS, SGU_GROUP_DIM)
    mixed = jnp.einsum('gij,bcjgd->bcigd', w, vc) + b_s[:, :L].T[None, None, :, :, None]
    return u * mixed.reshape(B, T, SGU_WIDTH), vv


def memory_kv(mem, g_mem, w_mem_kv):
    B, N, _ = mem.shape
    kv = rmsnorm(mem, g_mem) @ w_mem_kv
    mk, mv = jnp.split(kv, 2, axis=-1)
    return (mk.reshape(B, N, MEM_HEADS, MEM_HEAD_DIM), mv.reshape(B, N, MEM_HEADS, MEM_HEAD_DIM))


def memory_attend(q, k, v):
    s = jnp.einsum('bqhd,bkhd->bhqk', q, k, preferred_element_type=jnp.float32) * (MEM_HEAD_DIM ** -0.5)
    p = jax.nn.softmax(s, axis=-1)
    return jnp.einsum('bhqk,bkhd->bqhd', p.astype(v.dtype), v)


def causal_dwconv(a_ext, w, b):
    T = a_ext.shape[1] - (CONV_WIDTH - 1)
    out = b + w[0] * a_ext[:, 0:T]
    for i in range(1, CONV_WIDTH):
        out = out + w[i] * a_ext[:, i:i + T]
    return out


def trunk_layer(x, mem_k, mem_v, hist, p):
    (g_pre_mix, w_in, b_forget, w_s, b_s, g_sgu, g_group_out, w_out, g_post_mix,
     g_pre_ffn, w_up, w_dw, b_dw, w_down, g_post_ffn) = p
    B, T, _ = x.shape
    h = rmsnorm(x, g_pre_mix)
    q, k, v, fg, gm, qm = jnp.split(h @ w_in, SPLITS, axis=-1)
    q = q.reshape(B, T, FOX_HEADS, HEAD_DIM)
    k = k.reshape(B, T, FOX_HEADS, HEAD_DIM)
    v = v.reshape(B, T, FOX_HEADS, HEAD_DIM)
    logf = jax.nn.log_sigmoid((fg + b_forget).astype(jnp.float32))
    if hist is None:
        c = jnp.cumsum(logf, axis=1)
        fox = fox_prompt(q, k, v, c)
        L = GMLP_CHUNK
        a_hist = jnp.zeros((B, CONV_WIDTH - 1, FFN_DIM), x.dtype)
    else:
        hk, hv, hlogf, a_hist = hist
        P = hk.shape[1]
        c = jnp.cumsum(jnp.concatenate([hlogf.astype(jnp.float32), logf], axis=1), axis=1)
        fox = fox_attend(q, jnp.concatenate([hk, k], axis=1), jnp.concatenate([hv, v], axis=1),
                         c[:, P:], c, P + jnp.arange(T), jnp.arange(P + T)).reshape(B, T, FOX_WIDTH)
        L = T
    sgu, v_rows = spatial_gating(gm, w_s, b_s, g_sgu, L)
    mem = memory_attend(qm.reshape(B, T, MEM_HEADS, MEM_HEAD_DIM), mem_k, mem_v).reshape(B, T, MEM_WIDTH)
    mixed = jnp.concatenate([
        rmsnorm(fox, g_group_out[:FOX_WIDTH]),
        rmsnorm(sgu, g_group_out[FOX_WIDTH:FOX_WIDTH + SGU_WIDTH]),
        rmsnorm(mem, g_group_out[FOX_WIDTH + SGU_WIDTH:])], axis=-1)
    x = x + rmsnorm(mixed @ w_out, g_post_mix)
    h2 = rmsnorm(x, g_pre_ffn)
    a, lin = jnp.split(h2 @ w_up, 2, axis=-1)
    a_ext = jnp.concatenate([a_hist.astype(a.dtype), a], axis=1)
    ffn = (jax.nn.silu(causal_dwconv(a_ext, w_dw, b_dw)) * lin) @ w_down
    x = x + rmsnorm(ffn, g_post_ffn)
    return x, (k, v, logf.astype(x.dtype), v_rows, a_ext[:, -(CONV_WIDTH - 1):])


def setup_inputs(seed: int = 0) -> dict:
    key = jax.random.key(seed)
    ks = jax.random.split(key, 32)
    nrm = lambda i, shape, s=1.0: s * jax.random.normal(ks[i], shape, jnp.float32)
    gain = lambda i, shape: 1.0 + 0.1 * jax.random.normal(ks[i], shape, jnp.float32)
    return {
        'x_prompt': nrm(0, (BATCH, SEQ, D_MODEL)),
        'x_sample': nrm(1, (DEC_BATCH, DEC_SEQ, D_MODEL)),
        'mem_prompt': nrm(2, (BATCH, N_MEM, D_MODEL)),
        'cache_fox_k': nrm(3, (DEPTH, DEC_BATCH, PAST_LEN, FOX_HEADS, HEAD_DIM)),
        'cache_fox_v': nrm(4, (DEPTH, DEC_BATCH, PAST_LEN, FOX_HEADS, HEAD_DIM)),
        'cache_fox_logf': jax.nn.log_sigmoid(3.0 + nrm(5, (DEPTH, DEC_BATCH, PAST_LEN, FOX_HEADS))),
        'cache_mem_k': nrm(6, (DEPTH, DEC_BATCH, N_MEM, MEM_HEADS, MEM_HEAD_DIM)),
        'cache_mem_v': nrm(7, (DEPTH, DEC_BATCH, N_MEM, MEM_HEADS, MEM_HEAD_DIM)),
        'cache_ffn_conv': nrm(8, (DEPTH, DEC_BATCH, CONV_WIDTH - 1, FFN_DIM)),
        'g_pre_mix': gain(9, (DEPTH, D_MODEL)),
        'w_in': nrm(10, (DEPTH, D_MODEL, IN_COLS), D_MODEL ** -0.5),
        'b_forget': 3.0 + nrm(11, (DEPTH, FOX_HEADS), 0.5),
        'w_spatial': nrm(12, (DEPTH, SGU_GROUPS, GMLP_CHUNK, GMLP_CHUNK), 0.5 * GMLP_CHUNK ** -0.5),
        'b_spatial': gain(13, (DEPTH, SGU_GROUPS, GMLP_CHUNK)),
        'g_sgu': gain(14, (DEPTH, SGU_WIDTH)),
        'g_mem': gain(15, (DEPTH, D_MODEL)),
        'w_mem_kv': nrm(16, (DEPTH, D_MODEL, 2 * MEM_WIDTH), D_MODEL ** -0.5),
        'g_group_out': gain(17, (DEPTH, MIX_WIDTH)),
        'w_out': nrm(18, (DEPTH, MIX_WIDTH, D_MODEL), MIX_WIDTH ** -0.5),
        'g_post_mix': gain(19, (DEPTH, D_MODEL)),
        'g_pre_ffn': gain(20, (DEPTH, D_MODEL)),
        'w_up': nrm(21, (DEPTH, D_MODEL, 2 * FFN_DIM), D_MODEL ** -0.5),
        'w_dwconv': nrm(22, (DEPTH, CONV_WIDTH, FFN_DIM), CONV_WIDTH ** -0.5),
        'b_dwconv': nrm(23, (DEPTH, FFN_DIM), 0.02),
        'w_down': nrm(24, (DEPTH, FFN_DIM, D_MODEL), FFN_DIM ** -0.5),
        'g_post_ffn': gain(25, (DEPTH, D_MODEL)),
    }


def reference(x_prompt, x_sample, mem_prompt, cache_fox_k, cache_fox_v, cache_fox_logf,
              cache_mem_k, cache_mem_v, cache_ffn_conv, g_pre_mix, w_in, b_forget, w_spatial,
              b_spatial, g_sgu, g_mem, w_mem_kv, g_group_out, w_out, g_post_mix, g_pre_ffn,
              w_up, w_dwconv, b_dwconv, w_down, g_post_ffn):
    yp, ys = x_prompt, x_sample
    pk_l, pv_l, plf_l, pmk_l, pmv_l, pconv_l = [], [], [], [], [], []
    sk_l, sv_l, slf_l, sgv_l, sconv_l = [], [], [], [], []
    for l in range(DEPTH):
        p = (g_pre_mix[l], w_in[l], b_forget[l], w_spatial[l], b_spatial[l], g_sgu[l],
             g_group_out[l], w_out[l], g_post_mix[l], g_pre_ffn[l], w_up[l], w_dwconv[l],
             b_dwconv[l], w_down[l], g_post_ffn[l])
        mk, mv = memory_kv(mem_prompt, g_mem[l], w_mem_kv[l])
        yp, (pk, pv, plf, _, pconv) = trunk_layer(yp, mk, mv, None, p)
        ys, (sk, sv, slf, sgv, sconv) = trunk_layer(
            ys, cache_mem_k[l], cache_mem_v[l],
            (cache_fox_k[l], cache_fox_v[l], cache_fox_logf[l], cache_ffn_conv[l]), p)
        pk_l.append(pk); pv_l.append(pv); plf_l.append(plf)
        pmk_l.append(mk); pmv_l.append(mv); pconv_l.append(pconv)
        sk_l.append(sk); sv_l.append(sv); slf_l.append(slf); sgv_l.append(sgv); sconv_l.append(sconv)
    return (yp, ys,
            jnp.stack(pk_l), jnp.stack(pv_l), jnp.stack(plf_l),
            jnp.stack(pmk_l), jnp.stack(pmv_l), jnp.stack(pconv_l),
            jnp.stack(sk_l), jnp.stack(sv_l), jnp.stack(slf_l),
            jnp.stack(sgv_l), jnp.stack(sconv_l))
```

```python
import numpy as np
import ml_dtypes
from contextlib import ExitStack
import concourse.bass as bass
import concourse.mybir as mybir
from concourse.bass_utils import run_bass_kernel_spmd

F32 = mybir.dt.float32
BF16 = mybir.dt.bfloat16
ALU = mybir.AluOpType
AF = mybir.ActivationFunctionType

ENGS = ("pe", "act", "dve", "pool", "sp")
STOP = None
DBG = set()


class _Stop(Exception):
    pass


def _chk(name):
    if STOP == name:
        raise _Stop()
NCORES = 8
L = 2
D = 1024
NT = 2080
FF = 2816
NJ = 22
EPS = 1e-6
NEG = -30000.0


class Op:
    __slots__ = ("eng", "fn", "idx", "deps", "dma_key", "marked", "seq", "cum")

    def __init__(self, eng, fn, idx, dma_key):
        self.eng = eng
        self.fn = fn
        self.idx = idx
        self.deps = ()
        self.dma_key = dma_key
        self.marked = False
        self.seq = 0
        self.cum = 0


class Sched:
    def __init__(self, nc):
        self.nc = nc
        self.streams = {e: [] for e in ENGS}
        self.last_writer = {}
        self.readers = {}
        self.dma_counts = {}
        self.dma_last = {}

    def op(self, eng, fn, reads=(), writes=(), dma_key=None):
        o = Op(eng, fn, len(self.streams[eng]), dma_key)
        deps = set()
        lw = self.last_writer
        rd = self.readers
        for t in reads:
            w = lw.get(t)
            if w is not None:
                deps.add(w)
            if type(t) is tuple and t[0] == "ps":
                for r in rd.get(t, ()):
                    if r.eng != eng:
                        deps.add(r)
        for t in writes:
            w = lw.get(t)
            if w is not None:
                deps.add(w)
            r = rd.get(t)
            if r:
                deps.update(r)
        for t in reads:
            rd.setdefault(t, []).append(o)
        for t in writes:
            lw[t] = o
            rd[t] = []
        deps.discard(o)
        if eng == "pe":
            deps = {d for d in deps if not (d.eng == "pe" and d.dma_key is None)}
        o.deps = deps
        if dma_key is not None:
            c = self.dma_counts.get(dma_key, 0) + 1
            self.dma_counts[dma_key] = c
            o.cum = 16 * c
            self.dma_last[dma_key] = o
        self.streams[eng].append(o)
        return o

    def barrier(self):
        lasts = [s[-1] for s in self.streams.values() if s]
        lasts += list(self.dma_last.values())
        for e in ENGS:
            o = Op(e, lambda eng: eng.nop(), len(self.streams[e]), None)
            o.deps = {d for d in lasts if not (d.eng == e and d.dma_key is None)}
            self.streams[e].append(o)

    def emit(self, stack):
        nc = self.nc
        for e in ENGS:
            for o in self.streams[e]:
                for d in o.deps:
                    if d.dma_key is None:
                        d.marked = True
        sems = {}
        for e in ENGS:
            n = 0
            for o in self.streams[e]:
                if o.dma_key is None and o.marked:
                    n += 1
                    o.seq = n
            if n:
                sems[e] = stack.enter_context(nc.semaphore("s_" + e))
        dsems = {}
        for k in self.dma_counts:
            dsems[k] = stack.enter_context(nc.semaphore("d%d" % len(dsems)))
        self.n_sems = len(sems) + len(dsems)
        final = {k: o.cum for k, o in self.dma_last.items()}

        def run(eng_name, eng):
            waited = {}
            for o in self.streams[eng_name]:
                need = {}
                for d in o.deps:
                    if d.dma_key is not None:
                        key = ("d", d.dma_key)
                        val = d.cum
                    else:
                        key = ("c", d.eng)
                        val = d.seq
                    if val > need.get(key, 0):
                        need[key] = val
                for key, val in need.items():
                    if waited.get(key, 0) >= val:
                        continue
                    waited[key] = val
                    s = dsems[key[1]] if key[0] == "d" else sems[key[1]]
                    eng.wait_ge(s, val)
                ins = o.fn(eng)
                if o.dma_key is not None:
                    ins.then_inc(dsems[o.dma_key], 16)
                elif o.marked:
                    ins.then_inc(sems[o.eng], 1)
            if eng_name == "sp":
                for k, v in final.items():
                    if waited.get(("d", k), 0) < v:
                        eng.wait_ge(dsems[k], v)

        with nc.Block() as block:
            @block.tensor
            def _(t):
                run("pe", t)

            @block.scalar
            def _(t):
                run("act", t)

            @block.vector
            def _(t):
                run("dve", t)

            @block.gpsimd
            def _(t):
                run("pool", t)

            @block.sync
            def _(t):
                run("sp", t)


def tile_rows(i):
    return 128 if i < 16 else 32


def tcols(i):
    return (128 * i, 128) if i < 16 else (2048, 32)


BLKS = [(0, 512), (512, 512), (1024, 512), (1536, 512), (2048, 32)]

PO = {}
_o = 0
for _n, _w in (("gpm", 8), ("ggo", 8), ("gmem", 8), ("gpf", 8), ("bs", 4), ("bss", 4),
               ("wdw", 66), ("bdw", 22), ("bf", 1), ("cconv", 88)):
    PO[_n] = _o
    _o += _w
PPL = _o
NPRM = PPL * L


def build():
    nc = bass.Bass("TRN2", target_bir_lowering=False)

    def din(name, shape, dt=F32):
        return nc.dram_tensor(name, list(shape), dt, kind="ExternalInput").ap()

    def dout(name, shape):
        return nc.dram_tensor(name, list(shape), F32, kind="ExternalOutput").ap()

    x_p = din("x_p", [2048, D])
    x_s = din("x_s", [32, D])
    mem = din("mem", [256, D])
    ckT = din("ckT", [L, 2, 8, 64, 1024])
    cv = din("cv", [L, 2, 1024, 512])
    clfT = din("clfT", [L, 2, 8, 1024])
    cmkT = din("cmkT", [L, 2, 4, 64, 256])
    cmv = din("cmv", [L, 2, 256, 256])
    w_in = din("w_in", [L, D, 2312])
    w_mkv = din("w_mkv", [L, D, 512])
    w_out = din("w_out", [L, D, D])
    w_up = din("w_up", [L, NJ, 128, 8, 256])
    w_dn = din("w_dn", [L, FF, D])
    g_post_mix = din("g_post_mix", [L, D])
    g_post_ffn = din("g_post_ffn", [L, D])
    g_sgu = din("g_sgu", [L, 256])
    wsT = din("wsT", [L, 4, 128, 128])
    prm_h = din("prm", [128, NPRM])
    cst_h = din("cst", [128, 288], BF16)

    y_p = dout("y_p", [2048, D])
    y_s = dout("y_s", [32, D])
    ofk = dout("ofk", [L, 2048, 512])
    ofv = dout("ofv", [L, 2048, 512])
    olf = dout("olf", [L, 8, 2048])
    omk = dout("omk", [L, 256, 256])
    omv = dout("omv", [L, 256, 256])
    oconv = dout("oconv", [L, 128, NJ, 2])
    osk = dout("osk", [L, 32, 512])
    osv = dout("osv", [L, 32, 512])
    oslf = dout("oslf", [L, 8, 32])
    ogv = dout("ogv", [L, 32, 256])
    osconv = dout("osconv", [L, 128, NJ, 2, 2])

    cscr_p = nc.dram_tensor("cscr_p", [8, 2, 2048], BF16).ap()
    cscr_s = nc.dram_tensor("cscr_s", [8, 2, 2, 1040], BF16).ap()

    with ExitStack() as st:
        S = Sched(nc)
        X = st.enter_context(nc.sbuf_tensor("X", [128, 17, D], F32))
        PRM = st.enter_context(nc.sbuf_tensor("PRM", [128, NPRM], F32))
        CST = st.enter_context(nc.sbuf_tensor("CST", [128, 288], BF16))
        WSTSt = st.enter_context(nc.sbuf_tensor("WSTS", [32, L * 4 * 32], BF16))
        PTSt = st.enter_context(nc.sbuf_tensor("PTS", [128, 3 * 32], BF16))
        SM = st.enter_context(nc.sbuf_tensor("SM", [128, 256], F32))
        ONEROW = st.enter_context(nc.sbuf_tensor("ONEROW", [128, 64], F32))
        remaining = nc.sbuf_bytes_remaining
        remaining = remaining() if callable(remaining) else remaining
        ARB = (remaining - 512) // 64 * 64
        AR = st.enter_context(nc.sbuf_tensor("AR", [128, ARB // 2], BF16))
        AR32 = AR.bitcast(F32)
        PS = [st.enter_context(nc.psum_tensor("ps%d" % k, [128, 512], F32)) for k in range(8)]
        PSB = [p.bitcast(BF16) for p in PS]

        IDENT = CST[:, 0:128]
        MASKNEG = CST[:, 128:256]
        MASKS = CST[0:32, 256:288]


        def WSTS(l, g):
            o = (l * 4 + g) * 32
            return WSTSt[0:32, o:o + 32]

        def prm(l, name, w, p0=0, p1=128, off=0):
            o = l * PPL + PO[name] + off
            return PRM[p0:p1, o:o + w]

        SM_SS, SM_SD, SM_RSTD = 0, 1, 2
        SM_R = 16
        SM_SSF = 70
        SM_T = 90

        def bf(off, n, p0=0, p1=128):
            return AR[p0:p1, off // 2: off // 2 + n]

        def f32(off, n, p0=0, p1=128):
            return AR32[p0:p1, off // 4: off // 4 + n]

        cur = [0]

        def carve(nbytes):
            o = cur[0]
            cur[0] = o + (nbytes + 63) // 64 * 64
            return o

        HT_O = carve(8 * NT * 2)
        MX_O = carve(8 * NT * 2)
        VA_O = carve(17 * 8 * 65 * 2)
        QK_O = carve(4 * NT * 2)
        WS_O = [carve(8320) for _ in range(3)]
        SCR_O = carve(8192)
        MKT_O = carve(2048)
        MVA_O = carve(1040)
        MKTt = AR[:, MKT_O // 2:MKT_O // 2 + 1024]
        MVAt = AR[:, MVA_O // 2:MVA_O // 2 + 520]
        MIX_END = cur[0]
        assert MIX_END <= ARB, (MIX_END, ARB)

        def HT(c, c0, n):
            return bf(HT_O + (c * NT + c0) * 2, n)

        def HT3(c0, n):
            return AR[:, HT_O // 2: HT_O // 2 + 8 * NT].rearrange("p (c t) -> p c t", c=8)[:, :, c0:c0 + n]

        def MX(c, c0, n, p0=0, p1=128):
            return bf(MX_O + (c * NT + c0) * 2, n, p0, p1)

        def VA(j, h, nk=128):
            return bf(VA_O + ((j * 8 + h) * 65) * 2, 65, 0, nk)

        def QK(slot, c0, n, p0, p1):
            return bf(QK_O + (slot * NT + c0) * 2, n, p0, p1)

        CT = lambda c0, n: f32(QK_O + c0 * 4, n, 96, 104)
        HI = lambda c0, n: bf(QK_O + 8320 + c0 * 2, n, 96, 104)
        LO = lambda c0, n: bf(QK_O + 8320 + 4160 + c0 * 2, n, 96, 104)

        def WS(s, c, c0, n):
            return bf(WS_O[s] + (c * 520 + c0) * 2, n)

        def WS3(s, ncol):
            return AR[:, WS_O[s] // 2: WS_O[s] // 2 + 8 * 520].rearrange("p (c n) -> p c n", c=8)[:, :, 0:ncol]

        def SCRb(g, n, p0=0, p1=128, off=0):
            return bf(SCR_O + g * 1024 + off * 2, n, p0, p1)

        def SCRf(g, n, p0=0, p1=128, off=0):
            return f32(SCR_O + g * 1024 + off * 4, n, p0, p1)

        def scr(*gs):
            return [("scr", g) for g in gs]

        def dma(eng, out, in_, reads=(), writes=(), key=None, slow=False):
            if slow:
                fn = lambda e: e.dma_start(out=out, in_=in_, allow_slow_non_contiguous=True)
            else:
                fn = lambda e: e.dma_start(out=out, in_=in_)
            return S.op(eng, fn, reads=reads, writes=writes, dma_key=key)

        psr = {"g": [0, 1], "s": [2, 3, 4], "o": [5, 6], "t": [7]}
        psi = {k: 0 for k in psr}

        def psnext(pool):
            lst = psr[pool]
            k = lst[psi[pool] % len(lst)]
            psi[pool] += 1
            return k

        def mm_group(out_fn, items, reads, writes):
            def fn(e):
                ins = None
                for (o, a, b, s0, s1) in items:
                    ins = e.matmul(o, a, b, start=s0, stop=s1, skip_group_check=True)
                return ins
            return S.op("pe", fn, reads=reads, writes=writes)

        dma("sp", PRM[:, :], prm_h, writes=["prm"], key="prm")
        dma("sp", CST[:, :], cst_h, writes=["cst"], key="cst")
        S.op("dve", lambda e: e.memset(WSTSt[:, :], 0.0), writes=["wsts"])
        for l in range(L):
            for b in range(2):
                dma("pool",
                    WSTSt[16 * b:16 * b + 16, l * 128:(l + 1) * 128].rearrange("p (g i) -> p g i", g=4)[:, :, 16 * b:16 * b + 16],
                    wsT[l, :, 0:16, 0:16].rearrange("g j i -> j g i"),
                    reads=["wsts"], writes=[("wsts", l, b)], key=("wsts", l), slow=True)
        S.op("dve", lambda e: e.memset(PTSt[:, :], 0.0), writes=["pts0", "pts1", "pts2"])
        S.op("dve", lambda e: e.memset(SM[:, 200:201], 1.0), writes=["one"])
        S.op("dve", lambda e: e.memset(ONEROW[:, :], 1.0), writes=["onerow"])
        ONE8 = SM[96:104, 200:201]

        for i in range(16):
            dma("sp", X[:, i, :], x_p[128 * i:128 * i + 128, :], writes=[("x", i)], key=("x", i))
        dma("sp", X[0:32, 16, :], x_s, writes=[("x", 16)], key=("x", 16))

        def prenorm_tile(src, rows, xtok, gcol, dst3, dst_tok, slot, extra_writes=(), hb=None, tpool="t", hb_toks=None, defer=False):
            if hb is None:
                hb = SCRb(2 * slot, 1024, 0, rows)
            hbt = list(hb_toks) if hb_toks is not None else scr(2 * slot, 2 * slot + 1)
            c0 = 3 * slot
            ss = SM[0:rows, c0:c0 + 1]
            sd = SM[0:rows, c0 + 1:c0 + 2]
            rs = SM[0:rows, c0 + 2:c0 + 3]
            S.op("act", lambda e: e.activation(hb, src, AF.Square, accum_out=ss),
                 reads=[xtok], writes=hbt + [("sm", c0)])
            S.op("act", lambda e: e.activation(sd, ss, AF.Ln, bias=EPS, scale=1.0 / D),
                 reads=[("sm", c0)], writes=[("sm", c0 + 1)])
            S.op("act", lambda e: e.activation(rs, sd, AF.Exp, scale=-0.5), reads=[("sm", c0 + 1)], writes=[("sm", c0 + 2)])
            S.op("act", lambda e: e.activation(hb, src, AF.Copy, scale=rs),
                 reads=[xtok, ("sm", c0 + 2)], writes=hbt)
            def part_b():
                k = psnext(tpool)

                def tr(e):
                    ins = None
                    for c in range(8):
                        ins = e.transpose(PSB[k][:, c * 128:c * 128 + rows], hb[:, c * 128:(c + 1) * 128],
                                          IDENT[0:rows, 0:rows])
                    return ins
                S.op("pe", tr, reads=hbt + ["cst"], writes=[("ps", k)])
                src3 = PSB[k][:, 0:1024].rearrange("p (c t) -> p c t", c=8)[:, :, 0:rows]
                S.op("dve", lambda e: e.tensor_tensor(dst3, src3, gcol.unsqueeze(2).to_broadcast([128, 8, rows]), ALU.mult),
                     reads=[("ps", k), "prm"], writes=[dst_tok] + list(extra_writes))
            if defer:
                return part_b
            part_b()

        PVQ = []

        def flush_pv():
            while PVQ:
                PVQ.pop(0)()

        def attend(qa_fn, K, ktiles, osubs, obank, tag):
            pendq = []
            first_done = [False]

            def pv(kt_i, kt, pt_ap, pt_tok):
                items = []
                if osubs == "fm":
                    lo_, hi_ = kt["lo"], kt["hi"]
                    items.append((PS[obank][0:65, lo_:hi_], kt["va"], pt_ap[:, lo_:hi_], kt_i == 0, kt_i == len(ktiles) - 1))
                    mm_group(None, items, reads=[pt_tok] + kt["vreads"], writes=[("ps", obank)])
                    return
                for (o_ap, c0, c1, fk, lk) in osubs:
                    if fk <= kt_i <= lk and kt["pv_lo"] <= c0:
                        items.append((o_ap, pt_ap[:, c0:c1], kt["va"], not first_done[0], kt_i == lk))
                        first_done[0] = True
                if items:
                    mm_group(None, items, reads=[pt_tok] + kt["vreads"], writes=[("ps", obank)])

            for kt_i, kt in enumerate(ktiles):
                k = psnext("s")
                nk, lo, hi = kt["nk"], kt["lo"], kt["hi"]
                items = []
                if kt["mask"] is not None and not kt.get("amask"):
                    mask_ap, mc = kt["mask"]
                    items.append((PS[k][0:nk, lo:lo + mc], kt["ka"], qa_fn(lo, mc), True, False))
                    items.append((PS[k][0:nk, lo:lo + mc], IDENT[0:nk, 0:nk], mask_ap, False, True))
                    if hi > lo + mc:
                        items.append((PS[k][0:nk, lo + mc:hi], kt["ka"], qa_fn(lo + mc, hi - lo - mc), True, True))
                else:
                    items.append((PS[k][0:nk, lo:hi], kt["ka"], qa_fn(lo, hi - lo), True, True))
                mm_group(None, items, reads=kt["reads"] + ["cst"], writes=[("ps", k)])
                if kt["pt"] == "rot":
                    g = attend.ptc % 4
                    attend.ptc += 1
                    pt_ap = SCRb(g, 512, 0, nk)
                    pt_tok = ("scr", g)
                else:
                    pt_ap = PTSt[0:nk, 32 * kt["pt"]:32 * kt["pt"] + 32]
                    pt_tok = "pts%d" % kt["pt"]
                S.op("act", lambda e, pt_ap=pt_ap, k=k, nk=nk, lo=lo, hi=hi:
                     e.activation(pt_ap[:, lo:hi], PS[k][0:nk, lo:hi], AF.Exp),
                     reads=[("ps", k)], writes=[pt_tok])
                if kt.get("amask"):
                    S.op("pool", lambda e, pt_ap=pt_ap, lo=lo: e.affine_select(
                        pt_ap[:, lo:lo + 128], pt_ap[:, lo:lo + 128], [[1, 128]], ALU.is_ge, 0.0, base=0, channel_multiplier=-1),
                        reads=[pt_tok], writes=[pt_tok])
                if kt["pt"] == "rot":
                    if len(PVQ) >= 2:
                        PVQ.pop(0)()
                    PVQ.append(lambda kt_i=kt_i, kt=kt, pt_ap=pt_ap, pt_tok=pt_tok: pv(kt_i, kt, pt_ap, pt_tok))
                else:
                    flush_pv()
                    pv(kt_i, kt, pt_ap, pt_tok)
        attend.ptc = 0

        def cs_all():
            return ["cw"]

        def post_norm_residual(ostg, rows, i, junk, junk_toks, GPap, gp_toks, o_toks):
            ssc = SM[0:rows, SM_T + 8:SM_T + 9]
            S.op("act", lambda e: e.activation(junk, ostg, AF.Square, accum_out=ssc),
                 reads=o_toks, writes=list(junk_toks) + ["pn"])
            S.op("act", lambda e: e.activation(ssc, ssc, AF.Ln, bias=EPS, scale=1.0 / D), reads=["pn"], writes=["pn"])
            S.op("act", lambda e: e.activation(ssc, ssc, AF.Exp, scale=-0.5), reads=["pn"], writes=["pn"])
            S.op("dve", lambda e: e.scalar_tensor_tensor(ostg, ostg, ssc, GPap[0:rows, :], ALU.mult, ALU.mult),
                 reads=list(o_toks) + list(gp_toks) + ["pn"], writes=o_toks)
            S.op("pool", lambda e: e.tensor_tensor(X[0:rows, i, :], X[0:rows, i, :], ostg, ALU.add),
                 reads=list(o_toks) + [("x", i)], writes=[("x", i)])

        def ffn_phase(l):
            psr.clear()
            psr.update({"u": [0, 1, 2, 4, 5, 6], "t": [3], "d": [4, 5, 6, 7]})
            for kk in psr:
                psi[kk] = 0
            HC = 1056
            o = [0]

            def cv_(nb):
                r = o[0]
                o[0] = r + (nb + 63) // 64 * 64
                return r
            H2_O = cv_(8 * HC * 2)
            ACT_O = cv_(NJ * HC * 2)
            WDN_O = cv_(NJ * 1024 * 2)
            WU_O = [cv_(4096) for _ in range(2)]
            AE_O = [cv_(2064) for _ in range(2)]
            LIN_O = [cv_(1024) for _ in range(2)]
            CONV_O = [cv_(2048) for _ in range(2)]
            OSTG_O = cv_(4096)
            GPF_O = cv_(4096)
            HBF_O = cv_(2048)
            SAVE_O = cv_(NJ * 2 * 4)
            SAVES_O = cv_(NJ * 4 * 4)
            assert o[0] <= ARB, (o[0], ARB)

            H2 = lambda c, c0, n: bf(H2_O + (c * HC + c0) * 2, n)
            H23 = lambda c0, n: AR[:, H2_O // 2:H2_O // 2 + 8 * HC].rearrange("p (c t) -> p c t", c=8)[:, :, c0:c0 + n]
            ACTB = lambda j, c0, n: bf(ACT_O + (j * HC + c0) * 2, n)
            WDN = lambda j, c0, n: bf(WDN_O + (j * 1024 + c0) * 2, n)
            WDN3 = AR[:, WDN_O // 2:WDN_O // 2 + NJ * 1024].rearrange("p (j n) -> p j n", j=NJ)
            WU = lambda s, c, c0, n: bf(WU_O[s] + (c * 256 + c0) * 2, n)
            WU3 = lambda s: AR[:, WU_O[s] // 2:WU_O[s] // 2 + 2048].rearrange("p (c n) -> p c n", c=8)
            OSTG = f32(OSTG_O, 1024)
            GPF = f32(GPF_O, 1024)
            HBF = bf(HBF_O, 1024)
            SAVE = f32(SAVE_O, NJ * 2)
            SAVES = f32(SAVES_O, NJ * 4)

            dma("sp", GPF, g_post_ffn[l:l + 1, :].broadcast_to([128, D]), writes=["gpf"], key="gp")
            wdn_src = w_dn[l].rearrange("(j p) n -> p j n", p=128)
            for part in range(2):
                dma("pool", WDN3[:, 11 * part:11 * part + 11, :], wdn_src[:, 11 * part:11 * part + 11, :],
                    writes=[("wdn", part)], key=("wdn", part))
            aecnt = 0

            def wu_load(j, s_):
                dma("pool", WU3(s_), w_up[l, j], writes=[("wu", s_, 0), ("wu", s_, 1)], key=("wu", s_, 0))

            for half in range(2):
                tiles = list(range(0, 8)) if half == 0 else list(range(8, 17))
                base = 0 if half == 0 else 1024
                blocks = [(0, 512), (512, 512)] + ([(1024, 32)] if half == 1 else [])
                h2toks = [("h2", (i if i < 8 else i - 8)) for i in tiles]
                if half == 0:
                    for i in tiles:
                        rows = tile_rows(i)
                        c0, n = tcols(i)
                        prenorm_tile(X[0:rows, i, :], rows, ("x", i), prm(l, "gpf", 8), H23(c0 - base, n), ("h2", (i if i < 8 else i - 8)), 0,
                                     hb=HBF[0:rows, :], tpool="t", hb_toks=["hbf"])
                wu_load(0, 0)
                pend_tail = None
                for j in range(NJ):
                    s = j % 2
                    if j + 1 < NJ:
                        wu_load(j + 1, (j + 1) % 2)
                    w0 = prm(l, "wdw", 1, 0, 128, 3 * j)
                    w1 = prm(l, "wdw", 1, 0, 128, 3 * j + 1)
                    w2 = prm(l, "wdw", 1, 0, 128, 3 * j + 2)
                    bd = prm(l, "bdw", 1, 0, 128, j)
                    for bidx, (lc0, n) in enumerate(blocks):
                        sample = (n == 32)
                        ka = psnext("u")
                        mm_group(None, [(PS[ka][:, 0:n], WU(s, c, 0, 128), H2(c, lc0, n), c == 0, c == 7) for c in range(8)],
                                 reads=h2toks + [("wu", s, 0)], writes=[("ps", ka)])
                        kl = psnext("u")
                        mm_group(None, [(PS[kl][:, 0:n], WU(s, c, 128, 128), H2(c, lc0, n), c == 0, c == 7) for c in range(8)],
                                 reads=h2toks + [("wu", s, 1)], writes=[("ps", kl)])
                        p = aecnt % 2
                        aecnt += 1
                        lin = bf(LIN_O[p], 512)
                        conv = f32(CONV_O[p], 512)
                        if sample:
                            AEs = f32(AE_O[p], 36).rearrange("p (b t) -> p b t", b=2)
                            S.op("dve", lambda e, AEs=AEs, j=j: e.tensor_copy(
                                AEs[:, :, 0:2], prm(l, "cconv", 4, 0, 128, 4 * j).rearrange("p (b r) -> p b r", b=2)),
                                reads=["prm"], writes=[("aeh", p)])
                            S.op("act", lambda e, AEs=AEs, ka=ka: e.copy(AEs[:, :, 2:18], PS[ka][:, 0:32].rearrange("p (b t) -> p b t", b=2)),
                                 reads=[("ps", ka)], writes=[("ae", p)])
                            taps = [AEs[:, :, k:k + 16] for k in range(3)]
                            cv3 = conv[:, 0:32].rearrange("p (b t) -> p b t", b=2)
                            psa = PS[ka][:, 0:32].rearrange("p (b t) -> p b t", b=2)
                            sil_out = f32(AE_O[p] + 256, 32)
                            sil_in = conv[:, 0:32]
                        else:
                            AEp = f32(AE_O[p], 514)
                            if bidx == 0 and half == 0:
                                S.op("dve", lambda e, AEp=AEp: e.memset(AEp[:, 0:2], 0.0), writes=[("aeh", p)])
                            elif bidx == 0:
                                S.op("dve", lambda e, AEp=AEp, j=j: e.tensor_copy(AEp[:, 0:2], SAVE[:, 2 * j:2 * j + 2]),
                                     reads=[("save", j)], writes=[("aeh", p)])
                            else:
                                prev = f32(AE_O[1 - p], 514)
                                S.op("dve", lambda e, AEp=AEp, prev=prev: e.tensor_copy(AEp[:, 0:2], prev[:, 512:514]),
                                     reads=[("aet", 1 - p)], writes=[("aeh", p)])
                            S.op("act", lambda e, AEp=AEp, ka=ka: e.copy(AEp[:, 2:514], PS[ka][:, 0:512]),
                                 reads=[("ps", ka)], writes=[("ae", p), ("aet", p)])
                            taps = [AEp[:, k:k + 512] for k in range(3)]
                            cv3 = conv
                            psa = PS[ka][:, 0:512]
                            sil_out = AEp[:, 0:512]
                            sil_in = conv
                        S.op("act", lambda e, cv3=cv3, psa=psa, w2=w2, bd=bd: e.activation(cv3, psa, AF.Identity, bias=bd, scale=w2),
                             reads=[("ps", ka), "prm"], writes=[("conv", p)])
                        S.op("act", lambda e, lin=lin, kl=kl, n=n: e.copy(lin[:, 0:n], PS[kl][:, 0:n]),
                             reads=[("ps", kl)], writes=[("lin", p)])
                        S.op("dve", lambda e, cv3=cv3, taps=taps, w1=w1: e.scalar_tensor_tensor(cv3, taps[1], w1, cv3, ALU.mult, ALU.add),
                             reads=[("ae", p), ("aeh", p), ("conv", p)], writes=[("conv", p)])
                        S.op("dve", lambda e, cv3=cv3, taps=taps, w0=w0: e.scalar_tensor_tensor(cv3, taps[0], w0, cv3, ALU.mult, ALU.add),
                             reads=[("ae", p), ("aeh", p), ("conv", p)], writes=[("conv", p)])
                        if not sample and bidx == 1:
                            S.op("dve", lambda e, AEp=AEp, j=j: e.tensor_copy(SAVE[:, 2 * j:2 * j + 2], AEp[:, 512:514]),
                                 reads=[("aet", p)], writes=[("save", j)])
                        if sample:
                            S.op("dve", lambda e, AEs=AEs, j=j: e.tensor_copy(
                                SAVES[:, 4 * j:4 * j + 4].rearrange("p (b r) -> p b r", b=2), AEs[:, :, 16:18]),
                                reads=[("ae", p)], writes=[("saves", j)])

                        def tail(p=p, sil_out=sil_out, sil_in=sil_in, j=j, lc0=lc0, n=n, lin=lin, half=half):
                            S.op("act", lambda e: e.activation(sil_out, sil_in, AF.Silu),
                                 reads=[("conv", p), ("ae", p), ("aeh", p)], writes=[("ae", p), ("aeh", p)])
                            S.op("pool", lambda e: e.tensor_tensor(ACTB(j, lc0, n), sil_out, lin[:, 0:n], ALU.mult),
                                 reads=[("ae", p), ("aeh", p), ("lin", p)], writes=[("actb", j, half)])
                        if pend_tail is not None:
                            pend_tail()
                        pend_tail = tail
                if pend_tail is not None:
                    pend_tail()
                    pend_tail = None
                if half == 1:
                    dma("sp", oconv[l], SAVE.rearrange("p (j r) -> p j r", r=2), reads=[("save", j) for j in range(NJ)], key="oconv")
                    dma("sp", osconv[l], SAVES.rearrange("p (j b r) -> p j b r", b=2, r=2), reads=[("saves", j) for j in range(NJ)], key="osconv")
                atoks = [("actb", j, half) for j in range(NJ)]
                nxt = list(range(8, 17)) if half == 0 else []
                for ti_, i in enumerate(tiles):
                    rows = tile_rows(i)
                    c0, n = tcols(i)
                    lc0 = c0 - base
                    defer_b = []
                    for i2 in (nxt[ti_:ti_ + 1] if ti_ < 7 else nxt[7:8]):
                        rows2 = tile_rows(i2)
                        c02, n2 = tcols(i2)
                        defer_b.append(prenorm_tile(X[0:rows2, i2, :], rows2, ("x", i2), prm(l, "gpf", 8), H23(c02 - 1024, n2),
                                                    ("h2", i2 - 8), 0, hb=HBF[0:rows2, :], tpool="t", hb_toks=["hbf"], defer=True))
                        if "nodefer" in DBG:
                            defer_b.pop()()
                    for hf in range(2):
                        kd = psnext("d")
                        mm_group(None, [(PS[kd][0:rows, :], ACTB(j, lc0, n), WDN(j, 512 * hf, 512), j == 0, j == NJ - 1) for j in range(NJ)],
                                 reads=atoks + [("wdn", 0), ("wdn", 1)], writes=[("ps", kd)])
                        if hf == 0:
                            S.op("act", lambda e, kd=kd, rows=rows: e.copy(OSTG[0:rows, 0:512], PS[kd][0:rows, :]),
                                 reads=[("ps", kd)], writes=["ostg0"])
                        else:
                            S.op("dve", lambda e, kd=kd, rows=rows: e.tensor_copy(OSTG[0:rows, 512:1024], PS[kd][0:rows, :]),
                                 reads=[("ps", kd)], writes=["ostg1"])
                    for fb in defer_b:
                        fb()
                    post_norm_residual(OSTG[0:rows, :], rows, i, HBF[0:rows, :], ["hbf"], GPF, ["gpf"], ["ostg0", "ostg1"])
                if half == 0:
                    prenorm_tile(X[0:32, 16, :], 32, ("x", 16), prm(l, "gpf", 8), H23(1024, 32), ("h2", 8), 0,
                                 hb=HBF[0:32, :], tpool="t", hb_toks=["hbf"])
        RFOX = lambda i, rows=128: SM[0:rows, 100 + i:101 + i]
        RSGU = lambda i, rows=128: SM[0:rows, 120 + i:121 + i]
        RMEM = lambda i, rows=128: SM[0:rows, 140 + i:141 + i]
        SSFOX = lambda i, rows=128: SM[0:rows, 160 + i:161 + i]
        SSMEM = lambda i, rows=128: SM[0:rows, 180 + i:181 + i]
        ALLHT = [("ht", i) for i in range(17)]

        class NSQ:
            def __init__(self):
                self.p1 = None
                self.p2 = None
                self.p3 = None

            def step(self, new_p1):
                if self.p3 is not None:
                    self.p3()
                    self.p3 = None
                if self.p2 is not None:
                    self.p3 = self.p2()
                    self.p2 = None
                if self.p1 is not None:
                    self.p2 = self.p1()
                self.p1 = new_p1

            def flush(self):
                self.step(None)
                self.step(None)
                self.step(None)
        nsq = NSQ()

        def wload(slot, src2d, ncol, extra_writes=()):
            dma("pool", WS3(slot, ncol), src2d.rearrange("(c p) n -> p c n", p=128),
                writes=[("ws", slot)] + list(extra_writes), key=("ws", slot))

        def KAS(b, c0, n, p0=0, p1=67):
            return bf(WS_O[0] + b * 3200 + c0 * 2, n, p0, p1)

        def VAS(b, j):
            return bf(WS_O[0] + b * 3200 + 2080 + j * 130, 65)

        def VAS3(b):
            return AR[:, (WS_O[0] + b * 3200 + 2080) // 2:(WS_O[0] + b * 3200 + 2080) // 2 + 520].rearrange("p (j k) -> p j k", k=65)

        SC_TOKS = [("sc", b, x) for b in range(2) for x in ("k", "v", "r", "one")]

        def norm_store_fm(obank, I, grp, chunk, p0, ssq4, l):
            rlrow = f32(SCR_O + 6 * 1024, 512, 64, 65)
            hfT = f32(SCR_O + 6 * 1024, 512, 0, 64)
            S.op("act", lambda e: e.activation(rlrow, PS[obank][64:65, 0:512], AF.Ln), reads=[("ps", obank)], writes=[("scr", 6, "r")])
            S.op("act", lambda e: e.activation(rlrow, rlrow, AF.Exp, scale=-1.0), reads=[("scr", 6, "r")], writes=[("scr", 6, "r")])
            kb = psnext("g")
            S.op("pe", lambda e: e.matmul(PS[kb][0:64, 0:512], ONEROW[64:65, 0:64], rlrow, start=True, stop=True, skip_group_check=True),
                 reads=[("scr", 6, "r"), "onerow"], writes=[("ps", kb)])
            S.op("dve", lambda e: e.tensor_copy(hfT, PS[kb][0:64, 0:512]), reads=[("ps", kb)], writes=scr(6, 7))
            S.op("dve", lambda e: e.tensor_tensor(hfT, PS[obank][0:64, 0:512], hfT, ALU.mult), reads=[("ps", obank)] + scr(6, 7), writes=scr(6, 7))
            return lambda: norm_store_fm2(I, grp, chunk, p0, ssq4, l)

        def norm_store_fm2(I, grp, chunk, p0, ssq4, l):
            hfT = f32(SCR_O + 6 * 1024, 512, 0, 64)
            sqb = bf(SCR_O + 4 * 1024, 512, 0, 64)
            S.op("act", lambda e: e.activation(MX(chunk, 512 * I, 512, p0, p0 + 64), hfT, AF.Copy,
                                               scale=prm(l, "ggo", 1, p0, p0 + 64, chunk)),
                 reads=scr(6, 7) + ["prm"], writes=[("mx", chunk, 4 * I + s_, p0) for s_ in range(4)])
            S.op("act", lambda e: e.activation(sqb, hfT, AF.Square), reads=scr(6, 7), writes=scr(4))
            return lambda: norm_store_fm3(I, grp, ssq4)

        def norm_store_fm3(I, grp, ssq4):
            sqb = bf(SCR_O + 4 * 1024, 512, 0, 64)
            kt = psnext("t")
            ones_col = bf(VA_O + 64 * 2, 1, 0, 64)

            def ssmm(e):
                ins = None
                for s_ in range(4):
                    ins = e.matmul(PS[kt][:, s_:s_ + 1], sqb[:, 128 * s_:128 * s_ + 128], ones_col, start=True, stop=True,
                                   skip_group_check=True)
                return ins
            S.op("pe", ssmm, reads=scr(4) + ["va_ones"], writes=[("ps", kt)])
            toks4 = [("ssq", grp, 4 * I + s_) for s_ in range(4)]
            S.op("dve", lambda e: e.tensor_tensor(ssq4, ssq4, PS[kt][:, 0:4], ALU.add), reads=[("ps", kt)] + toks4, writes=toks4)

        def norm_and_store_block(obank, I, grp, chunk, p0, ssq4, l):
            O4 = PS[obank][:, 0:260].rearrange("p (s k) -> p s k", k=65)
            rl4 = SM[:, SM_T + 10:SM_T + 14]
            sq4 = SM[:, SM_T + 14:SM_T + 18]
            S.op("dve", lambda e: e.reciprocal(rl4.unsqueeze(2), O4[:, :, 64:65]), reads=[("ps", obank)], writes=["rl4"])
            hf4 = SCRf(6, 256)
            hb4 = SCRb(7, 256)
            S.op("dve", lambda e: e.tensor_tensor(hf4.rearrange("p (s k) -> p s k", k=64), O4[:, :, 0:64],
                                                  rl4.unsqueeze(2).to_broadcast([128, 4, 64]), ALU.mult),
                 reads=[("ps", obank), "rl4"], writes=scr(6))

            def sqs(e):
                ins = None
                for s_ in range(4):
                    ins = e.activation(hb4[:, 64 * s_:64 * s_ + 64], hf4[:, 64 * s_:64 * s_ + 64], AF.Square,
                                       accum_out=sq4[:, s_:s_ + 1])
                return ins
            S.op("act", sqs, reads=scr(6), writes=scr(7) + ["sq4"])
            toks4 = [("ssq", grp, 4 * I + s_) for s_ in range(4)]
            S.op("dve", lambda e: e.tensor_tensor(ssq4, ssq4, sq4, ALU.add), reads=["sq4"] + toks4, writes=toks4)
            S.op("act", lambda e: e.copy(hb4, hf4), reads=scr(6, 7), writes=scr(7))
            kt = psnext("t")

            def tr(e):
                ins = None
                for s_ in range(4):
                    ins = e.transpose(PSB[kt][p0:p0 + 64, 128 * s_:128 * s_ + 128], hb4[:, 64 * s_:64 * s_ + 64], IDENT[:, :])
                return ins
            S.op("pe", tr, reads=scr(7) + ["cst"], writes=[("ps", kt)])
            S.op("dve", lambda e: e.tensor_scalar(MX(chunk, 512 * I, 512, p0, p0 + 64), PSB[kt][p0:p0 + 64, 0:512],
                                                  prm(l, "ggo", 1, p0, p0 + 64, chunk), None, ALU.mult),
                 reads=[("ps", kt), "prm"], writes=[("mx", chunk, 4 * I + s_, p0) for s_ in range(4)])

        def norm_and_store(obank, ocol, rows, tile_i, grp, chunk, p0, ssq_acc, l):
            rl = SM[0:rows, SM_T + 4:SM_T + 5]
            S.op("dve", lambda e: e.reciprocal(rl, PS[obank][0:rows, ocol + 64:ocol + 65]),
                 reads=[("ps", obank)], writes=["rl"])
            hf = SCRf(5, 64, 0, rows)
            hb = SCRb(5, 64, 0, rows, off=256)
            S.op("dve", lambda e: e.tensor_scalar(hf, PS[obank][0:rows, ocol:ocol + 64], rl, None, ALU.mult),
                 reads=[("ps", obank), "rl"], writes=scr(5))
            sq = SM[0:rows, SM_T + 5:SM_T + 6]
            S.op("act", lambda e: e.activation(hb, hf, AF.Square, accum_out=sq), reads=scr(5), writes=scr(5) + ["sq"])
            S.op("dve", lambda e: e.tensor_tensor(ssq_acc, ssq_acc, sq, ALU.add),
                 reads=["sq", ("ssq", grp, tile_i)], writes=[("ssq", grp, tile_i)])
            S.op("act", lambda e: e.copy(hb, hf), reads=scr(5), writes=scr(5))
            kt = psnext("t")
            c0, n = tcols(tile_i)
            S.op("pe", lambda e: e.transpose(PSB[kt][p0:p0 + 64, 0:rows], hb, IDENT[0:rows, 0:rows]),
                 reads=scr(5) + ["cst"], writes=[("ps", kt)])
            S.op("dve", lambda e: e.tensor_scalar(MX(chunk, c0, n, p0, p0 + 64), PSB[kt][p0:p0 + 64, 0:rows],
                                                  prm(l, "ggo", 1, p0, p0 + 64, chunk), None, ALU.mult),
                 reads=[("ps", kt), "prm"], writes=[("mx", chunk, tile_i, p0)])

        try:
            for l in range(L):
                psr.clear()
                psr.update({"g": [0, 1], "s": [2, 3, 4], "o": [5, 6], "t": [7], "w": [0, 1, 2, 3, 4, 5]})
                for kk in psr:
                    psi[kk] = 0
                S.op("dve", lambda e: e.memset(
                    AR[:, VA_O // 2: VA_O // 2 + 17 * 8 * 65].rearrange("p (a k) -> p a k", k=65)[:, :, 64:65], 1.0),
                    writes=["va_ones"])
                S.op("dve", lambda e: e.memset(MVAt[:, :].rearrange("p (a k) -> p a k", k=65)[:, :, 64:65], 1.0),
                     writes=["mva_ones"])
                for s in (0, 2):
                    S.op("dve", lambda e, s=s: e.memset(QK(s, 0, NT, 64, 67), -1.0), writes=[("qkc", s)])
                for s in (1, 3):
                    S.op("dve", lambda e, s=s: e.memset(QK(s, 0, NT, 64, 65), 1.0), writes=[("qkc", s)])
                wload(0, w_in[l][:, 1024:1536], 512, extra_writes=SC_TOKS if l > 0 else ())
                wload(1, w_in[l][:, 512:1024], 512)
                wload(2, w_in[l][:, 1536:2056], 520)

                SGC = MX_O + 12288
                WSTl = bf(SGC, 512)
                GSl = f32(SGC + 1024, 256)
                dma("pool", WSTl.rearrange("p (g i) -> p g i", g=4), wsT[l].rearrange("g j i -> j g i"),
                    writes=[("sguc", 0)], key="wst")
                S.op("dve", lambda e: e.memset(bf(SGC, 512, 64, 128).rearrange("p (g i) -> p g i", g=4)[:, :, 0:64], 0.0),
                     reads=[("sguc", 0)], writes=[("sguc", 0)])
                dma("sp", GSl, g_sgu[l:l + 1, :].broadcast_to([128, 256]), writes=[("sguc", 1)], key="gs")

                def sgu_ops(i):
                    st_ = 0 if "setA" in DBG else i % 2
                    base = MX_O if st_ == 0 else MX_O + 6144
                    tk = "sguA" if st_ == 0 else "sguB"
                    Sb = lambda g, n, p0=0, p1=128: bf(base + g * 1024, n, p0, p1)
                    Sf = lambda g, n, p0=0, p1=128: f32(base + g * 1024, n, p0, p1)
                    sc = lambda *gs: [(tk, g) for g in gs]
                    rows = tile_rows(i)
                    c0, n = tcols(i)
                    s1, s2 = [], []
                    k = psnext("s")
                    z = Sf(0, 512, 0, rows)
                    t1 = Sf(2, 512, 0, rows)
                    ssv = SM[0:rows, SM_T + 20 + 2 * st_:SM_T + 21 + 2 * st_]
                    sss = SM[0:rows, SM_T + 21 + 2 * st_:SM_T + 22 + 2 * st_]
                    ssvt, ssst = ("ssv", st_), ("sss", st_)
                    vvf = Sf(3, 256, 0, rows)
                    vvb = Sb(2, 256, 0, rows)
                    sg = Sf(4, 256, 0, rows)
                    sgb = Sb(5, 256, 0, rows)
                    s1.append(lambda: mm_group(None, [(PS[k][0:rows, :], HT(c, c0, n), WS(2, c, 8, 512), c == 0, c == 7) for c in range(8)],
                                               reads=[("ht", i), ("ws", 2)], writes=[("ps", k)]))
                    s1.append(lambda: S.op("act", lambda e: e.copy(z, PS[k][0:rows, :]), reads=[("ps", k)], writes=sc(0, 1)))
                    s1.append(lambda: S.op("dve", lambda e: e.tensor_tensor(t1, z, z, ALU.mult), reads=sc(0, 1), writes=sc(2, 3)))
                    s1.append(lambda: S.op("dve", lambda e: e.tensor_scalar(t1, t1, 0.044715, 1.0, ALU.mult, ALU.add), reads=sc(2, 3), writes=sc(2, 3)))
                    s1.append(lambda: S.op("dve", lambda e: e.tensor_tensor(t1, t1, z, ALU.mult), reads=sc(0, 1, 2, 3), writes=sc(2, 3)))
                    s1.append(lambda: S.op("act", lambda e: e.activation(t1, t1, AF.Exp, scale=-1.5957691216057308), reads=sc(2, 3), writes=sc(2, 3)))
                    s1.append(lambda: S.op("act", lambda e: e.activation(t1, t1, AF.Ln, bias=1.0, scale=1.0), reads=sc(2, 3), writes=sc(2, 3)))
                    s1.append(lambda: S.op("act", lambda e: e.activation(t1, t1, AF.Exp, scale=-1.0), reads=sc(2, 3), writes=sc(2, 3)))
                    s1.append(lambda: S.op("dve", lambda e: e.tensor_tensor(z, z, t1, ALU.mult), reads=sc(0, 1, 2, 3), writes=sc(0, 1)))
                    s1.append(lambda: S.op("act", lambda e: e.activation(t1[:, 0:256], z[:, 256:512], AF.Square, accum_out=ssv),
                                           reads=sc(0, 1), writes=sc(2) + [ssvt]))
                    s1.append(lambda: S.op("act", lambda e: e.activation(ssv, ssv, AF.Ln, bias=EPS, scale=1.0 / 256), reads=[ssvt], writes=[ssvt]))
                    s1.append(lambda: S.op("act", lambda e: e.activation(ssv, ssv, AF.Exp, scale=-0.5), reads=[ssvt], writes=[ssvt]))
                    s1.append(lambda: S.op("dve", lambda e: e.scalar_tensor_tensor(vvf, z[:, 256:512], ssv, GSl[0:rows, :], ALU.mult, ALU.mult),
                                           reads=sc(0, 1) + [("sguc", 1)] + [ssvt], writes=sc(3)))
                    s1.append(lambda: S.op("act", lambda e: e.copy(vvb, vvf), reads=sc(3), writes=sc(2)))
                    if i == 16:
                        s1.append(lambda: dma("sp", ogv[l], vvf, reads=sc(3), key="ogv"))
                    k2 = psnext("s")
                    if i < 16:
                        items = [(PS[k2][0:128, 64 * g:64 * g + 64], WSTl[:, 128 * g:128 * g + 128], vvb[:, 64 * g:64 * g + 64], True, True) for g in range(4)]
                        bsap = prm(l, "bs", 4)
                        wtok = [("sguc", 0)]
                    else:
                        items = [(PS[k2][0:32, 64 * g:64 * g + 64], WSTS(l, g), vvb[:, 64 * g:64 * g + 64], True, True) for g in range(4)]
                        bsap = prm(l, "bss", 4, 0, 32)
                        wtok = ["wsts"] + [("wsts", l, b) for b in range(2)]
                    s1.append(lambda: mm_group(None, items, reads=sc(2) + wtok, writes=[("ps", k2)]))
                    s2.append(lambda: S.op("dve", lambda e: e.tensor_tensor(
                        sg.rearrange("p (g d) -> p g d", g=4), PS[k2][0:rows, 0:256].rearrange("p (g d) -> p g d", g=4),
                        bsap.unsqueeze(2).to_broadcast([rows, 4, 64]), ALU.add),
                        reads=[("ps", k2), "prm"], writes=sc(4)))
                    s2.append(lambda: S.op("dve", lambda e: e.tensor_tensor(sg, sg, z[:, 0:256], ALU.mult), reads=sc(0, 1, 4), writes=sc(4)))
                    s2.append(lambda: S.op("act", lambda e: e.activation(sgb, sg, AF.Square, accum_out=sss),
                                           reads=sc(4), writes=sc(5) + [ssst]))
                    s2.append(lambda: S.op("act", lambda e: e.activation(sss, sss, AF.Ln, bias=EPS, scale=1.0 / 256), reads=[ssst], writes=[ssst]))
                    s2.append(lambda: S.op("act", lambda e: e.activation(RSGU(i, rows), sss, AF.Exp, scale=-0.5),
                                           reads=[ssst], writes=[("rsgu", i)]))
                    s2.append(lambda: S.op("act", lambda e: e.copy(sgb, sg), reads=sc(4, 5), writes=sc(5)))

                    def trs():
                        kt = psnext("o")

                        def tr(e):
                            ins = None
                            for cc in range(2):
                                ins = e.transpose(PSB[kt][:, cc * 128:cc * 128 + rows], sgb[:, cc * 128:(cc + 1) * 128], IDENT[0:rows, 0:rows])
                            return ins
                        S.op("pe", tr, reads=sc(5) + ["cst"], writes=[("ps", kt)])
                        for cc in range(2):
                            S.op("dve", lambda e, cc=cc, l=l: e.tensor_scalar(
                                MX(4 + cc, c0, n), PSB[kt][:, cc * 128:cc * 128 + rows], prm(l, "ggo", 1, 0, 128, 4 + cc), None, ALU.mult),
                                reads=[("ps", kt), "prm"], writes=[("mx", 4 + cc, i)])
                    s2.append(trs)
                    return s1, s2

                def interleave(a, b):
                    for q in range(max(len(a), len(b))):
                        if q < len(a):
                            a[q]()
                        if q < len(b):
                            b[q]()

                def vk_tile(i):
                    rows = tile_rows(i)
                    c0, n = tcols(i)
                    for which, slot, oprompt, osample, sg in (("v", 0, ofv, osv, 4), ("k", 1, ofk, osk, 6)):
                        k = psnext("g")
                        mm_group(None, [(PS[k][0:rows, :], HT(c, c0, n), WS(slot, c, 0, 512), c == 0, c == 7)
                                        for c in range(8)],
                                 reads=[("ht", i), ("ws", slot)], writes=[("ps", k)])
                        stg = SCRf(sg, 512, 0, rows)
                        S.op("act", lambda e, stg=stg, k=k, rows=rows: e.copy(stg, PS[k][0:rows, :]),
                             reads=[("ps", k)], writes=scr(sg, sg + 1))
                        if which == "v":
                            va3 = AR[0:rows, VA_O // 2 + i * 520: VA_O // 2 + (i + 1) * 520].rearrange(
                                "p (h k) -> p h k", k=65)[:, :, 0:64]
                            S.op("dve", lambda e, va3=va3, k=k, rows=rows: e.tensor_copy(
                                va3, PS[k][0:rows, :].rearrange("p (h k) -> p h k", k=64)),
                                reads=[("ps", k), "va_ones"], writes=[("va", i)])
                        dst = oprompt[l, 128 * i:128 * i + 128, :] if i < 16 else osample[l, :, :]
                        dma("sp", dst, stg, reads=scr(sg, sg + 1), key=("stg", sg))

                prev2 = []
                for i in range(17):
                    rows = tile_rows(i)
                    c0, n = tcols(i)
                    prenorm_tile(X[0:rows, i, :], rows, ("x", i), prm(l, "gpm", 8), HT3(c0, n), ("ht", i), i % 2)
                    if i >= 1:
                        vk_tile(i - 1)
                        s1_, s2_ = sgu_ops(i - 1)
                        interleave(s1_, prev2)
                        prev2 = s2_
                vk_tile(16)
                s1_, s2_ = sgu_ops(16)
                interleave(s1_, prev2)
                interleave([], s2_)
                S.op("dve", lambda e: e.memset(bf(MX_O + 15 * 1024, 2), 0.0),
                     writes=[("sguA", g) for g in range(6)] + [("sguB", g) for g in range(6)] + [("sguc", 0), ("sguc", 1)]
                     + [("mx", c, i, p0) for c in (0, 1, 2, 3) for i in range(17) for p0 in (0, 64)])

                _chk("vk%d" % l)
                NB = SM[0:8, 210:211]
                S.op("dve", lambda e, l=l: e.tensor_scalar(NB, prm(l, "bf", 1, 0, 8), -1.0, None, ALU.mult),
                     reads=["prm"], writes=["cw", "nb"])
                for bi, (c0, n) in enumerate(BLKS):
                    k = psnext("g")
                    mm_group(None, [(PS[k][0:8, 0:n], WS(2, c, 0, 8), HT(c, c0, n), c == 0, c == 7) for c in range(8)],
                             reads=ALLHT + [("ws", 2)], writes=["cw", ("ps", k)])
                    tmp = SCRf(0, 512, 96, 104)
                    S.op("act", lambda e, k=k, n=n, tmp=tmp: e.activation(tmp[:, 0:n], PS[k][0:8, 0:n], AF.Exp, bias=NB, scale=-1.0),
                         reads=[("ps", k), "nb"], writes=scr(0, 1))
                    S.op("act", lambda e, n=n, tmp=tmp: e.activation(tmp[:, 0:n], tmp[:, 0:n], AF.Ln, bias=1.0, scale=1.0),
                         reads=scr(0, 1), writes=scr(0, 1))
                    S.op("dve", lambda e, c0=c0, n=n, tmp=tmp: e.tensor_scalar(CT(c0, n), tmp[:, 0:n], -1.0, None, ALU.mult),
                         reads=scr(0, 1), writes=["cw", ("ct", bi)])
                dma("sp", olf[l], CT(0, 2048), reads=[("ct", b) for b in range(4)], writes=["cw"], key="olf")
                dma("sp", oslf[l], CT(2048, 32), reads=[("ct", 4)], writes=["cw"], key="olf2")
                SLF = SM[96:104, 220:252]
                S.op("dve", lambda e: e.tensor_copy(SLF, CT(2048, 32)), reads=[("ct", 4)], writes=["cw", "slf"])
                S.op("dve", lambda e: e.tensor_tensor_scan(CT(0, 2048), ONE8.to_broadcast([8, 2048]), CT(0, 2048), 0.0, ALU.mult, ALU.add),
                     reads=[("ct", b) for b in range(4)] + ["one"], writes=["cw", "ctp"])
                S.op("dve", lambda e: e.tensor_copy(HI(0, 2048), CT(0, 2048)), reads=["ctp"], writes=["cw", "hi"])
                S.op("dve", lambda e: e.tensor_tensor(CT(0, 2048), CT(0, 2048), HI(0, 2048), ALU.subtract),
                     reads=["ctp", "hi"], writes=["cw", "ctp"])
                S.op("dve", lambda e: e.tensor_copy(LO(0, 2048), CT(0, 2048)), reads=["ctp"], writes=["cw", "lo"])
                dma("sp", cscr_p[:, 0, :], HI(0, 2048), reads=["hi"], writes=["cw", "cscr_p"], key="cscr0")
                dma("sp", cscr_p[:, 1, :], LO(0, 2048), reads=["lo"], writes=["cw", "cscr_p2"], key="cscr1")
                CS = lambda b, c0, n: CT(b * 1040 + c0, n)
                for b in range(2):
                    dma("sp", CS(b, 0, 1024), clfT[l, b], reads=["ctp", "lo", ("ct", 4), "slf"], writes=["cw", ("cs", b)], key=("csl", b))
                    S.op("dve", lambda e, b=b: e.tensor_copy(CS(b, 1024, 16), SM[96:104, 220 + 16 * b:236 + 16 * b]),
                         reads=["slf", "ctp", "lo", ("ct", 4)], writes=["cw", ("cs2", b)])
                    S.op("dve", lambda e, b=b: e.tensor_tensor_scan(CS(b, 0, 1040), ONE8.to_broadcast([8, 1040]), CS(b, 0, 1040), 0.0, ALU.mult, ALU.add),
                         reads=[("cs", b), ("cs2", b), "one"], writes=["cw", ("csc", b)])
                S.op("dve", lambda e: e.tensor_copy(HI(0, 2080), CT(0, 2080)),
                     reads=[("csc", 0), ("csc", 1), "cscr_p", "cscr_p2"], writes=["cw", "hi"])
                S.op("dve", lambda e: e.tensor_tensor(CT(0, 2080), CT(0, 2080), HI(0, 2080), ALU.subtract),
                     reads=["hi"], writes=["cw", ("csc", 0), ("csc", 1)])
                S.op("dve", lambda e: e.tensor_copy(LO(0, 2080), CT(0, 2080)), reads=[("csc", 0), ("csc", 1), "cscr_p2"], writes=["cw", "lo"])
                for b in range(2):
                    dma("sp", cscr_s[:, 0, b, :], HI(b * 1040, 1040), reads=["hi"], writes=["cw", ("cscr_s", b, 0)], key=("cscrs", b, 0))
                    dma("sp", cscr_s[:, 1, b, :], LO(b * 1040, 1040), reads=["lo"], writes=["cw", ("cscr_s", b, 1)], key=("cscrs", b, 1))

                _chk("fg%d" % l)
                _chk("sgu%d" % l)
                wload(0, w_mkv[l], 512)
                MEMT3 = AR[:, (SCR_O + 4096) // 2:(SCR_O + 4096) // 2 + 2048].rearrange("p (c t) -> p c t", c=8)
                MEMT = lambda c, c0, n: bf(SCR_O + 4096 + (c * 256 + c0) * 2, n)
                memx = f32(WS_O[2], 1024)
                for mt in range(2):
                    dma("sp", memx, mem[128 * mt:128 * mt + 128, :], writes=[("ws", 2)], key=("ws", 2))
                    prenorm_tile(memx, 128, ("ws", 2), prm(l, "gmem", 8), MEMT3[:, :, 128 * mt:128 * mt + 128],
                                 ("memt", mt), 0, extra_writes=scr(4, 5, 6, 7))
                memt_toks = [("memt", 0), ("memt", 1)] + scr(4, 5, 6, 7)
                stgk = f32(WS_O[2], 512)
                for mt in range(2):
                    k = psnext("g")
                    mm_group(None, [(PS[k][:, :], MEMT(c, 128 * mt, 128), WS(0, c, 0, 512), c == 0, c == 7) for c in range(8)],
                             reads=memt_toks + [("ws", 0)], writes=[("ps", k)])
                    S.op("act", lambda e, k=k: e.copy(stgk, PS[k][:, :]), reads=[("ps", k)], writes=[("ws", 2)])
                    S.op("dve", lambda e, k=k, mt=mt: e.tensor_copy(
                        MVAt[:, mt * 260:(mt + 1) * 260].rearrange("p (h k) -> p h k", k=65)[:, :, 0:64],
                        PS[k][:, 256:512].rearrange("p (h k) -> p h k", k=64)),
                        reads=[("ps", k), "mva_ones"], writes=[("mva", mt)])
                    dma("sp", omk[l, 128 * mt:128 * mt + 128, :], stgk[:, 0:256], reads=[("ws", 2)], key="omk")
                    dma("sp", omv[l, 128 * mt:128 * mt + 128, :], stgk[:, 256:512], reads=[("ws", 2)], key="omv")
                for pr in range(2):
                    k = psnext("g")
                    mm_group(None, [(PS[k][:, 0:256], WS(0, c, 128 * pr, 128), MEMT(c, 0, 256), c == 0, c == 7) for c in range(8)],
                             reads=memt_toks + [("ws", 0)], writes=[("ps", k)])
                    for hh in range(2):
                        h = 2 * pr + hh
                        S.op("act", lambda e, k=k, hh=hh, h=h: e.copy(MKTt[0:64, h * 256:(h + 1) * 256], PS[k][64 * hh:64 * hh + 64, 0:256]),
                             reads=[("ps", k)], writes=[("mkt", h)])
                wload(2, w_in[l][:, 2056:2312], 256)
                for b in range(2):
                    S.op("dve", lambda e, b=b: e.memset(KAS(b, 0, 1040, 64, 65), 1.0), reads=[("ws", 0)], writes=[("ws", 0), ("sc", b, "one")])
                    S.op("dve", lambda e, b=b: e.memset(VAS3(b)[:, :, 64:65], 1.0), reads=[("ws", 0)], writes=[("ws", 0), ("sc", b, "one")])

                S.op("dve", lambda e: e.memset(SM[:, 160:200], 0.0),
                     writes=[("ssq", g, i) for g in (0, 2) for i in range(17)])

                _chk("memkv%d" % l)
                for pr in range(2):
                    for bi, (c0, n) in enumerate(BLKS):
                        k = psnext("g")
                        mm_group(None, [(PS[k][:, 0:n], WS(2, c, 128 * pr, 128), HT(c, c0, n), c == 0, c == 7) for c in range(8)],
                                 reads=ALLHT + [("ws", 2)], writes=[("ps", k)])
                        for hh in range(2):
                            S.op("act", lambda e, k=k, hh=hh, c0=c0, n=n: e.activation(
                                QK(2 * hh, c0, n, 0, 64), PS[k][64 * hh:64 * hh + 64, 0:n], AF.Copy, scale=0.125),
                                reads=[("ps", k)], writes=[("qa", hh, bi)])
                    for hh in range(2):
                        h = 2 * pr + hh
                        for I in range(4):
                            ob = psnext("o")
                            kts = [dict(ka=MKTt[0:64, h * 256 + 128 * j:h * 256 + 128 * j + 128], nk=128,
                                        va=MVAt[:, (j * 4 + h) * 65:(j * 4 + h) * 65 + 65], lo=0, hi=512, pv_lo=0,
                                        mask=None, pt="rot", reads=[("mkt", h), ("qa", hh, I)], vreads=[("mva", j), "mva_ones"])
                                   for j in range(2)]
                            osubs = [(PS[ob][:, 65 * s:65 * s + 65], 128 * s, 128 * s + 128, 0, 1) for s in range(4)]
                            attend(lambda c0, n, hh=hh, I=I: QK(2 * hh, 512 * I + c0, n, 0, 64), 64, kts, "fm", ob, "mem")
                            nsq.step(lambda ob=ob, I=I, pr=pr, hh=hh: norm_store_fm(ob, I, 2, 6 + pr, 64 * hh, SM[:, 180 + 4 * I:184 + 4 * I], l))
                        ob = psnext("g")
                        kts = []
                        for b in range(2):
                            dma("pool", KAS(b, 0, 256, 0, 64), cmkT[l, b, h], reads=[("ws", 0)], writes=[("sc", b, "k")], key=("sck", b))
                            dma("pool", VAS3(b)[:, 0:2, 0:64],
                                cmv[l, b].rearrange("(j p) (h d) -> p j h d", p=128, h=4)[:, :, h, :],
                                reads=[("ws", 0)], writes=[("sc", b, "v")], key=("scv", b), slow=True)
                            for j in range(2):
                                kts.append(dict(ka=KAS(b, 128 * j, 128, 0, 64), nk=128, va=VAS(b, j),
                                                lo=16 * b, hi=16 * b + 16, pv_lo=0, mask=None, pt=b,
                                                reads=[("sc", b, "k"), ("qa", hh, 4)], vreads=[("sc", b, "v"), ("sc", b, "one")]))
                        osubs = [(PS[ob][0:32, 0:65], 0, 32, 0, len(kts) - 1)]
                        attend(lambda c0, n, hh=hh: QK(2 * hh, 2048 + c0, n, 0, 64), 64, kts, osubs, ob, "mems")
                        norm_and_store(ob, 0, 32, 16, 2, 6 + pr, 64 * hh, SSMEM(16, 32), l)

                flush_pv()
                nsq.flush()
                _chk("memattn%d" % l)
                wload(2, w_in[l][:, 0:512], 512)
                for pr in range(4):
                    for bi, (c0, n) in enumerate(BLKS):
                        for qk, slot in ((0, 2), (1, 1)):
                            k = psnext("g")
                            mm_group(None, [(PS[k][:, 0:n], WS(slot, c, 128 * pr, 128), HT(c, c0, n), c == 0, c == 7) for c in range(8)],
                                     reads=ALLHT + [("ws", slot)], writes=[("ps", k)])
                            for hh in range(2):
                                if qk == 0:
                                    S.op("act", lambda e, k=k, hh=hh, c0=c0, n=n: e.activation(
                                        QK(2 * hh, c0, n, 0, 64), PS[k][64 * hh:64 * hh + 64, 0:n], AF.Copy, scale=0.125),
                                        reads=[("ps", k)], writes=[("qa", hh, bi)])
                                else:
                                    S.op("dve", lambda e, k=k, hh=hh, c0=c0, n=n: e.tensor_copy(
                                        QK(2 * hh + 1, c0, n, 0, 64), PS[k][64 * hh:64 * hh + 64, 0:n]),
                                        reads=[("ps", k)], writes=[("ka", hh, bi)])
                    for hh in range(2):
                        h = 2 * pr + hh
                        dma("sp", QK(2 * hh, 0, 2048, 64, 65), cscr_p[h:h + 1, 0, :], reads=["cscr_p", ("qkc", 2 * hh)],
                            writes=[("qar", hh, 0)], key=("qar", hh))
                        dma("sp", QK(2 * hh + 1, 0, 2048, 65, 66), cscr_p[h:h + 1, 0, :], reads=["cscr_p", ("qkc", 2 * hh + 1)],
                            writes=[("kar", hh, 0)], key=("kar", hh))
                        dma("sp", QK(2 * hh + 1, 0, 2048, 66, 67), cscr_p[h:h + 1, 1, :], reads=["cscr_p2"],
                            writes=[("kar", hh, 1)], key=("kar", hh))
                        for b in range(2):
                            dma("sp", QK(2 * hh, 2048 + 16 * b, 16, 64, 65), cscr_s[h:h + 1, 0, b, 1024:1040],
                                reads=[("cscr_s", b, 0)], writes=[("qar", hh, 1 + b)], key=("qar", hh))
                            dma("sp", QK(2 * hh + 1, 2048 + 16 * b, 16, 65, 66), cscr_s[h:h + 1, 0, b, 1024:1040],
                                reads=[("cscr_s", b, 0)], writes=[("kar", hh, 2 + b)], key=("kar", hh))
                            dma("sp", QK(2 * hh + 1, 2048 + 16 * b, 16, 66, 67), cscr_s[h:h + 1, 1, b, 1024:1040],
                                reads=[("cscr_s", b, 1)], writes=[("kar", hh, 4 + b)], key=("kar", hh))
                        qar = [("qar", hh, x) for x in range(3)] + [("qkc", 2 * hh)]
                        kar = [("kar", hh, x) for x in range(6)] + [("qkc", 2 * hh + 1)]
                        for I in range(4):
                            ob = psnext("o")
                            kts = []
                            for j in range(4 * I + 4):
                                a = j - 4 * I
                                kts.append(dict(ka=QK(2 * hh + 1, 128 * j, 128, 0, 67), nk=128, va=VA(j, h),
                                                lo=(128 * a if a >= 0 else 0), hi=512, pv_lo=(128 * a if a >= 0 else 0),
                                                mask=((MASKNEG, 128) if a >= 0 else None), amask=(a >= 0), pt="rot",
                                                reads=[("ka", hh, j // 4), ("qa", hh, I)] + qar + kar,
                                                vreads=[("va", j), "va_ones"]))
                            osubs = [(PS[ob][:, 65 * s:65 * s + 65], 128 * s, 128 * s + 128, 0, 4 * I + s) for s in range(4)]
                            attend(lambda c0, n, hh=hh, I=I: QK(2 * hh, 512 * I + c0, n, 0, 67), 67, kts, "fm", ob, "fox")
                            nsq.step(lambda ob=ob, I=I, pr=pr, hh=hh: norm_store_fm(ob, I, 0, pr, 64 * hh, SM[:, 160 + 4 * I:164 + 4 * I], l))
                        ob = psnext("g")
                        kts = []
                        for b in range(2):
                            dma("pool", KAS(b, 0, 1024, 0, 64), ckT[l, b, h], reads=[("ws", 0)], writes=[("sc", b, "k")], key=("sck", b))
                            dma("pool", VAS3(b)[:, :, 0:64],
                                cv[l, b].rearrange("(j p) (h d) -> p j h d", p=128, h=8)[:, :, h, :],
                                reads=[("ws", 0)], writes=[("sc", b, "v")], key=("scv", b), slow=True)
                            dma("sp", KAS(b, 0, 1024, 65, 66), cscr_s[h:h + 1, 0, b, 0:1024], reads=[("cscr_s", b, 0), ("ws", 0)],
                                writes=[("sc", b, "r")], key=("scr", b))
                            dma("sp", KAS(b, 0, 1024, 66, 67), cscr_s[h:h + 1, 1, b, 0:1024], reads=[("cscr_s", b, 1), ("ws", 0)],
                                writes=[("sc", b, "r2")], key=("scr", b))
                            for j in range(8):
                                kts.append(dict(ka=KAS(b, 128 * j, 128), nk=128, va=VAS(b, j),
                                                lo=16 * b, hi=16 * b + 16, pv_lo=0, mask=None, pt=b,
                                                reads=[("sc", b, "k"), ("sc", b, "r"), ("sc", b, "r2"), ("sc", b, "one"), ("qa", hh, 4)] + qar,
                                                vreads=[("sc", b, "v"), ("sc", b, "one")]))
                        kts.append(dict(ka=QK(2 * hh + 1, 2048, 32, 0, 67), nk=32, va=VA(16, h, 32), lo=0, hi=32, pv_lo=0,
                                        mask=(MASKS, 32), pt=2, reads=[("ka", hh, 4), ("qa", hh, 4)] + qar + kar,
                                        vreads=[("va", 16), "va_ones"]))
                        osubs = [(PS[ob][0:32, 0:65], 0, 32, 0, len(kts) - 1)]
                        attend(lambda c0, n, hh=hh: QK(2 * hh, 2048 + c0, n, 0, 67), 67, kts, osubs, ob, "foxs")
                        norm_and_store(ob, 0, 32, 16, 0, pr, 64 * hh, SSFOX(16, 32), l)

                flush_pv()
                nsq.flush()
                _chk("fox%d" % l)
                S.op("act", lambda e: e.activation(SM[:, 100:117], SM[:, 160:177], AF.Ln, bias=EPS, scale=1.0 / 512),
                     reads=[("ssq", 0, i) for i in range(17)], writes=["rfox"])
                S.op("act", lambda e: e.activation(SM[:, 100:117], SM[:, 100:117], AF.Exp, scale=-0.5), reads=["rfox"], writes=["rfox"])
                S.op("act", lambda e: e.activation(SM[:, 140:157], SM[:, 180:197], AF.Ln, bias=EPS, scale=1.0 / 256),
                     reads=[("ssq", 2, i) for i in range(17)], writes=["rmem"])
                S.op("act", lambda e: e.activation(SM[:, 140:157], SM[:, 140:157], AF.Exp, scale=-0.5), reads=["rmem"], writes=["rmem"])

                wload(0, w_out[l][:, 0:512], 512, extra_writes=SC_TOKS)
                wload(1, w_out[l][:, 512:1024], 512)
                GP = SCRf(4, 1024)
                dma("sp", GP, g_post_mix[l:l + 1, :].broadcast_to([128, D]), writes=scr(4, 5, 6, 7), key="gp")
                for i in range(17):
                    rows = tile_rows(i)
                    c0, n = tcols(i)
                    ostg = SCRf(0, 1024, 0, rows)
                    mxr = [("mx", c, i, p0) for c in (0, 1, 2, 3, 6, 7) for p0 in (0, 64)] + [("mx", 4, i), ("mx", 5, i)]
                    last_b = None
                    for hf in range(2):
                        banks = [psnext("w") for _ in range(3)]
                        for gi, (cs, b) in enumerate(zip(((0, 1, 2, 3), (4, 5), (6, 7)), banks)):
                            mm_group(None, [(PS[b][0:rows, :], MX(c, c0, n), WS(hf, c, 0, 512), c == cs[0], c == cs[-1]) for c in cs],
                                     reads=mxr + [("ws", hf)], writes=[("ps", b)])
                        o_h = ostg[:, 512 * hf:512 * hf + 512]
                        S.op("act", lambda e, o_h=o_h, b=banks[0], rows=rows, i=i: e.activation(o_h, PS[b][0:rows, :], AF.Copy, scale=RFOX(i, rows)),
                             reads=[("ps", banks[0]), "rfox"], writes=scr(2 * hf, 2 * hf + 1))
                        S.op("dve", lambda e, o_h=o_h, b=banks[1], rows=rows, i=i: e.scalar_tensor_tensor(o_h, PS[b][0:rows, :], RSGU(i, rows), o_h, ALU.mult, ALU.add),
                             reads=[("ps", banks[1]), ("rsgu", i)] + scr(2 * hf, 2 * hf + 1), writes=scr(2 * hf, 2 * hf + 1))
                        S.op("dve", lambda e, o_h=o_h, b=banks[2], rows=rows, i=i: e.scalar_tensor_tensor(o_h, PS[b][0:rows, :], RMEM(i, rows), o_h, ALU.mult, ALU.add),
                             reads=[("ps", banks[2]), "rmem"] + scr(2 * hf, 2 * hf + 1), writes=scr(2 * hf, 2 * hf + 1))
                        last_b = banks
                    post_norm_residual(ostg, rows, i, bf(WS_O[2], 1024, 0, rows), [("ws", 2)], GP, scr(4, 5, 6, 7), scr(0, 1, 2, 3))

                _chk("wout%d" % l)
                S.barrier()
                ffn_phase(l)
                S.barrier()

        except _Stop:
            pass
        for i in range(16):
            dma("sp", y_p[128 * i:128 * i + 128, :], X[:, i, :], reads=[("x", i)], key=("x", i))
        dma("sp", y_s, X[0:32, 16, :], reads=[("x", 16)], key=("x", 16))
        S.emit(st)
    return nc


_NC_CACHE = {}


def _prep_inputs(inp):
    f = lambda a: np.ascontiguousarray(np.asarray(a, dtype=np.float32))
    x_prompt = f(inp["x_prompt"]); x_sample = f(inp["x_sample"]); mem_prompt = f(inp["mem_prompt"])
    cfk = f(inp["cache_fox_k"]); cfv = f(inp["cache_fox_v"]); clf = f(inp["cache_fox_logf"])
    cmk = f(inp["cache_mem_k"]); cmvv = f(inp["cache_mem_v"]); cconv = f(inp["cache_ffn_conv"])
    ckT_all = np.ascontiguousarray(cfk.transpose(0, 1, 3, 4, 2))
    cv_all = cfv.reshape(L, 16, 1024, 512)
    clfT_all = np.ascontiguousarray(clf.transpose(0, 1, 3, 2))
    cmkT_all = np.ascontiguousarray(cmk.transpose(0, 1, 3, 4, 2))
    cmv_all = cmvv.reshape(L, 16, 256, 256)
    wsT = np.ascontiguousarray(f(inp["w_spatial"]).transpose(0, 1, 3, 2))
    fm8 = lambda g: f(g).reshape(L, 8, 128).transpose(0, 2, 1)
    b_sp = f(inp["b_spatial"])
    w_dw = f(inp["w_dwconv"]); b_dw = f(inp["b_dwconv"]); b_f = f(inp["b_forget"])
    cst = np.zeros((128, 288), dtype=ml_dtypes.bfloat16)
    cst[:, 0:128] = np.eye(128, dtype=np.float32).astype(ml_dtypes.bfloat16)
    kk, qq = np.meshgrid(np.arange(128), np.arange(128), indexing="ij")
    cst[:, 128:256] = np.where(kk <= qq, 0.0, NEG).astype(ml_dtypes.bfloat16)
    k2, q2 = np.meshgrid(np.arange(32), np.arange(32), indexing="ij")
    ok = (k2 // 16 == q2 // 16) & (k2 % 16 <= q2 % 16)
    cst[0:32, 256:288] = np.where(ok, 0.0, NEG).astype(ml_dtypes.bfloat16)
    wu = f(inp["w_up"])
    wu_a = wu[:, :, 0:FF].reshape(L, 8, 128, NJ, 128)
    wu_l = wu[:, :, FF:2 * FF].reshape(L, 8, 128, NJ, 128)
    w_up_r = np.ascontiguousarray(np.concatenate([wu_a, wu_l], axis=4).transpose(0, 3, 2, 1, 4))
    shared = dict(
        w_in=f(inp["w_in"]), w_mkv=f(inp["w_mem_kv"]), w_out=f(inp["w_out"]), w_up=w_up_r,
        w_dn=f(inp["w_down"]), g_post_mix=f(inp["g_post_mix"]), g_post_ffn=f(inp["g_post_ffn"]),
        g_sgu=f(inp["g_sgu"]), wsT=wsT, cst=cst)
    gpm = fm8(inp["g_pre_mix"]); ggo = fm8(inp["g_group_out"]); gmem = fm8(inp["g_mem"]); gpf = fm8(inp["g_pre_ffn"])
    in_maps = []
    for c in range(NCORES):
        prm = np.zeros((128, NPRM), dtype=np.float32)
        for l in range(L):
            o = l * PPL
            prm[:, o + PO["gpm"]:o + PO["gpm"] + 8] = gpm[l]
            prm[:, o + PO["ggo"]:o + PO["ggo"] + 8] = ggo[l]
            prm[:, o + PO["gmem"]:o + PO["gmem"] + 8] = gmem[l]
            prm[:, o + PO["gpf"]:o + PO["gpf"] + 8] = gpf[l]
            prm[:, o + PO["bs"]:o + PO["bs"] + 4] = b_sp[l].T
            prm[0:16, o + PO["bss"]:o + PO["bss"] + 4] = b_sp[l][:, 0:16].T
            prm[16:32, o + PO["bss"]:o + PO["bss"] + 4] = b_sp[l][:, 0:16].T
            prm[:, o + PO["wdw"]:o + PO["wdw"] + 66] = w_dw[l].reshape(3, NJ, 128).transpose(2, 1, 0).reshape(128, 66)
            prm[:, o + PO["bdw"]:o + PO["bdw"] + 22] = b_dw[l].reshape(NJ, 128).T
            prm[0:8, o + PO["bf"]] = b_f[l]
            cc = cconv[l, 2 * c:2 * c + 2]
            prm[:, o + PO["cconv"]:o + PO["cconv"] + 88] = cc.reshape(2, 2, NJ, 128).transpose(3, 2, 0, 1).reshape(128, 88)
        m = dict(shared)
        m.update(
            x_p=x_prompt[c], x_s=x_sample[2 * c:2 * c + 2].reshape(32, D), mem=mem_prompt[c],
            ckT=np.ascontiguousarray(ckT_all[:, 2 * c:2 * c + 2]), cv=np.ascontiguousarray(cv_all[:, 2 * c:2 * c + 2]),
            clfT=np.ascontiguousarray(clfT_all[:, 2 * c:2 * c + 2]), cmkT=np.ascontiguousarray(cmkT_all[:, 2 * c:2 * c + 2]),
            cmv=np.ascontiguousarray(cmv_all[:, 2 * c:2 * c + 2]), prm=prm)
        in_maps.append(m)
    return in_maps


def kernel(**inp):
    in_maps = _prep_inputs(inp)
    if "nc" not in _NC_CACHE:
        _NC_CACHE["nc"] = build()
    nc = _NC_CACHE["nc"]
    res = run_bass_kernel_spmd(nc, in_maps, core_ids=list(range(NCORES)))
    R = res.results
    cat = lambda name: np.stack([np.asarray(r[name], dtype=np.float32) for r in R], axis=0)
    y_p = cat("y_p")
    y_s = cat("y_s").reshape(16, 16, D)
    fk = cat("ofk").transpose(1, 0, 2, 3).reshape(L, 8, 2048, 8, 64)
    fv = cat("ofv").transpose(1, 0, 2, 3).reshape(L, 8, 2048, 8, 64)
    lf = cat("olf").transpose(1, 0, 3, 2)
    mk = cat("omk").transpose(1, 0, 2, 3).reshape(L, 8, 256, 4, 64)
    mv = cat("omv").transpose(1, 0, 2, 3).reshape(L, 8, 256, 4, 64)
    cvp = cat("oconv").transpose(1, 0, 4, 3, 2).reshape(L, 8, 2, FF)
    sk = cat("osk").transpose(1, 0, 2, 3).reshape(L, 16, 16, 8, 64)
    sv = cat("osv").transpose(1, 0, 2, 3).reshape(L, 16, 16, 8, 64)
    slf = cat("oslf").transpose(1, 0, 3, 2).reshape(L, 16, 16, 8)
    gv = cat("ogv").transpose(1, 0, 2, 3).reshape(L, 16, 16, 256)
    cvs = cat("osconv").transpose(1, 0, 4, 5, 3, 2).reshape(L, 16, 2, FF)
    outs = (y_p, y_s, fk, fv, lf, mk, mv, cvp, sk, sv, slf, gv, cvs)
    return tuple(np.ascontiguousarray(o, dtype=np.float32) for o in outs)
```

```python
import numpy as np
import ml_dtypes
from contextlib import ExitStack
import concourse.bass as bass
import concourse.mybir as mybir
from concourse.bass_utils import run_bass_kernel_spmd

F32 = mybir.dt.float32
BF16 = mybir.dt.bfloat16
ALU = mybir.AluOpType
AF = mybir.ActivationFunctionType

ENGS = ("pe", "act", "dve", "pool", "sp")
STOP = None
DBG = set()


class _Stop(Exception):
    pass


def _chk(name):
    if STOP == name:
        raise _Stop()
NCORES = 8
L = 2
D = 1024
NT = 2080
FF = 2816
NJ = 22
EPS = 1e-6
NEG = -30000.0


class Op:
    __slots__ = ("eng", "fn", "idx", "deps", "dma_key", "marked", "seq", "cum")

    def __init__(self, eng, fn, idx, dma_key):
        self.eng = eng
        self.fn = fn
        self.idx = idx
        self.deps = ()
        self.dma_key = dma_key
        self.marked = False
        self.seq = 0
        self.cum = 0


class Sched:
    def __init__(self, nc):
        self.nc = nc
        self.streams = {e: [] for e in ENGS}
        self.last_writer = {}
        self.readers = {}
        self.dma_counts = {}
        self.dma_last = {}

    def op(self, eng, fn, reads=(), writes=(), dma_key=None):
        o = Op(eng, fn, len(self.streams[eng]), dma_key)
        deps = set()
        lw = self.last_writer
        rd = self.readers
        for t in reads:
            w = lw.get(t)
            if w is not None:
                deps.add(w)
            if type(t) is tuple and t[0] == "ps":
                for r in rd.get(t, ()):
                    if r.eng != eng:
                        deps.add(r)
        for t in writes:
            w = lw.get(t)
            if w is not None:
                deps.add(w)
            r = rd.get(t)
            if r:
                deps.update(r)
        for t in reads:
            rd.setdefault(t, []).append(o)
        for t in writes:
            lw[t] = o
            rd[t] = []
        deps.discard(o)
        if eng == "pe":
            deps = {d for d in deps if not (d.eng == "pe" and d.dma_key is None)}
        o.deps = deps
        if dma_key is not None:
            c = self.dma_counts.get(dma_key, 0) + 1
            self.dma_counts[dma_key] = c
            o.cum = 16 * c
            self.dma_last[dma_key] = o
        self.streams[eng].append(o)
        return o

    def barrier(self):
        lasts = [s[-1] for s in self.streams.values() if s]
        lasts += list(self.dma_last.values())
        for e in ENGS:
            o = Op(e, lambda eng: eng.nop(), len(self.streams[e]), None)
            o.deps = {d for d in lasts if not (d.eng == e and d.dma_key is None)}
            self.streams[e].append(o)

    def emit(self, stack):
        nc = self.nc
        for e in ENGS:
            for o in self.streams[e]:
                for d in o.deps:
                    if d.dma_key is None:
                        d.marked = True
        sems = {}
        for e in ENGS:
            n = 0
            for o in self.streams[e]:
                if o.dma_key is None and o.marked:
                    n += 1
                    o.seq = n
            if n:
                sems[e] = stack.enter_context(nc.semaphore("s_" + e))
        dsems = {}
        for k in self.dma_counts:
            dsems[k] = stack.enter_context(nc.semaphore("d%d" % len(dsems)))
        self.n_sems = len(sems) + len(dsems)
        final = {k: o.cum for k, o in self.dma_last.items()}

        def run(eng_name, eng):
            waited = {}
            for o in self.streams[eng_name]:
                need = {}
                for d in o.deps:
                    if d.dma_key is not None:
                        key = ("d", d.dma_key)
                        val = d.cum
                    else:
                        key = ("c", d.eng)
                        val = d.seq
                    if val > need.get(key, 0):
                        need[key] = val
                for key, val in need.items():
                    if waited.get(key, 0) >= val:
                        continue
                    waited[key] = val
                    s = dsems[key[1]] if key[0] == "d" else sems[key[1]]
                    eng.wait_ge(s, val)
                ins = o.fn(eng)
                if o.dma_key is not None:
                    ins.then_inc(dsems[o.dma_key], 16)
                elif o.marked:
                    ins.then_inc(sems[o.eng], 1)
            if eng_name == "sp":
                for k, v in final.items():
                    if waited.get(("d", k), 0) < v:
                        eng.wait_ge(dsems[k], v)

        with nc.Block() as block:
            @block.tensor
            def _(t):
                run("pe", t)

            @block.scalar
            def _(t):
                run("act", t)

            @block.vector
            def _(t):
                run("dve", t)

            @block.gpsimd
            def _(t):
                run("pool", t)

            @block.sync
            def _(t):
                run("sp", t)


def tile_rows(i):
    return 128 if i < 16 else 32


def tcols(i):
    return (128 * i, 128) if i < 16 else (2048, 32)


BLKS = [(0, 512), (512, 512), (1024, 512), (1536, 512), (2048, 32)]

PO = {}
_o = 0
for _n, _w in (("gpm", 8), ("ggo", 8), ("gmem", 8), ("gpf", 8), ("bs", 4), ("bss", 4),
               ("wdw", 66), ("bdw", 22), ("bf", 1), ("cconv", 88)):
    PO[_n] = _o
    _o += _w
PPL = _o
NPRM = PPL * L


def build():
    nc = bass.Bass("TRN2", target_bir_lowering=False)

    def din(name, shape, dt=F32):
        return nc.dram_tensor(name, list(shape), dt, kind="ExternalInput").ap()

    def dout(name, shape):
        return nc.dram_tensor(name, list(shape), F32, kind="ExternalOutput").ap()

    x_p = din("x_p", [2048, D])
    x_s = din("x_s", [32, D])
    mem = din("mem", [256, D])
    ckT = din("ckT", [L, 2, 8, 64, 1024])
    cv = din("cv", [L, 2, 1024, 512])
    clfT = din("clfT", [L, 2, 8, 1024])
    cmkT = din("cmkT", [L, 2, 4, 64, 256])
    cmv = din("cmv", [L, 2, 256, 256])
    w_in = din("w_in", [L, D, 2312])
    w_mkv = din("w_mkv", [L, D, 512])
    w_out = din("w_out", [L, D, D])
    w_up = din("w_up", [L, NJ, 128, 8, 256])
    w_dn = din("w_dn", [L, FF, D])
    g_post_mix = din("g_post_mix", [L, D])
    g_post_ffn = din("g_post_ffn", [L, D])
    g_sgu = din("g_sgu", [L, 256])
    wsT = din("wsT", [L, 4, 128, 128])
    prm_h = din("prm", [128, NPRM])
    cst_h = din("cst", [128, 288], BF16)

    y_p = dout("y_p", [2048, D])
    y_s = dout("y_s", [32, D])
    ofk = dout("ofk", [L, 2048, 512])
    ofv = dout("ofv", [L, 2048, 512])
    olf = dout("olf", [L, 8, 2048])
    omk = dout("omk", [L, 256, 256])
    omv = dout("omv", [L, 256, 256])
    oconv = dout("oconv", [L, 128, NJ, 2])
    osk = dout("osk", [L, 32, 512])
    osv = dout("osv", [L, 32, 512])
    oslf = dout("oslf", [L, 8, 32])
    ogv = dout("ogv", [L, 32, 256])
    osconv = dout("osconv", [L, 128, NJ, 2, 2])

    cscr_p = nc.dram_tensor("cscr_p", [8, 2, 2048], BF16).ap()
    cscr_s = nc.dram_tensor("cscr_s", [8, 2, 2, 1040], BF16).ap()

    with ExitStack() as st:
        S = Sched(nc)
        X = st.enter_context(nc.sbuf_tensor("X", [128, 17, D], F32))
        PRM = st.enter_context(nc.sbuf_tensor("PRM", [128, NPRM], F32))
        CST = st.enter_context(nc.sbuf_tensor("CST", [128, 288], BF16))
        WSTSt = st.enter_context(nc.sbuf_tensor("WSTS", [32, L * 4 * 32], BF16))
        PTSt = st.enter_context(nc.sbuf_tensor("PTS", [128, 3 * 32], BF16))
        SM = st.enter_context(nc.sbuf_tensor("SM", [128, 256], F32))
        ONEROW = st.enter_context(nc.sbuf_tensor("ONEROW", [128, 64], F32))
        remaining = nc.sbuf_bytes_remaining
        remaining = remaining() if callable(remaining) else remaining
        ARB = (remaining - 512) // 64 * 64
        AR = st.enter_context(nc.sbuf_tensor("AR", [128, ARB // 2], BF16))
        AR32 = AR.bitcast(F32)
        PS = [st.enter_context(nc.psum_tensor("ps%d" % k, [128, 512], F32)) for k in range(8)]
        PSB = [p.bitcast(BF16) for p in PS]

        IDENT = CST[:, 0:128]
        MASKNEG = CST[:, 128:256]
        MASKS = CST[0:32, 256:288]


        def WSTS(l, g):
            o = (l * 4 + g) * 32
            return WSTSt[0:32, o:o + 32]

        def prm(l, name, w, p0=0, p1=128, off=0):
            o = l * PPL + PO[name] + off
            return PRM[p0:p1, o:o + w]

        SM_SS, SM_SD, SM_RSTD = 0, 1, 2
        SM_R = 16
        SM_SSF = 70
        SM_T = 90

        def bf(off, n, p0=0, p1=128):
            return AR[p0:p1, off // 2: off // 2 + n]

        def f32(off, n, p0=0, p1=128):
            return AR32[p0:p1, off // 4: off // 4 + n]

        cur = [0]

        def carve(nbytes):
            o = cur[0]
            cur[0] = o + (nbytes + 63) // 64 * 64
            return o

        HT_O = carve(8 * NT * 2)
        MX_O = carve(8 * NT * 2)
        VA_O = carve(17 * 8 * 65 * 2)
        QK_O = carve(4 * NT * 2)
        WS_O = [carve(8320) for _ in range(3)]
        SCR_O = carve(8192)
        MKT_O = carve(2048)
        MVA_O = carve(1040)
        MKTt = AR[:, MKT_O // 2:MKT_O // 2 + 1024]
        MVAt = AR[:, MVA_O // 2:MVA_O // 2 + 520]
        MIX_END = cur[0]
        assert MIX_END <= ARB, (MIX_END, ARB)

        def HT(c, c0, n):
            return bf(HT_O + (c * NT + c0) * 2, n)

        def HT3(c0, n):
            return AR[:, HT_O // 2: HT_O // 2 + 8 * NT].rearrange("p (c t) -> p c t", c=8)[:, :, c0:c0 + n]

        def MX(c, c0, n, p0=0, p1=128):
            return bf(MX_O + (c * NT + c0) * 2, n, p0, p1)

        def VA(j, h, nk=128):
            return bf(VA_O + ((j * 8 + h) * 65) * 2, 65, 0, nk)

        def QK(slot, c0, n, p0, p1):
            return bf(QK_O + (slot * NT + c0) * 2, n, p0, p1)

        CT = lambda c0, n: f32(QK_O + c0 * 4, n, 96, 104)
        HI = lambda c0, n: bf(QK_O + 8320 + c0 * 2, n, 96, 104)
        LO = lambda c0, n: bf(QK_O + 8320 + 4160 + c0 * 2, n, 96, 104)

        def WS(s, c, c0, n):
            return bf(WS_O[s] + (c * 520 + c0) * 2, n)

        def WS3(s, ncol):
            return AR[:, WS_O[s] // 2: WS_O[s] // 2 + 8 * 520].rearrange("p (c n) -> p c n", c=8)[:, :, 0:ncol]

        def SCRb(g, n, p0=0, p1=128, off=0):
            return bf(SCR_O + g * 1024 + off * 2, n, p0, p1)

        def SCRf(g, n, p0=0, p1=128, off=0):
            return f32(SCR_O + g * 1024 + off * 4, n, p0, p1)

        def scr(*gs):
            return [("scr", g) for g in gs]

        def dma(eng, out, in_, reads=(), writes=(), key=None, slow=False):
            if slow:
                fn = lambda e: e.dma_start(out=out, in_=in_, allow_slow_non_contiguous=True)
            else:
                fn = lambda e: e.dma_start(out=out, in_=in_)
            return S.op(eng, fn, reads=reads, writes=writes, dma_key=key)

        psr = {"g": [0, 1], "s": [2, 3, 4], "o": [5, 6], "t": [7]}
        psi = {k: 0 for k in psr}

        def psnext(pool):
            lst = psr[pool]
            k = lst[psi[pool] % len(lst)]
            psi[pool] += 1
            return k

        def mm_group(out_fn, items, reads, writes):
            def fn(e):
                ins = None
                for (o, a, b, s0, s1) in items:
                    ins = e.matmul(o, a, b, start=s0, stop=s1, skip_group_check=True)
                return ins
            return S.op("pe", fn, reads=reads, writes=writes)

        dma("sp", PRM[:, :], prm_h, writes=["prm"], key="prm")
        dma("sp", CST[:, :], cst_h, writes=["cst"], key="cst")
        S.op("dve", lambda e: e.memset(WSTSt[:, :], 0.0), writes=["wsts"])
        for l in range(L):
            for b in range(2):
                dma("pool",
                    WSTSt[16 * b:16 * b + 16, l * 128:(l + 1) * 128].rearrange("p (g i) -> p g i", g=4)[:, :, 16 * b:16 * b + 16],
                    wsT[l, :, 0:16, 0:16].rearrange("g j i -> j g i"),
                    reads=["wsts"], writes=[("wsts", l, b)], key=("wsts", l), slow=True)
        S.op("dve", lambda e: e.memset(PTSt[:, :], 0.0), writes=["pts0", "pts1", "pts2"])
        S.op("dve", lambda e: e.memset(SM[:, 200:201], 1.0), writes=["one"])
        S.op("dve", lambda e: e.memset(ONEROW[:, :], 1.0), writes=["onerow"])
        ONE8 = SM[96:104, 200:201]

        for i in range(16):
            dma("sp", X[:, i, :], x_p[128 * i:128 * i + 128, :], writes=[("x", i)], key=("x", i))
        dma("sp", X[0:32, 16, :], x_s, writes=[("x", 16)], key=("x", 16))

        def prenorm_tile(src, rows, xtok, gcol, dst3, dst_tok, slot, extra_writes=(), hb=None, tpool="t", hb_toks=None, defer=False):
            if hb is None:
                hb = SCRb(2 * slot, 1024, 0, rows)
            hbt = list(hb_toks) if hb_toks is not None else scr(2 * slot, 2 * slot + 1)
            c0 = 3 * slot
            ss = SM[0:rows, c0:c0 + 1]
            sd = SM[0:rows, c0 + 1:c0 + 2]
            rs = SM[0:rows, c0 + 2:c0 + 3]
            S.op("act", lambda e: e.activation(hb, src, AF.Square, accum_out=ss),
                 reads=[xtok], writes=hbt + [("sm", c0)])
            S.op("act", lambda e: e.activation(sd, ss, AF.Ln, bias=EPS, scale=1.0 / D),
                 reads=[("sm", c0)], writes=[("sm", c0 + 1)])
            S.op("act", lambda e: e.activation(rs, sd, AF.Exp, scale=-0.5), reads=[("sm", c0 + 1)], writes=[("sm", c0 + 2)])
            S.op("act", lambda e: e.activation(hb, src, AF.Copy, scale=rs),
                 reads=[xtok, ("sm", c0 + 2)], writes=hbt)
            def part_b():
                k = psnext(tpool)

                def tr(e):
                    ins = None
                    for c in range(8):
                        ins = e.transpose(PSB[k][:, c * 128:c * 128 + rows], hb[:, c * 128:(c + 1) * 128],
                                          IDENT[0:rows, 0:rows])
                    return ins
                S.op("pe", tr, reads=hbt + ["cst"], writes=[("ps", k)])
                src3 = PSB[k][:, 0:1024].rearrange("p (c t) -> p c t", c=8)[:, :, 0:rows]
                S.op("dve", lambda e: e.tensor_tensor(dst3, src3, gcol.unsqueeze(2).to_broadcast([128, 8, rows]), ALU.mult),
                     reads=[("ps", k), "prm"], writes=[dst_tok] + list(extra_writes))
            if defer:
                return part_b
            part_b()

        PVQ = []

        def flush_pv():
            while PVQ:
                PVQ.pop(0)()

        def attend(qa_fn, K, ktiles, osubs, obank, tag):
            pendq = []
            first_done = [False]

            def pv(kt_i, kt, pt_ap, pt_tok):
                items = []
                if osubs == "fm":
                    lo_, hi_ = kt["lo"], kt["hi"]
                    items.append((PS[obank][0:65, lo_:hi_], kt["va"], pt_ap[:, lo_:hi_], kt_i == 0, kt_i == len(ktiles) - 1))
                    mm_group(None, items, reads=[pt_tok] + kt["vreads"], writes=[("ps", obank)])
                    return
                for (o_ap, c0, c1, fk, lk) in osubs:
                    if fk <= kt_i <= lk and kt["pv_lo"] <= c0:
                        items.append((o_ap, pt_ap[:, c0:c1], kt["va"], not first_done[0], kt_i == lk))
                        first_done[0] = True
                if items:
                    mm_group(None, items, reads=[pt_tok] + kt["vreads"], writes=[("ps", obank)])

            for kt_i, kt in enumerate(ktiles):
                k = psnext("s")
                nk, lo, hi = kt["nk"], kt["lo"], kt["hi"]
                items = []
                if kt["mask"] is not None:
                    mask_ap, mc = kt["mask"]
                    items.append((PS[k][0:nk, lo:lo + mc], kt["ka"], qa_fn(lo, mc), True, False))
                    items.append((PS[k][0:nk, lo:lo + mc], IDENT[0:nk, 0:nk], mask_ap, False, True))
                    if hi > lo + mc:
                        items.append((PS[k][0:nk, lo + mc:hi], kt["ka"], qa_fn(lo + mc, hi - lo - mc), True, True))
                else:
                    items.append((PS[k][0:nk, lo:hi], kt["ka"], qa_fn(lo, hi - lo), True, True))
                mm_group(None, items, reads=kt["reads"] + ["cst"], writes=[("ps", k)])
                if kt["pt"] == "rot":
                    g = attend.ptc % 4
                    attend.ptc += 1
                    pt_ap = SCRb(g, 512, 0, nk)
                    pt_tok = ("scr", g)
                else:
                    pt_ap = PTSt[0:nk, 32 * kt["pt"]:32 * kt["pt"] + 32]
                    pt_tok = "pts%d" % kt["pt"]
                S.op("act", lambda e, pt_ap=pt_ap, k=k, nk=nk, lo=lo, hi=hi:
                     e.activation(pt_ap[:, lo:hi], PS[k][0:nk, lo:hi], AF.Exp),
                     reads=[("ps", k)], writes=[pt_tok])
                if kt["pt"] == "rot":
                    if len(PVQ) >= 2:
                        PVQ.pop(0)()
                    PVQ.append(lambda kt_i=kt_i, kt=kt, pt_ap=pt_ap, pt_tok=pt_tok: pv(kt_i, kt, pt_ap, pt_tok))
                else:
                    flush_pv()
                    pv(kt_i, kt, pt_ap, pt_tok)
        attend.ptc = 0

        def cs_all():
            return ["cw"]

        def post_norm_residual(ostg, rows, i, junk, junk_toks, GPap, gp_toks, o_toks):
            ssc = SM[0:rows, SM_T + 8:SM_T + 9]
            S.op("act", lambda e: e.activation(junk, ostg, AF.Square, accum_out=ssc),
                 reads=o_toks, writes=list(junk_toks) + ["pn"])
            S.op("act", lambda e: e.activation(ssc, ssc, AF.Ln, bias=EPS, scale=1.0 / D), reads=["pn"], writes=["pn"])
            S.op("act", lambda e: e.activation(ssc, ssc, AF.Exp, scale=-0.5), reads=["pn"], writes=["pn"])
            S.op("dve", lambda e: e.scalar_tensor_tensor(ostg, ostg, ssc, GPap[0:rows, :], ALU.mult, ALU.mult),
                 reads=list(o_toks) + list(gp_toks) + ["pn"], writes=o_toks)
            S.op("pool", lambda e: e.tensor_tensor(X[0:rows, i, :], X[0:rows, i, :], ostg, ALU.add),
                 reads=list(o_toks) + [("x", i)], writes=[("x", i)])

        def ffn_phase(l):
            psr.clear()
            psr.update({"u": [0, 1, 2, 4, 5, 6], "t": [3], "d": [4, 5, 6, 7]})
            for kk in psr:
                psi[kk] = 0
            HC = 1056
            o = [0]

            def cv_(nb):
                r = o[0]
                o[0] = r + (nb + 63) // 64 * 64
                return r
            H2_O = cv_(8 * HC * 2)
            ACT_O = cv_(NJ * HC * 2)
            WDN_O = cv_(NJ * 1024 * 2)
            WU_O = [cv_(4096) for _ in range(2)]
            AE_O = [cv_(2064) for _ in range(2)]
            LIN_O = [cv_(1024) for _ in range(2)]
            CONV_O = [cv_(2048) for _ in range(2)]
            OSTG_O = cv_(4096)
            GPF_O = cv_(4096)
            HBF_O = cv_(2048)
            SAVE_O = cv_(NJ * 2 * 4)
            SAVES_O = cv_(NJ * 4 * 4)
            assert o[0] <= ARB, (o[0], ARB)

            H2 = lambda c, c0, n: bf(H2_O + (c * HC + c0) * 2, n)
            H23 = lambda c0, n: AR[:, H2_O // 2:H2_O // 2 + 8 * HC].rearrange("p (c t) -> p c t", c=8)[:, :, c0:c0 + n]
            ACTB = lambda j, c0, n: bf(ACT_O + (j * HC + c0) * 2, n)
            WDN = lambda j, c0, n: bf(WDN_O + (j * 1024 + c0) * 2, n)
            WDN3 = AR[:, WDN_O // 2:WDN_O // 2 + NJ * 1024].rearrange("p (j n) -> p j n", j=NJ)
            WU = lambda s, c, c0, n: bf(WU_O[s] + (c * 256 + c0) * 2, n)
            WU3 = lambda s: AR[:, WU_O[s] // 2:WU_O[s] // 2 + 2048].rearrange("p (c n) -> p c n", c=8)
            OSTG = f32(OSTG_O, 1024)
            GPF = f32(GPF_O, 1024)
            HBF = bf(HBF_O, 1024)
            SAVE = f32(SAVE_O, NJ * 2)
            SAVES = f32(SAVES_O, NJ * 4)

            dma("sp", GPF, g_post_ffn[l:l + 1, :].broadcast_to([128, D]), writes=["gpf"], key="gp")
            wdn_src = w_dn[l].rearrange("(j p) n -> p j n", p=128)
            for part in range(2):
                dma("pool", WDN3[:, 11 * part:11 * part + 11, :], wdn_src[:, 11 * part:11 * part + 11, :],
                    writes=[("wdn", part)], key=("wdn", part))
            aecnt = 0

            def wu_load(j, s_):
                dma("pool", WU3(s_), w_up[l, j], writes=[("wu", s_, 0), ("wu", s_, 1)], key=("wu", s_, 0))

            for half in range(2):
                tiles = list(range(0, 8)) if half == 0 else list(range(8, 17))
                base = 0 if half == 0 else 1024
                blocks = [(0, 512), (512, 512)] + ([(1024, 32)] if half == 1 else [])
                h2toks = [("h2", (i if i < 8 else i - 8)) for i in tiles]
                if half == 0:
                    for i in tiles:
                        rows = tile_rows(i)
                        c0, n = tcols(i)
                        prenorm_tile(X[0:rows, i, :], rows, ("x", i), prm(l, "gpf", 8), H23(c0 - base, n), ("h2", (i if i < 8 else i - 8)), 0,
                                     hb=HBF[0:rows, :], tpool="t", hb_toks=["hbf"])
                wu_load(0, 0)
                pend_tail = None
                for j in range(NJ):
                    s = j % 2
                    if j + 1 < NJ:
                        wu_load(j + 1, (j + 1) % 2)
                    w0 = prm(l, "wdw", 1, 0, 128, 3 * j)
                    w1 = prm(l, "wdw", 1, 0, 128, 3 * j + 1)
                    w2 = prm(l, "wdw", 1, 0, 128, 3 * j + 2)
                    bd = prm(l, "bdw", 1, 0, 128, j)
                    for bidx, (lc0, n) in enumerate(blocks):
                        sample = (n == 32)
                        ka = psnext("u")
                        mm_group(None, [(PS[ka][:, 0:n], WU(s, c, 0, 128), H2(c, lc0, n), c == 0, c == 7) for c in range(8)],
                                 reads=h2toks + [("wu", s, 0)], writes=[("ps", ka)])
                        kl = psnext("u")
                        mm_group(None, [(PS[kl][:, 0:n], WU(s, c, 128, 128), H2(c, lc0, n), c == 0, c == 7) for c in range(8)],
                                 reads=h2toks + [("wu", s, 1)], writes=[("ps", kl)])
                        p = aecnt % 2
                        aecnt += 1
                        lin = bf(LIN_O[p], 512)
                        conv = f32(CONV_O[p], 512)
                        if sample:
                            AEs = f32(AE_O[p], 36).rearrange("p (b t) -> p b t", b=2)
                            S.op("dve", lambda e, AEs=AEs, j=j: e.tensor_copy(
                                AEs[:, :, 0:2], prm(l, "cconv", 4, 0, 128, 4 * j).rearrange("p (b r) -> p b r", b=2)),
                                reads=["prm"], writes=[("aeh", p)])
                            S.op("act", lambda e, AEs=AEs, ka=ka: e.copy(AEs[:, :, 2:18], PS[ka][:, 0:32].rearrange("p (b t) -> p b t", b=2)),
                                 reads=[("ps", ka)], writes=[("ae", p)])
                            taps = [AEs[:, :, k:k + 16] for k in range(3)]
                            cv3 = conv[:, 0:32].rearrange("p (b t) -> p b t", b=2)
                            psa = PS[ka][:, 0:32].rearrange("p (b t) -> p b t", b=2)
                            sil_out = f32(AE_O[p] + 256, 32)
                            sil_in = conv[:, 0:32]
                        else:
                            AEp = f32(AE_O[p], 514)
                            if bidx == 0 and half == 0:
                                S.op("dve", lambda e, AEp=AEp: e.memset(AEp[:, 0:2], 0.0), writes=[("aeh", p)])
                            elif bidx == 0:
                                S.op("dve", lambda e, AEp=AEp, j=j: e.tensor_copy(AEp[:, 0:2], SAVE[:, 2 * j:2 * j + 2]),
                                     reads=[("save", j)], writes=[("aeh", p)])
                            else:
                                prev = f32(AE_O[1 - p], 514)
                                S.op("dve", lambda e, AEp=AEp, prev=prev: e.tensor_copy(AEp[:, 0:2], prev[:, 512:514]),
                                     reads=[("aet", 1 - p)], writes=[("aeh", p)])
                            S.op("act", lambda e, AEp=AEp, ka=ka: e.copy(AEp[:, 2:514], PS[ka][:, 0:512]),
                                 reads=[("ps", ka)], writes=[("ae", p), ("aet", p)])
                            taps = [AEp[:, k:k + 512] for k in range(3)]
                            cv3 = conv
                            psa = PS[ka][:, 0:512]
                            sil_out = AEp[:, 0:512]
                            sil_in = conv
                        S.op("act", lambda e, cv3=cv3, psa=psa, w2=w2, bd=bd: e.activation(cv3, psa, AF.Identity, bias=bd, scale=w2),
                             reads=[("ps", ka), "prm"], writes=[("conv", p)])
                        S.op("act", lambda e, lin=lin, kl=kl, n=n: e.copy(lin[:, 0:n], PS[kl][:, 0:n]),
                             reads=[("ps", kl)], writes=[("lin", p)])
                        S.op("dve", lambda e, cv3=cv3, taps=taps, w1=w1: e.scalar_tensor_tensor(cv3, taps[1], w1, cv3, ALU.mult, ALU.add),
                             reads=[("ae", p), ("aeh", p), ("conv", p)], writes=[("conv", p)])
                        S.op("dve", lambda e, cv3=cv3, taps=taps, w0=w0: e.scalar_tensor_tensor(cv3, taps[0], w0, cv3, ALU.mult, ALU.add),
                             reads=[("ae", p), ("aeh", p), ("conv", p)], writes=[("conv", p)])
                        if not sample and bidx == 1:
                            S.op("dve", lambda e, AEp=AEp, j=j: e.tensor_copy(SAVE[:, 2 * j:2 * j + 2], AEp[:, 512:514]),
                                 reads=[("aet", p)], writes=[("save", j)])
                        if sample:
                            S.op("dve", lambda e, AEs=AEs, j=j: e.tensor_copy(
                                SAVES[:, 4 * j:4 * j + 4].rearrange("p (b r) -> p b r", b=2), AEs[:, :, 16:18]),
                                reads=[("ae", p)], writes=[("saves", j)])

                        def tail(p=p, sil_out=sil_out, sil_in=sil_in, j=j, lc0=lc0, n=n, lin=lin, half=half):
                            S.op("act", lambda e: e.activation(sil_out, sil_in, AF.Silu),
                                 reads=[("conv", p), ("ae", p), ("aeh", p)], writes=[("ae", p), ("aeh", p)])
                            S.op("pool", lambda e: e.tensor_tensor(ACTB(j, lc0, n), sil_out, lin[:, 0:n], ALU.mult),
                                 reads=[("ae", p), ("aeh", p), ("lin", p)], writes=[("actb", j, half)])
                        if pend_tail is not None:
                            pend_tail()
                        pend_tail = tail
                if pend_tail is not None:
                    pend_tail()
                    pend_tail = None
                if half == 1:
                    dma("sp", oconv[l], SAVE.rearrange("p (j r) -> p j r", r=2), reads=[("save", j) for j in range(NJ)], key="oconv")
                    dma("sp", osconv[l], SAVES.rearrange("p (j b r) -> p j b r", b=2, r=2), reads=[("saves", j) for j in range(NJ)], key="osconv")
                atoks = [("actb", j, half) for j in range(NJ)]
                nxt = list(range(8, 17)) if half == 0 else []
                for ti_, i in enumerate(tiles):
                    rows = tile_rows(i)
                    c0, n = tcols(i)
                    lc0 = c0 - base
                    defer_b = []
                    for i2 in (nxt[ti_:ti_ + 1] if ti_ < 7 else nxt[7:8]):
                        rows2 = tile_rows(i2)
                        c02, n2 = tcols(i2)
                        defer_b.append(prenorm_tile(X[0:rows2, i2, :], rows2, ("x", i2), prm(l, "gpf", 8), H23(c02 - 1024, n2),
                                                    ("h2", i2 - 8), 0, hb=HBF[0:rows2, :], tpool="t", hb_toks=["hbf"], defer=True))
                        if "nodefer" in DBG:
                            defer_b.pop()()
                    for hf in range(2):
                        kd = psnext("d")
                        mm_group(None, [(PS[kd][0:rows, :], ACTB(j, lc0, n), WDN(j, 512 * hf, 512), j == 0, j == NJ - 1) for j in range(NJ)],
                                 reads=atoks + [("wdn", 0), ("wdn", 1)], writes=[("ps", kd)])
                        if hf == 0:
                            S.op("act", lambda e, kd=kd, rows=rows: e.copy(OSTG[0:rows, 0:512], PS[kd][0:rows, :]),
                                 reads=[("ps", kd)], writes=["ostg0"])
                        else:
                            S.op("dve", lambda e, kd=kd, rows=rows: e.tensor_copy(OSTG[0:rows, 512:1024], PS[kd][0:rows, :]),
                                 reads=[("ps", kd)], writes=["ostg1"])
                    for fb in defer_b:
                        fb()
                    post_norm_residual(OSTG[0:rows, :], rows, i, HBF[0:rows, :], ["hbf"], GPF, ["gpf"], ["ostg0", "ostg1"])
                if half == 0:
                    prenorm_tile(X[0:32, 16, :], 32, ("x", 16), prm(l, "gpf", 8), H23(1024, 32), ("h2", 8), 0,
                                 hb=HBF[0:32, :], tpool="t", hb_toks=["hbf"])
        RFOX = lambda i, rows=128: SM[0:rows, 100 + i:101 + i]
        RSGU = lambda i, rows=128: SM[0:rows, 120 + i:121 + i]
        RMEM = lambda i, rows=128: SM[0:rows, 140 + i:141 + i]
        SSFOX = lambda i, rows=128: SM[0:rows, 160 + i:161 + i]
        SSMEM = lambda i, rows=128: SM[0:rows, 180 + i:181 + i]
        ALLHT = [("ht", i) for i in range(17)]

        class NSQ:
            def __init__(self):
                self.p1 = None
                self.p2 = None
                self.p3 = None

            def step(self, new_p1):
                if self.p3 is not None:
                    self.p3()
                    self.p3 = None
                if self.p2 is not None:
                    self.p3 = self.p2()
                    self.p2 = None
                if self.p1 is not None:
                    self.p2 = self.p1()
                self.p1 = new_p1

            def flush(self):
                self.step(None)
                self.step(None)
                self.step(None)
        nsq = NSQ()

        def wload(slot, src2d, ncol, extra_writes=()):
            dma("pool", WS3(slot, ncol), src2d.rearrange("(c p) n -> p c n", p=128),
                writes=[("ws", slot)] + list(extra_writes), key=("ws", slot))

        def KAS(b, c0, n, p0=0, p1=67):
            return bf(WS_O[0] + b * 3200 + c0 * 2, n, p0, p1)

        def VAS(b, j):
            return bf(WS_O[0] + b * 3200 + 2080 + j * 130, 65)

        def VAS3(b):
            return AR[:, (WS_O[0] + b * 3200 + 2080) // 2:(WS_O[0] + b * 3200 + 2080) // 2 + 520].rearrange("p (j k) -> p j k", k=65)

        SC_TOKS = [("sc", b, x) for b in range(2) for x in ("k", "v", "r", "one")]

        def norm_store_fm(obank, I, grp, chunk, p0, ssq4, l):
            rlrow = f32(SCR_O + 6 * 1024, 512, 64, 65)
            hfT = f32(SCR_O + 6 * 1024, 512, 0, 64)
            S.op("act", lambda e: e.activation(rlrow, PS[obank][64:65, 0:512], AF.Ln), reads=[("ps", obank)], writes=[("scr", 6, "r")])
            S.op("act", lambda e: e.activation(rlrow, rlrow, AF.Exp, scale=-1.0), reads=[("scr", 6, "r")], writes=[("scr", 6, "r")])
            kb = psnext("g")
            S.op("pe", lambda e: e.matmul(PS[kb][0:64, 0:512], ONEROW[64:65, 0:64], rlrow, start=True, stop=True, skip_group_check=True),
                 reads=[("scr", 6, "r"), "onerow"], writes=[("ps", kb)])
            S.op("dve", lambda e: e.tensor_copy(hfT, PS[kb][0:64, 0:512]), reads=[("ps", kb)], writes=scr(6, 7))
            S.op("dve", lambda e: e.tensor_tensor(hfT, PS[obank][0:64, 0:512], hfT, ALU.mult), reads=[("ps", obank)] + scr(6, 7), writes=scr(6, 7))
            return lambda: norm_store_fm2(I, grp, chunk, p0, ssq4, l)

        def norm_store_fm2(I, grp, chunk, p0, ssq4, l):
            hfT = f32(SCR_O + 6 * 1024, 512, 0, 64)
            sqb = bf(SCR_O + 4 * 1024, 512, 0, 64)
            S.op("act", lambda e: e.activation(MX(chunk, 512 * I, 512, p0, p0 + 64), hfT, AF.Copy,
                                               scale=prm(l, "ggo", 1, p0, p0 + 64, chunk)),
                 reads=scr(6, 7) + ["prm"], writes=[("mx", chunk, 4 * I + s_, p0) for s_ in range(4)])
            S.op("act", lambda e: e.activation(sqb, hfT, AF.Square), reads=scr(6, 7), writes=scr(4))
            return lambda: norm_store_fm3(I, grp, ssq4)

        def norm_store_fm3(I, grp, ssq4):
            sqb = bf(SCR_O + 4 * 1024, 512, 0, 64)
            kt = psnext("t")
            ones_col = bf(VA_O + 64 * 2, 1, 0, 64)

            def ssmm(e):
                ins = None
                for s_ in range(4):
                    ins = e.matmul(PS[kt][:, s_:s_ + 1], sqb[:, 128 * s_:128 * s_ + 128], ones_col, start=True, stop=True,
                                   skip_group_check=True)
                return ins
            S.op("pe", ssmm, reads=scr(4) + ["va_ones"], writes=[("ps", kt)])
            toks4 = [("ssq", grp, 4 * I + s_) for s_ in range(4)]
            S.op("dve", lambda e: e.tensor_tensor(ssq4, ssq4, PS[kt][:, 0:4], ALU.add), reads=[("ps", kt)] + toks4, writes=toks4)

        def norm_and_store_block(obank, I, grp, chunk, p0, ssq4, l):
            O4 = PS[obank][:, 0:260].rearrange("p (s k) -> p s k", k=65)
            rl4 = SM[:, SM_T + 10:SM_T + 14]
            sq4 = SM[:, SM_T + 14:SM_T + 18]
            S.op("dve", lambda e: e.reciprocal(rl4.unsqueeze(2), O4[:, :, 64:65]), reads=[("ps", obank)], writes=["rl4"])
            hf4 = SCRf(6, 256)
            hb4 = SCRb(7, 256)
            S.op("dve", lambda e: e.tensor_tensor(hf4.rearrange("p (s k) -> p s k", k=64), O4[:, :, 0:64],
                                                  rl4.unsqueeze(2).to_broadcast([128, 4, 64]), ALU.mult),
                 reads=[("ps", obank), "rl4"], writes=scr(6))

            def sqs(e):
                ins = None
                for s_ in range(4):
                    ins = e.activation(hb4[:, 64 * s_:64 * s_ + 64], hf4[:, 64 * s_:64 * s_ + 64], AF.Square,
                                       accum_out=sq4[:, s_:s_ + 1])
                return ins
            S.op("act", sqs, reads=scr(6), writes=scr(7) + ["sq4"])
            toks4 = [("ssq", grp, 4 * I + s_) for s_ in range(4)]
            S.op("dve", lambda e: e.tensor_tensor(ssq4, ssq4, sq4, ALU.add), reads=["sq4"] + toks4, writes=toks4)
            S.op("act", lambda e: e.copy(hb4, hf4), reads=scr(6, 7), writes=scr(7))
            kt = psnext("t")

            def tr(e):
                ins = None
                for s_ in range(4):
                    ins = e.transpose(PSB[kt][p0:p0 + 64, 128 * s_:128 * s_ + 128], hb4[:, 64 * s_:64 * s_ + 64], IDENT[:, :])
                return ins
            S.op("pe", tr, reads=scr(7) + ["cst"], writes=[("ps", kt)])
            S.op("dve", lambda e: e.tensor_scalar(MX(chunk, 512 * I, 512, p0, p0 + 64), PSB[kt][p0:p0 + 64, 0:512],
                                                  prm(l, "ggo", 1, p0, p0 + 64, chunk), None, ALU.mult),
                 reads=[("ps", kt), "prm"], writes=[("mx", chunk, 4 * I + s_, p0) for s_ in range(4)])

        def norm_and_store(obank, ocol, rows, tile_i, grp, chunk, p0, ssq_acc, l):
            rl = SM[0:rows, SM_T + 4:SM_T + 5]
            S.op("dve", lambda e: e.reciprocal(rl, PS[obank][0:rows, ocol + 64:ocol + 65]),
                 reads=[("ps", obank)], writes=["rl"])
            hf = SCRf(5, 64, 0, rows)
            hb = SCRb(5, 64, 0, rows, off=256)
            S.op("dve", lambda e: e.tensor_scalar(hf, PS[obank][0:rows, ocol:ocol + 64], rl, None, ALU.mult),
                 reads=[("ps", obank), "rl"], writes=scr(5))
            sq = SM[0:rows, SM_T + 5:SM_T + 6]
            S.op("act", lambda e: e.activation(hb, hf, AF.Square, accum_out=sq), reads=scr(5), writes=scr(5) + ["sq"])
            S.op("dve", lambda e: e.tensor_tensor(ssq_acc, ssq_acc, sq, ALU.add),
                 reads=["sq", ("ssq", grp, tile_i)], writes=[("ssq", grp, tile_i)])
            S.op("act", lambda e: e.copy(hb, hf), reads=scr(5), writes=scr(5))
            kt = psnext("t")
            c0, n = tcols(tile_i)
            S.op("pe", lambda e: e.transpose(PSB[kt][p0:p0 + 64, 0:rows], hb, IDENT[0:rows, 0:rows]),
                 reads=scr(5) + ["cst"], writes=[("ps", kt)])
            S.op("dve", lambda e: e.tensor_scalar(MX(chunk, c0, n, p0, p0 + 64), PSB[kt][p0:p0 + 64, 0:rows],
                                                  prm(l, "ggo", 1, p0, p0 + 64, chunk), None, ALU.mult),
                 reads=[("ps", kt), "prm"], writes=[("mx", chunk, tile_i, p0)])

        try:
            for l in range(L):
                psr.clear()
                psr.update({"g": [0, 1], "s": [2, 3, 4], "o": [5, 6], "t": [7], "w": [0, 1, 2, 3, 4, 5]})
                for kk in psr:
                    psi[kk] = 0
                S.op("dve", lambda e: e.memset(
                    AR[:, VA_O // 2: VA_O // 2 + 17 * 8 * 65].rearrange("p (a k) -> p a k", k=65)[:, :, 64:65], 1.0),
                    writes=["va_ones"])
                S.op("dve", lambda e: e.memset(MVAt[:, :].rearrange("p (a k) -> p a k", k=65)[:, :, 64:65], 1.0),
                     writes=["mva_ones"])
                for s in (0, 2):
                    S.op("dve", lambda e, s=s: e.memset(QK(s, 0, NT, 64, 67), -1.0), writes=[("qkc", s)])
                for s in (1, 3):
                    S.op("dve", lambda e, s=s: e.memset(QK(s, 0, NT, 64, 65), 1.0), writes=[("qkc", s)])
                wload(0, w_in[l][:, 1024:1536], 512, extra_writes=SC_TOKS if l > 0 else ())
                wload(1, w_in[l][:, 512:1024], 512)
                wload(2, w_in[l][:, 1536:2056], 520)

                SGC = MX_O + 12288
                WSTl = bf(SGC, 512)
                GSl = f32(SGC + 1024, 256)
                dma("pool", WSTl.rearrange("p (g i) -> p g i", g=4), wsT[l].rearrange("g j i -> j g i"),
                    writes=[("sguc", 0)], key="wst")
                S.op("dve", lambda e: e.memset(bf(SGC, 512, 64, 128).rearrange("p (g i) -> p g i", g=4)[:, :, 0:64], 0.0),
                     reads=[("sguc", 0)], writes=[("sguc", 0)])
                dma("sp", GSl, g_sgu[l:l + 1, :].broadcast_to([128, 256]), writes=[("sguc", 1)], key="gs")

                def sgu_ops(i):
                    st_ = 0 if "setA" in DBG else i % 2
                    base = MX_O if st_ == 0 else MX_O + 6144
                    tk = "sguA" if st_ == 0 else "sguB"
                    Sb = lambda g, n, p0=0, p1=128: bf(base + g * 1024, n, p0, p1)
                    Sf = lambda g, n, p0=0, p1=128: f32(base + g * 1024, n, p0, p1)
                    sc = lambda *gs: [(tk, g) for g in gs]
                    rows = tile_rows(i)
                    c0, n = tcols(i)
                    s1, s2 = [], []
                    k = psnext("s")
                    z = Sf(0, 512, 0, rows)
                    t1 = Sf(2, 512, 0, rows)
                    ssv = SM[0:rows, SM_T + 20 + 2 * st_:SM_T + 21 + 2 * st_]
                    sss = SM[0:rows, SM_T + 21 + 2 * st_:SM_T + 22 + 2 * st_]
                    ssvt, ssst = ("ssv", st_), ("sss", st_)
                    vvf = Sf(3, 256, 0, rows)
                    vvb = Sb(2, 256, 0, rows)
                    sg = Sf(4, 256, 0, rows)
                    sgb = Sb(5, 256, 0, rows)
                    s1.append(lambda: mm_group(None, [(PS[k][0:rows, :], HT(c, c0, n), WS(2, c, 8, 512), c == 0, c == 7) for c in range(8)],
                                               reads=[("ht", i), ("ws", 2)], writes=[("ps", k)]))
                    s1.append(lambda: S.op("act", lambda e: e.copy(z, PS[k][0:rows, :]), reads=[("ps", k)], writes=sc(0, 1)))
                    s1.append(lambda: S.op("dve", lambda e: e.tensor_tensor(t1, z, z, ALU.mult), reads=sc(0, 1), writes=sc(2, 3)))
                    s1.append(lambda: S.op("dve", lambda e: e.tensor_scalar(t1, t1, 0.044715, 1.0, ALU.mult, ALU.add), reads=sc(2, 3), writes=sc(2, 3)))
                    s1.append(lambda: S.op("dve", lambda e: e.tensor_tensor(t1, t1, z, ALU.mult), reads=sc(0, 1, 2, 3), writes=sc(2, 3)))
                    s1.append(lambda: S.op("act", lambda e: e.activation(t1, t1, AF.Exp, scale=-1.5957691216057308), reads=sc(2, 3), writes=sc(2, 3)))
                    s1.append(lambda: S.op("act", lambda e: e.activation(t1, t1, AF.Ln, bias=1.0, scale=1.0), reads=sc(2, 3), writes=sc(2, 3)))
                    s1.append(lambda: S.op("act", lambda e: e.activation(t1, t1, AF.Exp, scale=-1.0), reads=sc(2, 3), writes=sc(2, 3)))
                    s1.append(lambda: S.op("dve", lambda e: e.tensor_tensor(z, z, t1, ALU.mult), reads=sc(0, 1, 2, 3), writes=sc(0, 1)))
                    s1.append(lambda: S.op("act", lambda e: e.activation(t1[:, 0:256], z[:, 256:512], AF.Square, accum_out=ssv),
                                           reads=sc(0, 1), writes=sc(2) + [ssvt]))
                    s1.append(lambda: S.op("act", lambda e: e.activation(ssv, ssv, AF.Ln, bias=EPS, scale=1.0 / 256), reads=[ssvt], writes=[ssvt]))
                    s1.append(lambda: S.op("act", lambda e: e.activation(ssv, ssv, AF.Exp, scale=-0.5), reads=[ssvt], writes=[ssvt]))
                    s1.append(lambda: S.op("dve", lambda e: e.scalar_tensor_tensor(vvf, z[:, 256:512], ssv, GSl[0:rows, :], ALU.mult, ALU.mult),
                                           reads=sc(0, 1) + [("sguc", 1)] + [ssvt], writes=sc(3)))
                    s1.append(lambda: S.op("act", lambda e: e.copy(vvb, vvf), reads=sc(3), writes=sc(2)))
                    if i == 16:
                        s1.append(lambda: dma("sp", ogv[l], vvf, reads=sc(3), key="ogv"))
                    k2 = psnext("s")
                    if i < 16:
                        items = [(PS[k2][0:128, 64 * g:64 * g + 64], WSTl[:, 128 * g:128 * g + 128], vvb[:, 64 * g:64 * g + 64], True, True) for g in range(4)]
                        bsap = prm(l, "bs", 4)
                        wtok = [("sguc", 0)]
                    else:
                        items = [(PS[k2][0:32, 64 * g:64 * g + 64], WSTS(l, g), vvb[:, 64 * g:64 * g + 64], True, True) for g in range(4)]
                        bsap = prm(l, "bss", 4, 0, 32)
                        wtok = ["wsts"] + [("wsts", l, b) for b in range(2)]
                    s1.append(lambda: mm_group(None, items, reads=sc(2) + wtok, writes=[("ps", k2)]))
                    s2.append(lambda: S.op("dve", lambda e: e.tensor_tensor(
                        sg.rearrange("p (g d) -> p g d", g=4), PS[k2][0:rows, 0:256].rearrange("p (g d) -> p g d", g=4),
                        bsap.unsqueeze(2).to_broadcast([rows, 4, 64]), ALU.add),
                        reads=[("ps", k2), "prm"], writes=sc(4)))
                    s2.append(lambda: S.op("dve", lambda e: e.tensor_tensor(sg, sg, z[:, 0:256], ALU.mult), reads=sc(0, 1, 4), writes=sc(4)))
                    s2.append(lambda: S.op("act", lambda e: e.activation(sgb, sg, AF.Square, accum_out=sss),
                                           reads=sc(4), writes=sc(5) + [ssst]))
                    s2.append(lambda: S.op("act", lambda e: e.activation(sss, sss, AF.Ln, bias=EPS, scale=1.0 / 256), reads=[ssst], writes=[ssst]))
                    s2.append(lambda: S.op("act", lambda e: e.activation(RSGU(i, rows), sss, AF.Exp, scale=-0.5),
                                           reads=[ssst], writes=[("rsgu", i)]))
                    s2.append(lambda: S.op("act", lambda e: e.copy(sgb, sg), reads=sc(4, 5), writes=sc(5)))

                    def trs():
                        kt = psnext("o")

                        def tr(e):
                            ins = None
                            for cc in range(2):
                                ins = e.transpose(PSB[kt][:, cc * 128:cc * 128 + rows], sgb[:, cc * 128:(cc + 1) * 128], IDENT[0:rows, 0:rows])
                            return ins
                        S.op("pe", tr, reads=sc(5) + ["cst"], writes=[("ps", kt)])
                        for cc in range(2):
                            S.op("dve", lambda e, cc=cc, l=l: e.tensor_scalar(
                                MX(4 + cc, c0, n), PSB[kt][:, cc * 128:cc * 128 + rows], prm(l, "ggo", 1, 0, 128, 4 + cc), None, ALU.mult),
                                reads=[("ps", kt), "prm"], writes=[("mx", 4 + cc, i)])
                    s2.append(trs)
                    return s1, s2

                def interleave(a, b):
                    for q in range(max(len(a), len(b))):
                        if q < len(a):
                            a[q]()
                        if q < len(b):
                            b[q]()

                def vk_tile(i):
                    rows = tile_rows(i)
                    c0, n = tcols(i)
                    for which, slot, oprompt, osample, sg in (("v", 0, ofv, osv, 4), ("k", 1, ofk, osk, 6)):
                        k = psnext("g")
                        mm_group(None, [(PS[k][0:rows, :], HT(c, c0, n), WS(slot, c, 0, 512), c == 0, c == 7)
                                        for c in range(8)],
                                 reads=[("ht", i), ("ws", slot)], writes=[("ps", k)])
                        stg = SCRf(sg, 512, 0, rows)
                        S.op("act", lambda e, stg=stg, k=k, rows=rows: e.copy(stg, PS[k][0:rows, :]),
                             reads=[("ps", k)], writes=scr(sg, sg + 1))
                        if which == "v":
                            va3 = AR[0:rows, VA_O // 2 + i * 520: VA_O // 2 + (i + 1) * 520].rearrange(
                                "p (h k) -> p h k", k=65)[:, :, 0:64]
                            S.op("dve", lambda e, va3=va3, k=k, rows=rows: e.tensor_copy(
                                va3, PS[k][0:rows, :].rearrange("p (h k) -> p h k", k=64)),
                                reads=[("ps", k), "va_ones"], writes=[("va", i)])
                        dst = oprompt[l, 128 * i:128 * i + 128, :] if i < 16 else osample[l, :, :]
                        dma("sp", dst, stg, reads=scr(sg, sg + 1), key=("stg", sg))

                prev2 = []

                def pn_a(i):
                    rows = tile_rows(i)
                    c0, n = tcols(i)
                    return prenorm_tile(X[0:rows, i, :], rows, ("x", i), prm(l, "gpm", 8), HT3(c0, n), ("ht", i), i % 2, defer=True)
                pn_b = {0: pn_a(0)}
                for i in range(17):
                    if i + 1 < 17:
                        pn_b[i + 1] = pn_a(i + 1)
                    pn_b.pop(i)()
                    if i >= 1:
                        vk_tile(i - 1)
                        s1_, s2_ = sgu_ops(i - 1)
                        interleave(s1_, prev2)
                        prev2 = s2_
                vk_tile(16)
                s1_, s2_ = sgu_ops(16)
                interleave(s1_, prev2)
                interleave([], s2_)
                S.op("dve", lambda e: e.memset(bf(MX_O + 15 * 1024, 2), 0.0),
                     writes=[("sguA", g) for g in range(6)] + [("sguB", g) for g in range(6)] + [("sguc", 0), ("sguc", 1)]
                     + [("mx", c, i, p0) for c in (0, 1, 2, 3) for i in range(17) for p0 in (0, 64)])

                _chk("vk%d" % l)
                NB = SM[0:8, 210:211]
                S.op("dve", lambda e, l=l: e.tensor_scalar(NB, prm(l, "bf", 1, 0, 8), -1.0, None, ALU.mult),
                     reads=["prm"], writes=["cw", "nb"])
                for bi, (c0, n) in enumerate(BLKS):
                    k = psnext("g")
                    mm_group(None, [(PS[k][0:8, 0:n], WS(2, c, 0, 8), HT(c, c0, n), c == 0, c == 7) for c in range(8)],
                             reads=ALLHT + [("ws", 2)], writes=["cw", ("ps", k)])
                    tmp = SCRf(0, 512, 96, 104)
                    S.op("act", lambda e, k=k, n=n, tmp=tmp: e.activation(tmp[:, 0:n], PS[k][0:8, 0:n], AF.Exp, bias=NB, scale=-1.0),
                         reads=[("ps", k), "nb"], writes=scr(0, 1))
                    S.op("act", lambda e, n=n, tmp=tmp: e.activation(tmp[:, 0:n], tmp[:, 0:n], AF.Ln, bias=1.0, scale=1.0),
                         reads=scr(0, 1), writes=scr(0, 1))
                    S.op("dve", lambda e, c0=c0, n=n, tmp=tmp: e.tensor_scalar(CT(c0, n), tmp[:, 0:n], -1.0, None, ALU.mult),
                         reads=scr(0, 1), writes=["cw", ("ct", bi)])
                dma("sp", olf[l], CT(0, 2048), reads=[("ct", b) for b in range(4)], writes=["cw"], key="olf")
                dma("sp", oslf[l], CT(2048, 32), reads=[("ct", 4)], writes=["cw"], key="olf2")
                SLF = SM[96:104, 220:252]
                S.op("dve", lambda e: e.tensor_copy(SLF, CT(2048, 32)), reads=[("ct", 4)], writes=["cw", "slf"])
                S.op("dve", lambda e: e.tensor_tensor_scan(CT(0, 2048), ONE8.to_broadcast([8, 2048]), CT(0, 2048), 0.0, ALU.mult, ALU.add),
                     reads=[("ct", b) for b in range(4)] + ["one"], writes=["cw", "ctp"])
                S.op("dve", lambda e: e.tensor_copy(HI(0, 2048), CT(0, 2048)), reads=["ctp"], writes=["cw", "hi"])
                S.op("dve", lambda e: e.tensor_tensor(CT(0, 2048), CT(0, 2048), HI(0, 2048), ALU.subtract),
                     reads=["ctp", "hi"], writes=["cw", "ctp"])
                S.op("dve", lambda e: e.tensor_copy(LO(0, 2048), CT(0, 2048)), reads=["ctp"], writes=["cw", "lo"])
                dma("sp", cscr_p[:, 0, :], HI(0, 2048), reads=["hi"], writes=["cw", "cscr_p"], key="cscr0")
                dma("sp", cscr_p[:, 1, :], LO(0, 2048), reads=["lo"], writes=["cw", "cscr_p2"], key="cscr1")
                CS = lambda b, c0, n: CT(b * 1040 + c0, n)
                for b in range(2):
                    dma("sp", CS(b, 0, 1024), clfT[l, b], reads=["ctp", "lo", ("ct", 4), "slf"], writes=["cw", ("cs", b)], key=("csl", b))
                    S.op("dve", lambda e, b=b: e.tensor_copy(CS(b, 1024, 16), SM[96:104, 220 + 16 * b:236 + 16 * b]),
                         reads=["slf", "ctp", "lo", ("ct", 4)], writes=["cw", ("cs2", b)])
                    S.op("dve", lambda e, b=b: e.tensor_tensor_scan(CS(b, 0, 1040), ONE8.to_broadcast([8, 1040]), CS(b, 0, 1040), 0.0, ALU.mult, ALU.add),
                         reads=[("cs", b), ("cs2", b), "one"], writes=["cw", ("csc", b)])
                S.op("dve", lambda e: e.tensor_copy(HI(0, 2080), CT(0, 2080)),
                     reads=[("csc", 0), ("csc", 1), "cscr_p", "cscr_p2"], writes=["cw", "hi"])
                S.op("dve", lambda e: e.tensor_tensor(CT(0, 2080), CT(0, 2080), HI(0, 2080), ALU.subtract),
                     reads=["hi"], writes=["cw", ("csc", 0), ("csc", 1)])
                S.op("dve", lambda e: e.tensor_copy(LO(0, 2080), CT(0, 2080)), reads=[("csc", 0), ("csc", 1), "cscr_p2"], writes=["cw", "lo"])
                for b in range(2):
                    dma("sp", cscr_s[:, 0, b, :], HI(b * 1040, 1040), reads=["hi"], writes=["cw", ("cscr_s", b, 0)], key=("cscrs", b, 0))
                    dma("sp", cscr_s[:, 1, b, :], LO(b * 1040, 1040), reads=["lo"], writes=["cw", ("cscr_s", b, 1)], key=("cscrs", b, 1))

                _chk("fg%d" % l)
                _chk("sgu%d" % l)
                wload(0, w_mkv[l], 512)
                MEMT3 = AR[:, (SCR_O + 4096) // 2:(SCR_O + 4096) // 2 + 2048].rearrange("p (c t) -> p c t", c=8)
                MEMT = lambda c, c0, n: bf(SCR_O + 4096 + (c * 256 + c0) * 2, n)
                memx = f32(WS_O[2], 1024)
                for mt in range(2):
                    dma("sp", memx, mem[128 * mt:128 * mt + 128, :], writes=[("ws", 2)], key=("ws", 2))
                    prenorm_tile(memx, 128, ("ws", 2), prm(l, "gmem", 8), MEMT3[:, :, 128 * mt:128 * mt + 128],
                                 ("memt", mt), 0, extra_writes=scr(4, 5, 6, 7))
                memt_toks = [("memt", 0), ("memt", 1)] + scr(4, 5, 6, 7)
                stgk = f32(WS_O[2], 512)
                for mt in range(2):
                    k = psnext("g")
                    mm_group(None, [(PS[k][:, :], MEMT(c, 128 * mt, 128), WS(0, c, 0, 512), c == 0, c == 7) for c in range(8)],
                             reads=memt_toks + [("ws", 0)], writes=[("ps", k)])
                    S.op("act", lambda e, k=k: e.copy(stgk, PS[k][:, :]), reads=[("ps", k)], writes=[("ws", 2)])
                    S.op("dve", lambda e, k=k, mt=mt: e.tensor_copy(
                        MVAt[:, mt * 260:(mt + 1) * 260].rearrange("p (h k) -> p h k", k=65)[:, :, 0:64],
                        PS[k][:, 256:512].rearrange("p (h k) -> p h k", k=64)),
                        reads=[("ps", k), "mva_ones"], writes=[("mva", mt)])
                    dma("sp", omk[l, 128 * mt:128 * mt + 128, :], stgk[:, 0:256], reads=[("ws", 2)], key="omk")
                    dma("sp", omv[l, 128 * mt:128 * mt + 128, :], stgk[:, 256:512], reads=[("ws", 2)], key="omv")
                for pr in range(2):
                    k = psnext("g")
                    mm_group(None, [(PS[k][:, 0:256], WS(0, c, 128 * pr, 128), MEMT(c, 0, 256), c == 0, c == 7) for c in range(8)],
                             reads=memt_toks + [("ws", 0)], writes=[("ps", k)])
                    for hh in range(2):
                        h = 2 * pr + hh
                        S.op("act", lambda e, k=k, hh=hh, h=h: e.copy(MKTt[0:64, h * 256:(h + 1) * 256], PS[k][64 * hh:64 * hh + 64, 0:256]),
                             reads=[("ps", k)], writes=[("mkt", h)])
                wload(2, w_in[l][:, 2056:2312], 256)
                for b in range(2):
                    S.op("dve", lambda e, b=b: e.memset(KAS(b, 0, 1040, 64, 65), 1.0), reads=[("ws", 0)], writes=[("ws", 0), ("sc", b, "one")])
                    S.op("dve", lambda e, b=b: e.memset(VAS3(b)[:, :, 64:65], 1.0), reads=[("ws", 0)], writes=[("ws", 0), ("sc", b, "one")])

                S.op("dve", lambda e: e.memset(SM[:, 160:200], 0.0),
                     writes=[("ssq", g, i) for g in (0, 2) for i in range(17)])

                _chk("memkv%d" % l)
                for pr in range(2):
                    for bi, (c0, n) in enumerate(BLKS):
                        k = psnext("g")
                        mm_group(None, [(PS[k][:, 0:n], WS(2, c, 128 * pr, 128), HT(c, c0, n), c == 0, c == 7) for c in range(8)],
                                 reads=ALLHT + [("ws", 2)], writes=[("ps", k)])
                        for hh in range(2):
                            S.op("act", lambda e, k=k, hh=hh, c0=c0, n=n: e.activation(
                                QK(2 * hh, c0, n, 0, 64), PS[k][64 * hh:64 * hh + 64, 0:n], AF.Copy, scale=0.125),
                                reads=[("ps", k)], writes=[("qa", hh, bi)])
                    for hh in range(2):
                        h = 2 * pr + hh
                        for I in range(4):
                            ob = psnext("o")
                            kts = [dict(ka=MKTt[0:64, h * 256 + 128 * j:h * 256 + 128 * j + 128], nk=128,
                                        va=MVAt[:, (j * 4 + h) * 65:(j * 4 + h) * 65 + 65], lo=0, hi=512, pv_lo=0,
                                        mask=None, pt="rot", reads=[("mkt", h), ("qa", hh, I)], vreads=[("mva", j), "mva_ones"])
                                   for j in range(2)]
                            osubs = [(PS[ob][:, 65 * s:65 * s + 65], 128 * s, 128 * s + 128, 0, 1) for s in range(4)]
                            attend(lambda c0, n, hh=hh, I=I: QK(2 * hh, 512 * I + c0, n, 0, 64), 64, kts, "fm", ob, "mem")
                            nsq.step(lambda ob=ob, I=I, pr=pr, hh=hh: norm_store_fm(ob, I, 2, 6 + pr, 64 * hh, SM[:, 180 + 4 * I:184 + 4 * I], l))
                        ob = psnext("g")
                        kts = []
                        for b in range(2):
                            dma("pool", KAS(b, 0, 256, 0, 64), cmkT[l, b, h], reads=[("ws", 0)], writes=[("sc", b, "k")], key=("sck", b))
                            dma("pool", VAS3(b)[:, 0:2, 0:64],
                                cmv[l, b].rearrange("(j p) (h d) -> p j h d", p=128, h=4)[:, :, h, :],
                                reads=[("ws", 0)], writes=[("sc", b, "v")], key=("scv", b), slow=True)
                            for j in range(2):
                                kts.append(dict(ka=KAS(b, 128 * j, 128, 0, 64), nk=128, va=VAS(b, j),
                                                lo=16 * b, hi=16 * b + 16, pv_lo=0, mask=None, pt=b,
                                                reads=[("sc", b, "k"), ("qa", hh, 4)], vreads=[("sc", b, "v"), ("sc", b, "one")]))
                        osubs = [(PS[ob][0:32, 0:65], 0, 32, 0, len(kts) - 1)]
                        attend(lambda c0, n, hh=hh: QK(2 * hh, 2048 + c0, n, 0, 64), 64, kts, osubs, ob, "mems")
                        norm_and_store(ob, 0, 32, 16, 2, 6 + pr, 64 * hh, SSMEM(16, 32), l)

                flush_pv()
                nsq.flush()
                _chk("memattn%d" % l)
                wload(2, w_in[l][:, 0:512], 512)
                for pr in range(4):
                    for bi, (c0, n) in enumerate(BLKS):
                        for qk, slot in ((0, 2), (1, 1)):
                            k = psnext("g")
                            mm_group(None, [(PS[k][:, 0:n], WS(slot, c, 128 * pr, 128), HT(c, c0, n), c == 0, c == 7) for c in range(8)],
                                     reads=ALLHT + [("ws", slot)], writes=[("ps", k)])
                            for hh in range(2):
                                if qk == 0:
                                    S.op("act", lambda e, k=k, hh=hh, c0=c0, n=n: e.activation(
                                        QK(2 * hh, c0, n, 0, 64), PS[k][64 * hh:64 * hh + 64, 0:n], AF.Copy, scale=0.125),
                                        reads=[("ps", k)], writes=[("qa", hh, bi)])
                                else:
                                    S.op("dve", lambda e, k=k, hh=hh, c0=c0, n=n: e.tensor_copy(
                                        QK(2 * hh + 1, c0, n, 0, 64), PS[k][64 * hh:64 * hh + 64, 0:n]),
                                        reads=[("ps", k)], writes=[("ka", hh, bi)])
                    for hh in range(2):
                        h = 2 * pr + hh
                        dma("sp", QK(2 * hh, 0, 2048, 64, 65), cscr_p[h:h + 1, 0, :], reads=["cscr_p", ("qkc", 2 * hh)],
                            writes=[("qar", hh, 0)], key=("qar", hh))
                        dma("sp", QK(2 * hh + 1, 0, 2048, 65, 66), cscr_p[h:h + 1, 0, :], reads=["cscr_p", ("qkc", 2 * hh + 1)],
                            writes=[("kar", hh, 0)], key=("kar", hh))
                        dma("sp", QK(2 * hh + 1, 0, 2048, 66, 67), cscr_p[h:h + 1, 1, :], reads=["cscr_p2"],
                            writes=[("kar", hh, 1)], key=("kar", hh))
                        for b in range(2):
                            dma("sp", QK(2 * hh, 2048 + 16 * b, 16, 64, 65), cscr_s[h:h + 1, 0, b, 1024:1040],
                                reads=[("cscr_s", b, 0)], writes=[("qar", hh, 1 + b)], key=("qar", hh))
                            dma("sp", QK(2 * hh + 1, 2048 + 16 * b, 16, 65, 66), cscr_s[h:h + 1, 0, b, 1024:1040],
                                reads=[("cscr_s", b, 0)], writes=[("kar", hh, 2 + b)], key=("kar", hh))
                            dma("sp", QK(2 * hh + 1, 2048 + 16 * b, 16, 66, 67), cscr_s[h:h + 1, 1, b, 1024:1040],
                                reads=[("cscr_s", b, 1)], writes=[("kar", hh, 4 + b)], key=("kar", hh))
                        qar = [("qar", hh, x) for x in range(3)] + [("qkc", 2 * hh)]
                        kar = [("kar", hh, x) for x in range(6)] + [("qkc", 2 * hh + 1)]
                        for I in range(4):
                            ob = psnext("o")
                            kts = []
                            for j in range(4 * I + 4):
                                a = j - 4 * I
                                kts.append(dict(ka=QK(2 * hh + 1, 128 * j, 128, 0, 67), nk=128, va=VA(j, h),
                                                lo=(128 * a if a >= 0 else 0), hi=512, pv_lo=(128 * a if a >= 0 else 0),
                                                mask=((MASKNEG, 128) if a >= 0 else None), pt="rot",
                                                reads=[("ka", hh, j // 4), ("qa", hh, I)] + qar + kar,
                                                vreads=[("va", j), "va_ones"]))
                            osubs = [(PS[ob][:, 65 * s:65 * s + 65], 128 * s, 128 * s + 128, 0, 4 * I + s) for s in range(4)]
                            attend(lambda c0, n, hh=hh, I=I: QK(2 * hh, 512 * I + c0, n, 0, 67), 67, kts, "fm", ob, "fox")
                            nsq.step(lambda ob=ob, I=I, pr=pr, hh=hh: norm_store_fm(ob, I, 0, pr, 64 * hh, SM[:, 160 + 4 * I:164 + 4 * I], l))
                        ob = psnext("g")
                        kts = []
                        for b in range(2):
                            dma("pool", KAS(b, 0, 1024, 0, 64), ckT[l, b, h], reads=[("ws", 0)], writes=[("sc", b, "k")], key=("sck", b))
                            dma("pool", VAS3(b)[:, :, 0:64],
                                cv[l, b].rearrange("(j p) (h d) -> p j h d", p=128, h=8)[:, :, h, :],
                                reads=[("ws", 0)], writes=[("sc", b, "v")], key=("scv", b), slow=True)
                            dma("sp", KAS(b, 0, 1024, 65, 66), cscr_s[h:h + 1, 0, b, 0:1024], reads=[("cscr_s", b, 0), ("ws", 0)],
                                writes=[("sc", b, "r")], key=("scr", b))
                            dma("sp", KAS(b, 0, 1024, 66, 67), cscr_s[h:h + 1, 1, b, 0:1024], reads=[("cscr_s", b, 1), ("ws", 0)],
                                writes=[("sc", b, "r2")], key=("scr", b))
                            for j in range(8):
                                kts.append(dict(ka=KAS(b, 128 * j, 128), nk=128, va=VAS(b, j),
                                                lo=16 * b, hi=16 * b + 16, pv_lo=0, mask=None, pt=b,
                                                reads=[("sc", b, "k"), ("sc", b, "r"), ("sc", b, "r2"), ("sc", b, "one"), ("qa", hh, 4)] + qar,
                                                vreads=[("sc", b, "v"), ("sc", b, "one")]))
                        kts.append(dict(ka=QK(2 * hh + 1, 2048, 32, 0, 67), nk=32, va=VA(16, h, 32), lo=0, hi=32, pv_lo=0,
                                        mask=(MASKS, 32), pt=2, reads=[("ka", hh, 4), ("qa", hh, 4)] + qar + kar,
                                        vreads=[("va", 16), "va_ones"]))
                        osubs = [(PS[ob][0:32, 0:65], 0, 32, 0, len(kts) - 1)]
                        attend(lambda c0, n, hh=hh: QK(2 * hh, 2048 + c0, n, 0, 67), 67, kts, osubs, ob, "foxs")
                        norm_and_store(ob, 0, 32, 16, 0, pr, 64 * hh, SSFOX(16, 32), l)

                flush_pv()
                nsq.flush()
                _chk("fox%d" % l)
                S.op("act", lambda e: e.activation(SM[:, 100:117], SM[:, 160:177], AF.Ln, bias=EPS, scale=1.0 / 512),
                     reads=[("ssq", 0, i) for i in range(17)], writes=["rfox"])
                S.op("act", lambda e: e.activation(SM[:, 100:117], SM[:, 100:117], AF.Exp, scale=-0.5), reads=["rfox"], writes=["rfox"])
                S.op("act", lambda e: e.activation(SM[:, 140:157], SM[:, 180:197], AF.Ln, bias=EPS, scale=1.0 / 256),
                     reads=[("ssq", 2, i) for i in range(17)], writes=["rmem"])
                S.op("act", lambda e: e.activation(SM[:, 140:157], SM[:, 140:157], AF.Exp, scale=-0.5), reads=["rmem"], writes=["rmem"])

                wload(0, w_out[l][:, 0:512], 512, extra_writes=SC_TOKS)
                wload(1, w_out[l][:, 512:1024], 512)
                GP = SCRf(4, 1024)
                dma("sp", GP, g_post_mix[l:l + 1, :].broadcast_to([128, D]), writes=scr(4, 5, 6, 7), key="gp")
                for i in range(17):
                    rows = tile_rows(i)
                    c0, n = tcols(i)
                    ostg = SCRf(0, 1024, 0, rows)
                    mxr = [("mx", c, i, p0) for c in (0, 1, 2, 3, 6, 7) for p0 in (0, 64)] + [("mx", 4, i), ("mx", 5, i)]
                    last_b = None
                    for hf in range(2):
                        banks = [psnext("w") for _ in range(3)]
                        for gi, (cs, b) in enumerate(zip(((0, 1, 2, 3), (4, 5), (6, 7)), banks)):
                            mm_group(None, [(PS[b][0:rows, :], MX(c, c0, n), WS(hf, c, 0, 512), c == cs[0], c == cs[-1]) for c in cs],
                                     reads=mxr + [("ws", hf)], writes=[("ps", b)])
                        o_h = ostg[:, 512 * hf:512 * hf + 512]
                        S.op("act", lambda e, o_h=o_h, b=banks[0], rows=rows, i=i: e.activation(o_h, PS[b][0:rows, :], AF.Copy, scale=RFOX(i, rows)),
                             reads=[("ps", banks[0]), "rfox"], writes=scr(2 * hf, 2 * hf + 1))
                        S.op("dve", lambda e, o_h=o_h, b=banks[1], rows=rows, i=i: e.scalar_tensor_tensor(o_h, PS[b][0:rows, :], RSGU(i, rows), o_h, ALU.mult, ALU.add),
                             reads=[("ps", banks[1]), ("rsgu", i)] + scr(2 * hf, 2 * hf + 1), writes=scr(2 * hf, 2 * hf + 1))
                        S.op("dve", lambda e, o_h=o_h, b=banks[2], rows=rows, i=i: e.scalar_tensor_tensor(o_h, PS[b][0:rows, :], RMEM(i, rows), o_h, ALU.mult, ALU.add),
                             reads=[("ps", banks[2]), "rmem"] + scr(2 * hf, 2 * hf + 1), writes=scr(2 * hf, 2 * hf + 1))
                        last_b = banks
                    post_norm_residual(ostg, rows, i, bf(WS_O[2], 1024, 0, rows), [("ws", 2)], GP, scr(4, 5, 6, 7), scr(0, 1, 2, 3))

                _chk("wout%d" % l)
                S.barrier()
                ffn_phase(l)
                S.barrier()

        except _Stop:
            pass
        for i in range(16):
            dma("sp", y_p[128 * i:128 * i + 128, :], X[:, i, :], reads=[("x", i)], key=("x", i))
        dma("sp", y_s, X[0:32, 16, :], reads=[("x", 16)], key=("x", 16))
        S.emit(st)
    return nc


_NC_CACHE = {}


def _prep_inputs(inp):
    f = lambda a: np.ascontiguousarray(np.asarray(a, dtype=np.float32))
    x_prompt = f(inp["x_prompt"]); x_sample = f(inp["x_sample"]); mem_prompt = f(inp["mem_prompt"])
    cfk = f(inp["cache_fox_k"]); cfv = f(inp["cache_fox_v"]); clf = f(inp["cache_fox_logf"])
    cmk = f(inp["cache_mem_k"]); cmvv = f(inp["cache_mem_v"]); cconv = f(inp["cache_ffn_conv"])
    ckT_all = np.ascontiguousarray(cfk.transpose(0, 1, 3, 4, 2))
    cv_all = cfv.reshape(L, 16, 1024, 512)
    clfT_all = np.ascontiguousarray(clf.transpose(0, 1, 3, 2))
    cmkT_all = np.ascontiguousarray(cmk.transpose(0, 1, 3, 4, 2))
    cmv_all = cmvv.reshape(L, 16, 256, 256)
    wsT = np.ascontiguousarray(f(inp["w_spatial"]).transpose(0, 1, 3, 2))
    fm8 = lambda g: f(g).reshape(L, 8, 128).transpose(0, 2, 1)
    b_sp = f(inp["b_spatial"])
    w_dw = f(inp["w_dwconv"]); b_dw = f(inp["b_dwconv"]); b_f = f(inp["b_forget"])
    cst = np.zeros((128, 288), dtype=ml_dtypes.bfloat16)
    cst[:, 0:128] = np.eye(128, dtype=np.float32).astype(ml_dtypes.bfloat16)
    kk, qq = np.meshgrid(np.arange(128), np.arange(128), indexing="ij")
    cst[:, 128:256] = np.where(kk <= qq, 0.0, NEG).astype(ml_dtypes.bfloat16)
    k2, q2 = np.meshgrid(np.arange(32), np.arange(32), indexing="ij")
    ok = (k2 // 16 == q2 // 16) & (k2 % 16 <= q2 % 16)
    cst[0:32, 256:288] = np.where(ok, 0.0, NEG).astype(ml_dtypes.bfloat16)
    wu = f(inp["w_up"])
    wu_a = wu[:, :, 0:FF].reshape(L, 8, 128, NJ, 128)
    wu_l = wu[:, :, FF:2 * FF].reshape(L, 8, 128, NJ, 128)
    w_up_r = np.ascontiguousarray(np.concatenate([wu_a, wu_l], axis=4).transpose(0, 3, 2, 1, 4))
    shared = dict(
        w_in=f(inp["w_in"]), w_mkv=f(inp["w_mem_kv"]), w_out=f(inp["w_out"]), w_up=w_up_r,
        w_dn=f(inp["w_down"]), g_post_mix=f(inp["g_post_mix"]), g_post_ffn=f(inp["g_post_ffn"]),
        g_sgu=f(inp["g_sgu"]), wsT=wsT, cst=cst)
    gpm = fm8(inp["g_pre_mix"]); ggo = fm8(inp["g_group_out"]); gmem = fm8(inp["g_mem"]); gpf = fm8(inp["g_pre_ffn"])
    in_maps = []
    for c in range(NCORES):
        prm = np.zeros((128, NPRM), dtype=np.float32)
        for l in range(L):
            o = l * PPL
            prm[:, o + PO["gpm"]:o + PO["gpm"] + 8] = gpm[l]
            prm[:, o + PO["ggo"]:o + PO["ggo"] + 8] = ggo[l]
            prm[:, o + PO["gmem"]:o + PO["gmem"] + 8] = gmem[l]
            prm[:, o + PO["gpf"]:o + PO["gpf"] + 8] = gpf[l]
            prm[:, o + PO["bs"]:o + PO["bs"] + 4] = b_sp[l].T
            prm[0:16, o + PO["bss"]:o + PO["bss"] + 4] = b_sp[l][:, 0:16].T
            prm[16:32, o + PO["bss"]:o + PO["bss"] + 4] = b_sp[l][:, 0:16].T
            prm[:, o + PO["wdw"]:o + PO["wdw"] + 66] = w_dw[l].reshape(3, NJ, 128).transpose(2, 1, 0).reshape(128, 66)
            prm[:, o + PO["bdw"]:o + PO["bdw"] + 22] = b_dw[l].reshape(NJ, 128).T
            prm[0:8, o + PO["bf"]] = b_f[l]
            cc = cconv[l, 2 * c:2 * c + 2]
            prm[:, o + PO["cconv"]:o + PO["cconv"] + 88] = cc.reshape(2, 2, NJ, 128).transpose(3, 2, 0, 1).reshape(128, 88)
        m = dict(shared)
        m.update(
            x_p=x_prompt[c], x_s=x_sample[2 * c:2 * c + 2].reshape(32, D), mem=mem_prompt[c],
            ckT=np.ascontiguousarray(ckT_all[:, 2 * c:2 * c + 2]), cv=np.ascontiguousarray(cv_all[:, 2 * c:2 * c + 2]),
            clfT=np.ascontiguousarray(clfT_all[:, 2 * c:2 * c + 2]), cmkT=np.ascontiguousarray(cmkT_all[:, 2 * c:2 * c + 2]),
            cmv=np.ascontiguousarray(cmv_all[:, 2 * c:2 * c + 2]), prm=prm)
        in_maps.append(m)
    return in_maps


def kernel(**inp):
    in_maps = _prep_inputs(inp)
    if "nc" not in _NC_CACHE:
        _NC_CACHE["nc"] = build()
    nc = _NC_CACHE["nc"]
    res = run_bass_kernel_spmd(nc, in_maps, core_ids=list(range(NCORES)))
    R = res.results
    cat = lambda name: np.stack([np.asarray(r[name], dtype=np.float32) for r in R], axis=0)
    y_p = cat("y_p")
    y_s = cat("y_s").reshape(16, 16, D)
    fk = cat("ofk").transpose(1, 0, 2, 3).reshape(L, 8, 2048, 8, 64)
    fv = cat("ofv").transpose(1, 0, 2, 3).reshape(L, 8, 2048, 8, 64)
    lf = cat("olf").transpose(1, 0, 3, 2)
    mk = cat("omk").transpose(1, 0, 2, 3).reshape(L, 8, 256, 4, 64)
    mv = cat("omv").transpose(1, 0, 2, 3).reshape(L, 8, 256, 4, 64)
    cvp = cat("oconv").transpose(1, 0, 4, 3, 2).reshape(L, 8, 2, FF)
    sk = cat("osk").transpose(1, 0, 2, 3).reshape(L, 16, 16, 8, 64)
    sv = cat("osv").transpose(1, 0, 2, 3).reshape(L, 16, 16, 8, 64)
    slf = cat("oslf").transpose(1, 0, 3, 2).reshape(L, 16, 16, 8)
    gv = cat("ogv").transpose(1, 0, 2, 3).reshape(L, 16, 16, 256)
    cvs = cat("osconv").transpose(1, 0, 4, 5, 3, 2).reshape(L, 16, 2, FF)
    outs = (y_p, y_s, fk, fv, lf, mk, mv, cvp, sk, sv, slf, gv, cvs)
    return tuple(np.ascontiguousarray(o, dtype=np.float32) for o in outs)
```

```python
import numpy as np
import ml_dtypes
from contextlib import ExitStack
import concourse.bass as bass
import concourse.mybir as mybir
from concourse.bass_utils import run_bass_kernel_spmd

F32 = mybir.dt.float32
BF16 = mybir.dt.bfloat16
ALU = mybir.AluOpType
AF = mybir.ActivationFunctionType

ENGS = ("pe", "act", "dve", "pool", "sp")
STOP = None
DBG = set()


class _Stop(Exception):
    pass


def _chk(name):
    if STOP == name:
        raise _Stop()
NCORES = 8
L = 2
D = 1024
NT = 2080
FF = 2816
NJ = 22
EPS = 1e-6
NEG = -30000.0


class Op:
    __slots__ = ("eng", "fn", "idx", "deps", "dma_key", "marked", "seq", "cum")

    def __init__(self, eng, fn, idx, dma_key):
        self.eng = eng
        self.fn = fn
        self.idx = idx
        self.deps = ()
        self.dma_key = dma_key
        self.marked = False
        self.seq = 0
        self.cum = 0


class Sched:
    def __init__(self, nc):
        self.nc = nc
        self.streams = {e: [] for e in ENGS}
        self.last_writer = {}
        self.readers = {}
        self.dma_counts = {}
        self.dma_last = {}

    def op(self, eng, fn, reads=(), writes=(), dma_key=None):
        o = Op(eng, fn, len(self.streams[eng]), dma_key)
        deps = set()
        lw = self.last_writer
        rd = self.readers
        for t in reads:
            w = lw.get(t)
            if w is not None:
                deps.add(w)
            if type(t) is tuple and t[0] == "ps":
                for r in rd.get(t, ()):
                    if r.eng != eng:
                        deps.add(r)
        for t in writes:
            w = lw.get(t)
            if w is not None:
                deps.add(w)
            r = rd.get(t)
            if r:
                deps.update(r)
        for t in reads:
            rd.setdefault(t, []).append(o)
        for t in writes:
            lw[t] = o
            rd[t] = []
        deps.discard(o)
        if eng == "pe":
            deps = {d for d in deps if not (d.eng == "pe" and d.dma_key is None)}
        o.deps = deps
        if dma_key is not None:
            c = self.dma_counts.get(dma_key, 0) + 1
            self.dma_counts[dma_key] = c
            o.cum = 16 * c
            self.dma_last[dma_key] = o
        self.streams[eng].append(o)
        return o

    def barrier(self):
        lasts = [s[-1] for s in self.streams.values() if s]
        lasts += list(self.dma_last.values())
        for e in ENGS:
            o = Op(e, lambda eng: eng.nop(), len(self.streams[e]), None)
            o.deps = {d for d in lasts if not (d.eng == e and d.dma_key is None)}
            self.streams[e].append(o)

    def emit(self, stack):
        nc = self.nc
        for e in ENGS:
            for o in self.streams[e]:
                for d in o.deps:
                    if d.dma_key is None:
                        d.marked = True
        sems = {}
        for e in ENGS:
            n = 0
            for o in self.streams[e]:
                if o.dma_key is None and o.marked:
                    n += 1
                    o.seq = n
            if n:
                sems[e] = stack.enter_context(nc.semaphore("s_" + e))
        dsems = {}
        for k in self.dma_counts:
            dsems[k] = stack.enter_context(nc.semaphore("d%d" % len(dsems)))
        self.n_sems = len(sems) + len(dsems)
        final = {k: o.cum for k, o in self.dma_last.items()}

        def run(eng_name, eng):
            waited = {}
            for o in self.streams[eng_name]:
                need = {}
                for d in o.deps:
                    if d.dma_key is not None:
                        key = ("d", d.dma_key)
                        val = d.cum
                    else:
                        key = ("c", d.eng)
                        val = d.seq
                    if val > need.get(key, 0):
                        need[key] = val
                for key, val in need.items():
                    if waited.get(key, 0) >= val:
                        continue
                    waited[key] = val
                    s = dsems[key[1]] if key[0] == "d" else sems[key[1]]
                    eng.wait_ge(s, val)
                ins = o.fn(eng)
                if o.dma_key is not None:
                    ins.then_inc(dsems[o.dma_key], 16)
                elif o.marked:
                    ins.then_inc(sems[o.eng], 1)
            if eng_name == "sp":
                for k, v in final.items():
                    if waited.get(("d", k), 0) < v:
                        eng.wait_ge(dsems[k], v)

        with nc.Block() as block:
            @block.tensor
            def _(t):
                run("pe", t)

            @block.scalar
            def _(t):
                run("act", t)

            @block.vector
            def _(t):
                run("dve", t)

            @block.gpsimd
            def _(t):
                run("pool", t)

            @block.sync
            def _(t):
                run("sp", t)


def tile_rows(i):
    return 128 if i < 16 else 32


def tcols(i):
    return (128 * i, 128) if i < 16 else (2048, 32)


BLKS = [(0, 512), (512, 512), (1024, 512), (1536, 512), (2048, 32)]

PO = {}
_o = 0
for _n, _w in (("gpm", 8), ("ggo", 8), ("gmem", 8), ("gpf", 8), ("bs", 4), ("bss", 4),
               ("wdw", 66), ("bdw", 22), ("bf", 1), ("cconv", 88)):
    PO[_n] = _o
    _o += _w
PPL = _o
NPRM = PPL * L


def build():
    nc = bass.Bass("TRN2", target_bir_lowering=False)

    def din(name, shape, dt=F32):
        return nc.dram_tensor(name, list(shape), dt, kind="ExternalInput").ap()

    def dout(name, shape):
        return nc.dram_tensor(name, list(shape), F32, kind="ExternalOutput").ap()

    x_p = din("x_p", [2048, D])
    x_s = din("x_s", [32, D])
    mem = din("mem", [256, D])
    ckT = din("ckT", [L, 2, 8, 64, 1024])
    cv = din("cv", [L, 2, 1024, 512])
    clfT = din("clfT", [L, 2, 8, 1024])
    cmkT = din("cmkT", [L, 2, 4, 64, 256])
    cmv = din("cmv", [L, 2, 256, 256])
    w_in = din("w_in", [L, D, 2312])
    w_mkv = din("w_mkv", [L, D, 512])
    w_out = din("w_out", [L, D, D])
    w_up = din("w_up", [L, NJ, 128, 8, 256])
    w_dn = din("w_dn", [L, FF, D])
    g_post_mix = din("g_post_mix", [L, D])
    g_post_ffn = din("g_post_ffn", [L, D])
    g_sgu = din("g_sgu", [L, 256])
    wsT = din("wsT", [L, 4, 128, 128])
    prm_h = din("prm", [128, NPRM])
    cst_h = din("cst", [128, 288], BF16)

    y_p = dout("y_p", [2048, D])
    y_s = dout("y_s", [32, D])
    ofk = dout("ofk", [L, 2048, 512])
    ofv = dout("ofv", [L, 2048, 512])
    olf = dout("olf", [L, 8, 2048])
    omk = dout("omk", [L, 256, 256])
    omv = dout("omv", [L, 256, 256])
    oconv = dout("oconv", [L, 128, NJ, 2])
    osk = dout("osk", [L, 32, 512])
    osv = dout("osv", [L, 32, 512])
    oslf = dout("oslf", [L, 8, 32])
    ogv = dout("ogv", [L, 32, 256])
    osconv = dout("osconv", [L, 128, NJ, 2, 2])

    cscr_p = nc.dram_tensor("cscr_p", [8, 2, 2048], BF16).ap()
    cscr_s = nc.dram_tensor("cscr_s", [8, 2, 2, 1040], BF16).ap()

    with ExitStack() as st:
        S = Sched(nc)
        X = st.enter_context(nc.sbuf_tensor("X", [128, 17, D], F32))
        PRM = st.enter_context(nc.sbuf_tensor("PRM", [128, NPRM], F32))
        CST = st.enter_context(nc.sbuf_tensor("CST", [128, 288], BF16))
        WSTSt = st.enter_context(nc.sbuf_tensor("WSTS", [32, L * 4 * 32], BF16))
        PTSt = st.enter_context(nc.sbuf_tensor("PTS", [128, 3 * 32], BF16))
        SM = st.enter_context(nc.sbuf_tensor("SM", [128, 256], F32))
        ONEROW = st.enter_context(nc.sbuf_tensor("ONEROW", [128, 64], F32))
        remaining = nc.sbuf_bytes_remaining
        remaining = remaining() if callable(remaining) else remaining
        ARB = (remaining - 512) // 64 * 64
        AR = st.enter_context(nc.sbuf_tensor("AR", [128, ARB // 2], BF16))
        AR32 = AR.bitcast(F32)
        PS = [st.enter_context(nc.psum_tensor("ps%d" % k, [128, 512], F32)) for k in range(8)]
        PSB = [p.bitcast(BF16) for p in PS]

        IDENT = CST[:, 0:128]
        MASKNEG = CST[:, 128:256]
        MASKS = CST[0:32, 256:288]


        def WSTS(l, g):
            o = (l * 4 + g) * 32
            return WSTSt[0:32, o:o + 32]

        def prm(l, name, w, p0=0, p1=128, off=0):
            o = l * PPL + PO[name] + off
            return PRM[p0:p1, o:o + w]

        SM_SS, SM_SD, SM_RSTD = 0, 1, 2
        SM_R = 16
        SM_SSF = 70
        SM_T = 90

        def bf(off, n, p0=0, p1=128):
            return AR[p0:p1, off // 2: off // 2 + n]

        def f32(off, n, p0=0, p1=128):
            return AR32[p0:p1, off // 4: off // 4 + n]

        cur = [0]

        def carve(nbytes):
            o = cur[0]
            cur[0] = o + (nbytes + 63) // 64 * 64
            return o

        HT_O = carve(8 * NT * 2)
        MX_O = carve(8 * NT * 2)
        VA_O = carve(17 * 8 * 65 * 2)
        QK_O = carve(4 * NT * 2)
        WS_O = [carve(8320) for _ in range(3)]
        SCR_O = carve(8192)
        MKT_O = carve(2048)
        MVA_O = carve(1040)
        MKTt = AR[:, MKT_O // 2:MKT_O // 2 + 1024]
        MVAt = AR[:, MVA_O // 2:MVA_O // 2 + 520]
        MIX_END = cur[0]
        assert MIX_END <= ARB, (MIX_END, ARB)

        def HT(c, c0, n):
            return bf(HT_O + (c * NT + c0) * 2, n)

        def HT3(c0, n):
            return AR[:, HT_O // 2: HT_O // 2 + 8 * NT].rearrange("p (c t) -> p c t", c=8)[:, :, c0:c0 + n]

        def MX(c, c0, n, p0=0, p1=128):
            return bf(MX_O + (c * NT + c0) * 2, n, p0, p1)

        def VA(j, h, nk=128):
            return bf(VA_O + ((j * 8 + h) * 65) * 2, 65, 0, nk)

        def VA128(j, h):
            return bf(VA_O + ((j * 8 + h) * 65) * 2, 128, 0, 128)

        def QK(slot, c0, n, p0, p1):
            return bf(QK_O + (slot * NT + c0) * 2, n, p0, p1)

        CT = lambda c0, n: f32(QK_O + c0 * 4, n, 96, 104)
        HI = lambda c0, n: bf(QK_O + 8320 + c0 * 2, n, 96, 104)
        LO = lambda c0, n: bf(QK_O + 8320 + 4160 + c0 * 2, n, 96, 104)

        def WS(s, c, c0, n):
            return bf(WS_O[s] + (c * 520 + c0) * 2, n)

        def WS3(s, ncol):
            return AR[:, WS_O[s] // 2: WS_O[s] // 2 + 8 * 520].rearrange("p (c n) -> p c n", c=8)[:, :, 0:ncol]

        def SCRb(g, n, p0=0, p1=128, off=0):
            return bf(SCR_O + g * 1024 + off * 2, n, p0, p1)

        def SCRf(g, n, p0=0, p1=128, off=0):
            return f32(SCR_O + g * 1024 + off * 4, n, p0, p1)

        def scr(*gs):
            return [("scr", g) for g in gs]

        def dma(eng, out, in_, reads=(), writes=(), key=None, slow=False):
            if slow:
                fn = lambda e: e.dma_start(out=out, in_=in_, allow_slow_non_contiguous=True)
            else:
                fn = lambda e: e.dma_start(out=out, in_=in_)
            return S.op(eng, fn, reads=reads, writes=writes, dma_key=key)

        psr = {"g": [0, 1], "s": [2, 3, 4], "o": [5, 6], "t": [7]}
        psi = {k: 0 for k in psr}

        def psnext(pool):
            lst = psr[pool]
            k = lst[psi[pool] % len(lst)]
            psi[pool] += 1
            return k

        def mm_group(out_fn, items, reads, writes):
            def fn(e):
                ins = None
                for (o, a, b, s0, s1) in items:
                    ins = e.matmul(o, a, b, start=s0, stop=s1, skip_group_check=True)
                return ins
            return S.op("pe", fn, reads=reads, writes=writes)

        dma("sp", PRM[:, :], prm_h, writes=["prm"], key="prm")
        dma("sp", CST[:, :], cst_h, writes=["cst"], key="cst")
        S.op("dve", lambda e: e.memset(WSTSt[:, :], 0.0), writes=["wsts"])
        for l in range(L):
            for b in range(2):
                dma("pool",
                    WSTSt[16 * b:16 * b + 16, l * 128:(l + 1) * 128].rearrange("p (g i) -> p g i", g=4)[:, :, 16 * b:16 * b + 16],
                    wsT[l, :, 0:16, 0:16].rearrange("g j i -> j g i"),
                    reads=["wsts"], writes=[("wsts", l, b)], key=("wsts", l), slow=True)
        S.op("dve", lambda e: e.memset(PTSt[:, :], 0.0), writes=["pts0", "pts1", "pts2"])
        S.op("dve", lambda e: e.memset(SM[:, 200:201], 1.0), writes=["one"])
        S.op("dve", lambda e: e.memset(ONEROW[:, :], 1.0), writes=["onerow"])
        ONE8 = SM[96:104, 200:201]

        for i in range(16):
            dma("sp", X[:, i, :], x_p[128 * i:128 * i + 128, :], writes=[("x", i)], key=("x", i))
        dma("sp", X[0:32, 16, :], x_s, writes=[("x", 16)], key=("x", 16))

        def prenorm_tile(src, rows, xtok, gcol, dst3, dst_tok, slot, extra_writes=(), hb=None, tpool="t", hb_toks=None, defer=False):
            if hb is None:
                hb = SCRb(2 * slot, 1024, 0, rows)
            hbt = list(hb_toks) if hb_toks is not None else scr(2 * slot, 2 * slot + 1)
            c0 = 3 * slot
            ss = SM[0:rows, c0:c0 + 1]
            sd = SM[0:rows, c0 + 1:c0 + 2]
            rs = SM[0:rows, c0 + 2:c0 + 3]
            S.op("act", lambda e: e.activation(hb, src, AF.Square, accum_out=ss),
                 reads=[xtok], writes=hbt + [("sm", c0)])
            S.op("act", lambda e: e.activation(sd, ss, AF.Ln, bias=EPS, scale=1.0 / D),
                 reads=[("sm", c0)], writes=[("sm", c0 + 1)])
            S.op("act", lambda e: e.activation(rs, sd, AF.Exp, scale=-0.5), reads=[("sm", c0 + 1)], writes=[("sm", c0 + 2)])
            S.op("act", lambda e: e.activation(hb, src, AF.Copy, scale=rs),
                 reads=[xtok, ("sm", c0 + 2)], writes=hbt)
            def part_b():
                k = psnext(tpool)

                def tr(e):
                    ins = None
                    for c in range(8):
                        ins = e.transpose(PSB[k][:, c * 128:c * 128 + rows], hb[:, c * 128:(c + 1) * 128],
                                          IDENT[0:rows, 0:rows])
                    return ins
                S.op("pe", tr, reads=hbt + ["cst"], writes=[("ps", k)])
                src3 = PSB[k][:, 0:1024].rearrange("p (c t) -> p c t", c=8)[:, :, 0:rows]
                S.op("dve", lambda e: e.tensor_tensor(dst3, src3, gcol.unsqueeze(2).to_broadcast([128, 8, rows]), ALU.mult),
                     reads=[("ps", k), "prm"], writes=[dst_tok] + list(extra_writes))
            if defer:
                return part_b
            part_b()

        PVQ = []

        def flush_pv():
            while PVQ:
                PVQ.pop(0)()

        def attend(qa_fn, K, ktiles, osubs, obank, tag):
            pendq = []
            first_done = [False]

            def pv(kt_i, kt, pt_ap, pt_tok):
                items = []
                if osubs == "fm":
                    lo_, hi_ = kt["lo"], kt["hi"]
                    mrows = kt["va"].shape[1]
                    items.append((PS[obank][0:mrows, lo_:hi_], kt["va"], pt_ap[:, lo_:hi_], kt_i == 0, kt_i == len(ktiles) - 1))
                    mm_group(None, items, reads=[pt_tok] + kt["vreads"], writes=[("ps", obank)])
                    return
                for (o_ap, c0, c1, fk, lk) in osubs:
                    if fk <= kt_i <= lk and kt["pv_lo"] <= c0:
                        items.append((o_ap, pt_ap[:, c0:c1], kt["va"], not first_done[0], kt_i == lk))
                        first_done[0] = True
                if items:
                    mm_group(None, items, reads=[pt_tok] + kt["vreads"], writes=[("ps", obank)])

            for kt_i, kt in enumerate(ktiles):
                k = psnext("s")
                nk, lo, hi = kt["nk"], kt["lo"], kt["hi"]
                items = []
                if kt["mask"] is not None:
                    mask_ap, mc = kt["mask"]
                    items.append((PS[k][0:nk, lo:lo + mc], kt["ka"], qa_fn(lo, mc), True, False))
                    items.append((PS[k][0:nk, lo:lo + mc], IDENT[0:nk, 0:nk], mask_ap, False, True))
                    if hi > lo + mc:
                        items.append((PS[k][0:nk, lo + mc:hi], kt["ka"], qa_fn(lo + mc, hi - lo - mc), True, True))
                else:
                    items.append((PS[k][0:nk, lo:hi], kt["ka"], qa_fn(lo, hi - lo), True, True))
                mm_group(None, items, reads=kt["reads"] + ["cst"], writes=[("ps", k)])
                if kt["pt"] == "rot":
                    g = attend.ptc % 4
                    attend.ptc += 1
                    pt_ap = SCRb(g, 512, 0, nk)
                    pt_tok = ("scr", g)
                else:
                    pt_ap = PTSt[0:nk, 32 * kt["pt"]:32 * kt["pt"] + 32]
                    pt_tok = "pts%d" % kt["pt"]
                S.op("act", lambda e, pt_ap=pt_ap, k=k, nk=nk, lo=lo, hi=hi:
                     e.activation(pt_ap[:, lo:hi], PS[k][0:nk, lo:hi], AF.Exp),
                     reads=[("ps", k)], writes=[pt_tok])
                if kt["pt"] == "rot":
                    if len(PVQ) >= 2:
                        PVQ.pop(0)()
                    PVQ.append(lambda kt_i=kt_i, kt=kt, pt_ap=pt_ap, pt_tok=pt_tok: pv(kt_i, kt, pt_ap, pt_tok))
                else:
                    flush_pv()
                    pv(kt_i, kt, pt_ap, pt_tok)
        attend.ptc = 0

        def cs_all():
            return ["cw"]

        def post_norm_residual(ostg, rows, i, junk, junk_toks, GPap, gp_toks, o_toks):
            ssc = SM[0:rows, SM_T + 8:SM_T + 9]
            S.op("act", lambda e: e.activation(junk, ostg, AF.Square, accum_out=ssc),
                 reads=o_toks, writes=list(junk_toks) + ["pn"])
            S.op("act", lambda e: e.activation(ssc, ssc, AF.Ln, bias=EPS, scale=1.0 / D), reads=["pn"], writes=["pn"])
            S.op("act", lambda e: e.activation(ssc, ssc, AF.Exp, scale=-0.5), reads=["pn"], writes=["pn"])
            S.op("dve", lambda e: e.scalar_tensor_tensor(ostg, ostg, ssc, GPap[0:rows, :], ALU.mult, ALU.mult),
                 reads=list(o_toks) + list(gp_toks) + ["pn"], writes=o_toks)
            S.op("pool", lambda e: e.tensor_tensor(X[0:rows, i, :], X[0:rows, i, :], ostg, ALU.add),
                 reads=list(o_toks) + [("x", i)], writes=[("x", i)])

        def ffn_phase(l):
            psr.clear()
            psr.update({"u": [0, 1, 2, 4, 5, 6], "t": [3], "d": [4, 5, 6, 7]})
            for kk in psr:
                psi[kk] = 0
            HC = 1056
            o = [0]

            def cv_(nb):
                r = o[0]
                o[0] = r + (nb + 63) // 64 * 64
                return r
            H2_O = cv_(8 * HC * 2)
            ACT_O = cv_(NJ * HC * 2)
            WDN_O = cv_(NJ * 1024 * 2)
            WU_O = [cv_(4096) for _ in range(2)]
            AE_O = [cv_(2064) for _ in range(2)]
            LIN_O = [cv_(1024) for _ in range(2)]
            CONV_O = [cv_(2048) for _ in range(2)]
            OSTG_O = cv_(4096)
            GPF_O = cv_(4096)
            HBF_O = cv_(2048)
            SAVE_O = cv_(NJ * 2 * 4)
            SAVES_O = cv_(NJ * 4 * 4)
            assert o[0] <= ARB, (o[0], ARB)

            H2 = lambda c, c0, n: bf(H2_O + (c * HC + c0) * 2, n)
            H23 = lambda c0, n: AR[:, H2_O // 2:H2_O // 2 + 8 * HC].rearrange("p (c t) -> p c t", c=8)[:, :, c0:c0 + n]
            ACTB = lambda j, c0, n: bf(ACT_O + (j * HC + c0) * 2, n)
            WDN = lambda j, c0, n: bf(WDN_O + (j * 1024 + c0) * 2, n)
            WDN3 = AR[:, WDN_O // 2:WDN_O // 2 + NJ * 1024].rearrange("p (j n) -> p j n", j=NJ)
            WU = lambda s, c, c0, n: bf(WU_O[s] + (c * 256 + c0) * 2, n)
            WU3 = lambda s: AR[:, WU_O[s] // 2:WU_O[s] // 2 + 2048].rearrange("p (c n) -> p c n", c=8)
            OSTG = f32(OSTG_O, 1024)
            GPF = f32(GPF_O, 1024)
            HBF = bf(HBF_O, 1024)
            SAVE = f32(SAVE_O, NJ * 2)
            SAVES = f32(SAVES_O, NJ * 4)

            dma("sp", GPF, g_post_ffn[l:l + 1, :].broadcast_to([128, D]), writes=["gpf"], key="gp")
            wdn_src = w_dn[l].rearrange("(j p) n -> p j n", p=128)
            for part in range(2):
                dma("pool", WDN3[:, 11 * part:11 * part + 11, :], wdn_src[:, 11 * part:11 * part + 11, :],
                    writes=[("wdn", part)], key=("wdn", part))
            aecnt = 0

            def wu_load(j, s_):
                dma("pool", WU3(s_), w_up[l, j], writes=[("wu", s_, 0), ("wu", s_, 1)], key=("wu", s_, 0))

            for half in range(2):
                tiles = list(range(0, 8)) if half == 0 else list(range(8, 17))
                base = 0 if half == 0 else 1024
                blocks = [(0, 512), (512, 512)] + ([(1024, 32)] if half == 1 else [])
                h2toks = [("h2", (i if i < 8 else i - 8)) for i in tiles]
                if half == 0:
                    for i in tiles:
                        rows = tile_rows(i)
                        c0, n = tcols(i)
                        prenorm_tile(X[0:rows, i, :], rows, ("x", i), prm(l, "gpf", 8), H23(c0 - base, n), ("h2", (i if i < 8 else i - 8)), 0,
                                     hb=HBF[0:rows, :], tpool="t", hb_toks=["hbf"])
                wu_load(0, 0)
                pend_tail = None
                for j in range(NJ):
                    s = j % 2
                    if j + 1 < NJ:
                        wu_load(j + 1, (j + 1) % 2)
                    w0 = prm(l, "wdw", 1, 0, 128, 3 * j)
                    w1 = prm(l, "wdw", 1, 0, 128, 3 * j + 1)
                    w2 = prm(l, "wdw", 1, 0, 128, 3 * j + 2)
                    bd = prm(l, "bdw", 1, 0, 128, j)
                    for bidx, (lc0, n) in enumerate(blocks):
                        sample = (n == 32)
                        ka = psnext("u")
                        mm_group(None, [(PS[ka][:, 0:n], WU(s, c, 0, 128), H2(c, lc0, n), c == 0, c == 7) for c in range(8)],
                                 reads=h2toks + [("wu", s, 0)], writes=[("ps", ka)])
                        kl = psnext("u")
                        mm_group(None, [(PS[kl][:, 0:n], WU(s, c, 128, 128), H2(c, lc0, n), c == 0, c == 7) for c in range(8)],
                                 reads=h2toks + [("wu", s, 1)], writes=[("ps", kl)])
                        p = aecnt % 2
                        aecnt += 1
                        lin = bf(LIN_O[p], 512)
                        conv = f32(CONV_O[p], 512)
                        if sample:
                            AEs = f32(AE_O[p], 36).rearrange("p (b t) -> p b t", b=2)
                            S.op("dve", lambda e, AEs=AEs, j=j: e.tensor_copy(
                                AEs[:, :, 0:2], prm(l, "cconv", 4, 0, 128, 4 * j).rearrange("p (b r) -> p b r", b=2)),
                                reads=["prm"], writes=[("aeh", p)])
                            S.op("act", lambda e, AEs=AEs, ka=ka: e.copy(AEs[:, :, 2:18], PS[ka][:, 0:32].rearrange("p (b t) -> p b t", b=2)),
                                 reads=[("ps", ka)], writes=[("ae", p)])
                            taps = [AEs[:, :, k:k + 16] for k in range(3)]
                            cv3 = conv[:, 0:32].rearrange("p (b t) -> p b t", b=2)
                            psa = PS[ka][:, 0:32].rearrange("p (b t) -> p b t", b=2)
                            sil_out = f32(AE_O[p] + 256, 32)
                            sil_in = conv[:, 0:32]
                        else:
                            AEp = f32(AE_O[p], 514)
                            if bidx == 0 and half == 0:
                                S.op("dve", lambda e, AEp=AEp: e.memset(AEp[:, 0:2], 0.0), writes=[("aeh", p)])
                            elif bidx == 0:
                                S.op("dve", lambda e, AEp=AEp, j=j: e.tensor_copy(AEp[:, 0:2], SAVE[:, 2 * j:2 * j + 2]),
                                     reads=[("save", j)], writes=[("aeh", p)])
                            else:
                                prev = f32(AE_O[1 - p], 514)
                                S.op("dve", lambda e, AEp=AEp, prev=prev: e.tensor_copy(AEp[:, 0:2], prev[:, 512:514]),
                                     reads=[("aet", 1 - p)], writes=[("aeh", p)])
                            S.op("act", lambda e, AEp=AEp, ka=ka: e.copy(AEp[:, 2:514], PS[ka][:, 0:512]),
                                 reads=[("ps", ka)], writes=[("ae", p), ("aet", p)])
                            taps = [AEp[:, k:k + 512] for k in range(3)]
                            cv3 = conv
                            psa = PS[ka][:, 0:512]
                            sil_out = AEp[:, 0:512]
                            sil_in = conv
                        S.op("act", lambda e, cv3=cv3, psa=psa, w2=w2, bd=bd: e.activation(cv3, psa, AF.Identity, bias=bd, scale=w2),
                             reads=[("ps", ka), "prm"], writes=[("conv", p)])
                        S.op("act", lambda e, lin=lin, kl=kl, n=n: e.copy(lin[:, 0:n], PS[kl][:, 0:n]),
                             reads=[("ps", kl)], writes=[("lin", p)])
                        S.op("dve", lambda e, cv3=cv3, taps=taps, w1=w1: e.scalar_tensor_tensor(cv3, taps[1], w1, cv3, ALU.mult, ALU.add),
                             reads=[("ae", p), ("aeh", p), ("conv", p)], writes=[("conv", p)])
                        S.op("dve", lambda e, cv3=cv3, taps=taps, w0=w0: e.scalar_tensor_tensor(cv3, taps[0], w0, cv3, ALU.mult, ALU.add),
                             reads=[("ae", p), ("aeh", p), ("conv", p)], writes=[("conv", p)])
                        if not sample and bidx == 1:
                            S.op("dve", lambda e, AEp=AEp, j=j: e.tensor_copy(SAVE[:, 2 * j:2 * j + 2], AEp[:, 512:514]),
                                 reads=[("aet", p)], writes=[("save", j)])
                        if sample:
                            S.op("dve", lambda e, AEs=AEs, j=j: e.tensor_copy(
                                SAVES[:, 4 * j:4 * j + 4].rearrange("p (b r) -> p b r", b=2), AEs[:, :, 16:18]),
                                reads=[("ae", p)], writes=[("saves", j)])

                        def tail(p=p, sil_out=sil_out, sil_in=sil_in, j=j, lc0=lc0, n=n, lin=lin, half=half):
                            S.op("act", lambda e: e.activation(sil_out, sil_in, AF.Silu),
                                 reads=[("conv", p), ("ae", p), ("aeh", p)], writes=[("ae", p), ("aeh", p)])
                            S.op("pool", lambda e: e.tensor_tensor(ACTB(j, lc0, n), sil_out, lin[:, 0:n], ALU.mult),
                                 reads=[("ae", p), ("aeh", p), ("lin", p)], writes=[("actb", j, half)])
                        if pend_tail is not None:
                            pend_tail()
                        pend_tail = tail
                if pend_tail is not None:
                    pend_tail()
                    pend_tail = None
                if half == 1:
                    dma("sp", oconv[l], SAVE.rearrange("p (j r) -> p j r", r=2), reads=[("save", j) for j in range(NJ)], key="oconv")
                    dma("sp", osconv[l], SAVES.rearrange("p (j b r) -> p j b r", b=2, r=2), reads=[("saves", j) for j in range(NJ)], key="osconv")
                atoks = [("actb", j, half) for j in range(NJ)]
                nxt = list(range(8, 17)) if half == 0 else []
                for ti_, i in enumerate(tiles):
                    rows = tile_rows(i)
                    c0, n = tcols(i)
                    lc0 = c0 - base
                    defer_b = []
                    for i2 in (nxt[ti_:ti_ + 1] if ti_ < 7 else nxt[7:8]):
                        rows2 = tile_rows(i2)
                        c02, n2 = tcols(i2)
                        defer_b.append(prenorm_tile(X[0:rows2, i2, :], rows2, ("x", i2), prm(l, "gpf", 8), H23(c02 - 1024, n2),
                                                    ("h2", i2 - 8), 0, hb=HBF[0:rows2, :], tpool="t", hb_toks=["hbf"], defer=True))
                        if "nodefer" in DBG:
                            defer_b.pop()()
                    for hf in range(2):
                        kd = psnext("d")
                        mm_group(None, [(PS[kd][0:rows, :], ACTB(j, lc0, n), WDN(j, 512 * hf, 512), j == 0, j == NJ - 1) for j in range(NJ)],
                                 reads=atoks + [("wdn", 0), ("wdn", 1)], writes=[("ps", kd)])
                        if hf == 0:
                            S.op("act", lambda e, kd=kd, rows=rows: e.copy(OSTG[0:rows, 0:512], PS[kd][0:rows, :]),
                                 reads=[("ps", kd)], writes=["ostg0"])
                        else:
                            S.op("dve", lambda e, kd=kd, rows=rows: e.tensor_copy(OSTG[0:rows, 512:1024], PS[kd][0:rows, :]),
                                 reads=[("ps", kd)], writes=["ostg1"])
                    for fb in defer_b:
                        fb()
                    post_norm_residual(OSTG[0:rows, :], rows, i, HBF[0:rows, :], ["hbf"], GPF, ["gpf"], ["ostg0", "ostg1"])
                if half == 0:
                    prenorm_tile(X[0:32, 16, :], 32, ("x", 16), prm(l, "gpf", 8), H23(1024, 32), ("h2", 8), 0,
                                 hb=HBF[0:32, :], tpool="t", hb_toks=["hbf"])
        RFOX = lambda i, rows=128: SM[0:rows, 100 + i:101 + i]
        RSGU = lambda i, rows=128: SM[0:rows, 120 + i:121 + i]
        RMEM = lambda i, rows=128: SM[0:rows, 140 + i:141 + i]
        SSFOX = lambda i, rows=128: SM[0:rows, 160 + i:161 + i]
        SSMEM = lambda i, rows=128: SM[0:rows, 180 + i:181 + i]
        ALLHT = [("ht", i) for i in range(17)]

        class NSQ:
            def __init__(self):
                self.p1 = None
                self.p2 = None
                self.p3 = None

            def step(self, new_p1):
                if self.p3 is not None:
                    self.p3()
                    self.p3 = None
                if self.p2 is not None:
                    self.p3 = self.p2()
                    self.p2 = None
                if self.p1 is not None:
                    self.p2 = self.p1()
                self.p1 = new_p1

            def flush(self):
                self.step(None)
                self.step(None)
                self.step(None)
        nsq = NSQ()

        def wload(slot, src2d, ncol, extra_writes=()):
            dma("pool", WS3(slot, ncol), src2d.rearrange("(c p) n -> p c n", p=128),
                writes=[("ws", slot)] + list(extra_writes), key=("ws", slot))

        def KAS(b, c0, n, p0=0, p1=67):
            return bf(WS_O[0] + b * 3200 + c0 * 2, n, p0, p1)

        def VAS(b, j):
            return bf(WS_O[0] + b * 3200 + 2080 + j * 130, 65)

        def VAS3(b):
            return AR[:, (WS_O[0] + b * 3200 + 2080) // 2:(WS_O[0] + b * 3200 + 2080) // 2 + 520].rearrange("p (j k) -> p j k", k=65)

        SC_TOKS = [("sc", b, x) for b in range(2) for x in ("k", "v", "r", "one")]

        def norm_store_fm(obank, I, grp, chunk, p0, ssq4, l):
            rlrow = f32(SCR_O + 6 * 1024, 512, 64, 65)
            hfT = f32(SCR_O + 6 * 1024, 512, 0, 64)
            S.op("act", lambda e: e.activation(rlrow, PS[obank][64:65, 0:512], AF.Ln), reads=[("ps", obank)], writes=[("scr", 6, "r")])
            S.op("act", lambda e: e.activation(rlrow, rlrow, AF.Exp, scale=-1.0), reads=[("scr", 6, "r")], writes=[("scr", 6, "r")])
            kb = psnext("g")
            S.op("pe", lambda e: e.matmul(PS[kb][0:64, 0:512], ONEROW[64:65, 0:64], rlrow, start=True, stop=True, skip_group_check=True),
                 reads=[("scr", 6, "r"), "onerow"], writes=[("ps", kb)])
            S.op("dve", lambda e: e.tensor_copy(hfT, PS[kb][0:64, 0:512]), reads=[("ps", kb)], writes=scr(6, 7))
            S.op("dve", lambda e: e.tensor_tensor(hfT, PS[obank][0:64, 0:512], hfT, ALU.mult), reads=[("ps", obank)] + scr(6, 7), writes=scr(6, 7))
            return lambda: norm_store_fm2(I, grp, chunk, p0, ssq4, l)

        def norm_store_fm2(I, grp, chunk, p0, ssq4, l):
            hfT = f32(SCR_O + 6 * 1024, 512, 0, 64)
            sqb = bf(SCR_O + 4 * 1024, 512, 0, 64)
            S.op("act", lambda e: e.activation(MX(chunk, 512 * I, 512, p0, p0 + 64), hfT, AF.Copy,
                                               scale=prm(l, "ggo", 1, p0, p0 + 64, chunk)),
                 reads=scr(6, 7) + ["prm"], writes=[("mx", chunk, 4 * I + s_, p0) for s_ in range(4)])
            S.op("act", lambda e: e.activation(sqb, hfT, AF.Square), reads=scr(6, 7), writes=scr(4))
            return lambda: norm_store_fm3(I, grp, ssq4)

        def norm_store_fm3(I, grp, ssq4):
            sqb = bf(SCR_O + 4 * 1024, 512, 0, 64)
            kt = psnext("t")
            ones_col = bf(VA_O + 64 * 2, 1, 0, 64)

            def ssmm(e):
                ins = None
                for s_ in range(4):
                    ins = e.matmul(PS[kt][:, s_:s_ + 1], sqb[:, 128 * s_:128 * s_ + 128], ones_col, start=True, stop=True,
                                   skip_group_check=True)
                return ins
            S.op("pe", ssmm, reads=scr(4) + ["va_ones"], writes=[("ps", kt)])
            toks4 = [("ssq", grp, 4 * I + s_) for s_ in range(4)]
            S.op("dve", lambda e: e.tensor_tensor(ssq4, ssq4, PS[kt][:, 0:4], ALU.add), reads=[("ps", kt)] + toks4, writes=toks4)

        def norm_and_store_block(obank, I, grp, chunk, p0, ssq4, l):
            O4 = PS[obank][:, 0:260].rearrange("p (s k) -> p s k", k=65)
            rl4 = SM[:, SM_T + 10:SM_T + 14]
            sq4 = SM[:, SM_T + 14:SM_T + 18]
            S.op("dve", lambda e: e.reciprocal(rl4.unsqueeze(2), O4[:, :, 64:65]), reads=[("ps", obank)], writes=["rl4"])
            hf4 = SCRf(6, 256)
            hb4 = SCRb(7, 256)
            S.op("dve", lambda e: e.tensor_tensor(hf4.rearrange("p (s k) -> p s k", k=64), O4[:, :, 0:64],
                                                  rl4.unsqueeze(2).to_broadcast([128, 4, 64]), ALU.mult),
                 reads=[("ps", obank), "rl4"], writes=scr(6))

            def sqs(e):
                ins = None
                for s_ in range(4):
                    ins = e.activation(hb4[:, 64 * s_:64 * s_ + 64], hf4[:, 64 * s_:64 * s_ + 64], AF.Square,
                                       accum_out=sq4[:, s_:s_ + 1])
                return ins
            S.op("act", sqs, reads=scr(6), writes=scr(7) + ["sq4"])
            toks4 = [("ssq", grp, 4 * I + s_) for s_ in range(4)]
            S.op("dve", lambda e: e.tensor_tensor(ssq4, ssq4, sq4, ALU.add), reads=["sq4"] + toks4, writes=toks4)
            S.op("act", lambda e: e.copy(hb4, hf4), reads=scr(6, 7), writes=scr(7))
            kt = psnext("t")

            def tr(e):
                ins = None
                for s_ in range(4):
                    ins = e.transpose(PSB[kt][p0:p0 + 64, 128 * s_:128 * s_ + 128], hb4[:, 64 * s_:64 * s_ + 64], IDENT[:, :])
                return ins
            S.op("pe", tr, reads=scr(7) + ["cst"], writes=[("ps", kt)])
            S.op("dve", lambda e: e.tensor_scalar(MX(chunk, 512 * I, 512, p0, p0 + 64), PSB[kt][p0:p0 + 64, 0:512],
                                                  prm(l, "ggo", 1, p0, p0 + 64, chunk), None, ALU.mult),
                 reads=[("ps", kt), "prm"], writes=[("mx", chunk, 4 * I + s_, p0) for s_ in range(4)])

        def norm_and_store(obank, ocol, rows, tile_i, grp, chunk, p0, ssq_acc, l):
            rl = SM[0:rows, SM_T + 4:SM_T + 5]
            S.op("dve", lambda e: e.reciprocal(rl, PS[obank][0:rows, ocol + 64:ocol + 65]),
                 reads=[("ps", obank)], writes=["rl"])
            hf = SCRf(5, 64, 0, rows)
            hb = SCRb(5, 64, 0, rows, off=256)
            S.op("dve", lambda e: e.tensor_scalar(hf, PS[obank][0:rows, ocol:ocol + 64], rl, None, ALU.mult),
                 reads=[("ps", obank), "rl"], writes=scr(5))
            sq = SM[0:rows, SM_T + 5:SM_T + 6]
            S.op("act", lambda e: e.activation(hb, hf, AF.Square, accum_out=sq), reads=scr(5), writes=scr(5) + ["sq"])
            S.op("dve", lambda e: e.tensor_tensor(ssq_acc, ssq_acc, sq, ALU.add),
                 reads=["sq", ("ssq", grp, tile_i)], writes=[("ssq", grp, tile_i)])
            S.op("act", lambda e: e.copy(hb, hf), reads=scr(5), writes=scr(5))
            kt = psnext("t")
            c0, n = tcols(tile_i)
            S.op("pe", lambda e: e.transpose(PSB[kt][p0:p0 + 64, 0:rows], hb, IDENT[0:rows, 0:rows]),
                 reads=scr(5) + ["cst"], writes=[("ps", kt)])
            S.op("dve", lambda e: e.tensor_scalar(MX(chunk, c0, n, p0, p0 + 64), PSB[kt][p0:p0 + 64, 0:rows],
                                                  prm(l, "ggo", 1, p0, p0 + 64, chunk), None, ALU.mult),
                 reads=[("ps", kt), "prm"], writes=[("mx", chunk, tile_i, p0)])

        try:
            for l in range(L):
                psr.clear()
                psr.update({"g": [0, 1], "s": [2, 3, 4], "o": [5, 6], "t": [7], "w": [0, 1, 2, 3, 4, 5]})
                for kk in psr:
                    psi[kk] = 0
                S.op("dve", lambda e: e.memset(
                    AR[:, VA_O // 2: VA_O // 2 + 17 * 8 * 65].rearrange("p (a k) -> p a k", k=65)[:, :, 64:65], 1.0),
                    writes=["va_ones"])
                S.op("dve", lambda e: e.memset(MVAt[:, :].rearrange("p (a k) -> p a k", k=65)[:, :, 64:65], 1.0),
                     writes=["mva_ones"])
                wload(0, w_in[l][:, 1024:1536], 512, extra_writes=SC_TOKS if l > 0 else ())
                wload(1, w_in[l][:, 512:1024], 512)
                wload(2, w_in[l][:, 1536:2056], 520)

                SGC = MX_O + 12288
                WSTl = bf(SGC, 512)
                GSl = f32(SGC + 1024, 256)
                dma("pool", WSTl.rearrange("p (g i) -> p g i", g=4), wsT[l].rearrange("g j i -> j g i"),
                    writes=[("sguc", 0)], key="wst")
                S.op("dve", lambda e: e.memset(bf(SGC, 512, 64, 128).rearrange("p (g i) -> p g i", g=4)[:, :, 0:64], 0.0),
                     reads=[("sguc", 0)], writes=[("sguc", 0)])
                dma("sp", GSl, g_sgu[l:l + 1, :].broadcast_to([128, 256]), writes=[("sguc", 1)], key="gs")

                def sgu_ops(i):
                    st_ = 0 if "setA" in DBG else i % 2
                    base = MX_O if st_ == 0 else MX_O + 6144
                    tk = "sguA" if st_ == 0 else "sguB"
                    Sb = lambda g, n, p0=0, p1=128: bf(base + g * 1024, n, p0, p1)
                    Sf = lambda g, n, p0=0, p1=128: f32(base + g * 1024, n, p0, p1)
                    sc = lambda *gs: [(tk, g) for g in gs]
                    rows = tile_rows(i)
                    c0, n = tcols(i)
                    s1, s2 = [], []
                    k = psnext("s")
                    z = Sf(0, 512, 0, rows)
                    t1 = Sf(2, 512, 0, rows)
                    ssv = SM[0:rows, SM_T + 20 + 2 * st_:SM_T + 21 + 2 * st_]
                    sss = SM[0:rows, SM_T + 21 + 2 * st_:SM_T + 22 + 2 * st_]
                    ssvt, ssst = ("ssv", st_), ("sss", st_)
                    vvf = Sf(3, 256, 0, rows)
                    vvb = Sb(2, 256, 0, rows)
                    sg = Sf(4, 256, 0, rows)
                    sgb = Sb(5, 256, 0, rows)
                    s1.append(lambda: mm_group(None, [(PS[k][0:rows, :], HT(c, c0, n), WS(2, c, 8, 512), c == 0, c == 7) for c in range(8)],
                                               reads=[("ht", i), ("ws", 2)], writes=[("ps", k)]))
                    s1.append(lambda: S.op("act", lambda e: e.copy(z, PS[k][0:rows, :]), reads=[("ps", k)], writes=sc(0, 1)))
                    s1.append(lambda: S.op("dve", lambda e: e.tensor_tensor(t1, z, z, ALU.mult), reads=sc(0, 1), writes=sc(2, 3)))
                    s1.append(lambda: S.op("dve", lambda e: e.tensor_scalar(t1, t1, 0.044715, 1.0, ALU.mult, ALU.add), reads=sc(2, 3), writes=sc(2, 3)))
                    s1.append(lambda: S.op("dve", lambda e: e.tensor_tensor(t1, t1, z, ALU.mult), reads=sc(0, 1, 2, 3), writes=sc(2, 3)))
                    s1.append(lambda: S.op("act", lambda e: e.activation(t1, t1, AF.Exp, scale=-1.5957691216057308), reads=sc(2, 3), writes=sc(2, 3)))
                    s1.append(lambda: S.op("act", lambda e: e.activation(t1, t1, AF.Ln, bias=1.0, scale=1.0), reads=sc(2, 3), writes=sc(2, 3)))
                    s1.append(lambda: S.op("act", lambda e: e.activation(t1, t1, AF.Exp, scale=-1.0), reads=sc(2, 3), writes=sc(2, 3)))
                    s1.append(lambda: S.op("dve", lambda e: e.tensor_tensor(z, z, t1, ALU.mult), reads=sc(0, 1, 2, 3), writes=sc(0, 1)))
                    s1.append(lambda: S.op("act", lambda e: e.activation(t1[:, 0:256], z[:, 256:512], AF.Square, accum_out=ssv),
                                           reads=sc(0, 1), writes=sc(2) + [ssvt]))
                    s1.append(lambda: S.op("act", lambda e: e.activation(ssv, ssv, AF.Ln, bias=EPS, scale=1.0 / 256), reads=[ssvt], writes=[ssvt]))
                    s1.append(lambda: S.op("act", lambda e: e.activation(ssv, ssv, AF.Exp, scale=-0.5), reads=[ssvt], writes=[ssvt]))
                    s1.append(lambda: S.op("dve", lambda e: e.scalar_tensor_tensor(vvf, z[:, 256:512], ssv, GSl[0:rows, :], ALU.mult, ALU.mult),
                                           reads=sc(0, 1) + [("sguc", 1)] + [ssvt], writes=sc(3)))
                    s1.append(lambda: S.op("act", lambda e: e.copy(vvb, vvf), reads=sc(3), writes=sc(2)))
                    if i == 16:
                        s1.append(lambda: dma("sp", ogv[l], vvf, reads=sc(3), key="ogv"))
                    k2 = psnext("s")
                    if i < 16:
                        items = [(PS[k2][0:128, 64 * g:64 * g + 64], WSTl[:, 128 * g:128 * g + 128], vvb[:, 64 * g:64 * g + 64], True, True) for g in range(4)]
                        bsap = prm(l, "bs", 4)
                        wtok = [("sguc", 0)]
                    else:
                        items = [(PS[k2][0:32, 64 * g:64 * g + 64], WSTS(l, g), vvb[:, 64 * g:64 * g + 64], True, True) for g in range(4)]
                        bsap = prm(l, "bss", 4, 0, 32)
                        wtok = ["wsts"] + [("wsts", l, b) for b in range(2)]
                    s1.append(lambda: mm_group(None, items, reads=sc(2) + wtok, writes=[("ps", k2)]))
                    s2.append(lambda: S.op("dve", lambda e: e.tensor_tensor(
                        sg.rearrange("p (g d) -> p g d", g=4), PS[k2][0:rows, 0:256].rearrange("p (g d) -> p g d", g=4),
                        bsap.unsqueeze(2).to_broadcast([rows, 4, 64]), ALU.add),
                        reads=[("ps", k2), "prm"], writes=sc(4)))
                    s2.append(lambda: S.op("dve", lambda e: e.tensor_tensor(sg, sg, z[:, 0:256], ALU.mult), reads=sc(0, 1, 4), writes=sc(4)))
                    s2.append(lambda: S.op("act", lambda e: e.activation(sgb, sg, AF.Square, accum_out=sss),
                                           reads=sc(4), writes=sc(5) + [ssst]))
                    s2.append(lambda: S.op("act", lambda e: e.activation(sss, sss, AF.Ln, bias=EPS, scale=1.0 / 256), reads=[ssst], writes=[ssst]))
                    s2.append(lambda: S.op("act", lambda e: e.activation(RSGU(i, rows), sss, AF.Exp, scale=-0.5),
                                           reads=[ssst], writes=[("rsgu", i)]))
                    s2.append(lambda: S.op("act", lambda e: e.copy(sgb, sg), reads=sc(4, 5), writes=sc(5)))

                    def trs():
                        kt = psnext("o")

                        def tr(e):
                            ins = None
                            for cc in range(2):
                                ins = e.transpose(PSB[kt][:, cc * 128:cc * 128 + rows], sgb[:, cc * 128:(cc + 1) * 128], IDENT[0:rows, 0:rows])
                            return ins
                        S.op("pe", tr, reads=sc(5) + ["cst"], writes=[("ps", kt)])
                        for cc in range(2):
                            S.op("dve", lambda e, cc=cc, l=l: e.tensor_scalar(
                                MX(4 + cc, c0, n), PSB[kt][:, cc * 128:cc * 128 + rows], prm(l, "ggo", 1, 0, 128, 4 + cc), None, ALU.mult),
                                reads=[("ps", kt), "prm"], writes=[("mx", 4 + cc, i)])
                    s2.append(trs)
                    return s1, s2

                def interleave(a, b):
                    for q in range(max(len(a), len(b))):
                        if q < len(a):
                            a[q]()
                        if q < len(b):
                            b[q]()

                def vk_tile(i):
                    rows = tile_rows(i)
                    c0, n = tcols(i)
                    for which, slot, oprompt, osample, sg in (("v", 0, ofv, osv, 4), ("k", 1, ofk, osk, 6)):
                        k = psnext("g")
                        mm_group(None, [(PS[k][0:rows, :], HT(c, c0, n), WS(slot, c, 0, 512), c == 0, c == 7)
                                        for c in range(8)],
                                 reads=[("ht", i), ("ws", slot)], writes=[("ps", k)])
                        stg = SCRf(sg, 512, 0, rows)
                        S.op("act", lambda e, stg=stg, k=k, rows=rows: e.copy(stg, PS[k][0:rows, :]),
                             reads=[("ps", k)], writes=scr(sg, sg + 1))
                        if which == "v":
                            va3 = AR[0:rows, VA_O // 2 + i * 520: VA_O // 2 + (i + 1) * 520].rearrange(
                                "p (h k) -> p h k", k=65)[:, :, 0:64]
                            S.op("dve", lambda e, va3=va3, k=k, rows=rows: e.tensor_copy(
                                va3, PS[k][0:rows, :].rearrange("p (h k) -> p h k", k=64)),
                                reads=[("ps", k), "va_ones"], writes=[("va", i)])
                        dst = oprompt[l, 128 * i:128 * i + 128, :] if i < 16 else osample[l, :, :]
                        dma("sp", dst, stg, reads=scr(sg, sg + 1), key=("stg", sg))

                prev2 = []

                def pn_a(i):
                    rows = tile_rows(i)
                    c0, n = tcols(i)
                    return prenorm_tile(X[0:rows, i, :], rows, ("x", i), prm(l, "gpm", 8), HT3(c0, n), ("ht", i), i % 2, defer=True)
                pn_b = {0: pn_a(0)}
                for i in range(17):
                    if i + 1 < 17:
                        pn_b[i + 1] = pn_a(i + 1)
                    pn_b.pop(i)()
                    if i >= 1:
                        vk_tile(i - 1)
                        s1_, s2_ = sgu_ops(i - 1)
                        interleave(s1_, prev2)
                        prev2 = s2_
                vk_tile(16)
                s1_, s2_ = sgu_ops(16)
                interleave(s1_, prev2)
                interleave([], s2_)
                S.op("dve", lambda e: e.memset(bf(MX_O + 15 * 1024, 2), 0.0),
                     writes=[("sguA", g) for g in range(6)] + [("sguB", g) for g in range(6)] + [("sguc", 0), ("sguc", 1)]
                     + [("mx", c, i, p0) for c in (0, 1, 2, 3) for i in range(17) for p0 in (0, 64)])

                _chk("vk%d" % l)
                NB = SM[0:8, 210:211]
                S.op("dve", lambda e, l=l: e.tensor_scalar(NB, prm(l, "bf", 1, 0, 8), -1.0, None, ALU.mult),
                     reads=["prm"], writes=["cw", "nb"])
                for bi, (c0, n) in enumerate(BLKS):
                    k = psnext("g")
                    mm_group(None, [(PS[k][0:8, 0:n], WS(2, c, 0, 8), HT(c, c0, n), c == 0, c == 7) for c in range(8)],
                             reads=ALLHT + [("ws", 2)], writes=["cw", ("ps", k)])
                    tmp = SCRf(0, 512, 96, 104)
                    S.op("act", lambda e, k=k, n=n, tmp=tmp: e.activation(tmp[:, 0:n], PS[k][0:8, 0:n], AF.Exp, bias=NB, scale=-1.0),
                         reads=[("ps", k), "nb"], writes=scr(0, 1))
                    S.op("act", lambda e, n=n, tmp=tmp: e.activation(tmp[:, 0:n], tmp[:, 0:n], AF.Ln, bias=1.0, scale=1.0),
                         reads=scr(0, 1), writes=scr(0, 1))
                    S.op("dve", lambda e, c0=c0, n=n, tmp=tmp: e.tensor_scalar(CT(c0, n), tmp[:, 0:n], -1.0, None, ALU.mult),
                         reads=scr(0, 1), writes=["cw", ("ct", bi)])
                dma("sp", olf[l], CT(0, 2048), reads=[("ct", b) for b in range(4)], writes=["cw"], key="olf")
                dma("sp", oslf[l], CT(2048, 32), reads=[("ct", 4)], writes=["cw"], key="olf2")
                SLF = SM[96:104, 220:252]
                S.op("dve", lambda e: e.tensor_copy(SLF, CT(2048, 32)), reads=[("ct", 4)], writes=["cw", "slf"])
                S.op("dve", lambda e: e.tensor_tensor_scan(CT(0, 2048), ONE8.to_broadcast([8, 2048]), CT(0, 2048), 0.0, ALU.mult, ALU.add),
                     reads=[("ct", b) for b in range(4)] + ["one"], writes=["cw", "ctp"])
                S.op("dve", lambda e: e.tensor_copy(HI(0, 2048), CT(0, 2048)), reads=["ctp"], writes=["cw", "hi"])
                S.op("dve", lambda e: e.tensor_tensor(CT(0, 2048), CT(0, 2048), HI(0, 2048), ALU.subtract),
                     reads=["ctp", "hi"], writes=["cw", "ctp"])
                S.op("dve", lambda e: e.tensor_copy(LO(0, 2048), CT(0, 2048)), reads=["ctp"], writes=["cw", "lo"])
                dma("sp", cscr_p[:, 0, :], HI(0, 2048), reads=["hi"], writes=["cw", "cscr_p"], key="cscr0")
                dma("sp", cscr_p[:, 1, :], LO(0, 2048), reads=["lo"], writes=["cw", "cscr_p2"], key="cscr1")
                CS = lambda b, c0, n: CT(b * 1040 + c0, n)
                for b in range(2):
                    dma("sp", CS(b, 0, 1024), clfT[l, b], reads=["ctp", "lo", ("ct", 4), "slf"], writes=["cw", ("cs", b)], key=("csl", b))
                    S.op("dve", lambda e, b=b: e.tensor_copy(CS(b, 1024, 16), SM[96:104, 220 + 16 * b:236 + 16 * b]),
                         reads=["slf", "ctp", "lo", ("ct", 4)], writes=["cw", ("cs2", b)])
                    S.op("dve", lambda e, b=b: e.tensor_tensor_scan(CS(b, 0, 1040), ONE8.to_broadcast([8, 1040]), CS(b, 0, 1040), 0.0, ALU.mult, ALU.add),
                         reads=[("cs", b), ("cs2", b), "one"], writes=["cw", ("csc", b)])
                S.op("dve", lambda e: e.tensor_copy(HI(0, 2080), CT(0, 2080)),
                     reads=[("csc", 0), ("csc", 1), "cscr_p", "cscr_p2"], writes=["cw", "hi"])
                S.op("dve", lambda e: e.tensor_tensor(CT(0, 2080), CT(0, 2080), HI(0, 2080), ALU.subtract),
                     reads=["hi"], writes=["cw", ("csc", 0), ("csc", 1)])
                S.op("dve", lambda e: e.tensor_copy(LO(0, 2080), CT(0, 2080)), reads=[("csc", 0), ("csc", 1), "cscr_p2"], writes=["cw", "lo"])
                for b in range(2):
                    dma("sp", cscr_s[:, 0, b, :], HI(b * 1040, 1040), reads=["hi"], writes=["cw", ("cscr_s", b, 0)], key=("cscrs", b, 0))
                    dma("sp", cscr_s[:, 1, b, :], LO(b * 1040, 1040), reads=["lo"], writes=["cw", ("cscr_s", b, 1)], key=("cscrs", b, 1))

                _chk("fg%d" % l)
                _chk("sgu%d" % l)
                S.op("dve", lambda e: e.memset(bf(QK_O, 4 * NT, 64, 128), 0.0),
                     writes=["cw", "hi", "lo"] + [("qkc", s_) for s_ in range(4)])
                for s_ in (0, 2):
                    S.op("dve", lambda e, s_=s_: e.memset(QK(s_, 0, NT, 64, 67), -1.0), writes=[("qkc", s_)])
                for s_ in (1, 3):
                    S.op("dve", lambda e, s_=s_: e.memset(QK(s_, 0, NT, 64, 65), 1.0), writes=[("qkc", s_)])
                wload(0, w_mkv[l], 512)
                MEMT3 = AR[:, (SCR_O + 4096) // 2:(SCR_O + 4096) // 2 + 2048].rearrange("p (c t) -> p c t", c=8)
                MEMT = lambda c, c0, n: bf(SCR_O + 4096 + (c * 256 + c0) * 2, n)
                memx = f32(WS_O[2], 1024)
                for mt in range(2):
                    dma("sp", memx, mem[128 * mt:128 * mt + 128, :], writes=[("ws", 2)], key=("ws", 2))
                    prenorm_tile(memx, 128, ("ws", 2), prm(l, "gmem", 8), MEMT3[:, :, 128 * mt:128 * mt + 128],
                                 ("memt", mt), 0, extra_writes=scr(4, 5, 6, 7))
                memt_toks = [("memt", 0), ("memt", 1)] + scr(4, 5, 6, 7)
                stgk = f32(WS_O[2], 512)
                for mt in range(2):
                    k = psnext("g")
                    mm_group(None, [(PS[k][:, :], MEMT(c, 128 * mt, 128), WS(0, c, 0, 512), c == 0, c == 7) for c in range(8)],
                             reads=memt_toks + [("ws", 0)], writes=[("ps", k)])
                    S.op("act", lambda e, k=k: e.copy(stgk, PS[k][:, :]), reads=[("ps", k)], writes=[("ws", 2)])
                    S.op("dve", lambda e, k=k, mt=mt: e.tensor_copy(
                        MVAt[:, mt * 260:(mt + 1) * 260].rearrange("p (h k) -> p h k", k=65)[:, :, 0:64],
                        PS[k][:, 256:512].rearrange("p (h k) -> p h k", k=64)),
                        reads=[("ps", k), "mva_ones"], writes=[("mva", mt)])
                    dma("sp", omk[l, 128 * mt:128 * mt + 128, :], stgk[:, 0:256], reads=[("ws", 2)], key="omk")
                    dma("sp", omv[l, 128 * mt:128 * mt + 128, :], stgk[:, 256:512], reads=[("ws", 2)], key="omv")
                for pr in range(2):
                    k = psnext("g")
                    mm_group(None, [(PS[k][:, 0:256], WS(0, c, 128 * pr, 128), MEMT(c, 0, 256), c == 0, c == 7) for c in range(8)],
                             reads=memt_toks + [("ws", 0)], writes=[("ps", k)])
                    for hh in range(2):
                        h = 2 * pr + hh
                        S.op("act", lambda e, k=k, hh=hh, h=h: e.copy(MKTt[0:64, h * 256:(h + 1) * 256], PS[k][64 * hh:64 * hh + 64, 0:256]),
                             reads=[("ps", k)], writes=[("mkt", h)])
                wload(2, w_in[l][:, 2056:2312], 256)
                for b in range(2):
                    S.op("dve", lambda e, b=b: e.memset(KAS(b, 0, 1040, 64, 65), 1.0), reads=[("ws", 0)], writes=[("ws", 0), ("sc", b, "one")])
                    S.op("dve", lambda e, b=b: e.memset(VAS3(b)[:, :, 64:65], 1.0), reads=[("ws", 0)], writes=[("ws", 0), ("sc", b, "one")])

                S.op("dve", lambda e: e.memset(SM[:, 160:200], 0.0),
                     writes=[("ssq", g, i) for g in (0, 2) for i in range(17)])

                _chk("memkv%d" % l)
                for pr in range(2):
                    for bi, (c0, n) in enumerate(BLKS):
                        k = psnext("g")
                        mm_group(None, [(PS[k][:, 0:n], WS(2, c, 128 * pr, 128), HT(c, c0, n), c == 0, c == 7) for c in range(8)],
                                 reads=ALLHT + [("ws", 2)], writes=[("ps", k)])
                        for hh in range(2):
                            S.op("act", lambda e, k=k, hh=hh, c0=c0, n=n: e.activation(
                                QK(2 * hh, c0, n, 0, 64), PS[k][64 * hh:64 * hh + 64, 0:n], AF.Copy, scale=0.125),
                                reads=[("ps", k)], writes=[("qa", hh, bi)])
                    for hh in range(2):
                        h = 2 * pr + hh
                        for I in range(4):
                            ob = psnext("o")
                            kts = [dict(ka=MKTt[0:64, h * 256 + 128 * j:h * 256 + 128 * j + 128], nk=128,
                                        va=MVAt[:, (j * 4 + h) * 65:(j * 4 + h) * 65 + 65], lo=0, hi=512, pv_lo=0,
                                        mask=None, pt="rot", reads=[("mkt", h), ("qa", hh, I)], vreads=[("mva", j), "mva_ones"])
                                   for j in range(2)]
                            osubs = [(PS[ob][:, 65 * s:65 * s + 65], 128 * s, 128 * s + 128, 0, 1) for s in range(4)]
                            attend(lambda c0, n, hh=hh, I=I: QK(2 * hh, 512 * I + c0, n, 0, 64), 64, kts, "fm", ob, "mem")
                            nsq.step(lambda ob=ob, I=I, pr=pr, hh=hh: norm_store_fm(ob, I, 2, 6 + pr, 64 * hh, SM[:, 180 + 4 * I:184 + 4 * I], l))
                        ob = psnext("g")
                        kts = []
                        for b in range(2):
                            dma("pool", KAS(b, 0, 256, 0, 64), cmkT[l, b, h], reads=[("ws", 0)], writes=[("sc", b, "k")], key=("sck", b))
                            dma("pool", VAS3(b)[:, 0:2, 0:64],
                                cmv[l, b].rearrange("(j p) (h d) -> p j h d", p=128, h=4)[:, :, h, :],
                                reads=[("ws", 0)], writes=[("sc", b, "v")], key=("scv", b), slow=True)
                            for j in range(2):
                                kts.append(dict(ka=KAS(b, 128 * j, 128, 0, 64), nk=128, va=VAS(b, j),
                                                lo=16 * b, hi=16 * b + 16, pv_lo=0, mask=None, pt=b,
                                                reads=[("sc", b, "k"), ("qa", hh, 4)], vreads=[("sc", b, "v"), ("sc", b, "one")]))
                        osubs = [(PS[ob][0:32, 0:65], 0, 32, 0, len(kts) - 1)]
                        attend(lambda c0, n, hh=hh: QK(2 * hh, 2048 + c0, n, 0, 64), 64, kts, osubs, ob, "mems")
                        norm_and_store(ob, 0, 32, 16, 2, 6 + pr, 64 * hh, SSMEM(16, 32), l)

                flush_pv()
                nsq.flush()
                _chk("memattn%d" % l)
                wload(2, w_in[l][:, 0:512], 512)
                for pr in range(4):
                    for bi, (c0, n) in enumerate(BLKS):
                        for qk, slot in ((0, 2), (1, 1)):
                            k = psnext("g")
                            mm_group(None, [(PS[k][:, 0:n], WS(slot, c, 128 * pr, 128), HT(c, c0, n), c == 0, c == 7) for c in range(8)],
                                     reads=ALLHT + [("ws", slot)], writes=[("ps", k)])
                            for hh in range(2):
                                if qk == 0:
                                    S.op("act", lambda e, k=k, hh=hh, c0=c0, n=n: e.activation(
                                        QK(2 * hh, c0, n, 0, 64), PS[k][64 * hh:64 * hh + 64, 0:n], AF.Copy, scale=0.125),
                                        reads=[("ps", k)], writes=[("qa", hh, bi)])
                                else:
                                    S.op("dve", lambda e, k=k, hh=hh, c0=c0, n=n: e.tensor_copy(
                                        QK(2 * hh + 1, c0, n, 0, 64), PS[k][64 * hh:64 * hh + 64, 0:n]),
                                        reads=[("ps", k)], writes=[("ka", hh, bi)])
                    for hh in range(2):
                        h = 2 * pr + hh
                        dma("sp", QK(2 * hh, 0, 2048, 64, 65), cscr_p[h:h + 1, 0, :], reads=["cscr_p", ("qkc", 2 * hh)],
                            writes=[("qar", hh, 0)], key=("qar", hh))
                        dma("sp", QK(2 * hh + 1, 0, 2048, 65, 66), cscr_p[h:h + 1, 0, :], reads=["cscr_p", ("qkc", 2 * hh + 1)],
                            writes=[("kar", hh, 0)], key=("kar", hh))
                        dma("sp", QK(2 * hh + 1, 0, 2048, 66, 67), cscr_p[h:h + 1, 1, :], reads=["cscr_p2"],
                            writes=[("kar", hh, 1)], key=("kar", hh))
                        for b in range(2):
                            dma("sp", QK(2 * hh, 2048 + 16 * b, 16, 64, 65), cscr_s[h:h + 1, 0, b, 1024:1040],
                                reads=[("cscr_s", b, 0)], writes=[("qar", hh, 1 + b)], key=("qar", hh))
                            dma("sp", QK(2 * hh + 1, 2048 + 16 * b, 16, 65, 66), cscr_s[h:h + 1, 0, b, 1024:1040],
                                reads=[("cscr_s", b, 0)], writes=[("kar", hh, 2 + b)], key=("kar", hh))
                            dma("sp", QK(2 * hh + 1, 2048 + 16 * b, 16, 66, 67), cscr_s[h:h + 1, 1, b, 1024:1040],
                                reads=[("cscr_s", b, 1)], writes=[("kar", hh, 4 + b)], key=("kar", hh))
                        qar = [("qar", hh, x) for x in range(3)] + [("qkc", 2 * hh)]
                        kar = [("kar", hh, x) for x in range(6)] + [("qkc", 2 * hh + 1)]
                        for I in range(4):
                            ob = psnext("o")
                            kts = []
                            for j in range(4 * I + 4):
                                a = j - 4 * I
                                kts.append(dict(ka=QK(2 * hh + 1, 128 * j, 128, 0, 128), nk=128, va=VA128(j, h),
                                                lo=(128 * a if a >= 0 else 0), hi=512, pv_lo=(128 * a if a >= 0 else 0),
                                                mask=((MASKNEG, 128) if a >= 0 else None), pt="rot",
                                                reads=[("ka", hh, j // 4), ("qa", hh, I)] + qar + kar,
                                                vreads=[("va", j), "va_ones"]))
                            osubs = [(PS[ob][:, 65 * s:65 * s + 65], 128 * s, 128 * s + 128, 0, 4 * I + s) for s in range(4)]
                            attend(lambda c0, n, hh=hh, I=I: QK(2 * hh, 512 * I + c0, n, 0, 128), 128, kts, "fm", ob, "fox")
                            nsq.step(lambda ob=ob, I=I, pr=pr, hh=hh: norm_store_fm(ob, I, 0, pr, 64 * hh, SM[:, 160 + 4 * I:164 + 4 * I], l))
                        ob = psnext("g")
                        kts = []
                        for b in range(2):
                            dma("pool", KAS(b, 0, 1024, 0, 64), ckT[l, b, h], reads=[("ws", 0)], writes=[("sc", b, "k")], key=("sck", b))
                            dma("pool", VAS3(b)[:, :, 0:64],
                                cv[l, b].rearrange("(j p) (h d) -> p j h d", p=128, h=8)[:, :, h, :],
                                reads=[("ws", 0)], writes=[("sc", b, "v")], key=("scv", b), slow=True)
                            dma("sp", KAS(b, 0, 1024, 65, 66), cscr_s[h:h + 1, 0, b, 0:1024], reads=[("cscr_s", b, 0), ("ws", 0)],
                                writes=[("sc", b, "r")], key=("scr", b))
                            dma("sp", KAS(b, 0, 1024, 66, 67), cscr_s[h:h + 1, 1, b, 0:1024], reads=[("cscr_s", b, 1), ("ws", 0)],
                                writes=[("sc", b, "r2")], key=("scr", b))
                            for j in range(8):
                                kts.append(dict(ka=KAS(b, 128 * j, 128), nk=128, va=VAS(b, j),
                                                lo=16 * b, hi=16 * b + 16, pv_lo=0, mask=None, pt=b,
                                                reads=[("sc", b, "k"), ("sc", b, "r"), ("sc", b, "r2"), ("sc", b, "one"), ("qa", hh, 4)] + qar,
                                                vreads=[("sc", b, "v"), ("sc", b, "one")]))
                        kts.append(dict(ka=QK(2 * hh + 1, 2048, 32, 0, 67), nk=32, va=VA(16, h, 32), lo=0, hi=32, pv_lo=0,
                                        mask=(MASKS, 32), pt=2, reads=[("ka", hh, 4), ("qa", hh, 4)] + qar + kar,
                                        vreads=[("va", 16), "va_ones"]))
                        osubs = [(PS[ob][0:32, 0:65], 0, 32, 0, len(kts) - 1)]
                        attend(lambda c0, n, hh=hh: QK(2 * hh, 2048 + c0, n, 0, 67), 67, kts, osubs, ob, "foxs")
                        norm_and_store(ob, 0, 32, 16, 0, pr, 64 * hh, SSFOX(16, 32), l)

                flush_pv()
                nsq.flush()
                _chk("fox%d" % l)
                S.op("act", lambda e: e.activation(SM[:, 100:117], SM[:, 160:177], AF.Ln, bias=EPS, scale=1.0 / 512),
                     reads=[("ssq", 0, i) for i in range(17)], writes=["rfox"])
                S.op("act", lambda e: e.activation(SM[:, 100:117], SM[:, 100:117], AF.Exp, scale=-0.5), reads=["rfox"], writes=["rfox"])
                S.op("act", lambda e: e.activation(SM[:, 140:157], SM[:, 180:197], AF.Ln, bias=EPS, scale=1.0 / 256),
                     reads=[("ssq", 2, i) for i in range(17)], writes=["rmem"])
                S.op("act", lambda e: e.activation(SM[:, 140:157], SM[:, 140:157], AF.Exp, scale=-0.5), reads=["rmem"], writes=["rmem"])

                wload(0, w_out[l][:, 0:512], 512, extra_writes=SC_TOKS)
                wload(1, w_out[l][:, 512:1024], 512)
                GP = SCRf(4, 1024)
                dma("sp", GP, g_post_mix[l:l + 1, :].broadcast_to([128, D]), writes=scr(4, 5, 6, 7), key="gp")
                for i in range(17):
                    rows = tile_rows(i)
                    c0, n = tcols(i)
                    ostg = SCRf(0, 1024, 0, rows)
                    mxr = [("mx", c, i, p0) for c in (0, 1, 2, 3, 6, 7) for p0 in (0, 64)] + [("mx", 4, i), ("mx", 5, i)]
                    last_b = None
                    for hf in range(2):
                        banks = [psnext("w") for _ in range(3)]
                        for gi, (cs, b) in enumerate(zip(((0, 1, 2, 3), (4, 5), (6, 7)), banks)):
                            mm_group(None, [(PS[b][0:rows, :], MX(c, c0, n), WS(hf, c, 0, 512), c == cs[0], c == cs[-1]) for c in cs],
                                     reads=mxr + [("ws", hf)], writes=[("ps", b)])
                        o_h = ostg[:, 512 * hf:512 * hf + 512]
                        S.op("act", lambda e, o_h=o_h, b=banks[0], rows=rows, i=i: e.activation(o_h, PS[b][0:rows, :], AF.Copy, scale=RFOX(i, rows)),
                             reads=[("ps", banks[0]), "rfox"], writes=scr(2 * hf, 2 * hf + 1))
                        S.op("dve", lambda e, o_h=o_h, b=banks[1], rows=rows, i=i: e.scalar_tensor_tensor(o_h, PS[b][0:rows, :], RSGU(i, rows), o_h, ALU.mult, ALU.add),
                             reads=[("ps", banks[1]), ("rsgu", i)] + scr(2 * hf, 2 * hf + 1), writes=scr(2 * hf, 2 * hf + 1))
                        S.op("dve", lambda e, o_h=o_h, b=banks[2], rows=rows, i=i: e.scalar_tensor_tensor(o_h, PS[b][0:rows, :], RMEM(i, rows), o_h, ALU.mult, ALU.add),
                             reads=[("ps", banks[2]), "rmem"] + scr(2 * hf, 2 * hf + 1), writes=scr(2 * hf, 2 * hf + 1))
                        last_b = banks
                    post_norm_residual(ostg, rows, i, bf(WS_O[2], 1024, 0, rows), [("ws", 2)], GP, scr(4, 5, 6, 7), scr(0, 1, 2, 3))

                _chk("wout%d" % l)
                S.barrier()
                ffn_phase(l)
                S.barrier()

        except _Stop:
            pass
        for i in range(16):
            dma("sp", y_p[128 * i:128 * i + 128, :], X[:, i, :], reads=[("x", i)], key=("x", i))
        dma("sp", y_s, X[0:32, 16, :], reads=[("x", 16)], key=("x", 16))
        S.emit(st)
    return nc


_NC_CACHE = {}


def _prep_inputs(inp):
    f = lambda a: np.ascontiguousarray(np.asarray(a, dtype=np.float32))
    x_prompt = f(inp["x_prompt"]); x_sample = f(inp["x_sample"]); mem_prompt = f(inp["mem_prompt"])
    cfk = f(inp["cache_fox_k"]); cfv = f(inp["cache_fox_v"]); clf = f(inp["cache_fox_logf"])
    cmk = f(inp["cache_mem_k"]); cmvv = f(inp["cache_mem_v"]); cconv = f(inp["cache_ffn_conv"])
    ckT_all = np.ascontiguousarray(cfk.transpose(0, 1, 3, 4, 2))
    cv_all = cfv.reshape(L, 16, 1024, 512)
    clfT_all = np.ascontiguousarray(clf.transpose(0, 1, 3, 2))
    cmkT_all = np.ascontiguousarray(cmk.transpose(0, 1, 3, 4, 2))
    cmv_all = cmvv.reshape(L, 16, 256, 256)
    wsT = np.ascontiguousarray(f(inp["w_spatial"]).transpose(0, 1, 3, 2))
    fm8 = lambda g: f(g).reshape(L, 8, 128).transpose(0, 2, 1)
    b_sp = f(inp["b_spatial"])
    w_dw = f(inp["w_dwconv"]); b_dw = f(inp["b_dwconv"]); b_f = f(inp["b_forget"])
    cst = np.zeros((128, 288), dtype=ml_dtypes.bfloat16)
    cst[:, 0:128] = np.eye(128, dtype=np.float32).astype(ml_dtypes.bfloat16)
    kk, qq = np.meshgrid(np.arange(128), np.arange(128), indexing="ij")
    cst[:, 128:256] = np.where(kk <= qq, 0.0, NEG).astype(ml_dtypes.bfloat16)
    k2, q2 = np.meshgrid(np.arange(32), np.arange(32), indexing="ij")
    ok = (k2 // 16 == q2 // 16) & (k2 % 16 <= q2 % 16)
    cst[0:32, 256:288] = np.where(ok, 0.0, NEG).astype(ml_dtypes.bfloat16)
    wu = f(inp["w_up"])
    wu_a = wu[:, :, 0:FF].reshape(L, 8, 128, NJ, 128)
    wu_l = wu[:, :, FF:2 * FF].reshape(L, 8, 128, NJ, 128)
    w_up_r = np.ascontiguousarray(np.concatenate([wu_a, wu_l], axis=4).transpose(0, 3, 2, 1, 4))
    shared = dict(
        w_in=f(inp["w_in"]), w_mkv=f(inp["w_mem_kv"]), w_out=f(inp["w_out"]), w_up=w_up_r,
        w_dn=f(inp["w_down"]), g_post_mix=f(inp["g_post_mix"]), g_post_ffn=f(inp["g_post_ffn"]),
        g_sgu=f(inp["g_sgu"]), wsT=wsT, cst=cst)
    gpm = fm8(inp["g_pre_mix"]); ggo = fm8(inp["g_group_out"]); gmem = fm8(inp["g_mem"]); gpf = fm8(inp["g_pre_ffn"])
    in_maps = []
    for c in range(NCORES):
        prm = np.zeros((128, NPRM), dtype=np.float32)
        for l in range(L):
            o = l * PPL
            prm[:, o + PO["gpm"]:o + PO["gpm"] + 8] = gpm[l]
            prm[:, o + PO["ggo"]:o + PO["ggo"] + 8] = ggo[l]
            prm[:, o + PO["gmem"]:o + PO["gmem"] + 8] = gmem[l]
            prm[:, o + PO["gpf"]:o + PO["gpf"] + 8] = gpf[l]
            prm[:, o + PO["bs"]:o + PO["bs"] + 4] = b_sp[l].T
            prm[0:16, o + PO["bss"]:o + PO["bss"] + 4] = b_sp[l][:, 0:16].T
            prm[16:32, o + PO["bss"]:o + PO["bss"] + 4] = b_sp[l][:, 0:16].T
            prm[:, o + PO["wdw"]:o + PO["wdw"] + 66] = w_dw[l].reshape(3, NJ, 128).transpose(2, 1, 0).reshape(128, 66)
            prm[:, o + PO["bdw"]:o + PO["bdw"] + 22] = b_dw[l].reshape(NJ, 128).T
            prm[0:8, o + PO["bf"]] = b_f[l]
            cc = cconv[l, 2 * c:2 * c + 2]
            prm[:, o + PO["cconv"]:o + PO["cconv"] + 88] = cc.reshape(2, 2, NJ, 128).transpose(3, 2, 0, 1).reshape(128, 88)
        m = dict(shared)
        m.update(
            x_p=x_prompt[c], x_s=x_sample[2 * c:2 * c + 2].reshape(32, D), mem=mem_prompt[c],
            ckT=np.ascontiguousarray(ckT_all[:, 2 * c:2 * c + 2]), cv=np.ascontiguousarray(cv_all[:, 2 * c:2 * c + 2]),
            clfT=np.ascontiguousarray(clfT_all[:, 2 * c:2 * c + 2]), cmkT=np.ascontiguousarray(cmkT_all[:, 2 * c:2 * c + 2]),
            cmv=np.ascontiguousarray(cmv_all[:, 2 * c:2 * c + 2]), prm=prm)
        in_maps.append(m)
    return in_maps


def kernel(**inp):
    in_maps = _prep_inputs(inp)
    if "nc" not in _NC_CACHE:
        _NC_CACHE["nc"] = build()
    nc = _NC_CACHE["nc"]
    res = run_bass_kernel_spmd(nc, in_maps, core_ids=list(range(NCORES)))
    R = res.results
    cat = lambda name: np.stack([np.asarray(r[name], dtype=np.float32) for r in R], axis=0)
    y_p = cat("y_p")
    y_s = cat("y_s").reshape(16, 16, D)
    fk = cat("ofk").transpose(1, 0, 2, 3).reshape(L, 8, 2048, 8, 64)
    fv = cat("ofv").transpose(1, 0, 2, 3).reshape(L, 8, 2048, 8, 64)
    lf = cat("olf").transpose(1, 0, 3, 2)
    mk = cat("omk").transpose(1, 0, 2, 3).reshape(L, 8, 256, 4, 64)
    mv = cat("omv").transpose(1, 0, 2, 3).reshape(L, 8, 256, 4, 64)
    cvp = cat("oconv").transpose(1, 0, 4, 3, 2).reshape(L, 8, 2, FF)
    sk = cat("osk").transpose(1, 0, 2, 3).reshape(L, 16, 16, 8, 64)
    sv = cat("osv").transpose(1, 0, 2, 3).reshape(L, 16, 16, 8, 64)
    slf = cat("oslf").transpose(1, 0, 3, 2).reshape(L, 16, 16, 8)
    gv = cat("ogv").transpose(1, 0, 2, 3).reshape(L, 16, 16, 256)
    cvs = cat("osconv").transpose(1, 0, 4, 5, 3, 2).reshape(L, 16, 2, FF)
    outs = (y_p, y_s, fk, fv, lf, mk, mv, cvp, sk, sv, slf, gv, cvs)
    return tuple(np.ascontiguousarray(o, dtype=np.float32) for o in outs)
```

```python
import numpy as np
import ml_dtypes
from contextlib import ExitStack
import concourse.bass as bass
import concourse.mybir as mybir
from concourse.bass_utils import run_bass_kernel_spmd

F32 = mybir.dt.float32
BF16 = mybir.dt.bfloat16
ALU = mybir.AluOpType
AF = mybir.ActivationFunctionType

ENGS = ("pe", "act", "dve", "pool", "sp")
STOP = None
DBG = set()


class _Stop(Exception):
    pass


def _chk(name):
    if STOP == name:
        raise _Stop()
NCORES = 8
L = 2
D = 1024
NT = 2080
FF = 2816
NJ = 22
EPS = 1e-6
NEG = -30000.0


class Op:
    __slots__ = ("eng", "fn", "idx", "deps", "dma_key", "marked", "seq", "cum")

    def __init__(self, eng, fn, idx, dma_key):
        self.eng = eng
        self.fn = fn
        self.idx = idx
        self.deps = ()
        self.dma_key = dma_key
        self.marked = False
        self.seq = 0
        self.cum = 0


class Sched:
    def __init__(self, nc):
        self.nc = nc
        self.streams = {e: [] for e in ENGS}
        self.last_writer = {}
        self.readers = {}
        self.dma_counts = {}
        self.dma_last = {}

    def op(self, eng, fn, reads=(), writes=(), dma_key=None):
        o = Op(eng, fn, len(self.streams[eng]), dma_key)
        deps = set()
        lw = self.last_writer
        rd = self.readers
        for t in reads:
            w = lw.get(t)
            if w is not None:
                deps.add(w)
            if type(t) is tuple and t[0] == "ps":
                for r in rd.get(t, ()):
                    if r.eng != eng:
                        deps.add(r)
        for t in writes:
            w = lw.get(t)
            if w is not None:
                deps.add(w)
            r = rd.get(t)
            if r:
                deps.update(r)
        for t in reads:
            rd.setdefault(t, []).append(o)
        for t in writes:
            lw[t] = o
            rd[t] = []
        deps.discard(o)
        if eng == "pe":
            deps = {d for d in deps if not (d.eng == "pe" and d.dma_key is None)}
        o.deps = deps
        if dma_key is not None:
            c = self.dma_counts.get(dma_key, 0) + 1
            self.dma_counts[dma_key] = c
            o.cum = 16 * c
            self.dma_last[dma_key] = o
        self.streams[eng].append(o)
        return o

    def barrier(self):
        lasts = [s[-1] for s in self.streams.values() if s]
        lasts += list(self.dma_last.values())
        for e in ENGS:
            o = Op(e, lambda eng: eng.nop(), len(self.streams[e]), None)
            o.deps = {d for d in lasts if not (d.eng == e and d.dma_key is None)}
            self.streams[e].append(o)

    def emit(self, stack):
        nc = self.nc
        for e in ENGS:
            for o in self.streams[e]:
                for d in o.deps:
                    if d.dma_key is None:
                        d.marked = True
        sems = {}
        for e in ENGS:
            n = 0
            for o in self.streams[e]:
                if o.dma_key is None and o.marked:
                    n += 1
                    o.seq = n
            if n:
                sems[e] = stack.enter_context(nc.semaphore("s_" + e))
        dsems = {}
        for k in self.dma_counts:
            dsems[k] = stack.enter_context(nc.semaphore("d%d" % len(dsems)))
        self.n_sems = len(sems) + len(dsems)
        final = {k: o.cum for k, o in self.dma_last.items()}

        def run(eng_name, eng):
            waited = {}
            for o in self.streams[eng_name]:
                need = {}
                for d in o.deps:
                    if d.dma_key is not None:
                        key = ("d", d.dma_key)
                        val = d.cum
                    else:
                        key = ("c", d.eng)
                        val = d.seq
                    if val > need.get(key, 0):
                        need[key] = val
                for key, val in need.items():
                    if waited.get(key, 0) >= val:
                        continue
                    waited[key] = val
                    s = dsems[key[1]] if key[0] == "d" else sems[key[1]]
                    eng.wait_ge(s, val)
                ins = o.fn(eng)
                if o.dma_key is not None:
                    ins.then_inc(dsems[o.dma_key], 16)
                elif o.marked:
                    ins.then_inc(sems[o.eng], 1)
            if eng_name == "sp":
                for k, v in final.items():
                    if waited.get(("d", k), 0) < v:
                        eng.wait_ge(dsems[k], v)

        with nc.Block() as block:
            @block.tensor
            def _(t):
                run("pe", t)

            @block.scalar
            def _(t):
                run("act", t)

            @block.vector
            def _(t):
                run("dve", t)

            @block.gpsimd
            def _(t):
                run("pool", t)

            @block.sync
            def _(t):
                run("sp", t)


def tile_rows(i):
    return 128 if i < 16 else 32


def tcols(i):
    return (128 * i, 128) if i < 16 else (2048, 32)


BLKS = [(0, 512), (512, 512), (1024, 512), (1536, 512), (2048, 32)]

PO = {}
_o = 0
for _n, _w in (("gpm", 8), ("ggo", 8), ("gmem", 8), ("gpf", 8), ("bs", 4), ("bss", 4),
               ("wdw", 66), ("bdw", 22), ("bf", 1), ("cconv", 88)):
    PO[_n] = _o
    _o += _w
PPL = _o
NPRM = PPL * L


def build():
    nc = bass.Bass("TRN2", target_bir_lowering=False)

    def din(name, shape, dt=F32):
        return nc.dram_tensor(name, list(shape), dt, kind="ExternalInput").ap()

    def dout(name, shape):
        return nc.dram_tensor(name, list(shape), F32, kind="ExternalOutput").ap()

    x_p = din("x_p", [2048, D])
    x_s = din("x_s", [32, D])
    mem = din("mem", [256, D])
    ckT = din("ckT", [L, 2, 8, 64, 1024])
    cv = din("cv", [L, 2, 1024, 512])
    clfT = din("clfT", [L, 2, 8, 1024])
    cmkT = din("cmkT", [L, 2, 4, 64, 256])
    cmv = din("cmv", [L, 2, 256, 256])
    w_in = din("w_in", [L, D, 2312])
    w_mkv = din("w_mkv", [L, D, 512])
    w_out = din("w_out", [L, D, D])
    w_up = din("w_up", [L, NJ, 128, 8, 256])
    w_dn = din("w_dn", [L, FF, D])
    g_post_mix = din("g_post_mix", [L, D])
    g_post_ffn = din("g_post_ffn", [L, D])
    g_sgu = din("g_sgu", [L, 256])
    wsT = din("wsT", [L, 4, 128, 128])
    prm_h = din("prm", [128, NPRM])
    cst_h = din("cst", [128, 288], BF16)

    y_p = dout("y_p", [2048, D])
    y_s = dout("y_s", [32, D])
    ofk = dout("ofk", [L, 2048, 512])
    ofv = dout("ofv", [L, 2048, 512])
    olf = dout("olf", [L, 8, 2048])
    omk = dout("omk", [L, 256, 256])
    omv = dout("omv", [L, 256, 256])
    oconv = dout("oconv", [L, 128, NJ, 2])
    osk = dout("osk", [L, 32, 512])
    osv = dout("osv", [L, 32, 512])
    oslf = dout("oslf", [L, 8, 32])
    ogv = dout("ogv", [L, 32, 256])
    osconv = dout("osconv", [L, 128, NJ, 2, 2])

    cscr_p = nc.dram_tensor("cscr_p", [8, 2, 2048], BF16).ap()
    cscr_s = nc.dram_tensor("cscr_s", [8, 2, 2, 1040], BF16).ap()

    with ExitStack() as st:
        S = Sched(nc)
        X = st.enter_context(nc.sbuf_tensor("X", [128, 17, D], F32))
        PRM = st.enter_context(nc.sbuf_tensor("PRM", [128, NPRM], F32))
        CST = st.enter_context(nc.sbuf_tensor("CST", [128, 288], BF16))
        WSTSt = st.enter_context(nc.sbuf_tensor("WSTS", [32, L * 4 * 32], BF16))
        PTSt = st.enter_context(nc.sbuf_tensor("PTS", [128, 3 * 32], BF16))
        SM = st.enter_context(nc.sbuf_tensor("SM", [128, 256], F32))
        ONEROW = st.enter_context(nc.sbuf_tensor("ONEROW", [128, 64], F32))
        remaining = nc.sbuf_bytes_remaining
        remaining = remaining() if callable(remaining) else remaining
        ARB = (remaining - 512) // 64 * 64
        AR = st.enter_context(nc.sbuf_tensor("AR", [128, ARB // 2], BF16))
        AR32 = AR.bitcast(F32)
        PS = [st.enter_context(nc.psum_tensor("ps%d" % k, [128, 512], F32)) for k in range(8)]
        PSB = [p.bitcast(BF16) for p in PS]

        IDENT = CST[:, 0:128]
        MASKNEG = CST[:, 128:256]
        MASKS = CST[0:32, 256:288]


        def WSTS(l, g):
            o = (l * 4 + g) * 32
            return WSTSt[0:32, o:o + 32]

        def prm(l, name, w, p0=0, p1=128, off=0):
            o = l * PPL + PO[name] + off
            return PRM[p0:p1, o:o + w]

        SM_SS, SM_SD, SM_RSTD = 0, 1, 2
        SM_R = 16
        SM_SSF = 70
        SM_T = 90

        def bf(off, n, p0=0, p1=128):
            return AR[p0:p1, off // 2: off // 2 + n]

        def f32(off, n, p0=0, p1=128):
            return AR32[p0:p1, off // 4: off // 4 + n]

        cur = [0]

        def carve(nbytes):
            o = cur[0]
            cur[0] = o + (nbytes + 63) // 64 * 64
            return o

        HT_O = carve(8 * NT * 2)
        MX_O = carve(8 * NT * 2)
        VA_O = carve(17 * 8 * 65 * 2)
        QK_O = carve(4 * NT * 2)
        WS_O = [carve(8320) for _ in range(3)]
        SCR_O = carve(8192)
        MKT_O = carve(2048)
        MVA_O = carve(1040)
        MKTt = AR[:, MKT_O // 2:MKT_O // 2 + 1024]
        MVAt = AR[:, MVA_O // 2:MVA_O // 2 + 520]
        MIX_END = cur[0]
        assert MIX_END <= ARB, (MIX_END, ARB)

        def HT(c, c0, n):
            return bf(HT_O + (c * NT + c0) * 2, n)

        def HT3(c0, n):
            return AR[:, HT_O // 2: HT_O // 2 + 8 * NT].rearrange("p (c t) -> p c t", c=8)[:, :, c0:c0 + n]

        def MX(c, c0, n, p0=0, p1=128):
            return bf(MX_O + (c * NT + c0) * 2, n, p0, p1)

        def VA(j, h, nk=128):
            return bf(VA_O + ((j * 8 + h) * 65) * 2, 65, 0, nk)

        def VA128(j, h):
            return bf(VA_O + ((j * 8 + h) * 65) * 2, 128, 0, 128)

        def QK(slot, c0, n, p0, p1):
            return bf(QK_O + (slot * NT + c0) * 2, n, p0, p1)

        CT = lambda c0, n: f32(QK_O + c0 * 4, n, 96, 104)
        HI = lambda c0, n: bf(QK_O + 8320 + c0 * 2, n, 96, 104)
        LO = lambda c0, n: bf(QK_O + 8320 + 4160 + c0 * 2, n, 96, 104)

        def WS(s, c, c0, n):
            return bf(WS_O[s] + (c * 520 + c0) * 2, n)

        def WS3(s, ncol):
            return AR[:, WS_O[s] // 2: WS_O[s] // 2 + 8 * 520].rearrange("p (c n) -> p c n", c=8)[:, :, 0:ncol]

        def SCRb(g, n, p0=0, p1=128, off=0):
            return bf(SCR_O + g * 1024 + off * 2, n, p0, p1)

        def SCRf(g, n, p0=0, p1=128, off=0):
            return f32(SCR_O + g * 1024 + off * 4, n, p0, p1)

        def scr(*gs):
            return [("scr", g) for g in gs]

        def dma(eng, out, in_, reads=(), writes=(), key=None, slow=False):
            if slow:
                fn = lambda e: e.dma_start(out=out, in_=in_, allow_slow_non_contiguous=True)
            else:
                fn = lambda e: e.dma_start(out=out, in_=in_)
            return S.op(eng, fn, reads=reads, writes=writes, dma_key=key)

        psr = {"g": [0, 1], "s": [2, 3, 4], "o": [5, 6], "t": [7]}
        psi = {k: 0 for k in psr}

        def psnext(pool):
            lst = psr[pool]
            k = lst[psi[pool] % len(lst)]
            psi[pool] += 1
            return k

        def mm_group(out_fn, items, reads, writes):
            def fn(e):
                ins = None
                for (o, a, b, s0, s1) in items:
                    ins = e.matmul(o, a, b, start=s0, stop=s1, skip_group_check=True)
                return ins
            return S.op("pe", fn, reads=reads, writes=writes)

        dma("sp", PRM[:, :], prm_h, writes=["prm"], key="prm")
        dma("sp", CST[:, :], cst_h, writes=["cst"], key="cst")
        S.op("dve", lambda e: e.memset(WSTSt[:, :], 0.0), writes=["wsts"])
        for l in range(L):
            for b in range(2):
                dma("pool",
                    WSTSt[16 * b:16 * b + 16, l * 128:(l + 1) * 128].rearrange("p (g i) -> p g i", g=4)[:, :, 16 * b:16 * b + 16],
                    wsT[l, :, 0:16, 0:16].rearrange("g j i -> j g i"),
                    reads=["wsts"], writes=[("wsts", l, b)], key=("wsts", l), slow=True)
        S.op("dve", lambda e: e.memset(PTSt[:, :], 0.0), writes=["pts0", "pts1", "pts2"])
        S.op("dve", lambda e: e.memset(SM[:, 200:201], 1.0), writes=["one"])
        S.op("dve", lambda e: e.memset(ONEROW[:, :], 1.0), writes=["onerow"])
        ONE8 = SM[96:104, 200:201]

        for i in range(16):
            dma("sp", X[:, i, :], x_p[128 * i:128 * i + 128, :], writes=[("x", i)], key=("x", i))
        dma("sp", X[0:32, 16, :], x_s, writes=[("x", 16)], key=("x", 16))

        def prenorm_tile(src, rows, xtok, gcol, dst3, dst_tok, slot, extra_writes=(), hb=None, tpool="t", hb_toks=None, defer=False):
            if hb is None:
                hb = SCRb(2 * slot, 1024, 0, rows)
            hbt = list(hb_toks) if hb_toks is not None else scr(2 * slot, 2 * slot + 1)
            c0 = 3 * slot
            ss = SM[0:rows, c0:c0 + 1]
            sd = SM[0:rows, c0 + 1:c0 + 2]
            rs = SM[0:rows, c0 + 2:c0 + 3]
            S.op("act", lambda e: e.activation(hb, src, AF.Square, accum_out=ss),
                 reads=[xtok], writes=hbt + [("sm", c0)])
            S.op("act", lambda e: e.activation(sd, ss, AF.Ln, bias=EPS, scale=1.0 / D),
                 reads=[("sm", c0)], writes=[("sm", c0 + 1)])
            S.op("act", lambda e: e.activation(rs, sd, AF.Exp, scale=-0.5), reads=[("sm", c0 + 1)], writes=[("sm", c0 + 2)])
            S.op("act", lambda e: e.activation(hb, src, AF.Copy, scale=rs),
                 reads=[xtok, ("sm", c0 + 2)], writes=hbt)
            def part_b():
                k = psnext(tpool)

                def tr(e):
                    ins = None
                    for c in range(8):
                        ins = e.transpose(PSB[k][:, c * 128:c * 128 + rows], hb[:, c * 128:(c + 1) * 128],
                                          IDENT[0:rows, 0:rows])
                    return ins
                S.op("pe", tr, reads=hbt + ["cst"], writes=[("ps", k)])
                src3 = PSB[k][:, 0:1024].rearrange("p (c t) -> p c t", c=8)[:, :, 0:rows]
                S.op("dve", lambda e: e.tensor_tensor(dst3, src3, gcol.unsqueeze(2).to_broadcast([128, 8, rows]), ALU.mult),
                     reads=[("ps", k), "prm"], writes=[dst_tok] + list(extra_writes))
            if defer:
                return part_b
            part_b()

        PVQ = []

        def flush_pv():
            while PVQ:
                PVQ.pop(0)()

        def attend(qa_fn, K, ktiles, osubs, obank, tag):
            pendq = []
            first_done = [False]

            def pv(kt_i, kt, pt_ap, pt_tok):
                items = []
                if osubs == "fm":
                    lo_, hi_ = kt["lo"], kt["hi"]
                    mrows = kt["va"].shape[1]
                    items.append((PS[obank][0:mrows, lo_:hi_], kt["va"], pt_ap[:, lo_:hi_], kt_i == 0, kt_i == len(ktiles) - 1))
                    mm_group(None, items, reads=[pt_tok] + kt["vreads"], writes=[("ps", obank)])
                    return
                for (o_ap, c0, c1, fk, lk) in osubs:
                    if fk <= kt_i <= lk and kt["pv_lo"] <= c0:
                        items.append((o_ap, pt_ap[:, c0:c1], kt["va"], not first_done[0], kt_i == lk))
                        first_done[0] = True
                if items:
                    mm_group(None, items, reads=[pt_tok] + kt["vreads"], writes=[("ps", obank)])

            for kt_i, kt in enumerate(ktiles):
                k = psnext("s")
                nk, lo, hi = kt["nk"], kt["lo"], kt["hi"]
                items = []
                if kt["mask"] is not None:
                    mask_ap, mc = kt["mask"]
                    items.append((PS[k][0:nk, lo:lo + mc], kt["ka"], qa_fn(lo, mc), True, False))
                    items.append((PS[k][0:nk, lo:lo + mc], IDENT[0:nk, 0:nk], mask_ap, False, True))
                    if hi > lo + mc:
                        items.append((PS[k][0:nk, lo + mc:hi], kt["ka"], qa_fn(lo + mc, hi - lo - mc), True, True))
                else:
                    items.append((PS[k][0:nk, lo:hi], kt["ka"], qa_fn(lo, hi - lo), True, True))
                mm_group(None, items, reads=kt["reads"] + ["cst"], writes=[("ps", k)])
                if kt["pt"] == "rot":
                    g = attend.ptc % 4
                    attend.ptc += 1
                    pt_ap = SCRb(g, 512, 0, nk)
                    pt_tok = ("scr", g)
                else:
                    pt_ap = PTSt[0:nk, 32 * kt["pt"]:32 * kt["pt"] + 32]
                    pt_tok = "pts%d" % kt["pt"]
                S.op("act", lambda e, pt_ap=pt_ap, k=k, nk=nk, lo=lo, hi=hi:
                     e.activation(pt_ap[:, lo:hi], PS[k][0:nk, lo:hi], AF.Exp),
                     reads=[("ps", k)], writes=[pt_tok])
                if kt["pt"] == "rot":
                    if len(PVQ) >= 2:
                        PVQ.pop(0)()
                    PVQ.append(lambda kt_i=kt_i, kt=kt, pt_ap=pt_ap, pt_tok=pt_tok: pv(kt_i, kt, pt_ap, pt_tok))
                else:
                    flush_pv()
                    pv(kt_i, kt, pt_ap, pt_tok)
        attend.ptc = 0

        def cs_all():
            return ["cw"]

        def post_norm_residual(ostg, rows, i, junk, junk_toks, GPap, gp_toks, o_toks):
            ssc = SM[0:rows, SM_T + 8:SM_T + 9]
            S.op("act", lambda e: e.activation(junk, ostg, AF.Square, accum_out=ssc),
                 reads=o_toks, writes=list(junk_toks) + ["pn"])
            S.op("act", lambda e: e.activation(ssc, ssc, AF.Ln, bias=EPS, scale=1.0 / D), reads=["pn"], writes=["pn"])
            S.op("act", lambda e: e.activation(ssc, ssc, AF.Exp, scale=-0.5), reads=["pn"], writes=["pn"])
            S.op("dve", lambda e: e.scalar_tensor_tensor(ostg, ostg, ssc, GPap[0:rows, :], ALU.mult, ALU.mult),
                 reads=list(o_toks) + list(gp_toks) + ["pn"], writes=o_toks)
            S.op("pool", lambda e: e.tensor_tensor(X[0:rows, i, :], X[0:rows, i, :], ostg, ALU.add),
                 reads=list(o_toks) + [("x", i)], writes=[("x", i)])

        def ffn_phase(l):
            psr.clear()
            psr.update({"u": [0, 1, 2, 4, 5, 6], "t": [3], "d": [4, 5, 6, 7]})
            for kk in psr:
                psi[kk] = 0
            HC = 1056
            o = [0]

            def cv_(nb):
                r = o[0]
                o[0] = r + (nb + 63) // 64 * 64
                return r
            H2_O = cv_(8 * HC * 2)
            ACT_O = cv_(NJ * HC * 2)
            WDN_O = cv_(NJ * 1024 * 2)
            WU_O = [cv_(4096) for _ in range(2)]
            AE_O = [cv_(2064) for _ in range(2)]
            LIN_O = [cv_(1024) for _ in range(2)]
            CONV_O = [cv_(2048) for _ in range(2)]
            OSTG_O = cv_(4096)
            GPF_O = cv_(4096)
            HBF_O = cv_(2048)
            SAVE_O = cv_(NJ * 2 * 4)
            SAVES_O = cv_(NJ * 4 * 4)
            assert o[0] <= ARB, (o[0], ARB)

            H2 = lambda c, c0, n: bf(H2_O + (c * HC + c0) * 2, n)
            H23 = lambda c0, n: AR[:, H2_O // 2:H2_O // 2 + 8 * HC].rearrange("p (c t) -> p c t", c=8)[:, :, c0:c0 + n]
            ACTB = lambda j, c0, n: bf(ACT_O + (j * HC + c0) * 2, n)
            WDN = lambda j, c0, n: bf(WDN_O + (j * 1024 + c0) * 2, n)
            WDN3 = AR[:, WDN_O // 2:WDN_O // 2 + NJ * 1024].rearrange("p (j n) -> p j n", j=NJ)
            WU = lambda s, c, c0, n: bf(WU_O[s] + (c * 256 + c0) * 2, n)
            WU3 = lambda s: AR[:, WU_O[s] // 2:WU_O[s] // 2 + 2048].rearrange("p (c n) -> p c n", c=8)
            OSTG = f32(OSTG_O, 1024)
            GPF = f32(GPF_O, 1024)
            HBF = bf(HBF_O, 1024)
            SAVE = f32(SAVE_O, NJ * 2)
            SAVES = f32(SAVES_O, NJ * 4)

            dma("sp", GPF, g_post_ffn[l:l + 1, :].broadcast_to([128, D]), writes=["gpf"], key="gp")
            wdn_src = w_dn[l].rearrange("(j p) n -> p j n", p=128)
            for part in range(2):
                dma("pool", WDN3[:, 11 * part:11 * part + 11, :], wdn_src[:, 11 * part:11 * part + 11, :],
                    writes=[("wdn", part)], key=("wdn", part))
            aecnt = 0

            def wu_load(j, s_):
                dma("pool", WU3(s_), w_up[l, j], writes=[("wu", s_, 0), ("wu", s_, 1)], key=("wu", s_, 0))

            for half in range(2):
                tiles = list(range(0, 8)) if half == 0 else list(range(8, 17))
                base = 0 if half == 0 else 1024
                blocks = [(0, 512), (512, 512)] + ([(1024, 32)] if half == 1 else [])
                h2toks = [("h2", (i if i < 8 else i - 8)) for i in tiles]
                if half == 0:
                    HB2 = bf(AE_O[0], 1024)

                    def pa(i):
                        c0, n = tcols(i)
                        if i % 2 == 0:
                            return prenorm_tile(X[:, i, :], 128, ("x", i), prm(l, "gpf", 8), H23(c0, n), ("h2", i), 0,
                                                hb=HBF[:, :], tpool="t", hb_toks=["hbf"], defer=True)
                        return prenorm_tile(X[:, i, :], 128, ("x", i), prm(l, "gpf", 8), H23(c0, n), ("h2", i), 1,
                                            hb=HB2, tpool="t", hb_toks=[("ae", 0), ("aeh", 0), ("aet", 0)], defer=True)
                    pbq = {0: pa(0)}
                    for i in tiles:
                        if i + 1 < 8:
                            pbq[i + 1] = pa(i + 1)
                        pbq.pop(i)()
                wu_load(0, 0)
                pend_tail = None
                for j in range(NJ):
                    s = j % 2
                    if j + 1 < NJ:
                        wu_load(j + 1, (j + 1) % 2)
                    w0 = prm(l, "wdw", 1, 0, 128, 3 * j)
                    w1 = prm(l, "wdw", 1, 0, 128, 3 * j + 1)
                    w2 = prm(l, "wdw", 1, 0, 128, 3 * j + 2)
                    bd = prm(l, "bdw", 1, 0, 128, j)
                    for bidx, (lc0, n) in enumerate(blocks):
                        sample = (n == 32)
                        ka = psnext("u")
                        mm_group(None, [(PS[ka][:, 0:n], WU(s, c, 0, 128), H2(c, lc0, n), c == 0, c == 7) for c in range(8)],
                                 reads=h2toks + [("wu", s, 0)], writes=[("ps", ka)])
                        kl = psnext("u")
                        mm_group(None, [(PS[kl][:, 0:n], WU(s, c, 128, 128), H2(c, lc0, n), c == 0, c == 7) for c in range(8)],
                                 reads=h2toks + [("wu", s, 1)], writes=[("ps", kl)])
                        p = aecnt % 2
                        aecnt += 1
                        lin = bf(LIN_O[p], 512)
                        conv = f32(CONV_O[p], 512)
                        if sample:
                            AEs = f32(AE_O[p], 36).rearrange("p (b t) -> p b t", b=2)
                            S.op("dve", lambda e, AEs=AEs, j=j: e.tensor_copy(
                                AEs[:, :, 0:2], prm(l, "cconv", 4, 0, 128, 4 * j).rearrange("p (b r) -> p b r", b=2)),
                                reads=["prm"], writes=[("aeh", p)])
                            S.op("act", lambda e, AEs=AEs, ka=ka: e.copy(AEs[:, :, 2:18], PS[ka][:, 0:32].rearrange("p (b t) -> p b t", b=2)),
                                 reads=[("ps", ka)], writes=[("ae", p)])
                            taps = [AEs[:, :, k:k + 16] for k in range(3)]
                            cv3 = conv[:, 0:32].rearrange("p (b t) -> p b t", b=2)
                            psa = PS[ka][:, 0:32].rearrange("p (b t) -> p b t", b=2)
                            sil_out = f32(AE_O[p] + 256, 32)
                            sil_in = conv[:, 0:32]
                        else:
                            AEp = f32(AE_O[p], 514)
                            if bidx == 0 and half == 0:
                                S.op("dve", lambda e, AEp=AEp: e.memset(AEp[:, 0:2], 0.0), writes=[("aeh", p)])
                            elif bidx == 0:
                                S.op("dve", lambda e, AEp=AEp, j=j: e.tensor_copy(AEp[:, 0:2], SAVE[:, 2 * j:2 * j + 2]),
                                     reads=[("save", j)], writes=[("aeh", p)])
                            else:
                                prev = f32(AE_O[1 - p], 514)
                                S.op("dve", lambda e, AEp=AEp, prev=prev: e.tensor_copy(AEp[:, 0:2], prev[:, 512:514]),
                                     reads=[("aet", 1 - p)], writes=[("aeh", p)])
                            S.op("act", lambda e, AEp=AEp, ka=ka: e.copy(AEp[:, 2:514], PS[ka][:, 0:512]),
                                 reads=[("ps", ka)], writes=[("ae", p), ("aet", p)])
                            taps = [AEp[:, k:k + 512] for k in range(3)]
                            cv3 = conv
                            psa = PS[ka][:, 0:512]
                            sil_out = AEp[:, 0:512]
                            sil_in = conv
                        S.op("act", lambda e, cv3=cv3, psa=psa, w2=w2, bd=bd: e.activation(cv3, psa, AF.Identity, bias=bd, scale=w2),
                             reads=[("ps", ka), "prm"], writes=[("conv", p)])
                        S.op("act", lambda e, lin=lin, kl=kl, n=n: e.copy(lin[:, 0:n], PS[kl][:, 0:n]),
                             reads=[("ps", kl)], writes=[("lin", p)])
                        S.op("dve", lambda e, cv3=cv3, taps=taps, w1=w1: e.scalar_tensor_tensor(cv3, taps[1], w1, cv3, ALU.mult, ALU.add),
                             reads=[("ae", p), ("aeh", p), ("conv", p)], writes=[("conv", p)])
                        S.op("dve", lambda e, cv3=cv3, taps=taps, w0=w0: e.scalar_tensor_tensor(cv3, taps[0], w0, cv3, ALU.mult, ALU.add),
                             reads=[("ae", p), ("aeh", p), ("conv", p)], writes=[("conv", p)])
                        if not sample and bidx == 1:
                            S.op("dve", lambda e, AEp=AEp, j=j: e.tensor_copy(SAVE[:, 2 * j:2 * j + 2], AEp[:, 512:514]),
                                 reads=[("aet", p)], writes=[("save", j)])
                        if sample:
                            S.op("dve", lambda e, AEs=AEs, j=j: e.tensor_copy(
                                SAVES[:, 4 * j:4 * j + 4].rearrange("p (b r) -> p b r", b=2), AEs[:, :, 16:18]),
                                reads=[("ae", p)], writes=[("saves", j)])

                        def tail(p=p, sil_out=sil_out, sil_in=sil_in, j=j, lc0=lc0, n=n, lin=lin, half=half):
                            S.op("act", lambda e: e.activation(sil_out, sil_in, AF.Silu),
                                 reads=[("conv", p), ("ae", p), ("aeh", p)], writes=[("ae", p), ("aeh", p)])
                            S.op("pool", lambda e: e.tensor_tensor(ACTB(j, lc0, n), sil_out, lin[:, 0:n], ALU.mult),
                                 reads=[("ae", p), ("aeh", p), ("lin", p)], writes=[("actb", j, half)])
                        if pend_tail is not None:
                            pend_tail()
                        pend_tail = tail
                if pend_tail is not None:
                    pend_tail()
                    pend_tail = None
                if half == 1:
                    dma("sp", oconv[l], SAVE.rearrange("p (j r) -> p j r", r=2), reads=[("save", j) for j in range(NJ)], key="oconv")
                    dma("sp", osconv[l], SAVES.rearrange("p (j b r) -> p j b r", b=2, r=2), reads=[("saves", j) for j in range(NJ)], key="osconv")
                atoks = [("actb", j, half) for j in range(NJ)]
                nxt = list(range(8, 17)) if half == 0 else []
                for ti_, i in enumerate(tiles):
                    rows = tile_rows(i)
                    c0, n = tcols(i)
                    lc0 = c0 - base
                    defer_b = []
                    for i2 in (nxt[ti_:ti_ + 1] if ti_ < 7 else nxt[7:8]):
                        rows2 = tile_rows(i2)
                        c02, n2 = tcols(i2)
                        defer_b.append(prenorm_tile(X[0:rows2, i2, :], rows2, ("x", i2), prm(l, "gpf", 8), H23(c02 - 1024, n2),
                                                    ("h2", i2 - 8), 0, hb=HBF[0:rows2, :], tpool="t", hb_toks=["hbf"], defer=True))
                        if "nodefer" in DBG:
                            defer_b.pop()()
                    for hf in range(2):
                        kd = psnext("d")
                        mm_group(None, [(PS[kd][0:rows, :], ACTB(j, lc0, n), WDN(j, 512 * hf, 512), j == 0, j == NJ - 1) for j in range(NJ)],
                                 reads=atoks + [("wdn", 0), ("wdn", 1)], writes=[("ps", kd)])
                        if hf == 0:
                            S.op("act", lambda e, kd=kd, rows=rows: e.copy(OSTG[0:rows, 0:512], PS[kd][0:rows, :]),
                                 reads=[("ps", kd)], writes=["ostg0"])
                        else:
                            S.op("dve", lambda e, kd=kd, rows=rows: e.tensor_copy(OSTG[0:rows, 512:1024], PS[kd][0:rows, :]),
                                 reads=[("ps", kd)], writes=["ostg1"])
                    for fb in defer_b:
                        fb()
                    post_norm_residual(OSTG[0:rows, :], rows, i, HBF[0:rows, :], ["hbf"], GPF, ["gpf"], ["ostg0", "ostg1"])
                if half == 0:
                    prenorm_tile(X[0:32, 16, :], 32, ("x", 16), prm(l, "gpf", 8), H23(1024, 32), ("h2", 8), 0,
                                 hb=HBF[0:32, :], tpool="t", hb_toks=["hbf"])
        RFOX = lambda i, rows=128: SM[0:rows, 100 + i:101 + i]
        RSGU = lambda i, rows=128: SM[0:rows, 120 + i:121 + i]
        RMEM = lambda i, rows=128: SM[0:rows, 140 + i:141 + i]
        SSFOX = lambda i, rows=128: SM[0:rows, 160 + i:161 + i]
        SSMEM = lambda i, rows=128: SM[0:rows, 180 + i:181 + i]
        ALLHT = [("ht", i) for i in range(17)]

        class NSQ:
            def __init__(self):
                self.p1 = None
                self.p2 = None
                self.p3 = None

            def step(self, new_p1):
                if self.p3 is not None:
                    self.p3()
                    self.p3 = None
                if self.p2 is not None:
                    self.p3 = self.p2()
                    self.p2 = None
                if self.p1 is not None:
                    self.p2 = self.p1()
                self.p1 = new_p1

            def flush(self):
                self.step(None)
                self.step(None)
                self.step(None)
        nsq = NSQ()

        def wload(slot, src2d, ncol, extra_writes=()):
            dma("pool", WS3(slot, ncol), src2d.rearrange("(c p) n -> p c n", p=128),
                writes=[("ws", slot)] + list(extra_writes), key=("ws", slot))

        def KAS(b, c0, n, p0=0, p1=67):
            return bf(WS_O[0] + b * 3200 + c0 * 2, n, p0, p1)

        def VAS(b, j):
            return bf(WS_O[0] + b * 3200 + 2080 + j * 130, 65)

        def VAS3(b):
            return AR[:, (WS_O[0] + b * 3200 + 2080) // 2:(WS_O[0] + b * 3200 + 2080) // 2 + 520].rearrange("p (j k) -> p j k", k=65)

        SC_TOKS = [("sc", b, x) for b in range(2) for x in ("k", "v", "r", "one")]

        def norm_store_fm(obank, I, grp, chunk, p0, ssq4, l):
            rlrow = f32(SCR_O + 6 * 1024, 512, 64, 65)
            hfT = f32(SCR_O + 6 * 1024, 512, 0, 64)
            S.op("act", lambda e: e.activation(rlrow, PS[obank][64:65, 0:512], AF.Ln), reads=[("ps", obank)], writes=[("scr", 6, "r")])
            S.op("act", lambda e: e.activation(rlrow, rlrow, AF.Exp, scale=-1.0), reads=[("scr", 6, "r")], writes=[("scr", 6, "r")])
            kb = psnext("g")
            S.op("pe", lambda e: e.matmul(PS[kb][0:64, 0:512], ONEROW[64:65, 0:64], rlrow, start=True, stop=True, skip_group_check=True),
                 reads=[("scr", 6, "r"), "onerow"], writes=[("ps", kb)])
            S.op("dve", lambda e: e.tensor_copy(hfT, PS[kb][0:64, 0:512]), reads=[("ps", kb)], writes=scr(6, 7))
            S.op("dve", lambda e: e.tensor_tensor(hfT, PS[obank][0:64, 0:512], hfT, ALU.mult), reads=[("ps", obank)] + scr(6, 7), writes=scr(6, 7))
            return lambda: norm_store_fm2(I, grp, chunk, p0, ssq4, l)

        def norm_store_fm2(I, grp, chunk, p0, ssq4, l):
            hfT = f32(SCR_O + 6 * 1024, 512, 0, 64)
            sqb = bf(SCR_O + 4 * 1024, 512, 0, 64)
            S.op("act", lambda e: e.activation(MX(chunk, 512 * I, 512, p0, p0 + 64), hfT, AF.Copy,
                                               scale=prm(l, "ggo", 1, p0, p0 + 64, chunk)),
                 reads=scr(6, 7) + ["prm"], writes=[("mx", chunk, 4 * I + s_, p0) for s_ in range(4)])
            S.op("act", lambda e: e.activation(sqb, hfT, AF.Square), reads=scr(6, 7), writes=scr(4))
            return lambda: norm_store_fm3(I, grp, ssq4)

        def norm_store_fm3(I, grp, ssq4):
            sqb = bf(SCR_O + 4 * 1024, 512, 0, 64)
            kt = psnext("t")
            ones_col = bf(VA_O + 64 * 2, 1, 0, 64)

            def ssmm(e):
                ins = None
                for s_ in range(4):
                    ins = e.matmul(PS[kt][:, s_:s_ + 1], sqb[:, 128 * s_:128 * s_ + 128], ones_col, start=True, stop=True,
                                   skip_group_check=True)
                return ins
            S.op("pe", ssmm, reads=scr(4) + ["va_ones"], writes=[("ps", kt)])
            toks4 = [("ssq", grp, 4 * I + s_) for s_ in range(4)]
            S.op("dve", lambda e: e.tensor_tensor(ssq4, ssq4, PS[kt][:, 0:4], ALU.add), reads=[("ps", kt)] + toks4, writes=toks4)

        def norm_and_store_block(obank, I, grp, chunk, p0, ssq4, l):
            O4 = PS[obank][:, 0:260].rearrange("p (s k) -> p s k", k=65)
            rl4 = SM[:, SM_T + 10:SM_T + 14]
            sq4 = SM[:, SM_T + 14:SM_T + 18]
            S.op("dve", lambda e: e.reciprocal(rl4.unsqueeze(2), O4[:, :, 64:65]), reads=[("ps", obank)], writes=["rl4"])
            hf4 = SCRf(6, 256)
            hb4 = SCRb(7, 256)
            S.op("dve", lambda e: e.tensor_tensor(hf4.rearrange("p (s k) -> p s k", k=64), O4[:, :, 0:64],
                                                  rl4.unsqueeze(2).to_broadcast([128, 4, 64]), ALU.mult),
                 reads=[("ps", obank), "rl4"], writes=scr(6))

            def sqs(e):
                ins = None
                for s_ in range(4):
                    ins = e.activation(hb4[:, 64 * s_:64 * s_ + 64], hf4[:, 64 * s_:64 * s_ + 64], AF.Square,
                                       accum_out=sq4[:, s_:s_ + 1])
                return ins
            S.op("act", sqs, reads=scr(6), writes=scr(7) + ["sq4"])
            toks4 = [("ssq", grp, 4 * I + s_) for s_ in range(4)]
            S.op("dve", lambda e: e.tensor_tensor(ssq4, ssq4, sq4, ALU.add), reads=["sq4"] + toks4, writes=toks4)
            S.op("act", lambda e: e.copy(hb4, hf4), reads=scr(6, 7), writes=scr(7))
            kt = psnext("t")

            def tr(e):
                ins = None
                for s_ in range(4):
                    ins = e.transpose(PSB[kt][p0:p0 + 64, 128 * s_:128 * s_ + 128], hb4[:, 64 * s_:64 * s_ + 64], IDENT[:, :])
                return ins
            S.op("pe", tr, reads=scr(7) + ["cst"], writes=[("ps", kt)])
            S.op("dve", lambda e: e.tensor_scalar(MX(chunk, 512 * I, 512, p0, p0 + 64), PSB[kt][p0:p0 + 64, 0:512],
                                                  prm(l, "ggo", 1, p0, p0 + 64, chunk), None, ALU.mult),
                 reads=[("ps", kt), "prm"], writes=[("mx", chunk, 4 * I + s_, p0) for s_ in range(4)])

        def norm_and_store(obank, ocol, rows, tile_i, grp, chunk, p0, ssq_acc, l):
            rl = SM[0:rows, SM_T + 4:SM_T + 5]
            S.op("dve", lambda e: e.reciprocal(rl, PS[obank][0:rows, ocol + 64:ocol + 65]),
                 reads=[("ps", obank)], writes=["rl"])
            hf = SCRf(5, 64, 0, rows)
            hb = SCRb(5, 64, 0, rows, off=256)
            S.op("dve", lambda e: e.tensor_scalar(hf, PS[obank][0:rows, ocol:ocol + 64], rl, None, ALU.mult),
                 reads=[("ps", obank), "rl"], writes=scr(5))
            sq = SM[0:rows, SM_T + 5:SM_T + 6]
            S.op("act", lambda e: e.activation(hb, hf, AF.Square, accum_out=sq), reads=scr(5), writes=scr(5) + ["sq"])
            S.op("dve", lambda e: e.tensor_tensor(ssq_acc, ssq_acc, sq, ALU.add),
                 reads=["sq", ("ssq", grp, tile_i)], writes=[("ssq", grp, tile_i)])
            S.op("act", lambda e: e.copy(hb, hf), reads=scr(5), writes=scr(5))
            kt = psnext("t")
            c0, n = tcols(tile_i)
            S.op("pe", lambda e: e.transpose(PSB[kt][p0:p0 + 64, 0:rows], hb, IDENT[0:rows, 0:rows]),
                 reads=scr(5) + ["cst"], writes=[("ps", kt)])
            S.op("dve", lambda e: e.tensor_scalar(MX(chunk, c0, n, p0, p0 + 64), PSB[kt][p0:p0 + 64, 0:rows],
                                                  prm(l, "ggo", 1, p0, p0 + 64, chunk), None, ALU.mult),
                 reads=[("ps", kt), "prm"], writes=[("mx", chunk, tile_i, p0)])

        try:
            for l in range(L):
                psr.clear()
                psr.update({"g": [0, 1], "s": [2, 3, 4], "o": [5, 6], "t": [7], "w": [0, 1, 2, 3, 4, 5]})
                for kk in psr:
                    psi[kk] = 0
                S.op("dve", lambda e: e.memset(
                    AR[:, VA_O // 2: VA_O // 2 + 17 * 8 * 65].rearrange("p (a k) -> p a k", k=65)[:, :, 64:65], 1.0),
                    writes=["va_ones"])
                S.op("dve", lambda e: e.memset(MVAt[:, :].rearrange("p (a k) -> p a k", k=65)[:, :, 64:65], 1.0),
                     writes=["mva_ones"])
                S.op("dve", lambda e: e.memset(MKTt[64:128, :], 0.0), writes=["mkt_zero"])
                wload(0, w_in[l][:, 1024:1536], 512, extra_writes=SC_TOKS if l > 0 else ())
                wload(1, w_in[l][:, 512:1024], 512)
                wload(2, w_in[l][:, 1536:2056], 520)

                SGC = MX_O + 12288
                WSTl = bf(SGC, 512)
                GSl = f32(SGC + 1024, 256)
                dma("pool", WSTl.rearrange("p (g i) -> p g i", g=4), wsT[l].rearrange("g j i -> j g i"),
                    writes=[("sguc", 0)], key="wst")
                S.op("dve", lambda e: e.memset(bf(SGC, 512, 64, 128).rearrange("p (g i) -> p g i", g=4)[:, :, 0:64], 0.0),
                     reads=[("sguc", 0)], writes=[("sguc", 0)])
                dma("sp", GSl, g_sgu[l:l + 1, :].broadcast_to([128, 256]), writes=[("sguc", 1)], key="gs")

                def sgu_ops(i):
                    st_ = 0 if "setA" in DBG else i % 2
                    base = MX_O if st_ == 0 else MX_O + 6144
                    tk = "sguA" if st_ == 0 else "sguB"
                    Sb = lambda g, n, p0=0, p1=128: bf(base + g * 1024, n, p0, p1)
                    Sf = lambda g, n, p0=0, p1=128: f32(base + g * 1024, n, p0, p1)
                    sc = lambda *gs: [(tk, g) for g in gs]
                    rows = tile_rows(i)
                    c0, n = tcols(i)
                    s1, s2 = [], []
                    k = psnext("s")
                    z = Sf(0, 512, 0, rows)
                    t1 = Sf(2, 512, 0, rows)
                    ssv = SM[0:rows, SM_T + 20 + 2 * st_:SM_T + 21 + 2 * st_]
                    sss = SM[0:rows, SM_T + 21 + 2 * st_:SM_T + 22 + 2 * st_]
                    ssvt, ssst = ("ssv", st_), ("sss", st_)
                    vvf = Sf(3, 256, 0, rows)
                    vvb = Sb(2, 256, 0, rows)
                    sg = Sf(4, 256, 0, rows)
                    sgb = Sb(5, 256, 0, rows)
                    s1.append(lambda: mm_group(None, [(PS[k][0:rows, :], HT(c, c0, n), WS(2, c, 8, 512), c == 0, c == 7) for c in range(8)],
                                               reads=[("ht", i), ("ws", 2)], writes=[("ps", k)]))
                    s1.append(lambda: S.op("act", lambda e: e.copy(z, PS[k][0:rows, :]), reads=[("ps", k)], writes=sc(0, 1)))
                    s1.append(lambda: S.op("dve", lambda e: e.tensor_tensor(t1, z, z, ALU.mult), reads=sc(0, 1), writes=sc(2, 3)))
                    s1.append(lambda: S.op("dve", lambda e: e.tensor_scalar(t1, t1, 0.044715, 1.0, ALU.mult, ALU.add), reads=sc(2, 3), writes=sc(2, 3)))
                    s1.append(lambda: S.op("dve", lambda e: e.tensor_tensor(t1, t1, z, ALU.mult), reads=sc(0, 1, 2, 3), writes=sc(2, 3)))
                    s1.append(lambda: S.op("act", lambda e: e.activation(t1, t1, AF.Exp, scale=-1.5957691216057308), reads=sc(2, 3), writes=sc(2, 3)))
                    s1.append(lambda: S.op("act", lambda e: e.activation(t1, t1, AF.Ln, bias=1.0, scale=1.0), reads=sc(2, 3), writes=sc(2, 3)))
                    s1.append(lambda: S.op("act", lambda e: e.activation(t1, t1, AF.Exp, scale=-1.0), reads=sc(2, 3), writes=sc(2, 3)))
                    s1.append(lambda: S.op("dve", lambda e: e.tensor_tensor(z, z, t1, ALU.mult), reads=sc(0, 1, 2, 3), writes=sc(0, 1)))
                    s1.append(lambda: S.op("act", lambda e: e.activation(t1[:, 0:256], z[:, 256:512], AF.Square, accum_out=ssv),
                                           reads=sc(0, 1), writes=sc(2) + [ssvt]))
                    s1.append(lambda: S.op("act", lambda e: e.activation(ssv, ssv, AF.Ln, bias=EPS, scale=1.0 / 256), reads=[ssvt], writes=[ssvt]))
                    s1.append(lambda: S.op("act", lambda e: e.activation(ssv, ssv, AF.Exp, scale=-0.5), reads=[ssvt], writes=[ssvt]))
                    s1.append(lambda: S.op("dve", lambda e: e.scalar_tensor_tensor(vvf, z[:, 256:512], ssv, GSl[0:rows, :], ALU.mult, ALU.mult),
                                           reads=sc(0, 1) + [("sguc", 1)] + [ssvt], writes=sc(3)))
                    s1.append(lambda: S.op("act", lambda e: e.copy(vvb, vvf), reads=sc(3), writes=sc(2)))
                    if i == 16:
                        s1.append(lambda: dma("sp", ogv[l], vvf, reads=sc(3), key="ogv"))
                    k2 = psnext("s")
                    if i < 16:
                        items = [(PS[k2][0:128, 64 * g:64 * g + 64], WSTl[:, 128 * g:128 * g + 128], vvb[:, 64 * g:64 * g + 64], True, True) for g in range(4)]
                        bsap = prm(l, "bs", 4)
                        wtok = [("sguc", 0)]
                    else:
                        items = [(PS[k2][0:32, 64 * g:64 * g + 64], WSTS(l, g), vvb[:, 64 * g:64 * g + 64], True, True) for g in range(4)]
                        bsap = prm(l, "bss", 4, 0, 32)
                        wtok = ["wsts"] + [("wsts", l, b) for b in range(2)]
                    s1.append(lambda: mm_group(None, items, reads=sc(2) + wtok, writes=[("ps", k2)]))
                    s2.append(lambda: S.op("dve", lambda e: e.tensor_tensor(
                        sg.rearrange("p (g d) -> p g d", g=4), PS[k2][0:rows, 0:256].rearrange("p (g d) -> p g d", g=4),
                        bsap.unsqueeze(2).to_broadcast([rows, 4, 64]), ALU.add),
                        reads=[("ps", k2), "prm"], writes=sc(4)))
                    s2.append(lambda: S.op("dve", lambda e: e.tensor_tensor(sg, sg, z[:, 0:256], ALU.mult), reads=sc(0, 1, 4), writes=sc(4)))
                    s2.append(lambda: S.op("act", lambda e: e.activation(sgb, sg, AF.Square, accum_out=sss),
                                           reads=sc(4), writes=sc(5) + [ssst]))
                    s2.append(lambda: S.op("act", lambda e: e.activation(sss, sss, AF.Ln, bias=EPS, scale=1.0 / 256), reads=[ssst], writes=[ssst]))
                    s2.append(lambda: S.op("act", lambda e: e.activation(RSGU(i, rows), sss, AF.Exp, scale=-0.5),
                                           reads=[ssst], writes=[("rsgu", i)]))
                    s2.append(lambda: S.op("act", lambda e: e.copy(sgb, sg), reads=sc(4, 5), writes=sc(5)))

                    def trs():
                        kt = psnext("o")

                        def tr(e):
                            ins = None
                            for cc in range(2):
                                ins = e.transpose(PSB[kt][:, cc * 128:cc * 128 + rows], sgb[:, cc * 128:(cc + 1) * 128], IDENT[0:rows, 0:rows])
                            return ins
                        S.op("pe", tr, reads=sc(5) + ["cst"], writes=[("ps", kt)])
                        for cc in range(2):
                            S.op("dve", lambda e, cc=cc, l=l: e.tensor_scalar(
                                MX(4 + cc, c0, n), PSB[kt][:, cc * 128:cc * 128 + rows], prm(l, "ggo", 1, 0, 128, 4 + cc), None, ALU.mult),
                                reads=[("ps", kt), "prm"], writes=[("mx", 4 + cc, i)])
                    s2.append(trs)
                    return s1, s2

                def interleave(a, b):
                    for q in range(max(len(a), len(b))):
                        if q < len(a):
                            a[q]()
                        if q < len(b):
                            b[q]()

                def vk_tile(i):
                    rows = tile_rows(i)
                    c0, n = tcols(i)
                    for which, slot, oprompt, osample, sg in (("v", 0, ofv, osv, 4), ("k", 1, ofk, osk, 6)):
                        k = psnext("g")
                        mm_group(None, [(PS[k][0:rows, :], HT(c, c0, n), WS(slot, c, 0, 512), c == 0, c == 7)
                                        for c in range(8)],
                                 reads=[("ht", i), ("ws", slot)], writes=[("ps", k)])
                        stg = SCRf(sg, 512, 0, rows)
                        S.op("act", lambda e, stg=stg, k=k, rows=rows: e.copy(stg, PS[k][0:rows, :]),
                             reads=[("ps", k)], writes=scr(sg, sg + 1))
                        if which == "v":
                            va3 = AR[0:rows, VA_O // 2 + i * 520: VA_O // 2 + (i + 1) * 520].rearrange(
                                "p (h k) -> p h k", k=65)[:, :, 0:64]
                            S.op("dve", lambda e, va3=va3, k=k, rows=rows: e.tensor_copy(
                                va3, PS[k][0:rows, :].rearrange("p (h k) -> p h k", k=64)),
                                reads=[("ps", k), "va_ones"], writes=[("va", i)])
                        dst = oprompt[l, 128 * i:128 * i + 128, :] if i < 16 else osample[l, :, :]
                        dma("sp", dst, stg, reads=scr(sg, sg + 1), key=("stg", sg))

                prev2 = []

                def pn_a(i):
                    rows = tile_rows(i)
                    c0, n = tcols(i)
                    return prenorm_tile(X[0:rows, i, :], rows, ("x", i), prm(l, "gpm", 8), HT3(c0, n), ("ht", i), i % 2, defer=True)
                pn_b = {0: pn_a(0)}
                for i in range(17):
                    if i + 1 < 17:
                        pn_b[i + 1] = pn_a(i + 1)
                    pn_b.pop(i)()
                    if i >= 1:
                        vk_tile(i - 1)
                        s1_, s2_ = sgu_ops(i - 1)
                        interleave(s1_, prev2)
                        prev2 = s2_
                vk_tile(16)
                s1_, s2_ = sgu_ops(16)
                interleave(s1_, prev2)
                interleave([], s2_)
                S.op("dve", lambda e: e.memset(bf(MX_O + 15 * 1024, 2), 0.0),
                     writes=[("sguA", g) for g in range(6)] + [("sguB", g) for g in range(6)] + [("sguc", 0), ("sguc", 1)]
                     + [("mx", c, i, p0) for c in (0, 1, 2, 3) for i in range(17) for p0 in (0, 64)])

                _chk("vk%d" % l)
                NB = SM[0:8, 210:211]
                S.op("dve", lambda e, l=l: e.tensor_scalar(NB, prm(l, "bf", 1, 0, 8), -1.0, None, ALU.mult),
                     reads=["prm"], writes=["cw", "nb"])
                for bi, (c0, n) in enumerate(BLKS):
                    k = psnext("g")
                    mm_group(None, [(PS[k][0:8, 0:n], WS(2, c, 0, 8), HT(c, c0, n), c == 0, c == 7) for c in range(8)],
                             reads=ALLHT + [("ws", 2)], writes=["cw", ("ps", k)])
                    tmp = SCRf(0, 512, 96, 104)
                    S.op("act", lambda e, k=k, n=n, tmp=tmp: e.activation(tmp[:, 0:n], PS[k][0:8, 0:n], AF.Exp, bias=NB, scale=-1.0),
                         reads=[("ps", k), "nb"], writes=scr(0, 1))
                    S.op("act", lambda e, n=n, tmp=tmp: e.activation(tmp[:, 0:n], tmp[:, 0:n], AF.Ln, bias=1.0, scale=1.0),
                         reads=scr(0, 1), writes=scr(0, 1))
                    S.op("dve", lambda e, c0=c0, n=n, tmp=tmp: e.tensor_scalar(CT(c0, n), tmp[:, 0:n], -1.0, None, ALU.mult),
                         reads=scr(0, 1), writes=["cw", ("ct", bi)])
                dma("sp", olf[l], CT(0, 2048), reads=[("ct", b) for b in range(4)], writes=["cw"], key="olf")
                dma("sp", oslf[l], CT(2048, 32), reads=[("ct", 4)], writes=["cw"], key="olf2")
                SLF = SM[96:104, 220:252]
                S.op("dve", lambda e: e.tensor_copy(SLF, CT(2048, 32)), reads=[("ct", 4)], writes=["cw", "slf"])
                S.op("dve", lambda e: e.tensor_tensor_scan(CT(0, 2048), ONE8.to_broadcast([8, 2048]), CT(0, 2048), 0.0, ALU.mult, ALU.add),
                     reads=[("ct", b) for b in range(4)] + ["one"], writes=["cw", "ctp"])
                S.op("dve", lambda e: e.tensor_copy(HI(0, 2048), CT(0, 2048)), reads=["ctp"], writes=["cw", "hi"])
                S.op("dve", lambda e: e.tensor_tensor(CT(0, 2048), CT(0, 2048), HI(0, 2048), ALU.subtract),
                     reads=["ctp", "hi"], writes=["cw", "ctp"])
                S.op("dve", lambda e: e.tensor_copy(LO(0, 2048), CT(0, 2048)), reads=["ctp"], writes=["cw", "lo"])
                dma("sp", cscr_p[:, 0, :], HI(0, 2048), reads=["hi"], writes=["cw", "cscr_p"], key="cscr0")
                dma("sp", cscr_p[:, 1, :], LO(0, 2048), reads=["lo"], writes=["cw", "cscr_p2"], key="cscr1")
                CS = lambda b, c0, n: CT(b * 1040 + c0, n)
                for b in range(2):
                    dma("sp", CS(b, 0, 1024), clfT[l, b], reads=["ctp", "lo", ("ct", 4), "slf"], writes=["cw", ("cs", b)], key=("csl", b))
                    S.op("dve", lambda e, b=b: e.tensor_copy(CS(b, 1024, 16), SM[96:104, 220 + 16 * b:236 + 16 * b]),
                         reads=["slf", "ctp", "lo", ("ct", 4)], writes=["cw", ("cs2", b)])
                    S.op("dve", lambda e, b=b: e.tensor_tensor_scan(CS(b, 0, 1040), ONE8.to_broadcast([8, 1040]), CS(b, 0, 1040), 0.0, ALU.mult, ALU.add),
                         reads=[("cs", b), ("cs2", b), "one"], writes=["cw", ("csc", b)])
                S.op("dve", lambda e: e.tensor_copy(HI(0, 2080), CT(0, 2080)),
                     reads=[("csc", 0), ("csc", 1), "cscr_p", "cscr_p2"], writes=["cw", "hi"])
                S.op("dve", lambda e: e.tensor_tensor(CT(0, 2080), CT(0, 2080), HI(0, 2080), ALU.subtract),
                     reads=["hi"], writes=["cw", ("csc", 0), ("csc", 1)])
                S.op("dve", lambda e: e.tensor_copy(LO(0, 2080), CT(0, 2080)), reads=[("csc", 0), ("csc", 1), "cscr_p2"], writes=["cw", "lo"])
                for b in range(2):
                    dma("sp", cscr_s[:, 0, b, :], HI(b * 1040, 1040), reads=["hi"], writes=["cw", ("cscr_s", b, 0)], key=("cscrs", b, 0))
                    dma("sp", cscr_s[:, 1, b, :], LO(b * 1040, 1040), reads=["lo"], writes=["cw", ("cscr_s", b, 1)], key=("cscrs", b, 1))

                _chk("fg%d" % l)
                _chk("sgu%d" % l)
                S.op("dve", lambda e: e.memset(bf(QK_O, 4 * NT, 64, 128), 0.0),
                     writes=["cw", "hi", "lo"] + [("qkc", s_) for s_ in range(4)])
                for s_ in (0, 2):
                    S.op("dve", lambda e, s_=s_: e.memset(QK(s_, 0, NT, 64, 67), -1.0), writes=[("qkc", s_)])
                for s_ in (1, 3):
                    S.op("dve", lambda e, s_=s_: e.memset(QK(s_, 0, NT, 64, 65), 1.0), writes=[("qkc", s_)])
                wload(0, w_mkv[l], 512)
                MEMT3 = AR[:, (SCR_O + 4096) // 2:(SCR_O + 4096) // 2 + 2048].rearrange("p (c t) -> p c t", c=8)
                MEMT = lambda c, c0, n: bf(SCR_O + 4096 + (c * 256 + c0) * 2, n)
                memx = f32(WS_O[2], 1024)
                for mt in range(2):
                    dma("sp", memx, mem[128 * mt:128 * mt + 128, :], writes=[("ws", 2)], key=("ws", 2))
                    prenorm_tile(memx, 128, ("ws", 2), prm(l, "gmem", 8), MEMT3[:, :, 128 * mt:128 * mt + 128],
                                 ("memt", mt), 0, extra_writes=scr(4, 5, 6, 7))
                memt_toks = [("memt", 0), ("memt", 1)] + scr(4, 5, 6, 7)
                stgk = f32(WS_O[2], 512)
                for mt in range(2):
                    k = psnext("g")
                    mm_group(None, [(PS[k][:, :], MEMT(c, 128 * mt, 128), WS(0, c, 0, 512), c == 0, c == 7) for c in range(8)],
                             reads=memt_toks + [("ws", 0)], writes=[("ps", k)])
                    S.op("act", lambda e, k=k: e.copy(stgk, PS[k][:, :]), reads=[("ps", k)], writes=[("ws", 2)])
                    S.op("dve", lambda e, k=k, mt=mt: e.tensor_copy(
                        MVAt[:, mt * 260:(mt + 1) * 260].rearrange("p (h k) -> p h k", k=65)[:, :, 0:64],
                        PS[k][:, 256:512].rearrange("p (h k) -> p h k", k=64)),
                        reads=[("ps", k), "mva_ones"], writes=[("mva", mt)])
                    dma("sp", omk[l, 128 * mt:128 * mt + 128, :], stgk[:, 0:256], reads=[("ws", 2)], key="omk")
                    dma("sp", omv[l, 128 * mt:128 * mt + 128, :], stgk[:, 256:512], reads=[("ws", 2)], key="omv")
                for pr in range(2):
                    k = psnext("g")
                    mm_group(None, [(PS[k][:, 0:256], WS(0, c, 128 * pr, 128), MEMT(c, 0, 256), c == 0, c == 7) for c in range(8)],
                             reads=memt_toks + [("ws", 0)], writes=[("ps", k)])
                    for hh in range(2):
                        h = 2 * pr + hh
                        S.op("act", lambda e, k=k, hh=hh, h=h: e.copy(MKTt[0:64, h * 256:(h + 1) * 256], PS[k][64 * hh:64 * hh + 64, 0:256]),
                             reads=[("ps", k)], writes=[("mkt", h)])
                wload(2, w_in[l][:, 2056:2312], 256)
                for b in range(2):
                    S.op("dve", lambda e, b=b: e.memset(KAS(b, 0, 1040, 64, 65), 1.0), reads=[("ws", 0)], writes=[("ws", 0), ("sc", b, "one")])
                    S.op("dve", lambda e, b=b: e.memset(VAS3(b)[:, :, 64:65], 1.0), reads=[("ws", 0)], writes=[("ws", 0), ("sc", b, "one")])

                S.op("dve", lambda e: e.memset(SM[:, 160:200], 0.0),
                     writes=[("ssq", g, i) for g in (0, 2) for i in range(17)])

                _chk("memkv%d" % l)
                for pr in range(2):
                    for bi, (c0, n) in enumerate(BLKS):
                        k = psnext("g")
                        mm_group(None, [(PS[k][:, 0:n], WS(2, c, 128 * pr, 128), HT(c, c0, n), c == 0, c == 7) for c in range(8)],
                                 reads=ALLHT + [("ws", 2)], writes=[("ps", k)])
                        for hh in range(2):
                            S.op("act", lambda e, k=k, hh=hh, c0=c0, n=n: e.activation(
                                QK(2 * hh, c0, n, 0, 64), PS[k][64 * hh:64 * hh + 64, 0:n], AF.Copy, scale=0.125),
                                reads=[("ps", k)], writes=[("qa", hh, bi)])
                    for hh in range(2):
                        h = 2 * pr + hh
                        for I in range(4):
                            ob = psnext("o")
                            kts = [dict(ka=MKTt[0:128, h * 256 + 128 * j:h * 256 + 128 * j + 128], nk=128,
                                        va=bf(MVA_O + ((j * 4 + h) * 65) * 2, 128), lo=0, hi=512, pv_lo=0,
                                        mask=None, pt="rot", reads=[("mkt", h), ("qa", hh, I), "mkt_zero", ("qkc", 0), ("qkc", 2)],
                                        vreads=[("mva", j), "mva_ones"])
                                   for j in range(2)]
                            osubs = [(PS[ob][:, 65 * s:65 * s + 65], 128 * s, 128 * s + 128, 0, 1) for s in range(4)]
                            attend(lambda c0, n, hh=hh, I=I: QK(2 * hh, 512 * I + c0, n, 0, 128), 128, kts, "fm", ob, "mem")
                            nsq.step(lambda ob=ob, I=I, pr=pr, hh=hh: norm_store_fm(ob, I, 2, 6 + pr, 64 * hh, SM[:, 180 + 4 * I:184 + 4 * I], l))
                        ob = psnext("g")
                        kts = []
                        for b in range(2):
                            dma("pool", KAS(b, 0, 256, 0, 64), cmkT[l, b, h], reads=[("ws", 0)], writes=[("sc", b, "k")], key=("sck", b))
                            dma("pool", VAS3(b)[:, 0:2, 0:64],
                                cmv[l, b].rearrange("(j p) (h d) -> p j h d", p=128, h=4)[:, :, h, :],
                                reads=[("ws", 0)], writes=[("sc", b, "v")], key=("scv", b), slow=True)
                            for j in range(2):
                                kts.append(dict(ka=KAS(b, 128 * j, 128, 0, 64), nk=128, va=VAS(b, j),
                                                lo=16 * b, hi=16 * b + 16, pv_lo=0, mask=None, pt=b,
                                                reads=[("sc", b, "k"), ("qa", hh, 4)], vreads=[("sc", b, "v"), ("sc", b, "one")]))
                        osubs = [(PS[ob][0:32, 0:65], 0, 32, 0, len(kts) - 1)]
                        attend(lambda c0, n, hh=hh: QK(2 * hh, 2048 + c0, n, 0, 64), 64, kts, osubs, ob, "mems")
                        norm_and_store(ob, 0, 32, 16, 2, 6 + pr, 64 * hh, SSMEM(16, 32), l)

                flush_pv()
                nsq.flush()
                _chk("memattn%d" % l)
                wload(2, w_in[l][:, 0:512], 512)
                for pr in range(4):
                    for bi, (c0, n) in enumerate(BLKS):
                        for qk, slot in ((0, 2), (1, 1)):
                            k = psnext("g")
                            mm_group(None, [(PS[k][:, 0:n], WS(slot, c, 128 * pr, 128), HT(c, c0, n), c == 0, c == 7) for c in range(8)],
                                     reads=ALLHT + [("ws", slot)], writes=[("ps", k)])
                            for hh in range(2):
                                if qk == 0:
                                    S.op("act", lambda e, k=k, hh=hh, c0=c0, n=n: e.activation(
                                        QK(2 * hh, c0, n, 0, 64), PS[k][64 * hh:64 * hh + 64, 0:n], AF.Copy, scale=0.125),
                                        reads=[("ps", k)], writes=[("qa", hh, bi)])
                                else:
                                    S.op("dve", lambda e, k=k, hh=hh, c0=c0, n=n: e.tensor_copy(
                                        QK(2 * hh + 1, c0, n, 0, 64), PS[k][64 * hh:64 * hh + 64, 0:n]),
                                        reads=[("ps", k)], writes=[("ka", hh, bi)])
                    for hh in range(2):
                        h = 2 * pr + hh
                        dma("sp", QK(2 * hh, 0, 2048, 64, 65), cscr_p[h:h + 1, 0, :], reads=["cscr_p", ("qkc", 2 * hh)],
                            writes=[("qar", hh, 0)], key=("qar", hh))
                        dma("sp", QK(2 * hh + 1, 0, 2048, 65, 66), cscr_p[h:h + 1, 0, :], reads=["cscr_p", ("qkc", 2 * hh + 1)],
                            writes=[("kar", hh, 0)], key=("kar", hh))
                        dma("sp", QK(2 * hh + 1, 0, 2048, 66, 67), cscr_p[h:h + 1, 1, :], reads=["cscr_p2"],
                            writes=[("kar", hh, 1)], key=("kar", hh))
                        for b in range(2):
                            dma("sp", QK(2 * hh, 2048 + 16 * b, 16, 64, 65), cscr_s[h:h + 1, 0, b, 1024:1040],
                                reads=[("cscr_s", b, 0)], writes=[("qar", hh, 1 + b)], key=("qar", hh))
                            dma("sp", QK(2 * hh + 1, 2048 + 16 * b, 16, 65, 66), cscr_s[h:h + 1, 0, b, 1024:1040],
                                reads=[("cscr_s", b, 0)], writes=[("kar", hh, 2 + b)], key=("kar", hh))
                            dma("sp", QK(2 * hh + 1, 2048 + 16 * b, 16, 66, 67), cscr_s[h:h + 1, 1, b, 1024:1040],
                                reads=[("cscr_s", b, 1)], writes=[("kar", hh, 4 + b)], key=("kar", hh))
                        qar = [("qar", hh, x) for x in range(3)] + [("qkc", 2 * hh)]
                        kar = [("kar", hh, x) for x in range(6)] + [("qkc", 2 * hh + 1)]
                        for I in range(4):
                            ob = psnext("o")
                            kts = []
                            for j in range(4 * I + 4):
                                a = j - 4 * I
                                kts.append(dict(ka=QK(2 * hh + 1, 128 * j, 128, 0, 128), nk=128, va=VA128(j, h),
                                                lo=(128 * a if a >= 0 else 0), hi=512, pv_lo=(128 * a if a >= 0 else 0),
                                                mask=((MASKNEG, 128) if a >= 0 else None), pt="rot",
                                                reads=[("ka", hh, j // 4), ("qa", hh, I)] + qar + kar,
                                                vreads=[("va", j), "va_ones"]))
                            osubs = [(PS[ob][:, 65 * s:65 * s + 65], 128 * s, 128 * s + 128, 0, 4 * I + s) for s in range(4)]
                            attend(lambda c0, n, hh=hh, I=I: QK(2 * hh, 512 * I + c0, n, 0, 128), 128, kts, "fm", ob, "fox")
                            nsq.step(lambda ob=ob, I=I, pr=pr, hh=hh: norm_store_fm(ob, I, 0, pr, 64 * hh, SM[:, 160 + 4 * I:164 + 4 * I], l))
                        ob = psnext("g")
                        kts = []
                        for b in range(2):
                            dma("pool", KAS(b, 0, 1024, 0, 64), ckT[l, b, h], reads=[("ws", 0)], writes=[("sc", b, "k")], key=("sck", b))
                            dma("pool", VAS3(b)[:, :, 0:64],
                                cv[l, b].rearrange("(j p) (h d) -> p j h d", p=128, h=8)[:, :, h, :],
                                reads=[("ws", 0)], writes=[("sc", b, "v")], key=("scv", b), slow=True)
                            dma("sp", KAS(b, 0, 1024, 65, 66), cscr_s[h:h + 1, 0, b, 0:1024], reads=[("cscr_s", b, 0), ("ws", 0)],
                                writes=[("sc", b, "r")], key=("scr", b))
                            dma("sp", KAS(b, 0, 1024, 66, 67), cscr_s[h:h + 1, 1, b, 0:1024], reads=[("cscr_s", b, 1), ("ws", 0)],
                                writes=[("sc", b, "r2")], key=("scr", b))
                            for j in range(8):
                                kts.append(dict(ka=KAS(b, 128 * j, 128), nk=128, va=VAS(b, j),
                                                lo=16 * b, hi=16 * b + 16, pv_lo=0, mask=None, pt=b,
                                                reads=[("sc", b, "k"), ("sc", b, "r"), ("sc", b, "r2"), ("sc", b, "one"), ("qa", hh, 4)] + qar,
                                                vreads=[("sc", b, "v"), ("sc", b, "one")]))
                        kts.append(dict(ka=QK(2 * hh + 1, 2048, 32, 0, 67), nk=32, va=VA(16, h, 32), lo=0, hi=32, pv_lo=0,
                                        mask=(MASKS, 32), pt=2, reads=[("ka", hh, 4), ("qa", hh, 4)] + qar + kar,
                                        vreads=[("va", 16), "va_ones"]))
                        osubs = [(PS[ob][0:32, 0:65], 0, 32, 0, len(kts) - 1)]
                        attend(lambda c0, n, hh=hh: QK(2 * hh, 2048 + c0, n, 0, 67), 67, kts, osubs, ob, "foxs")
                        norm_and_store(ob, 0, 32, 16, 0, pr, 64 * hh, SSFOX(16, 32), l)

                flush_pv()
                nsq.flush()
                _chk("fox%d" % l)
                S.op("act", lambda e: e.activation(SM[:, 100:117], SM[:, 160:177], AF.Ln, bias=EPS, scale=1.0 / 512),
                     reads=[("ssq", 0, i) for i in range(17)], writes=["rfox"])
                S.op("act", lambda e: e.activation(SM[:, 100:117], SM[:, 100:117], AF.Exp, scale=-0.5), reads=["rfox"], writes=["rfox"])
                S.op("act", lambda e: e.activation(SM[:, 140:157], SM[:, 180:197], AF.Ln, bias=EPS, scale=1.0 / 256),
                     reads=[("ssq", 2, i) for i in range(17)], writes=["rmem"])
                S.op("act", lambda e: e.activation(SM[:, 140:157], SM[:, 140:157], AF.Exp, scale=-0.5), reads=["rmem"], writes=["rmem"])

                wload(0, w_out[l][:, 0:512], 512, extra_writes=SC_TOKS)
                wload(1, w_out[l][:, 512:1024], 512)
                GP = SCRf(4, 1024)
                dma("sp", GP, g_post_mix[l:l + 1, :].broadcast_to([128, D]), writes=scr(4, 5, 6, 7), key="gp")
                for i in range(17):
                    rows = tile_rows(i)
                    c0, n = tcols(i)
                    ostg = SCRf(0, 1024, 0, rows)
                    mxr = [("mx", c, i, p0) for c in (0, 1, 2, 3, 6, 7) for p0 in (0, 64)] + [("mx", 4, i), ("mx", 5, i)]
                    last_b = None
                    for hf in range(2):
                        banks = [psnext("w") for _ in range(3)]
                        for gi, (cs, b) in enumerate(zip(((0, 1, 2, 3), (4, 5), (6, 7)), banks)):
                            mm_group(None, [(PS[b][0:rows, :], MX(c, c0, n), WS(hf, c, 0, 512), c == cs[0], c == cs[-1]) for c in cs],
                                     reads=mxr + [("ws", hf)], writes=[("ps", b)])
                        o_h = ostg[:, 512 * hf:512 * hf + 512]
                        S.op("act", lambda e, o_h=o_h, b=banks[0], rows=rows, i=i: e.activation(o_h, PS[b][0:rows, :], AF.Copy, scale=RFOX(i, rows)),
                             reads=[("ps", banks[0]), "rfox"], writes=scr(2 * hf, 2 * hf + 1))
                        S.op("dve", lambda e, o_h=o_h, b=banks[1], rows=rows, i=i: e.scalar_tensor_tensor(o_h, PS[b][0:rows, :], RSGU(i, rows), o_h, ALU.mult, ALU.add),
                             reads=[("ps", banks[1]), ("rsgu", i)] + scr(2 * hf, 2 * hf + 1), writes=scr(2 * hf, 2 * hf + 1))
                        S.op("dve", lambda e, o_h=o_h, b=banks[2], rows=rows, i=i: e.scalar_tensor_tensor(o_h, PS[b][0:rows, :], RMEM(i, rows), o_h, ALU.mult, ALU.add),
                             reads=[("ps", banks[2]), "rmem"] + scr(2 * hf, 2 * hf + 1), writes=scr(2 * hf, 2 * hf + 1))
                        last_b = banks
                    post_norm_residual(ostg, rows, i, bf(WS_O[2], 1024, 0, rows), [("ws", 2)], GP, scr(4, 5, 6, 7), scr(0, 1, 2, 3))

                _chk("wout%d" % l)
                S.barrier()
                ffn_phase(l)
                S.barrier()

        except _Stop:
            pass
        for i in range(16):
            dma("sp", y_p[128 * i:128 * i + 128, :], X[:, i, :], reads=[("x", i)], key=("x", i))
        dma("sp", y_s, X[0:32, 16, :], reads=[("x", 16)], key=("x", 16))
        S.emit(st)
    return nc


_NC_CACHE = {}


def _prep_inputs(inp):
    f = lambda a: np.ascontiguousarray(np.asarray(a, dtype=np.float32))
    x_prompt = f(inp["x_prompt"]); x_sample = f(inp["x_sample"]); mem_prompt = f(inp["mem_prompt"])
    cfk = f(inp["cache_fox_k"]); cfv = f(inp["cache_fox_v"]); clf = f(inp["cache_fox_logf"])
    cmk = f(inp["cache_mem_k"]); cmvv = f(inp["cache_mem_v"]); cconv = f(inp["cache_ffn_conv"])
    ckT_all = np.ascontiguousarray(cfk.transpose(0, 1, 3, 4, 2))
    cv_all = cfv.reshape(L, 16, 1024, 512)
    clfT_all = np.ascontiguousarray(clf.transpose(0, 1, 3, 2))
    cmkT_all = np.ascontiguousarray(cmk.transpose(0, 1, 3, 4, 2))
    cmv_all = cmvv.reshape(L, 16, 256, 256)
    wsT = np.ascontiguousarray(f(inp["w_spatial"]).transpose(0, 1, 3, 2))
    fm8 = lambda g: f(g).reshape(L, 8, 128).transpose(0, 2, 1)
    b_sp = f(inp["b_spatial"])
    w_dw = f(inp["w_dwconv"]); b_dw = f(inp["b_dwconv"]); b_f = f(inp["b_forget"])
    cst = np.zeros((128, 288), dtype=ml_dtypes.bfloat16)
    cst[:, 0:128] = np.eye(128, dtype=np.float32).astype(ml_dtypes.bfloat16)
    kk, qq = np.meshgrid(np.arange(128), np.arange(128), indexing="ij")
    cst[:, 128:256] = np.where(kk <= qq, 0.0, NEG).astype(ml_dtypes.bfloat16)
    k2, q2 = np.meshgrid(np.arange(32), np.arange(32), indexing="ij")
    ok = (k2 // 16 == q2 // 16) & (k2 % 16 <= q2 % 16)
    cst[0:32, 256:288] = np.where(ok, 0.0, NEG).astype(ml_dtypes.bfloat16)
    wu = f(inp["w_up"])
    wu_a = wu[:, :, 0:FF].reshape(L, 8, 128, NJ, 128)
    wu_l = wu[:, :, FF:2 * FF].reshape(L, 8, 128, NJ, 128)
    w_up_r = np.ascontiguousarray(np.concatenate([wu_a, wu_l], axis=4).transpose(0, 3, 2, 1, 4))
    shared = dict(
        w_in=f(inp["w_in"]), w_mkv=f(inp["w_mem_kv"]), w_out=f(inp["w_out"]), w_up=w_up_r,
        w_dn=f(inp["w_down"]), g_post_mix=f(inp["g_post_mix"]), g_post_ffn=f(inp["g_post_ffn"]),
        g_sgu=f(inp["g_sgu"]), wsT=wsT, cst=cst)
    gpm = fm8(inp["g_pre_mix"]); ggo = fm8(inp["g_group_out"]); gmem = fm8(inp["g_mem"]); gpf = fm8(inp["g_pre_ffn"])
    in_maps = []
    for c in range(NCORES):
        prm = np.zeros((128, NPRM), dtype=np.float32)
        for l in range(L):
            o = l * PPL
            prm[:, o + PO["gpm"]:o + PO["gpm"] + 8] = gpm[l]
            prm[:, o + PO["ggo"]:o + PO["ggo"] + 8] = ggo[l]
            prm[:, o + PO["gmem"]:o + PO["gmem"] + 8] = gmem[l]
            prm[:, o + PO["gpf"]:o + PO["gpf"] + 8] = gpf[l]
            prm[:, o + PO["bs"]:o + PO["bs"] + 4] = b_sp[l].T
            prm[0:16, o + PO["bss"]:o + PO["bss"] + 4] = b_sp[l][:, 0:16].T
            prm[16:32, o + PO["bss"]:o + PO["bss"] + 4] = b_sp[l][:, 0:16].T
            prm[:, o + PO["wdw"]:o + PO["wdw"] + 66] = w_dw[l].reshape(3, NJ, 128).transpose(2, 1, 0).reshape(128, 66)
            prm[:, o + PO["bdw"]:o + PO["bdw"] + 22] = b_dw[l].reshape(NJ, 128).T
            prm[0:8, o + PO["bf"]] = b_f[l]
            cc = cconv[l, 2 * c:2 * c + 2]
            prm[:, o + PO["cconv"]:o + PO["cconv"] + 88] = cc.reshape(2, 2, NJ, 128).transpose(3, 2, 0, 1).reshape(128, 88)
        m = dict(shared)
        m.update(
            x_p=x_prompt[c], x_s=x_sample[2 * c:2 * c + 2].reshape(32, D), mem=mem_prompt[c],
            ckT=np.ascontiguousarray(ckT_all[:, 2 * c:2 * c + 2]), cv=np.ascontiguousarray(cv_all[:, 2 * c:2 * c + 2]),
            clfT=np.ascontiguousarray(clfT_all[:, 2 * c:2 * c + 2]), cmkT=np.ascontiguousarray(cmkT_all[:, 2 * c:2 * c + 2]),
            cmv=np.ascontiguousarray(cmv_all[:, 2 * c:2 * c + 2]), prm=prm)
        in_maps.append(m)
    return in_maps


def kernel(**inp):
    in_maps = _prep_inputs(inp)
    if "nc" not in _NC_CACHE:
        _NC_CACHE["nc"] = build()
    nc = _NC_CACHE["nc"]
    res = run_bass_kernel_spmd(nc, in_maps, core_ids=list(range(NCORES)))
    R = res.results
    cat = lambda name: np.stack([np.asarray(r[name], dtype=np.float32) for r in R], axis=0)
    y_p = cat("y_p")
    y_s = cat("y_s").reshape(16, 16, D)
    fk = cat("ofk").transpose(1, 0, 2, 3).reshape(L, 8, 2048, 8, 64)
    fv = cat("ofv").transpose(1, 0, 2, 3).reshape(L, 8, 2048, 8, 64)
    lf = cat("olf").transpose(1, 0, 3, 2)
    mk = cat("omk").transpose(1, 0, 2, 3).reshape(L, 8, 256, 4, 64)
    mv = cat("omv").transpose(1, 0, 2, 3).reshape(L, 8, 256, 4, 64)
    cvp = cat("oconv").transpose(1, 0, 4, 3, 2).reshape(L, 8, 2, FF)
    sk = cat("osk").transpose(1, 0, 2, 3).reshape(L, 16, 16, 8, 64)
    sv = cat("osv").transpose(1, 0, 2, 3).reshape(L, 16, 16, 8, 64)
    slf = cat("oslf").transpose(1, 0, 3, 2).reshape(L, 16, 16, 8)
    gv = cat("ogv").transpose(1, 0, 2, 3).reshape(L, 16, 16, 256)
    cvs = cat("osconv").transpose(1, 0, 4, 5, 3, 2).reshape(L, 16, 2, FF)
    outs = (y_p, y_s, fk, fv, lf, mk, mv, cvp, sk, sv, slf, gv, cvs)
    return tuple(np.ascontiguousarray(o, dtype=np.float32) for o in outs)
```

```python
import numpy as np
import ml_dtypes
from contextlib import ExitStack
import concourse.bass as bass
import concourse.mybir as mybir
from concourse.bass_utils import run_bass_kernel_spmd

F32 = mybir.dt.float32
BF16 = mybir.dt.bfloat16
ALU = mybir.AluOpType
AF = mybir.ActivationFunctionType

ENGS = ("pe", "act", "dve", "pool", "sp")
STOP = None
DBG = set()


class _Stop(Exception):
    pass


def _chk(name):
    if STOP == name:
        raise _Stop()
NCORES = 8
L = 2
D = 1024
NT = 2080
FF = 2816
NJ = 22
EPS = 1e-6
NEG = -30000.0


class Op:
    __slots__ = ("eng", "fn", "idx", "deps", "dma_key", "marked", "seq", "cum")

    def __init__(self, eng, fn, idx, dma_key):
        self.eng = eng
        self.fn = fn
        self.idx = idx
        self.deps = ()
        self.dma_key = dma_key
        self.marked = False
        self.seq = 0
        self.cum = 0


class Sched:
    def __init__(self, nc):
        self.nc = nc
        self.streams = {e: [] for e in ENGS}
        self.last_writer = {}
        self.readers = {}
        self.dma_counts = {}
        self.dma_last = {}

    def op(self, eng, fn, reads=(), writes=(), dma_key=None):
        o = Op(eng, fn, len(self.streams[eng]), dma_key)
        deps = set()
        lw = self.last_writer
        rd = self.readers
        for t in reads:
            w = lw.get(t)
            if w is not None:
                deps.add(w)
            if type(t) is tuple and t[0] == "ps":
                for r in rd.get(t, ()):
                    if r.eng != eng:
                        deps.add(r)
        for t in writes:
            w = lw.get(t)
            if w is not None:
                deps.add(w)
            r = rd.get(t)
            if r:
                deps.update(r)
        for t in reads:
            rd.setdefault(t, []).append(o)
        for t in writes:
            lw[t] = o
            rd[t] = []
        deps.discard(o)
        if eng == "pe":
            deps = {d for d in deps if not (d.eng == "pe" and d.dma_key is None)}
        o.deps = deps
        if dma_key is not None:
            c = self.dma_counts.get(dma_key, 0) + 1
            self.dma_counts[dma_key] = c
            o.cum = 16 * c
            self.dma_last[dma_key] = o
        self.streams[eng].append(o)
        return o

    def barrier(self):
        lasts = [s[-1] for s in self.streams.values() if s]
        lasts += list(self.dma_last.values())
        for e in ENGS:
            o = Op(e, lambda eng: eng.nop(), len(self.streams[e]), None)
            o.deps = {d for d in lasts if not (d.eng == e and d.dma_key is None)}
            self.streams[e].append(o)

    def emit(self, stack):
        nc = self.nc
        for e in ENGS:
            for o in self.streams[e]:
                for d in o.deps:
                    if d.dma_key is None:
                        d.marked = True
        sems = {}
        for e in ENGS:
            n = 0
            for o in self.streams[e]:
                if o.dma_key is None and o.marked:
                    n += 1
                    o.seq = n
            if n:
                sems[e] = stack.enter_context(nc.semaphore("s_" + e))
        dsems = {}
        for k in self.dma_counts:
            dsems[k] = stack.enter_context(nc.semaphore("d%d" % len(dsems)))
        self.n_sems = len(sems) + len(dsems)
        final = {k: o.cum for k, o in self.dma_last.items()}

        def run(eng_name, eng):
            waited = {}
            for o in self.streams[eng_name]:
                need = {}
                for d in o.deps:
                    if d.dma_key is not None:
                        key = ("d", d.dma_key)
                        val = d.cum
                    else:
                        key = ("c", d.eng)
                        val = d.seq
                    if val > need.get(key, 0):
                        need[key] = val
                for key, val in need.items():
                    if waited.get(key, 0) >= val:
                        continue
                    waited[key] = val
                    s = dsems[key[1]] if key[0] == "d" else sems[key[1]]
                    eng.wait_ge(s, val)
                ins = o.fn(eng)
                if o.dma_key is not None:
                    ins.then_inc(dsems[o.dma_key], 16)
                elif o.marked:
                    ins.then_inc(sems[o.eng], 1)
            if eng_name == "sp":
                for k, v in final.items():
                    if waited.get(("d", k), 0) < v:
                        eng.wait_ge(dsems[k], v)

        with nc.Block() as block:
            @block.tensor
            def _(t):
                run("pe", t)

            @block.scalar
            def _(t):
                run("act", t)

            @block.vector
            def _(t):
                run("dve", t)

            @block.gpsimd
            def _(t):
                run("pool", t)

            @block.sync
            def _(t):
                run("sp", t)


def tile_rows(i):
    return 128 if i < 16 else 32


def tcols(i):
    return (128 * i, 128) if i < 16 else (2048, 32)


BLKS = [(0, 512), (512, 512), (1024, 512), (1536, 512), (2048, 32)]

PO = {}
_o = 0
for _n, _w in (("gpm", 8), ("ggo", 8), ("gmem", 8), ("gpf", 8), ("bs", 4), ("bss", 4),
               ("wdw", 66), ("bdw", 22), ("bf", 1), ("cconv", 88)):
    PO[_n] = _o
    _o += _w
PPL = _o
NPRM = PPL * L


def build():
    nc = bass.Bass("TRN2", target_bir_lowering=False)

    def din(name, shape, dt=F32):
        return nc.dram_tensor(name, list(shape), dt, kind="ExternalInput").ap()

    def dout(name, shape):
        return nc.dram_tensor(name, list(shape), F32, kind="ExternalOutput").ap()

    x_p = din("x_p", [2048, D])
    x_s = din("x_s", [32, D])
    mem = din("mem", [256, D])
    ckT = din("ckT", [L, 2, 8, 64, 1024])
    cv = din("cv", [L, 2, 1024, 512])
    clfT = din("clfT", [L, 2, 8, 1024])
    cmkT = din("cmkT", [L, 2, 4, 64, 256])
    cmv = din("cmv", [L, 2, 256, 256])
    w_in = din("w_in", [L, D, 2312])
    w_mkv = din("w_mkv", [L, D, 512])
    w_out = din("w_out", [L, D, D])
    w_up = din("w_up", [L, NJ, 128, 8, 256])
    w_dn = din("w_dn", [L, FF, D])
    g_post_mix = din("g_post_mix", [L, D])
    g_post_ffn = din("g_post_ffn", [L, D])
    g_sgu = din("g_sgu", [L, 256])
    wsT = din("wsT", [L, 4, 128, 128])
    prm_h = din("prm", [128, NPRM])
    cst_h = din("cst", [128, 288], BF16)

    y_p = dout("y_p", [2048, D])
    y_s = dout("y_s", [32, D])
    ofk = dout("ofk", [L, 2048, 512])
    ofv = dout("ofv", [L, 2048, 512])
    olf = dout("olf", [L, 8, 2048])
    omk = dout("omk", [L, 256, 256])
    omv = dout("omv", [L, 256, 256])
    oconv = dout("oconv", [L, 128, NJ, 2])
    osk = dout("osk", [L, 32, 512])
    osv = dout("osv", [L, 32, 512])
    oslf = dout("oslf", [L, 8, 32])
    ogv = dout("ogv", [L, 32, 256])
    osconv = dout("osconv", [L, 128, NJ, 2, 2])

    cscr_p = nc.dram_tensor("cscr_p", [8, 2, 2048], BF16).ap()
    cscr_s = nc.dram_tensor("cscr_s", [8, 2, 2, 1040], BF16).ap()

    with ExitStack() as st:
        S = Sched(nc)
        X = st.enter_context(nc.sbuf_tensor("X", [128, 17, D], F32))
        PRM = st.enter_context(nc.sbuf_tensor("PRM", [128, NPRM], F32))
        CST = st.enter_context(nc.sbuf_tensor("CST", [128, 288], BF16))
        WSTSt = st.enter_context(nc.sbuf_tensor("WSTS", [32, L * 4 * 32], BF16))
        PTSt = st.enter_context(nc.sbuf_tensor("PTS", [128, 7 * 32], BF16))
        SM = st.enter_context(nc.sbuf_tensor("SM", [128, 256], F32))
        ONEROW = st.enter_context(nc.sbuf_tensor("ONEROW", [128, 64], F32))
        remaining = nc.sbuf_bytes_remaining
        remaining = remaining() if callable(remaining) else remaining
        ARB = (remaining - 512) // 64 * 64
        AR = st.enter_context(nc.sbuf_tensor("AR", [128, ARB // 2], BF16))
        AR32 = AR.bitcast(F32)
        PS = [st.enter_context(nc.psum_tensor("ps%d" % k, [128, 512], F32)) for k in range(8)]
        PSB = [p.bitcast(BF16) for p in PS]

        IDENT = CST[:, 0:128]
        MASKNEG = CST[:, 128:256]
        MASKS = CST[0:32, 256:288]


        def WSTS(l, g):
            o = (l * 4 + g) * 32
            return WSTSt[0:32, o:o + 32]

        def prm(l, name, w, p0=0, p1=128, off=0):
            o = l * PPL + PO[name] + off
            return PRM[p0:p1, o:o + w]

        SM_SS, SM_SD, SM_RSTD = 0, 1, 2
        SM_R = 16
        SM_SSF = 70
        SM_T = 90

        def bf(off, n, p0=0, p1=128):
            return AR[p0:p1, off // 2: off // 2 + n]

        def f32(off, n, p0=0, p1=128):
            return AR32[p0:p1, off // 4: off // 4 + n]

        cur = [0]

        def carve(nbytes):
            o = cur[0]
            cur[0] = o + (nbytes + 63) // 64 * 64
            return o

        HT_O = carve(8 * NT * 2)
        MX_O = carve(8 * NT * 2)
        VA_O = carve(17 * 8 * 65 * 2)
        QK_O = carve(4 * NT * 2)
        WS_O = [carve(8320) for _ in range(3)]
        SCR_O = carve(8192)
        MKT_O = carve(2048)
        MVA_O = carve(1040)
        MKTt = AR[:, MKT_O // 2:MKT_O // 2 + 1024]
        MVAt = AR[:, MVA_O // 2:MVA_O // 2 + 520]
        MIX_END = cur[0]
        assert MIX_END <= ARB, (MIX_END, ARB)

        def HT(c, c0, n):
            return bf(HT_O + (c * NT + c0) * 2, n)

        def HT3(c0, n):
            return AR[:, HT_O // 2: HT_O // 2 + 8 * NT].rearrange("p (c t) -> p c t", c=8)[:, :, c0:c0 + n]

        def MX(c, c0, n, p0=0, p1=128):
            return bf(MX_O + (c * NT + c0) * 2, n, p0, p1)

        def VA(j, h, nk=128):
            return bf(VA_O + ((j * 8 + h) * 65) * 2, 65, 0, nk)

        def VA128(j, h):
            return bf(VA_O + ((j * 8 + h) * 65) * 2, 128, 0, 128)

        def QK(slot, c0, n, p0, p1):
            return bf(QK_O + (slot * NT + c0) * 2, n, p0, p1)

        CT = lambda c0, n: f32(QK_O + c0 * 4, n, 96, 104)
        HI = lambda c0, n: bf(QK_O + 8320 + c0 * 2, n, 96, 104)
        LO = lambda c0, n: bf(QK_O + 8320 + 4160 + c0 * 2, n, 96, 104)

        def WS(s, c, c0, n):
            return bf(WS_O[s] + (c * 520 + c0) * 2, n)

        def WS3(s, ncol):
            return AR[:, WS_O[s] // 2: WS_O[s] // 2 + 8 * 520].rearrange("p (c n) -> p c n", c=8)[:, :, 0:ncol]

        def SCRb(g, n, p0=0, p1=128, off=0):
            return bf(SCR_O + g * 1024 + off * 2, n, p0, p1)

        def SCRf(g, n, p0=0, p1=128, off=0):
            return f32(SCR_O + g * 1024 + off * 4, n, p0, p1)

        def scr(*gs):
            return [("scr", g) for g in gs]

        def dma(eng, out, in_, reads=(), writes=(), key=None, slow=False):
            if slow:
                fn = lambda e: e.dma_start(out=out, in_=in_, allow_slow_non_contiguous=True)
            else:
                fn = lambda e: e.dma_start(out=out, in_=in_)
            return S.op(eng, fn, reads=reads, writes=writes, dma_key=key)

        psr = {"g": [0, 1], "s": [2, 3, 4], "o": [5, 6], "t": [7]}
        psi = {k: 0 for k in psr}

        def psnext(pool):
            lst = psr[pool]
            k = lst[psi[pool] % len(lst)]
            psi[pool] += 1
            return k

        def mm_group(out_fn, items, reads, writes):
            def fn(e):
                ins = None
                for (o, a, b, s0, s1) in items:
                    ins = e.matmul(o, a, b, start=s0, stop=s1, skip_group_check=True)
                return ins
            return S.op("pe", fn, reads=reads, writes=writes)

        dma("sp", PRM[:, :], prm_h, writes=["prm"], key="prm")
        dma("sp", CST[:, :], cst_h, writes=["cst"], key="cst")
        S.op("dve", lambda e: e.memset(WSTSt[:, :], 0.0), writes=["wsts"])
        for l in range(L):
            for b in range(2):
                dma("pool",
                    WSTSt[16 * b:16 * b + 16, l * 128:(l + 1) * 128].rearrange("p (g i) -> p g i", g=4)[:, :, 16 * b:16 * b + 16],
                    wsT[l, :, 0:16, 0:16].rearrange("g j i -> j g i"),
                    reads=["wsts"], writes=[("wsts", l, b)], key=("wsts", l), slow=True)
        S.op("dve", lambda e: e.memset(PTSt[:, :], 0.0), writes=["pts%d" % q for q in range(7)])
        S.op("dve", lambda e: e.memset(SM[:, 200:201], 1.0), writes=["one"])
        S.op("dve", lambda e: e.memset(ONEROW[:, :], 1.0), writes=["onerow"])
        ONE8 = SM[96:104, 200:201]

        for i in range(16):
            dma("sp", X[:, i, :], x_p[128 * i:128 * i + 128, :], writes=[("x", i)], key=("x", i))
        dma("sp", X[0:32, 16, :], x_s, writes=[("x", 16)], key=("x", 16))

        def prenorm_tile(src, rows, xtok, gcol, dst3, dst_tok, slot, extra_writes=(), hb=None, tpool="t", hb_toks=None, defer=False):
            if hb is None:
                hb = SCRb(2 * slot, 1024, 0, rows)
            hbt = list(hb_toks) if hb_toks is not None else scr(2 * slot, 2 * slot + 1)
            c0 = 3 * slot
            ss = SM[0:rows, c0:c0 + 1]
            sd = SM[0:rows, c0 + 1:c0 + 2]
            rs = SM[0:rows, c0 + 2:c0 + 3]
            S.op("act", lambda e: e.activation(hb, src, AF.Square, accum_out=ss),
                 reads=[xtok], writes=hbt + [("sm", c0)])
            S.op("act", lambda e: e.activation(sd, ss, AF.Ln, bias=EPS, scale=1.0 / D),
                 reads=[("sm", c0)], writes=[("sm", c0 + 1)])
            S.op("act", lambda e: e.activation(rs, sd, AF.Exp, scale=-0.5), reads=[("sm", c0 + 1)], writes=[("sm", c0 + 2)])
            S.op("act", lambda e: e.activation(hb, src, AF.Copy, scale=rs),
                 reads=[xtok, ("sm", c0 + 2)], writes=hbt)
            def part_b():
                k = psnext(tpool)

                def tr(e):
                    ins = None
                    for c in range(8):
                        ins = e.transpose(PSB[k][:, c * 128:c * 128 + rows], hb[:, c * 128:(c + 1) * 128],
                                          IDENT[0:rows, 0:rows])
                    return ins
                S.op("pe", tr, reads=hbt + ["cst"], writes=[("ps", k)])
                src3 = PSB[k][:, 0:1024].rearrange("p (c t) -> p c t", c=8)[:, :, 0:rows]
                S.op("dve", lambda e: e.tensor_tensor(dst3, src3, gcol.unsqueeze(2).to_broadcast([128, 8, rows]), ALU.mult),
                     reads=[("ps", k), "prm"], writes=[dst_tok] + list(extra_writes))
            if defer:
                return part_b
            part_b()

        PVQ = []

        def flush_pv():
            while PVQ:
                PVQ.pop(0)()

        def attend(qa_fn, K, ktiles, osubs, obank, tag):
            pendq = []
            first_done = [False]

            def pv(kt_i, kt, pt_ap, pt_tok):
                items = []
                if osubs == "fm":
                    lo_, hi_ = kt["lo"], kt["hi"]
                    mrows = kt["va"].shape[1]
                    items.append((PS[obank][0:mrows, lo_:hi_], kt["va"], pt_ap[:, lo_:hi_], kt_i == 0, kt_i == len(ktiles) - 1))
                    mm_group(None, items, reads=[pt_tok] + kt["vreads"], writes=[("ps", obank)])
                    return
                for (o_ap, c0, c1, fk, lk) in osubs:
                    if fk <= kt_i <= lk and kt["pv_lo"] <= c0:
                        items.append((o_ap, pt_ap[:, c0:c1], kt["va"], not first_done[0], kt_i == lk))
                        first_done[0] = True
                if items:
                    mm_group(None, items, reads=[pt_tok] + kt["vreads"], writes=[("ps", obank)])

            for kt_i, kt in enumerate(ktiles):
                k = psnext("s")
                nk, lo, hi = kt["nk"], kt["lo"], kt["hi"]
                items = []
                if kt["mask"] is not None:
                    mask_ap, mc = kt["mask"]
                    items.append((PS[k][0:nk, lo:lo + mc], kt["ka"], qa_fn(lo, mc), True, False))
                    items.append((PS[k][0:nk, lo:lo + mc], IDENT[0:nk, 0:nk], mask_ap, False, True))
                    if hi > lo + mc:
                        items.append((PS[k][0:nk, lo + mc:hi], kt["ka"], qa_fn(lo + mc, hi - lo - mc), True, True))
                else:
                    items.append((PS[k][0:nk, lo:hi], kt["ka"], qa_fn(lo, hi - lo), True, True))
                mm_group(None, items, reads=kt["reads"] + ["cst"], writes=[("ps", k)])
                if kt["pt"] == "rot":
                    g = attend.ptc % 4
                    attend.ptc += 1
                    pt_ap = SCRb(g, 512, 0, nk)
                    pt_tok = ("scr", g)
                else:
                    pt_ap = PTSt[0:nk, 32 * kt["pt"]:32 * kt["pt"] + 32]
                    pt_tok = "pts%d" % kt["pt"]
                S.op("act", lambda e, pt_ap=pt_ap, k=k, nk=nk, lo=lo, hi=hi:
                     e.activation(pt_ap[:, lo:hi], PS[k][0:nk, lo:hi], AF.Exp),
                     reads=[("ps", k)], writes=[pt_tok])
                while len(PVQ) >= 2:
                    PVQ.pop(0)()
                PVQ.append(lambda kt_i=kt_i, kt=kt, pt_ap=pt_ap, pt_tok=pt_tok: pv(kt_i, kt, pt_ap, pt_tok))
        attend.ptc = 0

        def cs_all():
            return ["cw"]

        def post_norm_residual(ostg, rows, i, junk, junk_toks, GPap, gp_toks, o_toks):
            ssc = SM[0:rows, SM_T + 8:SM_T + 9]
            S.op("act", lambda e: e.activation(junk, ostg, AF.Square, accum_out=ssc),
                 reads=o_toks, writes=list(junk_toks) + ["pn"])
            S.op("act", lambda e: e.activation(ssc, ssc, AF.Ln, bias=EPS, scale=1.0 / D), reads=["pn"], writes=["pn"])
            S.op("act", lambda e: e.activation(ssc, ssc, AF.Exp, scale=-0.5), reads=["pn"], writes=["pn"])
            S.op("dve", lambda e: e.scalar_tensor_tensor(ostg, ostg, ssc, GPap[0:rows, :], ALU.mult, ALU.mult),
                 reads=list(o_toks) + list(gp_toks) + ["pn"], writes=o_toks)
            S.op("pool", lambda e: e.tensor_tensor(X[0:rows, i, :], X[0:rows, i, :], ostg, ALU.add),
                 reads=list(o_toks) + [("x", i)], writes=[("x", i)])

        def ffn_phase(l):
            psr.clear()
            psr.update({"u": [0, 1, 2, 4, 5, 6], "t": [3], "d": [4, 5, 6, 7]})
            for kk in psr:
                psi[kk] = 0
            HC = 1056
            o = [0]

            def cv_(nb):
                r = o[0]
                o[0] = r + (nb + 63) // 64 * 64
                return r
            H2_O = cv_(8 * HC * 2)
            ACT_O = cv_(NJ * HC * 2)
            WDN_O = cv_(NJ * 1024 * 2)
            WU_O = [cv_(4096) for _ in range(2)]
            AE_O = [cv_(2064) for _ in range(2)]
            LIN_O = [cv_(1024) for _ in range(2)]
            CONV_O = [cv_(2048) for _ in range(2)]
            OSTG_O = cv_(4096)
            GPF_O = cv_(4096)
            HBF_O = cv_(2048)
            SAVE_O = cv_(NJ * 2 * 4)
            SAVES_O = cv_(NJ * 4 * 4)
            assert o[0] <= ARB, (o[0], ARB)

            H2 = lambda c, c0, n: bf(H2_O + (c * HC + c0) * 2, n)
            H23 = lambda c0, n: AR[:, H2_O // 2:H2_O // 2 + 8 * HC].rearrange("p (c t) -> p c t", c=8)[:, :, c0:c0 + n]
            ACTB = lambda j, c0, n: bf(ACT_O + (j * HC + c0) * 2, n)
            WDN = lambda j, c0, n: bf(WDN_O + (j * 1024 + c0) * 2, n)
            WDN3 = AR[:, WDN_O // 2:WDN_O // 2 + NJ * 1024].rearrange("p (j n) -> p j n", j=NJ)
            WU = lambda s, c, c0, n: bf(WU_O[s] + (c * 256 + c0) * 2, n)
            WU3 = lambda s: AR[:, WU_O[s] // 2:WU_O[s] // 2 + 2048].rearrange("p (c n) -> p c n", c=8)
            OSTG = f32(OSTG_O, 1024)
            GPF = f32(GPF_O, 1024)
            HBF = bf(HBF_O, 1024)
            SAVE = f32(SAVE_O, NJ * 2)
            SAVES = f32(SAVES_O, NJ * 4)

            dma("sp", GPF, g_post_ffn[l:l + 1, :].broadcast_to([128, D]), writes=["gpf"], key="gp")
            wdn_src = w_dn[l].rearrange("(j p) n -> p j n", p=128)
            for part in range(2):
                dma("pool", WDN3[:, 11 * part:11 * part + 11, :], wdn_src[:, 11 * part:11 * part + 11, :],
                    writes=[("wdn", part)], key=("wdn", part))
            aecnt = 0

            def wu_load(j, s_):
                dma("pool", WU3(s_), w_up[l, j], writes=[("wu", s_, 0), ("wu", s_, 1)], key=("wu", s_, 0))

            for half in range(2):
                tiles = list(range(0, 8)) if half == 0 else list(range(8, 17))
                base = 0 if half == 0 else 1024
                blocks = [(0, 512), (512, 512)] + ([(1024, 32)] if half == 1 else [])
                h2toks = [("h2", (i if i < 8 else i - 8)) for i in tiles]
                if half == 0:
                    HB2 = bf(AE_O[0], 1024)

                    def pa(i):
                        c0, n = tcols(i)
                        if i % 2 == 0:
                            return prenorm_tile(X[:, i, :], 128, ("x", i), prm(l, "gpf", 8), H23(c0, n), ("h2", i), 0,
                                                hb=HBF[:, :], tpool="t", hb_toks=["hbf"], defer=True)
                        return prenorm_tile(X[:, i, :], 128, ("x", i), prm(l, "gpf", 8), H23(c0, n), ("h2", i), 1,
                                            hb=HB2, tpool="t", hb_toks=[("ae", 0), ("aeh", 0), ("aet", 0)], defer=True)
                    pbq = {0: pa(0)}
                    for i in tiles:
                        if i + 1 < 8:
                            pbq[i + 1] = pa(i + 1)
                        pbq.pop(i)()
                wu_load(0, 0)
                pend_tail = None
                for j in range(NJ):
                    s = j % 2
                    if j + 1 < NJ:
                        wu_load(j + 1, (j + 1) % 2)
                    w0 = prm(l, "wdw", 1, 0, 128, 3 * j)
                    w1 = prm(l, "wdw", 1, 0, 128, 3 * j + 1)
                    w2 = prm(l, "wdw", 1, 0, 128, 3 * j + 2)
                    bd = prm(l, "bdw", 1, 0, 128, j)
                    for bidx, (lc0, n) in enumerate(blocks):
                        sample = (n == 32)
                        ka = psnext("u")
                        mm_group(None, [(PS[ka][:, 0:n], WU(s, c, 0, 128), H2(c, lc0, n), c == 0, c == 7) for c in range(8)],
                                 reads=h2toks + [("wu", s, 0)], writes=[("ps", ka)])
                        kl = psnext("u")
                        mm_group(None, [(PS[kl][:, 0:n], WU(s, c, 128, 128), H2(c, lc0, n), c == 0, c == 7) for c in range(8)],
                                 reads=h2toks + [("wu", s, 1)], writes=[("ps", kl)])
                        p = aecnt % 2
                        aecnt += 1
                        lin = bf(LIN_O[p], 512)
                        conv = f32(CONV_O[p], 512)
                        if sample:
                            AEs = f32(AE_O[p], 36).rearrange("p (b t) -> p b t", b=2)
                            S.op("dve", lambda e, AEs=AEs, j=j: e.tensor_copy(
                                AEs[:, :, 0:2], prm(l, "cconv", 4, 0, 128, 4 * j).rearrange("p (b r) -> p b r", b=2)),
                                reads=["prm"], writes=[("aeh", p)])
                            S.op("act", lambda e, AEs=AEs, ka=ka: e.copy(AEs[:, :, 2:18], PS[ka][:, 0:32].rearrange("p (b t) -> p b t", b=2)),
                                 reads=[("ps", ka)], writes=[("ae", p)])
                            taps = [AEs[:, :, k:k + 16] for k in range(3)]
                            cv3 = conv[:, 0:32].rearrange("p (b t) -> p b t", b=2)
                            psa = PS[ka][:, 0:32].rearrange("p (b t) -> p b t", b=2)
                            sil_out = f32(AE_O[p] + 256, 32)
                            sil_in = conv[:, 0:32]
                        else:
                            AEp = f32(AE_O[p], 514)
                            if bidx == 0 and half == 0:
                                S.op("dve", lambda e, AEp=AEp: e.memset(AEp[:, 0:2], 0.0), writes=[("aeh", p)])
                            elif bidx == 0:
                                S.op("dve", lambda e, AEp=AEp, j=j: e.tensor_copy(AEp[:, 0:2], SAVE[:, 2 * j:2 * j + 2]),
                                     reads=[("save", j)], writes=[("aeh", p)])
                            else:
                                prev = f32(AE_O[1 - p], 514)
                                S.op("dve", lambda e, AEp=AEp, prev=prev: e.tensor_copy(AEp[:, 0:2], prev[:, 512:514]),
                                     reads=[("aet", 1 - p)], writes=[("aeh", p)])
                            S.op("act", lambda e, AEp=AEp, ka=ka: e.copy(AEp[:, 2:514], PS[ka][:, 0:512]),
                                 reads=[("ps", ka)], writes=[("ae", p), ("aet", p)])
                            taps = [AEp[:, k:k + 512] for k in range(3)]
                            cv3 = conv
                            psa = PS[ka][:, 0:512]
                            sil_out = AEp[:, 0:512]
                            sil_in = conv
                        S.op("act", lambda e, cv3=cv3, psa=psa, w2=w2, bd=bd: e.activation(cv3, psa, AF.Identity, bias=bd, scale=w2),
                             reads=[("ps", ka), "prm"], writes=[("conv", p)])
                        S.op("act", lambda e, lin=lin, kl=kl, n=n: e.copy(lin[:, 0:n], PS[kl][:, 0:n]),
                             reads=[("ps", kl)], writes=[("lin", p)])
                        S.op("dve", lambda e, cv3=cv3, taps=taps, w1=w1: e.scalar_tensor_tensor(cv3, taps[1], w1, cv3, ALU.mult, ALU.add),
                             reads=[("ae", p), ("aeh", p), ("conv", p)], writes=[("conv", p)])
                        S.op("dve", lambda e, cv3=cv3, taps=taps, w0=w0: e.scalar_tensor_tensor(cv3, taps[0], w0, cv3, ALU.mult, ALU.add),
                             reads=[("ae", p), ("aeh", p), ("conv", p)], writes=[("conv", p)])
                        if not sample and bidx == 1:
                            S.op("dve", lambda e, AEp=AEp, j=j: e.tensor_copy(SAVE[:, 2 * j:2 * j + 2], AEp[:, 512:514]),
                                 reads=[("aet", p)], writes=[("save", j)])
                        if sample:
                            S.op("dve", lambda e, AEs=AEs, j=j: e.tensor_copy(
                                SAVES[:, 4 * j:4 * j + 4].rearrange("p (b r) -> p b r", b=2), AEs[:, :, 16:18]),
                                reads=[("ae", p)], writes=[("saves", j)])

                        def tail(p=p, sil_out=sil_out, sil_in=sil_in, j=j, lc0=lc0, n=n, lin=lin, half=half):
                            S.op("act", lambda e: e.activation(sil_out, sil_in, AF.Silu),
                                 reads=[("conv", p), ("ae", p), ("aeh", p)], writes=[("ae", p), ("aeh", p)])
                            S.op("pool", lambda e: e.tensor_tensor(ACTB(j, lc0, n), sil_out, lin[:, 0:n], ALU.mult),
                                 reads=[("ae", p), ("aeh", p), ("lin", p)], writes=[("actb", j, half)])
                        if pend_tail is not None:
                            pend_tail()
                        pend_tail = tail
                if pend_tail is not None:
                    pend_tail()
                    pend_tail = None
                if half == 1:
                    dma("sp", oconv[l], SAVE.rearrange("p (j r) -> p j r", r=2), reads=[("save", j) for j in range(NJ)], key="oconv")
                    dma("sp", osconv[l], SAVES.rearrange("p (j b r) -> p j b r", b=2, r=2), reads=[("saves", j) for j in range(NJ)], key="osconv")
                atoks = [("actb", j, half) for j in range(NJ)]
                nxt = list(range(8, 17)) if half == 0 else []
                for ti_, i in enumerate(tiles):
                    rows = tile_rows(i)
                    c0, n = tcols(i)
                    lc0 = c0 - base
                    defer_b = []
                    for i2 in (nxt[ti_:ti_ + 1] if ti_ < 7 else nxt[7:8]):
                        rows2 = tile_rows(i2)
                        c02, n2 = tcols(i2)
                        defer_b.append(prenorm_tile(X[0:rows2, i2, :], rows2, ("x", i2), prm(l, "gpf", 8), H23(c02 - 1024, n2),
                                                    ("h2", i2 - 8), 0, hb=HBF[0:rows2, :], tpool="t", hb_toks=["hbf"], defer=True))
                        if "nodefer" in DBG:
                            defer_b.pop()()
                    for hf in range(2):
                        kd = psnext("d")
                        mm_group(None, [(PS[kd][0:rows, :], ACTB(j, lc0, n), WDN(j, 512 * hf, 512), j == 0, j == NJ - 1) for j in range(NJ)],
                                 reads=atoks + [("wdn", 0), ("wdn", 1)], writes=[("ps", kd)])
                        if hf == 0:
                            S.op("act", lambda e, kd=kd, rows=rows: e.copy(OSTG[0:rows, 0:512], PS[kd][0:rows, :]),
                                 reads=[("ps", kd)], writes=["ostg0"])
                        else:
                            S.op("dve", lambda e, kd=kd, rows=rows: e.tensor_copy(OSTG[0:rows, 512:1024], PS[kd][0:rows, :]),
                                 reads=[("ps", kd)], writes=["ostg1"])
                    for fb in defer_b:
                        fb()
                    post_norm_residual(OSTG[0:rows, :], rows, i, HBF[0:rows, :], ["hbf"], GPF, ["gpf"], ["ostg0", "ostg1"])
                if half == 0:
                    prenorm_tile(X[0:32, 16, :], 32, ("x", 16), prm(l, "gpf", 8), H23(1024, 32), ("h2", 8), 0,
                                 hb=HBF[0:32, :], tpool="t", hb_toks=["hbf"])
        RFOX = lambda i, rows=128: SM[0:rows, 100 + i:101 + i]
        RSGU = lambda i, rows=128: SM[0:rows, 120 + i:121 + i]
        RMEM = lambda i, rows=128: SM[0:rows, 140 + i:141 + i]
        SSFOX = lambda i, rows=128: SM[0:rows, 160 + i:161 + i]
        SSMEM = lambda i, rows=128: SM[0:rows, 180 + i:181 + i]
        ALLHT = [("ht", i) for i in range(17)]

        class NSQ:
            def __init__(self):
                self.p1 = None
                self.p2 = None
                self.p3 = None

            def step(self, new_p1):
                if self.p3 is not None:
                    self.p3()
                    self.p3 = None
                if self.p2 is not None:
                    self.p3 = self.p2()
                    self.p2 = None
                if self.p1 is not None:
                    self.p2 = self.p1()
                self.p1 = new_p1

            def flush(self):
                self.step(None)
                self.step(None)
                self.step(None)
        nsq = NSQ()

        def wload(slot, src2d, ncol, extra_writes=()):
            dma("pool", WS3(slot, ncol), src2d.rearrange("(c p) n -> p c n", p=128),
                writes=[("ws", slot)] + list(extra_writes), key=("ws", slot))

        def KAS(b, c0, n, p0=0, p1=67):
            return bf(WS_O[0] + b * 3200 + c0 * 2, n, p0, p1)

        def VAS(b, j):
            return bf(WS_O[0] + b * 3200 + 2080 + j * 130, 65)

        def VAS3(b):
            return AR[:, (WS_O[0] + b * 3200 + 2080) // 2:(WS_O[0] + b * 3200 + 2080) // 2 + 520].rearrange("p (j k) -> p j k", k=65)

        SC_TOKS = [("sc", b, x) for b in range(2) for x in ("k", "v", "r", "one")]

        def norm_store_fm(obank, I, grp, chunk, p0, ssq4, l):
            rlrow = f32(SCR_O + 6 * 1024, 512, 64, 65)
            hfT = f32(SCR_O + 6 * 1024, 512, 0, 64)
            S.op("act", lambda e: e.activation(rlrow, PS[obank][64:65, 0:512], AF.Ln), reads=[("ps", obank)], writes=[("scr", 6, "r")])
            S.op("act", lambda e: e.activation(rlrow, rlrow, AF.Exp, scale=-1.0), reads=[("scr", 6, "r")], writes=[("scr", 6, "r")])
            kb = psnext("g")
            S.op("pe", lambda e: e.matmul(PS[kb][0:64, 0:512], ONEROW[64:65, 0:64], rlrow, start=True, stop=True, skip_group_check=True),
                 reads=[("scr", 6, "r"), "onerow"], writes=[("ps", kb)])
            S.op("dve", lambda e: e.tensor_copy(hfT, PS[kb][0:64, 0:512]), reads=[("ps", kb)], writes=scr(6, 7))
            S.op("dve", lambda e: e.tensor_tensor(hfT, PS[obank][0:64, 0:512], hfT, ALU.mult), reads=[("ps", obank)] + scr(6, 7), writes=scr(6, 7))
            return lambda: norm_store_fm2(I, grp, chunk, p0, ssq4, l)

        def norm_store_fm2(I, grp, chunk, p0, ssq4, l):
            hfT = f32(SCR_O + 6 * 1024, 512, 0, 64)
            sqb = bf(SCR_O + 4 * 1024, 512, 0, 64)
            S.op("act", lambda e: e.activation(MX(chunk, 512 * I, 512, p0, p0 + 64), hfT, AF.Copy,
                                               scale=prm(l, "ggo", 1, p0, p0 + 64, chunk)),
                 reads=scr(6, 7) + ["prm"], writes=[("mx", chunk, 4 * I + s_, p0) for s_ in range(4)])
            S.op("act", lambda e: e.activation(sqb, hfT, AF.Square), reads=scr(6, 7), writes=scr(4))
            return lambda: norm_store_fm3(I, grp, ssq4)

        def norm_store_fm3(I, grp, ssq4):
            sqb = bf(SCR_O + 4 * 1024, 512, 0, 64)
            kt = psnext("t")
            ones_col = bf(VA_O + 64 * 2, 1, 0, 64)

            def ssmm(e):
                ins = None
                for s_ in range(4):
                    ins = e.matmul(PS[kt][:, s_:s_ + 1], sqb[:, 128 * s_:128 * s_ + 128], ones_col, start=True, stop=True,
                                   skip_group_check=True)
                return ins
            S.op("pe", ssmm, reads=scr(4) + ["va_ones"], writes=[("ps", kt)])
            toks4 = [("ssq", grp, 4 * I + s_) for s_ in range(4)]
            S.op("dve", lambda e: e.tensor_tensor(ssq4, ssq4, PS[kt][:, 0:4], ALU.add), reads=[("ps", kt)] + toks4, writes=toks4)

        def norm_and_store_block(obank, I, grp, chunk, p0, ssq4, l):
            O4 = PS[obank][:, 0:260].rearrange("p (s k) -> p s k", k=65)
            rl4 = SM[:, SM_T + 10:SM_T + 14]
            sq4 = SM[:, SM_T + 14:SM_T + 18]
            S.op("dve", lambda e: e.reciprocal(rl4.unsqueeze(2), O4[:, :, 64:65]), reads=[("ps", obank)], writes=["rl4"])
            hf4 = SCRf(6, 256)
            hb4 = SCRb(7, 256)
            S.op("dve", lambda e: e.tensor_tensor(hf4.rearrange("p (s k) -> p s k", k=64), O4[:, :, 0:64],
                                                  rl4.unsqueeze(2).to_broadcast([128, 4, 64]), ALU.mult),
                 reads=[("ps", obank), "rl4"], writes=scr(6))

            def sqs(e):
                ins = None
                for s_ in range(4):
                    ins = e.activation(hb4[:, 64 * s_:64 * s_ + 64], hf4[:, 64 * s_:64 * s_ + 64], AF.Square,
                                       accum_out=sq4[:, s_:s_ + 1])
                return ins
            S.op("act", sqs, reads=scr(6), writes=scr(7) + ["sq4"])
            toks4 = [("ssq", grp, 4 * I + s_) for s_ in range(4)]
            S.op("dve", lambda e: e.tensor_tensor(ssq4, ssq4, sq4, ALU.add), reads=["sq4"] + toks4, writes=toks4)
            S.op("act", lambda e: e.copy(hb4, hf4), reads=scr(6, 7), writes=scr(7))
            kt = psnext("t")

            def tr(e):
                ins = None
                for s_ in range(4):
                    ins = e.transpose(PSB[kt][p0:p0 + 64, 128 * s_:128 * s_ + 128], hb4[:, 64 * s_:64 * s_ + 64], IDENT[:, :])
                return ins
            S.op("pe", tr, reads=scr(7) + ["cst"], writes=[("ps", kt)])
            S.op("dve", lambda e: e.tensor_scalar(MX(chunk, 512 * I, 512, p0, p0 + 64), PSB[kt][p0:p0 + 64, 0:512],
                                                  prm(l, "ggo", 1, p0, p0 + 64, chunk), None, ALU.mult),
                 reads=[("ps", kt), "prm"], writes=[("mx", chunk, 4 * I + s_, p0) for s_ in range(4)])

        def norm_and_store(obank, ocol, rows, tile_i, grp, chunk, p0, ssq_acc, l):
            rl = SM[0:rows, SM_T + 4:SM_T + 5]
            S.op("dve", lambda e: e.reciprocal(rl, PS[obank][0:rows, ocol + 64:ocol + 65]),
                 reads=[("ps", obank)], writes=["rl"])
            hf = SCRf(5, 64, 0, rows)
            hb = SCRb(5, 64, 0, rows, off=256)
            S.op("dve", lambda e: e.tensor_scalar(hf, PS[obank][0:rows, ocol:ocol + 64], rl, None, ALU.mult),
                 reads=[("ps", obank), "rl"], writes=scr(5))
            sq = SM[0:rows, SM_T + 5:SM_T + 6]
            S.op("act", lambda e: e.activation(hb, hf, AF.Square, accum_out=sq), reads=scr(5), writes=scr(5) + ["sq"])
            S.op("dve", lambda e: e.tensor_tensor(ssq_acc, ssq_acc, sq, ALU.add),
                 reads=["sq", ("ssq", grp, tile_i)], writes=[("ssq", grp, tile_i)])
            S.op("act", lambda e: e.copy(hb, hf), reads=scr(5), writes=scr(5))
            kt = psnext("t")
            c0, n = tcols(tile_i)
            S.op("pe", lambda e: e.transpose(PSB[kt][p0:p0 + 64, 0:rows], hb, IDENT[0:rows, 0:rows]),
                 reads=scr(5) + ["cst"], writes=[("ps", kt)])
            S.op("dve", lambda e: e.tensor_scalar(MX(chunk, c0, n, p0, p0 + 64), PSB[kt][p0:p0 + 64, 0:rows],
                                                  prm(l, "ggo", 1, p0, p0 + 64, chunk), None, ALU.mult),
                 reads=[("ps", kt), "prm"], writes=[("mx", chunk, tile_i, p0)])

        try:
            for l in range(L):
                psr.clear()
                psr.update({"g": [0, 1], "s": [2, 3, 4], "o": [5, 6], "t": [7], "w": [0, 1, 2, 3, 4, 5]})
                for kk in psr:
                    psi[kk] = 0
                S.op("dve", lambda e: e.memset(
                    AR[:, VA_O // 2: VA_O // 2 + 17 * 8 * 65].rearrange("p (a k) -> p a k", k=65)[:, :, 64:65], 1.0),
                    writes=["va_ones"])
                S.op("dve", lambda e: e.memset(MVAt[:, :].rearrange("p (a k) -> p a k", k=65)[:, :, 64:65], 1.0),
                     writes=["mva_ones"])
                S.op("dve", lambda e: e.memset(MKTt[64:128, :], 0.0), writes=["mkt_zero"])
                wload(0, w_in[l][:, 1024:1536], 512, extra_writes=SC_TOKS if l > 0 else ())
                wload(1, w_in[l][:, 512:1024], 512)
                wload(2, w_in[l][:, 1536:2056], 520)

                SGC = MX_O + 12288
                WSTl = bf(SGC, 512)
                GSl = f32(SGC + 1024, 256)
                dma("pool", WSTl.rearrange("p (g i) -> p g i", g=4), wsT[l].rearrange("g j i -> j g i"),
                    writes=[("sguc", 0)], key="wst")
                S.op("dve", lambda e: e.memset(bf(SGC, 512, 64, 128).rearrange("p (g i) -> p g i", g=4)[:, :, 0:64], 0.0),
                     reads=[("sguc", 0)], writes=[("sguc", 0)])
                dma("sp", GSl, g_sgu[l:l + 1, :].broadcast_to([128, 256]), writes=[("sguc", 1)], key="gs")

                def sgu_ops(i):
                    st_ = 0 if "setA" in DBG else i % 2
                    base = MX_O if st_ == 0 else MX_O + 6144
                    tk = "sguA" if st_ == 0 else "sguB"
                    Sb = lambda g, n, p0=0, p1=128: bf(base + g * 1024, n, p0, p1)
                    Sf = lambda g, n, p0=0, p1=128: f32(base + g * 1024, n, p0, p1)
                    sc = lambda *gs: [(tk, g) for g in gs]
                    rows = tile_rows(i)
                    c0, n = tcols(i)
                    s1, s2 = [], []
                    k = psnext("s")
                    z = Sf(0, 512, 0, rows)
                    t1 = Sf(2, 512, 0, rows)
                    ssv = SM[0:rows, SM_T + 20 + 2 * st_:SM_T + 21 + 2 * st_]
                    sss = SM[0:rows, SM_T + 21 + 2 * st_:SM_T + 22 + 2 * st_]
                    ssvt, ssst = ("ssv", st_), ("sss", st_)
                    vvf = Sf(3, 256, 0, rows)
                    vvb = Sb(2, 256, 0, rows)
                    sg = Sf(4, 256, 0, rows)
                    sgb = Sb(5, 256, 0, rows)
                    s1.append(lambda: mm_group(None, [(PS[k][0:rows, :], HT(c, c0, n), WS(2, c, 8, 512), c == 0, c == 7) for c in range(8)],
                                               reads=[("ht", i), ("ws", 2)], writes=[("ps", k)]))
                    s1.append(lambda: S.op("act", lambda e: e.copy(z, PS[k][0:rows, :]), reads=[("ps", k)], writes=sc(0, 1)))
                    s1.append(lambda: S.op("dve", lambda e: e.tensor_tensor(t1, z, z, ALU.mult), reads=sc(0, 1), writes=sc(2, 3)))
                    s1.append(lambda: S.op("dve", lambda e: e.tensor_scalar(t1, t1, 0.044715, 1.0, ALU.mult, ALU.add), reads=sc(2, 3), writes=sc(2, 3)))
                    s1.append(lambda: S.op("dve", lambda e: e.tensor_tensor(t1, t1, z, ALU.mult), reads=sc(0, 1, 2, 3), writes=sc(2, 3)))
                    s1.append(lambda: S.op("act", lambda e: e.activation(t1, t1, AF.Exp, scale=-1.5957691216057308), reads=sc(2, 3), writes=sc(2, 3)))
                    s1.append(lambda: S.op("act", lambda e: e.activation(t1, t1, AF.Ln, bias=1.0, scale=1.0), reads=sc(2, 3), writes=sc(2, 3)))
                    s1.append(lambda: S.op("act", lambda e: e.activation(t1, t1, AF.Exp, scale=-1.0), reads=sc(2, 3), writes=sc(2, 3)))
                    s1.append(lambda: S.op("dve", lambda e: e.tensor_tensor(z, z, t1, ALU.mult), reads=sc(0, 1, 2, 3), writes=sc(0, 1)))
                    s1.append(lambda: S.op("act", lambda e: e.activation(t1[:, 0:256], z[:, 256:512], AF.Square, accum_out=ssv),
                                           reads=sc(0, 1), writes=sc(2) + [ssvt]))
                    s1.append(lambda: S.op("act", lambda e: e.activation(ssv, ssv, AF.Ln, bias=EPS, scale=1.0 / 256), reads=[ssvt], writes=[ssvt]))
                    s1.append(lambda: S.op("act", lambda e: e.activation(ssv, ssv, AF.Exp, scale=-0.5), reads=[ssvt], writes=[ssvt]))
                    s1.append(lambda: S.op("dve", lambda e: e.scalar_tensor_tensor(vvf, z[:, 256:512], ssv, GSl[0:rows, :], ALU.mult, ALU.mult),
                                           reads=sc(0, 1) + [("sguc", 1)] + [ssvt], writes=sc(3)))
                    s1.append(lambda: S.op("act", lambda e: e.copy(vvb, vvf), reads=sc(3), writes=sc(2)))
                    if i == 16:
                        s1.append(lambda: dma("sp", ogv[l], vvf, reads=sc(3), key="ogv"))
                    k2 = psnext("s")
                    if i < 16:
                        items = [(PS[k2][0:128, 64 * g:64 * g + 64], WSTl[:, 128 * g:128 * g + 128], vvb[:, 64 * g:64 * g + 64], True, True) for g in range(4)]
                        bsap = prm(l, "bs", 4)
                        wtok = [("sguc", 0)]
                    else:
                        items = [(PS[k2][0:32, 64 * g:64 * g + 64], WSTS(l, g), vvb[:, 64 * g:64 * g + 64], True, True) for g in range(4)]
                        bsap = prm(l, "bss", 4, 0, 32)
                        wtok = ["wsts"] + [("wsts", l, b) for b in range(2)]
                    s1.append(lambda: mm_group(None, items, reads=sc(2) + wtok, writes=[("ps", k2)]))
                    s2.append(lambda: S.op("dve", lambda e: e.tensor_tensor(
                        sg.rearrange("p (g d) -> p g d", g=4), PS[k2][0:rows, 0:256].rearrange("p (g d) -> p g d", g=4),
                        bsap.unsqueeze(2).to_broadcast([rows, 4, 64]), ALU.add),
                        reads=[("ps", k2), "prm"], writes=sc(4)))
                    s2.append(lambda: S.op("dve", lambda e: e.tensor_tensor(sg, sg, z[:, 0:256], ALU.mult), reads=sc(0, 1, 4), writes=sc(4)))
                    s2.append(lambda: S.op("act", lambda e: e.activation(sgb, sg, AF.Square, accum_out=sss),
                                           reads=sc(4), writes=sc(5) + [ssst]))
                    s2.append(lambda: S.op("act", lambda e: e.activation(sss, sss, AF.Ln, bias=EPS, scale=1.0 / 256), reads=[ssst], writes=[ssst]))
                    s2.append(lambda: S.op("act", lambda e: e.activation(RSGU(i, rows), sss, AF.Exp, scale=-0.5),
                                           reads=[ssst], writes=[("rsgu", i)]))
                    s2.append(lambda: S.op("act", lambda e: e.copy(sgb, sg), reads=sc(4, 5), writes=sc(5)))

                    def trs():
                        kt = psnext("o")

                        def tr(e):
                            ins = None
                            for cc in range(2):
                                ins = e.transpose(PSB[kt][:, cc * 128:cc * 128 + rows], sgb[:, cc * 128:(cc + 1) * 128], IDENT[0:rows, 0:rows])
                            return ins
                        S.op("pe", tr, reads=sc(5) + ["cst"], writes=[("ps", kt)])
                        for cc in range(2):
                            S.op("dve", lambda e, cc=cc, l=l: e.tensor_scalar(
                                MX(4 + cc, c0, n), PSB[kt][:, cc * 128:cc * 128 + rows], prm(l, "ggo", 1, 0, 128, 4 + cc), None, ALU.mult),
                                reads=[("ps", kt), "prm"], writes=[("mx", 4 + cc, i)])
                    s2.append(trs)
                    return s1, s2

                def interleave(a, b):
                    for q in range(max(len(a), len(b))):
                        if q < len(a):
                            a[q]()
                        if q < len(b):
                            b[q]()

                def vk_tile(i):
                    rows = tile_rows(i)
                    c0, n = tcols(i)
                    for which, slot, oprompt, osample, sg in (("v", 0, ofv, osv, 4), ("k", 1, ofk, osk, 6)):
                        k = psnext("g")
                        mm_group(None, [(PS[k][0:rows, :], HT(c, c0, n), WS(slot, c, 0, 512), c == 0, c == 7)
                                        for c in range(8)],
                                 reads=[("ht", i), ("ws", slot)], writes=[("ps", k)])
                        stg = SCRf(sg, 512, 0, rows)
                        S.op("act", lambda e, stg=stg, k=k, rows=rows: e.copy(stg, PS[k][0:rows, :]),
                             reads=[("ps", k)], writes=scr(sg, sg + 1))
                        if which == "v":
                            va3 = AR[0:rows, VA_O // 2 + i * 520: VA_O // 2 + (i + 1) * 520].rearrange(
                                "p (h k) -> p h k", k=65)[:, :, 0:64]
                            S.op("dve", lambda e, va3=va3, k=k, rows=rows: e.tensor_copy(
                                va3, PS[k][0:rows, :].rearrange("p (h k) -> p h k", k=64)),
                                reads=[("ps", k), "va_ones"], writes=[("va", i)])
                        dst = oprompt[l, 128 * i:128 * i + 128, :] if i < 16 else osample[l, :, :]
                        dma("sp", dst, stg, reads=scr(sg, sg + 1), key=("stg", sg))

                prev2 = []

                def pn_a(i):
                    rows = tile_rows(i)
                    c0, n = tcols(i)
                    return prenorm_tile(X[0:rows, i, :], rows, ("x", i), prm(l, "gpm", 8), HT3(c0, n), ("ht", i), i % 2, defer=True)
                pn_b = {0: pn_a(0)}
                for i in range(17):
                    if i + 1 < 17:
                        pn_b[i + 1] = pn_a(i + 1)
                    pn_b.pop(i)()
                    if i >= 1:
                        vk_tile(i - 1)
                        s1_, s2_ = sgu_ops(i - 1)
                        interleave(s1_, prev2)
                        prev2 = s2_
                vk_tile(16)
                s1_, s2_ = sgu_ops(16)
                interleave(s1_, prev2)
                interleave([], s2_)
                S.op("dve", lambda e: e.memset(bf(MX_O + 15 * 1024, 2), 0.0),
                     writes=[("sguA", g) for g in range(6)] + [("sguB", g) for g in range(6)] + [("sguc", 0), ("sguc", 1)]
                     + [("mx", c, i, p0) for c in (0, 1, 2, 3) for i in range(17) for p0 in (0, 64)])

                _chk("vk%d" % l)
                NB = SM[0:8, 210:211]
                S.op("dve", lambda e, l=l: e.tensor_scalar(NB, prm(l, "bf", 1, 0, 8), -1.0, None, ALU.mult),
                     reads=["prm"], writes=["cw", "nb"])
                for bi, (c0, n) in enumerate(BLKS):
                    k = psnext("g")
                    mm_group(None, [(PS[k][0:8, 0:n], WS(2, c, 0, 8), HT(c, c0, n), c == 0, c == 7) for c in range(8)],
                             reads=ALLHT + [("ws", 2)], writes=["cw", ("ps", k)])
                    tmp = SCRf(0, 512, 96, 104)
                    S.op("act", lambda e, k=k, n=n, tmp=tmp: e.activation(tmp[:, 0:n], PS[k][0:8, 0:n], AF.Exp, bias=NB, scale=-1.0),
                         reads=[("ps", k), "nb"], writes=scr(0, 1))
                    S.op("act", lambda e, n=n, tmp=tmp: e.activation(tmp[:, 0:n], tmp[:, 0:n], AF.Ln, bias=1.0, scale=1.0),
                         reads=scr(0, 1), writes=scr(0, 1))
                    S.op("dve", lambda e, c0=c0, n=n, tmp=tmp: e.tensor_scalar(CT(c0, n), tmp[:, 0:n], -1.0, None, ALU.mult),
                         reads=scr(0, 1), writes=["cw", ("ct", bi)])
                dma("sp", olf[l], CT(0, 2048), reads=[("ct", b) for b in range(4)], writes=["cw"], key="olf")
                dma("sp", oslf[l], CT(2048, 32), reads=[("ct", 4)], writes=["cw"], key="olf2")
                SLF = SM[96:104, 220:252]
                S.op("dve", lambda e: e.tensor_copy(SLF, CT(2048, 32)), reads=[("ct", 4)], writes=["cw", "slf"])
                S.op("dve", lambda e: e.tensor_tensor_scan(CT(0, 2048), ONE8.to_broadcast([8, 2048]), CT(0, 2048), 0.0, ALU.mult, ALU.add),
                     reads=[("ct", b) for b in range(4)] + ["one"], writes=["cw", "ctp"])
                S.op("dve", lambda e: e.tensor_copy(HI(0, 2048), CT(0, 2048)), reads=["ctp"], writes=["cw", "hi"])
                S.op("dve", lambda e: e.tensor_tensor(CT(0, 2048), CT(0, 2048), HI(0, 2048), ALU.subtract),
                     reads=["ctp", "hi"], writes=["cw", "ctp"])
                S.op("dve", lambda e: e.tensor_copy(LO(0, 2048), CT(0, 2048)), reads=["ctp"], writes=["cw", "lo"])
                dma("sp", cscr_p[:, 0, :], HI(0, 2048), reads=["hi"], writes=["cw", "cscr_p"], key="cscr0")
                dma("sp", cscr_p[:, 1, :], LO(0, 2048), reads=["lo"], writes=["cw", "cscr_p2"], key="cscr1")
                CS = lambda b, c0, n: CT(b * 1040 + c0, n)
                for b in range(2):
                    dma("sp", CS(b, 0, 1024), clfT[l, b], reads=["ctp", "lo", ("ct", 4), "slf"], writes=["cw", ("cs", b)], key=("csl", b))
                    S.op("dve", lambda e, b=b: e.tensor_copy(CS(b, 1024, 16), SM[96:104, 220 + 16 * b:236 + 16 * b]),
                         reads=["slf", "ctp", "lo", ("ct", 4)], writes=["cw", ("cs2", b)])
                    S.op("dve", lambda e, b=b: e.tensor_tensor_scan(CS(b, 0, 1040), ONE8.to_broadcast([8, 1040]), CS(b, 0, 1040), 0.0, ALU.mult, ALU.add),
                         reads=[("cs", b), ("cs2", b), "one"], writes=["cw", ("csc", b)])
                S.op("dve", lambda e: e.tensor_copy(HI(0, 2080), CT(0, 2080)),
                     reads=[("csc", 0), ("csc", 1), "cscr_p", "cscr_p2"], writes=["cw", "hi"])
                S.op("dve", lambda e: e.tensor_tensor(CT(0, 2080), CT(0, 2080), HI(0, 2080), ALU.subtract),
                     reads=["hi"], writes=["cw", ("csc", 0), ("csc", 1)])
                S.op("dve", lambda e: e.tensor_copy(LO(0, 2080), CT(0, 2080)), reads=[("csc", 0), ("csc", 1), "cscr_p2"], writes=["cw", "lo"])
                for b in range(2):
                    dma("sp", cscr_s[:, 0, b, :], HI(b * 1040, 1040), reads=["hi"], writes=["cw", ("cscr_s", b, 0)], key=("cscrs", b, 0))
                    dma("sp", cscr_s[:, 1, b, :], LO(b * 1040, 1040), reads=["lo"], writes=["cw", ("cscr_s", b, 1)], key=("cscrs", b, 1))

                _chk("fg%d" % l)
                _chk("sgu%d" % l)
                S.op("dve", lambda e: e.memset(bf(QK_O, 4 * NT, 64, 128), 0.0),
                     writes=["cw", "hi", "lo"] + [("qkc", s_) for s_ in range(4)])
                for s_ in (0, 2):
                    S.op("dve", lambda e, s_=s_: e.memset(QK(s_, 0, NT, 64, 67), -1.0), writes=[("qkc", s_)])
                for s_ in (1, 3):
                    S.op("dve", lambda e, s_=s_: e.memset(QK(s_, 0, NT, 64, 65), 1.0), writes=[("qkc", s_)])
                wload(0, w_mkv[l], 512)
                MEMT3 = AR[:, (SCR_O + 4096) // 2:(SCR_O + 4096) // 2 + 2048].rearrange("p (c t) -> p c t", c=8)
                MEMT = lambda c, c0, n: bf(SCR_O + 4096 + (c * 256 + c0) * 2, n)
                memx = f32(WS_O[2], 1024)
                for mt in range(2):
                    dma("sp", memx, mem[128 * mt:128 * mt + 128, :], writes=[("ws", 2)], key=("ws", 2))
                    prenorm_tile(memx, 128, ("ws", 2), prm(l, "gmem", 8), MEMT3[:, :, 128 * mt:128 * mt + 128],
                                 ("memt", mt), 0, extra_writes=scr(4, 5, 6, 7))
                memt_toks = [("memt", 0), ("memt", 1)] + scr(4, 5, 6, 7)
                stgk = f32(WS_O[2], 512)
                for mt in range(2):
                    k = psnext("g")
                    mm_group(None, [(PS[k][:, :], MEMT(c, 128 * mt, 128), WS(0, c, 0, 512), c == 0, c == 7) for c in range(8)],
                             reads=memt_toks + [("ws", 0)], writes=[("ps", k)])
                    S.op("act", lambda e, k=k: e.copy(stgk, PS[k][:, :]), reads=[("ps", k)], writes=[("ws", 2)])
                    S.op("dve", lambda e, k=k, mt=mt: e.tensor_copy(
                        MVAt[:, mt * 260:(mt + 1) * 260].rearrange("p (h k) -> p h k", k=65)[:, :, 0:64],
                        PS[k][:, 256:512].rearrange("p (h k) -> p h k", k=64)),
                        reads=[("ps", k), "mva_ones"], writes=[("mva", mt)])
                    dma("sp", omk[l, 128 * mt:128 * mt + 128, :], stgk[:, 0:256], reads=[("ws", 2)], key="omk")
                    dma("sp", omv[l, 128 * mt:128 * mt + 128, :], stgk[:, 256:512], reads=[("ws", 2)], key="omv")
                for pr in range(2):
                    k = psnext("g")
                    mm_group(None, [(PS[k][:, 0:256], WS(0, c, 128 * pr, 128), MEMT(c, 0, 256), c == 0, c == 7) for c in range(8)],
                             reads=memt_toks + [("ws", 0)], writes=[("ps", k)])
                    for hh in range(2):
                        h = 2 * pr + hh
                        S.op("act", lambda e, k=k, hh=hh, h=h: e.copy(MKTt[0:64, h * 256:(h + 1) * 256], PS[k][64 * hh:64 * hh + 64, 0:256]),
                             reads=[("ps", k)], writes=[("mkt", h)])
                wload(2, w_in[l][:, 2056:2312], 256)
                for b in range(2):
                    S.op("dve", lambda e, b=b: e.memset(KAS(b, 0, 1040, 64, 65), 1.0), reads=[("ws", 0)], writes=[("ws", 0), ("sc", b, "one")])
                    S.op("dve", lambda e, b=b: e.memset(VAS3(b)[:, :, 64:65], 1.0), reads=[("ws", 0)], writes=[("ws", 0), ("sc", b, "one")])

                S.op("dve", lambda e: e.memset(SM[:, 160:200], 0.0),
                     writes=[("ssq", g, i) for g in (0, 2) for i in range(17)])

                _chk("memkv%d" % l)
                for pr in range(2):
                    for bi, (c0, n) in enumerate(BLKS):
                        k = psnext("g")
                        mm_group(None, [(PS[k][:, 0:n], WS(2, c, 128 * pr, 128), HT(c, c0, n), c == 0, c == 7) for c in range(8)],
                                 reads=ALLHT + [("ws", 2)], writes=[("ps", k)])
                        for hh in range(2):
                            S.op("act", lambda e, k=k, hh=hh, c0=c0, n=n: e.activation(
                                QK(2 * hh, c0, n, 0, 64), PS[k][64 * hh:64 * hh + 64, 0:n], AF.Copy, scale=0.125),
                                reads=[("ps", k)], writes=[("qa", hh, bi)])
                    for hh in range(2):
                        h = 2 * pr + hh
                        for I in range(4):
                            ob = psnext("o")
                            kts = [dict(ka=MKTt[0:128, h * 256 + 128 * j:h * 256 + 128 * j + 128], nk=128,
                                        va=bf(MVA_O + ((j * 4 + h) * 65) * 2, 128), lo=0, hi=512, pv_lo=0,
                                        mask=None, pt="rot", reads=[("mkt", h), ("qa", hh, I), "mkt_zero", ("qkc", 0), ("qkc", 2)],
                                        vreads=[("mva", j), "mva_ones"])
                                   for j in range(2)]
                            osubs = [(PS[ob][:, 65 * s:65 * s + 65], 128 * s, 128 * s + 128, 0, 1) for s in range(4)]
                            attend(lambda c0, n, hh=hh, I=I: QK(2 * hh, 512 * I + c0, n, 0, 128), 128, kts, "fm", ob, "mem")
                            nsq.step(lambda ob=ob, I=I, pr=pr, hh=hh: norm_store_fm(ob, I, 2, 6 + pr, 64 * hh, SM[:, 180 + 4 * I:184 + 4 * I], l))
                        ob = psnext("g")
                        kts = []
                        for b in range(2):
                            dma("pool", KAS(b, 0, 256, 0, 64), cmkT[l, b, h], reads=[("ws", 0)], writes=[("sc", b, "k")], key=("sck", b))
                            dma("pool", VAS3(b)[:, 0:2, 0:64],
                                cmv[l, b].rearrange("(j p) (h d) -> p j h d", p=128, h=4)[:, :, h, :],
                                reads=[("ws", 0)], writes=[("sc", b, "v")], key=("scv", b), slow=True)
                            for j in range(2):
                                kts.append(dict(ka=KAS(b, 128 * j, 128, 0, 64), nk=128, va=VAS(b, j),
                                                lo=16 * b, hi=16 * b + 16, pv_lo=0, mask=None, pt=3 * b + (j % 3),
                                                reads=[("sc", b, "k"), ("qa", hh, 4)], vreads=[("sc", b, "v"), ("sc", b, "one")]))
                        osubs = [(PS[ob][0:32, 0:65], 0, 32, 0, len(kts) - 1)]
                        attend(lambda c0, n, hh=hh: QK(2 * hh, 2048 + c0, n, 0, 64), 64, kts, osubs, ob, "mems")
                        flush_pv()
                        norm_and_store(ob, 0, 32, 16, 2, 6 + pr, 64 * hh, SSMEM(16, 32), l)

                flush_pv()
                nsq.flush()
                _chk("memattn%d" % l)
                wload(2, w_in[l][:, 0:512], 512)
                for pr in range(4):
                    for bi, (c0, n) in enumerate(BLKS):
                        for qk, slot in ((0, 2), (1, 1)):
                            k = psnext("g")
                            mm_group(None, [(PS[k][:, 0:n], WS(slot, c, 128 * pr, 128), HT(c, c0, n), c == 0, c == 7) for c in range(8)],
                                     reads=ALLHT + [("ws", slot)], writes=[("ps", k)])
                            for hh in range(2):
                                if qk == 0:
                                    S.op("act", lambda e, k=k, hh=hh, c0=c0, n=n: e.activation(
                                        QK(2 * hh, c0, n, 0, 64), PS[k][64 * hh:64 * hh + 64, 0:n], AF.Copy, scale=0.125),
                                        reads=[("ps", k)], writes=[("qa", hh, bi)])
                                else:
                                    S.op("dve", lambda e, k=k, hh=hh, c0=c0, n=n: e.tensor_copy(
                                        QK(2 * hh + 1, c0, n, 0, 64), PS[k][64 * hh:64 * hh + 64, 0:n]),
                                        reads=[("ps", k)], writes=[("ka", hh, bi)])
                    for hh in range(2):
                        h = 2 * pr + hh
                        dma("sp", QK(2 * hh, 0, 2048, 64, 65), cscr_p[h:h + 1, 0, :], reads=["cscr_p", ("qkc", 2 * hh)],
                            writes=[("qar", hh, 0)], key=("qar", hh))
                        dma("sp", QK(2 * hh + 1, 0, 2048, 65, 66), cscr_p[h:h + 1, 0, :], reads=["cscr_p", ("qkc", 2 * hh + 1)],
                            writes=[("kar", hh, 0)], key=("kar", hh))
                        dma("sp", QK(2 * hh + 1, 0, 2048, 66, 67), cscr_p[h:h + 1, 1, :], reads=["cscr_p2"],
                            writes=[("kar", hh, 1)], key=("kar", hh))
                        for b in range(2):
                            dma("sp", QK(2 * hh, 2048 + 16 * b, 16, 64, 65), cscr_s[h:h + 1, 0, b, 1024:1040],
                                reads=[("cscr_s", b, 0)], writes=[("qar", hh, 1 + b)], key=("qar", hh))
                            dma("sp", QK(2 * hh + 1, 2048 + 16 * b, 16, 65, 66), cscr_s[h:h + 1, 0, b, 1024:1040],
                                reads=[("cscr_s", b, 0)], writes=[("kar", hh, 2 + b)], key=("kar", hh))
                            dma("sp", QK(2 * hh + 1, 2048 + 16 * b, 16, 66, 67), cscr_s[h:h + 1, 1, b, 1024:1040],
                                reads=[("cscr_s", b, 1)], writes=[("kar", hh, 4 + b)], key=("kar", hh))
                        qar = [("qar", hh, x) for x in range(3)] + [("qkc", 2 * hh)]
                        kar = [("kar", hh, x) for x in range(6)] + [("qkc", 2 * hh + 1)]
                        for I in range(4):
                            ob = psnext("o")
                            kts = []
                            for j in range(4 * I + 4):
                                a = j - 4 * I
                                kts.append(dict(ka=QK(2 * hh + 1, 128 * j, 128, 0, 128), nk=128, va=VA128(j, h),
                                                lo=(128 * a if a >= 0 else 0), hi=512, pv_lo=(128 * a if a >= 0 else 0),
                                                mask=((MASKNEG, 128) if a >= 0 else None), pt="rot",
                                                reads=[("ka", hh, j // 4), ("qa", hh, I)] + qar + kar,
                                                vreads=[("va", j), "va_ones"]))
                            osubs = [(PS[ob][:, 65 * s:65 * s + 65], 128 * s, 128 * s + 128, 0, 4 * I + s) for s in range(4)]
                            attend(lambda c0, n, hh=hh, I=I: QK(2 * hh, 512 * I + c0, n, 0, 128), 128, kts, "fm", ob, "fox")
                            nsq.step(lambda ob=ob, I=I, pr=pr, hh=hh: norm_store_fm(ob, I, 0, pr, 64 * hh, SM[:, 160 + 4 * I:164 + 4 * I], l))
                        ob = psnext("g")
                        kts = []
                        for b in range(2):
                            dma("pool", KAS(b, 0, 1024, 0, 64), ckT[l, b, h], reads=[("ws", 0)], writes=[("sc", b, "k")], key=("sck", b))
                            dma("pool", VAS3(b)[:, :, 0:64],
                                cv[l, b].rearrange("(j p) (h d) -> p j h d", p=128, h=8)[:, :, h, :],
                                reads=[("ws", 0)], writes=[("sc", b, "v")], key=("scv", b), slow=True)
                            dma("sp", KAS(b, 0, 1024, 65, 66), cscr_s[h:h + 1, 0, b, 0:1024], reads=[("cscr_s", b, 0), ("ws", 0)],
                                writes=[("sc", b, "r")], key=("scr", b))
                            dma("sp", KAS(b, 0, 1024, 66, 67), cscr_s[h:h + 1, 1, b, 0:1024], reads=[("cscr_s", b, 1), ("ws", 0)],
                                writes=[("sc", b, "r2")], key=("scr", b))
                            for j in range(8):
                                kts.append(dict(ka=KAS(b, 128 * j, 128), nk=128, va=VAS(b, j),
                                                lo=16 * b, hi=16 * b + 16, pv_lo=0, mask=None, pt=3 * b + (j % 3),
                                                reads=[("sc", b, "k"), ("sc", b, "r"), ("sc", b, "r2"), ("sc", b, "one"), ("qa", hh, 4)] + qar,
                                                vreads=[("sc", b, "v"), ("sc", b, "one")]))
                        kts.append(dict(ka=QK(2 * hh + 1, 2048, 32, 0, 67), nk=32, va=VA(16, h, 32), lo=0, hi=32, pv_lo=0,
                                        mask=(MASKS, 32), pt=6, reads=[("ka", hh, 4), ("qa", hh, 4)] + qar + kar,
                                        vreads=[("va", 16), "va_ones"]))
                        osubs = [(PS[ob][0:32, 0:65], 0, 32, 0, len(kts) - 1)]
                        attend(lambda c0, n, hh=hh: QK(2 * hh, 2048 + c0, n, 0, 67), 67, kts, osubs, ob, "foxs")
                        flush_pv()
                        norm_and_store(ob, 0, 32, 16, 0, pr, 64 * hh, SSFOX(16, 32), l)

                flush_pv()
                nsq.flush()
                _chk("fox%d" % l)
                S.op("act", lambda e: e.activation(SM[:, 100:117], SM[:, 160:177], AF.Ln, bias=EPS, scale=1.0 / 512),
                     reads=[("ssq", 0, i) for i in range(17)], writes=["rfox"])
                S.op("act", lambda e: e.activation(SM[:, 100:117], SM[:, 100:117], AF.Exp, scale=-0.5), reads=["rfox"], writes=["rfox"])
                S.op("act", lambda e: e.activation(SM[:, 140:157], SM[:, 180:197], AF.Ln, bias=EPS, scale=1.0 / 256),
                     reads=[("ssq", 2, i) for i in range(17)], writes=["rmem"])
                S.op("act", lambda e: e.activation(SM[:, 140:157], SM[:, 140:157], AF.Exp, scale=-0.5), reads=["rmem"], writes=["rmem"])

                wload(0, w_out[l][:, 0:512], 512, extra_writes=SC_TOKS)
                wload(1, w_out[l][:, 512:1024], 512)
                GP = SCRf(4, 1024)
                dma("sp", GP, g_post_mix[l:l + 1, :].broadcast_to([128, D]), writes=scr(4, 5, 6, 7), key="gp")
                for i in range(17):
                    rows = tile_rows(i)
                    c0, n = tcols(i)
                    ostg = SCRf(0, 1024, 0, rows)
                    mxr = [("mx", c, i, p0) for c in (0, 1, 2, 3, 6, 7) for p0 in (0, 64)] + [("mx", 4, i), ("mx", 5, i)]
                    last_b = None
                    for hf in range(2):
                        banks = [psnext("w") for _ in range(3)]
                        for gi, (cs, b) in enumerate(zip(((0, 1, 2, 3), (4, 5), (6, 7)), banks)):
                            mm_group(None, [(PS[b][0:rows, :], MX(c, c0, n), WS(hf, c, 0, 512), c == cs[0], c == cs[-1]) for c in cs],
                                     reads=mxr + [("ws", hf)], writes=[("ps", b)])
                        o_h = ostg[:, 512 * hf:512 * hf + 512]
                        S.op("act", lambda e, o_h=o_h, b=banks[0], rows=rows, i=i: e.activation(o_h, PS[b][0:rows, :], AF.Copy, scale=RFOX(i, rows)),
                             reads=[("ps", banks[0]), "rfox"], writes=scr(2 * hf, 2 * hf + 1))
                        S.op("dve", lambda e, o_h=o_h, b=banks[1], rows=rows, i=i: e.scalar_tensor_tensor(o_h, PS[b][0:rows, :], RSGU(i, rows), o_h, ALU.mult, ALU.add),
                             reads=[("ps", banks[1]), ("rsgu", i)] + scr(2 * hf, 2 * hf + 1), writes=scr(2 * hf, 2 * hf + 1))
                        S.op("dve", lambda e, o_h=o_h, b=banks[2], rows=rows, i=i: e.scalar_tensor_tensor(o_h, PS[b][0:rows, :], RMEM(i, rows), o_h, ALU.mult, ALU.add),
                             reads=[("ps", banks[2]), "rmem"] + scr(2 * hf, 2 * hf + 1), writes=scr(2 * hf, 2 * hf + 1))
                        last_b = banks
                    post_norm_residual(ostg, rows, i, bf(WS_O[2], 1024, 0, rows), [("ws", 2)], GP, scr(4, 5, 6, 7), scr(0, 1, 2, 3))

                _chk("wout%d" % l)
                S.barrier()
                ffn_phase(l)
                S.barrier()

        except _Stop:
            pass
        for i in range(16):
            dma("sp", y_p[128 * i:128 * i + 128, :], X[:, i, :], reads=[("x", i)], key=("x", i))
        dma("sp", y_s, X[0:32, 16, :], reads=[("x", 16)], key=("x", 16))
        S.emit(st)
    return nc


_NC_CACHE = {}


def _prep_inputs(inp):
    f = lambda a: np.ascontiguousarray(np.asarray(a, dtype=np.float32))
    x_prompt = f(inp["x_prompt"]); x_sample = f(inp["x_sample"]); mem_prompt = f(inp["mem_prompt"])
    cfk = f(inp["cache_fox_k"]); cfv = f(inp["cache_fox_v"]); clf = f(inp["cache_fox_logf"])
    cmk = f(inp["cache_mem_k"]); cmvv = f(inp["cache_mem_v"]); cconv = f(inp["cache_ffn_conv"])
    ckT_all = np.ascontiguousarray(cfk.transpose(0, 1, 3, 4, 2))
    cv_all = cfv.reshape(L, 16, 1024, 512)
    clfT_all = np.ascontiguousarray(clf.transpose(0, 1, 3, 2))
    cmkT_all = np.ascontiguousarray(cmk.transpose(0, 1, 3, 4, 2))
    cmv_all = cmvv.reshape(L, 16, 256, 256)
    wsT = np.ascontiguousarray(f(inp["w_spatial"]).transpose(0, 1, 3, 2))
    fm8 = lambda g: f(g).reshape(L, 8, 128).transpose(0, 2, 1)
    b_sp = f(inp["b_spatial"])
    w_dw = f(inp["w_dwconv"]); b_dw = f(inp["b_dwconv"]); b_f = f(inp["b_forget"])
    cst = np.zeros((128, 288), dtype=ml_dtypes.bfloat16)
    cst[:, 0:128] = np.eye(128, dtype=np.float32).astype(ml_dtypes.bfloat16)
    kk, qq = np.meshgrid(np.arange(128), np.arange(128), indexing="ij")
    cst[:, 128:256] = np.where(kk <= qq, 0.0, NEG).astype(ml_dtypes.bfloat16)
    k2, q2 = np.meshgrid(np.arange(32), np.arange(32), indexing="ij")
    ok = (k2 // 16 == q2 // 16) & (k2 % 16 <= q2 % 16)
    cst[0:32, 256:288] = np.where(ok, 0.0, NEG).astype(ml_dtypes.bfloat16)
    wu = f(inp["w_up"])
    wu_a = wu[:, :, 0:FF].reshape(L, 8, 128, NJ, 128)
    wu_l = wu[:, :, FF:2 * FF].reshape(L, 8, 128, NJ, 128)
    w_up_r = np.ascontiguousarray(np.concatenate([wu_a, wu_l], axis=4).transpose(0, 3, 2, 1, 4))
    shared = dict(
        w_in=f(inp["w_in"]), w_mkv=f(inp["w_mem_kv"]), w_out=f(inp["w_out"]), w_up=w_up_r,
        w_dn=f(inp["w_down"]), g_post_mix=f(inp["g_post_mix"]), g_post_ffn=f(inp["g_post_ffn"]),
        g_sgu=f(inp["g_sgu"]), wsT=wsT, cst=cst)
    gpm = fm8(inp["g_pre_mix"]); ggo = fm8(inp["g_group_out"]); gmem = fm8(inp["g_mem"]); gpf = fm8(inp["g_pre_ffn"])
    in_maps = []
    for c in range(NCORES):
        prm = np.zeros((128, NPRM), dtype=np.float32)
        for l in range(L):
            o = l * PPL
            prm[:, o + PO["gpm"]:o + PO["gpm"] + 8] = gpm[l]
            prm[:, o + PO["ggo"]:o + PO["ggo"] + 8] = ggo[l]
            prm[:, o + PO["gmem"]:o + PO["gmem"] + 8] = gmem[l]
            prm[:, o + PO["gpf"]:o + PO["gpf"] + 8] = gpf[l]
            prm[:, o + PO["bs"]:o + PO["bs"] + 4] = b_sp[l].T
            prm[0:16, o + PO["bss"]:o + PO["bss"] + 4] = b_sp[l][:, 0:16].T
            prm[16:32, o + PO["bss"]:o + PO["bss"] + 4] = b_sp[l][:, 0:16].T
            prm[:, o + PO["wdw"]:o + PO["wdw"] + 66] = w_dw[l].reshape(3, NJ, 128).transpose(2, 1, 0).reshape(128, 66)
            prm[:, o + PO["bdw"]:o + PO["bdw"] + 22] = b_dw[l].reshape(NJ, 128).T
            prm[0:8, o + PO["bf"]] = b_f[l]
            cc = cconv[l, 2 * c:2 * c + 2]
            prm[:, o + PO["cconv"]:o + PO["cconv"] + 88] = cc.reshape(2, 2, NJ, 128).transpose(3, 2, 0, 1).reshape(128, 88)
        m = dict(shared)
        m.update(
            x_p=x_prompt[c], x_s=x_sample[2 * c:2 * c + 2].reshape(32, D), mem=mem_prompt[c],
            ckT=np.ascontiguousarray(ckT_all[:, 2 * c:2 * c + 2]), cv=np.ascontiguousarray(cv_all[:, 2 * c:2 * c + 2]),
            clfT=np.ascontiguousarray(clfT_all[:, 2 * c:2 * c + 2]), cmkT=np.ascontiguousarray(cmkT_all[:, 2 * c:2 * c + 2]),
            cmv=np.ascontiguousarray(cmv_all[:, 2 * c:2 * c + 2]), prm=prm)
        in_maps.append(m)
    return in_maps


def kernel(**inp):
    in_maps = _prep_inputs(inp)
    if "nc" not in _NC_CACHE:
        _NC_CACHE["nc"] = build()
    nc = _NC_CACHE["nc"]
    res = run_bass_kernel_spmd(nc, in_maps, core_ids=list(range(NCORES)))
    R = res.results
    cat = lambda name: np.stack([np.asarray(r[name], dtype=np.float32) for r in R], axis=0)
    y_p = cat("y_p")
    y_s = cat("y_s").reshape(16, 16, D)
    fk = cat("ofk").transpose(1, 0, 2, 3).reshape(L, 8, 2048, 8, 64)
    fv = cat("ofv").transpose(1, 0, 2, 3).reshape(L, 8, 2048, 8, 64)
    lf = cat("olf").transpose(1, 0, 3, 2)
    mk = cat("omk").transpose(1, 0, 2, 3).reshape(L, 8, 256, 4, 64)
    mv = cat("omv").transpose(1, 0, 2, 3).reshape(L, 8, 256, 4, 64)
    cvp = cat("oconv").transpose(1, 0, 4, 3, 2).reshape(L, 8, 2, FF)
    sk = cat("osk").transpose(1, 0, 2, 3).reshape(L, 16, 16, 8, 64)
    sv = cat("osv").transpose(1, 0, 2, 3).reshape(L, 16, 16, 8, 64)
    slf = cat("oslf").transpose(1, 0, 3, 2).reshape(L, 16, 16, 8)
    gv = cat("ogv").transpose(1, 0, 2, 3).reshape(L, 16, 16, 256)
    cvs = cat("osconv").transpose(1, 0, 4, 5, 3, 2).reshape(L, 16, 2, FF)
    outs = (y_p, y_s, fk, fv, lf, mk, mv, cvp, sk, sv, slf, gv, cvs)
    return tuple(np.ascontiguousarray(o, dtype=np.float32) for o in outs)
```

```python
import numpy as np
import ml_dtypes
from contextlib import ExitStack
import concourse.bass as bass
import concourse.mybir as mybir
from concourse.bass_utils import run_bass_kernel_spmd

F32 = mybir.dt.float32
BF16 = mybir.dt.bfloat16
ALU = mybir.AluOpType
AF = mybir.ActivationFunctionType

ENGS = ("pe", "act", "dve", "pool", "sp")
STOP = None
DBG = set()


class _Stop(Exception):
    pass


def _chk(name):
    if STOP == name:
        raise _Stop()
NCORES = 8
L = 2
D = 1024
NT = 2080
FF = 2816
NJ = 22
EPS = 1e-6
NEG = -30000.0


class Op:
    __slots__ = ("eng", "fn", "idx", "deps", "dma_key", "marked", "seq", "cum")

    def __init__(self, eng, fn, idx, dma_key):
        self.eng = eng
        self.fn = fn
        self.idx = idx
        self.deps = ()
        self.dma_key = dma_key
        self.marked = False
        self.seq = 0
        self.cum = 0


class Sched:
    def __init__(self, nc):
        self.nc = nc
        self.streams = {e: [] for e in ENGS}
        self.last_writer = {}
        self.readers = {}
        self.dma_counts = {}
        self.dma_last = {}

    def op(self, eng, fn, reads=(), writes=(), dma_key=None):
        o = Op(eng, fn, len(self.streams[eng]), dma_key)
        deps = set()
        lw = self.last_writer
        rd = self.readers
        for t in reads:
            w = lw.get(t)
            if w is not None:
                deps.add(w)
            if type(t) is tuple and t[0] == "ps":
                for r in rd.get(t, ()):
                    if r.eng != eng:
                        deps.add(r)
        for t in writes:
            w = lw.get(t)
            if w is not None:
                deps.add(w)
            r = rd.get(t)
            if r:
                deps.update(r)
        for t in reads:
            rd.setdefault(t, []).append(o)
        for t in writes:
            lw[t] = o
            rd[t] = []
        deps.discard(o)
        if eng == "pe":
            deps = {d for d in deps if not (d.eng == "pe" and d.dma_key is None)}
        o.deps = deps
        if dma_key is not None:
            c = self.dma_counts.get(dma_key, 0) + 1
            self.dma_counts[dma_key] = c
            o.cum = 16 * c
            self.dma_last[dma_key] = o
        self.streams[eng].append(o)
        return o

    def barrier(self):
        lasts = [s[-1] for s in self.streams.values() if s]
        lasts += list(self.dma_last.values())
        for e in ENGS:
            o = Op(e, lambda eng: eng.nop(), len(self.streams[e]), None)
            o.deps = {d for d in lasts if not (d.eng == e and d.dma_key is None)}
            self.streams[e].append(o)

    def emit(self, stack):
        nc = self.nc
        for e in ENGS:
            for o in self.streams[e]:
                for d in o.deps:
                    if d.dma_key is None:
                        d.marked = True
        sems = {}
        for e in ENGS:
            n = 0
            for o in self.streams[e]:
                if o.dma_key is None and o.marked:
                    n += 1
                    o.seq = n
            if n:
                sems[e] = stack.enter_context(nc.semaphore("s_" + e))
        dsems = {}
        for k in self.dma_counts:
            dsems[k] = stack.enter_context(nc.semaphore("d%d" % len(dsems)))
        self.n_sems = len(sems) + len(dsems)
        final = {k: o.cum for k, o in self.dma_last.items()}

        def run(eng_name, eng):
            waited = {}
            for o in self.streams[eng_name]:
                need = {}
                for d in o.deps:
                    if d.dma_key is not None:
                        key = ("d", d.dma_key)
                        val = d.cum
                    else:
                        key = ("c", d.eng)
                        val = d.seq
                    if val > need.get(key, 0):
                        need[key] = val
                for key, val in need.items():
                    if waited.get(key, 0) >= val:
                        continue
                    waited[key] = val
                    s = dsems[key[1]] if key[0] == "d" else sems[key[1]]
                    eng.wait_ge(s, val)
                ins = o.fn(eng)
                if o.dma_key is not None:
                    ins.then_inc(dsems[o.dma_key], 16)
                elif o.marked:
                    ins.then_inc(sems[o.eng], 1)
            if eng_name == "sp":
                for k, v in final.items():
                    if waited.get(("d", k), 0) < v:
                        eng.wait_ge(dsems[k], v)

        with nc.Block() as block:
            @block.tensor
            def _(t):
                run("pe", t)

            @block.scalar
            def _(t):
                run("act", t)

            @block.vector
            def _(t):
                run("dve", t)

            @block.gpsimd
            def _(t):
                run("pool", t)

            @block.sync
            def _(t):
                run("sp", t)


def tile_rows(i):
    return 128 if i < 16 else 32


def tcols(i):
    return (128 * i, 128) if i < 16 else (2048, 32)


BLKS = [(0, 512), (512, 512), (1024, 512), (1536, 512), (2048, 32)]

PO = {}
_o = 0
for _n, _w in (("gpm", 8), ("ggo", 8), ("gmem", 8), ("gpf", 8), ("bs", 4), ("bss", 4),
               ("wdw", 66), ("bdw", 22), ("bf", 1), ("cconv", 88)):
    PO[_n] = _o
    _o += _w
PPL = _o
NPRM = PPL * L


def build():
    nc = bass.Bass("TRN2", target_bir_lowering=False)

    def din(name, shape, dt=F32):
        return nc.dram_tensor(name, list(shape), dt, kind="ExternalInput").ap()

    def dout(name, shape):
        return nc.dram_tensor(name, list(shape), F32, kind="ExternalOutput").ap()

    x_p = din("x_p", [2048, D])
    x_s = din("x_s", [32, D])
    mem = din("mem", [256, D])
    ckT = din("ckT", [L, 2, 8, 64, 1024])
    cv = din("cv", [L, 2, 1024, 512])
    clfT = din("clfT", [L, 2, 8, 1024])
    cmkT = din("cmkT", [L, 2, 4, 64, 256])
    cmv = din("cmv", [L, 2, 256, 256])
    w_in = din("w_in", [L, D, 2312])
    w_mkv = din("w_mkv", [L, D, 512])
    w_out = din("w_out", [L, D, D])
    w_up = din("w_up", [L, NJ, 128, 8, 256])
    w_dn = din("w_dn", [L, FF, D])
    g_post_mix = din("g_post_mix", [L, D])
    g_post_ffn = din("g_post_ffn", [L, D])
    g_sgu = din("g_sgu", [L, 256])
    wsT = din("wsT", [L, 4, 128, 128])
    prm_h = din("prm", [128, NPRM])
    cst_h = din("cst", [128, 288], BF16)

    y_p = dout("y_p", [2048, D])
    y_s = dout("y_s", [32, D])
    ofk = dout("ofk", [L, 2048, 512])
    ofv = dout("ofv", [L, 2048, 512])
    olf = dout("olf", [L, 8, 2048])
    omk = dout("omk", [L, 256, 256])
    omv = dout("omv", [L, 256, 256])
    oconv = dout("oconv", [L, 128, NJ, 2])
    osk = dout("osk", [L, 32, 512])
    osv = dout("osv", [L, 32, 512])
    oslf = dout("oslf", [L, 8, 32])
    ogv = dout("ogv", [L, 32, 256])
    osconv = dout("osconv", [L, 128, NJ, 2, 2])

    cscr_p = nc.dram_tensor("cscr_p", [8, 2, 2048], BF16).ap()
    cscr_s = nc.dram_tensor("cscr_s", [8, 2, 2, 1040], BF16).ap()

    with ExitStack() as st:
        S = Sched(nc)
        X = st.enter_context(nc.sbuf_tensor("X", [128, 17, D], F32))
        PRM = st.enter_context(nc.sbuf_tensor("PRM", [128, NPRM], F32))
        CST = st.enter_context(nc.sbuf_tensor("CST", [128, 288], BF16))
        WSTSt = st.enter_context(nc.sbuf_tensor("WSTS", [32, L * 4 * 32], BF16))
        PTSt = st.enter_context(nc.sbuf_tensor("PTS", [128, 7 * 32], BF16))
        SM = st.enter_context(nc.sbuf_tensor("SM", [128, 256], F32))
        ONEROW = st.enter_context(nc.sbuf_tensor("ONEROW", [128, 64], F32))
        remaining = nc.sbuf_bytes_remaining
        remaining = remaining() if callable(remaining) else remaining
        ARB = (remaining - 512) // 64 * 64
        AR = st.enter_context(nc.sbuf_tensor("AR", [128, ARB // 2], BF16))
        AR32 = AR.bitcast(F32)
        PS = [st.enter_context(nc.psum_tensor("ps%d" % k, [128, 512], F32)) for k in range(8)]
        PSB = [p.bitcast(BF16) for p in PS]

        IDENT = CST[:, 0:128]
        MASKNEG = CST[:, 128:256]
        MASKS = CST[0:32, 256:288]


        def WSTS(l, g):
            o = (l * 4 + g) * 32
            return WSTSt[0:32, o:o + 32]

        def prm(l, name, w, p0=0, p1=128, off=0):
            o = l * PPL + PO[name] + off
            return PRM[p0:p1, o:o + w]

        SM_SS, SM_SD, SM_RSTD = 0, 1, 2
        SM_R = 16
        SM_SSF = 70
        SM_T = 90

        def bf(off, n, p0=0, p1=128):
            return AR[p0:p1, off // 2: off // 2 + n]

        def f32(off, n, p0=0, p1=128):
            return AR32[p0:p1, off // 4: off // 4 + n]

        cur = [0]

        def carve(nbytes):
            o = cur[0]
            cur[0] = o + (nbytes + 63) // 64 * 64
            return o

        HT_O = carve(8 * NT * 2)
        MX_O = carve(8 * NT * 2)
        VA_O = carve(17 * 8 * 65 * 2)
        QK_O = carve(4 * NT * 2)
        WS_O = [carve(8320) for _ in range(3)]
        SCR_O = carve(8192)
        MKT_O = carve(2048)
        MVA_O = carve(1040)
        MKTt = AR[:, MKT_O // 2:MKT_O // 2 + 1024]
        MVAt = AR[:, MVA_O // 2:MVA_O // 2 + 520]
        MIX_END = cur[0]
        assert MIX_END <= ARB, (MIX_END, ARB)

        def HT(c, c0, n):
            return bf(HT_O + (c * NT + c0) * 2, n)

        def HT3(c0, n):
            return AR[:, HT_O // 2: HT_O // 2 + 8 * NT].rearrange("p (c t) -> p c t", c=8)[:, :, c0:c0 + n]

        def MX(c, c0, n, p0=0, p1=128):
            return bf(MX_O + (c * NT + c0) * 2, n, p0, p1)

        def VA(j, h, nk=128):
            return bf(VA_O + ((j * 8 + h) * 65) * 2, 65, 0, nk)

        def VA128(j, h):
            return bf(VA_O + ((j * 8 + h) * 65) * 2, 128, 0, 128)

        def QK(slot, c0, n, p0, p1):
            return bf(QK_O + (slot * NT + c0) * 2, n, p0, p1)

        CT = lambda c0, n: f32(QK_O + c0 * 4, n, 96, 104)
        HI = lambda c0, n: bf(QK_O + 8320 + c0 * 2, n, 96, 104)
        LO = lambda c0, n: bf(QK_O + 8320 + 4160 + c0 * 2, n, 96, 104)

        def WS(s, c, c0, n):
            return bf(WS_O[s] + (c * 520 + c0) * 2, n)

        def WS3(s, ncol):
            return AR[:, WS_O[s] // 2: WS_O[s] // 2 + 8 * 520].rearrange("p (c n) -> p c n", c=8)[:, :, 0:ncol]

        def SCRb(g, n, p0=0, p1=128, off=0):
            return bf(SCR_O + g * 1024 + off * 2, n, p0, p1)

        def SCRf(g, n, p0=0, p1=128, off=0):
            return f32(SCR_O + g * 1024 + off * 4, n, p0, p1)

        def scr(*gs):
            return [("scr", g) for g in gs]

        def dma(eng, out, in_, reads=(), writes=(), key=None, slow=False):
            if slow:
                fn = lambda e: e.dma_start(out=out, in_=in_, allow_slow_non_contiguous=True)
            else:
                fn = lambda e: e.dma_start(out=out, in_=in_)
            return S.op(eng, fn, reads=reads, writes=writes, dma_key=key)

        psr = {"g": [0, 1], "s": [2, 3, 4], "o": [5, 6], "t": [7]}
        psi = {k: 0 for k in psr}

        def psnext(pool):
            lst = psr[pool]
            k = lst[psi[pool] % len(lst)]
            psi[pool] += 1
            return k

        def mm_group(out_fn, items, reads, writes):
            def fn(e):
                ins = None
                for (o, a, b, s0, s1) in items:
                    ins = e.matmul(o, a, b, start=s0, stop=s1, skip_group_check=True)
                return ins
            return S.op("pe", fn, reads=reads, writes=writes)

        dma("sp", PRM[:, :], prm_h, writes=["prm"], key="prm")
        dma("sp", CST[:, :], cst_h, writes=["cst"], key="cst")
        S.op("dve", lambda e: e.memset(WSTSt[:, :], 0.0), writes=["wsts"])
        for l in range(L):
            for b in range(2):
                dma("pool",
                    WSTSt[16 * b:16 * b + 16, l * 128:(l + 1) * 128].rearrange("p (g i) -> p g i", g=4)[:, :, 16 * b:16 * b + 16],
                    wsT[l, :, 0:16, 0:16].rearrange("g j i -> j g i"),
                    reads=["wsts"], writes=[("wsts", l, b)], key=("wsts", l), slow=True)
        S.op("dve", lambda e: e.memset(PTSt[:, :], 0.0), writes=["pts%d" % q for q in range(7)])
        S.op("dve", lambda e: e.memset(SM[:, 200:201], 1.0), writes=["one"])
        S.op("dve", lambda e: e.memset(ONEROW[:, :], 1.0), writes=["onerow"])
        ONE8 = SM[96:104, 200:201]

        for i in range(16):
            dma("sp", X[:, i, :], x_p[128 * i:128 * i + 128, :], writes=[("x", i)], key=("x", i))
        dma("sp", X[0:32, 16, :], x_s, writes=[("x", 16)], key=("x", 16))

        def prenorm_tile(src, rows, xtok, gcol, dst3, dst_tok, slot, extra_writes=(), hb=None, tpool="t", hb_toks=None, defer=False):
            if hb is None:
                hb = SCRb(2 * slot, 1024, 0, rows)
            hbt = list(hb_toks) if hb_toks is not None else scr(2 * slot, 2 * slot + 1)
            c0 = 3 * slot
            ss = SM[0:rows, c0:c0 + 1]
            sd = SM[0:rows, c0 + 1:c0 + 2]
            rs = SM[0:rows, c0 + 2:c0 + 3]
            S.op("act", lambda e: e.activation(hb, src, AF.Square, accum_out=ss),
                 reads=[xtok], writes=hbt + [("sm", c0)])
            S.op("act", lambda e: e.activation(sd, ss, AF.Ln, bias=EPS, scale=1.0 / D),
                 reads=[("sm", c0)], writes=[("sm", c0 + 1)])
            S.op("act", lambda e: e.activation(rs, sd, AF.Exp, scale=-0.5), reads=[("sm", c0 + 1)], writes=[("sm", c0 + 2)])
            S.op("act", lambda e: e.activation(hb, src, AF.Copy, scale=rs),
                 reads=[xtok, ("sm", c0 + 2)], writes=hbt)
            def part_b():
                k = psnext(tpool)

                def tr(e):
                    ins = None
                    for c in range(8):
                        ins = e.transpose(PSB[k][:, c * 128:c * 128 + rows], hb[:, c * 128:(c + 1) * 128],
                                          IDENT[0:rows, 0:rows])
                    return ins
                S.op("pe", tr, reads=hbt + ["cst"], writes=[("ps", k)])
                src3 = PSB[k][:, 0:1024].rearrange("p (c t) -> p c t", c=8)[:, :, 0:rows]
                S.op("dve", lambda e: e.tensor_tensor(dst3, src3, gcol.unsqueeze(2).to_broadcast([128, 8, rows]), ALU.mult),
                     reads=[("ps", k), "prm"], writes=[dst_tok] + list(extra_writes))
            if defer:
                return part_b
            part_b()

        PVQ = []

        def flush_pv():
            while PVQ:
                PVQ.pop(0)()

        def attend(qa_fn, K, ktiles, osubs, obank, tag):
            pendq = []
            first_done = [False]

            def pv(kt_i, kt, pt_ap, pt_tok):
                items = []
                if osubs == "fm":
                    lo_, hi_ = kt["lo"], kt["hi"]
                    mrows = kt["va"].shape[1]
                    items.append((PS[obank][0:mrows, lo_:hi_], kt["va"], pt_ap[:, lo_:hi_], kt_i == 0, kt_i == len(ktiles) - 1))
                    mm_group(None, items, reads=[pt_tok] + kt["vreads"], writes=[("ps", obank)])
                    return
                for (o_ap, c0, c1, fk, lk) in osubs:
                    if fk <= kt_i <= lk and kt["pv_lo"] <= c0:
                        items.append((o_ap, pt_ap[:, c0:c1], kt["va"], not first_done[0], kt_i == lk))
                        first_done[0] = True
                if items:
                    mm_group(None, items, reads=[pt_tok] + kt["vreads"], writes=[("ps", obank)])

            for kt_i, kt in enumerate(ktiles):
                k = psnext("s")
                nk, lo, hi = kt["nk"], kt["lo"], kt["hi"]
                items = []
                if kt["mask"] is not None:
                    mask_ap, mc = kt["mask"]
                    items.append((PS[k][0:nk, lo:lo + mc], kt["ka"], qa_fn(lo, mc), True, False))
                    items.append((PS[k][0:nk, lo:lo + mc], IDENT[0:nk, 0:nk], mask_ap, False, True))
                    if hi > lo + mc:
                        items.append((PS[k][0:nk, lo + mc:hi], kt["ka"], qa_fn(lo + mc, hi - lo - mc), True, True))
                else:
                    items.append((PS[k][0:nk, lo:hi], kt["ka"], qa_fn(lo, hi - lo), True, True))
                mm_group(None, items, reads=kt["reads"] + ["cst"], writes=[("ps", k)])
                if kt["pt"] == "rot":
                    g = attend.ptc % 4
                    attend.ptc += 1
                    pt_ap = SCRb(g, 512, 0, nk)
                    pt_tok = ("scr", g)
                else:
                    pt_ap = PTSt[0:nk, 32 * kt["pt"]:32 * kt["pt"] + 32]
                    pt_tok = "pts%d" % kt["pt"]
                S.op("act", lambda e, pt_ap=pt_ap, k=k, nk=nk, lo=lo, hi=hi:
                     e.activation(pt_ap[:, lo:hi], PS[k][0:nk, lo:hi], AF.Exp),
                     reads=[("ps", k)], writes=[pt_tok])
                while len(PVQ) >= 2:
                    PVQ.pop(0)()
                PVQ.append(lambda kt_i=kt_i, kt=kt, pt_ap=pt_ap, pt_tok=pt_tok: pv(kt_i, kt, pt_ap, pt_tok))
        attend.ptc = 0

        def cs_all():
            return ["cw"]

        def post_norm_residual(ostg, rows, i, junk, junk_toks, GPap, gp_toks, o_toks):
            ssc = SM[0:rows, SM_T + 8:SM_T + 9]
            S.op("act", lambda e: e.activation(junk, ostg, AF.Square, accum_out=ssc),
                 reads=o_toks, writes=list(junk_toks) + ["pn"])
            S.op("act", lambda e: e.activation(ssc, ssc, AF.Ln, bias=EPS, scale=1.0 / D), reads=["pn"], writes=["pn"])
            S.op("act", lambda e: e.activation(ssc, ssc, AF.Exp, scale=-0.5), reads=["pn"], writes=["pn"])
            S.op("dve", lambda e: e.scalar_tensor_tensor(ostg, ostg, ssc, GPap[0:rows, :], ALU.mult, ALU.mult),
                 reads=list(o_toks) + list(gp_toks) + ["pn"], writes=o_toks)
            S.op("pool", lambda e: e.tensor_tensor(X[0:rows, i, :], X[0:rows, i, :], ostg, ALU.add),
                 reads=list(o_toks) + [("x", i)], writes=[("x", i)])

        def ffn_phase(l):
            psr.clear()
            psr.update({"u": [0, 1, 2, 4, 5, 6], "t": [3], "d": [4, 5, 6, 7]})
            for kk in psr:
                psi[kk] = 0
            HC = 1056
            o = [0]

            def cv_(nb):
                r = o[0]
                o[0] = r + (nb + 63) // 64 * 64
                return r
            H2_O = cv_(8 * HC * 2)
            ACT_O = cv_(NJ * HC * 2)
            WDN_O = cv_(NJ * 1024 * 2)
            WU_O = [cv_(4096) for _ in range(2)]
            AE_O = [cv_(2064) for _ in range(2)]
            LIN_O = [cv_(1024) for _ in range(2)]
            CONV_O = [cv_(2048) for _ in range(2)]
            OSTG_O = cv_(4096)
            GPF_O = cv_(4096)
            HBF_O = cv_(2048)
            SAVE_O = cv_(NJ * 2 * 4)
            SAVES_O = cv_(NJ * 4 * 4)
            assert o[0] <= ARB, (o[0], ARB)

            H2 = lambda c, c0, n: bf(H2_O + (c * HC + c0) * 2, n)
            H23 = lambda c0, n: AR[:, H2_O // 2:H2_O // 2 + 8 * HC].rearrange("p (c t) -> p c t", c=8)[:, :, c0:c0 + n]
            ACTB = lambda j, c0, n: bf(ACT_O + (j * HC + c0) * 2, n)
            WDN = lambda j, c0, n: bf(WDN_O + (j * 1024 + c0) * 2, n)
            WDN3 = AR[:, WDN_O // 2:WDN_O // 2 + NJ * 1024].rearrange("p (j n) -> p j n", j=NJ)
            WU = lambda s, c, c0, n: bf(WU_O[s] + (c * 256 + c0) * 2, n)
            WU3 = lambda s: AR[:, WU_O[s] // 2:WU_O[s] // 2 + 2048].rearrange("p (c n) -> p c n", c=8)
            OSTG = f32(OSTG_O, 1024)
            GPF = f32(GPF_O, 1024)
            HBF = bf(HBF_O, 1024)
            SAVE = f32(SAVE_O, NJ * 2)
            SAVES = f32(SAVES_O, NJ * 4)

            dma("sp", GPF, g_post_ffn[l:l + 1, :].broadcast_to([128, D]), writes=["gpf"], key="gp")
            wdn_src = w_dn[l].rearrange("(j p) n -> p j n", p=128)
            for part in range(2):
                dma("pool", WDN3[:, 11 * part:11 * part + 11, :], wdn_src[:, 11 * part:11 * part + 11, :],
                    writes=[("wdn", part)], key=("wdn", part))
            aecnt = 0

            def wu_load(j, s_):
                dma("pool", WU3(s_), w_up[l, j], writes=[("wu", s_, 0), ("wu", s_, 1)], key=("wu", s_, 0))

            for half in range(2):
                tiles = list(range(0, 8)) if half == 0 else list(range(8, 17))
                base = 0 if half == 0 else 1024
                blocks = [(0, 512), (512, 512)] + ([(1024, 32)] if half == 1 else [])
                h2toks = [("h2", (i if i < 8 else i - 8)) for i in tiles]
                if half == 0:
                    HB2 = bf(AE_O[0], 1024)

                    def pa(i):
                        c0, n = tcols(i)
                        if i % 2 == 0:
                            return prenorm_tile(X[:, i, :], 128, ("x", i), prm(l, "gpf", 8), H23(c0, n), ("h2", i), 0,
                                                hb=HBF[:, :], tpool="t", hb_toks=["hbf"], defer=True)
                        return prenorm_tile(X[:, i, :], 128, ("x", i), prm(l, "gpf", 8), H23(c0, n), ("h2", i), 1,
                                            hb=HB2, tpool="t", hb_toks=[("ae", 0), ("aeh", 0), ("aet", 0)], defer=True)
                    pbq = {0: pa(0)}
                    for i in tiles:
                        if i + 1 < 8:
                            pbq[i + 1] = pa(i + 1)
                        pbq.pop(i)()
                wu_load(0, 0)
                pend_tail = None
                for j in range(NJ):
                    s = j % 2
                    if j + 1 < NJ:
                        wu_load(j + 1, (j + 1) % 2)
                    w0 = prm(l, "wdw", 1, 0, 128, 3 * j)
                    w1 = prm(l, "wdw", 1, 0, 128, 3 * j + 1)
                    w2 = prm(l, "wdw", 1, 0, 128, 3 * j + 2)
                    bd = prm(l, "bdw", 1, 0, 128, j)
                    for bidx, (lc0, n) in enumerate(blocks):
                        sample = (n == 32)
                        ka = psnext("u")
                        mm_group(None, [(PS[ka][:, 0:n], WU(s, c, 0, 128), H2(c, lc0, n), c == 0, c == 7) for c in range(8)],
                                 reads=h2toks + [("wu", s, 0)], writes=[("ps", ka)])
                        kl = psnext("u")
                        mm_group(None, [(PS[kl][:, 0:n], WU(s, c, 128, 128), H2(c, lc0, n), c == 0, c == 7) for c in range(8)],
                                 reads=h2toks + [("wu", s, 1)], writes=[("ps", kl)])
                        p = aecnt % 2
                        aecnt += 1
                        lin = bf(LIN_O[p], 512)
                        conv = f32(CONV_O[p], 512)
                        if sample:
                            AEs = f32(AE_O[p], 36).rearrange("p (b t) -> p b t", b=2)
                            S.op("dve", lambda e, AEs=AEs, j=j: e.tensor_copy(
                                AEs[:, :, 0:2], prm(l, "cconv", 4, 0, 128, 4 * j).rearrange("p (b r) -> p b r", b=2)),
                                reads=["prm"], writes=[("aeh", p)])
                            S.op("act", lambda e, AEs=AEs, ka=ka: e.copy(AEs[:, :, 2:18], PS[ka][:, 0:32].rearrange("p (b t) -> p b t", b=2)),
                                 reads=[("ps", ka)], writes=[("ae", p)])
                            taps = [AEs[:, :, k:k + 16] for k in range(3)]
                            cv3 = conv[:, 0:32].rearrange("p (b t) -> p b t", b=2)
                            psa = PS[ka][:, 0:32].rearrange("p (b t) -> p b t", b=2)
                            sil_out = f32(AE_O[p] + 256, 32)
                            sil_in = conv[:, 0:32]
                        else:
                            AEp = f32(AE_O[p], 514)
                            if bidx == 0 and half == 0:
                                S.op("dve", lambda e, AEp=AEp: e.memset(AEp[:, 0:2], 0.0), writes=[("aeh", p)])
                            elif bidx == 0:
                                S.op("dve", lambda e, AEp=AEp, j=j: e.tensor_copy(AEp[:, 0:2], SAVE[:, 2 * j:2 * j + 2]),
                                     reads=[("save", j)], writes=[("aeh", p)])
                            else:
                                prev = f32(AE_O[1 - p], 514)
                                S.op("dve", lambda e, AEp=AEp, prev=prev: e.tensor_copy(AEp[:, 0:2], prev[:, 512:514]),
                                     reads=[("aet", 1 - p)], writes=[("aeh", p)])
                            S.op("act", lambda e, AEp=AEp, ka=ka: e.copy(AEp[:, 2:514], PS[ka][:, 0:512]),
                                 reads=[("ps", ka)], writes=[("ae", p), ("aet", p)])
                            taps = [AEp[:, k:k + 512] for k in range(3)]
                            cv3 = conv
                            psa = PS[ka][:, 0:512]
                            sil_out = AEp[:, 0:512]
                            sil_in = conv
                        S.op("act", lambda e, cv3=cv3, psa=psa, w2=w2, bd=bd: e.activation(cv3, psa, AF.Identity, bias=bd, scale=w2),
                             reads=[("ps", ka), "prm"], writes=[("conv", p)])
                        S.op("act", lambda e, lin=lin, kl=kl, n=n: e.copy(lin[:, 0:n], PS[kl][:, 0:n]),
                             reads=[("ps", kl)], writes=[("lin", p)])
                        S.op("dve", lambda e, cv3=cv3, taps=taps, w1=w1: e.scalar_tensor_tensor(cv3, taps[1], w1, cv3, ALU.mult, ALU.add),
                             reads=[("ae", p), ("aeh", p), ("conv", p)], writes=[("conv", p)])
                        S.op("dve", lambda e, cv3=cv3, taps=taps, w0=w0: e.scalar_tensor_tensor(cv3, taps[0], w0, cv3, ALU.mult, ALU.add),
                             reads=[("ae", p), ("aeh", p), ("conv", p)], writes=[("conv", p)])
                        if not sample and bidx == 1:
                            S.op("dve", lambda e, AEp=AEp, j=j: e.tensor_copy(SAVE[:, 2 * j:2 * j + 2], AEp[:, 512:514]),
                                 reads=[("aet", p)], writes=[("save", j)])
                        if sample:
                            S.op("dve", lambda e, AEs=AEs, j=j: e.tensor_copy(
                                SAVES[:, 4 * j:4 * j + 4].rearrange("p (b r) -> p b r", b=2), AEs[:, :, 16:18]),
                                reads=[("ae", p)], writes=[("saves", j)])

                        def tail(p=p, sil_out=sil_out, sil_in=sil_in, j=j, lc0=lc0, n=n, lin=lin, half=half):
                            S.op("act", lambda e: e.activation(sil_out, sil_in, AF.Silu),
                                 reads=[("conv", p), ("ae", p), ("aeh", p)], writes=[("ae", p), ("aeh", p)])
                            S.op("pool", lambda e: e.tensor_tensor(ACTB(j, lc0, n), sil_out, lin[:, 0:n], ALU.mult),
                                 reads=[("ae", p), ("aeh", p), ("lin", p)], writes=[("actb", j, half)])
                        if pend_tail is not None:
                            pend_tail()
                        pend_tail = tail
                if pend_tail is not None:
                    pend_tail()
                    pend_tail = None
                if half == 1:
                    dma("sp", oconv[l], SAVE.rearrange("p (j r) -> p j r", r=2), reads=[("save", j) for j in range(NJ)], key="oconv")
                    dma("sp", osconv[l], SAVES.rearrange("p (j b r) -> p j b r", b=2, r=2), reads=[("saves", j) for j in range(NJ)], key="osconv")
                atoks = [("actb", j, half) for j in range(NJ)]
                nxt = list(range(8, 17)) if half == 0 else []
                for ti_, i in enumerate(tiles):
                    rows = tile_rows(i)
                    c0, n = tcols(i)
                    lc0 = c0 - base
                    defer_b = []
                    for i2 in (nxt[ti_:ti_ + 1] if ti_ < 7 else nxt[7:8]):
                        rows2 = tile_rows(i2)
                        c02, n2 = tcols(i2)
                        defer_b.append(prenorm_tile(X[0:rows2, i2, :], rows2, ("x", i2), prm(l, "gpf", 8), H23(c02 - 1024, n2),
                                                    ("h2", i2 - 8), 0, hb=HBF[0:rows2, :], tpool="t", hb_toks=["hbf"], defer=True))
                        if "nodefer" in DBG:
                            defer_b.pop()()
                    for hf in range(2):
                        kd = psnext("d")
                        mm_group(None, [(PS[kd][0:rows, :], ACTB(j, lc0, n), WDN(j, 512 * hf, 512), j == 0, j == NJ - 1) for j in range(NJ)],
                                 reads=atoks + [("wdn", 0), ("wdn", 1)], writes=[("ps", kd)])
                        if hf == 0:
                            S.op("act", lambda e, kd=kd, rows=rows: e.copy(OSTG[0:rows, 0:512], PS[kd][0:rows, :]),
                                 reads=[("ps", kd)], writes=["ostg0"])
                        else:
                            S.op("dve", lambda e, kd=kd, rows=rows: e.tensor_copy(OSTG[0:rows, 512:1024], PS[kd][0:rows, :]),
                                 reads=[("ps", kd)], writes=["ostg1"])
                    for fb in defer_b:
                        fb()
                    post_norm_residual(OSTG[0:rows, :], rows, i, HBF[0:rows, :], ["hbf"], GPF, ["gpf"], ["ostg0", "ostg1"])
                    if l == L - 1:
                        if i < 16:
                            dma("sp", y_p[128 * i:128 * i + 128, :], X[:, i, :], reads=[("x", i)], key=("x", i))
                        else:
                            dma("sp", y_s, X[0:32, 16, :], reads=[("x", 16)], key=("x", 16))
                if half == 0:
                    prenorm_tile(X[0:32, 16, :], 32, ("x", 16), prm(l, "gpf", 8), H23(1024, 32), ("h2", 8), 0,
                                 hb=HBF[0:32, :], tpool="t", hb_toks=["hbf"])
        RFOX = lambda i, rows=128: SM[0:rows, 100 + i:101 + i]
        RSGU = lambda i, rows=128: SM[0:rows, 120 + i:121 + i]
        RMEM = lambda i, rows=128: SM[0:rows, 140 + i:141 + i]
        SSFOX = lambda i, rows=128: SM[0:rows, 160 + i:161 + i]
        SSMEM = lambda i, rows=128: SM[0:rows, 180 + i:181 + i]
        ALLHT = [("ht", i) for i in range(17)]

        class NSQ:
            def __init__(self):
                self.p1 = None
                self.p2 = None
                self.p3 = None

            def step(self, new_p1):
                if self.p3 is not None:
                    self.p3()
                    self.p3 = None
                if self.p2 is not None:
                    self.p3 = self.p2()
                    self.p2 = None
                if self.p1 is not None:
                    self.p2 = self.p1()
                self.p1 = new_p1

            def flush(self):
                self.step(None)
                self.step(None)
                self.step(None)
        nsq = NSQ()

        def wload(slot, src2d, ncol, extra_writes=()):
            dma("pool", WS3(slot, ncol), src2d.rearrange("(c p) n -> p c n", p=128),
                writes=[("ws", slot)] + list(extra_writes), key=("ws", slot))

        def KAS(b, c0, n, p0=0, p1=67):
            return bf(WS_O[0] + b * 3200 + c0 * 2, n, p0, p1)

        def VAS(b, j):
            return bf(WS_O[0] + b * 3200 + 2080 + j * 130, 65)

        def VAS3(b):
            return AR[:, (WS_O[0] + b * 3200 + 2080) // 2:(WS_O[0] + b * 3200 + 2080) // 2 + 520].rearrange("p (j k) -> p j k", k=65)

        SC_TOKS = [("sc", b, x) for b in range(2) for x in ("k", "v", "r", "one")]

        def norm_store_fm(obank, I, grp, chunk, p0, ssq4, l):
            rlrow = f32(SCR_O + 6 * 1024, 512, 64, 65)
            hfT = f32(SCR_O + 6 * 1024, 512, 0, 64)
            S.op("act", lambda e: e.activation(rlrow, PS[obank][64:65, 0:512], AF.Ln), reads=[("ps", obank)], writes=[("scr", 6, "r")])
            S.op("act", lambda e: e.activation(rlrow, rlrow, AF.Exp, scale=-1.0), reads=[("scr", 6, "r")], writes=[("scr", 6, "r")])
            kb = psnext("g")
            S.op("pe", lambda e: e.matmul(PS[kb][0:64, 0:512], ONEROW[64:65, 0:64], rlrow, start=True, stop=True, skip_group_check=True),
                 reads=[("scr", 6, "r"), "onerow"], writes=[("ps", kb)])
            S.op("dve", lambda e: e.tensor_copy(hfT, PS[kb][0:64, 0:512]), reads=[("ps", kb)], writes=scr(6, 7))
            S.op("dve", lambda e: e.tensor_tensor(hfT, PS[obank][0:64, 0:512], hfT, ALU.mult), reads=[("ps", obank)] + scr(6, 7), writes=scr(6, 7))
            return lambda: norm_store_fm2(I, grp, chunk, p0, ssq4, l)

        def norm_store_fm2(I, grp, chunk, p0, ssq4, l):
            hfT = f32(SCR_O + 6 * 1024, 512, 0, 64)
            sqb = bf(SCR_O + 4 * 1024, 512, 0, 64)
            S.op("act", lambda e: e.activation(MX(chunk, 512 * I, 512, p0, p0 + 64), hfT, AF.Copy,
                                               scale=prm(l, "ggo", 1, p0, p0 + 64, chunk)),
                 reads=scr(6, 7) + ["prm"], writes=[("mx", chunk, 4 * I + s_, p0) for s_ in range(4)])
            S.op("act", lambda e: e.activation(sqb, hfT, AF.Square), reads=scr(6, 7), writes=scr(4))
            return lambda: norm_store_fm3(I, grp, ssq4)

        def norm_store_fm3(I, grp, ssq4):
            sqb = bf(SCR_O + 4 * 1024, 512, 0, 64)
            kt = psnext("t")
            ones_col = bf(VA_O + 64 * 2, 1, 0, 64)

            def ssmm(e):
                ins = None
                for s_ in range(4):
                    ins = e.matmul(PS[kt][:, s_:s_ + 1], sqb[:, 128 * s_:128 * s_ + 128], ones_col, start=True, stop=True,
                                   skip_group_check=True)
                return ins
            S.op("pe", ssmm, reads=scr(4) + ["va_ones"], writes=[("ps", kt)])
            toks4 = [("ssq", grp, 4 * I + s_) for s_ in range(4)]
            S.op("dve", lambda e: e.tensor_tensor(ssq4, ssq4, PS[kt][:, 0:4], ALU.add), reads=[("ps", kt)] + toks4, writes=toks4)

        def norm_and_store_block(obank, I, grp, chunk, p0, ssq4, l):
            O4 = PS[obank][:, 0:260].rearrange("p (s k) -> p s k", k=65)
            rl4 = SM[:, SM_T + 10:SM_T + 14]
            sq4 = SM[:, SM_T + 14:SM_T + 18]
            S.op("dve", lambda e: e.reciprocal(rl4.unsqueeze(2), O4[:, :, 64:65]), reads=[("ps", obank)], writes=["rl4"])
            hf4 = SCRf(6, 256)
            hb4 = SCRb(7, 256)
            S.op("dve", lambda e: e.tensor_tensor(hf4.rearrange("p (s k) -> p s k", k=64), O4[:, :, 0:64],
                                                  rl4.unsqueeze(2).to_broadcast([128, 4, 64]), ALU.mult),
                 reads=[("ps", obank), "rl4"], writes=scr(6))

            def sqs(e):
                ins = None
                for s_ in range(4):
                    ins = e.activation(hb4[:, 64 * s_:64 * s_ + 64], hf4[:, 64 * s_:64 * s_ + 64], AF.Square,
                                       accum_out=sq4[:, s_:s_ + 1])
                return ins
            S.op("act", sqs, reads=scr(6), writes=scr(7) + ["sq4"])
            toks4 = [("ssq", grp, 4 * I + s_) for s_ in range(4)]
            S.op("dve", lambda e: e.tensor_tensor(ssq4, ssq4, sq4, ALU.add), reads=["sq4"] + toks4, writes=toks4)
            S.op("act", lambda e: e.copy(hb4, hf4), reads=scr(6, 7), writes=scr(7))
            kt = psnext("t")

            def tr(e):
                ins = None
                for s_ in range(4):
                    ins = e.transpose(PSB[kt][p0:p0 + 64, 128 * s_:128 * s_ + 128], hb4[:, 64 * s_:64 * s_ + 64], IDENT[:, :])
                return ins
            S.op("pe", tr, reads=scr(7) + ["cst"], writes=[("ps", kt)])
            S.op("dve", lambda e: e.tensor_scalar(MX(chunk, 512 * I, 512, p0, p0 + 64), PSB[kt][p0:p0 + 64, 0:512],
                                                  prm(l, "ggo", 1, p0, p0 + 64, chunk), None, ALU.mult),
                 reads=[("ps", kt), "prm"], writes=[("mx", chunk, 4 * I + s_, p0) for s_ in range(4)])

        def norm_and_store(obank, ocol, rows, tile_i, grp, chunk, p0, ssq_acc, l):
            rl = SM[0:rows, SM_T + 4:SM_T + 5]
            S.op("dve", lambda e: e.reciprocal(rl, PS[obank][0:rows, ocol + 64:ocol + 65]),
                 reads=[("ps", obank)], writes=["rl"])
            hf = SCRf(5, 64, 0, rows)
            hb = SCRb(5, 64, 0, rows, off=256)
            S.op("dve", lambda e: e.tensor_scalar(hf, PS[obank][0:rows, ocol:ocol + 64], rl, None, ALU.mult),
                 reads=[("ps", obank), "rl"], writes=scr(5))
            sq = SM[0:rows, SM_T + 5:SM_T + 6]
            S.op("act", lambda e: e.activation(hb, hf, AF.Square, accum_out=sq), reads=scr(5), writes=scr(5) + ["sq"])
            S.op("dve", lambda e: e.tensor_tensor(ssq_acc, ssq_acc, sq, ALU.add),
                 reads=["sq", ("ssq", grp, tile_i)], writes=[("ssq", grp, tile_i)])
            S.op("act", lambda e: e.copy(hb, hf), reads=scr(5), writes=scr(5))
            kt = psnext("t")
            c0, n = tcols(tile_i)
            S.op("pe", lambda e: e.transpose(PSB[kt][p0:p0 + 64, 0:rows], hb, IDENT[0:rows, 0:rows]),
                 reads=scr(5) + ["cst"], writes=[("ps", kt)])
            S.op("dve", lambda e: e.tensor_scalar(MX(chunk, c0, n, p0, p0 + 64), PSB[kt][p0:p0 + 64, 0:rows],
                                                  prm(l, "ggo", 1, p0, p0 + 64, chunk), None, ALU.mult),
                 reads=[("ps", kt), "prm"], writes=[("mx", chunk, tile_i, p0)])

        try:
            for l in range(L):
                psr.clear()
                psr.update({"g": [0, 1], "s": [2, 3, 4], "o": [5, 6], "t": [7], "w": [0, 1, 2, 3, 4, 5]})
                for kk in psr:
                    psi[kk] = 0
                S.op("dve", lambda e: e.memset(
                    AR[:, VA_O // 2: VA_O // 2 + 17 * 8 * 65].rearrange("p (a k) -> p a k", k=65)[:, :, 64:65], 1.0),
                    writes=["va_ones"])
                S.op("dve", lambda e: e.memset(MVAt[:, :].rearrange("p (a k) -> p a k", k=65)[:, :, 64:65], 1.0),
                     writes=["mva_ones"])
                S.op("dve", lambda e: e.memset(MKTt[64:128, :], 0.0), writes=["mkt_zero"])
                wload(0, w_in[l][:, 1024:1536], 512, extra_writes=SC_TOKS if l > 0 else ())
                wload(1, w_in[l][:, 512:1024], 512)
                wload(2, w_in[l][:, 1536:2056], 520)

                SGC = MX_O + 12288
                WSTl = bf(SGC, 512)
                GSl = f32(SGC + 1024, 256)
                dma("pool", WSTl.rearrange("p (g i) -> p g i", g=4), wsT[l].rearrange("g j i -> j g i"),
                    writes=[("sguc", 0)], key="wst")
                S.op("dve", lambda e: e.memset(bf(SGC, 512, 64, 128).rearrange("p (g i) -> p g i", g=4)[:, :, 0:64], 0.0),
                     reads=[("sguc", 0)], writes=[("sguc", 0)])
                dma("sp", GSl, g_sgu[l:l + 1, :].broadcast_to([128, 256]), writes=[("sguc", 1)], key="gs")

                def sgu_ops(i):
                    st_ = 0 if "setA" in DBG else i % 2
                    base = MX_O if st_ == 0 else MX_O + 6144
                    tk = "sguA" if st_ == 0 else "sguB"
                    Sb = lambda g, n, p0=0, p1=128: bf(base + g * 1024, n, p0, p1)
                    Sf = lambda g, n, p0=0, p1=128: f32(base + g * 1024, n, p0, p1)
                    sc = lambda *gs: [(tk, g) for g in gs]
                    rows = tile_rows(i)
                    c0, n = tcols(i)
                    s1, s2 = [], []
                    k = psnext("s")
                    z = Sf(0, 512, 0, rows)
                    t1 = Sf(2, 512, 0, rows)
                    ssv = SM[0:rows, SM_T + 20 + 2 * st_:SM_T + 21 + 2 * st_]
                    sss = SM[0:rows, SM_T + 21 + 2 * st_:SM_T + 22 + 2 * st_]
                    ssvt, ssst = ("ssv", st_), ("sss", st_)
                    vvf = Sf(3, 256, 0, rows)
                    vvb = Sb(2, 256, 0, rows)
                    sg = Sf(4, 256, 0, rows)
                    sgb = Sb(5, 256, 0, rows)
                    s1.append(lambda: mm_group(None, [(PS[k][0:rows, :], HT(c, c0, n), WS(2, c, 8, 512), c == 0, c == 7) for c in range(8)],
                                               reads=[("ht", i), ("ws", 2)], writes=[("ps", k)]))
                    s1.append(lambda: S.op("act", lambda e: e.copy(z, PS[k][0:rows, :]), reads=[("ps", k)], writes=sc(0, 1)))
                    s1.append(lambda: S.op("dve", lambda e: e.tensor_tensor(t1, z, z, ALU.mult), reads=sc(0, 1), writes=sc(2, 3)))
                    s1.append(lambda: S.op("dve", lambda e: e.tensor_scalar(t1, t1, 0.044715, 1.0, ALU.mult, ALU.add), reads=sc(2, 3), writes=sc(2, 3)))
                    s1.append(lambda: S.op("dve", lambda e: e.tensor_tensor(t1, t1, z, ALU.mult), reads=sc(0, 1, 2, 3), writes=sc(2, 3)))
                    s1.append(lambda: S.op("act", lambda e: e.activation(t1, t1, AF.Exp, scale=-1.5957691216057308), reads=sc(2, 3), writes=sc(2, 3)))
                    s1.append(lambda: S.op("act", lambda e: e.activation(t1, t1, AF.Ln, bias=1.0, scale=1.0), reads=sc(2, 3), writes=sc(2, 3)))
                    s1.append(lambda: S.op("act", lambda e: e.activation(t1, t1, AF.Exp, scale=-1.0), reads=sc(2, 3), writes=sc(2, 3)))
                    s1.append(lambda: S.op("dve", lambda e: e.tensor_tensor(z, z, t1, ALU.mult), reads=sc(0, 1, 2, 3), writes=sc(0, 1)))
                    s1.append(lambda: S.op("act", lambda e: e.activation(t1[:, 0:256], z[:, 256:512], AF.Square, accum_out=ssv),
                                           reads=sc(0, 1), writes=sc(2) + [ssvt]))
                    s1.append(lambda: S.op("act", lambda e: e.activation(ssv, ssv, AF.Ln, bias=EPS, scale=1.0 / 256), reads=[ssvt], writes=[ssvt]))
                    s1.append(lambda: S.op("act", lambda e: e.activation(ssv, ssv, AF.Exp, scale=-0.5), reads=[ssvt], writes=[ssvt]))
                    s1.append(lambda: S.op("dve", lambda e: e.scalar_tensor_tensor(vvf, z[:, 256:512], ssv, GSl[0:rows, :], ALU.mult, ALU.mult),
                                           reads=sc(0, 1) + [("sguc", 1)] + [ssvt], writes=sc(3)))
                    s1.append(lambda: S.op("act", lambda e: e.copy(vvb, vvf), reads=sc(3), writes=sc(2)))
                    if i == 16:
                        s1.append(lambda: dma("sp", ogv[l], vvf, reads=sc(3), key="ogv"))
                    k2 = psnext("s")
                    if i < 16:
                        items = [(PS[k2][0:128, 64 * g:64 * g + 64], WSTl[:, 128 * g:128 * g + 128], vvb[:, 64 * g:64 * g + 64], True, True) for g in range(4)]
                        bsap = prm(l, "bs", 4)
                        wtok = [("sguc", 0)]
                    else:
                        items = [(PS[k2][0:32, 64 * g:64 * g + 64], WSTS(l, g), vvb[:, 64 * g:64 * g + 64], True, True) for g in range(4)]
                        bsap = prm(l, "bss", 4, 0, 32)
                        wtok = ["wsts"] + [("wsts", l, b) for b in range(2)]
                    s1.append(lambda: mm_group(None, items, reads=sc(2) + wtok, writes=[("ps", k2)]))
                    s2.append(lambda: S.op("dve", lambda e: e.tensor_tensor(
                        sg.rearrange("p (g d) -> p g d", g=4), PS[k2][0:rows, 0:256].rearrange("p (g d) -> p g d", g=4),
                        bsap.unsqueeze(2).to_broadcast([rows, 4, 64]), ALU.add),
                        reads=[("ps", k2), "prm"], writes=sc(4)))
                    s2.append(lambda: S.op("dve", lambda e: e.tensor_tensor(sg, sg, z[:, 0:256], ALU.mult), reads=sc(0, 1, 4), writes=sc(4)))
                    s2.append(lambda: S.op("act", lambda e: e.activation(sgb, sg, AF.Square, accum_out=sss),
                                           reads=sc(4), writes=sc(5) + [ssst]))
                    s2.append(lambda: S.op("act", lambda e: e.activation(sss, sss, AF.Ln, bias=EPS, scale=1.0 / 256), reads=[ssst], writes=[ssst]))
                    s2.append(lambda: S.op("act", lambda e: e.activation(RSGU(i, rows), sss, AF.Exp, scale=-0.5),
                                           reads=[ssst], writes=[("rsgu", i)]))
                    s2.append(lambda: S.op("act", lambda e: e.copy(sgb, sg), reads=sc(4, 5), writes=sc(5)))

                    def trs():
                        kt = psnext("o")

                        def tr(e):
                            ins = None
                            for cc in range(2):
                                ins = e.transpose(PSB[kt][:, cc * 128:cc * 128 + rows], sgb[:, cc * 128:(cc + 1) * 128], IDENT[0:rows, 0:rows])
                            return ins
                        S.op("pe", tr, reads=sc(5) + ["cst"], writes=[("ps", kt)])
                        for cc in range(2):
                            S.op("dve", lambda e, cc=cc, l=l: e.tensor_scalar(
                                MX(4 + cc, c0, n), PSB[kt][:, cc * 128:cc * 128 + rows], prm(l, "ggo", 1, 0, 128, 4 + cc), None, ALU.mult),
                                reads=[("ps", kt), "prm"], writes=[("mx", 4 + cc, i)])
                    s2.append(trs)
                    return s1, s2

                def interleave(a, b):
                    for q in range(max(len(a), len(b))):
                        if q < len(a):
                            a[q]()
                        if q < len(b):
                            b[q]()

                def vk_tile(i):
                    rows = tile_rows(i)
                    c0, n = tcols(i)
                    for which, slot, oprompt, osample, sg in (("v", 0, ofv, osv, 4), ("k", 1, ofk, osk, 6)):
                        k = psnext("g")
                        mm_group(None, [(PS[k][0:rows, :], HT(c, c0, n), WS(slot, c, 0, 512), c == 0, c == 7)
                                        for c in range(8)],
                                 reads=[("ht", i), ("ws", slot)], writes=[("ps", k)])
                        stg = SCRf(sg, 512, 0, rows)
                        S.op("act", lambda e, stg=stg, k=k, rows=rows: e.copy(stg, PS[k][0:rows, :]),
                             reads=[("ps", k)], writes=scr(sg, sg + 1))
                        if which == "v":
                            va3 = AR[0:rows, VA_O // 2 + i * 520: VA_O // 2 + (i + 1) * 520].rearrange(
                                "p (h k) -> p h k", k=65)[:, :, 0:64]
                            S.op("dve", lambda e, va3=va3, k=k, rows=rows: e.tensor_copy(
                                va3, PS[k][0:rows, :].rearrange("p (h k) -> p h k", k=64)),
                                reads=[("ps", k), "va_ones"], writes=[("va", i)])
                        dst = oprompt[l, 128 * i:128 * i + 128, :] if i < 16 else osample[l, :, :]
                        dma("sp", dst, stg, reads=scr(sg, sg + 1), key=("stg", sg))

                prev2 = []

                def pn_a(i):
                    rows = tile_rows(i)
                    c0, n = tcols(i)
                    return prenorm_tile(X[0:rows, i, :], rows, ("x", i), prm(l, "gpm", 8), HT3(c0, n), ("ht", i), i % 2, defer=True)
                pn_b = {0: pn_a(0)}
                for i in range(17):
                    if i + 1 < 17:
                        pn_b[i + 1] = pn_a(i + 1)
                    pn_b.pop(i)()
                    if i >= 1:
                        vk_tile(i - 1)
                        s1_, s2_ = sgu_ops(i - 1)
                        interleave(s1_, prev2)
                        prev2 = s2_
                vk_tile(16)
                s1_, s2_ = sgu_ops(16)
                interleave(s1_, prev2)
                interleave([], s2_)
                S.op("dve", lambda e: e.memset(bf(MX_O + 15 * 1024, 2), 0.0),
                     writes=[("sguA", g) for g in range(6)] + [("sguB", g) for g in range(6)] + [("sguc", 0), ("sguc", 1)]
                     + [("mx", c, i, p0) for c in (0, 1, 2, 3) for i in range(17) for p0 in (0, 64)])

                _chk("vk%d" % l)
                NB = SM[0:8, 210:211]
                S.op("dve", lambda e, l=l: e.tensor_scalar(NB, prm(l, "bf", 1, 0, 8), -1.0, None, ALU.mult),
                     reads=["prm"], writes=["cw", "nb"])
                for bi, (c0, n) in enumerate(BLKS):
                    k = psnext("g")
                    mm_group(None, [(PS[k][0:8, 0:n], WS(2, c, 0, 8), HT(c, c0, n), c == 0, c == 7) for c in range(8)],
                             reads=ALLHT + [("ws", 2)], writes=["cw", ("ps", k)])
                    tmp = SCRf(0, 512, 96, 104)
                    S.op("act", lambda e, k=k, n=n, tmp=tmp: e.activation(tmp[:, 0:n], PS[k][0:8, 0:n], AF.Exp, bias=NB, scale=-1.0),
                         reads=[("ps", k), "nb"], writes=scr(0, 1))
                    S.op("act", lambda e, n=n, tmp=tmp: e.activation(tmp[:, 0:n], tmp[:, 0:n], AF.Ln, bias=1.0, scale=1.0),
                         reads=scr(0, 1), writes=scr(0, 1))
                    S.op("dve", lambda e, c0=c0, n=n, tmp=tmp: e.tensor_scalar(CT(c0, n), tmp[:, 0:n], -1.0, None, ALU.mult),
                         reads=scr(0, 1), writes=["cw", ("ct", bi)])
                dma("sp", olf[l], CT(0, 2048), reads=[("ct", b) for b in range(4)], writes=["cw"], key="olf")
                dma("sp", oslf[l], CT(2048, 32), reads=[("ct", 4)], writes=["cw"], key="olf2")
                SLF = SM[96:104, 220:252]
                S.op("dve", lambda e: e.tensor_copy(SLF, CT(2048, 32)), reads=[("ct", 4)], writes=["cw", "slf"])
                S.op("dve", lambda e: e.tensor_tensor_scan(CT(0, 2048), ONE8.to_broadcast([8, 2048]), CT(0, 2048), 0.0, ALU.mult, ALU.add),
                     reads=[("ct", b) for b in range(4)] + ["one"], writes=["cw", "ctp"])
                S.op("dve", lambda e: e.tensor_copy(HI(0, 2048), CT(0, 2048)), reads=["ctp"], writes=["cw", "hi"])
                S.op("dve", lambda e: e.tensor_tensor(CT(0, 2048), CT(0, 2048), HI(0, 2048), ALU.subtract),
                     reads=["ctp", "hi"], writes=["cw", "ctp"])
                S.op("dve", lambda e: e.tensor_copy(LO(0, 2048), CT(0, 2048)), reads=["ctp"], writes=["cw", "lo"])
                dma("sp", cscr_p[:, 0, :], HI(0, 2048), reads=["hi"], writes=["cw", "cscr_p"], key="cscr0")
                dma("sp", cscr_p[:, 1, :], LO(0, 2048), reads=["lo"], writes=["cw", "cscr_p2"], key="cscr1")
                CS = lambda b, c0, n: CT(b * 1040 + c0, n)
                for b in range(2):
                    dma("sp", CS(b, 0, 1024), clfT[l, b], reads=["ctp", "lo", ("ct", 4), "slf"], writes=["cw", ("cs", b)], key=("csl", b))
                    S.op("dve", lambda e, b=b: e.tensor_copy(CS(b, 1024, 16), SM[96:104, 220 + 16 * b:236 + 16 * b]),
                         reads=["slf", "ctp", "lo", ("ct", 4)], writes=["cw", ("cs2", b)])
                    S.op("dve", lambda e, b=b: e.tensor_tensor_scan(CS(b, 0, 1040), ONE8.to_broadcast([8, 1040]), CS(b, 0, 1040), 0.0, ALU.mult, ALU.add),
                         reads=[("cs", b), ("cs2", b), "one"], writes=["cw", ("csc", b)])
                S.op("dve", lambda e: e.tensor_copy(HI(0, 2080), CT(0, 2080)),
                     reads=[("csc", 0), ("csc", 1), "cscr_p", "cscr_p2"], writes=["cw", "hi"])
                S.op("dve", lambda e: e.tensor_tensor(CT(0, 2080), CT(0, 2080), HI(0, 2080), ALU.subtract),
                     reads=["hi"], writes=["cw", ("csc", 0), ("csc", 1)])
                S.op("dve", lambda e: e.tensor_copy(LO(0, 2080), CT(0, 2080)), reads=[("csc", 0), ("csc", 1), "cscr_p2"], writes=["cw", "lo"])
                for b in range(2):
                    dma("sp", cscr_s[:, 0, b, :], HI(b * 1040, 1040), reads=["hi"], writes=["cw", ("cscr_s", b, 0)], key=("cscrs", b, 0))
                    dma("sp", cscr_s[:, 1, b, :], LO(b * 1040, 1040), reads=["lo"], writes=["cw", ("cscr_s", b, 1)], key=("cscrs", b, 1))

                _chk("fg%d" % l)
                _chk("sgu%d" % l)
                S.op("dve", lambda e: e.memset(bf(QK_O, 4 * NT, 64, 128), 0.0),
                     writes=["cw", "hi", "lo"] + [("qkc", s_) for s_ in range(4)])
                for s_ in (0, 2):
                    S.op("dve", lambda e, s_=s_: e.memset(QK(s_, 0, NT, 64, 67), -1.0), writes=[("qkc", s_)])
                for s_ in (1, 3):
                    S.op("dve", lambda e, s_=s_: e.memset(QK(s_, 0, NT, 64, 65), 1.0), writes=[("qkc", s_)])
                wload(0, w_mkv[l], 512)
                MEMT3 = AR[:, (SCR_O + 4096) // 2:(SCR_O + 4096) // 2 + 2048].rearrange("p (c t) -> p c t", c=8)
                MEMT = lambda c, c0, n: bf(SCR_O + 4096 + (c * 256 + c0) * 2, n)
                memx = f32(WS_O[2], 1024)
                for mt in range(2):
                    dma("sp", memx, mem[128 * mt:128 * mt + 128, :], writes=[("ws", 2)], key=("ws", 2))
                    prenorm_tile(memx, 128, ("ws", 2), prm(l, "gmem", 8), MEMT3[:, :, 128 * mt:128 * mt + 128],
                                 ("memt", mt), 0, extra_writes=scr(4, 5, 6, 7))
                memt_toks = [("memt", 0), ("memt", 1)] + scr(4, 5, 6, 7)
                stgk = f32(WS_O[2], 512)
                for mt in range(2):
                    k = psnext("g")
                    mm_group(None, [(PS[k][:, :], MEMT(c, 128 * mt, 128), WS(0, c, 0, 512), c == 0, c == 7) for c in range(8)],
                             reads=memt_toks + [("ws", 0)], writes=[("ps", k)])
                    S.op("act", lambda e, k=k: e.copy(stgk, PS[k][:, :]), reads=[("ps", k)], writes=[("ws", 2)])
                    S.op("dve", lambda e, k=k, mt=mt: e.tensor_copy(
                        MVAt[:, mt * 260:(mt + 1) * 260].rearrange("p (h k) -> p h k", k=65)[:, :, 0:64],
                        PS[k][:, 256:512].rearrange("p (h k) -> p h k", k=64)),
                        reads=[("ps", k), "mva_ones"], writes=[("mva", mt)])
                    dma("sp", omk[l, 128 * mt:128 * mt + 128, :], stgk[:, 0:256], reads=[("ws", 2)], key="omk")
                    dma("sp", omv[l, 128 * mt:128 * mt + 128, :], stgk[:, 256:512], reads=[("ws", 2)], key="omv")
                for pr in range(2):
                    k = psnext("g")
                    mm_group(None, [(PS[k][:, 0:256], WS(0, c, 128 * pr, 128), MEMT(c, 0, 256), c == 0, c == 7) for c in range(8)],
                             reads=memt_toks + [("ws", 0)], writes=[("ps", k)])
                    for hh in range(2):
                        h = 2 * pr + hh
                        S.op("act", lambda e, k=k, hh=hh, h=h: e.copy(MKTt[0:64, h * 256:(h + 1) * 256], PS[k][64 * hh:64 * hh + 64, 0:256]),
                             reads=[("ps", k)], writes=[("mkt", h)])
                wload(2, w_in[l][:, 2056:2312], 256)
                for b in range(2):
                    S.op("dve", lambda e, b=b: e.memset(KAS(b, 0, 1040, 64, 65), 1.0), reads=[("ws", 0)], writes=[("ws", 0), ("sc", b, "one")])
                    S.op("dve", lambda e, b=b: e.memset(VAS3(b)[:, :, 64:65], 1.0), reads=[("ws", 0)], writes=[("ws", 0), ("sc", b, "one")])

                S.op("dve", lambda e: e.memset(SM[:, 160:200], 0.0),
                     writes=[("ssq", g, i) for g in (0, 2) for i in range(17)])

                _chk("memkv%d" % l)
                for pr in range(2):
                    for bi, (c0, n) in enumerate(BLKS):
                        k = psnext("g")
                        mm_group(None, [(PS[k][:, 0:n], WS(2, c, 128 * pr, 128), HT(c, c0, n), c == 0, c == 7) for c in range(8)],
                                 reads=ALLHT + [("ws", 2)], writes=[("ps", k)])
                        for hh in range(2):
                            S.op("act", lambda e, k=k, hh=hh, c0=c0, n=n: e.activation(
                                QK(2 * hh, c0, n, 0, 64), PS[k][64 * hh:64 * hh + 64, 0:n], AF.Copy, scale=0.125),
                                reads=[("ps", k)], writes=[("qa", hh, bi)])
                    for hh in range(2):
                        h = 2 * pr + hh
                        for I in range(4):
                            ob = psnext("o")
                            kts = [dict(ka=MKTt[0:128, h * 256 + 128 * j:h * 256 + 128 * j + 128], nk=128,
                                        va=bf(MVA_O + ((j * 4 + h) * 65) * 2, 128), lo=0, hi=512, pv_lo=0,
                                        mask=None, pt="rot", reads=[("mkt", h), ("qa", hh, I), "mkt_zero", ("qkc", 0), ("qkc", 2)],
                                        vreads=[("mva", j), "mva_ones"])
                                   for j in range(2)]
                            osubs = [(PS[ob][:, 65 * s:65 * s + 65], 128 * s, 128 * s + 128, 0, 1) for s in range(4)]
                            attend(lambda c0, n, hh=hh, I=I: QK(2 * hh, 512 * I + c0, n, 0, 128), 128, kts, "fm", ob, "mem")
                            nsq.step(lambda ob=ob, I=I, pr=pr, hh=hh: norm_store_fm(ob, I, 2, 6 + pr, 64 * hh, SM[:, 180 + 4 * I:184 + 4 * I], l))
                        ob = psnext("g")
                        kts = []
                        for b in range(2):
                            dma("pool", KAS(b, 0, 256, 0, 64), cmkT[l, b, h], reads=[("ws", 0)], writes=[("sc", b, "k")], key=("sck", b))
                            dma("pool", VAS3(b)[:, 0:2, 0:64],
                                cmv[l, b].rearrange("(j p) (h d) -> p j h d", p=128, h=4)[:, :, h, :],
                                reads=[("ws", 0)], writes=[("sc", b, "v")], key=("scv", b), slow=True)
                            for j in range(2):
                                kts.append(dict(ka=KAS(b, 128 * j, 128, 0, 64), nk=128, va=VAS(b, j),
                                                lo=16 * b, hi=16 * b + 16, pv_lo=0, mask=None, pt=3 * b + (j % 3),
                                                reads=[("sc", b, "k"), ("qa", hh, 4)], vreads=[("sc", b, "v"), ("sc", b, "one")]))
                        osubs = [(PS[ob][0:32, 0:65], 0, 32, 0, len(kts) - 1)]
                        attend(lambda c0, n, hh=hh: QK(2 * hh, 2048 + c0, n, 0, 64), 64, kts, osubs, ob, "mems")
                        flush_pv()
                        norm_and_store(ob, 0, 32, 16, 2, 6 + pr, 64 * hh, SSMEM(16, 32), l)

                flush_pv()
                nsq.flush()
                _chk("memattn%d" % l)
                wload(2, w_in[l][:, 0:512], 512)
                for pr in range(4):
                    for bi, (c0, n) in enumerate(BLKS):
                        for qk, slot in ((0, 2), (1, 1)):
                            k = psnext("g")
                            mm_group(None, [(PS[k][:, 0:n], WS(slot, c, 128 * pr, 128), HT(c, c0, n), c == 0, c == 7) for c in range(8)],
                                     reads=ALLHT + [("ws", slot)], writes=[("ps", k)])
                            for hh in range(2):
                                if qk == 0:
                                    S.op("act", lambda e, k=k, hh=hh, c0=c0, n=n: e.activation(
                                        QK(2 * hh, c0, n, 0, 64), PS[k][64 * hh:64 * hh + 64, 0:n], AF.Copy, scale=0.125),
                                        reads=[("ps", k)], writes=[("qa", hh, bi)])
                                else:
                                    S.op("dve", lambda e, k=k, hh=hh, c0=c0, n=n: e.tensor_copy(
                                        QK(2 * hh + 1, c0, n, 0, 64), PS[k][64 * hh:64 * hh + 64, 0:n]),
                                        reads=[("ps", k)], writes=[("ka", hh, bi)])
                    for hh in range(2):
                        h = 2 * pr + hh
                        dma("sp", QK(2 * hh, 0, 2048, 64, 65), cscr_p[h:h + 1, 0, :], reads=["cscr_p", ("qkc", 2 * hh)],
                            writes=[("qar", hh, 0)], key=("qar", hh))
                        dma("sp", QK(2 * hh + 1, 0, 2048, 65, 66), cscr_p[h:h + 1, 0, :], reads=["cscr_p", ("qkc", 2 * hh + 1)],
                            writes=[("kar", hh, 0)], key=("kar", hh))
                        dma("sp", QK(2 * hh + 1, 0, 2048, 66, 67), cscr_p[h:h + 1, 1, :], reads=["cscr_p2"],
                            writes=[("kar", hh, 1)], key=("kar", hh))
                        for b in range(2):
                            dma("sp", QK(2 * hh, 2048 + 16 * b, 16, 64, 65), cscr_s[h:h + 1, 0, b, 1024:1040],
                                reads=[("cscr_s", b, 0)], writes=[("qar", hh, 1 + b)], key=("qar", hh))
                            dma("sp", QK(2 * hh + 1, 2048 + 16 * b, 16, 65, 66), cscr_s[h:h + 1, 0, b, 1024:1040],
                                reads=[("cscr_s", b, 0)], writes=[("kar", hh, 2 + b)], key=("kar", hh))
                            dma("sp", QK(2 * hh + 1, 2048 + 16 * b, 16, 66, 67), cscr_s[h:h + 1, 1, b, 1024:1040],
                                reads=[("cscr_s", b, 1)], writes=[("kar", hh, 4 + b)], key=("kar", hh))
                        qar = [("qar", hh, x) for x in range(3)] + [("qkc", 2 * hh)]
                        kar = [("kar", hh, x) for x in range(6)] + [("qkc", 2 * hh + 1)]
                        for I in range(4):
                            ob = psnext("o")
                            kts = []
                            for j in range(4 * I + 4):
                                a = j - 4 * I
                                kts.append(dict(ka=QK(2 * hh + 1, 128 * j, 128, 0, 128), nk=128, va=VA128(j, h),
                                                lo=(128 * a if a >= 0 else 0), hi=512, pv_lo=(128 * a if a >= 0 else 0),
                                                mask=((MASKNEG, 128) if a >= 0 else None), pt="rot",
                                                reads=[("ka", hh, j // 4), ("qa", hh, I)] + qar + kar,
                                                vreads=[("va", j), "va_ones"]))
                            osubs = [(PS[ob][:, 65 * s:65 * s + 65], 128 * s, 128 * s + 128, 0, 4 * I + s) for s in range(4)]
                            attend(lambda c0, n, hh=hh, I=I: QK(2 * hh, 512 * I + c0, n, 0, 128), 128, kts, "fm", ob, "fox")
                            nsq.step(lambda ob=ob, I=I, pr=pr, hh=hh: norm_store_fm(ob, I, 0, pr, 64 * hh, SM[:, 160 + 4 * I:164 + 4 * I], l))
                        ob = psnext("g")
                        kts = []
                        for b in range(2):
                            dma("pool", KAS(b, 0, 1024, 0, 64), ckT[l, b, h], reads=[("ws", 0)], writes=[("sc", b, "k")], key=("sck", b))
                            dma("pool", VAS3(b)[:, :, 0:64],
                                cv[l, b].rearrange("(j p) (h d) -> p j h d", p=128, h=8)[:, :, h, :],
                                reads=[("ws", 0)], writes=[("sc", b, "v")], key=("scv", b), slow=True)
                            dma("sp", KAS(b, 0, 1024, 65, 66), cscr_s[h:h + 1, 0, b, 0:1024], reads=[("cscr_s", b, 0), ("ws", 0)],
                                writes=[("sc", b, "r")], key=("scr", b))
                            dma("sp", KAS(b, 0, 1024, 66, 67), cscr_s[h:h + 1, 1, b, 0:1024], reads=[("cscr_s", b, 1), ("ws", 0)],
                                writes=[("sc", b, "r2")], key=("scr", b))
                            for j in range(8):
                                kts.append(dict(ka=KAS(b, 128 * j, 128), nk=128, va=VAS(b, j),
                                                lo=16 * b, hi=16 * b + 16, pv_lo=0, mask=None, pt=3 * b + (j % 3),
                                                reads=[("sc", b, "k"), ("sc", b, "r"), ("sc", b, "r2"), ("sc", b, "one"), ("qa", hh, 4)] + qar,
                                                vreads=[("sc", b, "v"), ("sc", b, "one")]))
                        kts.append(dict(ka=QK(2 * hh + 1, 2048, 32, 0, 67), nk=32, va=VA(16, h, 32), lo=0, hi=32, pv_lo=0,
                                        mask=(MASKS, 32), pt=6, reads=[("ka", hh, 4), ("qa", hh, 4)] + qar + kar,
                                        vreads=[("va", 16), "va_ones"]))
                        osubs = [(PS[ob][0:32, 0:65], 0, 32, 0, len(kts) - 1)]
                        attend(lambda c0, n, hh=hh: QK(2 * hh, 2048 + c0, n, 0, 67), 67, kts, osubs, ob, "foxs")
                        flush_pv()
                        norm_and_store(ob, 0, 32, 16, 0, pr, 64 * hh, SSFOX(16, 32), l)

                flush_pv()
                nsq.flush()
                _chk("fox%d" % l)
                S.op("act", lambda e: e.activation(SM[:, 100:117], SM[:, 160:177], AF.Ln, bias=EPS, scale=1.0 / 512),
                     reads=[("ssq", 0, i) for i in range(17)], writes=["rfox"])
                S.op("act", lambda e: e.activation(SM[:, 100:117], SM[:, 100:117], AF.Exp, scale=-0.5), reads=["rfox"], writes=["rfox"])
                S.op("act", lambda e: e.activation(SM[:, 140:157], SM[:, 180:197], AF.Ln, bias=EPS, scale=1.0 / 256),
                     reads=[("ssq", 2, i) for i in range(17)], writes=["rmem"])
                S.op("act", lambda e: e.activation(SM[:, 140:157], SM[:, 140:157], AF.Exp, scale=-0.5), reads=["rmem"], writes=["rmem"])

                wload(0, w_out[l][:, 0:512], 512, extra_writes=SC_TOKS)
                wload(1, w_out[l][:, 512:1024], 512)
                GP = SCRf(4, 1024)
                dma("sp", GP, g_post_mix[l:l + 1, :].broadcast_to([128, D]), writes=scr(4, 5, 6, 7), key="gp")
                for i in range(17):
                    rows = tile_rows(i)
                    c0, n = tcols(i)
                    ostg = SCRf(0, 1024, 0, rows)
                    mxr = [("mx", c, i, p0) for c in (0, 1, 2, 3, 6, 7) for p0 in (0, 64)] + [("mx", 4, i), ("mx", 5, i)]
                    last_b = None
                    for hf in range(2):
                        banks = [psnext("w") for _ in range(3)]
                        for gi, (cs, b) in enumerate(zip(((0, 1, 2, 3), (4, 5), (6, 7)), banks)):
                            mm_group(None, [(PS[b][0:rows, :], MX(c, c0, n), WS(hf, c, 0, 512), c == cs[0], c == cs[-1]) for c in cs],
                                     reads=mxr + [("ws", hf)], writes=[("ps", b)])
                        o_h = ostg[:, 512 * hf:512 * hf + 512]
                        S.op("act", lambda e, o_h=o_h, b=banks[0], rows=rows, i=i: e.activation(o_h, PS[b][0:rows, :], AF.Copy, scale=RFOX(i, rows)),
                             reads=[("ps", banks[0]), "rfox"], writes=scr(2 * hf, 2 * hf + 1))
                        S.op("dve", lambda e, o_h=o_h, b=banks[1], rows=rows, i=i: e.scalar_tensor_tensor(o_h, PS[b][0:rows, :], RSGU(i, rows), o_h, ALU.mult, ALU.add),
                             reads=[("ps", banks[1]), ("rsgu", i)] + scr(2 * hf, 2 * hf + 1), writes=scr(2 * hf, 2 * hf + 1))
                        S.op("dve", lambda e, o_h=o_h, b=banks[2], rows=rows, i=i: e.scalar_tensor_tensor(o_h, PS[b][0:rows, :], RMEM(i, rows), o_h, ALU.mult, ALU.add),
                             reads=[("ps", banks[2]), "rmem"] + scr(2 * hf, 2 * hf + 1), writes=scr(2 * hf, 2 * hf + 1))
                        last_b = banks
                    post_norm_residual(ostg, rows, i, bf(WS_O[2], 1024, 0, rows), [("ws", 2)], GP, scr(4, 5, 6, 7), scr(0, 1, 2, 3))

                _chk("wout%d" % l)
                S.barrier()
                ffn_phase(l)
                if l < L - 1:
                    S.barrier()

        except _Stop:
            pass
        S.emit(st)
    return nc


_NC_CACHE = {}


def _prep_inputs(inp):
    f = lambda a: np.ascontiguousarray(np.asarray(a, dtype=np.float32))
    x_prompt = f(inp["x_prompt"]); x_sample = f(inp["x_sample"]); mem_prompt = f(inp["mem_prompt"])
    cfk = f(inp["cache_fox_k"]); cfv = f(inp["cache_fox_v"]); clf = f(inp["cache_fox_logf"])
    cmk = f(inp["cache_mem_k"]); cmvv = f(inp["cache_mem_v"]); cconv = f(inp["cache_ffn_conv"])
    ckT_all = np.ascontiguousarray(cfk.transpose(0, 1, 3, 4, 2))
    cv_all = cfv.reshape(L, 16, 1024, 512)
    clfT_all = np.ascontiguousarray(clf.transpose(0, 1, 3, 2))
    cmkT_all = np.ascontiguousarray(cmk.transpose(0, 1, 3, 4, 2))
    cmv_all = cmvv.reshape(L, 16, 256, 256)
    wsT = np.ascontiguousarray(f(inp["w_spatial"]).transpose(0, 1, 3, 2))
    fm8 = lambda g: f(g).reshape(L, 8, 128).transpose(0, 2, 1)
    b_sp = f(inp["b_spatial"])
    w_dw = f(inp["w_dwconv"]); b_dw = f(inp["b_dwconv"]); b_f = f(inp["b_forget"])
    cst = np.zeros((128, 288), dtype=ml_dtypes.bfloat16)
    cst[:, 0:128] = np.eye(128, dtype=np.float32).astype(ml_dtypes.bfloat16)
    kk, qq = np.meshgrid(np.arange(128), np.arange(128), indexing="ij")
    cst[:, 128:256] = np.where(kk <= qq, 0.0, NEG).astype(ml_dtypes.bfloat16)
    k2, q2 = np.meshgrid(np.arange(32), np.arange(32), indexing="ij")
    ok = (k2 // 16 == q2 // 16) & (k2 % 16 <= q2 % 16)
    cst[0:32, 256:288] = np.where(ok, 0.0, NEG).astype(ml_dtypes.bfloat16)
    wu = f(inp["w_up"])
    wu_a = wu[:, :, 0:FF].reshape(L, 8, 128, NJ, 128)
    wu_l = wu[:, :, FF:2 * FF].reshape(L, 8, 128, NJ, 128)
    w_up_r = np.ascontiguousarray(np.concatenate([wu_a, wu_l], axis=4).transpose(0, 3, 2, 1, 4))
    shared = dict(
        w_in=f(inp["w_in"]), w_mkv=f(inp["w_mem_kv"]), w_out=f(inp["w_out"]), w_up=w_up_r,
        w_dn=f(inp["w_down"]), g_post_mix=f(inp["g_post_mix"]), g_post_ffn=f(inp["g_post_ffn"]),
        g_sgu=f(inp["g_sgu"]), wsT=wsT, cst=cst)
    gpm = fm8(inp["g_pre_mix"]); ggo = fm8(inp["g_group_out"]); gmem = fm8(inp["g_mem"]); gpf = fm8(inp["g_pre_ffn"])
    in_maps = []
    for c in range(NCORES):
        prm = np.zeros((128, NPRM), dtype=np.float32)
        for l in range(L):
            o = l * PPL
            prm[:, o + PO["gpm"]:o + PO["gpm"] + 8] = gpm[l]
            prm[:, o + PO["ggo"]:o + PO["ggo"] + 8] = ggo[l]
            prm[:, o + PO["gmem"]:o + PO["gmem"] + 8] = gmem[l]
            prm[:, o + PO["gpf"]:o + PO["gpf"] + 8] = gpf[l]
            prm[:, o + PO["bs"]:o + PO["bs"] + 4] = b_sp[l].T
            prm[0:16, o + PO["bss"]:o + PO["bss"] + 4] = b_sp[l][:, 0:16].T
            prm[16:32, o + PO["bss"]:o + PO["bss"] + 4] = b_sp[l][:, 0:16].T
            prm[:, o + PO["wdw"]:o + PO["wdw"] + 66] = w_dw[l].reshape(3, NJ, 128).transpose(2, 1, 0).reshape(128, 66)
            prm[:, o + PO["bdw"]:o + PO["bdw"] + 22] = b_dw[l].reshape(NJ, 128).T
            prm[0:8, o + PO["bf"]] = b_f[l]
            cc = cconv[l, 2 * c:2 * c + 2]
            prm[:, o + PO["cconv"]:o + PO["cconv"] + 88] = cc.reshape(2, 2, NJ, 128).transpose(3, 2, 0, 1).reshape(128, 88)
        m = dict(shared)
        m.update(
            x_p=x_prompt[c], x_s=x_sample[2 * c:2 * c + 2].reshape(32, D), mem=mem_prompt[c],
            ckT=np.ascontiguousarray(ckT_all[:, 2 * c:2 * c + 2]), cv=np.ascontiguousarray(cv_all[:, 2 * c:2 * c + 2]),
            clfT=np.ascontiguousarray(clfT_all[:, 2 * c:2 * c + 2]), cmkT=np.ascontiguousarray(cmkT_all[:, 2 * c:2 * c + 2]),
            cmv=np.ascontiguousarray(cmv_all[:, 2 * c:2 * c + 2]), prm=prm)
        in_maps.append(m)
    return in_maps


def kernel(**inp):
    in_maps = _prep_inputs(inp)
    if "nc" not in _NC_CACHE:
        _NC_CACHE["nc"] = build()
    nc = _NC_CACHE["nc"]
    res = run_bass_kernel_spmd(nc, in_maps, core_ids=list(range(NCORES)))
    R = res.results
    cat = lambda name: np.stack([np.asarray(r[name], dtype=np.float32) for r in R], axis=0)
    y_p = cat("y_p")
    y_s = cat("y_s").reshape(16, 16, D)
    fk = cat("ofk").transpose(1, 0, 2, 3).reshape(L, 8, 2048, 8, 64)
    fv = cat("ofv").transpose(1, 0, 2, 3).reshape(L, 8, 2048, 8, 64)
    lf = cat("olf").transpose(1, 0, 3, 2)
    mk = cat("omk").transpose(1, 0, 2, 3).reshape(L, 8, 256, 4, 64)
    mv = cat("omv").transpose(1, 0, 2, 3).reshape(L, 8, 256, 4, 64)
    cvp = cat("oconv").transpose(1, 0, 4, 3, 2).reshape(L, 8, 2, FF)
    sk = cat("osk").transpose(1, 0, 2, 3).reshape(L, 16, 16, 8, 64)
    sv = cat("osv").transpose(1, 0, 2, 3).reshape(L, 16, 16, 8, 64)
    slf = cat("oslf").transpose(1, 0, 3, 2).reshape(L, 16, 16, 8)
    gv = cat("ogv").transpose(1, 0, 2, 3).reshape(L, 16, 16, 256)
    cvs = cat("osconv").transpose(1, 0, 4, 5, 3, 2).reshape(L, 16, 2, FF)
    outs = (y_p, y_s, fk, fv, lf, mk, mv, cvp, sk, sv, slf, gv, cvs)
    return tuple(np.ascontiguousarray(o, dtype=np.float32) for o in outs)
```

```python
import numpy as np
import ml_dtypes
from contextlib import ExitStack
import concourse.bass as bass
import concourse.mybir as mybir
from concourse.bass_utils import run_bass_kernel_spmd

F32 = mybir.dt.float32
BF16 = mybir.dt.bfloat16
ALU = mybir.AluOpType
AF = mybir.ActivationFunctionType

ENGS = ("pe", "act", "dve", "pool", "sp")
STOP = None
DBG = set()


class _Stop(Exception):
    pass


def _chk(name):
    if STOP == name:
        raise _Stop()
NCORES = 8
L = 2
D = 1024
NT = 2080
FF = 2816
NJ = 22
EPS = 1e-6
NEG = -30000.0


class Op:
    __slots__ = ("eng", "fn", "idx", "deps", "dma_key", "marked", "seq", "cum")

    def __init__(self, eng, fn, idx, dma_key):
        self.eng = eng
        self.fn = fn
        self.idx = idx
        self.deps = ()
        self.dma_key = dma_key
        self.marked = False
        self.seq = 0
        self.cum = 0


class Sched:
    def __init__(self, nc):
        self.nc = nc
        self.streams = {e: [] for e in ENGS}
        self.last_writer = {}
        self.readers = {}
        self.dma_counts = {}
        self.dma_last = {}

    def op(self, eng, fn, reads=(), writes=(), dma_key=None):
        o = Op(eng, fn, len(self.streams[eng]), dma_key)
        deps = set()
        lw = self.last_writer
        rd = self.readers
        for t in reads:
            w = lw.get(t)
            if w is not None:
                deps.add(w)
            if type(t) is tuple and t[0] == "ps":
                for r in rd.get(t, ()):
                    if r.eng != eng:
                        deps.add(r)
        for t in writes:
            w = lw.get(t)
            if w is not None:
                deps.add(w)
            r = rd.get(t)
            if r:
                deps.update(r)
        for t in reads:
            rd.setdefault(t, []).append(o)
        for t in writes:
            lw[t] = o
            rd[t] = []
        deps.discard(o)
        if eng == "pe":
            deps = {d for d in deps if not (d.eng == "pe" and d.dma_key is None)}
        o.deps = deps
        if dma_key is not None:
            c = self.dma_counts.get(dma_key, 0) + 1
            self.dma_counts[dma_key] = c
            o.cum = 16 * c
            self.dma_last[dma_key] = o
        self.streams[eng].append(o)
        return o

    def barrier(self):
        lasts = [s[-1] for s in self.streams.values() if s]
        lasts += list(self.dma_last.values())
        for e in ENGS:
            o = Op(e, lambda eng: eng.nop(), len(self.streams[e]), None)
            o.deps = {d for d in lasts if not (d.eng == e and d.dma_key is None)}
            self.streams[e].append(o)

    def emit(self, stack):
        nc = self.nc
        for e in ENGS:
            for o in self.streams[e]:
                for d in o.deps:
                    if d.dma_key is None:
                        d.marked = True
        sems = {}
        for e in ENGS:
            n = 0
            for o in self.streams[e]:
                if o.dma_key is None and o.marked:
                    n += 1
                    o.seq = n
            if n:
                sems[e] = stack.enter_context(nc.semaphore("s_" + e))
        dsems = {}
        for k in self.dma_counts:
            dsems[k] = stack.enter_context(nc.semaphore("d%d" % len(dsems)))
        self.n_sems = len(sems) + len(dsems)
        final = {k: o.cum for k, o in self.dma_last.items()}

        def run(eng_name, eng):
            waited = {}
            for o in self.streams[eng_name]:
                need = {}
                for d in o.deps:
                    if d.dma_key is not None:
                        key = ("d", d.dma_key)
                        val = d.cum
                    else:
                        key = ("c", d.eng)
                        val = d.seq
                    if val > need.get(key, 0):
                        need[key] = val
                for key, val in need.items():
                    if waited.get(key, 0) >= val:
                        continue
                    waited[key] = val
                    s = dsems[key[1]] if key[0] == "d" else sems[key[1]]
                    eng.wait_ge(s, val)
                ins = o.fn(eng)
                if o.dma_key is not None:
                    ins.then_inc(dsems[o.dma_key], 16)
                elif o.marked:
                    ins.then_inc(sems[o.eng], 1)
            if eng_name == "sp":
                for k, v in final.items():
                    if waited.get(("d", k), 0) < v:
                        eng.wait_ge(dsems[k], v)

        with nc.Block() as block:
            @block.tensor
            def _(t):
                run("pe", t)

            @block.scalar
            def _(t):
                run("act", t)

            @block.vector
            def _(t):
                run("dve", t)

            @block.gpsimd
            def _(t):
                run("pool", t)

            @block.sync
            def _(t):
                run("sp", t)


def tile_rows(i):
    return 128 if i < 16 else 32


def tcols(i):
    return (128 * i, 128) if i < 16 else (2048, 32)


BLKS = [(0, 512), (512, 512), (1024, 512), (1536, 512), (2048, 32)]

PO = {}
_o = 0
for _n, _w in (("gpm", 8), ("ggo", 8), ("gmem", 8), ("gpf", 8), ("bs", 4), ("bss", 4),
               ("wdw", 66), ("bdw", 22), ("bf", 1), ("cconv", 88)):
    PO[_n] = _o
    _o += _w
PPL = _o
NPRM = PPL * L


def build():
    nc = bass.Bass("TRN2", target_bir_lowering=False)

    def din(name, shape, dt=F32):
        return nc.dram_tensor(name, list(shape), dt, kind="ExternalInput").ap()

    def dout(name, shape):
        return nc.dram_tensor(name, list(shape), F32, kind="ExternalOutput").ap()

    x_p = din("x_p", [2048, D])
    x_s = din("x_s", [32, D])
    mem = din("mem", [256, D])
    ckT = din("ckT", [L, 2, 8, 64, 1024])
    cv = din("cv", [L, 2, 1024, 512])
    clfT = din("clfT", [L, 2, 8, 1024])
    cmkT = din("cmkT", [L, 2, 4, 64, 256])
    cmv = din("cmv", [L, 2, 256, 256])
    w_in = din("w_in", [L, D, 2312])
    w_mkv = din("w_mkv", [L, D, 512])
    w_out = din("w_out", [L, D, D])
    w_up = din("w_up", [L, NJ, 128, 8, 256])
    w_dn = din("w_dn", [L, FF, D])
    g_post_mix = din("g_post_mix", [L, D])
    g_post_ffn = din("g_post_ffn", [L, D])
    g_sgu = din("g_sgu", [L, 256])
    wsT = din("wsT", [L, 4, 128, 128])
    prm_h = din("prm", [128, NPRM])
    cst_h = din("cst", [128, 288], BF16)

    y_p = dout("y_p", [2048, D])
    y_s = dout("y_s", [32, D])
    ofk = dout("ofk", [L, 2048, 512])
    ofv = dout("ofv", [L, 2048, 512])
    olf = dout("olf", [L, 8, 2048])
    omk = dout("omk", [L, 256, 256])
    omv = dout("omv", [L, 256, 256])
    oconv = dout("oconv", [L, 128, NJ, 2])
    osk = dout("osk", [L, 32, 512])
    osv = dout("osv", [L, 32, 512])
    oslf = dout("oslf", [L, 8, 32])
    ogv = dout("ogv", [L, 32, 256])
    osconv = dout("osconv", [L, 128, NJ, 2, 2])

    cscr_p = nc.dram_tensor("cscr_p", [8, 2, 2048], BF16).ap()
    cscr_s = nc.dram_tensor("cscr_s", [8, 2, 2, 1040], BF16).ap()

    with ExitStack() as st:
        S = Sched(nc)
        X = st.enter_context(nc.sbuf_tensor("X", [128, 17, D], F32))
        PRM = st.enter_context(nc.sbuf_tensor("PRM", [128, NPRM], F32))
        CST = st.enter_context(nc.sbuf_tensor("CST", [128, 288], BF16))
        WSTSt = st.enter_context(nc.sbuf_tensor("WSTS", [32, L * 4 * 32], BF16))
        PTSt = st.enter_context(nc.sbuf_tensor("PTS", [128, 7 * 32], BF16))
        SM = st.enter_context(nc.sbuf_tensor("SM", [128, 256], F32))
        ONEROW = st.enter_context(nc.sbuf_tensor("ONEROW", [128, 64], F32))
        remaining = nc.sbuf_bytes_remaining
        remaining = remaining() if callable(remaining) else remaining
        ARB = (remaining - 512) // 64 * 64
        AR = st.enter_context(nc.sbuf_tensor("AR", [128, ARB // 2], BF16))
        AR32 = AR.bitcast(F32)
        PS = [st.enter_context(nc.psum_tensor("ps%d" % k, [128, 512], F32)) for k in range(8)]
        PSB = [p.bitcast(BF16) for p in PS]

        IDENT = CST[:, 0:128]
        MASKNEG = CST[:, 128:256]
        MASKS = CST[0:32, 256:288]


        def WSTS(l, g):
            o = (l * 4 + g) * 32
            return WSTSt[0:32, o:o + 32]

        def prm(l, name, w, p0=0, p1=128, off=0):
            o = l * PPL + PO[name] + off
            return PRM[p0:p1, o:o + w]

        SM_SS, SM_SD, SM_RSTD = 0, 1, 2
        SM_R = 16
        SM_SSF = 70
        SM_T = 90

        def bf(off, n, p0=0, p1=128):
            return AR[p0:p1, off // 2: off // 2 + n]

        def f32(off, n, p0=0, p1=128):
            return AR32[p0:p1, off // 4: off // 4 + n]

        cur = [0]

        def carve(nbytes):
            o = cur[0]
            cur[0] = o + (nbytes + 63) // 64 * 64
            return o

        HT_O = carve(8 * NT * 2)
        MX_O = carve(8 * NT * 2)
        VA_O = carve(17 * 8 * 65 * 2)
        QK_O = carve(4 * NT * 2)
        WS_O = [carve(8320) for _ in range(3)]
        SCR_O = carve(8192)
        MKT_O = carve(2048)
        MVA_O = carve(1040)
        MKTt = AR[:, MKT_O // 2:MKT_O // 2 + 1024]
        MVAt = AR[:, MVA_O // 2:MVA_O // 2 + 520]
        MIX_END = cur[0]
        assert MIX_END <= ARB, (MIX_END, ARB)

        def HT(c, c0, n):
            return bf(HT_O + (c * NT + c0) * 2, n)

        def HT3(c0, n):
            return AR[:, HT_O // 2: HT_O // 2 + 8 * NT].rearrange("p (c t) -> p c t", c=8)[:, :, c0:c0 + n]

        def MX(c, c0, n, p0=0, p1=128):
            return bf(MX_O + (c * NT + c0) * 2, n, p0, p1)

        def VA(j, h, nk=128):
            return bf(VA_O + ((j * 8 + h) * 65) * 2, 65, 0, nk)

        def VA128(j, h):
            return bf(VA_O + ((j * 8 + h) * 65) * 2, 128, 0, 128)

        def QK(slot, c0, n, p0, p1):
            return bf(QK_O + (slot * NT + c0) * 2, n, p0, p1)

        CT = lambda c0, n: f32(QK_O + c0 * 4, n, 96, 104)
        HI = lambda c0, n: bf(QK_O + 8320 + c0 * 2, n, 96, 104)
        LO = lambda c0, n: bf(QK_O + 8320 + 4160 + c0 * 2, n, 96, 104)

        def WS(s, c, c0, n):
            return bf(WS_O[s] + (c * 520 + c0) * 2, n)

        def WS3(s, ncol):
            return AR[:, WS_O[s] // 2: WS_O[s] // 2 + 8 * 520].rearrange("p (c n) -> p c n", c=8)[:, :, 0:ncol]

        def SCRb(g, n, p0=0, p1=128, off=0):
            return bf(SCR_O + g * 1024 + off * 2, n, p0, p1)

        def SCRf(g, n, p0=0, p1=128, off=0):
            return f32(SCR_O + g * 1024 + off * 4, n, p0, p1)

        def scr(*gs):
            return [("scr", g) for g in gs]

        def dma(eng, out, in_, reads=(), writes=(), key=None, slow=False):
            if slow:
                fn = lambda e: e.dma_start(out=out, in_=in_, allow_slow_non_contiguous=True)
            else:
                fn = lambda e: e.dma_start(out=out, in_=in_)
            return S.op(eng, fn, reads=reads, writes=writes, dma_key=key)

        psr = {"g": [0, 1], "s": [2, 3, 4], "o": [5, 6], "t": [7]}
        psi = {k: 0 for k in psr}

        def psnext(pool):
            lst = psr[pool]
            k = lst[psi[pool] % len(lst)]
            psi[pool] += 1
            return k

        def mm_group(out_fn, items, reads, writes):
            def fn(e):
                ins = None
                for (o, a, b, s0, s1) in items:
                    ins = e.matmul(o, a, b, start=s0, stop=s1, skip_group_check=True)
                return ins
            return S.op("pe", fn, reads=reads, writes=writes)

        dma("sp", PRM[:, :], prm_h, writes=["prm"], key="prm")
        dma("sp", CST[:, :], cst_h, writes=["cst"], key="cst")
        S.op("dve", lambda e: e.memset(WSTSt[:, :], 0.0), writes=["wsts"])
        for l in range(L):
            for b in range(2):
                dma("pool",
                    WSTSt[16 * b:16 * b + 16, l * 128:(l + 1) * 128].rearrange("p (g i) -> p g i", g=4)[:, :, 16 * b:16 * b + 16],
                    wsT[l, :, 0:16, 0:16].rearrange("g j i -> j g i"),
                    reads=["wsts"], writes=[("wsts", l, b)], key=("wsts", l), slow=True)
        S.op("dve", lambda e: e.memset(PTSt[:, :], 0.0), writes=["pts%d" % q for q in range(7)])
        S.op("dve", lambda e: e.memset(SM[:, 200:201], 1.0), writes=["one"])
        S.op("dve", lambda e: e.memset(ONEROW[:, :], 1.0), writes=["onerow"])
        ONE8 = SM[96:104, 200:201]

        for i in range(16):
            dma("sp", X[:, i, :], x_p[128 * i:128 * i + 128, :], writes=[("x", i)], key=("x", i))
        dma("sp", X[0:32, 16, :], x_s, writes=[("x", 16)], key=("x", 16))

        def prenorm_tile(src, rows, xtok, gcol, dst3, dst_tok, slot, extra_writes=(), hb=None, tpool="t", hb_toks=None, defer=False):
            if hb is None:
                hb = SCRb(2 * slot, 1024, 0, rows)
            hbt = list(hb_toks) if hb_toks is not None else scr(2 * slot, 2 * slot + 1)
            c0 = 3 * slot
            ss = SM[0:rows, c0:c0 + 1]
            sd = SM[0:rows, c0 + 1:c0 + 2]
            rs = SM[0:rows, c0 + 2:c0 + 3]
            S.op("act", lambda e: e.activation(hb, src, AF.Square, accum_out=ss),
                 reads=[xtok], writes=hbt + [("sm", c0)])
            S.op("act", lambda e: e.activation(sd, ss, AF.Ln, bias=EPS, scale=1.0 / D),
                 reads=[("sm", c0)], writes=[("sm", c0 + 1)])
            S.op("act", lambda e: e.activation(rs, sd, AF.Exp, scale=-0.5), reads=[("sm", c0 + 1)], writes=[("sm", c0 + 2)])
            S.op("act", lambda e: e.activation(hb, src, AF.Copy, scale=rs),
                 reads=[xtok, ("sm", c0 + 2)], writes=hbt)
            def part_b():
                k = psnext(tpool)

                def tr(e):
                    ins = None
                    for c in range(8):
                        ins = e.transpose(PSB[k][:, c * 128:c * 128 + rows], hb[:, c * 128:(c + 1) * 128],
                                          IDENT[0:rows, 0:rows])
                    return ins
                S.op("pe", tr, reads=hbt + ["cst"], writes=[("ps", k)])
                src3 = PSB[k][:, 0:1024].rearrange("p (c t) -> p c t", c=8)[:, :, 0:rows]
                S.op("dve", lambda e: e.tensor_tensor(dst3, src3, gcol.unsqueeze(2).to_broadcast([128, 8, rows]), ALU.mult),
                     reads=[("ps", k), "prm"], writes=[dst_tok] + list(extra_writes))
            if defer:
                return part_b
            part_b()

        PVQ = []

        def flush_pv():
            while PVQ:
                PVQ.pop(0)()

        def attend(qa_fn, K, ktiles, osubs, obank, tag):
            pendq = []
            first_done = [False]

            def pv(kt_i, kt, pt_ap, pt_tok):
                items = []
                if osubs == "fm":
                    lo_, hi_ = kt["lo"], kt["hi"]
                    mrows = kt["va"].shape[1]
                    items.append((PS[obank][0:mrows, lo_:hi_], kt["va"], pt_ap[:, lo_:hi_], kt_i == 0, kt_i == len(ktiles) - 1))
                    mm_group(None, items, reads=[pt_tok] + kt["vreads"], writes=[("ps", obank)])
                    return
                for (o_ap, c0, c1, fk, lk) in osubs:
                    if fk <= kt_i <= lk and kt["pv_lo"] <= c0:
                        items.append((o_ap, pt_ap[:, c0:c1], kt["va"], not first_done[0], kt_i == lk))
                        first_done[0] = True
                if items:
                    mm_group(None, items, reads=[pt_tok] + kt["vreads"], writes=[("ps", obank)])

            for kt_i, kt in enumerate(ktiles):
                k = psnext("s")
                nk, lo, hi = kt["nk"], kt["lo"], kt["hi"]
                items = []
                if kt["mask"] is not None:
                    mask_ap, mc = kt["mask"]
                    items.append((PS[k][0:nk, lo:lo + mc], kt["ka"], qa_fn(lo, mc), True, False))
                    items.append((PS[k][0:nk, lo:lo + mc], IDENT[0:nk, 0:nk], mask_ap, False, True))
                    if hi > lo + mc:
                        items.append((PS[k][0:nk, lo + mc:hi], kt["ka"], qa_fn(lo + mc, hi - lo - mc), True, True))
                else:
                    items.append((PS[k][0:nk, lo:hi], kt["ka"], qa_fn(lo, hi - lo), True, True))
                mm_group(None, items, reads=kt["reads"] + ["cst"], writes=[("ps", k)])
                if kt["pt"] == "rot":
                    g = attend.ptc % 4
                    attend.ptc += 1
                    pt_ap = SCRb(g, 512, 0, nk)
                    pt_tok = ("scr", g)
                else:
                    pt_ap = PTSt[0:nk, 32 * kt["pt"]:32 * kt["pt"] + 32]
                    pt_tok = "pts%d" % kt["pt"]
                S.op("act", lambda e, pt_ap=pt_ap, k=k, nk=nk, lo=lo, hi=hi:
                     e.activation(pt_ap[:, lo:hi], PS[k][0:nk, lo:hi], AF.Exp),
                     reads=[("ps", k)], writes=[pt_tok])
                while len(PVQ) >= 2:
                    PVQ.pop(0)()
                PVQ.append(lambda kt_i=kt_i, kt=kt, pt_ap=pt_ap, pt_tok=pt_tok: pv(kt_i, kt, pt_ap, pt_tok))
        attend.ptc = 0

        def cs_all():
            return ["cw"]

        def post_norm_residual(ostg, rows, i, junk, junk_toks, GPap, gp_toks, o_toks):
            ssc = SM[0:rows, SM_T + 8:SM_T + 9]
            S.op("act", lambda e: e.activation(junk, ostg, AF.Square, accum_out=ssc),
                 reads=o_toks, writes=list(junk_toks) + ["pn"])
            S.op("act", lambda e: e.activation(ssc, ssc, AF.Ln, bias=EPS, scale=1.0 / D), reads=["pn"], writes=["pn"])
            S.op("act", lambda e: e.activation(ssc, ssc, AF.Exp, scale=-0.5), reads=["pn"], writes=["pn"])
            S.op("dve", lambda e: e.scalar_tensor_tensor(ostg, ostg, ssc, GPap[0:rows, :], ALU.mult, ALU.mult),
                 reads=list(o_toks) + list(gp_toks) + ["pn"], writes=o_toks)
            S.op("pool", lambda e: e.tensor_tensor(X[0:rows, i, :], X[0:rows, i, :], ostg, ALU.add),
                 reads=list(o_toks) + [("x", i)], writes=[("x", i)])

        def ffn_phase(l):
            psr.clear()
            psr.update({"u": [0, 1, 2, 4, 5, 6], "t": [3], "d": [4, 5, 6, 7]})
            for kk in psr:
                psi[kk] = 0
            HC = 1056
            o = [0]

            def cv_(nb):
                r = o[0]
                o[0] = r + (nb + 63) // 64 * 64
                return r
            H2_O = cv_(8 * HC * 2)
            ACT_O = cv_(NJ * HC * 2)
            WDN_O = cv_(NJ * 1024 * 2)
            WU_O = [cv_(4096) for _ in range(2)]
            AE_O = [cv_(2064) for _ in range(2)]
            LIN_O = [cv_(1024) for _ in range(2)]
            CONV_O = [cv_(2048) for _ in range(2)]
            OSTG_O = cv_(4096)
            GPF_O = cv_(4096)
            HBF_O = cv_(2048)
            SAVE_O = cv_(NJ * 2 * 4)
            SAVES_O = cv_(NJ * 4 * 4)
            assert o[0] <= ARB, (o[0], ARB)

            H2 = lambda c, c0, n: bf(H2_O + (c * HC + c0) * 2, n)
            H23 = lambda c0, n: AR[:, H2_O // 2:H2_O // 2 + 8 * HC].rearrange("p (c t) -> p c t", c=8)[:, :, c0:c0 + n]
            ACTB = lambda j, c0, n: bf(ACT_O + (j * HC + c0) * 2, n)
            WDN = lambda j, c0, n: bf(WDN_O + (j * 1024 + c0) * 2, n)
            WDN3 = AR[:, WDN_O // 2:WDN_O // 2 + NJ * 1024].rearrange("p (j n) -> p j n", j=NJ)
            WU = lambda s, c, c0, n: bf(WU_O[s] + (c * 256 + c0) * 2, n)
            WU3 = lambda s: AR[:, WU_O[s] // 2:WU_O[s] // 2 + 2048].rearrange("p (c n) -> p c n", c=8)
            OSTG = f32(OSTG_O, 1024)
            GPF = f32(GPF_O, 1024)
            HBF = bf(HBF_O, 1024)
            SAVE = f32(SAVE_O, NJ * 2)
            SAVES = f32(SAVES_O, NJ * 4)

            dma("sp", GPF, g_post_ffn[l:l + 1, :].broadcast_to([128, D]), writes=["gpf"], key="gp")
            wdn_src = w_dn[l].rearrange("(j p) n -> p j n", p=128)
            for part in range(2):
                dma("pool", WDN3[:, 11 * part:11 * part + 11, :], wdn_src[:, 11 * part:11 * part + 11, :],
                    writes=[("wdn", part)], key=("wdn", part))
            aecnt = 0

            def wu_load(j, s_):
                dma("pool", WU3(s_), w_up[l, j], writes=[("wu", s_, 0), ("wu", s_, 1)], key=("wu", s_, 0))

            for half in range(2):
                tiles = list(range(0, 8)) if half == 0 else list(range(8, 17))
                base = 0 if half == 0 else 1024
                blocks = [(0, 512), (512, 512)] + ([(1024, 32)] if half == 1 else [])
                h2toks = [("h2", (i if i < 8 else i - 8)) for i in tiles]
                if half == 0:
                    HB2 = bf(AE_O[0], 1024)

                    def pa(i):
                        c0, n = tcols(i)
                        if i % 2 == 0:
                            return prenorm_tile(X[:, i, :], 128, ("x", i), prm(l, "gpf", 8), H23(c0, n), ("h2", i), 0,
                                                hb=HBF[:, :], tpool="t", hb_toks=["hbf"], defer=True)
                        return prenorm_tile(X[:, i, :], 128, ("x", i), prm(l, "gpf", 8), H23(c0, n), ("h2", i), 1,
                                            hb=HB2, tpool="t", hb_toks=[("ae", 0), ("aeh", 0), ("aet", 0)], defer=True)
                    pbq = {0: pa(0)}
                    for i in tiles:
                        if i + 1 < 8:
                            pbq[i + 1] = pa(i + 1)
                        pbq.pop(i)()
                wu_load(0, 0)
                pend_tail = None
                for j in range(NJ):
                    s = j % 2
                    if j + 1 < NJ:
                        wu_load(j + 1, (j + 1) % 2)
                    w0 = prm(l, "wdw", 1, 0, 128, 3 * j)
                    w1 = prm(l, "wdw", 1, 0, 128, 3 * j + 1)
                    w2 = prm(l, "wdw", 1, 0, 128, 3 * j + 2)
                    bd = prm(l, "bdw", 1, 0, 128, j)
                    for bidx, (lc0, n) in enumerate(blocks):
                        sample = (n == 32)
                        ka = psnext("u")
                        mm_group(None, [(PS[ka][:, 0:n], WU(s, c, 0, 128), H2(c, lc0, n), c == 0, c == 7) for c in range(8)],
                                 reads=h2toks + [("wu", s, 0)], writes=[("ps", ka)])
                        kl = psnext("u")
                        mm_group(None, [(PS[kl][:, 0:n], WU(s, c, 128, 128), H2(c, lc0, n), c == 0, c == 7) for c in range(8)],
                                 reads=h2toks + [("wu", s, 1)], writes=[("ps", kl)])
                        p = aecnt % 2
                        aecnt += 1
                        lin = bf(LIN_O[p], 512)
                        conv = f32(CONV_O[p], 512)
                        if sample:
                            AEs = f32(AE_O[p], 36).rearrange("p (b t) -> p b t", b=2)
                            S.op("dve", lambda e, AEs=AEs, j=j: e.tensor_copy(
                                AEs[:, :, 0:2], prm(l, "cconv", 4, 0, 128, 4 * j).rearrange("p (b r) -> p b r", b=2)),
                                reads=["prm"], writes=[("aeh", p)])
                            S.op("act", lambda e, AEs=AEs, ka=ka: e.copy(AEs[:, :, 2:18], PS[ka][:, 0:32].rearrange("p (b t) -> p b t", b=2)),
                                 reads=[("ps", ka)], writes=[("ae", p)])
                            taps = [AEs[:, :, k:k + 16] for k in range(3)]
                            cv3 = conv[:, 0:32].rearrange("p (b t) -> p b t", b=2)
                            psa = PS[ka][:, 0:32].rearrange("p (b t) -> p b t", b=2)
                            sil_out = f32(AE_O[p] + 256, 32)
                            sil_in = conv[:, 0:32]
                        else:
                            AEp = f32(AE_O[p], 514)
                            if bidx == 0 and half == 0:
                                S.op("dve", lambda e, AEp=AEp: e.memset(AEp[:, 0:2], 0.0), writes=[("aeh", p)])
                            elif bidx == 0:
                                S.op("dve", lambda e, AEp=AEp, j=j: e.tensor_copy(AEp[:, 0:2], SAVE[:, 2 * j:2 * j + 2]),
                                     reads=[("save", j)], writes=[("aeh", p)])
                            else:
                                prev = f32(AE_O[1 - p], 514)
                                S.op("dve", lambda e, AEp=AEp, prev=prev: e.tensor_copy(AEp[:, 0:2], prev[:, 512:514]),
                                     reads=[("aet", 1 - p)], writes=[("aeh", p)])
                            S.op("act", lambda e, AEp=AEp, ka=ka: e.copy(AEp[:, 2:514], PS[ka][:, 0:512]),
                                 reads=[("ps", ka)], writes=[("ae", p), ("aet", p)])
                            taps = [AEp[:, k:k + 512] for k in range(3)]
                            cv3 = conv
                            psa = PS[ka][:, 0:512]
                            sil_out = AEp[:, 0:512]
                            sil_in = conv
                        S.op("act", lambda e, cv3=cv3, psa=psa, w2=w2, bd=bd: e.activation(cv3, psa, AF.Identity, bias=bd, scale=w2),
                             reads=[("ps", ka), "prm"], writes=[("conv", p)])
                        S.op("act", lambda e, lin=lin, kl=kl, n=n: e.copy(lin[:, 0:n], PS[kl][:, 0:n]),
                             reads=[("ps", kl)], writes=[("lin", p)])
                        S.op("dve", lambda e, cv3=cv3, taps=taps, w1=w1: e.scalar_tensor_tensor(cv3, taps[1], w1, cv3, ALU.mult, ALU.add),
                             reads=[("ae", p), ("aeh", p), ("conv", p)], writes=[("conv", p)])
                        S.op("dve", lambda e, cv3=cv3, taps=taps, w0=w0: e.scalar_tensor_tensor(cv3, taps[0], w0, cv3, ALU.mult, ALU.add),
                             reads=[("ae", p), ("aeh", p), ("conv", p)], writes=[("conv", p)])
                        if not sample and bidx == 1:
                            S.op("dve", lambda e, AEp=AEp, j=j: e.tensor_copy(SAVE[:, 2 * j:2 * j + 2], AEp[:, 512:514]),
                                 reads=[("aet", p)], writes=[("save", j)])
                        if sample:
                            S.op("dve", lambda e, AEs=AEs, j=j: e.tensor_copy(
                                SAVES[:, 4 * j:4 * j + 4].rearrange("p (b r) -> p b r", b=2), AEs[:, :, 16:18]),
                                reads=[("ae", p)], writes=[("saves", j)])

                        def tail(p=p, sil_out=sil_out, sil_in=sil_in, j=j, lc0=lc0, n=n, lin=lin, half=half):
                            S.op("act", lambda e: e.activation(sil_out, sil_in, AF.Silu),
                                 reads=[("conv", p), ("ae", p), ("aeh", p)], writes=[("ae", p), ("aeh", p)])
                            S.op("pool", lambda e: e.tensor_tensor(ACTB(j, lc0, n), sil_out, lin[:, 0:n], ALU.mult),
                                 reads=[("ae", p), ("aeh", p), ("lin", p)], writes=[("actb", j, half)])
                        if pend_tail is not None:
                            pend_tail()
                        pend_tail = tail
                if pend_tail is not None:
                    pend_tail()
                    pend_tail = None
                if half == 1:
                    dma("sp", oconv[l], SAVE.rearrange("p (j r) -> p j r", r=2), reads=[("save", j) for j in range(NJ)], key="oconv")
                    dma("sp", osconv[l], SAVES.rearrange("p (j b r) -> p j b r", b=2, r=2), reads=[("saves", j) for j in range(NJ)], key="osconv")
                atoks = [("actb", j, half) for j in range(NJ)]
                nxt = list(range(8, 17)) if half == 0 else []
                for ti_, i in enumerate(tiles):
                    rows = tile_rows(i)
                    c0, n = tcols(i)
                    lc0 = c0 - base
                    defer_b = []
                    for i2 in (nxt[ti_:ti_ + 1] if ti_ < 7 else nxt[7:8]):
                        rows2 = tile_rows(i2)
                        c02, n2 = tcols(i2)
                        defer_b.append(prenorm_tile(X[0:rows2, i2, :], rows2, ("x", i2), prm(l, "gpf", 8), H23(c02 - 1024, n2),
                                                    ("h2", i2 - 8), 0, hb=HBF[0:rows2, :], tpool="t", hb_toks=["hbf"], defer=True))
                        if "nodefer" in DBG:
                            defer_b.pop()()
                    for hf in range(2):
                        kd = psnext("d")
                        mm_group(None, [(PS[kd][0:rows, :], ACTB(j, lc0, n), WDN(j, 512 * hf, 512), j == 0, j == NJ - 1) for j in range(NJ)],
                                 reads=atoks + [("wdn", 0), ("wdn", 1)], writes=[("ps", kd)])
                        if hf == 0:
                            S.op("act", lambda e, kd=kd, rows=rows: e.copy(OSTG[0:rows, 0:512], PS[kd][0:rows, :]),
                                 reads=[("ps", kd)], writes=["ostg0"])
                        else:
                            S.op("dve", lambda e, kd=kd, rows=rows: e.tensor_copy(OSTG[0:rows, 512:1024], PS[kd][0:rows, :]),
                                 reads=[("ps", kd)], writes=["ostg1"])
                    for fb in defer_b:
                        fb()
                    post_norm_residual(OSTG[0:rows, :], rows, i, HBF[0:rows, :], ["hbf"], GPF, ["gpf"], ["ostg0", "ostg1"])
                    if l == L - 1:
                        if i < 16:
                            dma("sp", y_p[128 * i:128 * i + 128, :], X[:, i, :], reads=[("x", i)], key=("x", i))
                        else:
                            dma("sp", y_s, X[0:32, 16, :], reads=[("x", 16)], key=("x", 16))
                if half == 0:
                    prenorm_tile(X[0:32, 16, :], 32, ("x", 16), prm(l, "gpf", 8), H23(1024, 32), ("h2", 8), 0,
                                 hb=HBF[0:32, :], tpool="t", hb_toks=["hbf"])
        RFOX = lambda i, rows=128: SM[0:rows, 100 + i:101 + i]
        RSGU = lambda i, rows=128: SM[0:rows, 120 + i:121 + i]
        RMEM = lambda i, rows=128: SM[0:rows, 140 + i:141 + i]
        SSFOX = lambda i, rows=128: SM[0:rows, 160 + i:161 + i]
        SSMEM = lambda i, rows=128: SM[0:rows, 180 + i:181 + i]
        ALLHT = [("ht", i) for i in range(17)]

        class NSQ:
            def __init__(self):
                self.p1 = None
                self.p2 = None
                self.p3 = None

            def step(self, new_p1):
                if self.p3 is not None:
                    self.p3()
                    self.p3 = None
                if self.p2 is not None:
                    self.p3 = self.p2()
                    self.p2 = None
                if self.p1 is not None:
                    self.p2 = self.p1()
                self.p1 = new_p1

            def flush(self):
                self.step(None)
                self.step(None)
                self.step(None)
        nsq = NSQ()

        def wload(slot, src2d, ncol, extra_writes=()):
            dma("pool", WS3(slot, ncol), src2d.rearrange("(c p) n -> p c n", p=128),
                writes=[("ws", slot)] + list(extra_writes), key=("ws", slot))

        def KAS(b, c0, n, p0=0, p1=67):
            return bf(WS_O[0] + b * 3200 + c0 * 2, n, p0, p1)

        def VAS(b, j):
            return bf(WS_O[0] + b * 3200 + 2080 + j * 130, 65)

        def VAS3(b):
            return AR[:, (WS_O[0] + b * 3200 + 2080) // 2:(WS_O[0] + b * 3200 + 2080) // 2 + 520].rearrange("p (j k) -> p j k", k=65)

        SC_TOKS = [("sc", b, x) for b in range(2) for x in ("k", "v", "r", "one")]

        def norm_store_fm(obank, I, grp, chunk, p0, ssq4, l):
            rlrow = f32(SCR_O + 6 * 1024, 512, 64, 65)
            hfT = f32(SCR_O + 6 * 1024, 512, 0, 64)
            S.op("act", lambda e: e.activation(rlrow, PS[obank][64:65, 0:512], AF.Ln), reads=[("ps", obank)], writes=[("scr", 6, "r")])
            S.op("act", lambda e: e.activation(rlrow, rlrow, AF.Exp, scale=-1.0), reads=[("scr", 6, "r")], writes=[("scr", 6, "r")])
            kb = psnext("g")
            S.op("pe", lambda e: e.matmul(PS[kb][0:64, 0:512], ONEROW[64:65, 0:64], rlrow, start=True, stop=True, skip_group_check=True),
                 reads=[("scr", 6, "r"), "onerow"], writes=[("ps", kb)])
            S.op("dve", lambda e: e.tensor_copy(hfT, PS[kb][0:64, 0:512]), reads=[("ps", kb)], writes=scr(6, 7))
            S.op("dve", lambda e: e.tensor_tensor(hfT, PS[obank][0:64, 0:512], hfT, ALU.mult), reads=[("ps", obank)] + scr(6, 7), writes=scr(6, 7))
            return lambda: norm_store_fm2(I, grp, chunk, p0, ssq4, l)

        def norm_store_fm2(I, grp, chunk, p0, ssq4, l):
            hfT = f32(SCR_O + 6 * 1024, 512, 0, 64)
            sqb = bf(SCR_O + 4 * 1024, 512, 0, 64)
            S.op("act", lambda e: e.activation(MX(chunk, 512 * I, 512, p0, p0 + 64), hfT, AF.Copy,
                                               scale=prm(l, "ggo", 1, p0, p0 + 64, chunk)),
                 reads=scr(6, 7) + ["prm"], writes=[("mx", chunk, 4 * I + s_, p0) for s_ in range(4)])
            S.op("act", lambda e: e.activation(sqb, hfT, AF.Square), reads=scr(6, 7), writes=scr(4))
            return lambda: norm_store_fm3(I, grp, ssq4)

        def norm_store_fm3(I, grp, ssq4):
            sqb = bf(SCR_O + 4 * 1024, 512, 0, 64)
            kt = psnext("t")
            ones_col = bf(VA_O + 64 * 2, 1, 0, 64)

            def ssmm(e):
                ins = None
                for s_ in range(4):
                    ins = e.matmul(PS[kt][:, s_:s_ + 1], sqb[:, 128 * s_:128 * s_ + 128], ones_col, start=True, stop=True,
                                   skip_group_check=True)
                return ins
            S.op("pe", ssmm, reads=scr(4) + ["va_ones"], writes=[("ps", kt)])
            toks4 = [("ssq", grp, 4 * I + s_) for s_ in range(4)]
            S.op("dve", lambda e: e.tensor_tensor(ssq4, ssq4, PS[kt][:, 0:4], ALU.add), reads=[("ps", kt)] + toks4, writes=toks4)

        def norm_and_store_block(obank, I, grp, chunk, p0, ssq4, l):
            O4 = PS[obank][:, 0:260].rearrange("p (s k) -> p s k", k=65)
            rl4 = SM[:, SM_T + 10:SM_T + 14]
            sq4 = SM[:, SM_T + 14:SM_T + 18]
            S.op("dve", lambda e: e.reciprocal(rl4.unsqueeze(2), O4[:, :, 64:65]), reads=[("ps", obank)], writes=["rl4"])
            hf4 = SCRf(6, 256)
            hb4 = SCRb(7, 256)
            S.op("dve", lambda e: e.tensor_tensor(hf4.rearrange("p (s k) -> p s k", k=64), O4[:, :, 0:64],
                                                  rl4.unsqueeze(2).to_broadcast([128, 4, 64]), ALU.mult),
                 reads=[("ps", obank), "rl4"], writes=scr(6))

            def sqs(e):
                ins = None
                for s_ in range(4):
                    ins = e.activation(hb4[:, 64 * s_:64 * s_ + 64], hf4[:, 64 * s_:64 * s_ + 64], AF.Square,
                                       accum_out=sq4[:, s_:s_ + 1])
                return ins
            S.op("act", sqs, reads=scr(6), writes=scr(7) + ["sq4"])
            toks4 = [("ssq", grp, 4 * I + s_) for s_ in range(4)]
            S.op("dve", lambda e: e.tensor_tensor(ssq4, ssq4, sq4, ALU.add), reads=["sq4"] + toks4, writes=toks4)
            S.op("act", lambda e: e.copy(hb4, hf4), reads=scr(6, 7), writes=scr(7))
            kt = psnext("t")

            def tr(e):
                ins = None
                for s_ in range(4):
                    ins = e.transpose(PSB[kt][p0:p0 + 64, 128 * s_:128 * s_ + 128], hb4[:, 64 * s_:64 * s_ + 64], IDENT[:, :])
                return ins
            S.op("pe", tr, reads=scr(7) + ["cst"], writes=[("ps", kt)])
            S.op("dve", lambda e: e.tensor_scalar(MX(chunk, 512 * I, 512, p0, p0 + 64), PSB[kt][p0:p0 + 64, 0:512],
                                                  prm(l, "ggo", 1, p0, p0 + 64, chunk), None, ALU.mult),
                 reads=[("ps", kt), "prm"], writes=[("mx", chunk, 4 * I + s_, p0) for s_ in range(4)])

        def norm_and_store(obank, ocol, rows, tile_i, grp, chunk, p0, ssq_acc, l):
            rl = SM[0:rows, SM_T + 4:SM_T + 5]
            S.op("dve", lambda e: e.reciprocal(rl, PS[obank][0:rows, ocol + 64:ocol + 65]),
                 reads=[("ps", obank)], writes=["rl"])
            hf = SCRf(5, 64, 0, rows)
            hb = SCRb(5, 64, 0, rows, off=256)
            S.op("dve", lambda e: e.tensor_scalar(hf, PS[obank][0:rows, ocol:ocol + 64], rl, None, ALU.mult),
                 reads=[("ps", obank), "rl"], writes=scr(5))
            sq = SM[0:rows, SM_T + 5:SM_T + 6]
            S.op("act", lambda e: e.activation(hb, hf, AF.Square, accum_out=sq), reads=scr(5), writes=scr(5) + ["sq"])
            S.op("dve", lambda e: e.tensor_tensor(ssq_acc, ssq_acc, sq, ALU.add),
                 reads=["sq", ("ssq", grp, tile_i)], writes=[("ssq", grp, tile_i)])
            S.op("act", lambda e: e.copy(hb, hf), reads=scr(5), writes=scr(5))
            kt = psnext("t")
            c0, n = tcols(tile_i)
            S.op("pe", lambda e: e.transpose(PSB[kt][p0:p0 + 64, 0:rows], hb, IDENT[0:rows, 0:rows]),
                 reads=scr(5) + ["cst"], writes=[("ps", kt)])
            S.op("dve", lambda e: e.tensor_scalar(MX(chunk, c0, n, p0, p0 + 64), PSB[kt][p0:p0 + 64, 0:rows],
                                                  prm(l, "ggo", 1, p0, p0 + 64, chunk), None, ALU.mult),
                 reads=[("ps", kt), "prm"], writes=[("mx", chunk, tile_i, p0)])

        try:
            for l in range(L):
                psr.clear()
                psr.update({"g": [0, 1], "s": [2, 3, 4], "o": [5, 6], "t": [7], "w": [0, 1, 2, 3, 4, 5]})
                for kk in psr:
                    psi[kk] = 0
                S.op("dve", lambda e: e.memset(
                    AR[:, VA_O // 2: VA_O // 2 + 17 * 8 * 65].rearrange("p (a k) -> p a k", k=65)[:, :, 64:65], 1.0),
                    writes=["va_ones"])
                S.op("dve", lambda e: e.memset(MVAt[:, :].rearrange("p (a k) -> p a k", k=65)[:, :, 64:65], 1.0),
                     writes=["mva_ones"])
                S.op("dve", lambda e: e.memset(MKTt[64:128, :], 0.0), writes=["mkt_zero"])
                wload(0, w_in[l][:, 1024:1536], 512, extra_writes=SC_TOKS if l > 0 else ())
                wload(1, w_in[l][:, 512:1024], 512)
                wload(2, w_in[l][:, 1536:2056], 520)

                SGC = MX_O + 12288
                WSTl = bf(SGC, 512)
                GSl = f32(SGC + 1024, 256)
                dma("pool", WSTl.rearrange("p (g i) -> p g i", g=4), wsT[l].rearrange("g j i -> j g i"),
                    writes=[("sguc", 0)], key="wst")
                S.op("dve", lambda e: e.memset(bf(SGC, 512, 64, 128).rearrange("p (g i) -> p g i", g=4)[:, :, 0:64], 0.0),
                     reads=[("sguc", 0)], writes=[("sguc", 0)])
                dma("sp", GSl, g_sgu[l:l + 1, :].broadcast_to([128, 256]), writes=[("sguc", 1)], key="gs")

                def sgu_ops(i):
                    st_ = 0 if "setA" in DBG else i % 2
                    base = MX_O if st_ == 0 else MX_O + 6144
                    tk = "sguA" if st_ == 0 else "sguB"
                    Sb = lambda g, n, p0=0, p1=128: bf(base + g * 1024, n, p0, p1)
                    Sf = lambda g, n, p0=0, p1=128: f32(base + g * 1024, n, p0, p1)
                    sc = lambda *gs: [(tk, g) for g in gs]
                    rows = tile_rows(i)
                    c0, n = tcols(i)
                    s1, s2 = [], []
                    k = psnext("s")
                    z = Sf(0, 512, 0, rows)
                    t1 = Sf(2, 512, 0, rows)
                    ssv = SM[0:rows, SM_T + 20 + 2 * st_:SM_T + 21 + 2 * st_]
                    sss = SM[0:rows, SM_T + 21 + 2 * st_:SM_T + 22 + 2 * st_]
                    ssvt, ssst = ("ssv", st_), ("sss", st_)
                    vvf = Sf(3, 256, 0, rows)
                    vvb = Sb(2, 256, 0, rows)
                    sg = Sf(4, 256, 0, rows)
                    sgb = Sb(5, 256, 0, rows)
                    s1.append(lambda: mm_group(None, [(PS[k][0:rows, :], HT(c, c0, n), WS(2, c, 8, 512), c == 0, c == 7) for c in range(8)],
                                               reads=[("ht", i), ("ws", 2)], writes=[("ps", k)]))
                    s1.append(lambda: S.op("act", lambda e: e.copy(z, PS[k][0:rows, :]), reads=[("ps", k)], writes=sc(0, 1)))
                    s1.append(lambda: S.op("dve", lambda e: e.tensor_tensor(t1, z, z, ALU.mult), reads=sc(0, 1), writes=sc(2, 3)))
                    s1.append(lambda: S.op("dve", lambda e: e.tensor_scalar(t1, t1, 0.044715, 1.0, ALU.mult, ALU.add), reads=sc(2, 3), writes=sc(2, 3)))
                    s1.append(lambda: S.op("dve", lambda e: e.tensor_tensor(t1, t1, z, ALU.mult), reads=sc(0, 1, 2, 3), writes=sc(2, 3)))
                    s1.append(lambda: S.op("act", lambda e: e.activation(t1, t1, AF.Exp, scale=-1.5957691216057308), reads=sc(2, 3), writes=sc(2, 3)))
                    s1.append(lambda: S.op("act", lambda e: e.activation(t1, t1, AF.Ln, bias=1.0, scale=1.0), reads=sc(2, 3), writes=sc(2, 3)))
                    s1.append(lambda: S.op("act", lambda e: e.activation(t1, t1, AF.Exp, scale=-1.0), reads=sc(2, 3), writes=sc(2, 3)))
                    s1.append(lambda: S.op("dve", lambda e: e.tensor_tensor(z, z, t1, ALU.mult), reads=sc(0, 1, 2, 3), writes=sc(0, 1)))
                    s1.append(lambda: S.op("act", lambda e: e.activation(t1[:, 0:256], z[:, 256:512], AF.Square, accum_out=ssv),
                                           reads=sc(0, 1), writes=sc(2) + [ssvt]))
                    s1.append(lambda: S.op("act", lambda e: e.activation(ssv, ssv, AF.Ln, bias=EPS, scale=1.0 / 256), reads=[ssvt], writes=[ssvt]))
                    s1.append(lambda: S.op("act", lambda e: e.activation(ssv, ssv, AF.Exp, scale=-0.5), reads=[ssvt], writes=[ssvt]))
                    s1.append(lambda: S.op("dve", lambda e: e.scalar_tensor_tensor(vvf, z[:, 256:512], ssv, GSl[0:rows, :], ALU.mult, ALU.mult),
                                           reads=sc(0, 1) + [("sguc", 1)] + [ssvt], writes=sc(3)))
                    s1.append(lambda: S.op("act", lambda e: e.copy(vvb, vvf), reads=sc(3), writes=sc(2)))
                    if i == 16:
                        s1.append(lambda: dma("sp", ogv[l], vvf, reads=sc(3), key="ogv"))
                    k2 = psnext("s")
                    if i < 16:
                        items = [(PS[k2][0:128, 64 * g:64 * g + 64], WSTl[:, 128 * g:128 * g + 128], vvb[:, 64 * g:64 * g + 64], True, True) for g in range(4)]
                        bsap = prm(l, "bs", 4)
                        wtok = [("sguc", 0)]
                    else:
                        items = [(PS[k2][0:32, 64 * g:64 * g + 64], WSTS(l, g), vvb[:, 64 * g:64 * g + 64], True, True) for g in range(4)]
                        bsap = prm(l, "bss", 4, 0, 32)
                        wtok = ["wsts"] + [("wsts", l, b) for b in range(2)]
                    s1.append(lambda: mm_group(None, items, reads=sc(2) + wtok, writes=[("ps", k2)]))
                    s2.append(lambda: S.op("dve", lambda e: e.tensor_tensor(
                        sg.rearrange("p (g d) -> p g d", g=4), PS[k2][0:rows, 0:256].rearrange("p (g d) -> p g d", g=4),
                        bsap.unsqueeze(2).to_broadcast([rows, 4, 64]), ALU.add),
                        reads=[("ps", k2), "prm"], writes=sc(4)))
                    s2.append(lambda: S.op("dve", lambda e: e.tensor_tensor(sg, sg, z[:, 0:256], ALU.mult), reads=sc(0, 1, 4), writes=sc(4)))
                    s2.append(lambda: S.op("act", lambda e: e.activation(sgb, sg, AF.Square, accum_out=sss),
                                           reads=sc(4), writes=sc(5) + [ssst]))
                    s2.append(lambda: S.op("act", lambda e: e.activation(sss, sss, AF.Ln, bias=EPS, scale=1.0 / 256), reads=[ssst], writes=[ssst]))
                    s2.append(lambda: S.op("act", lambda e: e.activation(RSGU(i, rows), sss, AF.Exp, scale=-0.5),
                                           reads=[ssst], writes=[("rsgu", i)]))
                    s2.append(lambda: S.op("act", lambda e: e.copy(sgb, sg), reads=sc(4, 5), writes=sc(5)))

                    def trs():
                        kt = psnext("o")

                        def tr(e):
                            ins = None
                            for cc in range(2):
                                ins = e.transpose(PSB[kt][:, cc * 128:cc * 128 + rows], sgb[:, cc * 128:(cc + 1) * 128], IDENT[0:rows, 0:rows])
                            return ins
                        S.op("pe", tr, reads=sc(5) + ["cst"], writes=[("ps", kt)])
                        for cc in range(2):
                            S.op("dve", lambda e, cc=cc, l=l: e.tensor_scalar(
                                MX(4 + cc, c0, n), PSB[kt][:, cc * 128:cc * 128 + rows], prm(l, "ggo", 1, 0, 128, 4 + cc), None, ALU.mult),
                                reads=[("ps", kt), "prm"], writes=[("mx", 4 + cc, i)])
                    s2.append(trs)
                    return s1, s2

                def interleave(a, b):
                    for q in range(max(len(a), len(b))):
                        if q < len(a):
                            a[q]()
                        if q < len(b):
                            b[q]()

                def vk_tile(i):
                    rows = tile_rows(i)
                    c0, n = tcols(i)
                    for which, slot, oprompt, osample, sg in (("v", 0, ofv, osv, 4), ("k", 1, ofk, osk, 6)):
                        k = psnext("g")
                        mm_group(None, [(PS[k][0:rows, :], HT(c, c0, n), WS(slot, c, 0, 512), c == 0, c == 7)
                                        for c in range(8)],
                                 reads=[("ht", i), ("ws", slot)], writes=[("ps", k)])
                        stg = SCRf(sg, 512, 0, rows)
                        S.op("act", lambda e, stg=stg, k=k, rows=rows: e.copy(stg, PS[k][0:rows, :]),
                             reads=[("ps", k)], writes=scr(sg, sg + 1))
                        if which == "v":
                            va3 = AR[0:rows, VA_O // 2 + i * 520: VA_O // 2 + (i + 1) * 520].rearrange(
                                "p (h k) -> p h k", k=65)[:, :, 0:64]
                            S.op("dve", lambda e, va3=va3, k=k, rows=rows: e.tensor_copy(
                                va3, PS[k][0:rows, :].rearrange("p (h k) -> p h k", k=64)),
                                reads=[("ps", k), "va_ones"], writes=[("va", i)])
                        dst = oprompt[l, 128 * i:128 * i + 128, :] if i < 16 else osample[l, :, :]
                        dma("sp", dst, stg, reads=scr(sg, sg + 1), key=("stg", sg))

                sgu_pipe = {"B": [], "C": []}

                def sgu_step(t):
                    if t is not None:
                        s1_, s2_ = sgu_ops(t)
                        sA, sB, sC = s1_[:-1], [s1_[-1]] + s2_[:-1], [s2_[-1]]
                    else:
                        sA, sB, sC = [], [], []
                    interleave(sA, sgu_pipe["B"])
                    for f_ in sgu_pipe["C"]:
                        f_()
                    sgu_pipe["C"] = sgu_pipe["B_c"] if "B_c" in sgu_pipe else []
                    sgu_pipe["B"] = sB
                    sgu_pipe["B_c"] = sC

                def pn_a(i):
                    rows = tile_rows(i)
                    c0, n = tcols(i)
                    return prenorm_tile(X[0:rows, i, :], rows, ("x", i), prm(l, "gpm", 8), HT3(c0, n), ("ht", i), i % 2, defer=True)
                pn_b = {0: pn_a(0)}
                for i in range(17):
                    if i + 1 < 17:
                        pn_b[i + 1] = pn_a(i + 1)
                    pn_b.pop(i)()
                    if i >= 1:
                        vk_tile(i - 1)
                        sgu_step(i - 1)
                vk_tile(16)
                sgu_step(16)
                sgu_step(None)
                sgu_step(None)
                S.op("dve", lambda e: e.memset(bf(MX_O + 15 * 1024, 2), 0.0),
                     writes=[("sguA", g) for g in range(6)] + [("sguB", g) for g in range(6)] + [("sguc", 0), ("sguc", 1)]
                     + [("mx", c, i, p0) for c in (0, 1, 2, 3) for i in range(17) for p0 in (0, 64)])

                _chk("vk%d" % l)
                NB = SM[0:8, 210:211]
                S.op("dve", lambda e, l=l: e.tensor_scalar(NB, prm(l, "bf", 1, 0, 8), -1.0, None, ALU.mult),
                     reads=["prm"], writes=["cw", "nb"])
                for bi, (c0, n) in enumerate(BLKS):
                    k = psnext("g")
                    mm_group(None, [(PS[k][0:8, 0:n], WS(2, c, 0, 8), HT(c, c0, n), c == 0, c == 7) for c in range(8)],
                             reads=ALLHT + [("ws", 2)], writes=["cw", ("ps", k)])
                    tmp = SCRf(0, 512, 96, 104)
                    S.op("act", lambda e, k=k, n=n, tmp=tmp: e.activation(tmp[:, 0:n], PS[k][0:8, 0:n], AF.Exp, bias=NB, scale=-1.0),
                         reads=[("ps", k), "nb"], writes=scr(0, 1))
                    S.op("act", lambda e, n=n, tmp=tmp: e.activation(tmp[:, 0:n], tmp[:, 0:n], AF.Ln, bias=1.0, scale=1.0),
                         reads=scr(0, 1), writes=scr(0, 1))
                    S.op("dve", lambda e, c0=c0, n=n, tmp=tmp: e.tensor_scalar(CT(c0, n), tmp[:, 0:n], -1.0, None, ALU.mult),
                         reads=scr(0, 1), writes=["cw", ("ct", bi)])
                dma("sp", olf[l], CT(0, 2048), reads=[("ct", b) for b in range(4)], writes=["cw"], key="olf")
                dma("sp", oslf[l], CT(2048, 32), reads=[("ct", 4)], writes=["cw"], key="olf2")
                SLF = SM[96:104, 220:252]
                S.op("dve", lambda e: e.tensor_copy(SLF, CT(2048, 32)), reads=[("ct", 4)], writes=["cw", "slf"])
                S.op("dve", lambda e: e.tensor_tensor_scan(CT(0, 2048), ONE8.to_broadcast([8, 2048]), CT(0, 2048), 0.0, ALU.mult, ALU.add),
                     reads=[("ct", b) for b in range(4)] + ["one"], writes=["cw", "ctp"])
                S.op("dve", lambda e: e.tensor_copy(HI(0, 2048), CT(0, 2048)), reads=["ctp"], writes=["cw", "hi"])
                S.op("dve", lambda e: e.tensor_tensor(CT(0, 2048), CT(0, 2048), HI(0, 2048), ALU.subtract),
                     reads=["ctp", "hi"], writes=["cw", "ctp"])
                S.op("dve", lambda e: e.tensor_copy(LO(0, 2048), CT(0, 2048)), reads=["ctp"], writes=["cw", "lo"])
                dma("sp", cscr_p[:, 0, :], HI(0, 2048), reads=["hi"], writes=["cw", "cscr_p"], key="cscr0")
                dma("sp", cscr_p[:, 1, :], LO(0, 2048), reads=["lo"], writes=["cw", "cscr_p2"], key="cscr1")
                CS = lambda b, c0, n: CT(b * 1040 + c0, n)
                for b in range(2):
                    dma("sp", CS(b, 0, 1024), clfT[l, b], reads=["ctp", "lo", ("ct", 4), "slf"], writes=["cw", ("cs", b)], key=("csl", b))
                    S.op("dve", lambda e, b=b: e.tensor_copy(CS(b, 1024, 16), SM[96:104, 220 + 16 * b:236 + 16 * b]),
                         reads=["slf", "ctp", "lo", ("ct", 4)], writes=["cw", ("cs2", b)])
                    S.op("dve", lambda e, b=b: e.tensor_tensor_scan(CS(b, 0, 1040), ONE8.to_broadcast([8, 1040]), CS(b, 0, 1040), 0.0, ALU.mult, ALU.add),
                         reads=[("cs", b), ("cs2", b), "one"], writes=["cw", ("csc", b)])
                S.op("dve", lambda e: e.tensor_copy(HI(0, 2080), CT(0, 2080)),
                     reads=[("csc", 0), ("csc", 1), "cscr_p", "cscr_p2"], writes=["cw", "hi"])
                S.op("dve", lambda e: e.tensor_tensor(CT(0, 2080), CT(0, 2080), HI(0, 2080), ALU.subtract),
                     reads=["hi"], writes=["cw", ("csc", 0), ("csc", 1)])
                S.op("dve", lambda e: e.tensor_copy(LO(0, 2080), CT(0, 2080)), reads=[("csc", 0), ("csc", 1), "cscr_p2"], writes=["cw", "lo"])
                for b in range(2):
                    dma("sp", cscr_s[:, 0, b, :], HI(b * 1040, 1040), reads=["hi"], writes=["cw", ("cscr_s", b, 0)], key=("cscrs", b, 0))
                    dma("sp", cscr_s[:, 1, b, :], LO(b * 1040, 1040), reads=["lo"], writes=["cw", ("cscr_s", b, 1)], key=("cscrs", b, 1))

                _chk("fg%d" % l)
                _chk("sgu%d" % l)
                S.op("dve", lambda e: e.memset(bf(QK_O, 4 * NT, 64, 128), 0.0),
                     writes=["cw", "hi", "lo"] + [("qkc", s_) for s_ in range(4)])
                for s_ in (0, 2):
                    S.op("dve", lambda e, s_=s_: e.memset(QK(s_, 0, NT, 64, 67), -1.0), writes=[("qkc", s_)])
                for s_ in (1, 3):
                    S.op("dve", lambda e, s_=s_: e.memset(QK(s_, 0, NT, 64, 65), 1.0), writes=[("qkc", s_)])
                wload(0, w_mkv[l], 512)
                MEMT3 = AR[:, (SCR_O + 4096) // 2:(SCR_O + 4096) // 2 + 2048].rearrange("p (c t) -> p c t", c=8)
                MEMT = lambda c, c0, n: bf(SCR_O + 4096 + (c * 256 + c0) * 2, n)
                memx = f32(WS_O[2], 1024)
                for mt in range(2):
                    dma("sp", memx, mem[128 * mt:128 * mt + 128, :], writes=[("ws", 2)], key=("ws", 2))
                    prenorm_tile(memx, 128, ("ws", 2), prm(l, "gmem", 8), MEMT3[:, :, 128 * mt:128 * mt + 128],
                                 ("memt", mt), 0, extra_writes=scr(4, 5, 6, 7))
                memt_toks = [("memt", 0), ("memt", 1)] + scr(4, 5, 6, 7)
                stgk = f32(WS_O[2], 512)
                for mt in range(2):
                    k = psnext("g")
                    mm_group(None, [(PS[k][:, :], MEMT(c, 128 * mt, 128), WS(0, c, 0, 512), c == 0, c == 7) for c in range(8)],
                             reads=memt_toks + [("ws", 0)], writes=[("ps", k)])
                    S.op("act", lambda e, k=k: e.copy(stgk, PS[k][:, :]), reads=[("ps", k)], writes=[("ws", 2)])
                    S.op("dve", lambda e, k=k, mt=mt: e.tensor_copy(
                        MVAt[:, mt * 260:(mt + 1) * 260].rearrange("p (h k) -> p h k", k=65)[:, :, 0:64],
                        PS[k][:, 256:512].rearrange("p (h k) -> p h k", k=64)),
                        reads=[("ps", k), "mva_ones"], writes=[("mva", mt)])
                    dma("sp", omk[l, 128 * mt:128 * mt + 128, :], stgk[:, 0:256], reads=[("ws", 2)], key="omk")
                    dma("sp", omv[l, 128 * mt:128 * mt + 128, :], stgk[:, 256:512], reads=[("ws", 2)], key="omv")
                for pr in range(2):
                    k = psnext("g")
                    mm_group(None, [(PS[k][:, 0:256], WS(0, c, 128 * pr, 128), MEMT(c, 0, 256), c == 0, c == 7) for c in range(8)],
                             reads=memt_toks + [("ws", 0)], writes=[("ps", k)])
                    for hh in range(2):
                        h = 2 * pr + hh
                        S.op("act", lambda e, k=k, hh=hh, h=h: e.copy(MKTt[0:64, h * 256:(h + 1) * 256], PS[k][64 * hh:64 * hh + 64, 0:256]),
                             reads=[("ps", k)], writes=[("mkt", h)])
                wload(2, w_in[l][:, 2056:2312], 256)
                for b in range(2):
                    S.op("dve", lambda e, b=b: e.memset(KAS(b, 0, 1040, 64, 65), 1.0), reads=[("ws", 0)], writes=[("ws", 0), ("sc", b, "one")])
                    S.op("dve", lambda e, b=b: e.memset(VAS3(b)[:, :, 64:65], 1.0), reads=[("ws", 0)], writes=[("ws", 0), ("sc", b, "one")])

                S.op("dve", lambda e: e.memset(SM[:, 160:200], 0.0),
                     writes=[("ssq", g, i) for g in (0, 2) for i in range(17)])

                _chk("memkv%d" % l)
                for pr in range(2):
                    for bi, (c0, n) in enumerate(BLKS):
                        k = psnext("g")
                        mm_group(None, [(PS[k][:, 0:n], WS(2, c, 128 * pr, 128), HT(c, c0, n), c == 0, c == 7) for c in range(8)],
                                 reads=ALLHT + [("ws", 2)], writes=[("ps", k)])
                        for hh in range(2):
                            S.op("act", lambda e, k=k, hh=hh, c0=c0, n=n: e.activation(
                                QK(2 * hh, c0, n, 0, 64), PS[k][64 * hh:64 * hh + 64, 0:n], AF.Copy, scale=0.125),
                                reads=[("ps", k)], writes=[("qa", hh, bi)])
                    for hh in range(2):
                        h = 2 * pr + hh
                        for I in range(4):
                            ob = psnext("o")
                            kts = [dict(ka=MKTt[0:128, h * 256 + 128 * j:h * 256 + 128 * j + 128], nk=128,
                                        va=bf(MVA_O + ((j * 4 + h) * 65) * 2, 128), lo=0, hi=512, pv_lo=0,
                                        mask=None, pt="rot", reads=[("mkt", h), ("qa", hh, I), "mkt_zero", ("qkc", 0), ("qkc", 2)],
                                        vreads=[("mva", j), "mva_ones"])
                                   for j in range(2)]
                            osubs = [(PS[ob][:, 65 * s:65 * s + 65], 128 * s, 128 * s + 128, 0, 1) for s in range(4)]
                            attend(lambda c0, n, hh=hh, I=I: QK(2 * hh, 512 * I + c0, n, 0, 128), 128, kts, "fm", ob, "mem")
                            nsq.step(lambda ob=ob, I=I, pr=pr, hh=hh: norm_store_fm(ob, I, 2, 6 + pr, 64 * hh, SM[:, 180 + 4 * I:184 + 4 * I], l))
                        ob = psnext("g")
                        kts = []
                        for b in range(2):
                            dma("pool", KAS(b, 0, 256, 0, 64), cmkT[l, b, h], reads=[("ws", 0)], writes=[("sc", b, "k")], key=("sck", b))
                            dma("pool", VAS3(b)[:, 0:2, 0:64],
                                cmv[l, b].rearrange("(j p) (h d) -> p j h d", p=128, h=4)[:, :, h, :],
                                reads=[("ws", 0)], writes=[("sc", b, "v")], key=("scv", b), slow=True)
                            for j in range(2):
                                kts.append(dict(ka=KAS(b, 128 * j, 128, 0, 64), nk=128, va=VAS(b, j),
                                                lo=16 * b, hi=16 * b + 16, pv_lo=0, mask=None, pt=3 * b + (j % 3),
                                                reads=[("sc", b, "k"), ("qa", hh, 4)], vreads=[("sc", b, "v"), ("sc", b, "one")]))
                        osubs = [(PS[ob][0:32, 0:65], 0, 32, 0, len(kts) - 1)]
                        attend(lambda c0, n, hh=hh: QK(2 * hh, 2048 + c0, n, 0, 64), 64, kts, osubs, ob, "mems")
                        flush_pv()
                        norm_and_store(ob, 0, 32, 16, 2, 6 + pr, 64 * hh, SSMEM(16, 32), l)

                flush_pv()
                nsq.flush()
                _chk("memattn%d" % l)
                wload(2, w_in[l][:, 0:512], 512)
                for pr in range(4):
                    for bi, (c0, n) in enumerate(BLKS):
                        for qk, slot in ((0, 2), (1, 1)):
                            k = psnext("g")
                            mm_group(None, [(PS[k][:, 0:n], WS(slot, c, 128 * pr, 128), HT(c, c0, n), c == 0, c == 7) for c in range(8)],
                                     reads=ALLHT + [("ws", slot)], writes=[("ps", k)])
                            for hh in range(2):
                                if qk == 0:
                                    S.op("act", lambda e, k=k, hh=hh, c0=c0, n=n: e.activation(
                                        QK(2 * hh, c0, n, 0, 64), PS[k][64 * hh:64 * hh + 64, 0:n], AF.Copy, scale=0.125),
                                        reads=[("ps", k)], writes=[("qa", hh, bi)])
                                else:
                                    S.op("dve", lambda e, k=k, hh=hh, c0=c0, n=n: e.tensor_copy(
                                        QK(2 * hh + 1, c0, n, 0, 64), PS[k][64 * hh:64 * hh + 64, 0:n]),
                                        reads=[("ps", k)], writes=[("ka", hh, bi)])
                    for hh in range(2):
                        h = 2 * pr + hh
                        dma("sp", QK(2 * hh, 0, 2048, 64, 65), cscr_p[h:h + 1, 0, :], reads=["cscr_p", ("qkc", 2 * hh)],
                            writes=[("qar", hh, 0)], key=("qar", hh))
                        dma("sp", QK(2 * hh + 1, 0, 2048, 65, 66), cscr_p[h:h + 1, 0, :], reads=["cscr_p", ("qkc", 2 * hh + 1)],
                            writes=[("kar", hh, 0)], key=("kar", hh))
                        dma("sp", QK(2 * hh + 1, 0, 2048, 66, 67), cscr_p[h:h + 1, 1, :], reads=["cscr_p2"],
                            writes=[("kar", hh, 1)], key=("kar", hh))
                        for b in range(2):
                            dma("sp", QK(2 * hh, 2048 + 16 * b, 16, 64, 65), cscr_s[h:h + 1, 0, b, 1024:1040],
                                reads=[("cscr_s", b, 0)], writes=[("qar", hh, 1 + b)], key=("qar", hh))
                            dma("sp", QK(2 * hh + 1, 2048 + 16 * b, 16, 65, 66), cscr_s[h:h + 1, 0, b, 1024:1040],
                                reads=[("cscr_s", b, 0)], writes=[("kar", hh, 2 + b)], key=("kar", hh))
                            dma("sp", QK(2 * hh + 1, 2048 + 16 * b, 16, 66, 67), cscr_s[h:h + 1, 1, b, 1024:1040],
                                reads=[("cscr_s", b, 1)], writes=[("kar", hh, 4 + b)], key=("kar", hh))
                        qar = [("qar", hh, x) for x in range(3)] + [("qkc", 2 * hh)]
                        kar = [("kar", hh, x) for x in range(6)] + [("qkc", 2 * hh + 1)]
                        for I in range(4):
                            ob = psnext("o")
                            kts = []
                            for j in range(4 * I + 4):
                                a = j - 4 * I
                                kts.append(dict(ka=QK(2 * hh + 1, 128 * j, 128, 0, 128), nk=128, va=VA128(j, h),
                                                lo=(128 * a if a >= 0 else 0), hi=512, pv_lo=(128 * a if a >= 0 else 0),
                                                mask=((MASKNEG, 128) if a >= 0 else None), pt="rot",
                                                reads=[("ka", hh, j // 4), ("qa", hh, I)] + qar + kar,
                                                vreads=[("va", j), "va_ones"]))
                            osubs = [(PS[ob][:, 65 * s:65 * s + 65], 128 * s, 128 * s + 128, 0, 4 * I + s) for s in range(4)]
                            attend(lambda c0, n, hh=hh, I=I: QK(2 * hh, 512 * I + c0, n, 0, 128), 128, kts, "fm", ob, "fox")
                            nsq.step(lambda ob=ob, I=I, pr=pr, hh=hh: norm_store_fm(ob, I, 0, pr, 64 * hh, SM[:, 160 + 4 * I:164 + 4 * I], l))
                        ob = psnext("g")
                        kts = []
                        for b in range(2):
                            dma("pool", KAS(b, 0, 1024, 0, 64), ckT[l, b, h], reads=[("ws", 0)], writes=[("sc", b, "k")], key=("sck", b))
                            dma("pool", VAS3(b)[:, :, 0:64],
                                cv[l, b].rearrange("(j p) (h d) -> p j h d", p=128, h=8)[:, :, h, :],
                                reads=[("ws", 0)], writes=[("sc", b, "v")], key=("scv", b), slow=True)
                            dma("sp", KAS(b, 0, 1024, 65, 66), cscr_s[h:h + 1, 0, b, 0:1024], reads=[("cscr_s", b, 0), ("ws", 0)],
                                writes=[("sc", b, "r")], key=("scr", b))
                            dma("sp", KAS(b, 0, 1024, 66, 67), cscr_s[h:h + 1, 1, b, 0:1024], reads=[("cscr_s", b, 1), ("ws", 0)],
                                writes=[("sc", b, "r2")], key=("scr", b))
                            for j in range(8):
                                kts.append(dict(ka=KAS(b, 128 * j, 128), nk=128, va=VAS(b, j),
                                                lo=16 * b, hi=16 * b + 16, pv_lo=0, mask=None, pt=3 * b + (j % 3),
                                                reads=[("sc", b, "k"), ("sc", b, "r"), ("sc", b, "r2"), ("sc", b, "one"), ("qa", hh, 4)] + qar,
                                                vreads=[("sc", b, "v"), ("sc", b, "one")]))
                        kts.append(dict(ka=QK(2 * hh + 1, 2048, 32, 0, 67), nk=32, va=VA(16, h, 32), lo=0, hi=32, pv_lo=0,
                                        mask=(MASKS, 32), pt=6, reads=[("ka", hh, 4), ("qa", hh, 4)] + qar + kar,
                                        vreads=[("va", 16), "va_ones"]))
                        osubs = [(PS[ob][0:32, 0:65], 0, 32, 0, len(kts) - 1)]
                        attend(lambda c0, n, hh=hh: QK(2 * hh, 2048 + c0, n, 0, 67), 67, kts, osubs, ob, "foxs")
                        flush_pv()
                        norm_and_store(ob, 0, 32, 16, 0, pr, 64 * hh, SSFOX(16, 32), l)

                flush_pv()
                nsq.flush()
                _chk("fox%d" % l)
                S.op("act", lambda e: e.activation(SM[:, 100:117], SM[:, 160:177], AF.Ln, bias=EPS, scale=1.0 / 512),
                     reads=[("ssq", 0, i) for i in range(17)], writes=["rfox"])
                S.op("act", lambda e: e.activation(SM[:, 100:117], SM[:, 100:117], AF.Exp, scale=-0.5), reads=["rfox"], writes=["rfox"])
                S.op("act", lambda e: e.activation(SM[:, 140:157], SM[:, 180:197], AF.Ln, bias=EPS, scale=1.0 / 256),
                     reads=[("ssq", 2, i) for i in range(17)], writes=["rmem"])
                S.op("act", lambda e: e.activation(SM[:, 140:157], SM[:, 140:157], AF.Exp, scale=-0.5), reads=["rmem"], writes=["rmem"])

                wload(0, w_out[l][:, 0:512], 512, extra_writes=SC_TOKS)
                wload(1, w_out[l][:, 512:1024], 512)
                GP = SCRf(4, 1024)
                dma("sp", GP, g_post_mix[l:l + 1, :].broadcast_to([128, D]), writes=scr(4, 5, 6, 7), key="gp")
                for i in range(17):
                    rows = tile_rows(i)
                    c0, n = tcols(i)
                    ostg = SCRf(0, 1024, 0, rows)
                    mxr = [("mx", c, i, p0) for c in (0, 1, 2, 3, 6, 7) for p0 in (0, 64)] + [("mx", 4, i), ("mx", 5, i)]
                    last_b = None
                    for hf in range(2):
                        banks = [psnext("w") for _ in range(3)]
                        for gi, (cs, b) in enumerate(zip(((0, 1, 2, 3), (4, 5), (6, 7)), banks)):
                            mm_group(None, [(PS[b][0:rows, :], MX(c, c0, n), WS(hf, c, 0, 512), c == cs[0], c == cs[-1]) for c in cs],
                                     reads=mxr + [("ws", hf)], writes=[("ps", b)])
                        o_h = ostg[:, 512 * hf:512 * hf + 512]
                        S.op("act", lambda e, o_h=o_h, b=banks[0], rows=rows, i=i: e.activation(o_h, PS[b][0:rows, :], AF.Copy, scale=RFOX(i, rows)),
                             reads=[("ps", banks[0]), "rfox"], writes=scr(2 * hf, 2 * hf + 1))
                        S.op("dve", lambda e, o_h=o_h, b=banks[1], rows=rows, i=i: e.scalar_tensor_tensor(o_h, PS[b][0:rows, :], RSGU(i, rows), o_h, ALU.mult, ALU.add),
                             reads=[("ps", banks[1]), ("rsgu", i)] + scr(2 * hf, 2 * hf + 1), writes=scr(2 * hf, 2 * hf + 1))
                        S.op("dve", lambda e, o_h=o_h, b=banks[2], rows=rows, i=i: e.scalar_tensor_tensor(o_h, PS[b][0:rows, :], RMEM(i, rows), o_h, ALU.mult, ALU.add),
                             reads=[("ps", banks[2]), "rmem"] + scr(2 * hf, 2 * hf + 1), writes=scr(2 * hf, 2 * hf + 1))
                        last_b = banks
                    post_norm_residual(ostg, rows, i, bf(WS_O[2], 1024, 0, rows), [("ws", 2)], GP, scr(4, 5, 6, 7), scr(0, 1, 2, 3))

                _chk("wout%d" % l)
                S.barrier()
                ffn_phase(l)
                if l < L - 1:
                    S.barrier()

        except _Stop:
            pass
        S.emit(st)
    return nc


_NC_CACHE = {}


def _prep_inputs(inp):
    f = lambda a: np.ascontiguousarray(np.asarray(a, dtype=np.float32))
    x_prompt = f(inp["x_prompt"]); x_sample = f(inp["x_sample"]); mem_prompt = f(inp["mem_prompt"])
    cfk = f(inp["cache_fox_k"]); cfv = f(inp["cache_fox_v"]); clf = f(inp["cache_fox_logf"])
    cmk = f(inp["cache_mem_k"]); cmvv = f(inp["cache_mem_v"]); cconv = f(inp["cache_ffn_conv"])
    ckT_all = np.ascontiguousarray(cfk.transpose(0, 1, 3, 4, 2))
    cv_all = cfv.reshape(L, 16, 1024, 512)
    clfT_all = np.ascontiguousarray(clf.transpose(0, 1, 3, 2))
    cmkT_all = np.ascontiguousarray(cmk.transpose(0, 1, 3, 4, 2))
    cmv_all = cmvv.reshape(L, 16, 256, 256)
    wsT = np.ascontiguousarray(f(inp["w_spatial"]).transpose(0, 1, 3, 2))
    fm8 = lambda g: f(g).reshape(L, 8, 128).transpose(0, 2, 1)
    b_sp = f(inp["b_spatial"])
    w_dw = f(inp["w_dwconv"]); b_dw = f(inp["b_dwconv"]); b_f = f(inp["b_forget"])
    cst = np.zeros((128, 288), dtype=ml_dtypes.bfloat16)
    cst[:, 0:128] = np.eye(128, dtype=np.float32).astype(ml_dtypes.bfloat16)
    kk, qq = np.meshgrid(np.arange(128), np.arange(128), indexing="ij")
    cst[:, 128:256] = np.where(kk <= qq, 0.0, NEG).astype(ml_dtypes.bfloat16)
    k2, q2 = np.meshgrid(np.arange(32), np.arange(32), indexing="ij")
    ok = (k2 // 16 == q2 // 16) & (k2 % 16 <= q2 % 16)
    cst[0:32, 256:288] = np.where(ok, 0.0, NEG).astype(ml_dtypes.bfloat16)
    wu = f(inp["w_up"])
    wu_a = wu[:, :, 0:FF].reshape(L, 8, 128, NJ, 128)
    wu_l = wu[:, :, FF:2 * FF].reshape(L, 8, 128, NJ, 128)
    w_up_r = np.ascontiguousarray(np.concatenate([wu_a, wu_l], axis=4).transpose(0, 3, 2, 1, 4))
    shared = dict(
        w_in=f(inp["w_in"]), w_mkv=f(inp["w_mem_kv"]), w_out=f(inp["w_out"]), w_up=w_up_r,
        w_dn=f(inp["w_down"]), g_post_mix=f(inp["g_post_mix"]), g_post_ffn=f(inp["g_post_ffn"]),
        g_sgu=f(inp["g_sgu"]), wsT=wsT, cst=cst)
    gpm = fm8(inp["g_pre_mix"]); ggo = fm8(inp["g_group_out"]); gmem = fm8(inp["g_mem"]); gpf = fm8(inp["g_pre_ffn"])
    in_maps = []
    for c in range(NCORES):
        prm = np.zeros((128, NPRM), dtype=np.float32)
        for l in range(L):
            o = l * PPL
            prm[:, o + PO["gpm"]:o + PO["gpm"] + 8] = gpm[l]
            prm[:, o + PO["ggo"]:o + PO["ggo"] + 8] = ggo[l]
            prm[:, o + PO["gmem"]:o + PO["gmem"] + 8] = gmem[l]
            prm[:, o + PO["gpf"]:o + PO["gpf"] + 8] = gpf[l]
            prm[:, o + PO["bs"]:o + PO["bs"] + 4] = b_sp[l].T
            prm[0:16, o + PO["bss"]:o + PO["bss"] + 4] = b_sp[l][:, 0:16].T
            prm[16:32, o + PO["bss"]:o + PO["bss"] + 4] = b_sp[l][:, 0:16].T
            prm[:, o + PO["wdw"]:o + PO["wdw"] + 66] = w_dw[l].reshape(3, NJ, 128).transpose(2, 1, 0).reshape(128, 66)
            prm[:, o + PO["bdw"]:o + PO["bdw"] + 22] = b_dw[l].reshape(NJ, 128).T
            prm[0:8, o + PO["bf"]] = b_f[l]
            cc = cconv[l, 2 * c:2 * c + 2]
            prm[:, o + PO["cconv"]:o + PO["cconv"] + 88] = cc.reshape(2, 2, NJ, 128).transpose(3, 2, 0, 1).reshape(128, 88)
        m = dict(shared)
        m.update(
            x_p=x_prompt[c], x_s=x_sample[2 * c:2 * c + 2].reshape(32, D), mem=mem_prompt[c],
            ckT=np.ascontiguousarray(ckT_all[:, 2 * c:2 * c + 2]), cv=np.ascontiguousarray(cv_all[:, 2 * c:2 * c + 2]),
            clfT=np.ascontiguousarray(clfT_all[:, 2 * c:2 * c + 2]), cmkT=np.ascontiguousarray(cmkT_all[:, 2 * c:2 * c + 2]),
            cmv=np.ascontiguousarray(cmv_all[:, 2 * c:2 * c + 2]), prm=prm)
        in_maps.append(m)
    return in_maps


def kernel(**inp):
    in_maps = _prep_inputs(inp)
    if "nc" not in _NC_CACHE:
        _NC_CACHE["nc"] = build()
    nc = _NC_CACHE["nc"]
    res = run_bass_kernel_spmd(nc, in_maps, core_ids=list(range(NCORES)))
    R = res.results
    cat = lambda name: np.stack([np.asarray(r[name], dtype=np.float32) for r in R], axis=0)
    y_p = cat("y_p")
    y_s = cat("y_s").reshape(16, 16, D)
    fk = cat("ofk").transpose(1, 0, 2, 3).reshape(L, 8, 2048, 8, 64)
    fv = cat("ofv").transpose(1, 0, 2, 3).reshape(L, 8, 2048, 8, 64)
    lf = cat("olf").transpose(1, 0, 3, 2)
    mk = cat("omk").transpose(1, 0, 2, 3).reshape(L, 8, 256, 4, 64)
    mv = cat("omv").transpose(1, 0, 2, 3).reshape(L, 8, 256, 4, 64)
    cvp = cat("oconv").transpose(1, 0, 4, 3, 2).reshape(L, 8, 2, FF)
    sk = cat("osk").transpose(1, 0, 2, 3).reshape(L, 16, 16, 8, 64)
    sv = cat("osv").transpose(1, 0, 2, 3).reshape(L, 16, 16, 8, 64)
    slf = cat("oslf").transpose(1, 0, 3, 2).reshape(L, 16, 16, 8)
    gv = cat("ogv").transpose(1, 0, 2, 3).reshape(L, 16, 16, 256)
    cvs = cat("osconv").transpose(1, 0, 4, 5, 3, 2).reshape(L, 16, 2, FF)
    outs = (y_p, y_s, fk, fv, lf, mk, mv, cvp, sk, sv, slf, gv, cvs)
    return tuple(np.ascontiguousarray(o, dtype=np.float32) for o in outs)
```

```python
import numpy as np
import ml_dtypes
from contextlib import ExitStack
import concourse.bass as bass
import concourse.mybir as mybir
from concourse.bass_utils import run_bass_kernel_spmd

F32 = mybir.dt.float32
BF16 = mybir.dt.bfloat16
ALU = mybir.AluOpType
AF = mybir.ActivationFunctionType

ENGS = ("pe", "act", "dve", "pool", "sp")
STOP = None
DBG = set()


class _Stop(Exception):
    pass


def _chk(name):
    if STOP == name:
        raise _Stop()
NCORES = 8
L = 2
D = 1024
NT = 2080
FF = 2816
NJ = 22
EPS = 1e-6
NEG = -30000.0


class Op:
    __slots__ = ("eng", "fn", "idx", "deps", "dma_key", "marked", "seq", "cum")

    def __init__(self, eng, fn, idx, dma_key):
        self.eng = eng
        self.fn = fn
        self.idx = idx
        self.deps = ()
        self.dma_key = dma_key
        self.marked = False
        self.seq = 0
        self.cum = 0


class Sched:
    def __init__(self, nc):
        self.nc = nc
        self.streams = {e: [] for e in ENGS}
        self.last_writer = {}
        self.readers = {}
        self.dma_counts = {}
        self.dma_last = {}

    def op(self, eng, fn, reads=(), writes=(), dma_key=None):
        o = Op(eng, fn, len(self.streams[eng]), dma_key)
        deps = set()
        lw = self.last_writer
        rd = self.readers
        for t in reads:
            w = lw.get(t)
            if w is not None:
                deps.add(w)
            if type(t) is tuple and t[0] == "ps":
                for r in rd.get(t, ()):
                    if r.eng != eng:
                        deps.add(r)
        for t in writes:
            w = lw.get(t)
            if w is not None:
                deps.add(w)
            r = rd.get(t)
            if r:
                deps.update(r)
        for t in reads:
            rd.setdefault(t, []).append(o)
        for t in writes:
            lw[t] = o
            rd[t] = []
        deps.discard(o)
        if eng == "pe":
            deps = {d for d in deps if not (d.eng == "pe" and d.dma_key is None)}
        o.deps = deps
        if dma_key is not None:
            c = self.dma_counts.get(dma_key, 0) + 1
            self.dma_counts[dma_key] = c
            o.cum = 16 * c
            self.dma_last[dma_key] = o
        self.streams[eng].append(o)
        return o

    def barrier(self):
        lasts = [s[-1] for s in self.streams.values() if s]
        lasts += list(self.dma_last.values())
        for e in ENGS:
            o = Op(e, lambda eng: eng.nop(), len(self.streams[e]), None)
            o.deps = {d for d in lasts if not (d.eng == e and d.dma_key is None)}
            self.streams[e].append(o)

    def emit(self, stack):
        nc = self.nc
        for e in ENGS:
            for o in self.streams[e]:
                for d in o.deps:
                    if d.dma_key is None:
                        d.marked = True
        sems = {}
        for e in ENGS:
            n = 0
            for o in self.streams[e]:
                if o.dma_key is None and o.marked:
                    n += 1
                    o.seq = n
            if n:
                sems[e] = stack.enter_context(nc.semaphore("s_" + e))
        dsems = {}
        for k in self.dma_counts:
            dsems[k] = stack.enter_context(nc.semaphore("d%d" % len(dsems)))
        self.n_sems = len(sems) + len(dsems)
        final = {k: o.cum for k, o in self.dma_last.items()}

        def run(eng_name, eng):
            waited = {}
            for o in self.streams[eng_name]:
                need = {}
                for d in o.deps:
                    if d.dma_key is not None:
                        key = ("d", d.dma_key)
                        val = d.cum
                    else:
                        key = ("c", d.eng)
                        val = d.seq
                    if val > need.get(key, 0):
                        need[key] = val
                for key, val in need.items():
                    if waited.get(key, 0) >= val:
                        continue
                    waited[key] = val
                    s = dsems[key[1]] if key[0] == "d" else sems[key[1]]
                    eng.wait_ge(s, val)
                ins = o.fn(eng)
                if o.dma_key is not None:
                    ins.then_inc(dsems[o.dma_key], 16)
                elif o.marked:
                    ins.then_inc(sems[o.eng], 1)
            if eng_name == "sp":
                for k, v in final.items():
                    if waited.get(("d", k), 0) < v:
                        eng.wait_ge(dsems[k], v)

        with nc.Block() as block:
            @block.tensor
            def _(t):
                run("pe", t)

            @block.scalar
            def _(t):
                run("act", t)

            @block.vector
            def _(t):
                run("dve", t)

            @block.gpsimd
            def _(t):
                run("pool", t)

            @block.sync
            def _(t):
                run("sp", t)


def tile_rows(i):
    return 128 if i < 16 else 32


def tcols(i):
    return (128 * i, 128) if i < 16 else (2048, 32)


BLKS = [(0, 512), (512, 512), (1024, 512), (1536, 512), (2048, 32)]

PO = {}
_o = 0
for _n, _w in (("gpm", 8), ("ggo", 8), ("gmem", 8), ("gpf", 8), ("bs", 4), ("bss", 4),
               ("wdw", 66), ("bdw", 22), ("bf", 1), ("cconv", 88)):
    PO[_n] = _o
    _o += _w
PPL = _o
NPRM = PPL * L


def build():
    nc = bass.Bass("TRN2", target_bir_lowering=False)

    def din(name, shape, dt=F32):
        return nc.dram_tensor(name, list(shape), dt, kind="ExternalInput").ap()

    def dout(name, shape):
        return nc.dram_tensor(name, list(shape), F32, kind="ExternalOutput").ap()

    x_p = din("x_p", [2048, D])
    x_s = din("x_s", [32, D])
    mem = din("mem", [256, D])
    ckT = din("ckT", [L, 2, 8, 64, 1024])
    cv = din("cv", [L, 2, 1024, 512])
    clfT = din("clfT", [L, 2, 8, 1024])
    cmkT = din("cmkT", [L, 2, 4, 64, 256])
    cmv = din("cmv", [L, 2, 256, 256])
    w_in = din("w_in", [L, D, 2312])
    w_mkv = din("w_mkv", [L, D, 512])
    w_out = din("w_out", [L, D, D])
    w_up = din("w_up", [L, NJ, 128, 8, 256])
    w_dn = din("w_dn", [L, FF, D])
    g_post_mix = din("g_post_mix", [L, D])
    g_post_ffn = din("g_post_ffn", [L, D])
    g_sgu = din("g_sgu", [L, 256])
    wsT = din("wsT", [L, 4, 128, 128])
    prm_h = din("prm", [128, NPRM])
    cst_h = din("cst", [128, 288], BF16)

    y_p = dout("y_p", [2048, D])
    y_s = dout("y_s", [32, D])
    ofk = dout("ofk", [L, 2048, 512])
    ofv = dout("ofv", [L, 2048, 512])
    olf = dout("olf", [L, 8, 2048])
    omk = dout("omk", [L, 256, 256])
    omv = dout("omv", [L, 256, 256])
    oconv = dout("oconv", [L, 128, NJ, 2])
    osk = dout("osk", [L, 32, 512])
    osv = dout("osv", [L, 32, 512])
    oslf = dout("oslf", [L, 8, 32])
    ogv = dout("ogv", [L, 32, 256])
    osconv = dout("osconv", [L, 128, NJ, 2, 2])

    cscr_p = nc.dram_tensor("cscr_p", [8, 2, 2048], BF16).ap()
    cscr_s = nc.dram_tensor("cscr_s", [8, 2, 2, 1040], BF16).ap()

    with ExitStack() as st:
        S = Sched(nc)
        X = st.enter_context(nc.sbuf_tensor("X", [128, 17, D], F32))
        PRM = st.enter_context(nc.sbuf_tensor("PRM", [128, NPRM], F32))
        CST = st.enter_context(nc.sbuf_tensor("CST", [128, 288], BF16))
        WSTSt = st.enter_context(nc.sbuf_tensor("WSTS", [32, L * 4 * 32], BF16))
        PTSt = st.enter_context(nc.sbuf_tensor("PTS", [128, 7 * 32], BF16))
        SM = st.enter_context(nc.sbuf_tensor("SM", [128, 256], F32))
        ONEROW = st.enter_context(nc.sbuf_tensor("ONEROW", [128, 64], F32))
        remaining = nc.sbuf_bytes_remaining
        remaining = remaining() if callable(remaining) else remaining
        ARB = (remaining - 512) // 64 * 64
        AR = st.enter_context(nc.sbuf_tensor("AR", [128, ARB // 2], BF16))
        AR32 = AR.bitcast(F32)
        PS = [st.enter_context(nc.psum_tensor("ps%d" % k, [128, 512], F32)) for k in range(8)]
        PSB = [p.bitcast(BF16) for p in PS]

        IDENT = CST[:, 0:128]
        MASKNEG = CST[:, 128:256]
        MASKS = CST[0:32, 256:288]


        def WSTS(l, g):
            o = (l * 4 + g) * 32
            return WSTSt[0:32, o:o + 32]

        def prm(l, name, w, p0=0, p1=128, off=0):
            o = l * PPL + PO[name] + off
            return PRM[p0:p1, o:o + w]

        SM_SS, SM_SD, SM_RSTD = 0, 1, 2
        SM_R = 16
        SM_SSF = 70
        SM_T = 90

        def bf(off, n, p0=0, p1=128):
            return AR[p0:p1, off // 2: off // 2 + n]

        def f32(off, n, p0=0, p1=128):
            return AR32[p0:p1, off // 4: off // 4 + n]

        cur = [0]

        def carve(nbytes):
            o = cur[0]
            cur[0] = o + (nbytes + 63) // 64 * 64
            return o

        HT_O = carve(8 * NT * 2)
        MX_O = carve(8 * NT * 2)
        VA_O = carve(17 * 8 * 65 * 2)
        QK_O = carve(4 * NT * 2)
        WS_O = [carve(8320) for _ in range(3)]
        SCR_O = carve(8192)
        MKT_O = carve(2048)
        MVA_O = carve(1040)
        MKTt = AR[:, MKT_O // 2:MKT_O // 2 + 1024]
        MVAt = AR[:, MVA_O // 2:MVA_O // 2 + 520]
        MIX_END = cur[0]
        assert MIX_END <= ARB, (MIX_END, ARB)

        def HT(c, c0, n):
            return bf(HT_O + (c * NT + c0) * 2, n)

        def HT3(c0, n):
            return AR[:, HT_O // 2: HT_O // 2 + 8 * NT].rearrange("p (c t) -> p c t", c=8)[:, :, c0:c0 + n]

        def MX(c, c0, n, p0=0, p1=128):
            return bf(MX_O + (c * NT + c0) * 2, n, p0, p1)

        def VA(j, h, nk=128):
            return bf(VA_O + ((j * 8 + h) * 65) * 2, 65, 0, nk)

        def VA128(j, h):
            return bf(VA_O + ((j * 8 + h) * 65) * 2, 128, 0, 128)

        def QK(slot, c0, n, p0, p1):
            return bf(QK_O + (slot * NT + c0) * 2, n, p0, p1)

        CT = lambda c0, n: f32(QK_O + c0 * 4, n, 96, 104)
        HI = lambda c0, n: bf(QK_O + 8320 + c0 * 2, n, 96, 104)
        LO = lambda c0, n: bf(QK_O + 8320 + 4160 + c0 * 2, n, 96, 104)

        def WS(s, c, c0, n):
            return bf(WS_O[s] + (c * 520 + c0) * 2, n)

        def WS3(s, ncol):
            return AR[:, WS_O[s] // 2: WS_O[s] // 2 + 8 * 520].rearrange("p (c n) -> p c n", c=8)[:, :, 0:ncol]

        def SCRb(g, n, p0=0, p1=128, off=0):
            return bf(SCR_O + g * 1024 + off * 2, n, p0, p1)

        def SCRf(g, n, p0=0, p1=128, off=0):
            return f32(SCR_O + g * 1024 + off * 4, n, p0, p1)

        def scr(*gs):
            return [("scr", g) for g in gs]

        def dma(eng, out, in_, reads=(), writes=(), key=None, slow=False):
            if slow:
                fn = lambda e: e.dma_start(out=out, in_=in_, allow_slow_non_contiguous=True)
            else:
                fn = lambda e: e.dma_start(out=out, in_=in_)
            return S.op(eng, fn, reads=reads, writes=writes, dma_key=key)

        psr = {"g": [0, 1], "s": [2, 3, 4], "o": [5, 6], "t": [7]}
        psi = {k: 0 for k in psr}

        def psnext(pool):
            lst = psr[pool]
            k = lst[psi[pool] % len(lst)]
            psi[pool] += 1
            return k

        def mm_group(out_fn, items, reads, writes):
            def fn(e):
                ins = None
                for (o, a, b, s0, s1) in items:
                    ins = e.matmul(o, a, b, start=s0, stop=s1, skip_group_check=True)
                return ins
            return S.op("pe", fn, reads=reads, writes=writes)

        dma("sp", PRM[:, :], prm_h, writes=["prm"], key="prm")
        dma("sp", CST[:, :], cst_h, writes=["cst"], key="cst")
        S.op("dve", lambda e: e.memset(WSTSt[:, :], 0.0), writes=["wsts"])
        for l in range(L):
            for b in range(2):
                dma("pool",
                    WSTSt[16 * b:16 * b + 16, l * 128:(l + 1) * 128].rearrange("p (g i) -> p g i", g=4)[:, :, 16 * b:16 * b + 16],
                    wsT[l, :, 0:16, 0:16].rearrange("g j i -> j g i"),
                    reads=["wsts"], writes=[("wsts", l, b)], key=("wsts", l), slow=True)
        S.op("dve", lambda e: e.memset(PTSt[:, :], 0.0), writes=["pts%d" % q for q in range(7)])
        S.op("dve", lambda e: e.memset(SM[:, 200:201], 1.0), writes=["one"])
        S.op("dve", lambda e: e.memset(ONEROW[:, :], 1.0), writes=["onerow"])
        ONE8 = SM[96:104, 200:201]

        for i in range(16):
            dma("sp", X[:, i, :], x_p[128 * i:128 * i + 128, :], writes=[("x", i)], key=("x", i))
        dma("sp", X[0:32, 16, :], x_s, writes=[("x", 16)], key=("x", 16))

        def prenorm_tile(src, rows, xtok, gcol, dst3, dst_tok, slot, extra_writes=(), hb=None, tpool="t", hb_toks=None, defer=False):
            if hb is None:
                hb = SCRb(2 * slot, 1024, 0, rows)
            hbt = list(hb_toks) if hb_toks is not None else scr(2 * slot, 2 * slot + 1)
            c0 = 3 * slot
            ss = SM[0:rows, c0:c0 + 1]
            sd = SM[0:rows, c0 + 1:c0 + 2]
            rs = SM[0:rows, c0 + 2:c0 + 3]
            S.op("act", lambda e: e.activation(hb, src, AF.Square, accum_out=ss),
                 reads=[xtok], writes=hbt + [("sm", c0)])
            S.op("act", lambda e: e.activation(sd, ss, AF.Ln, bias=EPS, scale=1.0 / D),
                 reads=[("sm", c0)], writes=[("sm", c0 + 1)])
            S.op("act", lambda e: e.activation(rs, sd, AF.Exp, scale=-0.5), reads=[("sm", c0 + 1)], writes=[("sm", c0 + 2)])
            S.op("act", lambda e: e.activation(hb, src, AF.Copy, scale=rs),
                 reads=[xtok, ("sm", c0 + 2)], writes=hbt)
            def part_b():
                k = psnext(tpool)

                def tr(e):
                    ins = None
                    for c in range(8):
                        ins = e.transpose(PSB[k][:, c * 128:c * 128 + rows], hb[:, c * 128:(c + 1) * 128],
                                          IDENT[0:rows, 0:rows])
                    return ins
                S.op("pe", tr, reads=hbt + ["cst"], writes=[("ps", k)])
                src3 = PSB[k][:, 0:1024].rearrange("p (c t) -> p c t", c=8)[:, :, 0:rows]
                S.op("dve", lambda e: e.tensor_tensor(dst3, src3, gcol.unsqueeze(2).to_broadcast([128, 8, rows]), ALU.mult),
                     reads=[("ps", k), "prm"], writes=[dst_tok] + list(extra_writes))
            if defer:
                return part_b
            part_b()

        PVQ = []

        def flush_pv():
            while PVQ:
                PVQ.pop(0)()

        def attend(qa_fn, K, ktiles, osubs, obank, tag):
            pendq = []
            first_done = [False]

            def pv(kt_i, kt, pt_ap, pt_tok):
                items = []
                if osubs == "fm":
                    lo_, hi_ = kt["lo"], kt["hi"]
                    mrows = kt["va"].shape[1]
                    items.append((PS[obank][0:mrows, lo_:hi_], kt["va"], pt_ap[:, lo_:hi_], kt_i == 0, kt_i == len(ktiles) - 1))
                    mm_group(None, items, reads=[pt_tok] + kt["vreads"], writes=[("ps", obank)])
                    return
                for (o_ap, c0, c1, fk, lk) in osubs:
                    if fk <= kt_i <= lk and kt["pv_lo"] <= c0:
                        items.append((o_ap, pt_ap[:, c0:c1], kt["va"], not first_done[0], kt_i == lk))
                        first_done[0] = True
                if items:
                    mm_group(None, items, reads=[pt_tok] + kt["vreads"], writes=[("ps", obank)])

            for kt_i, kt in enumerate(ktiles):
                k = psnext("s")
                nk, lo, hi = kt["nk"], kt["lo"], kt["hi"]
                items = []
                if kt["mask"] is not None:
                    mask_ap, mc = kt["mask"]
                    items.append((PS[k][0:nk, lo:lo + mc], kt["ka"], qa_fn(lo, mc), True, False))
                    items.append((PS[k][0:nk, lo:lo + mc], IDENT[0:nk, 0:nk], mask_ap, False, True))
                    if hi > lo + mc:
                        items.append((PS[k][0:nk, lo + mc:hi], kt["ka"], qa_fn(lo + mc, hi - lo - mc), True, True))
                else:
                    items.append((PS[k][0:nk, lo:hi], kt["ka"], qa_fn(lo, hi - lo), True, True))
                mm_group(None, items, reads=kt["reads"] + ["cst"], writes=[("ps", k)])
                if kt["pt"] == "rot":
                    g = attend.ptc % 4
                    attend.ptc += 1
                    pt_ap = SCRb(g, 512, 0, nk)
                    pt_tok = ("scr", g)
                else:
                    pt_ap = PTSt[0:nk, 32 * kt["pt"]:32 * kt["pt"] + 32]
                    pt_tok = "pts%d" % kt["pt"]
                S.op("act", lambda e, pt_ap=pt_ap, k=k, nk=nk, lo=lo, hi=hi:
                     e.activation(pt_ap[:, lo:hi], PS[k][0:nk, lo:hi], AF.Exp),
                     reads=[("ps", k)], writes=[pt_tok])
                while len(PVQ) >= 2:
                    PVQ.pop(0)()
                PVQ.append(lambda kt_i=kt_i, kt=kt, pt_ap=pt_ap, pt_tok=pt_tok: pv(kt_i, kt, pt_ap, pt_tok))
        attend.ptc = 0

        def cs_all():
            return ["cw"]

        def post_norm_residual(ostg, rows, i, junk, junk_toks, GPap, gp_toks, o_toks):
            ssc = SM[0:rows, SM_T + 8:SM_T + 9]
            S.op("act", lambda e: e.activation(junk, ostg, AF.Square, accum_out=ssc),
                 reads=o_toks, writes=list(junk_toks) + ["pn"])
            S.op("act", lambda e: e.activation(ssc, ssc, AF.Ln, bias=EPS, scale=1.0 / D), reads=["pn"], writes=["pn"])
            S.op("act", lambda e: e.activation(ssc, ssc, AF.Exp, scale=-0.5), reads=["pn"], writes=["pn"])
            S.op("dve", lambda e: e.scalar_tensor_tensor(ostg, ostg, ssc, GPap[0:rows, :], ALU.mult, ALU.mult),
                 reads=list(o_toks) + list(gp_toks) + ["pn"], writes=o_toks)
            S.op("pool", lambda e: e.tensor_tensor(X[0:rows, i, :], X[0:rows, i, :], ostg, ALU.add),
                 reads=list(o_toks) + [("x", i)], writes=[("x", i)])

        def ffn_phase(l):
            psr.clear()
            psr.update({"u": [0, 1, 2, 4, 5, 6], "t": [3], "d": [4, 5, 6, 7]})
            for kk in psr:
                psi[kk] = 0
            HC = 1056
            o = [0]

            def cv_(nb):
                r = o[0]
                o[0] = r + (nb + 63) // 64 * 64
                return r
            H2_O = cv_(8 * HC * 2)
            ACT_O = cv_(NJ * HC * 2)
            WDN_O = cv_(NJ * 1024 * 2)
            WU_O = [cv_(4096) for _ in range(2)]
            AE_O = [cv_(2064) for _ in range(2)]
            LIN_O = [cv_(1024) for _ in range(2)]
            CONV_O = [cv_(2048) for _ in range(2)]
            OSTG_O = cv_(4096)
            GPF_O = cv_(4096)
            HBF_O = cv_(2048)
            SAVE_O = cv_(NJ * 2 * 4)
            SAVES_O = cv_(NJ * 4 * 4)
            assert o[0] <= ARB, (o[0], ARB)

            H2 = lambda c, c0, n: bf(H2_O + (c * HC + c0) * 2, n)
            H23 = lambda c0, n: AR[:, H2_O // 2:H2_O // 2 + 8 * HC].rearrange("p (c t) -> p c t", c=8)[:, :, c0:c0 + n]
            ACTB = lambda j, c0, n: bf(ACT_O + (j * HC + c0) * 2, n)
            WDN = lambda j, c0, n: bf(WDN_O + (j * 1024 + c0) * 2, n)
            WDN3 = AR[:, WDN_O // 2:WDN_O // 2 + NJ * 1024].rearrange("p (j n) -> p j n", j=NJ)
            WUB = lambda s: (WU_O[s] if s < 2 else OSTG_O)
            WU = lambda s, c, c0, n: bf(WUB(s) + (c * 256 + c0) * 2, n)
            WU3 = lambda s: AR[:, WUB(s) // 2:WUB(s) // 2 + 2048].rearrange("p (c n) -> p c n", c=8)
            wutok = lambda s, x: (("wu", s, x) if s < 2 else ("ostg%d" % x))
            OSTG = f32(OSTG_O, 1024)
            GPF = f32(GPF_O, 1024)
            HBF = bf(HBF_O, 1024)
            SAVE = f32(SAVE_O, NJ * 2)
            SAVES = f32(SAVES_O, NJ * 4)

            dma("sp", GPF, g_post_ffn[l:l + 1, :].broadcast_to([128, D]), writes=["gpf"], key="gp")
            wdn_src = w_dn[l].rearrange("(j p) n -> p j n", p=128)
            for part in range(2):
                dma("pool", WDN3[:, 11 * part:11 * part + 11, :], wdn_src[:, 11 * part:11 * part + 11, :],
                    writes=[("wdn", part)], key=("wdn", part))
            aecnt = 0

            def wu_load(j, s_):
                dma("pool", WU3(s_), w_up[l, j], writes=[wutok(s_, 0), wutok(s_, 1)], key=("wu", s_, 0))

            for half in range(2):
                tiles = list(range(0, 8)) if half == 0 else list(range(8, 17))
                base = 0 if half == 0 else 1024
                blocks = [(0, 512), (512, 512)] + ([(1024, 32)] if half == 1 else [])
                h2toks = [("h2", (i if i < 8 else i - 8)) for i in tiles]
                if half == 0:
                    HB2 = bf(AE_O[0], 1024)

                    def pa(i):
                        c0, n = tcols(i)
                        if i % 2 == 0:
                            return prenorm_tile(X[:, i, :], 128, ("x", i), prm(l, "gpf", 8), H23(c0, n), ("h2", i), 0,
                                                hb=HBF[:, :], tpool="t", hb_toks=["hbf"], defer=True)
                        return prenorm_tile(X[:, i, :], 128, ("x", i), prm(l, "gpf", 8), H23(c0, n), ("h2", i), 1,
                                            hb=HB2, tpool="t", hb_toks=[("ae", 0), ("aeh", 0), ("aet", 0)], defer=True)
                    pbq = {0: pa(0)}
                    for i in tiles:
                        if i + 1 < 8:
                            pbq[i + 1] = pa(i + 1)
                        pbq.pop(i)()
                wu_load(0, 0)
                wu_load(1, 1)
                pend_tail = None
                for j in range(NJ):
                    s = j % 3
                    if j + 2 < NJ:
                        wu_load(j + 2, (j + 2) % 3)
                    w0 = prm(l, "wdw", 1, 0, 128, 3 * j)
                    w1 = prm(l, "wdw", 1, 0, 128, 3 * j + 1)
                    w2 = prm(l, "wdw", 1, 0, 128, 3 * j + 2)
                    bd = prm(l, "bdw", 1, 0, 128, j)
                    for bidx, (lc0, n) in enumerate(blocks):
                        sample = (n == 32)
                        ka = psnext("u")
                        mm_group(None, [(PS[ka][:, 0:n], WU(s, c, 0, 128), H2(c, lc0, n), c == 0, c == 7) for c in range(8)],
                                 reads=h2toks + [wutok(s, 0)], writes=[("ps", ka)])
                        kl = psnext("u")
                        mm_group(None, [(PS[kl][:, 0:n], WU(s, c, 128, 128), H2(c, lc0, n), c == 0, c == 7) for c in range(8)],
                                 reads=h2toks + [wutok(s, 1)], writes=[("ps", kl)])
                        p = aecnt % 2
                        aecnt += 1
                        lin = bf(LIN_O[p], 512)
                        conv = f32(CONV_O[p], 512)
                        if sample:
                            AEs = f32(AE_O[p], 36).rearrange("p (b t) -> p b t", b=2)
                            S.op("dve", lambda e, AEs=AEs, j=j: e.tensor_copy(
                                AEs[:, :, 0:2], prm(l, "cconv", 4, 0, 128, 4 * j).rearrange("p (b r) -> p b r", b=2)),
                                reads=["prm"], writes=[("aeh", p)])
                            S.op("act", lambda e, AEs=AEs, ka=ka: e.copy(AEs[:, :, 2:18], PS[ka][:, 0:32].rearrange("p (b t) -> p b t", b=2)),
                                 reads=[("ps", ka)], writes=[("ae", p)])
                            taps = [AEs[:, :, k:k + 16] for k in range(3)]
                            cv3 = conv[:, 0:32].rearrange("p (b t) -> p b t", b=2)
                            psa = PS[ka][:, 0:32].rearrange("p (b t) -> p b t", b=2)
                            sil_out = f32(AE_O[p] + 256, 32)
                            sil_in = conv[:, 0:32]
                        else:
                            AEp = f32(AE_O[p], 514)
                            if bidx == 0 and half == 0:
                                S.op("dve", lambda e, AEp=AEp: e.memset(AEp[:, 0:2], 0.0), writes=[("aeh", p)])
                            elif bidx == 0:
                                S.op("dve", lambda e, AEp=AEp, j=j: e.tensor_copy(AEp[:, 0:2], SAVE[:, 2 * j:2 * j + 2]),
                                     reads=[("save", j)], writes=[("aeh", p)])
                            else:
                                prev = f32(AE_O[1 - p], 514)
                                S.op("dve", lambda e, AEp=AEp, prev=prev: e.tensor_copy(AEp[:, 0:2], prev[:, 512:514]),
                                     reads=[("aet", 1 - p)], writes=[("aeh", p)])
                            S.op("act", lambda e, AEp=AEp, ka=ka: e.copy(AEp[:, 2:514], PS[ka][:, 0:512]),
                                 reads=[("ps", ka)], writes=[("ae", p), ("aet", p)])
                            taps = [AEp[:, k:k + 512] for k in range(3)]
                            cv3 = conv
                            psa = PS[ka][:, 0:512]
                            sil_out = AEp[:, 0:512]
                            sil_in = conv
                        S.op("act", lambda e, cv3=cv3, psa=psa, w2=w2, bd=bd: e.activation(cv3, psa, AF.Identity, bias=bd, scale=w2),
                             reads=[("ps", ka), "prm"], writes=[("conv", p)])
                        S.op("act", lambda e, lin=lin, kl=kl, n=n: e.copy(lin[:, 0:n], PS[kl][:, 0:n]),
                             reads=[("ps", kl)], writes=[("lin", p)])
                        S.op("dve", lambda e, cv3=cv3, taps=taps, w1=w1: e.scalar_tensor_tensor(cv3, taps[1], w1, cv3, ALU.mult, ALU.add),
                             reads=[("ae", p), ("aeh", p), ("conv", p)], writes=[("conv", p)])
                        S.op("dve", lambda e, cv3=cv3, taps=taps, w0=w0: e.scalar_tensor_tensor(cv3, taps[0], w0, cv3, ALU.mult, ALU.add),
                             reads=[("ae", p), ("aeh", p), ("conv", p)], writes=[("conv", p)])
                        if not sample and bidx == 1:
                            S.op("dve", lambda e, AEp=AEp, j=j: e.tensor_copy(SAVE[:, 2 * j:2 * j + 2], AEp[:, 512:514]),
                                 reads=[("aet", p)], writes=[("save", j)])
                        if sample:
                            S.op("dve", lambda e, AEs=AEs, j=j: e.tensor_copy(
                                SAVES[:, 4 * j:4 * j + 4].rearrange("p (b r) -> p b r", b=2), AEs[:, :, 16:18]),
                                reads=[("ae", p)], writes=[("saves", j)])

                        def tail(p=p, sil_out=sil_out, sil_in=sil_in, j=j, lc0=lc0, n=n, lin=lin, half=half):
                            S.op("act", lambda e: e.activation(sil_out, sil_in, AF.Silu),
                                 reads=[("conv", p), ("ae", p), ("aeh", p)], writes=[("ae", p), ("aeh", p)])
                            S.op("pool", lambda e: e.tensor_tensor(ACTB(j, lc0, n), sil_out, lin[:, 0:n], ALU.mult),
                                 reads=[("ae", p), ("aeh", p), ("lin", p)], writes=[("actb", j, half)])
                        if pend_tail is not None:
                            pend_tail()
                        pend_tail = tail
                if pend_tail is not None:
                    pend_tail()
                    pend_tail = None
                if half == 1:
                    dma("sp", oconv[l], SAVE.rearrange("p (j r) -> p j r", r=2), reads=[("save", j) for j in range(NJ)], key="oconv")
                    dma("sp", osconv[l], SAVES.rearrange("p (j b r) -> p j b r", b=2, r=2), reads=[("saves", j) for j in range(NJ)], key="osconv")
                atoks = [("actb", j, half) for j in range(NJ)]
                nxt = list(range(8, 17)) if half == 0 else []
                for ti_, i in enumerate(tiles):
                    rows = tile_rows(i)
                    c0, n = tcols(i)
                    lc0 = c0 - base
                    defer_b = []
                    for i2 in (nxt[ti_:ti_ + 1] if ti_ < 7 else nxt[7:8]):
                        rows2 = tile_rows(i2)
                        c02, n2 = tcols(i2)
                        defer_b.append(prenorm_tile(X[0:rows2, i2, :], rows2, ("x", i2), prm(l, "gpf", 8), H23(c02 - 1024, n2),
                                                    ("h2", i2 - 8), 0, hb=HBF[0:rows2, :], tpool="t", hb_toks=["hbf"], defer=True))
                        if "nodefer" in DBG:
                            defer_b.pop()()
                    for hf in range(2):
                        kd = psnext("d")
                        mm_group(None, [(PS[kd][0:rows, :], ACTB(j, lc0, n), WDN(j, 512 * hf, 512), j == 0, j == NJ - 1) for j in range(NJ)],
                                 reads=atoks + [("wdn", 0), ("wdn", 1)], writes=[("ps", kd)])
                        if hf == 0:
                            S.op("act", lambda e, kd=kd, rows=rows: e.copy(OSTG[0:rows, 0:512], PS[kd][0:rows, :]),
                                 reads=[("ps", kd)], writes=["ostg0"])
                        else:
                            S.op("dve", lambda e, kd=kd, rows=rows: e.tensor_copy(OSTG[0:rows, 512:1024], PS[kd][0:rows, :]),
                                 reads=[("ps", kd)], writes=["ostg1"])
                    for fb in defer_b:
                        fb()
                    post_norm_residual(OSTG[0:rows, :], rows, i, HBF[0:rows, :], ["hbf"], GPF, ["gpf"], ["ostg0", "ostg1"])
                    if l == L - 1:
                        if i < 16:
                            dma("sp", y_p[128 * i:128 * i + 128, :], X[:, i, :], reads=[("x", i)], key=("x", i))
                        else:
                            dma("sp", y_s, X[0:32, 16, :], reads=[("x", 16)], key=("x", 16))
                if half == 0:
                    prenorm_tile(X[0:32, 16, :], 32, ("x", 16), prm(l, "gpf", 8), H23(1024, 32), ("h2", 8), 0,
                                 hb=HBF[0:32, :], tpool="t", hb_toks=["hbf"])
        RFOX = lambda i, rows=128: SM[0:rows, 100 + i:101 + i]
        RSGU = lambda i, rows=128: SM[0:rows, 120 + i:121 + i]
        RMEM = lambda i, rows=128: SM[0:rows, 140 + i:141 + i]
        SSFOX = lambda i, rows=128: SM[0:rows, 160 + i:161 + i]
        SSMEM = lambda i, rows=128: SM[0:rows, 180 + i:181 + i]
        ALLHT = [("ht", i) for i in range(17)]

        class NSQ:
            def __init__(self):
                self.p1 = None
                self.p2 = None
                self.p3 = None

            def step(self, new_p1):
                if self.p3 is not None:
                    self.p3()
                    self.p3 = None
                if self.p2 is not None:
                    self.p3 = self.p2()
                    self.p2 = None
                if self.p1 is not None:
                    self.p2 = self.p1()
                self.p1 = new_p1

            def flush(self):
                self.step(None)
                self.step(None)
                self.step(None)
        nsq = NSQ()

        def wload(slot, src2d, ncol, extra_writes=()):
            dma("pool", WS3(slot, ncol), src2d.rearrange("(c p) n -> p c n", p=128),
                writes=[("ws", slot)] + list(extra_writes), key=("ws", slot))

        def KAS(b, c0, n, p0=0, p1=67):
            return bf(WS_O[0] + b * 3200 + c0 * 2, n, p0, p1)

        def VAS(b, j):
            return bf(WS_O[0] + b * 3200 + 2080 + j * 130, 65)

        def VAS3(b):
            return AR[:, (WS_O[0] + b * 3200 + 2080) // 2:(WS_O[0] + b * 3200 + 2080) // 2 + 520].rearrange("p (j k) -> p j k", k=65)

        SC_TOKS = [("sc", b, x) for b in range(2) for x in ("k", "v", "r", "one")]

        def norm_store_fm(obank, I, grp, chunk, p0, ssq4, l):
            rlrow = f32(SCR_O + 6 * 1024, 512, 64, 65)
            hfT = f32(SCR_O + 6 * 1024, 512, 0, 64)
            S.op("act", lambda e: e.activation(rlrow, PS[obank][64:65, 0:512], AF.Ln), reads=[("ps", obank)], writes=[("scr", 6, "r")])
            S.op("act", lambda e: e.activation(rlrow, rlrow, AF.Exp, scale=-1.0), reads=[("scr", 6, "r")], writes=[("scr", 6, "r")])
            kb = psnext("g")
            S.op("pe", lambda e: e.matmul(PS[kb][0:64, 0:512], ONEROW[64:65, 0:64], rlrow, start=True, stop=True, skip_group_check=True),
                 reads=[("scr", 6, "r"), "onerow"], writes=[("ps", kb)])
            S.op("dve", lambda e: e.tensor_copy(hfT, PS[kb][0:64, 0:512]), reads=[("ps", kb)], writes=scr(6, 7))
            S.op("dve", lambda e: e.tensor_tensor(hfT, PS[obank][0:64, 0:512], hfT, ALU.mult), reads=[("ps", obank)] + scr(6, 7), writes=scr(6, 7))
            return lambda: norm_store_fm2(I, grp, chunk, p0, ssq4, l)

        def norm_store_fm2(I, grp, chunk, p0, ssq4, l):
            hfT = f32(SCR_O + 6 * 1024, 512, 0, 64)
            sqb = bf(SCR_O + 4 * 1024, 512, 0, 64)
            S.op("act", lambda e: e.activation(MX(chunk, 512 * I, 512, p0, p0 + 64), hfT, AF.Copy,
                                               scale=prm(l, "ggo", 1, p0, p0 + 64, chunk)),
                 reads=scr(6, 7) + ["prm"], writes=[("mx", chunk, 4 * I + s_, p0) for s_ in range(4)])
            S.op("pool", lambda e: e.tensor_tensor(sqb, hfT, hfT, ALU.mult), reads=scr(6, 7), writes=scr(4))
            return lambda: norm_store_fm3(I, grp, ssq4)

        def norm_store_fm3(I, grp, ssq4):
            sqb = bf(SCR_O + 4 * 1024, 512, 0, 64)
            kt = psnext("t")
            ones_col = bf(VA_O + 64 * 2, 1, 0, 64)

            def ssmm(e):
                ins = None
                for s_ in range(4):
                    ins = e.matmul(PS[kt][:, s_:s_ + 1], sqb[:, 128 * s_:128 * s_ + 128], ones_col, start=True, stop=True,
                                   skip_group_check=True)
                return ins
            S.op("pe", ssmm, reads=scr(4) + ["va_ones"], writes=[("ps", kt)])
            toks4 = [("ssq", grp, 4 * I + s_) for s_ in range(4)]
            S.op("dve", lambda e: e.tensor_tensor(ssq4, ssq4, PS[kt][:, 0:4], ALU.add), reads=[("ps", kt)] + toks4, writes=toks4)

        def norm_and_store_block(obank, I, grp, chunk, p0, ssq4, l):
            O4 = PS[obank][:, 0:260].rearrange("p (s k) -> p s k", k=65)
            rl4 = SM[:, SM_T + 10:SM_T + 14]
            sq4 = SM[:, SM_T + 14:SM_T + 18]
            S.op("dve", lambda e: e.reciprocal(rl4.unsqueeze(2), O4[:, :, 64:65]), reads=[("ps", obank)], writes=["rl4"])
            hf4 = SCRf(6, 256)
            hb4 = SCRb(7, 256)
            S.op("dve", lambda e: e.tensor_tensor(hf4.rearrange("p (s k) -> p s k", k=64), O4[:, :, 0:64],
                                                  rl4.unsqueeze(2).to_broadcast([128, 4, 64]), ALU.mult),
                 reads=[("ps", obank), "rl4"], writes=scr(6))

            def sqs(e):
                ins = None
                for s_ in range(4):
                    ins = e.activation(hb4[:, 64 * s_:64 * s_ + 64], hf4[:, 64 * s_:64 * s_ + 64], AF.Square,
                                       accum_out=sq4[:, s_:s_ + 1])
                return ins
            S.op("act", sqs, reads=scr(6), writes=scr(7) + ["sq4"])
            toks4 = [("ssq", grp, 4 * I + s_) for s_ in range(4)]
            S.op("dve", lambda e: e.tensor_tensor(ssq4, ssq4, sq4, ALU.add), reads=["sq4"] + toks4, writes=toks4)
            S.op("act", lambda e: e.copy(hb4, hf4), reads=scr(6, 7), writes=scr(7))
            kt = psnext("t")

            def tr(e):
                ins = None
                for s_ in range(4):
                    ins = e.transpose(PSB[kt][p0:p0 + 64, 128 * s_:128 * s_ + 128], hb4[:, 64 * s_:64 * s_ + 64], IDENT[:, :])
                return ins
            S.op("pe", tr, reads=scr(7) + ["cst"], writes=[("ps", kt)])
            S.op("dve", lambda e: e.tensor_scalar(MX(chunk, 512 * I, 512, p0, p0 + 64), PSB[kt][p0:p0 + 64, 0:512],
                                                  prm(l, "ggo", 1, p0, p0 + 64, chunk), None, ALU.mult),
                 reads=[("ps", kt), "prm"], writes=[("mx", chunk, 4 * I + s_, p0) for s_ in range(4)])

        def norm_and_store(obank, ocol, rows, tile_i, grp, chunk, p0, ssq_acc, l):
            rl = SM[0:rows, SM_T + 4:SM_T + 5]
            S.op("dve", lambda e: e.reciprocal(rl, PS[obank][0:rows, ocol + 64:ocol + 65]),
                 reads=[("ps", obank)], writes=["rl"])
            hf = SCRf(5, 64, 0, rows)
            hb = SCRb(5, 64, 0, rows, off=256)
            S.op("dve", lambda e: e.tensor_scalar(hf, PS[obank][0:rows, ocol:ocol + 64], rl, None, ALU.mult),
                 reads=[("ps", obank), "rl"], writes=scr(5))
            sq = SM[0:rows, SM_T + 5:SM_T + 6]
            S.op("act", lambda e: e.activation(hb, hf, AF.Square, accum_out=sq), reads=scr(5), writes=scr(5) + ["sq"])
            S.op("dve", lambda e: e.tensor_tensor(ssq_acc, ssq_acc, sq, ALU.add),
                 reads=["sq", ("ssq", grp, tile_i)], writes=[("ssq", grp, tile_i)])
            S.op("act", lambda e: e.copy(hb, hf), reads=scr(5), writes=scr(5))
            kt = psnext("t")
            c0, n = tcols(tile_i)
            S.op("pe", lambda e: e.transpose(PSB[kt][p0:p0 + 64, 0:rows], hb, IDENT[0:rows, 0:rows]),
                 reads=scr(5) + ["cst"], writes=[("ps", kt)])
            S.op("dve", lambda e: e.tensor_scalar(MX(chunk, c0, n, p0, p0 + 64), PSB[kt][p0:p0 + 64, 0:rows],
                                                  prm(l, "ggo", 1, p0, p0 + 64, chunk), None, ALU.mult),
                 reads=[("ps", kt), "prm"], writes=[("mx", chunk, tile_i, p0)])

        try:
            for l in range(L):
                psr.clear()
                psr.update({"g": [0, 1], "s": [2, 3, 4], "o": [5, 6], "t": [7], "w": [0, 1, 2, 3, 4, 5]})
                for kk in psr:
                    psi[kk] = 0
                S.op("dve", lambda e: e.memset(
                    AR[:, VA_O // 2: VA_O // 2 + 17 * 8 * 65].rearrange("p (a k) -> p a k", k=65)[:, :, 64:65], 1.0),
                    writes=["va_ones"])
                S.op("dve", lambda e: e.memset(MVAt[:, :].rearrange("p (a k) -> p a k", k=65)[:, :, 64:65], 1.0),
                     writes=["mva_ones"])
                S.op("dve", lambda e: e.memset(MKTt[64:128, :], 0.0), writes=["mkt_zero"])
                wload(0, w_in[l][:, 1024:1536], 512, extra_writes=SC_TOKS if l > 0 else ())
                wload(1, w_in[l][:, 512:1024], 512)
                wload(2, w_in[l][:, 1536:2056], 520)

                SGC = MX_O + 12288
                WSTl = bf(SGC, 512)
                GSl = f32(SGC + 1024, 256)
                dma("pool", WSTl.rearrange("p (g i) -> p g i", g=4), wsT[l].rearrange("g j i -> j g i"),
                    writes=[("sguc", 0)], key="wst")
                S.op("dve", lambda e: e.memset(bf(SGC, 512, 64, 128).rearrange("p (g i) -> p g i", g=4)[:, :, 0:64], 0.0),
                     reads=[("sguc", 0)], writes=[("sguc", 0)])
                dma("sp", GSl, g_sgu[l:l + 1, :].broadcast_to([128, 256]), writes=[("sguc", 1)], key="gs")

                def sgu_ops(i):
                    st_ = 0 if "setA" in DBG else i % 2
                    base = MX_O if st_ == 0 else MX_O + 6144
                    tk = "sguA" if st_ == 0 else "sguB"
                    Sb = lambda g, n, p0=0, p1=128: bf(base + g * 1024, n, p0, p1)
                    Sf = lambda g, n, p0=0, p1=128: f32(base + g * 1024, n, p0, p1)
                    sc = lambda *gs: [(tk, g) for g in gs]
                    rows = tile_rows(i)
                    c0, n = tcols(i)
                    s1, s2 = [], []
                    k = psnext("s")
                    z = Sf(0, 512, 0, rows)
                    t1 = Sf(2, 512, 0, rows)
                    ssv = SM[0:rows, SM_T + 20 + 2 * st_:SM_T + 21 + 2 * st_]
                    sss = SM[0:rows, SM_T + 21 + 2 * st_:SM_T + 22 + 2 * st_]
                    ssvt, ssst = ("ssv", st_), ("sss", st_)
                    vvf = Sf(3, 256, 0, rows)
                    vvb = Sb(2, 256, 0, rows)
                    sg = Sf(4, 256, 0, rows)
                    sgb = Sb(5, 256, 0, rows)
                    s1.append(lambda: mm_group(None, [(PS[k][0:rows, :], HT(c, c0, n), WS(2, c, 8, 512), c == 0, c == 7) for c in range(8)],
                                               reads=[("ht", i), ("ws", 2)], writes=[("ps", k)]))
                    s1.append(lambda: S.op("dve", lambda e: e.tensor_copy(z, PS[k][0:rows, :]), reads=[("ps", k)], writes=sc(0, 1)))
                    s1.append(lambda: S.op("dve", lambda e: e.tensor_tensor(t1, z, z, ALU.mult), reads=sc(0, 1), writes=sc(2, 3)))
                    s1.append(lambda: S.op("dve", lambda e: e.tensor_scalar(t1, t1, 0.044715, 1.0, ALU.mult, ALU.add), reads=sc(2, 3), writes=sc(2, 3)))
                    s1.append(lambda: S.op("dve", lambda e: e.tensor_tensor(t1, t1, z, ALU.mult), reads=sc(0, 1, 2, 3), writes=sc(2, 3)))
                    s1.append(lambda: S.op("act", lambda e: e.activation(t1, t1, AF.Exp, scale=-1.5957691216057308), reads=sc(2, 3), writes=sc(2, 3)))
                    s1.append(lambda: S.op("act", lambda e: e.activation(t1, t1, AF.Ln, bias=1.0, scale=1.0), reads=sc(2, 3), writes=sc(2, 3)))
                    s1.append(lambda: S.op("act", lambda e: e.activation(t1, t1, AF.Exp, scale=-1.0), reads=sc(2, 3), writes=sc(2, 3)))
                    s1.append(lambda: S.op("dve", lambda e: e.tensor_tensor(z, z, t1, ALU.mult), reads=sc(0, 1, 2, 3), writes=sc(0, 1)))
                    s1.append(lambda: S.op("act", lambda e: e.activation(t1[:, 0:256], z[:, 256:512], AF.Square, accum_out=ssv),
                                           reads=sc(0, 1), writes=sc(2) + [ssvt]))
                    s1.append(lambda: S.op("act", lambda e: e.activation(ssv, ssv, AF.Ln, bias=EPS, scale=1.0 / 256), reads=[ssvt], writes=[ssvt]))
                    s1.append(lambda: S.op("act", lambda e: e.activation(ssv, ssv, AF.Exp, scale=-0.5), reads=[ssvt], writes=[ssvt]))
                    s1.append(lambda: S.op("dve", lambda e: e.scalar_tensor_tensor(vvf, z[:, 256:512], ssv, GSl[0:rows, :], ALU.mult, ALU.mult),
                                           reads=sc(0, 1) + [("sguc", 1)] + [ssvt], writes=sc(3)))
                    s1.append(lambda: S.op("pool", lambda e: e.tensor_copy(vvb, vvf), reads=sc(3), writes=sc(2)))
                    if i == 16:
                        s1.append(lambda: dma("sp", ogv[l], vvf, reads=sc(3), key="ogv"))
                    k2 = psnext("s")
                    if i < 16:
                        items = [(PS[k2][0:128, 64 * g:64 * g + 64], WSTl[:, 128 * g:128 * g + 128], vvb[:, 64 * g:64 * g + 64], True, True) for g in range(4)]
                        bsap = prm(l, "bs", 4)
                        wtok = [("sguc", 0)]
                    else:
                        items = [(PS[k2][0:32, 64 * g:64 * g + 64], WSTS(l, g), vvb[:, 64 * g:64 * g + 64], True, True) for g in range(4)]
                        bsap = prm(l, "bss", 4, 0, 32)
                        wtok = ["wsts"] + [("wsts", l, b) for b in range(2)]
                    s1.append(lambda: mm_group(None, items, reads=sc(2) + wtok, writes=[("ps", k2)]))
                    s2.append(lambda: S.op("dve", lambda e: e.tensor_tensor(
                        sg.rearrange("p (g d) -> p g d", g=4), PS[k2][0:rows, 0:256].rearrange("p (g d) -> p g d", g=4),
                        bsap.unsqueeze(2).to_broadcast([rows, 4, 64]), ALU.add),
                        reads=[("ps", k2), "prm"], writes=sc(4)))
                    s2.append(lambda: S.op("dve", lambda e: e.tensor_tensor(sg, sg, z[:, 0:256], ALU.mult), reads=sc(0, 1, 4), writes=sc(4)))
                    s2.append(lambda: S.op("act", lambda e: e.activation(sgb, sg, AF.Square, accum_out=sss),
                                           reads=sc(4), writes=sc(5) + [ssst]))
                    s2.append(lambda: S.op("act", lambda e: e.activation(sss, sss, AF.Ln, bias=EPS, scale=1.0 / 256), reads=[ssst], writes=[ssst]))
                    s2.append(lambda: S.op("act", lambda e: e.activation(RSGU(i, rows), sss, AF.Exp, scale=-0.5),
                                           reads=[ssst], writes=[("rsgu", i)]))
                    s2.append(lambda: S.op("pool", lambda e: e.tensor_copy(sgb, sg), reads=sc(4, 5), writes=sc(5)))

                    def trs():
                        kt = psnext("o")

                        def tr(e):
                            ins = None
                            for cc in range(2):
                                ins = e.transpose(PSB[kt][:, cc * 128:cc * 128 + rows], sgb[:, cc * 128:(cc + 1) * 128], IDENT[0:rows, 0:rows])
                            return ins
                        S.op("pe", tr, reads=sc(5) + ["cst"], writes=[("ps", kt)])
                        for cc in range(2):
                            S.op("dve", lambda e, cc=cc, l=l: e.tensor_scalar(
                                MX(4 + cc, c0, n), PSB[kt][:, cc * 128:cc * 128 + rows], prm(l, "ggo", 1, 0, 128, 4 + cc), None, ALU.mult),
                                reads=[("ps", kt), "prm"], writes=[("mx", 4 + cc, i)])
                    s2.append(trs)
                    return s1, s2

                def interleave(a, b):
                    for q in range(max(len(a), len(b))):
                        if q < len(a):
                            a[q]()
                        if q < len(b):
                            b[q]()

                def vk_tile(i):
                    rows = tile_rows(i)
                    c0, n = tcols(i)
                    for which, slot, oprompt, osample, sg in (("v", 0, ofv, osv, 4), ("k", 1, ofk, osk, 6)):
                        k = psnext("g")
                        mm_group(None, [(PS[k][0:rows, :], HT(c, c0, n), WS(slot, c, 0, 512), c == 0, c == 7)
                                        for c in range(8)],
                                 reads=[("ht", i), ("ws", slot)], writes=[("ps", k)])
                        stg = SCRf(sg, 512, 0, rows)
                        S.op(("act" if which == "v" else "dve"), (lambda e, stg=stg, k=k, rows=rows: e.copy(stg, PS[k][0:rows, :])) if which == "v"
                             else (lambda e, stg=stg, k=k, rows=rows: e.tensor_copy(stg, PS[k][0:rows, :])),
                             reads=[("ps", k)], writes=scr(sg, sg + 1))
                        if which == "v":
                            va3 = AR[0:rows, VA_O // 2 + i * 520: VA_O // 2 + (i + 1) * 520].rearrange(
                                "p (h k) -> p h k", k=65)[:, :, 0:64]
                            S.op("dve", lambda e, va3=va3, k=k, rows=rows: e.tensor_copy(
                                va3, PS[k][0:rows, :].rearrange("p (h k) -> p h k", k=64)),
                                reads=[("ps", k), "va_ones"], writes=[("va", i)])
                        dst = oprompt[l, 128 * i:128 * i + 128, :] if i < 16 else osample[l, :, :]
                        dma("sp", dst, stg, reads=scr(sg, sg + 1), key=("stg", sg))

                sgu_pipe = {"B": [], "C": []}

                def sgu_step(t):
                    if t is not None:
                        s1_, s2_ = sgu_ops(t)
                        sA, sB, sC = s1_[:-1], [s1_[-1]] + s2_[:-1], [s2_[-1]]
                    else:
                        sA, sB, sC = [], [], []
                    interleave(sA, sgu_pipe["B"])
                    for f_ in sgu_pipe["C"]:
                        f_()
                    sgu_pipe["C"] = sgu_pipe["B_c"] if "B_c" in sgu_pipe else []
                    sgu_pipe["B"] = sB
                    sgu_pipe["B_c"] = sC

                def pn_a(i):
                    rows = tile_rows(i)
                    c0, n = tcols(i)
                    return prenorm_tile(X[0:rows, i, :], rows, ("x", i), prm(l, "gpm", 8), HT3(c0, n), ("ht", i), i % 2, defer=True)
                pn_b = {0: pn_a(0)}
                for i in range(17):
                    if i + 1 < 17:
                        pn_b[i + 1] = pn_a(i + 1)
                    pn_b.pop(i)()
                    if i >= 1:
                        vk_tile(i - 1)
                        sgu_step(i - 1)
                vk_tile(16)
                sgu_step(16)
                sgu_step(None)
                sgu_step(None)
                S.op("dve", lambda e: e.memset(bf(MX_O + 15 * 1024, 2), 0.0),
                     writes=[("sguA", g) for g in range(6)] + [("sguB", g) for g in range(6)] + [("sguc", 0), ("sguc", 1)]
                     + [("mx", c, i, p0) for c in (0, 1, 2, 3) for i in range(17) for p0 in (0, 64)])

                _chk("vk%d" % l)
                NB = SM[0:8, 210:211]
                S.op("dve", lambda e, l=l: e.tensor_scalar(NB, prm(l, "bf", 1, 0, 8), -1.0, None, ALU.mult),
                     reads=["prm"], writes=["cw", "nb"])
                for bi, (c0, n) in enumerate(BLKS):
                    k = psnext("g")
                    mm_group(None, [(PS[k][0:8, 0:n], WS(2, c, 0, 8), HT(c, c0, n), c == 0, c == 7) for c in range(8)],
                             reads=ALLHT + [("ws", 2)], writes=["cw", ("ps", k)])
                    tmp = SCRf(0, 512, 96, 104)
                    S.op("act", lambda e, k=k, n=n, tmp=tmp: e.activation(tmp[:, 0:n], PS[k][0:8, 0:n], AF.Exp, bias=NB, scale=-1.0),
                         reads=[("ps", k), "nb"], writes=scr(0, 1))
                    S.op("act", lambda e, n=n, tmp=tmp: e.activation(tmp[:, 0:n], tmp[:, 0:n], AF.Ln, bias=1.0, scale=1.0),
                         reads=scr(0, 1), writes=scr(0, 1))
                    S.op("dve", lambda e, c0=c0, n=n, tmp=tmp: e.tensor_scalar(CT(c0, n), tmp[:, 0:n], -1.0, None, ALU.mult),
                         reads=scr(0, 1), writes=["cw", ("ct", bi)])
                dma("sp", olf[l], CT(0, 2048), reads=[("ct", b) for b in range(4)], writes=["cw"], key="olf")
                dma("sp", oslf[l], CT(2048, 32), reads=[("ct", 4)], writes=["cw"], key="olf2")
                SLF = SM[96:104, 220:252]
                S.op("dve", lambda e: e.tensor_copy(SLF, CT(2048, 32)), reads=[("ct", 4)], writes=["cw", "slf"])
                S.op("dve", lambda e: e.tensor_tensor_scan(CT(0, 2048), ONE8.to_broadcast([8, 2048]), CT(0, 2048), 0.0, ALU.mult, ALU.add),
                     reads=[("ct", b) for b in range(4)] + ["one"], writes=["cw", "ctp"])
                S.op("dve", lambda e: e.tensor_copy(HI(0, 2048), CT(0, 2048)), reads=["ctp"], writes=["cw", "hi"])
                S.op("dve", lambda e: e.tensor_tensor(CT(0, 2048), CT(0, 2048), HI(0, 2048), ALU.subtract),
                     reads=["ctp", "hi"], writes=["cw", "ctp"])
                S.op("dve", lambda e: e.tensor_copy(LO(0, 2048), CT(0, 2048)), reads=["ctp"], writes=["cw", "lo"])
                dma("sp", cscr_p[:, 0, :], HI(0, 2048), reads=["hi"], writes=["cw", "cscr_p"], key="cscr0")
                dma("sp", cscr_p[:, 1, :], LO(0, 2048), reads=["lo"], writes=["cw", "cscr_p2"], key="cscr1")
                CS = lambda b, c0, n: CT(b * 1040 + c0, n)
                for b in range(2):
                    dma("sp", CS(b, 0, 1024), clfT[l, b], reads=["ctp", "lo", ("ct", 4), "slf"], writes=["cw", ("cs", b)], key=("csl", b))
                    S.op("dve", lambda e, b=b: e.tensor_copy(CS(b, 1024, 16), SM[96:104, 220 + 16 * b:236 + 16 * b]),
                         reads=["slf", "ctp", "lo", ("ct", 4)], writes=["cw", ("cs2", b)])
                    S.op("dve", lambda e, b=b: e.tensor_tensor_scan(CS(b, 0, 1040), ONE8.to_broadcast([8, 1040]), CS(b, 0, 1040), 0.0, ALU.mult, ALU.add),
                         reads=[("cs", b), ("cs2", b), "one"], writes=["cw", ("csc", b)])
                S.op("dve", lambda e: e.tensor_copy(HI(0, 2080), CT(0, 2080)),
                     reads=[("csc", 0), ("csc", 1), "cscr_p", "cscr_p2"], writes=["cw", "hi"])
                S.op("dve", lambda e: e.tensor_tensor(CT(0, 2080), CT(0, 2080), HI(0, 2080), ALU.subtract),
                     reads=["hi"], writes=["cw", ("csc", 0), ("csc", 1)])
                S.op("dve", lambda e: e.tensor_copy(LO(0, 2080), CT(0, 2080)), reads=[("csc", 0), ("csc", 1), "cscr_p2"], writes=["cw", "lo"])
                for b in range(2):
                    dma("sp", cscr_s[:, 0, b, :], HI(b * 1040, 1040), reads=["hi"], writes=["cw", ("cscr_s", b, 0)], key=("cscrs", b, 0))
                    dma("sp", cscr_s[:, 1, b, :], LO(b * 1040, 1040), reads=["lo"], writes=["cw", ("cscr_s", b, 1)], key=("cscrs", b, 1))

                _chk("fg%d" % l)
                _chk("sgu%d" % l)
                S.op("dve", lambda e: e.memset(bf(QK_O, 4 * NT, 64, 128), 0.0),
                     writes=["cw", "hi", "lo"] + [("qkc", s_) for s_ in range(4)])
                for s_ in (0, 2):
                    S.op("dve", lambda e, s_=s_: e.memset(QK(s_, 0, NT, 64, 67), -1.0), writes=[("qkc", s_)])
                for s_ in (1, 3):
                    S.op("dve", lambda e, s_=s_: e.memset(QK(s_, 0, NT, 64, 65), 1.0), writes=[("qkc", s_)])
                wload(0, w_mkv[l], 512)
                MEMT3 = AR[:, (SCR_O + 4096) // 2:(SCR_O + 4096) // 2 + 2048].rearrange("p (c t) -> p c t", c=8)
                MEMT = lambda c, c0, n: bf(SCR_O + 4096 + (c * 256 + c0) * 2, n)
                memx = f32(WS_O[2], 1024)
                for mt in range(2):
                    dma("sp", memx, mem[128 * mt:128 * mt + 128, :], writes=[("ws", 2)], key=("ws", 2))
                    prenorm_tile(memx, 128, ("ws", 2), prm(l, "gmem", 8), MEMT3[:, :, 128 * mt:128 * mt + 128],
                                 ("memt", mt), 0, extra_writes=scr(4, 5, 6, 7))
                memt_toks = [("memt", 0), ("memt", 1)] + scr(4, 5, 6, 7)
                stgk = f32(WS_O[2], 512)
                for mt in range(2):
                    k = psnext("g")
                    mm_group(None, [(PS[k][:, :], MEMT(c, 128 * mt, 128), WS(0, c, 0, 512), c == 0, c == 7) for c in range(8)],
                             reads=memt_toks + [("ws", 0)], writes=[("ps", k)])
                    S.op("act", lambda e, k=k: e.copy(stgk, PS[k][:, :]), reads=[("ps", k)], writes=[("ws", 2)])
                    S.op("dve", lambda e, k=k, mt=mt: e.tensor_copy(
                        MVAt[:, mt * 260:(mt + 1) * 260].rearrange("p (h k) -> p h k", k=65)[:, :, 0:64],
                        PS[k][:, 256:512].rearrange("p (h k) -> p h k", k=64)),
                        reads=[("ps", k), "mva_ones"], writes=[("mva", mt)])
                    dma("sp", omk[l, 128 * mt:128 * mt + 128, :], stgk[:, 0:256], reads=[("ws", 2)], key="omk")
                    dma("sp", omv[l, 128 * mt:128 * mt + 128, :], stgk[:, 256:512], reads=[("ws", 2)], key="omv")
                for pr in range(2):
                    k = psnext("g")
                    mm_group(None, [(PS[k][:, 0:256], WS(0, c, 128 * pr, 128), MEMT(c, 0, 256), c == 0, c == 7) for c in range(8)],
                             reads=memt_toks + [("ws", 0)], writes=[("ps", k)])
                    for hh in range(2):
                        h = 2 * pr + hh
                        S.op("act", lambda e, k=k, hh=hh, h=h: e.copy(MKTt[0:64, h * 256:(h + 1) * 256], PS[k][64 * hh:64 * hh + 64, 0:256]),
                             reads=[("ps", k)], writes=[("mkt", h)])
                wload(2, w_in[l][:, 2056:2312], 256)
                for b in range(2):
                    S.op("dve", lambda e, b=b: e.memset(KAS(b, 0, 1040, 64, 65), 1.0), reads=[("ws", 0)], writes=[("ws", 0), ("sc", b, "one")])
                    S.op("dve", lambda e, b=b: e.memset(VAS3(b)[:, :, 64:65], 1.0), reads=[("ws", 0)], writes=[("ws", 0), ("sc", b, "one")])

                S.op("dve", lambda e: e.memset(SM[:, 160:200], 0.0),
                     writes=[("ssq", g, i) for g in (0, 2) for i in range(17)])

                _chk("memkv%d" % l)
                for pr in range(2):
                    for bi, (c0, n) in enumerate(BLKS):
                        k = psnext("g")
                        mm_group(None, [(PS[k][:, 0:n], WS(2, c, 128 * pr, 128), HT(c, c0, n), c == 0, c == 7) for c in range(8)],
                                 reads=ALLHT + [("ws", 2)], writes=[("ps", k)])
                        for hh in range(2):
                            S.op("act", lambda e, k=k, hh=hh, c0=c0, n=n: e.activation(
                                QK(2 * hh, c0, n, 0, 64), PS[k][64 * hh:64 * hh + 64, 0:n], AF.Copy, scale=0.125),
                                reads=[("ps", k)], writes=[("qa", hh, bi)])
                    for hh in range(2):
                        h = 2 * pr + hh
                        for I in range(4):
                            ob = psnext("o")
                            kts = [dict(ka=MKTt[0:128, h * 256 + 128 * j:h * 256 + 128 * j + 128], nk=128,
                                        va=bf(MVA_O + ((j * 4 + h) * 65) * 2, 128), lo=0, hi=512, pv_lo=0,
                                        mask=None, pt="rot", reads=[("mkt", h), ("qa", hh, I), "mkt_zero", ("qkc", 0), ("qkc", 2)],
                                        vreads=[("mva", j), "mva_ones"])
                                   for j in range(2)]
                            osubs = [(PS[ob][:, 65 * s:65 * s + 65], 128 * s, 128 * s + 128, 0, 1) for s in range(4)]
                            attend(lambda c0, n, hh=hh, I=I: QK(2 * hh, 512 * I + c0, n, 0, 128), 128, kts, "fm", ob, "mem")
                            nsq.step(lambda ob=ob, I=I, pr=pr, hh=hh: norm_store_fm(ob, I, 2, 6 + pr, 64 * hh, SM[:, 180 + 4 * I:184 + 4 * I], l))
                        ob = psnext("g")
                        kts = []
                        for b in range(2):
                            dma("pool", KAS(b, 0, 256, 0, 64), cmkT[l, b, h], reads=[("ws", 0)], writes=[("sc", b, "k")], key=("sck", b))
                            dma("pool", VAS3(b)[:, 0:2, 0:64],
                                cmv[l, b].rearrange("(j p) (h d) -> p j h d", p=128, h=4)[:, :, h, :],
                                reads=[("ws", 0)], writes=[("sc", b, "v")], key=("scv", b), slow=True)
                            for j in range(2):
                                kts.append(dict(ka=KAS(b, 128 * j, 128, 0, 64), nk=128, va=VAS(b, j),
                                                lo=16 * b, hi=16 * b + 16, pv_lo=0, mask=None, pt=3 * b + (j % 3),
                                                reads=[("sc", b, "k"), ("qa", hh, 4)], vreads=[("sc", b, "v"), ("sc", b, "one")]))
                        osubs = [(PS[ob][0:32, 0:65], 0, 32, 0, len(kts) - 1)]
                        attend(lambda c0, n, hh=hh: QK(2 * hh, 2048 + c0, n, 0, 64), 64, kts, osubs, ob, "mems")
                        flush_pv()
                        norm_and_store(ob, 0, 32, 16, 2, 6 + pr, 64 * hh, SSMEM(16, 32), l)

                flush_pv()
                nsq.flush()
                _chk("memattn%d" % l)
                wload(2, w_in[l][:, 0:512], 512)
                for pr in range(4):
                    for bi, (c0, n) in enumerate(BLKS):
                        for qk, slot in ((0, 2), (1, 1)):
                            k = psnext("g")
                            mm_group(None, [(PS[k][:, 0:n], WS(slot, c, 128 * pr, 128), HT(c, c0, n), c == 0, c == 7) for c in range(8)],
                                     reads=ALLHT + [("ws", slot)], writes=[("ps", k)])
                            for hh in range(2):
                                if qk == 0:
                                    S.op("act", lambda e, k=k, hh=hh, c0=c0, n=n: e.activation(
                                        QK(2 * hh, c0, n, 0, 64), PS[k][64 * hh:64 * hh + 64, 0:n], AF.Copy, scale=0.125),
                                        reads=[("ps", k)], writes=[("qa", hh, bi)])
                                else:
                                    S.op("dve", lambda e, k=k, hh=hh, c0=c0, n=n: e.tensor_copy(
                                        QK(2 * hh + 1, c0, n, 0, 64), PS[k][64 * hh:64 * hh + 64, 0:n]),
                                        reads=[("ps", k)], writes=[("ka", hh, bi)])
                    for hh in range(2):
                        h = 2 * pr + hh
                        dma("sp", QK(2 * hh, 0, 2048, 64, 65), cscr_p[h:h + 1, 0, :], reads=["cscr_p", ("qkc", 2 * hh)],
                            writes=[("qar", hh, 0)], key=("qar", hh))
                        dma("sp", QK(2 * hh + 1, 0, 2048, 65, 66), cscr_p[h:h + 1, 0, :], reads=["cscr_p", ("qkc", 2 * hh + 1)],
                            writes=[("kar", hh, 0)], key=("kar", hh))
                        dma("sp", QK(2 * hh + 1, 0, 2048, 66, 67), cscr_p[h:h + 1, 1, :], reads=["cscr_p2"],
                            writes=[("kar", hh, 1)], key=("kar", hh))
                        for b in range(2):
                            dma("sp", QK(2 * hh, 2048 + 16 * b, 16, 64, 65), cscr_s[h:h + 1, 0, b, 1024:1040],
                                reads=[("cscr_s", b, 0)], writes=[("qar", hh, 1 + b)], key=("qar", hh))
                            dma("sp", QK(2 * hh + 1, 2048 + 16 * b, 16, 65, 66), cscr_s[h:h + 1, 0, b, 1024:1040],
                                reads=[("cscr_s", b, 0)], writes=[("kar", hh, 2 + b)], key=("kar", hh))
                            dma("sp", QK(2 * hh + 1, 2048 + 16 * b, 16, 66, 67), cscr_s[h:h + 1, 1, b, 1024:1040],
                                reads=[("cscr_s", b, 1)], writes=[("kar", hh, 4 + b)], key=("kar", hh))
                        qar = [("qar", hh, x) for x in range(3)] + [("qkc", 2 * hh)]
                        kar = [("kar", hh, x) for x in range(6)] + [("qkc", 2 * hh + 1)]
                        for I in range(4):
                            ob = psnext("o")
                            kts = []
                            for j in range(4 * I + 4):
                                a = j - 4 * I
                                kts.append(dict(ka=QK(2 * hh + 1, 128 * j, 128, 0, 128), nk=128, va=VA128(j, h),
                                                lo=(128 * a if a >= 0 else 0), hi=512, pv_lo=(128 * a if a >= 0 else 0),
                                                mask=((MASKNEG, 128) if a >= 0 else None), pt="rot",
                                                reads=[("ka", hh, j // 4), ("qa", hh, I)] + qar + kar,
                                                vreads=[("va", j), "va_ones"]))
                            osubs = [(PS[ob][:, 65 * s:65 * s + 65], 128 * s, 128 * s + 128, 0, 4 * I + s) for s in range(4)]
                            attend(lambda c0, n, hh=hh, I=I: QK(2 * hh, 512 * I + c0, n, 0, 128), 128, kts, "fm", ob, "fox")
                            nsq.step(lambda ob=ob, I=I, pr=pr, hh=hh: norm_store_fm(ob, I, 0, pr, 64 * hh, SM[:, 160 + 4 * I:164 + 4 * I], l))
                        ob = psnext("g")
                        kts = []
                        for b in range(2):
                            dma("pool", KAS(b, 0, 1024, 0, 64), ckT[l, b, h], reads=[("ws", 0)], writes=[("sc", b, "k")], key=("sck", b))
                            dma("pool", VAS3(b)[:, :, 0:64],
                                cv[l, b].rearrange("(j p) (h d) -> p j h d", p=128, h=8)[:, :, h, :],
                                reads=[("ws", 0)], writes=[("sc", b, "v")], key=("scv", b), slow=True)
                            dma("sp", KAS(b, 0, 1024, 65, 66), cscr_s[h:h + 1, 0, b, 0:1024], reads=[("cscr_s", b, 0), ("ws", 0)],
                                writes=[("sc", b, "r")], key=("scr", b))
                            dma("sp", KAS(b, 0, 1024, 66, 67), cscr_s[h:h + 1, 1, b, 0:1024], reads=[("cscr_s", b, 1), ("ws", 0)],
                                writes=[("sc", b, "r2")], key=("scr", b))
                            for j in range(8):
                                kts.append(dict(ka=KAS(b, 128 * j, 128), nk=128, va=VAS(b, j),
                                                lo=16 * b, hi=16 * b + 16, pv_lo=0, mask=None, pt=3 * b + (j % 3),
                                                reads=[("sc", b, "k"), ("sc", b, "r"), ("sc", b, "r2"), ("sc", b, "one"), ("qa", hh, 4)] + qar,
                                                vreads=[("sc", b, "v"), ("sc", b, "one")]))
                        kts.append(dict(ka=QK(2 * hh + 1, 2048, 32, 0, 67), nk=32, va=VA(16, h, 32), lo=0, hi=32, pv_lo=0,
                                        mask=(MASKS, 32), pt=6, reads=[("ka", hh, 4), ("qa", hh, 4)] + qar + kar,
                                        vreads=[("va", 16), "va_ones"]))
                        osubs = [(PS[ob][0:32, 0:65], 0, 32, 0, len(kts) - 1)]
                        attend(lambda c0, n, hh=hh: QK(2 * hh, 2048 + c0, n, 0, 67), 67, kts, osubs, ob, "foxs")
                        flush_pv()
                        norm_and_store(ob, 0, 32, 16, 0, pr, 64 * hh, SSFOX(16, 32), l)

                flush_pv()
                nsq.flush()
                _chk("fox%d" % l)
                S.op("act", lambda e: e.activation(SM[:, 100:117], SM[:, 160:177], AF.Ln, bias=EPS, scale=1.0 / 512),
                     reads=[("ssq", 0, i) for i in range(17)], writes=["rfox"])
                S.op("act", lambda e: e.activation(SM[:, 100:117], SM[:, 100:117], AF.Exp, scale=-0.5), reads=["rfox"], writes=["rfox"])
                S.op("act", lambda e: e.activation(SM[:, 140:157], SM[:, 180:197], AF.Ln, bias=EPS, scale=1.0 / 256),
                     reads=[("ssq", 2, i) for i in range(17)], writes=["rmem"])
                S.op("act", lambda e: e.activation(SM[:, 140:157], SM[:, 140:157], AF.Exp, scale=-0.5), reads=["rmem"], writes=["rmem"])

                wload(0, w_out[l][:, 0:512], 512, extra_writes=SC_TOKS)
                wload(1, w_out[l][:, 512:1024], 512)
                GP = SCRf(4, 1024)
                dma("sp", GP, g_post_mix[l:l + 1, :].broadcast_to([128, D]), writes=scr(4, 5, 6, 7), key="gp")
                for i in range(17):
                    rows = tile_rows(i)
                    c0, n = tcols(i)
                    ostg = SCRf(0, 1024, 0, rows)
                    mxr = [("mx", c, i, p0) for c in (0, 1, 2, 3, 6, 7) for p0 in (0, 64)] + [("mx", 4, i), ("mx", 5, i)]
                    last_b = None
                    for hf in range(2):
                        banks = [psnext("w") for _ in range(3)]
                        for gi, (cs, b) in enumerate(zip(((0, 1, 2, 3), (4, 5), (6, 7)), banks)):
                            mm_group(None, [(PS[b][0:rows, :], MX(c, c0, n), WS(hf, c, 0, 512), c == cs[0], c == cs[-1]) for c in cs],
                                     reads=mxr + [("ws", hf)], writes=[("ps", b)])
                        o_h = ostg[:, 512 * hf:512 * hf + 512]
                        S.op("act", lambda e, o_h=o_h, b=banks[0], rows=rows, i=i: e.activation(o_h, PS[b][0:rows, :], AF.Copy, scale=RFOX(i, rows)),
                             reads=[("ps", banks[0]), "rfox"], writes=scr(2 * hf, 2 * hf + 1))
                        S.op("dve", lambda e, o_h=o_h, b=banks[1], rows=rows, i=i: e.scalar_tensor_tensor(o_h, PS[b][0:rows, :], RSGU(i, rows), o_h, ALU.mult, ALU.add),
                             reads=[("ps", banks[1]), ("rsgu", i)] + scr(2 * hf, 2 * hf + 1), writes=scr(2 * hf, 2 * hf + 1))
                        S.op("dve", lambda e, o_h=o_h, b=banks[2], rows=rows, i=i: e.scalar_tensor_tensor(o_h, PS[b][0:rows, :], RMEM(i, rows), o_h, ALU.mult, ALU.add),
                             reads=[("ps", banks[2]), "rmem"] + scr(2 * hf, 2 * hf + 1), writes=scr(2 * hf, 2 * hf + 1))
                        last_b = banks
                    post_norm_residual(ostg, rows, i, bf(WS_O[2], 1024, 0, rows), [("ws", 2)], GP, scr(4, 5, 6, 7), scr(0, 1, 2, 3))

                _chk("wout%d" % l)
                S.barrier()
                ffn_phase(l)
                if l < L - 1:
                    S.barrier()

        except _Stop:
            pass
        S.emit(st)
    return nc


_NC_CACHE = {}


def _prep_inputs(inp):
    f = lambda a: np.ascontiguousarray(np.asarray(a, dtype=np.float32))
    x_prompt = f(inp["x_prompt"]); x_sample = f(inp["x_sample"]); mem_prompt = f(inp["mem_prompt"])
    cfk = f(inp["cache_fox_k"]); cfv = f(inp["cache_fox_v"]); clf = f(inp["cache_fox_logf"])
    cmk = f(inp["cache_mem_k"]); cmvv = f(inp["cache_mem_v"]); cconv = f(inp["cache_ffn_conv"])
    ckT_all = np.ascontiguousarray(cfk.transpose(0, 1, 3, 4, 2))
    cv_all = cfv.reshape(L, 16, 1024, 512)
    clfT_all = np.ascontiguousarray(clf.transpose(0, 1, 3, 2))
    cmkT_all = np.ascontiguousarray(cmk.transpose(0, 1, 3, 4, 2))
    cmv_all = cmvv.reshape(L, 16, 256, 256)
    wsT = np.ascontiguousarray(f(inp["w_spatial"]).transpose(0, 1, 3, 2))
    fm8 = lambda g: f(g).reshape(L, 8, 128).transpose(0, 2, 1)
    b_sp = f(inp["b_spatial"])
    w_dw = f(inp["w_dwconv"]); b_dw = f(inp["b_dwconv"]); b_f = f(inp["b_forget"])
    cst = np.zeros((128, 288), dtype=ml_dtypes.bfloat16)
    cst[:, 0:128] = np.eye(128, dtype=np.float32).astype(ml_dtypes.bfloat16)
    kk, qq = np.meshgrid(np.arange(128), np.arange(128), indexing="ij")
    cst[:, 128:256] = np.where(kk <= qq, 0.0, NEG).astype(ml_dtypes.bfloat16)
    k2, q2 = np.meshgrid(np.arange(32), np.arange(32), indexing="ij")
    ok = (k2 // 16 == q2 // 16) & (k2 % 16 <= q2 % 16)
    cst[0:32, 256:288] = np.where(ok, 0.0, NEG).astype(ml_dtypes.bfloat16)
    wu = f(inp["w_up"])
    wu_a = wu[:, :, 0:FF].reshape(L, 8, 128, NJ, 128)
    wu_l = wu[:, :, FF:2 * FF].reshape(L, 8, 128, NJ, 128)
    w_up_r = np.ascontiguousarray(np.concatenate([wu_a, wu_l], axis=4).transpose(0, 3, 2, 1, 4))
    shared = dict(
        w_in=f(inp["w_in"]), w_mkv=f(inp["w_mem_kv"]), w_out=f(inp["w_out"]), w_up=w_up_r,
        w_dn=f(inp["w_down"]), g_post_mix=f(inp["g_post_mix"]), g_post_ffn=f(inp["g_post_ffn"]),
        g_sgu=f(inp["g_sgu"]), wsT=wsT, cst=cst)
    gpm = fm8(inp["g_pre_mix"]); ggo = fm8(inp["g_group_out"]); gmem = fm8(inp["g_mem"]); gpf = fm8(inp["g_pre_ffn"])
    in_maps = []
    for c in range(NCORES):
        prm = np.zeros((128, NPRM), dtype=np.float32)
        for l in range(L):
            o = l * PPL
            prm[:, o + PO["gpm"]:o + PO["gpm"] + 8] = gpm[l]
            prm[:, o + PO["ggo"]:o + PO["ggo"] + 8] = ggo[l]
            prm[:, o + PO["gmem"]:o + PO["gmem"] + 8] = gmem[l]
            prm[:, o + PO["gpf"]:o + PO["gpf"] + 8] = gpf[l]
            prm[:, o + PO["bs"]:o + PO["bs"] + 4] = b_sp[l].T
            prm[0:16, o + PO["bss"]:o + PO["bss"] + 4] = b_sp[l][:, 0:16].T
            prm[16:32, o + PO["bss"]:o + PO["bss"] + 4] = b_sp[l][:, 0:16].T
            prm[:, o + PO["wdw"]:o + PO["wdw"] + 66] = w_dw[l].reshape(3, NJ, 128).transpose(2, 1, 0).reshape(128, 66)
            prm[:, o + PO["bdw"]:o + PO["bdw"] + 22] = b_dw[l].reshape(NJ, 128).T
            prm[0:8, o + PO["bf"]] = b_f[l]
            cc = cconv[l, 2 * c:2 * c + 2]
            prm[:, o + PO["cconv"]:o + PO["cconv"] + 88] = cc.reshape(2, 2, NJ, 128).transpose(3, 2, 0, 1).reshape(128, 88)
        m = dict(shared)
        m.update(
            x_p=x_prompt[c], x_s=x_sample[2 * c:2 * c + 2].reshape(32, D), mem=mem_prompt[c],
            ckT=np.ascontiguousarray(ckT_all[:, 2 * c:2 * c + 2]), cv=np.ascontiguousarray(cv_all[:, 2 * c:2 * c + 2]),
            clfT=np.ascontiguousarray(clfT_all[:, 2 * c:2 * c + 2]), cmkT=np.ascontiguousarray(cmkT_all[:, 2 * c:2 * c + 2]),
            cmv=np.ascontiguousarray(cmv_all[:, 2 * c:2 * c + 2]), prm=prm)
        in_maps.append(m)
    return in_maps


def kernel(**inp):
    in_maps = _prep_inputs(inp)
    if "nc" not in _NC_CACHE:
        _NC_CACHE["nc"] = build()
    nc = _NC_CACHE["nc"]
    res = run_bass_kernel_spmd(nc, in_maps, core_ids=list(range(NCORES)))
    R = res.results
    cat = lambda name: np.stack([np.asarray(r[name], dtype=np.float32) for r in R], axis=0)
    y_p = cat("y_p")
    y_s = cat("y_s").reshape(16, 16, D)
    fk = cat("ofk").transpose(1, 0, 2, 3).reshape(L, 8, 2048, 8, 64)
    fv = cat("ofv").transpose(1, 0, 2, 3).reshape(L, 8, 2048, 8, 64)
    lf = cat("olf").transpose(1, 0, 3, 2)
    mk = cat("omk").transpose(1, 0, 2, 3).reshape(L, 8, 256, 4, 64)
    mv = cat("omv").transpose(1, 0, 2, 3).reshape(L, 8, 256, 4, 64)
    cvp = cat("oconv").transpose(1, 0, 4, 3, 2).reshape(L, 8, 2, FF)
    sk = cat("osk").transpose(1, 0, 2, 3).reshape(L, 16, 16, 8, 64)
    sv = cat("osv").transpose(1, 0, 2, 3).reshape(L, 16, 16, 8, 64)
    slf = cat("oslf").transpose(1, 0, 3, 2).reshape(L, 16, 16, 8)
    gv = cat("ogv").transpose(1, 0, 2, 3).reshape(L, 16, 16, 256)
    cvs = cat("osconv").transpose(1, 0, 4, 5, 3, 2).reshape(L, 16, 2, FF)
    outs = (y_p, y_s, fk, fv, lf, mk, mv, cvp, sk, sv, slf, gv, cvs)
    return tuple(np.ascontiguousarray(o, dtype=np.float32) for o in outs)
```

```python
import numpy as np
import ml_dtypes
from contextlib import ExitStack
import concourse.bass as bass
import concourse.mybir as mybir
from concourse.bass_utils import run_bass_kernel_spmd

F32 = mybir.dt.float32
BF16 = mybir.dt.bfloat16
ALU = mybir.AluOpType
AF = mybir.ActivationFunctionType

ENGS = ("pe", "act", "dve", "pool", "sp")
STOP = None
DBG = set()


class _Stop(Exception):
    pass


def _chk(name):
    if STOP == name:
        raise _Stop()
NCORES = 8
L = 2
D = 1024
NT = 2080
FF = 2816
NJ = 22
EPS = 1e-6
NEG = -30000.0


class Op:
    __slots__ = ("eng", "fn", "idx", "deps", "dma_key", "marked", "seq", "cum")

    def __init__(self, eng, fn, idx, dma_key):
        self.eng = eng
        self.fn = fn
        self.idx = idx
        self.deps = ()
        self.dma_key = dma_key
        self.marked = False
        self.seq = 0
        self.cum = 0


class Sched:
    def __init__(self, nc):
        self.nc = nc
        self.streams = {e: [] for e in ENGS}
        self.last_writer = {}
        self.readers = {}
        self.dma_counts = {}
        self.dma_last = {}

    def op(self, eng, fn, reads=(), writes=(), dma_key=None):
        o = Op(eng, fn, len(self.streams[eng]), dma_key)
        deps = set()
        lw = self.last_writer
        rd = self.readers
        for t in reads:
            w = lw.get(t)
            if w is not None:
                deps.add(w)
            if type(t) is tuple and t[0] == "ps":
                for r in rd.get(t, ()):
                    if r.eng != eng:
                        deps.add(r)
        for t in writes:
            w = lw.get(t)
            if w is not None:
                deps.add(w)
            r = rd.get(t)
            if r:
                deps.update(r)
        for t in reads:
            rd.setdefault(t, []).append(o)
        for t in writes:
            lw[t] = o
            rd[t] = []
        deps.discard(o)
        if eng == "pe":
            deps = {d for d in deps if not (d.eng == "pe" and d.dma_key is None)}
        o.deps = deps
        if dma_key is not None:
            c = self.dma_counts.get(dma_key, 0) + 1
            self.dma_counts[dma_key] = c
            o.cum = 16 * c
            self.dma_last[dma_key] = o
        self.streams[eng].append(o)
        return o

    def barrier(self):
        lasts = [s[-1] for s in self.streams.values() if s]
        lasts += list(self.dma_last.values())
        for e in ENGS:
            o = Op(e, lambda eng: eng.nop(), len(self.streams[e]), None)
            o.deps = {d for d in lasts if not (d.eng == e and d.dma_key is None)}
            self.streams[e].append(o)

    def emit(self, stack):
        nc = self.nc
        for e in ENGS:
            for o in self.streams[e]:
                for d in o.deps:
                    if d.dma_key is None:
                        d.marked = True
        sems = {}
        for e in ENGS:
            n = 0
            for o in self.streams[e]:
                if o.dma_key is None and o.marked:
                    n += 1
                    o.seq = n
            if n:
                sems[e] = stack.enter_context(nc.semaphore("s_" + e))
        dsems = {}
        for k in self.dma_counts:
            dsems[k] = stack.enter_context(nc.semaphore("d%d" % len(dsems)))
        self.n_sems = len(sems) + len(dsems)
        final = {k: o.cum for k, o in self.dma_last.items()}

        def run(eng_name, eng):
            waited = {}
            for o in self.streams[eng_name]:
                need = {}
                for d in o.deps:
                    if d.dma_key is not None:
                        key = ("d", d.dma_key)
                        val = d.cum
                    else:
                        key = ("c", d.eng)
                        val = d.seq
                    if val > need.get(key, 0):
                        need[key] = val
                for key, val in need.items():
                    if waited.get(key, 0) >= val:
                        continue
                    waited[key] = val
                    s = dsems[key[1]] if key[0] == "d" else sems[key[1]]
                    eng.wait_ge(s, val)
                ins = o.fn(eng)
                if o.dma_key is not None:
                    ins.then_inc(dsems[o.dma_key], 16)
                elif o.marked:
                    ins.then_inc(sems[o.eng], 1)
            if eng_name == "sp":
                for k, v in final.items():
                    if waited.get(("d", k), 0) < v:
                        eng.wait_ge(dsems[k], v)

        with nc.Block() as block:
            @block.tensor
            def _(t):
                run("pe", t)

            @block.scalar
            def _(t):
                run("act", t)

            @block.vector
            def _(t):
                run("dve", t)

            @block.gpsimd
            def _(t):
                run("pool", t)

            @block.sync
            def _(t):
                run("sp", t)


def tile_rows(i):
    return 128 if i < 16 else 32


def tcols(i):
    return (128 * i, 128) if i < 16 else (2048, 32)


BLKS = [(0, 512), (512, 512), (1024, 512), (1536, 512), (2048, 32)]

PO = {}
_o = 0
for _n, _w in (("gpm", 8), ("ggo", 8), ("gmem", 8), ("gpf", 8), ("bs", 4), ("bss", 4),
               ("wdw", 66), ("bdw", 22), ("bf", 1), ("cconv", 88)):
    PO[_n] = _o
    _o += _w
PPL = _o
NPRM = PPL * L


def build():
    nc = bass.Bass("TRN2", target_bir_lowering=False)

    def din(name, shape, dt=F32):
        return nc.dram_tensor(name, list(shape), dt, kind="ExternalInput").ap()

    def dout(name, shape):
        return nc.dram_tensor(name, list(shape), F32, kind="ExternalOutput").ap()

    x_p = din("x_p", [2048, D])
    x_s = din("x_s", [32, D])
    mem = din("mem", [256, D])
    ckT = din("ckT", [L, 2, 8, 64, 1024])
    cv = din("cv", [L, 2, 1024, 512])
    clfT = din("clfT", [L, 2, 8, 1024])
    cmkT = din("cmkT", [L, 2, 4, 64, 256])
    cmv = din("cmv", [L, 2, 256, 256])
    w_in = din("w_in", [L, D, 2312])
    w_mkv = din("w_mkv", [L, D, 512])
    w_out = din("w_out", [L, D, D])
    w_up = din("w_up", [L, NJ, 128, 8, 256])
    w_dn = din("w_dn", [L, FF, D])
    g_post_mix = din("g_post_mix", [L, D])
    g_post_ffn = din("g_post_ffn", [L, D])
    g_sgu = din("g_sgu", [L, 256])
    wsT = din("wsT", [L, 4, 128, 128])
    prm_h = din("prm", [128, NPRM])
    cst_h = din("cst", [128, 288], BF16)

    y_p = dout("y_p", [2048, D])
    y_s = dout("y_s", [32, D])
    ofk = dout("ofk", [L, 2048, 512])
    ofv = dout("ofv", [L, 2048, 512])
    olf = dout("olf", [L, 8, 2048])
    omk = dout("omk", [L, 256, 256])
    omv = dout("omv", [L, 256, 256])
    oconv = dout("oconv", [L, 128, NJ, 2])
    osk = dout("osk", [L, 32, 512])
    osv = dout("osv", [L, 32, 512])
    oslf = dout("oslf", [L, 8, 32])
    ogv = dout("ogv", [L, 32, 256])
    osconv = dout("osconv", [L, 128, NJ, 2, 2])

    cscr_p = nc.dram_tensor("cscr_p", [8, 2, 2048], BF16).ap()
    cscr_s = nc.dram_tensor("cscr_s", [8, 2, 2, 1040], BF16).ap()

    with ExitStack() as st:
        S = Sched(nc)
        X = st.enter_context(nc.sbuf_tensor("X", [128, 17, D], F32))
        PRM = st.enter_context(nc.sbuf_tensor("PRM", [128, NPRM], F32))
        CST = st.enter_context(nc.sbuf_tensor("CST", [128, 288], BF16))
        WSTSt = st.enter_context(nc.sbuf_tensor("WSTS", [32, L * 4 * 32], BF16))
        PTSt = st.enter_context(nc.sbuf_tensor("PTS", [128, 7 * 32], BF16))
        SM = st.enter_context(nc.sbuf_tensor("SM", [128, 256], F32))
        ONEROW = st.enter_context(nc.sbuf_tensor("ONEROW", [128, 64], F32))
        remaining = nc.sbuf_bytes_remaining
        remaining = remaining() if callable(remaining) else remaining
        ARB = (remaining - 512) // 64 * 64
        AR = st.enter_context(nc.sbuf_tensor("AR", [128, ARB // 2], BF16))
        AR32 = AR.bitcast(F32)
        PS = [st.enter_context(nc.psum_tensor("ps%d" % k, [128, 512], F32)) for k in range(8)]
        PSB = [p.bitcast(BF16) for p in PS]

        IDENT = CST[:, 0:128]
        MASKNEG = CST[:, 128:256]
        MASKS = CST[0:32, 256:288]


        def WSTS(l, g):
            o = (l * 4 + g) * 32
            return WSTSt[0:32, o:o + 32]

        def prm(l, name, w, p0=0, p1=128, off=0):
            o = l * PPL + PO[name] + off
            return PRM[p0:p1, o:o + w]

        SM_SS, SM_SD, SM_RSTD = 0, 1, 2
        SM_R = 16
        SM_SSF = 70
        SM_T = 90

        def bf(off, n, p0=0, p1=128):
            return AR[p0:p1, off // 2: off // 2 + n]

        def f32(off, n, p0=0, p1=128):
            return AR32[p0:p1, off // 4: off // 4 + n]

        cur = [0]

        def carve(nbytes):
            o = cur[0]
            cur[0] = o + (nbytes + 63) // 64 * 64
            return o

        HT_O = carve(8 * NT * 2)
        MX_O = carve(8 * NT * 2)
        VA_O = carve(17 * 8 * 65 * 2)
        QK_O = carve(4 * NT * 2)
        WS_O = [carve(8320) for _ in range(3)]
        SCR_O = carve(8192)
        MKT_O = carve(2048)
        MVA_O = carve(1040)
        MKTt = AR[:, MKT_O // 2:MKT_O // 2 + 1024]
        MVAt = AR[:, MVA_O // 2:MVA_O // 2 + 520]
        MIX_END = cur[0]
        assert MIX_END <= ARB, (MIX_END, ARB)

        def HT(c, c0, n):
            return bf(HT_O + (c * NT + c0) * 2, n)

        def HT3(c0, n):
            return AR[:, HT_O // 2: HT_O // 2 + 8 * NT].rearrange("p (c t) -> p c t", c=8)[:, :, c0:c0 + n]

        def MX(c, c0, n, p0=0, p1=128):
            return bf(MX_O + (c * NT + c0) * 2, n, p0, p1)

        def VA(j, h, nk=128):
            return bf(VA_O + ((j * 8 + h) * 65) * 2, 65, 0, nk)

        def VA128(j, h):
            return bf(VA_O + ((j * 8 + h) * 65) * 2, 128, 0, 128)

        def QK(slot, c0, n, p0, p1):
            return bf(QK_O + (slot * NT + c0) * 2, n, p0, p1)

        CT = lambda c0, n: f32(QK_O + c0 * 4, n, 96, 104)
        HI = lambda c0, n: bf(QK_O + 8320 + c0 * 2, n, 96, 104)
        LO = lambda c0, n: bf(QK_O + 8320 + 4160 + c0 * 2, n, 96, 104)

        def WS(s, c, c0, n):
            return bf(WS_O[s] + (c * 520 + c0) * 2, n)

        def WS3(s, ncol):
            return AR[:, WS_O[s] // 2: WS_O[s] // 2 + 8 * 520].rearrange("p (c n) -> p c n", c=8)[:, :, 0:ncol]

        def SCRb(g, n, p0=0, p1=128, off=0):
            return bf(SCR_O + g * 1024 + off * 2, n, p0, p1)

        def SCRf(g, n, p0=0, p1=128, off=0):
            return f32(SCR_O + g * 1024 + off * 4, n, p0, p1)

        def scr(*gs):
            return [("scr", g) for g in gs]

        def dma(eng, out, in_, reads=(), writes=(), key=None, slow=False):
            if slow:
                fn = lambda e: e.dma_start(out=out, in_=in_, allow_slow_non_contiguous=True)
            else:
                fn = lambda e: e.dma_start(out=out, in_=in_)
            return S.op(eng, fn, reads=reads, writes=writes, dma_key=key)

        psr = {"g": [0, 1], "s": [2, 3, 4], "o": [5, 6], "t": [7]}
        psi = {k: 0 for k in psr}

        def psnext(pool):
            lst = psr[pool]
            k = lst[psi[pool] % len(lst)]
            psi[pool] += 1
            return k

        def mm_group(out_fn, items, reads, writes):
            def fn(e):
                ins = None
                for (o, a, b, s0, s1) in items:
                    ins = e.matmul(o, a, b, start=s0, stop=s1, skip_group_check=True)
                return ins
            return S.op("pe", fn, reads=reads, writes=writes)

        dma("sp", PRM[:, :], prm_h, writes=["prm"], key="prm")
        dma("sp", CST[:, :], cst_h, writes=["cst"], key="cst")
        S.op("dve", lambda e: e.memset(WSTSt[:, :], 0.0), writes=["wsts"])
        for l in range(L):
            for b in range(2):
                dma("pool",
                    WSTSt[16 * b:16 * b + 16, l * 128:(l + 1) * 128].rearrange("p (g i) -> p g i", g=4)[:, :, 16 * b:16 * b + 16],
                    wsT[l, :, 0:16, 0:16].rearrange("g j i -> j g i"),
                    reads=["wsts"], writes=[("wsts", l, b)], key=("wsts", l), slow=True)
        S.op("dve", lambda e: e.memset(PTSt[:, :], 0.0), writes=["pts%d" % q for q in range(7)])
        S.op("dve", lambda e: e.memset(SM[:, 200:201], 1.0), writes=["one"])
        S.op("dve", lambda e: e.memset(ONEROW[:, :], 1.0), writes=["onerow"])
        ONE8 = SM[96:104, 200:201]

        for i in range(16):
            dma("sp", X[:, i, :], x_p[128 * i:128 * i + 128, :], writes=[("x", i)], key=("x", i))
        dma("sp", X[0:32, 16, :], x_s, writes=[("x", 16)], key=("x", 16))

        def prenorm_tile(src, rows, xtok, gcol, dst3, dst_tok, slot, extra_writes=(), hb=None, tpool="t", hb_toks=None, defer=False):
            if hb is None:
                hb = SCRb(2 * slot, 1024, 0, rows)
            hbt = list(hb_toks) if hb_toks is not None else scr(2 * slot, 2 * slot + 1)
            c0 = 3 * slot
            ss = SM[0:rows, c0:c0 + 1]
            sd = SM[0:rows, c0 + 1:c0 + 2]
            rs = SM[0:rows, c0 + 2:c0 + 3]
            S.op("act", lambda e: e.activation(hb, src, AF.Square, accum_out=ss),
                 reads=[xtok], writes=hbt + [("sm", c0)])
            S.op("act", lambda e: e.activation(sd, ss, AF.Ln, bias=EPS, scale=1.0 / D),
                 reads=[("sm", c0)], writes=[("sm", c0 + 1)])
            S.op("act", lambda e: e.activation(rs, sd, AF.Exp, scale=-0.5), reads=[("sm", c0 + 1)], writes=[("sm", c0 + 2)])
            S.op("act", lambda e: e.activation(hb, src, AF.Copy, scale=rs),
                 reads=[xtok, ("sm", c0 + 2)], writes=hbt)
            def part_b():
                k = psnext(tpool)

                def tr(e):
                    ins = None
                    for c in range(8):
                        ins = e.transpose(PSB[k][:, c * 128:c * 128 + rows], hb[:, c * 128:(c + 1) * 128],
                                          IDENT[0:rows, 0:rows])
                    return ins
                S.op("pe", tr, reads=hbt + ["cst"], writes=[("ps", k)])
                src3 = PSB[k][:, 0:1024].rearrange("p (c t) -> p c t", c=8)[:, :, 0:rows]
                S.op("dve", lambda e: e.tensor_tensor(dst3, src3, gcol.unsqueeze(2).to_broadcast([128, 8, rows]), ALU.mult),
                     reads=[("ps", k), "prm"], writes=[dst_tok] + list(extra_writes))
            if defer:
                return part_b
            part_b()

        PVQ = []

        def flush_pv():
            while PVQ:
                PVQ.pop(0)()

        def attend(qa_fn, K, ktiles, osubs, obank, tag):
            pendq = []
            first_done = [False]

            def pv(kt_i, kt, pt_ap, pt_tok):
                items = []
                if osubs == "fm":
                    lo_, hi_ = kt["lo"], kt["hi"]
                    mrows = kt["va"].shape[1]
                    items.append((PS[obank][0:mrows, lo_:hi_], kt["va"], pt_ap[:, lo_:hi_], kt_i == 0, kt_i == len(ktiles) - 1))
                    mm_group(None, items, reads=[pt_tok] + kt["vreads"], writes=[("ps", obank)])
                    return
                for (o_ap, c0, c1, fk, lk) in osubs:
                    if fk <= kt_i <= lk and kt["pv_lo"] <= c0:
                        items.append((o_ap, pt_ap[:, c0:c1], kt["va"], not first_done[0], kt_i == lk))
                        first_done[0] = True
                if items:
                    mm_group(None, items, reads=[pt_tok] + kt["vreads"], writes=[("ps", obank)])

            for kt_i, kt in enumerate(ktiles):
                k = psnext("s")
                nk, lo, hi = kt["nk"], kt["lo"], kt["hi"]
                items = []
                if kt["mask"] is not None:
                    mask_ap, mc = kt["mask"]
                    items.append((PS[k][0:nk, lo:lo + mc], kt["ka"], qa_fn(lo, mc), True, False))
                    items.append((PS[k][0:nk, lo:lo + mc], IDENT[0:nk, 0:nk], mask_ap, False, True))
                    if hi > lo + mc:
                        items.append((PS[k][0:nk, lo + mc:hi], kt["ka"], qa_fn(lo + mc, hi - lo - mc), True, True))
                else:
                    items.append((PS[k][0:nk, lo:hi], kt["ka"], qa_fn(lo, hi - lo), True, True))
                mm_group(None, items, reads=kt["reads"] + ["cst"], writes=[("ps", k)])
                if kt["pt"] == "rot":
                    g = attend.ptc % 4
                    attend.ptc += 1
                    pt_ap = SCRb(g, 512, 0, nk)
                    pt_tok = ("scr", g)
                else:
                    pt_ap = PTSt[0:nk, 32 * kt["pt"]:32 * kt["pt"] + 32]
                    pt_tok = "pts%d" % kt["pt"]
                S.op("act", lambda e, pt_ap=pt_ap, k=k, nk=nk, lo=lo, hi=hi:
                     e.activation(pt_ap[:, lo:hi], PS[k][0:nk, lo:hi], AF.Exp),
                     reads=[("ps", k)], writes=[pt_tok])
                while len(PVQ) >= 2:
                    PVQ.pop(0)()
                PVQ.append(lambda kt_i=kt_i, kt=kt, pt_ap=pt_ap, pt_tok=pt_tok: pv(kt_i, kt, pt_ap, pt_tok))
        attend.ptc = 0

        def cs_all():
            return ["cw"]

        def post_norm_residual(ostg, rows, i, junk, junk_toks, GPap, gp_toks, o_toks):
            ssc = SM[0:rows, SM_T + 8:SM_T + 9]
            S.op("act", lambda e: e.activation(junk, ostg, AF.Square, accum_out=ssc),
                 reads=o_toks, writes=list(junk_toks) + ["pn"])
            S.op("act", lambda e: e.activation(ssc, ssc, AF.Ln, bias=EPS, scale=1.0 / D), reads=["pn"], writes=["pn"])
            S.op("act", lambda e: e.activation(ssc, ssc, AF.Exp, scale=-0.5), reads=["pn"], writes=["pn"])
            S.op("dve", lambda e: e.scalar_tensor_tensor(ostg, ostg, ssc, GPap[0:rows, :], ALU.mult, ALU.mult),
                 reads=list(o_toks) + list(gp_toks) + ["pn"], writes=o_toks)
            S.op("pool", lambda e: e.tensor_tensor(X[0:rows, i, :], X[0:rows, i, :], ostg, ALU.add),
                 reads=list(o_toks) + [("x", i)], writes=[("x", i)])

        def ffn_phase(l):
            psr.clear()
            psr.update({"u": [0, 1, 2, 4, 5, 6], "t": [3], "d": [4, 5, 6, 7]})
            for kk in psr:
                psi[kk] = 0
            HC = 1056
            o = [0]

            def cv_(nb):
                r = o[0]
                o[0] = r + (nb + 63) // 64 * 64
                return r
            H2_O = cv_(8 * HC * 2)
            ACT_O = cv_(NJ * HC * 2)
            WDN_O = cv_(NJ * 1024 * 2)
            WU_O = [cv_(4096) for _ in range(2)]
            AE_O = [cv_(2064) for _ in range(2)]
            LIN_O = [cv_(1024) for _ in range(2)]
            CONV_O = [cv_(2048) for _ in range(2)]
            OSTG_O = cv_(4096)
            GPF_O = cv_(4096)
            HBF_O = cv_(2048)
            SAVE_O = cv_(NJ * 2 * 4)
            SAVES_O = cv_(NJ * 4 * 4)
            assert o[0] <= ARB, (o[0], ARB)

            H2 = lambda c, c0, n: bf(H2_O + (c * HC + c0) * 2, n)
            H23 = lambda c0, n: AR[:, H2_O // 2:H2_O // 2 + 8 * HC].rearrange("p (c t) -> p c t", c=8)[:, :, c0:c0 + n]
            ACTB = lambda j, c0, n: bf(ACT_O + (j * HC + c0) * 2, n)
            WDN = lambda j, c0, n: bf(WDN_O + (j * 1024 + c0) * 2, n)
            WDN3 = AR[:, WDN_O // 2:WDN_O // 2 + NJ * 1024].rearrange("p (j n) -> p j n", j=NJ)
            WUB = lambda s: (WU_O[s] if s < 2 else OSTG_O)
            WU = lambda s, c, c0, n: bf(WUB(s) + (c * 256 + c0) * 2, n)
            WU3 = lambda s: AR[:, WUB(s) // 2:WUB(s) // 2 + 2048].rearrange("p (c n) -> p c n", c=8)
            wutok = lambda s, x: (("wu", s, x) if s < 2 else ("ostg%d" % x))
            OSTG = f32(OSTG_O, 1024)
            GPF = f32(GPF_O, 1024)
            HBF = bf(HBF_O, 1024)
            SAVE = f32(SAVE_O, NJ * 2)
            SAVES = f32(SAVES_O, NJ * 4)

            dma("sp", GPF, g_post_ffn[l:l + 1, :].broadcast_to([128, D]), writes=["gpf"], key="gp")
            wdn_src = w_dn[l].rearrange("(j p) n -> p j n", p=128)
            for part in range(2):
                dma("pool", WDN3[:, 11 * part:11 * part + 11, :], wdn_src[:, 11 * part:11 * part + 11, :],
                    writes=[("wdn", part)], key=("wdn", part))
            aecnt = 0

            def wu_load(j, s_):
                dma("pool", WU3(s_), w_up[l, j], writes=[wutok(s_, 0), wutok(s_, 1)], key=("wu", s_, 0))

            for half in range(2):
                tiles = list(range(0, 8)) if half == 0 else list(range(8, 17))
                base = 0 if half == 0 else 1024
                blocks = [(0, 512), (512, 512)] + ([(1024, 32)] if half == 1 else [])
                h2toks = [("h2", (i if i < 8 else i - 8)) for i in tiles]
                if half == 0:
                    HB2 = bf(AE_O[0], 1024)

                    def pa(i):
                        c0, n = tcols(i)
                        if i % 2 == 0:
                            return prenorm_tile(X[:, i, :], 128, ("x", i), prm(l, "gpf", 8), H23(c0, n), ("h2", i), 0,
                                                hb=HBF[:, :], tpool="t", hb_toks=["hbf"], defer=True)
                        return prenorm_tile(X[:, i, :], 128, ("x", i), prm(l, "gpf", 8), H23(c0, n), ("h2", i), 1,
                                            hb=HB2, tpool="t", hb_toks=[("ae", 0), ("aeh", 0), ("aet", 0)], defer=True)
                    pbq = {0: pa(0)}
                    for i in tiles:
                        if i + 1 < 8:
                            pbq[i + 1] = pa(i + 1)
                        pbq.pop(i)()
                wu_load(0, 0)
                wu_load(1, 1)
                pend_tail = None
                for j in range(NJ):
                    s = j % 3
                    if j + 2 < NJ:
                        wu_load(j + 2, (j + 2) % 3)
                    w0 = prm(l, "wdw", 1, 0, 128, 3 * j)
                    w1 = prm(l, "wdw", 1, 0, 128, 3 * j + 1)
                    w2 = prm(l, "wdw", 1, 0, 128, 3 * j + 2)
                    bd = prm(l, "bdw", 1, 0, 128, j)
                    for bidx, (lc0, n) in enumerate(blocks):
                        sample = (n == 32)
                        ka = psnext("u")
                        mm_group(None, [(PS[ka][:, 0:n], WU(s, c, 0, 128), H2(c, lc0, n), c == 0, c == 7) for c in range(8)],
                                 reads=h2toks + [wutok(s, 0)], writes=[("ps", ka)])
                        kl = psnext("u")
                        mm_group(None, [(PS[kl][:, 0:n], WU(s, c, 128, 128), H2(c, lc0, n), c == 0, c == 7) for c in range(8)],
                                 reads=h2toks + [wutok(s, 1)], writes=[("ps", kl)])
                        p = aecnt % 2
                        aecnt += 1
                        lin = bf(LIN_O[p], 512)
                        conv = f32(CONV_O[p], 512)
                        if sample:
                            AEs = f32(AE_O[p], 36).rearrange("p (b t) -> p b t", b=2)
                            S.op("dve", lambda e, AEs=AEs, j=j: e.tensor_copy(
                                AEs[:, :, 0:2], prm(l, "cconv", 4, 0, 128, 4 * j).rearrange("p (b r) -> p b r", b=2)),
                                reads=["prm"], writes=[("aeh", p)])
                            S.op("act", lambda e, AEs=AEs, ka=ka: e.copy(AEs[:, :, 2:18], PS[ka][:, 0:32].rearrange("p (b t) -> p b t", b=2)),
                                 reads=[("ps", ka)], writes=[("ae", p)])
                            taps = [AEs[:, :, k:k + 16] for k in range(3)]
                            cv3 = conv[:, 0:32].rearrange("p (b t) -> p b t", b=2)
                            psa = PS[ka][:, 0:32].rearrange("p (b t) -> p b t", b=2)
                            sil_out = f32(AE_O[p] + 256, 32)
                            sil_in = conv[:, 0:32]
                        else:
                            AEp = f32(AE_O[p], 514)
                            if bidx == 0 and half == 0:
                                S.op("dve", lambda e, AEp=AEp: e.memset(AEp[:, 0:2], 0.0), writes=[("aeh", p)])
                            elif bidx == 0:
                                S.op("dve", lambda e, AEp=AEp, j=j: e.tensor_copy(AEp[:, 0:2], SAVE[:, 2 * j:2 * j + 2]),
                                     reads=[("save", j)], writes=[("aeh", p)])
                            else:
                                prev = f32(AE_O[1 - p], 514)
                                S.op("dve", lambda e, AEp=AEp, prev=prev: e.tensor_copy(AEp[:, 0:2], prev[:, 512:514]),
                                     reads=[("aet", 1 - p)], writes=[("aeh", p)])
                            S.op("act", lambda e, AEp=AEp, ka=ka: e.copy(AEp[:, 2:514], PS[ka][:, 0:512]),
                                 reads=[("ps", ka)], writes=[("ae", p), ("aet", p)])
                            taps = [AEp[:, k:k + 512] for k in range(3)]
                            cv3 = conv
                            psa = PS[ka][:, 0:512]
                            sil_out = AEp[:, 0:512]
                            sil_in = conv
                        S.op("act", lambda e, cv3=cv3, psa=psa, w2=w2, bd=bd: e.activation(cv3, psa, AF.Identity, bias=bd, scale=w2),
                             reads=[("ps", ka), "prm"], writes=[("conv", p)])
                        S.op("act", lambda e, lin=lin, kl=kl, n=n: e.copy(lin[:, 0:n], PS[kl][:, 0:n]),
                             reads=[("ps", kl)], writes=[("lin", p)])
                        S.op("dve", lambda e, cv3=cv3, taps=taps, w1=w1: e.scalar_tensor_tensor(cv3, taps[1], w1, cv3, ALU.mult, ALU.add),
                             reads=[("ae", p), ("aeh", p), ("conv", p)], writes=[("conv", p)])
                        S.op("dve", lambda e, cv3=cv3, taps=taps, w0=w0: e.scalar_tensor_tensor(cv3, taps[0], w0, cv3, ALU.mult, ALU.add),
                             reads=[("ae", p), ("aeh", p), ("conv", p)], writes=[("conv", p)])
                        if not sample and bidx == 1:
                            S.op("dve", lambda e, AEp=AEp, j=j: e.tensor_copy(SAVE[:, 2 * j:2 * j + 2], AEp[:, 512:514]),
                                 reads=[("aet", p)], writes=[("save", j)])
                        if sample:
                            S.op("dve", lambda e, AEs=AEs, j=j: e.tensor_copy(
                                SAVES[:, 4 * j:4 * j + 4].rearrange("p (b r) -> p b r", b=2), AEs[:, :, 16:18]),
                                reads=[("ae", p)], writes=[("saves", j)])

                        def tail(p=p, sil_out=sil_out, sil_in=sil_in, j=j, lc0=lc0, n=n, lin=lin, half=half):
                            S.op("act", lambda e: e.activation(sil_out, sil_in, AF.Silu),
                                 reads=[("conv", p), ("ae", p), ("aeh", p)], writes=[("ae", p), ("aeh", p)])
                            S.op("pool", lambda e: e.tensor_tensor(ACTB(j, lc0, n), sil_out, lin[:, 0:n], ALU.mult),
                                 reads=[("ae", p), ("aeh", p), ("lin", p)], writes=[("actb", j, half)])
                        if pend_tail is not None:
                            pend_tail()
                        pend_tail = tail
                if pend_tail is not None:
                    pend_tail()
                    pend_tail = None
                if half == 1:
                    dma("sp", oconv[l], SAVE.rearrange("p (j r) -> p j r", r=2), reads=[("save", j) for j in range(NJ)], key="oconv")
                    dma("sp", osconv[l], SAVES.rearrange("p (j b r) -> p j b r", b=2, r=2), reads=[("saves", j) for j in range(NJ)], key="osconv")
                atoks = [("actb", j, half) for j in range(NJ)]
                nxt = list(range(8, 17)) if half == 0 else []
                for ti_, i in enumerate(tiles):
                    rows = tile_rows(i)
                    c0, n = tcols(i)
                    lc0 = c0 - base
                    defer_b = []
                    for i2 in (nxt[ti_:ti_ + 1] if ti_ < 7 else nxt[7:8]):
                        rows2 = tile_rows(i2)
                        c02, n2 = tcols(i2)
                        defer_b.append(prenorm_tile(X[0:rows2, i2, :], rows2, ("x", i2), prm(l, "gpf", 8), H23(c02 - 1024, n2),
                                                    ("h2", i2 - 8), 0, hb=HBF[0:rows2, :], tpool="t", hb_toks=["hbf"], defer=True))
                        if "nodefer" in DBG:
                            defer_b.pop()()
                    for hf in range(2):
                        kd = psnext("d")
                        mm_group(None, [(PS[kd][0:rows, :], ACTB(j, lc0, n), WDN(j, 512 * hf, 512), j == 0, j == NJ - 1) for j in range(NJ)],
                                 reads=atoks + [("wdn", 0), ("wdn", 1)], writes=[("ps", kd)])
                        if hf == 0:
                            S.op("act", lambda e, kd=kd, rows=rows: e.copy(OSTG[0:rows, 0:512], PS[kd][0:rows, :]),
                                 reads=[("ps", kd)], writes=["ostg0"])
                        else:
                            S.op("dve", lambda e, kd=kd, rows=rows: e.tensor_copy(OSTG[0:rows, 512:1024], PS[kd][0:rows, :]),
                                 reads=[("ps", kd)], writes=["ostg1"])
                    for fb in defer_b:
                        fb()
                    post_norm_residual(OSTG[0:rows, :], rows, i, HBF[0:rows, :], ["hbf"], GPF, ["gpf"], ["ostg0", "ostg1"])
                    if l == L - 1:
                        if i < 16:
                            dma("sp", y_p[128 * i:128 * i + 128, :], X[:, i, :], reads=[("x", i)], key=("x", i))
                        else:
                            dma("sp", y_s, X[0:32, 16, :], reads=[("x", 16)], key=("x", 16))
                if half == 0:
                    prenorm_tile(X[0:32, 16, :], 32, ("x", 16), prm(l, "gpf", 8), H23(1024, 32), ("h2", 8), 0,
                                 hb=HBF[0:32, :], tpool="t", hb_toks=["hbf"])
        RFOX = lambda i, rows=128: SM[0:rows, 100 + i:101 + i]
        RSGU = lambda i, rows=128: SM[0:rows, 120 + i:121 + i]
        RMEM = lambda i, rows=128: SM[0:rows, 140 + i:141 + i]
        SSFOX = lambda i, rows=128: SM[0:rows, 160 + i:161 + i]
        SSMEM = lambda i, rows=128: SM[0:rows, 180 + i:181 + i]
        ALLHT = [("ht", i) for i in range(17)]

        class NSQ:
            def __init__(self):
                self.p1 = None
                self.p2 = None
                self.p3 = None

            def step(self, new_p1):
                if self.p3 is not None:
                    self.p3()
                    self.p3 = None
                if self.p2 is not None:
                    self.p3 = self.p2()
                    self.p2 = None
                if self.p1 is not None:
                    self.p2 = self.p1()
                self.p1 = new_p1

            def flush(self):
                self.step(None)
                self.step(None)
                self.step(None)
        nsq = NSQ()

        def wload(slot, src2d, ncol, extra_writes=()):
            dma("pool", WS3(slot, ncol), src2d.rearrange("(c p) n -> p c n", p=128),
                writes=[("ws", slot)] + list(extra_writes), key=("ws", slot))

        def KAS(b, c0, n, p0=0, p1=67):
            return bf(WS_O[0] + b * 3200 + c0 * 2, n, p0, p1)

        def VAS(b, j):
            return bf(WS_O[0] + b * 3200 + 2080 + j * 130, 65)

        def VAS3(b):
            return AR[:, (WS_O[0] + b * 3200 + 2080) // 2:(WS_O[0] + b * 3200 + 2080) // 2 + 520].rearrange("p (j k) -> p j k", k=65)

        SC_TOKS = [("sc", b, x) for b in range(2) for x in ("k", "v", "r", "one")]

        def norm_store_fm(obank, I, grp, chunk, p0, ssq4, l):
            rlrow = f32(SCR_O + 6 * 1024, 512, 64, 65)
            hfT = f32(SCR_O + 6 * 1024, 512, 0, 64)
            S.op("act", lambda e: e.activation(rlrow, PS[obank][64:65, 0:512], AF.Ln), reads=[("ps", obank)], writes=[("scr", 6, "r")])
            S.op("act", lambda e: e.activation(rlrow, rlrow, AF.Exp, scale=-1.0), reads=[("scr", 6, "r")], writes=[("scr", 6, "r")])
            kb = psnext("g")
            S.op("pe", lambda e: e.matmul(PS[kb][0:64, 0:512], ONEROW[64:65, 0:64], rlrow, start=True, stop=True, skip_group_check=True),
                 reads=[("scr", 6, "r"), "onerow"], writes=[("ps", kb)])
            S.op("dve", lambda e: e.tensor_copy(hfT, PS[kb][0:64, 0:512]), reads=[("ps", kb)], writes=scr(6, 7))
            S.op("dve", lambda e: e.tensor_tensor(hfT, PS[obank][0:64, 0:512], hfT, ALU.mult), reads=[("ps", obank)] + scr(6, 7), writes=scr(6, 7))
            return lambda: norm_store_fm2(I, grp, chunk, p0, ssq4, l)

        def norm_store_fm2(I, grp, chunk, p0, ssq4, l):
            hfT = f32(SCR_O + 6 * 1024, 512, 0, 64)
            sqb = bf(SCR_O + 4 * 1024, 512, 0, 64)
            S.op("act", lambda e: e.activation(MX(chunk, 512 * I, 512, p0, p0 + 64), hfT, AF.Copy,
                                               scale=prm(l, "ggo", 1, p0, p0 + 64, chunk)),
                 reads=scr(6, 7) + ["prm"], writes=[("mx", chunk, 4 * I + s_, p0) for s_ in range(4)])
            S.op("pool", lambda e: e.tensor_tensor(sqb, hfT, hfT, ALU.mult), reads=scr(6, 7), writes=scr(4))
            return lambda: norm_store_fm3(I, grp, ssq4)

        def norm_store_fm3(I, grp, ssq4):
            sqb = bf(SCR_O + 4 * 1024, 512, 0, 64)
            kt = psnext("t")
            ones_col = bf(VA_O + 64 * 2, 1, 0, 64)

            def ssmm(e):
                ins = None
                for s_ in range(4):
                    ins = e.matmul(PS[kt][:, s_:s_ + 1], sqb[:, 128 * s_:128 * s_ + 128], ones_col, start=True, stop=True,
                                   skip_group_check=True)
                return ins
            S.op("pe", ssmm, reads=scr(4) + ["va_ones"], writes=[("ps", kt)])
            toks4 = [("ssq", grp, 4 * I + s_) for s_ in range(4)]
            S.op("dve", lambda e: e.tensor_tensor(ssq4, ssq4, PS[kt][:, 0:4], ALU.add), reads=[("ps", kt)] + toks4, writes=toks4)

        def norm_and_store_block(obank, I, grp, chunk, p0, ssq4, l):
            O4 = PS[obank][:, 0:260].rearrange("p (s k) -> p s k", k=65)
            rl4 = SM[:, SM_T + 10:SM_T + 14]
            sq4 = SM[:, SM_T + 14:SM_T + 18]
            S.op("dve", lambda e: e.reciprocal(rl4.unsqueeze(2), O4[:, :, 64:65]), reads=[("ps", obank)], writes=["rl4"])
            hf4 = SCRf(6, 256)
            hb4 = SCRb(7, 256)
            S.op("dve", lambda e: e.tensor_tensor(hf4.rearrange("p (s k) -> p s k", k=64), O4[:, :, 0:64],
                                                  rl4.unsqueeze(2).to_broadcast([128, 4, 64]), ALU.mult),
                 reads=[("ps", obank), "rl4"], writes=scr(6))

            def sqs(e):
                ins = None
                for s_ in range(4):
                    ins = e.activation(hb4[:, 64 * s_:64 * s_ + 64], hf4[:, 64 * s_:64 * s_ + 64], AF.Square,
                                       accum_out=sq4[:, s_:s_ + 1])
                return ins
            S.op("act", sqs, reads=scr(6), writes=scr(7) + ["sq4"])
            toks4 = [("ssq", grp, 4 * I + s_) for s_ in range(4)]
            S.op("dve", lambda e: e.tensor_tensor(ssq4, ssq4, sq4, ALU.add), reads=["sq4"] + toks4, writes=toks4)
            S.op("act", lambda e: e.copy(hb4, hf4), reads=scr(6, 7), writes=scr(7))
            kt = psnext("t")

            def tr(e):
                ins = None
                for s_ in range(4):
                    ins = e.transpose(PSB[kt][p0:p0 + 64, 128 * s_:128 * s_ + 128], hb4[:, 64 * s_:64 * s_ + 64], IDENT[:, :])
                return ins
            S.op("pe", tr, reads=scr(7) + ["cst"], writes=[("ps", kt)])
            S.op("dve", lambda e: e.tensor_scalar(MX(chunk, 512 * I, 512, p0, p0 + 64), PSB[kt][p0:p0 + 64, 0:512],
                                                  prm(l, "ggo", 1, p0, p0 + 64, chunk), None, ALU.mult),
                 reads=[("ps", kt), "prm"], writes=[("mx", chunk, 4 * I + s_, p0) for s_ in range(4)])

        def norm_and_store(obank, ocol, rows, tile_i, grp, chunk, p0, ssq_acc, l):
            rl = SM[0:rows, SM_T + 4:SM_T + 5]
            S.op("dve", lambda e: e.reciprocal(rl, PS[obank][0:rows, ocol + 64:ocol + 65]),
                 reads=[("ps", obank)], writes=["rl"])
            hf = SCRf(5, 64, 0, rows)
            hb = SCRb(5, 64, 0, rows, off=256)
            S.op("dve", lambda e: e.tensor_scalar(hf, PS[obank][0:rows, ocol:ocol + 64], rl, None, ALU.mult),
                 reads=[("ps", obank), "rl"], writes=scr(5))
            sq = SM[0:rows, SM_T + 5:SM_T + 6]
            S.op("act", lambda e: e.activation(hb, hf, AF.Square, accum_out=sq), reads=scr(5), writes=scr(5) + ["sq"])
            S.op("dve", lambda e: e.tensor_tensor(ssq_acc, ssq_acc, sq, ALU.add),
                 reads=["sq", ("ssq", grp, tile_i)], writes=[("ssq", grp, tile_i)])
            S.op("act", lambda e: e.copy(hb, hf), reads=scr(5), writes=scr(5))
            kt = psnext("t")
            c0, n = tcols(tile_i)
            S.op("pe", lambda e: e.transpose(PSB[kt][p0:p0 + 64, 0:rows], hb, IDENT[0:rows, 0:rows]),
                 reads=scr(5) + ["cst"], writes=[("ps", kt)])
            S.op("dve", lambda e: e.tensor_scalar(MX(chunk, c0, n, p0, p0 + 64), PSB[kt][p0:p0 + 64, 0:rows],
                                                  prm(l, "ggo", 1, p0, p0 + 64, chunk), None, ALU.mult),
                 reads=[("ps", kt), "prm"], writes=[("mx", chunk, tile_i, p0)])

        try:
            for l in range(L):
                psr.clear()
                psr.update({"g": [0, 1], "s": [2, 3, 4], "o": [5, 6], "t": [7], "w": [0, 1, 2, 3, 4, 5]})
                for kk in psr:
                    psi[kk] = 0
                S.op("dve", lambda e: e.memset(
                    AR[:, VA_O // 2: VA_O // 2 + 17 * 8 * 65].rearrange("p (a k) -> p a k", k=65)[:, :, 64:65], 1.0),
                    writes=["va_ones"])
                S.op("dve", lambda e: e.memset(MVAt[:, :].rearrange("p (a k) -> p a k", k=65)[:, :, 64:65], 1.0),
                     writes=["mva_ones"])
                S.op("dve", lambda e: e.memset(MKTt[64:128, :], 0.0), writes=["mkt_zero"])
                wload(0, w_in[l][:, 1024:1536], 512, extra_writes=SC_TOKS if l > 0 else ())
                wload(1, w_in[l][:, 512:1024], 512)
                wload(2, w_in[l][:, 1536:2056], 520)

                SGC = MX_O + 12288
                WSTl = bf(SGC, 512)
                GSl = f32(SGC + 1024, 256)
                dma("pool", WSTl.rearrange("p (g i) -> p g i", g=4), wsT[l].rearrange("g j i -> j g i"),
                    writes=[("sguc", 0)], key="wst")
                S.op("dve", lambda e: e.memset(bf(SGC, 512, 64, 128).rearrange("p (g i) -> p g i", g=4)[:, :, 0:64], 0.0),
                     reads=[("sguc", 0)], writes=[("sguc", 0)])
                dma("sp", GSl, g_sgu[l:l + 1, :].broadcast_to([128, 256]), writes=[("sguc", 1)], key="gs")

                def sgu_ops(i):
                    st_ = 0 if "setA" in DBG else i % 2
                    base = MX_O if st_ == 0 else MX_O + 6144
                    tk = "sguA" if st_ == 0 else "sguB"
                    Sb = lambda g, n, p0=0, p1=128: bf(base + g * 1024, n, p0, p1)
                    Sf = lambda g, n, p0=0, p1=128: f32(base + g * 1024, n, p0, p1)
                    sc = lambda *gs: [(tk, g) for g in gs]
                    rows = tile_rows(i)
                    c0, n = tcols(i)
                    s1, s2 = [], []
                    k = psnext("s")
                    z = Sf(0, 512, 0, rows)
                    t1 = Sf(2, 512, 0, rows)
                    ssv = SM[0:rows, SM_T + 20 + 2 * st_:SM_T + 21 + 2 * st_]
                    sss = SM[0:rows, SM_T + 21 + 2 * st_:SM_T + 22 + 2 * st_]
                    ssvt, ssst = ("ssv", st_), ("sss", st_)
                    vvf = Sf(3, 256, 0, rows)
                    vvb = Sb(2, 256, 0, rows)
                    sg = Sf(4, 256, 0, rows)
                    sgb = Sb(5, 256, 0, rows)
                    s1.append(lambda: mm_group(None, [(PS[k][0:rows, :], HT(c, c0, n), WS(2, c, 8, 512), c == 0, c == 7) for c in range(8)],
                                               reads=[("ht", i), ("ws", 2)], writes=[("ps", k)]))
                    s1.append(lambda: S.op("dve", lambda e: e.tensor_copy(z, PS[k][0:rows, :]), reads=[("ps", k)], writes=sc(0, 1)))
                    s1.append(lambda: S.op("dve", lambda e: e.tensor_tensor(t1, z, z, ALU.mult), reads=sc(0, 1), writes=sc(2, 3)))
                    s1.append(lambda: S.op("dve", lambda e: e.tensor_scalar(t1, t1, 0.044715, 1.0, ALU.mult, ALU.add), reads=sc(2, 3), writes=sc(2, 3)))
                    s1.append(lambda: S.op("dve", lambda e: e.tensor_tensor(t1, t1, z, ALU.mult), reads=sc(0, 1, 2, 3), writes=sc(2, 3)))
                    s1.append(lambda: S.op("act", lambda e: e.activation(t1, t1, AF.Exp, scale=-1.5957691216057308), reads=sc(2, 3), writes=sc(2, 3)))
                    s1.append(lambda: S.op("act", lambda e: e.activation(t1, t1, AF.Ln, bias=1.0, scale=1.0), reads=sc(2, 3), writes=sc(2, 3)))
                    s1.append(lambda: S.op("act", lambda e: e.activation(t1, t1, AF.Exp, scale=-1.0), reads=sc(2, 3), writes=sc(2, 3)))
                    s1.append(lambda: S.op("dve", lambda e: e.tensor_tensor(z, z, t1, ALU.mult), reads=sc(0, 1, 2, 3), writes=sc(0, 1)))
                    s1.append(lambda: S.op("act", lambda e: e.activation(t1[:, 0:256], z[:, 256:512], AF.Square, accum_out=ssv),
                                           reads=sc(0, 1), writes=sc(2) + [ssvt]))
                    s1.append(lambda: S.op("act", lambda e: e.activation(ssv, ssv, AF.Ln, bias=EPS, scale=1.0 / 256), reads=[ssvt], writes=[ssvt]))
                    s1.append(lambda: S.op("act", lambda e: e.activation(ssv, ssv, AF.Exp, scale=-0.5), reads=[ssvt], writes=[ssvt]))
                    s1.append(lambda: S.op("dve", lambda e: e.scalar_tensor_tensor(vvf, z[:, 256:512], ssv, GSl[0:rows, :], ALU.mult, ALU.mult),
                                           reads=sc(0, 1) + [("sguc", 1)] + [ssvt], writes=sc(3)))
                    s1.append(lambda: S.op("dve", lambda e: e.tensor_copy(vvb, vvf), reads=sc(3), writes=sc(2)))
                    if i == 16:
                        s1.append(lambda: dma("sp", ogv[l], vvf, reads=sc(3), key="ogv"))
                    k2 = psnext("s")
                    if i < 16:
                        items = [(PS[k2][0:128, 64 * g:64 * g + 64], WSTl[:, 128 * g:128 * g + 128], vvb[:, 64 * g:64 * g + 64], True, True) for g in range(4)]
                        bsap = prm(l, "bs", 4)
                        wtok = [("sguc", 0)]
                    else:
                        items = [(PS[k2][0:32, 64 * g:64 * g + 64], WSTS(l, g), vvb[:, 64 * g:64 * g + 64], True, True) for g in range(4)]
                        bsap = prm(l, "bss", 4, 0, 32)
                        wtok = ["wsts"] + [("wsts", l, b) for b in range(2)]
                    s1.append(lambda: mm_group(None, items, reads=sc(2) + wtok, writes=[("ps", k2)]))
                    s2.append(lambda: S.op("dve", lambda e: e.tensor_tensor(
                        sg.rearrange("p (g d) -> p g d", g=4), PS[k2][0:rows, 0:256].rearrange("p (g d) -> p g d", g=4),
                        bsap.unsqueeze(2).to_broadcast([rows, 4, 64]), ALU.add),
                        reads=[("ps", k2), "prm"], writes=sc(4)))
                    s2.append(lambda: S.op("dve", lambda e: e.tensor_tensor(sg, sg, z[:, 0:256], ALU.mult), reads=sc(0, 1, 4), writes=sc(4)))
                    s2.append(lambda: S.op("act", lambda e: e.activation(sgb, sg, AF.Square, accum_out=sss),
                                           reads=sc(4), writes=sc(5) + [ssst]))
                    s2.append(lambda: S.op("act", lambda e: e.activation(sss, sss, AF.Ln, bias=EPS, scale=1.0 / 256), reads=[ssst], writes=[ssst]))
                    s2.append(lambda: S.op("act", lambda e: e.activation(RSGU(i, rows), sss, AF.Exp, scale=-0.5),
                                           reads=[ssst], writes=[("rsgu", i)]))
                    s2.append(lambda: S.op("dve", lambda e: e.tensor_copy(sgb, sg), reads=sc(4, 5), writes=sc(5)))

                    def trs():
                        kt = psnext("o")

                        def tr(e):
                            ins = None
                            for cc in range(2):
                                ins = e.transpose(PSB[kt][:, cc * 128:cc * 128 + rows], sgb[:, cc * 128:(cc + 1) * 128], IDENT[0:rows, 0:rows])
                            return ins
                        S.op("pe", tr, reads=sc(5) + ["cst"], writes=[("ps", kt)])
                        for cc in range(2):
                            S.op("dve", lambda e, cc=cc, l=l: e.tensor_scalar(
                                MX(4 + cc, c0, n), PSB[kt][:, cc * 128:cc * 128 + rows], prm(l, "ggo", 1, 0, 128, 4 + cc), None, ALU.mult),
                                reads=[("ps", kt), "prm"], writes=[("mx", 4 + cc, i)])
                    s2.append(trs)
                    return s1, s2

                def interleave(a, b):
                    for q in range(max(len(a), len(b))):
                        if q < len(a):
                            a[q]()
                        if q < len(b):
                            b[q]()

                def vk_tile(i):
                    rows = tile_rows(i)
                    c0, n = tcols(i)
                    for which, slot, oprompt, osample, sg in (("v", 0, ofv, osv, 4), ("k", 1, ofk, osk, 6)):
                        k = psnext("g")
                        mm_group(None, [(PS[k][0:rows, :], HT(c, c0, n), WS(slot, c, 0, 512), c == 0, c == 7)
                                        for c in range(8)],
                                 reads=[("ht", i), ("ws", slot)], writes=[("ps", k)])
                        stg = SCRf(sg, 512, 0, rows)
                        S.op(("act" if which == "v" else "dve"), (lambda e, stg=stg, k=k, rows=rows: e.copy(stg, PS[k][0:rows, :])) if which == "v"
                             else (lambda e, stg=stg, k=k, rows=rows: e.tensor_copy(stg, PS[k][0:rows, :])),
                             reads=[("ps", k)], writes=scr(sg, sg + 1))
                        if which == "v":
                            va3 = AR[0:rows, VA_O // 2 + i * 520: VA_O // 2 + (i + 1) * 520].rearrange(
                                "p (h k) -> p h k", k=65)[:, :, 0:64]
                            S.op("dve", lambda e, va3=va3, k=k, rows=rows: e.tensor_copy(
                                va3, PS[k][0:rows, :].rearrange("p (h k) -> p h k", k=64)),
                                reads=[("ps", k), "va_ones"], writes=[("va", i)])
                        dst = oprompt[l, 128 * i:128 * i + 128, :] if i < 16 else osample[l, :, :]
                        dma("sp", dst, stg, reads=scr(sg, sg + 1), key=("stg", sg))

                sgu_pipe = {"B": [], "C": []}

                def sgu_step(t):
                    if t is not None:
                        s1_, s2_ = sgu_ops(t)
                        sA, sB, sC = s1_[:-1], [s1_[-1]] + s2_[:-1], [s2_[-1]]
                    else:
                        sA, sB, sC = [], [], []
                    interleave(sA, sgu_pipe["B"])
                    for f_ in sgu_pipe["C"]:
                        f_()
                    sgu_pipe["C"] = sgu_pipe["B_c"] if "B_c" in sgu_pipe else []
                    sgu_pipe["B"] = sB
                    sgu_pipe["B_c"] = sC

                def pn_a(i):
                    rows = tile_rows(i)
                    c0, n = tcols(i)
                    return prenorm_tile(X[0:rows, i, :], rows, ("x", i), prm(l, "gpm", 8), HT3(c0, n), ("ht", i), i % 2, defer=True)
                pn_b = {0: pn_a(0)}
                for i in range(17):
                    if i + 1 < 17:
                        pn_b[i + 1] = pn_a(i + 1)
                    pn_b.pop(i)()
                    if i >= 1:
                        vk_tile(i - 1)
                        sgu_step(i - 1)
                vk_tile(16)
                sgu_step(16)
                sgu_step(None)
                sgu_step(None)
                S.op("dve", lambda e: e.memset(bf(MX_O + 15 * 1024, 2), 0.0),
                     writes=[("sguA", g) for g in range(6)] + [("sguB", g) for g in range(6)] + [("sguc", 0), ("sguc", 1)]
                     + [("mx", c, i, p0) for c in (0, 1, 2, 3) for i in range(17) for p0 in (0, 64)])

                _chk("vk%d" % l)
                NB = SM[0:8, 210:211]
                S.op("dve", lambda e, l=l: e.tensor_scalar(NB, prm(l, "bf", 1, 0, 8), -1.0, None, ALU.mult),
                     reads=["prm"], writes=["cw", "nb"])
                for bi, (c0, n) in enumerate(BLKS):
                    k = psnext("g")
                    mm_group(None, [(PS[k][0:8, 0:n], WS(2, c, 0, 8), HT(c, c0, n), c == 0, c == 7) for c in range(8)],
                             reads=ALLHT + [("ws", 2)], writes=["cw", ("ps", k)])
                    tmp = SCRf(0, 512, 96, 104)
                    S.op("act", lambda e, k=k, n=n, tmp=tmp: e.activation(tmp[:, 0:n], PS[k][0:8, 0:n], AF.Exp, bias=NB, scale=-1.0),
                         reads=[("ps", k), "nb"], writes=scr(0, 1))
                    S.op("act", lambda e, n=n, tmp=tmp: e.activation(tmp[:, 0:n], tmp[:, 0:n], AF.Ln, bias=1.0, scale=1.0),
                         reads=scr(0, 1), writes=scr(0, 1))
                    S.op("dve", lambda e, c0=c0, n=n, tmp=tmp: e.tensor_scalar(CT(c0, n), tmp[:, 0:n], -1.0, None, ALU.mult),
                         reads=scr(0, 1), writes=["cw", ("ct", bi)])
                dma("sp", olf[l], CT(0, 2048), reads=[("ct", b) for b in range(4)], writes=["cw"], key="olf")
                dma("sp", oslf[l], CT(2048, 32), reads=[("ct", 4)], writes=["cw"], key="olf2")
                SLF = SM[96:104, 220:252]
                S.op("dve", lambda e: e.tensor_copy(SLF, CT(2048, 32)), reads=[("ct", 4)], writes=["cw", "slf"])
                S.op("dve", lambda e: e.tensor_tensor_scan(CT(0, 2048), ONE8.to_broadcast([8, 2048]), CT(0, 2048), 0.0, ALU.mult, ALU.add),
                     reads=[("ct", b) for b in range(4)] + ["one"], writes=["cw", "ctp"])
                S.op("dve", lambda e: e.tensor_copy(HI(0, 2048), CT(0, 2048)), reads=["ctp"], writes=["cw", "hi"])
                S.op("dve", lambda e: e.tensor_tensor(CT(0, 2048), CT(0, 2048), HI(0, 2048), ALU.subtract),
                     reads=["ctp", "hi"], writes=["cw", "ctp"])
                S.op("dve", lambda e: e.tensor_copy(LO(0, 2048), CT(0, 2048)), reads=["ctp"], writes=["cw", "lo"])
                dma("sp", cscr_p[:, 0, :], HI(0, 2048), reads=["hi"], writes=["cw", "cscr_p"], key="cscr0")
                dma("sp", cscr_p[:, 1, :], LO(0, 2048), reads=["lo"], writes=["cw", "cscr_p2"], key="cscr1")
                CS = lambda b, c0, n: CT(b * 1040 + c0, n)
                for b in range(2):
                    dma("sp", CS(b, 0, 1024), clfT[l, b], reads=["ctp", "lo", ("ct", 4), "slf"], writes=["cw", ("cs", b)], key=("csl", b))
                    S.op("dve", lambda e, b=b: e.tensor_copy(CS(b, 1024, 16), SM[96:104, 220 + 16 * b:236 + 16 * b]),
                         reads=["slf", "ctp", "lo", ("ct", 4)], writes=["cw", ("cs2", b)])
                    S.op("dve", lambda e, b=b: e.tensor_tensor_scan(CS(b, 0, 1040), ONE8.to_broadcast([8, 1040]), CS(b, 0, 1040), 0.0, ALU.mult, ALU.add),
                         reads=[("cs", b), ("cs2", b), "one"], writes=["cw", ("csc", b)])
                S.op("dve", lambda e: e.tensor_copy(HI(0, 2080), CT(0, 2080)),
                     reads=[("csc", 0), ("csc", 1), "cscr_p", "cscr_p2"], writes=["cw", "hi"])
                S.op("dve", lambda e: e.tensor_tensor(CT(0, 2080), CT(0, 2080), HI(0, 2080), ALU.subtract),
                     reads=["hi"], writes=["cw", ("csc", 0), ("csc", 1)])
                S.op("dve", lambda e: e.tensor_copy(LO(0, 2080), CT(0, 2080)), reads=[("csc", 0), ("csc", 1), "cscr_p2"], writes=["cw", "lo"])
                for b in range(2):
                    dma("sp", cscr_s[:, 0, b, :], HI(b * 1040, 1040), reads=["hi"], writes=["cw", ("cscr_s", b, 0)], key=("cscrs", b, 0))
                    dma("sp", cscr_s[:, 1, b, :], LO(b * 1040, 1040), reads=["lo"], writes=["cw", ("cscr_s", b, 1)], key=("cscrs", b, 1))

                _chk("fg%d" % l)
                _chk("sgu%d" % l)
                S.op("dve", lambda e: e.memset(bf(QK_O, 4 * NT, 64, 128), 0.0),
                     writes=["cw", "hi", "lo"] + [("qkc", s_) for s_ in range(4)])
                for s_ in (0, 2):
                    S.op("dve", lambda e, s_=s_: e.memset(QK(s_, 0, NT, 64, 67), -1.0), writes=[("qkc", s_)])
                for s_ in (1, 3):
                    S.op("dve", lambda e, s_=s_: e.memset(QK(s_, 0, NT, 64, 65), 1.0), writes=[("qkc", s_)])
                wload(0, w_mkv[l], 512)
                MEMT3 = AR[:, (SCR_O + 4096) // 2:(SCR_O + 4096) // 2 + 2048].rearrange("p (c t) -> p c t", c=8)
                MEMT = lambda c, c0, n: bf(SCR_O + 4096 + (c * 256 + c0) * 2, n)
                memx = f32(WS_O[2], 1024)
                for mt in range(2):
                    dma("sp", memx, mem[128 * mt:128 * mt + 128, :], writes=[("ws", 2)], key=("ws", 2))
                    prenorm_tile(memx, 128, ("ws", 2), prm(l, "gmem", 8), MEMT3[:, :, 128 * mt:128 * mt + 128],
                                 ("memt", mt), 0, extra_writes=scr(4, 5, 6, 7))
                memt_toks = [("memt", 0), ("memt", 1)] + scr(4, 5, 6, 7)
                stgk = f32(WS_O[2], 512)
                for mt in range(2):
                    k = psnext("g")
                    mm_group(None, [(PS[k][:, :], MEMT(c, 128 * mt, 128), WS(0, c, 0, 512), c == 0, c == 7) for c in range(8)],
                             reads=memt_toks + [("ws", 0)], writes=[("ps", k)])
                    S.op("act", lambda e, k=k: e.copy(stgk, PS[k][:, :]), reads=[("ps", k)], writes=[("ws", 2)])
                    S.op("dve", lambda e, k=k, mt=mt: e.tensor_copy(
                        MVAt[:, mt * 260:(mt + 1) * 260].rearrange("p (h k) -> p h k", k=65)[:, :, 0:64],
                        PS[k][:, 256:512].rearrange("p (h k) -> p h k", k=64)),
                        reads=[("ps", k), "mva_ones"], writes=[("mva", mt)])
                    dma("sp", omk[l, 128 * mt:128 * mt + 128, :], stgk[:, 0:256], reads=[("ws", 2)], key="omk")
                    dma("sp", omv[l, 128 * mt:128 * mt + 128, :], stgk[:, 256:512], reads=[("ws", 2)], key="omv")
                for pr in range(2):
                    k = psnext("g")
                    mm_group(None, [(PS[k][:, 0:256], WS(0, c, 128 * pr, 128), MEMT(c, 0, 256), c == 0, c == 7) for c in range(8)],
                             reads=memt_toks + [("ws", 0)], writes=[("ps", k)])
                    for hh in range(2):
                        h = 2 * pr + hh
                        S.op("act", lambda e, k=k, hh=hh, h=h: e.copy(MKTt[0:64, h * 256:(h + 1) * 256], PS[k][64 * hh:64 * hh + 64, 0:256]),
                             reads=[("ps", k)], writes=[("mkt", h)])
                wload(2, w_in[l][:, 2056:2312], 256)
                for b in range(2):
                    S.op("dve", lambda e, b=b: e.memset(KAS(b, 0, 1040, 64, 65), 1.0), reads=[("ws", 0)], writes=[("ws", 0), ("sc", b, "one")])
                    S.op("dve", lambda e, b=b: e.memset(VAS3(b)[:, :, 64:65], 1.0), reads=[("ws", 0)], writes=[("ws", 0), ("sc", b, "one")])

                S.op("dve", lambda e: e.memset(SM[:, 160:200], 0.0),
                     writes=[("ssq", g, i) for g in (0, 2) for i in range(17)])

                _chk("memkv%d" % l)
                for pr in range(2):
                    for bi, (c0, n) in enumerate(BLKS):
                        k = psnext("g")
                        mm_group(None, [(PS[k][:, 0:n], WS(2, c, 128 * pr, 128), HT(c, c0, n), c == 0, c == 7) for c in range(8)],
                                 reads=ALLHT + [("ws", 2)], writes=[("ps", k)])
                        for hh in range(2):
                            S.op("act", lambda e, k=k, hh=hh, c0=c0, n=n: e.activation(
                                QK(2 * hh, c0, n, 0, 64), PS[k][64 * hh:64 * hh + 64, 0:n], AF.Copy, scale=0.125),
                                reads=[("ps", k)], writes=[("qa", hh, bi)])
                    for hh in range(2):
                        h = 2 * pr + hh
                        for I in range(4):
                            ob = psnext("o")
                            kts = [dict(ka=MKTt[0:128, h * 256 + 128 * j:h * 256 + 128 * j + 128], nk=128,
                                        va=bf(MVA_O + ((j * 4 + h) * 65) * 2, 128), lo=0, hi=512, pv_lo=0,
                                        mask=None, pt="rot", reads=[("mkt", h), ("qa", hh, I), "mkt_zero", ("qkc", 0), ("qkc", 2)],
                                        vreads=[("mva", j), "mva_ones"])
                                   for j in range(2)]
                            osubs = [(PS[ob][:, 65 * s:65 * s + 65], 128 * s, 128 * s + 128, 0, 1) for s in range(4)]
                            attend(lambda c0, n, hh=hh, I=I: QK(2 * hh, 512 * I + c0, n, 0, 128), 128, kts, "fm", ob, "mem")
                            nsq.step(lambda ob=ob, I=I, pr=pr, hh=hh: norm_store_fm(ob, I, 2, 6 + pr, 64 * hh, SM[:, 180 + 4 * I:184 + 4 * I], l))
                        ob = psnext("g")
                        kts = []
                        for b in range(2):
                            dma("pool", KAS(b, 0, 256, 0, 64), cmkT[l, b, h], reads=[("ws", 0)], writes=[("sc", b, "k")], key=("sck", b))
                            dma("pool", VAS3(b)[:, 0:2, 0:64],
                                cmv[l, b].rearrange("(j p) (h d) -> p j h d", p=128, h=4)[:, :, h, :],
                                reads=[("ws", 0)], writes=[("sc", b, "v")], key=("scv", b), slow=True)
                            for j in range(2):
                                kts.append(dict(ka=KAS(b, 128 * j, 128, 0, 64), nk=128, va=VAS(b, j),
                                                lo=16 * b, hi=16 * b + 16, pv_lo=0, mask=None, pt=3 * b + (j % 3),
                                                reads=[("sc", b, "k"), ("qa", hh, 4)], vreads=[("sc", b, "v"), ("sc", b, "one")]))
                        osubs = [(PS[ob][0:32, 0:65], 0, 32, 0, len(kts) - 1)]
                        attend(lambda c0, n, hh=hh: QK(2 * hh, 2048 + c0, n, 0, 64), 64, kts, osubs, ob, "mems")
                        flush_pv()
                        norm_and_store(ob, 0, 32, 16, 2, 6 + pr, 64 * hh, SSMEM(16, 32), l)

                flush_pv()
                nsq.flush()
                _chk("memattn%d" % l)
                wload(2, w_in[l][:, 0:512], 512)
                for pr in range(4):
                    for bi, (c0, n) in enumerate(BLKS):
                        for qk, slot in ((0, 2), (1, 1)):
                            k = psnext("g")
                            mm_group(None, [(PS[k][:, 0:n], WS(slot, c, 128 * pr, 128), HT(c, c0, n), c == 0, c == 7) for c in range(8)],
                                     reads=ALLHT + [("ws", slot)], writes=[("ps", k)])
                            for hh in range(2):
                                if qk == 0:
                                    S.op("act", lambda e, k=k, hh=hh, c0=c0, n=n: e.activation(
                                        QK(2 * hh, c0, n, 0, 64), PS[k][64 * hh:64 * hh + 64, 0:n], AF.Copy, scale=0.125),
                                        reads=[("ps", k)], writes=[("qa", hh, bi)])
                                else:
                                    S.op("dve", lambda e, k=k, hh=hh, c0=c0, n=n: e.tensor_copy(
                                        QK(2 * hh + 1, c0, n, 0, 64), PS[k][64 * hh:64 * hh + 64, 0:n]),
                                        reads=[("ps", k)], writes=[("ka", hh, bi)])
                    if pr == 3:
                        wload(2, w_out[l][:, 0:512], 512)
                        wload(1, w_out[l][:, 512:1024], 512)
                    for hh in range(2):
                        h = 2 * pr + hh
                        dma("sp", QK(2 * hh, 0, 2048, 64, 65), cscr_p[h:h + 1, 0, :], reads=["cscr_p", ("qkc", 2 * hh)],
                            writes=[("qar", hh, 0)], key=("qar", hh))
                        dma("sp", QK(2 * hh + 1, 0, 2048, 65, 66), cscr_p[h:h + 1, 0, :], reads=["cscr_p", ("qkc", 2 * hh + 1)],
                            writes=[("kar", hh, 0)], key=("kar", hh))
                        dma("sp", QK(2 * hh + 1, 0, 2048, 66, 67), cscr_p[h:h + 1, 1, :], reads=["cscr_p2"],
                            writes=[("kar", hh, 1)], key=("kar", hh))
                        for b in range(2):
                            dma("sp", QK(2 * hh, 2048 + 16 * b, 16, 64, 65), cscr_s[h:h + 1, 0, b, 1024:1040],
                                reads=[("cscr_s", b, 0)], writes=[("qar", hh, 1 + b)], key=("qar", hh))
                            dma("sp", QK(2 * hh + 1, 2048 + 16 * b, 16, 65, 66), cscr_s[h:h + 1, 0, b, 1024:1040],
                                reads=[("cscr_s", b, 0)], writes=[("kar", hh, 2 + b)], key=("kar", hh))
                            dma("sp", QK(2 * hh + 1, 2048 + 16 * b, 16, 66, 67), cscr_s[h:h + 1, 1, b, 1024:1040],
                                reads=[("cscr_s", b, 1)], writes=[("kar", hh, 4 + b)], key=("kar", hh))
                        qar = [("qar", hh, x) for x in range(3)] + [("qkc", 2 * hh)]
                        kar = [("kar", hh, x) for x in range(6)] + [("qkc", 2 * hh + 1)]
                        for I in range(4):
                            ob = psnext("o")
                            kts = []
                            for j in range(4 * I + 4):
                                a = j - 4 * I
                                kts.append(dict(ka=QK(2 * hh + 1, 128 * j, 128, 0, 128), nk=128, va=VA128(j, h),
                                                lo=(128 * a if a >= 0 else 0), hi=512, pv_lo=(128 * a if a >= 0 else 0),
                                                mask=((MASKNEG, 128) if a >= 0 else None), pt="rot",
                                                reads=[("ka", hh, j // 4), ("qa", hh, I)] + qar + kar,
                                                vreads=[("va", j), "va_ones"]))
                            osubs = [(PS[ob][:, 65 * s:65 * s + 65], 128 * s, 128 * s + 128, 0, 4 * I + s) for s in range(4)]
                            attend(lambda c0, n, hh=hh, I=I: QK(2 * hh, 512 * I + c0, n, 0, 128), 128, kts, "fm", ob, "fox")
                            nsq.step(lambda ob=ob, I=I, pr=pr, hh=hh: norm_store_fm(ob, I, 0, pr, 64 * hh, SM[:, 160 + 4 * I:164 + 4 * I], l))
                        ob = psnext("g")
                        kts = []
                        for b in range(2):
                            dma("pool", KAS(b, 0, 1024, 0, 64), ckT[l, b, h], reads=[("ws", 0)], writes=[("sc", b, "k")], key=("sck", b))
                            dma("pool", VAS3(b)[:, :, 0:64],
                                cv[l, b].rearrange("(j p) (h d) -> p j h d", p=128, h=8)[:, :, h, :],
                                reads=[("ws", 0)], writes=[("sc", b, "v")], key=("scv", b), slow=True)
                            dma("sp", KAS(b, 0, 1024, 65, 66), cscr_s[h:h + 1, 0, b, 0:1024], reads=[("cscr_s", b, 0), ("ws", 0)],
                                writes=[("sc", b, "r")], key=("scr", b))
                            dma("sp", KAS(b, 0, 1024, 66, 67), cscr_s[h:h + 1, 1, b, 0:1024], reads=[("cscr_s", b, 1), ("ws", 0)],
                                writes=[("sc", b, "r2")], key=("scr", b))
                            for j in range(8):
                                kts.append(dict(ka=KAS(b, 128 * j, 128), nk=128, va=VAS(b, j),
                                                lo=16 * b, hi=16 * b + 16, pv_lo=0, mask=None, pt=3 * b + (j % 3),
                                                reads=[("sc", b, "k"), ("sc", b, "r"), ("sc", b, "r2"), ("sc", b, "one"), ("qa", hh, 4)] + qar,
                                                vreads=[("sc", b, "v"), ("sc", b, "one")]))
                        kts.append(dict(ka=QK(2 * hh + 1, 2048, 32, 0, 67), nk=32, va=VA(16, h, 32), lo=0, hi=32, pv_lo=0,
                                        mask=(MASKS, 32), pt=6, reads=[("ka", hh, 4), ("qa", hh, 4)] + qar + kar,
                                        vreads=[("va", 16), "va_ones"]))
                        osubs = [(PS[ob][0:32, 0:65], 0, 32, 0, len(kts) - 1)]
                        attend(lambda c0, n, hh=hh: QK(2 * hh, 2048 + c0, n, 0, 67), 67, kts, osubs, ob, "foxs")
                        flush_pv()
                        norm_and_store(ob, 0, 32, 16, 0, pr, 64 * hh, SSFOX(16, 32), l)

                flush_pv()
                nsq.flush()
                _chk("fox%d" % l)
                S.op("act", lambda e: e.activation(SM[:, 100:117], SM[:, 160:177], AF.Ln, bias=EPS, scale=1.0 / 512),
                     reads=[("ssq", 0, i) for i in range(17)], writes=["rfox"])
                S.op("act", lambda e: e.activation(SM[:, 100:117], SM[:, 100:117], AF.Exp, scale=-0.5), reads=["rfox"], writes=["rfox"])
                S.op("act", lambda e: e.activation(SM[:, 140:157], SM[:, 180:197], AF.Ln, bias=EPS, scale=1.0 / 256),
                     reads=[("ssq", 2, i) for i in range(17)], writes=["rmem"])
                S.op("act", lambda e: e.activation(SM[:, 140:157], SM[:, 140:157], AF.Exp, scale=-0.5), reads=["rmem"], writes=["rmem"])

                GP = SCRf(4, 1024)
                dma("sp", GP, g_post_mix[l:l + 1, :].broadcast_to([128, D]), writes=scr(4, 5, 6, 7), key="gp")
                for i in range(17):
                    rows = tile_rows(i)
                    c0, n = tcols(i)
                    ostg = SCRf(0, 1024, 0, rows)
                    mxr = [("mx", c, i, p0) for c in (0, 1, 2, 3, 6, 7) for p0 in (0, 64)] + [("mx", 4, i), ("mx", 5, i)]
                    last_b = None
                    for hf in range(2):
                        banks = [psnext("w") for _ in range(3)]
                        for gi, (cs, b) in enumerate(zip(((0, 1, 2, 3), (4, 5), (6, 7)), banks)):
                            wsl = 2 if hf == 0 else 1
                            mm_group(None, [(PS[b][0:rows, :], MX(c, c0, n), WS(wsl, c, 0, 512), c == cs[0], c == cs[-1]) for c in cs],
                                     reads=mxr + [("ws", wsl)], writes=[("ps", b)])
                        o_h = ostg[:, 512 * hf:512 * hf + 512]
                        S.op("act", lambda e, o_h=o_h, b=banks[0], rows=rows, i=i: e.activation(o_h, PS[b][0:rows, :], AF.Copy, scale=RFOX(i, rows)),
                             reads=[("ps", banks[0]), "rfox"], writes=scr(2 * hf, 2 * hf + 1))
                        S.op("dve", lambda e, o_h=o_h, b=banks[1], rows=rows, i=i: e.scalar_tensor_tensor(o_h, PS[b][0:rows, :], RSGU(i, rows), o_h, ALU.mult, ALU.add),
                             reads=[("ps", banks[1]), ("rsgu", i)] + scr(2 * hf, 2 * hf + 1), writes=scr(2 * hf, 2 * hf + 1))
                        S.op("dve", lambda e, o_h=o_h, b=banks[2], rows=rows, i=i: e.scalar_tensor_tensor(o_h, PS[b][0:rows, :], RMEM(i, rows), o_h, ALU.mult, ALU.add),
                             reads=[("ps", banks[2]), "rmem"] + scr(2 * hf, 2 * hf + 1), writes=scr(2 * hf, 2 * hf + 1))
                        last_b = banks
                    post_norm_residual(ostg, rows, i, bf(WS_O[0], 1024, 0, rows), [("ws", 0)] + SC_TOKS, GP, scr(4, 5, 6, 7), scr(0, 1, 2, 3))

                _chk("wout%d" % l)
                S.barrier()
                ffn_phase(l)
                if l < L - 1:
                    S.barrier()

        except _Stop:
            pass
        S.emit(st)
    return nc


_NC_CACHE = {}


def _prep_inputs(inp):
    f = lambda a: np.ascontiguousarray(np.asarray(a, dtype=np.float32))
    x_prompt = f(inp["x_prompt"]); x_sample = f(inp["x_sample"]); mem_prompt = f(inp["mem_prompt"])
    cfk = f(inp["cache_fox_k"]); cfv = f(inp["cache_fox_v"]); clf = f(inp["cache_fox_logf"])
    cmk = f(inp["cache_mem_k"]); cmvv = f(inp["cache_mem_v"]); cconv = f(inp["cache_ffn_conv"])
    ckT_all = np.ascontiguousarray(cfk.transpose(0, 1, 3, 4, 2))
    cv_all = cfv.reshape(L, 16, 1024, 512)
    clfT_all = np.ascontiguousarray(clf.transpose(0, 1, 3, 2))
    cmkT_all = np.ascontiguousarray(cmk.transpose(0, 1, 3, 4, 2))
    cmv_all = cmvv.reshape(L, 16, 256, 256)
    wsT = np.ascontiguousarray(f(inp["w_spatial"]).transpose(0, 1, 3, 2))
    fm8 = lambda g: f(g).reshape(L, 8, 128).transpose(0, 2, 1)
    b_sp = f(inp["b_spatial"])
    w_dw = f(inp["w_dwconv"]); b_dw = f(inp["b_dwconv"]); b_f = f(inp["b_forget"])
    cst = np.zeros((128, 288), dtype=ml_dtypes.bfloat16)
    cst[:, 0:128] = np.eye(128, dtype=np.float32).astype(ml_dtypes.bfloat16)
    kk, qq = np.meshgrid(np.arange(128), np.arange(128), indexing="ij")
    cst[:, 128:256] = np.where(kk <= qq, 0.0, NEG).astype(ml_dtypes.bfloat16)
    k2, q2 = np.meshgrid(np.arange(32), np.arange(32), indexing="ij")
    ok = (k2 // 16 == q2 // 16) & (k2 % 16 <= q2 % 16)
    cst[0:32, 256:288] = np.where(ok, 0.0, NEG).astype(ml_dtypes.bfloat16)
    wu = f(inp["w_up"])
    wu_a = wu[:, :, 0:FF].reshape(L, 8, 128, NJ, 128)
    wu_l = wu[:, :, FF:2 * FF].reshape(L, 8, 128, NJ, 128)
    w_up_r = np.ascontiguousarray(np.concatenate([wu_a, wu_l], axis=4).transpose(0, 3, 2, 1, 4))
    shared = dict(
        w_in=f(inp["w_in"]), w_mkv=f(inp["w_mem_kv"]), w_out=f(inp["w_out"]), w_up=w_up_r,
        w_dn=f(inp["w_down"]), g_post_mix=f(inp["g_post_mix"]), g_post_ffn=f(inp["g_post_ffn"]),
        g_sgu=f(inp["g_sgu"]), wsT=wsT, cst=cst)
    gpm = fm8(inp["g_pre_mix"]); ggo = fm8(inp["g_group_out"]); gmem = fm8(inp["g_mem"]); gpf = fm8(inp["g_pre_ffn"])
    in_maps = []
    for c in range(NCORES):
        prm = np.zeros((128, NPRM), dtype=np.float32)
        for l in range(L):
            o = l * PPL
            prm[:, o + PO["gpm"]:o + PO["gpm"] + 8] = gpm[l]
            prm[:, o + PO["ggo"]:o + PO["ggo"] + 8] = ggo[l]
            prm[:, o + PO["gmem"]:o + PO["gmem"] + 8] = gmem[l]
            prm[:, o + PO["gpf"]:o + PO["gpf"] + 8] = gpf[l]
            prm[:, o + PO["bs"]:o + PO["bs"] + 4] = b_sp[l].T
            prm[0:16, o + PO["bss"]:o + PO["bss"] + 4] = b_sp[l][:, 0:16].T
            prm[16:32, o + PO["bss"]:o + PO["bss"] + 4] = b_sp[l][:, 0:16].T
            prm[:, o + PO["wdw"]:o + PO["wdw"] + 66] = w_dw[l].reshape(3, NJ, 128).transpose(2, 1, 0).reshape(128, 66)
            prm[:, o + PO["bdw"]:o + PO["bdw"] + 22] = b_dw[l].reshape(NJ, 128).T
            prm[0:8, o + PO["bf"]] = b_f[l]
            cc = cconv[l, 2 * c:2 * c + 2]
            prm[:, o + PO["cconv"]:o + PO["cconv"] + 88] = cc.reshape(2, 2, NJ, 128).transpose(3, 2, 0, 1).reshape(128, 88)
        m = dict(shared)
        m.update(
            x_p=x_prompt[c], x_s=x_sample[2 * c:2 * c + 2].reshape(32, D), mem=mem_prompt[c],
            ckT=np.ascontiguousarray(ckT_all[:, 2 * c:2 * c + 2]), cv=np.ascontiguousarray(cv_all[:, 2 * c:2 * c + 2]),
            clfT=np.ascontiguousarray(clfT_all[:, 2 * c:2 * c + 2]), cmkT=np.ascontiguousarray(cmkT_all[:, 2 * c:2 * c + 2]),
            cmv=np.ascontiguousarray(cmv_all[:, 2 * c:2 * c + 2]), prm=prm)
        in_maps.append(m)
    return in_maps


def kernel(**inp):
    in_maps = _prep_inputs(inp)
    if "nc" not in _NC_CACHE:
        _NC_CACHE["nc"] = build()
    nc = _NC_CACHE["nc"]
    res = run_bass_kernel_spmd(nc, in_maps, core_ids=list(range(NCORES)))
    R = res.results
    cat = lambda name: np.stack([np.asarray(r[name], dtype=np.float32) for r in R], axis=0)
    y_p = cat("y_p")
    y_s = cat("y_s").reshape(16, 16, D)
    fk = cat("ofk").transpose(1, 0, 2, 3).reshape(L, 8, 2048, 8, 64)
    fv = cat("ofv").transpose(1, 0, 2, 3).reshape(L, 8, 2048, 8, 64)
    lf = cat("olf").transpose(1, 0, 3, 2)
    mk = cat("omk").transpose(1, 0, 2, 3).reshape(L, 8, 256, 4, 64)
    mv = cat("omv").transpose(1, 0, 2, 3).reshape(L, 8, 256, 4, 64)
    cvp = cat("oconv").transpose(1, 0, 4, 3, 2).reshape(L, 8, 2, FF)
    sk = cat("osk").transpose(1, 0, 2, 3).reshape(L, 16, 16, 8, 64)
    sv = cat("osv").transpose(1, 0, 2, 3).reshape(L, 16, 16, 8, 64)
    slf = cat("oslf").transpose(1, 0, 3, 2).reshape(L, 16, 16, 8)
    gv = cat("ogv").transpose(1, 0, 2, 3).reshape(L, 16, 16, 256)
    cvs = cat("osconv").transpose(1, 0, 4, 5, 3, 2).reshape(L, 16, 2, FF)
    outs = (y_p, y_s, fk, fv, lf, mk, mv, cvp, sk, sv, slf, gv, cvs)
    return tuple(np.ascontiguousarray(o, dtype=np.float32) for o in outs)
```

```python
import numpy as np
import ml_dtypes
from contextlib import ExitStack
import concourse.bass as bass
import concourse.mybir as mybir
from concourse.bass_utils import run_bass_kernel_spmd

F32 = mybir.dt.float32
BF16 = mybir.dt.bfloat16
ALU = mybir.AluOpType
AF = mybir.ActivationFunctionType

ENGS = ("pe", "act", "dve", "pool", "sp")
STOP = None
DBG = set()


class _Stop(Exception):
    pass


def _chk(name):
    if STOP == name:
        raise _Stop()
NCORES = 8
L = 2
D = 1024
NT = 2080
FF = 2816
NJ = 22
EPS = 1e-6
NEG = -30000.0


class Op:
    __slots__ = ("eng", "fn", "idx", "deps", "dma_key", "marked", "seq", "cum")

    def __init__(self, eng, fn, idx, dma_key):
        self.eng = eng
        self.fn = fn
        self.idx = idx
        self.deps = ()
        self.dma_key = dma_key
        self.marked = False
        self.seq = 0
        self.cum = 0


class Sched:
    def __init__(self, nc):
        self.nc = nc
        self.streams = {e: [] for e in ENGS}
        self.last_writer = {}
        self.readers = {}
        self.dma_counts = {}
        self.dma_last = {}

    def op(self, eng, fn, reads=(), writes=(), dma_key=None):
        o = Op(eng, fn, len(self.streams[eng]), dma_key)
        deps = set()
        lw = self.last_writer
        rd = self.readers
        for t in reads:
            w = lw.get(t)
            if w is not None:
                deps.add(w)
            if type(t) is tuple and t[0] == "ps":
                for r in rd.get(t, ()):
                    if r.eng != eng:
                        deps.add(r)
        for t in writes:
            w = lw.get(t)
            if w is not None:
                deps.add(w)
            r = rd.get(t)
            if r:
                deps.update(r)
        for t in reads:
            rd.setdefault(t, []).append(o)
        for t in writes:
            lw[t] = o
            rd[t] = []
        deps.discard(o)
        if eng == "pe":
            deps = {d for d in deps if not (d.eng == "pe" and d.dma_key is None)}
        o.deps = deps
        if dma_key is not None:
            c = self.dma_counts.get(dma_key, 0) + 1
            self.dma_counts[dma_key] = c
            o.cum = 16 * c
            self.dma_last[dma_key] = o
        self.streams[eng].append(o)
        return o

    def barrier(self):
        lasts = [s[-1] for s in self.streams.values() if s]
        lasts += list(self.dma_last.values())
        for e in ENGS:
            o = Op(e, lambda eng: eng.nop(), len(self.streams[e]), None)
            o.deps = {d for d in lasts if not (d.eng == e and d.dma_key is None)}
            self.streams[e].append(o)

    def emit(self, stack):
        nc = self.nc
        for e in ENGS:
            for o in self.streams[e]:
                for d in o.deps:
                    if d.dma_key is None:
                        d.marked = True
        sems = {}
        for e in ENGS:
            n = 0
            for o in self.streams[e]:
                if o.dma_key is None and o.marked:
                    n += 1
                    o.seq = n
            if n:
                sems[e] = stack.enter_context(nc.semaphore("s_" + e))
        dsems = {}
        for k in self.dma_counts:
            dsems[k] = stack.enter_context(nc.semaphore("d%d" % len(dsems)))
        self.n_sems = len(sems) + len(dsems)
        final = {k: o.cum for k, o in self.dma_last.items()}

        def run(eng_name, eng):
            waited = {}
            for o in self.streams[eng_name]:
                need = {}
                for d in o.deps:
                    if d.dma_key is not None:
                        key = ("d", d.dma_key)
                        val = d.cum
                    else:
                        key = ("c", d.eng)
                        val = d.seq
                    if val > need.get(key, 0):
                        need[key] = val
                for key, val in need.items():
                    if waited.get(key, 0) >= val:
                        continue
                    waited[key] = val
                    s = dsems[key[1]] if key[0] == "d" else sems[key[1]]
                    eng.wait_ge(s, val)
                ins = o.fn(eng)
                if o.dma_key is not None:
                    ins.then_inc(dsems[o.dma_key], 16)
                elif o.marked:
                    ins.then_inc(sems[o.eng], 1)
            if eng_name == "sp":
                for k, v in final.items():
                    if waited.get(("d", k), 0) < v:
                        eng.wait_ge(dsems[k], v)

        with nc.Block() as block:
            @block.tensor
            def _(t):
                run("pe", t)

            @block.scalar
            def _(t):
                run("act", t)

            @block.vector
            def _(t):
                run("dve", t)

            @block.gpsimd
            def _(t):
                run("pool", t)

            @block.sync
            def _(t):
                run("sp", t)


def tile_rows(i):
    return 128 if i < 16 else 32


def tcols(i):
    return (128 * i, 128) if i < 16 else (2048, 32)


BLKS = [(0, 512), (512, 512), (1024, 512), (1536, 512), (2048, 32)]

PO = {}
_o = 0
for _n, _w in (("gpm", 8), ("ggo", 8), ("gmem", 8), ("gpf", 8), ("bs", 4), ("bss", 4),
               ("wdw", 66), ("bdw", 22), ("bf", 1), ("cconv", 88)):
    PO[_n] = _o
    _o += _w
PPL = _o
NPRM = PPL * L


def build():
    nc = bass.Bass("TRN2", target_bir_lowering=False)

    def din(name, shape, dt=F32):
        return nc.dram_tensor(name, list(shape), dt, kind="ExternalInput").ap()

    def dout(name, shape):
        return nc.dram_tensor(name, list(shape), F32, kind="ExternalOutput").ap()

    x_p = din("x_p", [2048, D])
    x_s = din("x_s", [32, D])
    mem = din("mem", [256, D])
    ckT = din("ckT", [L, 2, 8, 64, 1024])
    cv = din("cv", [L, 2, 1024, 512])
    clfT = din("clfT", [L, 2, 8, 1024])
    cmkT = din("cmkT", [L, 2, 4, 64, 256])
    cmv = din("cmv", [L, 2, 256, 256])
    w_in = din("w_in", [L, D, 2312])
    w_mkv = din("w_mkv", [L, D, 512])
    w_out = din("w_out", [L, D, D])
    w_up = din("w_up", [L, NJ, 128, 8, 256])
    w_dn = din("w_dn", [L, FF, D])
    g_post_mix = din("g_post_mix", [L, D])
    g_post_ffn = din("g_post_ffn", [L, D])
    g_sgu = din("g_sgu", [L, 256])
    wsT = din("wsT", [L, 4, 128, 128])
    prm_h = din("prm", [128, NPRM])
    cst_h = din("cst", [128, 288], BF16)

    y_p = dout("y_p", [2048, D])
    y_s = dout("y_s", [32, D])
    ofk = dout("ofk", [L, 2048, 512])
    ofv = dout("ofv", [L, 2048, 512])
    olf = dout("olf", [L, 8, 2048])
    omk = dout("omk", [L, 256, 256])
    omv = dout("omv", [L, 256, 256])
    oconv = dout("oconv", [L, 128, NJ, 2])
    osk = dout("osk", [L, 32, 512])
    osv = dout("osv", [L, 32, 512])
    oslf = dout("oslf", [L, 8, 32])
    ogv = dout("ogv", [L, 32, 256])
    osconv = dout("osconv", [L, 128, NJ, 2, 2])

    cscr_p = nc.dram_tensor("cscr_p", [8, 2, 2048], BF16).ap()
    cscr_s = nc.dram_tensor("cscr_s", [8, 2, 2, 1040], BF16).ap()

    with ExitStack() as st:
        S = Sched(nc)
        X = st.enter_context(nc.sbuf_tensor("X", [128, 17, D], F32))
        PRM = st.enter_context(nc.sbuf_tensor("PRM", [128, NPRM], F32))
        CST = st.enter_context(nc.sbuf_tensor("CST", [128, 288], BF16))
        WSTSt = st.enter_context(nc.sbuf_tensor("WSTS", [32, L * 4 * 32], BF16))
        PTSt = st.enter_context(nc.sbuf_tensor("PTS", [128, 7 * 32], BF16))
        SM = st.enter_context(nc.sbuf_tensor("SM", [128, 256], F32))
        ONEROW = st.enter_context(nc.sbuf_tensor("ONEROW", [128, 64], F32))
        remaining = nc.sbuf_bytes_remaining
        remaining = remaining() if callable(remaining) else remaining
        ARB = (remaining - 512) // 64 * 64
        AR = st.enter_context(nc.sbuf_tensor("AR", [128, ARB // 2], BF16))
        AR32 = AR.bitcast(F32)
        PS = [st.enter_context(nc.psum_tensor("ps%d" % k, [128, 512], F32)) for k in range(8)]
        PSB = [p.bitcast(BF16) for p in PS]

        IDENT = CST[:, 0:128]
        MASKNEG = CST[:, 128:256]
        MASKS = CST[0:32, 256:288]


        def WSTS(l, g):
            o = (l * 4 + g) * 32
            return WSTSt[0:32, o:o + 32]

        def prm(l, name, w, p0=0, p1=128, off=0):
            o = l * PPL + PO[name] + off
            return PRM[p0:p1, o:o + w]

        SM_SS, SM_SD, SM_RSTD = 0, 1, 2
        SM_R = 16
        SM_SSF = 70
        SM_T = 90

        def bf(off, n, p0=0, p1=128):
            return AR[p0:p1, off // 2: off // 2 + n]

        def f32(off, n, p0=0, p1=128):
            return AR32[p0:p1, off // 4: off // 4 + n]

        cur = [0]

        def carve(nbytes):
            o = cur[0]
            cur[0] = o + (nbytes + 63) // 64 * 64
            return o

        HT_O = carve(8 * NT * 2)
        MX_O = carve(8 * NT * 2)
        VA_O = carve(17 * 8 * 65 * 2)
        QK_O = carve(4 * NT * 2)
        WS_O = [carve(8320) for _ in range(3)]
        SCR_O = carve(8192)
        MKT_O = carve(2048)
        MVA_O = carve(1040)
        MKTt = AR[:, MKT_O // 2:MKT_O // 2 + 1024]
        MVAt = AR[:, MVA_O // 2:MVA_O // 2 + 520]
        MIX_END = cur[0]
        assert MIX_END <= ARB, (MIX_END, ARB)

        def HT(c, c0, n):
            return bf(HT_O + (c * NT + c0) * 2, n)

        def HT3(c0, n):
            return AR[:, HT_O // 2: HT_O // 2 + 8 * NT].rearrange("p (c t) -> p c t", c=8)[:, :, c0:c0 + n]

        def MX(c, c0, n, p0=0, p1=128):
            return bf(MX_O + (c * NT + c0) * 2, n, p0, p1)

        def VA(j, h, nk=128):
            return bf(VA_O + ((j * 8 + h) * 65) * 2, 65, 0, nk)

        def VA128(j, h):
            return bf(VA_O + ((j * 8 + h) * 65) * 2, 128, 0, 128)

        def QK(slot, c0, n, p0, p1):
            return bf(QK_O + (slot * NT + c0) * 2, n, p0, p1)

        CT = lambda c0, n: f32(QK_O + c0 * 4, n, 96, 104)
        HI = lambda c0, n: bf(QK_O + 8320 + c0 * 2, n, 96, 104)
        LO = lambda c0, n: bf(QK_O + 8320 + 4160 + c0 * 2, n, 96, 104)

        def WS(s, c, c0, n):
            return bf(WS_O[s] + (c * 520 + c0) * 2, n)

        def WS3(s, ncol):
            return AR[:, WS_O[s] // 2: WS_O[s] // 2 + 8 * 520].rearrange("p (c n) -> p c n", c=8)[:, :, 0:ncol]

        def SCRb(g, n, p0=0, p1=128, off=0):
            return bf(SCR_O + g * 1024 + off * 2, n, p0, p1)

        def SCRf(g, n, p0=0, p1=128, off=0):
            return f32(SCR_O + g * 1024 + off * 4, n, p0, p1)

        def scr(*gs):
            return [("scr", g) for g in gs]

        def dma(eng, out, in_, reads=(), writes=(), key=None, slow=False):
            if slow:
                fn = lambda e: e.dma_start(out=out, in_=in_, allow_slow_non_contiguous=True)
            else:
                fn = lambda e: e.dma_start(out=out, in_=in_)
            return S.op(eng, fn, reads=reads, writes=writes, dma_key=key)

        psr = {"g": [0, 1], "s": [2, 3, 4], "o": [5, 6], "t": [7]}
        psi = {k: 0 for k in psr}

        def psnext(pool):
            lst = psr[pool]
            k = lst[psi[pool] % len(lst)]
            psi[pool] += 1
            return k

        def mm_group(out_fn, items, reads, writes):
            def fn(e):
                ins = None
                for (o, a, b, s0, s1) in items:
                    ins = e.matmul(o, a, b, start=s0, stop=s1, skip_group_check=True)
                return ins
            return S.op("pe", fn, reads=reads, writes=writes)

        dma("sp", PRM[:, :], prm_h, writes=["prm"], key="prm")
        dma("sp", CST[:, :], cst_h, writes=["cst"], key="cst")
        S.op("dve", lambda e: e.memset(WSTSt[:, :], 0.0), writes=["wsts"])
        for l in range(L):
            for b in range(2):
                dma("pool",
                    WSTSt[16 * b:16 * b + 16, l * 128:(l + 1) * 128].rearrange("p (g i) -> p g i", g=4)[:, :, 16 * b:16 * b + 16],
                    wsT[l, :, 0:16, 0:16].rearrange("g j i -> j g i"),
                    reads=["wsts"], writes=[("wsts", l, b)], key=("wsts", l), slow=True)
        S.op("dve", lambda e: e.memset(PTSt[:, :], 0.0), writes=["pts%d" % q for q in range(7)])
        S.op("dve", lambda e: e.memset(SM[:, 200:201], 1.0), writes=["one"])
        S.op("dve", lambda e: e.memset(ONEROW[:, :], 1.0), writes=["onerow"])
        ONE8 = SM[96:104, 200:201]

        for i in range(16):
            dma("sp", X[:, i, :], x_p[128 * i:128 * i + 128, :], writes=[("x", i)], key=("x", i))
        dma("sp", X[0:32, 16, :], x_s, writes=[("x", 16)], key=("x", 16))

        def prenorm_tile(src, rows, xtok, gcol, dst3, dst_tok, slot, extra_writes=(), hb=None, tpool="t", hb_toks=None, defer=False):
            if hb is None:
                hb = SCRb(2 * slot, 1024, 0, rows)
            hbt = list(hb_toks) if hb_toks is not None else scr(2 * slot, 2 * slot + 1)
            c0 = 3 * slot
            ss = SM[0:rows, c0:c0 + 1]
            sd = SM[0:rows, c0 + 1:c0 + 2]
            rs = SM[0:rows, c0 + 2:c0 + 3]
            S.op("act", lambda e: e.activation(hb, src, AF.Square, accum_out=ss),
                 reads=[xtok], writes=hbt + [("sm", c0)])
            S.op("act", lambda e: e.activation(sd, ss, AF.Ln, bias=EPS, scale=1.0 / D),
                 reads=[("sm", c0)], writes=[("sm", c0 + 1)])
            S.op("act", lambda e: e.activation(rs, sd, AF.Exp, scale=-0.5), reads=[("sm", c0 + 1)], writes=[("sm", c0 + 2)])
            S.op("act", lambda e: e.activation(hb, src, AF.Copy, scale=rs),
                 reads=[xtok, ("sm", c0 + 2)], writes=hbt)
            def part_b():
                k = psnext(tpool)

                def tr(e):
                    ins = None
                    for c in range(8):
                        ins = e.transpose(PSB[k][:, c * 128:c * 128 + rows], hb[:, c * 128:(c + 1) * 128],
                                          IDENT[0:rows, 0:rows])
                    return ins
                S.op("pe", tr, reads=hbt + ["cst"], writes=[("ps", k)])
                src3 = PSB[k][:, 0:1024].rearrange("p (c t) -> p c t", c=8)[:, :, 0:rows]
                S.op("dve", lambda e: e.tensor_tensor(dst3, src3, gcol.unsqueeze(2).to_broadcast([128, 8, rows]), ALU.mult),
                     reads=[("ps", k), "prm"], writes=[dst_tok] + list(extra_writes))
            if defer:
                return part_b
            part_b()

        PVQ = []

        def flush_pv():
            while PVQ:
                PVQ.pop(0)()

        def attend(qa_fn, K, ktiles, osubs, obank, tag):
            pendq = []
            first_done = [False]

            def pv(kt_i, kt, pt_ap, pt_tok):
                items = []
                if osubs == "fm":
                    lo_, hi_ = kt["lo"], kt["hi"]
                    mrows = kt["va"].shape[1]
                    items.append((PS[obank][0:mrows, lo_:hi_], kt["va"], pt_ap[:, lo_:hi_], kt_i == 0, kt_i == len(ktiles) - 1))
                    mm_group(None, items, reads=[pt_tok] + kt["vreads"], writes=[("ps", obank)])
                    return
                for (o_ap, c0, c1, fk, lk) in osubs:
                    if fk <= kt_i <= lk and kt["pv_lo"] <= c0:
                        items.append((o_ap, pt_ap[:, c0:c1], kt["va"], not first_done[0], kt_i == lk))
                        first_done[0] = True
                if items:
                    mm_group(None, items, reads=[pt_tok] + kt["vreads"], writes=[("ps", obank)])

            for kt_i, kt in enumerate(ktiles):
                k = psnext("s")
                nk, lo, hi = kt["nk"], kt["lo"], kt["hi"]
                items = []
                if kt["mask"] is not None:
                    mask_ap, mc = kt["mask"]
                    items.append((PS[k][0:nk, lo:lo + mc], kt["ka"], qa_fn(lo, mc), True, False))
                    items.append((PS[k][0:nk, lo:lo + mc], IDENT[0:nk, 0:nk], mask_ap, False, True))
                    if hi > lo + mc:
                        items.append((PS[k][0:nk, lo + mc:hi], kt["ka"], qa_fn(lo + mc, hi - lo - mc), True, True))
                else:
                    items.append((PS[k][0:nk, lo:hi], kt["ka"], qa_fn(lo, hi - lo), True, True))
                mm_group(None, items, reads=kt["reads"] + ["cst"], writes=[("ps", k)])
                if kt["pt"] == "rot":
                    g = attend.ptc % 4
                    attend.ptc += 1
                    pt_ap = SCRb(g, 512, 0, nk)
                    pt_tok = ("scr", g)
                else:
                    pt_ap = PTSt[0:nk, 32 * kt["pt"]:32 * kt["pt"] + 32]
                    pt_tok = "pts%d" % kt["pt"]
                S.op("act", lambda e, pt_ap=pt_ap, k=k, nk=nk, lo=lo, hi=hi:
                     e.activation(pt_ap[:, lo:hi], PS[k][0:nk, lo:hi], AF.Exp),
                     reads=[("ps", k)], writes=[pt_tok])
                while len(PVQ) >= 2:
                    PVQ.pop(0)()
                PVQ.append(lambda kt_i=kt_i, kt=kt, pt_ap=pt_ap, pt_tok=pt_tok: pv(kt_i, kt, pt_ap, pt_tok))
        attend.ptc = 0

        def cs_all():
            return ["cw"]

        def post_norm_residual(ostg, rows, i, junk, junk_toks, GPap, gp_toks, o_toks):
            ssc = SM[0:rows, SM_T + 8:SM_T + 9]
            S.op("act", lambda e: e.activation(junk, ostg, AF.Square, accum_out=ssc),
                 reads=o_toks, writes=list(junk_toks) + ["pn"])
            S.op("act", lambda e: e.activation(ssc, ssc, AF.Ln, bias=EPS, scale=1.0 / D), reads=["pn"], writes=["pn"])
            S.op("act", lambda e: e.activation(ssc, ssc, AF.Exp, scale=-0.5), reads=["pn"], writes=["pn"])
            S.op("dve", lambda e: e.scalar_tensor_tensor(ostg, ostg, ssc, GPap[0:rows, :], ALU.mult, ALU.mult),
                 reads=list(o_toks) + list(gp_toks) + ["pn"], writes=o_toks)
            S.op("pool", lambda e: e.tensor_tensor(X[0:rows, i, :], X[0:rows, i, :], ostg, ALU.add),
                 reads=list(o_toks) + [("x", i)], writes=[("x", i)])

        def ffn_phase(l):
            psr.clear()
            psr.update({"u": [0, 1, 2, 4, 5, 6], "t": [3], "d": [4, 5, 6, 7]})
            for kk in psr:
                psi[kk] = 0
            HC = 1056
            o = [0]

            def cv_(nb):
                r = o[0]
                o[0] = r + (nb + 63) // 64 * 64
                return r
            H2_O = cv_(8 * HC * 2)
            ACT_O = cv_(NJ * HC * 2)
            WDN_O = cv_(NJ * 1024 * 2)
            WU_O = [cv_(4096) for _ in range(2)]
            AE_O = [cv_(2064) for _ in range(2)]
            LIN_O = [cv_(1024) for _ in range(2)]
            CONV_O = [cv_(2048) for _ in range(2)]
            OSTG_O = cv_(4096)
            GPF_O = cv_(4096)
            HBF_O = cv_(2048)
            SAVE_O = cv_(NJ * 2 * 4)
            SAVES_O = cv_(NJ * 4 * 4)
            assert o[0] <= ARB, (o[0], ARB)

            H2 = lambda c, c0, n: bf(H2_O + (c * HC + c0) * 2, n)
            H23 = lambda c0, n: AR[:, H2_O // 2:H2_O // 2 + 8 * HC].rearrange("p (c t) -> p c t", c=8)[:, :, c0:c0 + n]
            ACTB = lambda j, c0, n: bf(ACT_O + (j * HC + c0) * 2, n)
            WDN = lambda j, c0, n: bf(WDN_O + (j * 1024 + c0) * 2, n)
            WDN3 = AR[:, WDN_O // 2:WDN_O // 2 + NJ * 1024].rearrange("p (j n) -> p j n", j=NJ)
            WUB = lambda s: (WU_O[s] if s < 2 else OSTG_O)
            WU = lambda s, c, c0, n: bf(WUB(s) + (c * 256 + c0) * 2, n)
            WU3 = lambda s: AR[:, WUB(s) // 2:WUB(s) // 2 + 2048].rearrange("p (c n) -> p c n", c=8)
            wutok = lambda s, x: (("wu", s, x) if s < 2 else ("ostg%d" % x))
            OSTG = f32(OSTG_O, 1024)
            GPF = f32(GPF_O, 1024)
            HBF = bf(HBF_O, 1024)
            SAVE = f32(SAVE_O, NJ * 2)
            SAVES = f32(SAVES_O, NJ * 4)

            dma("sp", GPF, g_post_ffn[l:l + 1, :].broadcast_to([128, D]), writes=["gpf"], key="gp")
            wdn_src = w_dn[l].rearrange("(j p) n -> p j n", p=128)
            for part in range(2):
                dma("pool", WDN3[:, 11 * part:11 * part + 11, :], wdn_src[:, 11 * part:11 * part + 11, :],
                    writes=[("wdn", part)], key=("wdn", part))
            aecnt = 0

            def wu_load(j, s_):
                dma("pool", WU3(s_), w_up[l, j], writes=[wutok(s_, 0), wutok(s_, 1)], key=("wu", s_, 0))

            for half in range(2):
                tiles = list(range(0, 8)) if half == 0 else list(range(8, 17))
                base = 0 if half == 0 else 1024
                blocks = [(0, 512), (512, 512)] + ([(1024, 32)] if half == 1 else [])
                h2toks = [("h2", (i if i < 8 else i - 8)) for i in tiles]
                if half == 0:
                    HB2 = bf(AE_O[0], 1024)

                    def pa(i):
                        c0, n = tcols(i)
                        if i % 2 == 0:
                            return prenorm_tile(X[:, i, :], 128, ("x", i), prm(l, "gpf", 8), H23(c0, n), ("h2", i), 0,
                                                hb=HBF[:, :], tpool="t", hb_toks=["hbf"], defer=True)
                        return prenorm_tile(X[:, i, :], 128, ("x", i), prm(l, "gpf", 8), H23(c0, n), ("h2", i), 1,
                                            hb=HB2, tpool="t", hb_toks=[("ae", 0), ("aeh", 0), ("aet", 0)], defer=True)
                    pbq = {0: pa(0)}
                    for i in tiles:
                        if i + 1 < 8:
                            pbq[i + 1] = pa(i + 1)
                        pbq.pop(i)()
                wu_load(0, 0)
                wu_load(1, 1)
                pend_tail = None
                for j in range(NJ):
                    s = j % 3
                    if j + 2 < NJ:
                        wu_load(j + 2, (j + 2) % 3)
                    w0 = prm(l, "wdw", 1, 0, 128, 3 * j)
                    w1 = prm(l, "wdw", 1, 0, 128, 3 * j + 1)
                    w2 = prm(l, "wdw", 1, 0, 128, 3 * j + 2)
                    bd = prm(l, "bdw", 1, 0, 128, j)
                    for bidx, (lc0, n) in enumerate(blocks):
                        sample = (n == 32)
                        ka = psnext("u")
                        mm_group(None, [(PS[ka][:, 0:n], WU(s, c, 0, 128), H2(c, lc0, n), c == 0, c == 7) for c in range(8)],
                                 reads=h2toks + [wutok(s, 0)], writes=[("ps", ka)])
                        kl = psnext("u")
                        mm_group(None, [(PS[kl][:, 0:n], WU(s, c, 128, 128), H2(c, lc0, n), c == 0, c == 7) for c in range(8)],
                                 reads=h2toks + [wutok(s, 1)], writes=[("ps", kl)])
                        p = aecnt % 2
                        aecnt += 1
                        lin = bf(LIN_O[p], 512)
                        conv = f32(CONV_O[p], 512)
                        if sample:
                            AEs = f32(AE_O[p], 36).rearrange("p (b t) -> p b t", b=2)
                            S.op("dve", lambda e, AEs=AEs, j=j: e.tensor_copy(
                                AEs[:, :, 0:2], prm(l, "cconv", 4, 0, 128, 4 * j).rearrange("p (b r) -> p b r", b=2)),
                                reads=["prm"], writes=[("aeh", p)])
                            S.op("act", lambda e, AEs=AEs, ka=ka: e.copy(AEs[:, :, 2:18], PS[ka][:, 0:32].rearrange("p (b t) -> p b t", b=2)),
                                 reads=[("ps", ka)], writes=[("ae", p)])
                            taps = [AEs[:, :, k:k + 16] for k in range(3)]
                            cv3 = conv[:, 0:32].rearrange("p (b t) -> p b t", b=2)
                            psa = PS[ka][:, 0:32].rearrange("p (b t) -> p b t", b=2)
                            sil_out = f32(AE_O[p] + 256, 32)
                            sil_in = conv[:, 0:32]
                        else:
                            AEp = f32(AE_O[p], 514)
                            if bidx == 0 and half == 0:
                                S.op("dve", lambda e, AEp=AEp: e.memset(AEp[:, 0:2], 0.0), writes=[("aeh", p)])
                            elif bidx == 0:
                                S.op("dve", lambda e, AEp=AEp, j=j: e.tensor_copy(AEp[:, 0:2], SAVE[:, 2 * j:2 * j + 2]),
                                     reads=[("save", j)], writes=[("aeh", p)])
                            else:
                                prev = f32(AE_O[1 - p], 514)
                                S.op("dve", lambda e, AEp=AEp, prev=prev: e.tensor_copy(AEp[:, 0:2], prev[:, 512:514]),
                                     reads=[("aet", 1 - p)], writes=[("aeh", p)])
                            S.op("act", lambda e, AEp=AEp, ka=ka: e.copy(AEp[:, 2:514], PS[ka][:, 0:512]),
                                 reads=[("ps", ka)], writes=[("ae", p), ("aet", p)])
                            taps = [AEp[:, k:k + 512] for k in range(3)]
                            cv3 = conv
                            psa = PS[ka][:, 0:512]
                            sil_out = AEp[:, 0:512]
                            sil_in = conv
                        S.op("act", lambda e, cv3=cv3, psa=psa, w2=w2, bd=bd: e.activation(cv3, psa, AF.Identity, bias=bd, scale=w2),
                             reads=[("ps", ka), "prm"], writes=[("conv", p)])
                        S.op("act", lambda e, lin=lin, kl=kl, n=n: e.copy(lin[:, 0:n], PS[kl][:, 0:n]),
                             reads=[("ps", kl)], writes=[("lin", p)])
                        S.op("dve", lambda e, cv3=cv3, taps=taps, w1=w1: e.scalar_tensor_tensor(cv3, taps[1], w1, cv3, ALU.mult, ALU.add),
                             reads=[("ae", p), ("aeh", p), ("conv", p)], writes=[("conv", p)])
                        S.op("dve", lambda e, cv3=cv3, taps=taps, w0=w0: e.scalar_tensor_tensor(cv3, taps[0], w0, cv3, ALU.mult, ALU.add),
                             reads=[("ae", p), ("aeh", p), ("conv", p)], writes=[("conv", p)])
                        if not sample and bidx == 1:
                            S.op("dve", lambda e, AEp=AEp, j=j: e.tensor_copy(SAVE[:, 2 * j:2 * j + 2], AEp[:, 512:514]),
                                 reads=[("aet", p)], writes=[("save", j)])
                        if sample:
                            S.op("dve", lambda e, AEs=AEs, j=j: e.tensor_copy(
                                SAVES[:, 4 * j:4 * j + 4].rearrange("p (b r) -> p b r", b=2), AEs[:, :, 16:18]),
                                reads=[("ae", p)], writes=[("saves", j)])

                        def tail(p=p, sil_out=sil_out, sil_in=sil_in, j=j, lc0=lc0, n=n, lin=lin, half=half):
                            S.op("act", lambda e: e.activation(sil_out, sil_in, AF.Silu),
                                 reads=[("conv", p), ("ae", p), ("aeh", p)], writes=[("ae", p), ("aeh", p)])
                            S.op("pool", lambda e: e.tensor_tensor(ACTB(j, lc0, n), sil_out, lin[:, 0:n], ALU.mult),
                                 reads=[("ae", p), ("aeh", p), ("lin", p)], writes=[("actb", j, half)])
                        if pend_tail is not None:
                            pend_tail()
                        pend_tail = tail
                if pend_tail is not None:
                    pend_tail()
                    pend_tail = None
                if half == 1:
                    dma("sp", oconv[l], SAVE.rearrange("p (j r) -> p j r", r=2), reads=[("save", j) for j in range(NJ)], key="oconv")
                    dma("sp", osconv[l], SAVES.rearrange("p (j b r) -> p j b r", b=2, r=2), reads=[("saves", j) for j in range(NJ)], key="osconv")
                atoks = [("actb", j, half) for j in range(NJ)]
                nxt = list(range(8, 17)) if half == 0 else []
                for ti_, i in enumerate(tiles):
                    rows = tile_rows(i)
                    c0, n = tcols(i)
                    lc0 = c0 - base
                    defer_b = []
                    for i2 in (nxt[ti_:ti_ + 1] if ti_ < 7 else nxt[7:8]):
                        rows2 = tile_rows(i2)
                        c02, n2 = tcols(i2)
                        defer_b.append(prenorm_tile(X[0:rows2, i2, :], rows2, ("x", i2), prm(l, "gpf", 8), H23(c02 - 1024, n2),
                                                    ("h2", i2 - 8), 0, hb=HBF[0:rows2, :], tpool="t", hb_toks=["hbf"], defer=True))
                        if "nodefer" in DBG:
                            defer_b.pop()()
                    for hf in range(2):
                        kd = psnext("d")
                        mm_group(None, [(PS[kd][0:rows, :], ACTB(j, lc0, n), WDN(j, 512 * hf, 512), j == 0, j == NJ - 1) for j in range(NJ)],
                                 reads=atoks + [("wdn", 0), ("wdn", 1)], writes=[("ps", kd)])
                        if hf == 0:
                            S.op("act", lambda e, kd=kd, rows=rows: e.copy(OSTG[0:rows, 0:512], PS[kd][0:rows, :]),
                                 reads=[("ps", kd)], writes=["ostg0"])
                        else:
                            S.op("dve", lambda e, kd=kd, rows=rows: e.tensor_copy(OSTG[0:rows, 512:1024], PS[kd][0:rows, :]),
                                 reads=[("ps", kd)], writes=["ostg1"])
                    for fb in defer_b:
                        fb()
                    post_norm_residual(OSTG[0:rows, :], rows, i, HBF[0:rows, :], ["hbf"], GPF, ["gpf"], ["ostg0", "ostg1"])
                    if l == L - 1:
                        if i < 16:
                            dma("sp", y_p[128 * i:128 * i + 128, :], X[:, i, :], reads=[("x", i)], key=("x", i))
                        else:
                            dma("sp", y_s, X[0:32, 16, :], reads=[("x", 16)], key=("x", 16))
                if half == 0:
                    prenorm_tile(X[0:32, 16, :], 32, ("x", 16), prm(l, "gpf", 8), H23(1024, 32), ("h2", 8), 0,
                                 hb=HBF[0:32, :], tpool="t", hb_toks=["hbf"])
        RFOX = lambda i, rows=128: SM[0:rows, 100 + i:101 + i]
        RSGU = lambda i, rows=128: SM[0:rows, 120 + i:121 + i]
        RMEM = lambda i, rows=128: SM[0:rows, 140 + i:141 + i]
        SSFOX = lambda i, rows=128: SM[0:rows, 160 + i:161 + i]
        SSMEM = lambda i, rows=128: SM[0:rows, 180 + i:181 + i]
        ALLHT = [("ht", i) for i in range(17)]

        class NSQ:
            def __init__(self):
                self.p1 = None
                self.p2 = None
                self.p3 = None

            def step(self, new_p1):
                if self.p3 is not None:
                    self.p3()
                    self.p3 = None
                if self.p2 is not None:
                    self.p3 = self.p2()
                    self.p2 = None
                if self.p1 is not None:
                    self.p2 = self.p1()
                self.p1 = new_p1

            def flush(self):
                self.step(None)
                self.step(None)
                self.step(None)
        nsq = NSQ()

        def wload(slot, src2d, ncol, extra_writes=()):
            dma("pool", WS3(slot, ncol), src2d.rearrange("(c p) n -> p c n", p=128),
                writes=[("ws", slot)] + list(extra_writes), key=("ws", slot))

        def KAS(b, c0, n, p0=0, p1=67):
            return bf(WS_O[0] + b * 3200 + c0 * 2, n, p0, p1)

        def VAS(b, j):
            return bf(WS_O[0] + b * 3200 + 2080 + j * 130, 65)

        def VAS3(b):
            return AR[:, (WS_O[0] + b * 3200 + 2080) // 2:(WS_O[0] + b * 3200 + 2080) // 2 + 520].rearrange("p (j k) -> p j k", k=65)

        SC_TOKS = [("sc", b, x) for b in range(2) for x in ("k", "v", "r", "one")]

        def norm_store_fm(obank, I, grp, chunk, p0, ssq4, l):
            rlrow = f32(SCR_O + 6 * 1024, 512, 64, 65)
            hfT = f32(SCR_O + 6 * 1024, 512, 0, 64)
            S.op("act", lambda e: e.activation(rlrow, PS[obank][64:65, 0:512], AF.Ln), reads=[("ps", obank)], writes=[("scr", 6, "r")])
            S.op("act", lambda e: e.activation(rlrow, rlrow, AF.Exp, scale=-1.0), reads=[("scr", 6, "r")], writes=[("scr", 6, "r")])
            kb = psnext("g")
            S.op("pe", lambda e: e.matmul(PS[kb][0:64, 0:512], ONEROW[64:65, 0:64], rlrow, start=True, stop=True, skip_group_check=True),
                 reads=[("scr", 6, "r"), "onerow"], writes=[("ps", kb)])
            S.op("dve", lambda e: e.tensor_copy(hfT, PS[kb][0:64, 0:512]), reads=[("ps", kb)], writes=scr(6, 7))
            S.op("dve", lambda e: e.tensor_tensor(hfT, PS[obank][0:64, 0:512], hfT, ALU.mult), reads=[("ps", obank)] + scr(6, 7), writes=scr(6, 7))
            return lambda: norm_store_fm2(I, grp, chunk, p0, ssq4, l)

        def norm_store_fm2(I, grp, chunk, p0, ssq4, l):
            hfT = f32(SCR_O + 6 * 1024, 512, 0, 64)
            sqb = bf(SCR_O + 4 * 1024, 512, 0, 64)
            S.op("act", lambda e: e.activation(MX(chunk, 512 * I, 512, p0, p0 + 64), hfT, AF.Copy,
                                               scale=prm(l, "ggo", 1, p0, p0 + 64, chunk)),
                 reads=scr(6, 7) + ["prm"], writes=[("mx", chunk, 4 * I + s_, p0) for s_ in range(4)])
            S.op("pool", lambda e: e.tensor_tensor(sqb, hfT, hfT, ALU.mult), reads=scr(6, 7), writes=scr(4))
            return lambda: norm_store_fm3(I, grp, ssq4)

        def norm_store_fm3(I, grp, ssq4):
            sqb = bf(SCR_O + 4 * 1024, 512, 0, 64)
            kt = psnext("t")
            ones_col = bf(VA_O + 64 * 2, 1, 0, 64)

            def ssmm(e):
                ins = None
                for s_ in range(4):
                    ins = e.matmul(PS[kt][:, s_:s_ + 1], sqb[:, 128 * s_:128 * s_ + 128], ones_col, start=True, stop=True,
                                   skip_group_check=True)
                return ins
            S.op("pe", ssmm, reads=scr(4) + ["va_ones"], writes=[("ps", kt)])
            toks4 = [("ssq", grp, 4 * I + s_) for s_ in range(4)]
            S.op("dve", lambda e: e.tensor_tensor(ssq4, ssq4, PS[kt][:, 0:4], ALU.add), reads=[("ps", kt)] + toks4, writes=toks4)

        def norm_and_store_block(obank, I, grp, chunk, p0, ssq4, l):
            O4 = PS[obank][:, 0:260].rearrange("p (s k) -> p s k", k=65)
            rl4 = SM[:, SM_T + 10:SM_T + 14]
            sq4 = SM[:, SM_T + 14:SM_T + 18]
            S.op("dve", lambda e: e.reciprocal(rl4.unsqueeze(2), O4[:, :, 64:65]), reads=[("ps", obank)], writes=["rl4"])
            hf4 = SCRf(6, 256)
            hb4 = SCRb(7, 256)
            S.op("dve", lambda e: e.tensor_tensor(hf4.rearrange("p (s k) -> p s k", k=64), O4[:, :, 0:64],
                                                  rl4.unsqueeze(2).to_broadcast([128, 4, 64]), ALU.mult),
                 reads=[("ps", obank), "rl4"], writes=scr(6))

            def sqs(e):
                ins = None
                for s_ in range(4):
                    ins = e.activation(hb4[:, 64 * s_:64 * s_ + 64], hf4[:, 64 * s_:64 * s_ + 64], AF.Square,
                                       accum_out=sq4[:, s_:s_ + 1])
                return ins
            S.op("act", sqs, reads=scr(6), writes=scr(7) + ["sq4"])
            toks4 = [("ssq", grp, 4 * I + s_) for s_ in range(4)]
            S.op("dve", lambda e: e.tensor_tensor(ssq4, ssq4, sq4, ALU.add), reads=["sq4"] + toks4, writes=toks4)
            S.op("act", lambda e: e.copy(hb4, hf4), reads=scr(6, 7), writes=scr(7))
            kt = psnext("t")

            def tr(e):
                ins = None
                for s_ in range(4):
                    ins = e.transpose(PSB[kt][p0:p0 + 64, 128 * s_:128 * s_ + 128], hb4[:, 64 * s_:64 * s_ + 64], IDENT[:, :])
                return ins
            S.op("pe", tr, reads=scr(7) + ["cst"], writes=[("ps", kt)])
            S.op("dve", lambda e: e.tensor_scalar(MX(chunk, 512 * I, 512, p0, p0 + 64), PSB[kt][p0:p0 + 64, 0:512],
                                                  prm(l, "ggo", 1, p0, p0 + 64, chunk), None, ALU.mult),
                 reads=[("ps", kt), "prm"], writes=[("mx", chunk, 4 * I + s_, p0) for s_ in range(4)])

        def norm_and_store(obank, ocol, rows, tile_i, grp, chunk, p0, ssq_acc, l):
            rl = SM[0:rows, SM_T + 4:SM_T + 5]
            S.op("dve", lambda e: e.reciprocal(rl, PS[obank][0:rows, ocol + 64:ocol + 65]),
                 reads=[("ps", obank)], writes=["rl"])
            hf = SCRf(5, 64, 0, rows)
            hb = SCRb(5, 64, 0, rows, off=256)
            S.op("dve", lambda e: e.tensor_scalar(hf, PS[obank][0:rows, ocol:ocol + 64], rl, None, ALU.mult),
                 reads=[("ps", obank), "rl"], writes=scr(5))
            sq = SM[0:rows, SM_T + 5:SM_T + 6]
            S.op("act", lambda e: e.activation(hb, hf, AF.Square, accum_out=sq), reads=scr(5), writes=scr(5) + ["sq"])
            S.op("dve", lambda e: e.tensor_tensor(ssq_acc, ssq_acc, sq, ALU.add),
                 reads=["sq", ("ssq", grp, tile_i)], writes=[("ssq", grp, tile_i)])
            S.op("act", lambda e: e.copy(hb, hf), reads=scr(5), writes=scr(5))
            kt = psnext("t")
            c0, n = tcols(tile_i)
            S.op("pe", lambda e: e.transpose(PSB[kt][p0:p0 + 64, 0:rows], hb, IDENT[0:rows, 0:rows]),
                 reads=scr(5) + ["cst"], writes=[("ps", kt)])
            S.op("dve", lambda e: e.tensor_scalar(MX(chunk, c0, n, p0, p0 + 64), PSB[kt][p0:p0 + 64, 0:rows],
                                                  prm(l, "ggo", 1, p0, p0 + 64, chunk), None, ALU.mult),
                 reads=[("ps", kt), "prm"], writes=[("mx", chunk, tile_i, p0)])

        try:
            for l in range(L):
                psr.clear()
                psr.update({"g": [0, 1], "s": [2, 3, 4], "o": [5, 6], "t": [7], "w": [0, 1, 2, 3, 4, 5]})
                for kk in psr:
                    psi[kk] = 0
                S.op("dve", lambda e: e.memset(
                    AR[:, VA_O // 2: VA_O // 2 + 17 * 8 * 65].rearrange("p (a k) -> p a k", k=65)[:, :, 64:65], 1.0),
                    writes=["va_ones"])
                S.op("dve", lambda e: e.memset(MVAt[:, :].rearrange("p (a k) -> p a k", k=65)[:, :, 64:65], 1.0),
                     writes=["mva_ones"])
                S.op("dve", lambda e: e.memset(MKTt[64:128, :], 0.0), writes=["mkt_zero"])
                wload(0, w_in[l][:, 1024:1536], 512, extra_writes=SC_TOKS if l > 0 else ())
                wload(1, w_in[l][:, 512:1024], 512)
                wload(2, w_in[l][:, 1536:2056], 520)

                SGC = MX_O + 12288
                WSTl = bf(SGC, 512)
                GSl = f32(SGC + 1024, 256)
                dma("pool", WSTl.rearrange("p (g i) -> p g i", g=4), wsT[l].rearrange("g j i -> j g i"),
                    writes=[("sguc", 0)], key="wst")
                S.op("dve", lambda e: e.memset(bf(SGC, 512, 64, 128).rearrange("p (g i) -> p g i", g=4)[:, :, 0:64], 0.0),
                     reads=[("sguc", 0)], writes=[("sguc", 0)])
                dma("sp", GSl, g_sgu[l:l + 1, :].broadcast_to([128, 256]), writes=[("sguc", 1)], key="gs")

                def sgu_ops(i):
                    st_ = 0 if "setA" in DBG else i % 2
                    base = MX_O if st_ == 0 else MX_O + 6144
                    tk = "sguA" if st_ == 0 else "sguB"
                    Sb = lambda g, n, p0=0, p1=128: bf(base + g * 1024, n, p0, p1)
                    Sf = lambda g, n, p0=0, p1=128: f32(base + g * 1024, n, p0, p1)
                    sc = lambda *gs: [(tk, g) for g in gs]
                    rows = tile_rows(i)
                    c0, n = tcols(i)
                    s1, s2 = [], []
                    k = psnext("s")
                    z = Sf(0, 512, 0, rows)
                    t1 = Sf(2, 512, 0, rows)
                    ssv = SM[0:rows, SM_T + 20 + 2 * st_:SM_T + 21 + 2 * st_]
                    sss = SM[0:rows, SM_T + 21 + 2 * st_:SM_T + 22 + 2 * st_]
                    ssvt, ssst = ("ssv", st_), ("sss", st_)
                    vvf = Sf(3, 256, 0, rows)
                    vvb = Sb(2, 256, 0, rows)
                    sg = Sf(4, 256, 0, rows)
                    sgb = Sb(5, 256, 0, rows)
                    s1.append(lambda: mm_group(None, [(PS[k][0:rows, :], HT(c, c0, n), WS(2, c, 8, 512), c == 0, c == 7) for c in range(8)],
                                               reads=[("ht", i), ("ws", 2)], writes=[("ps", k)]))
                    s1.append(lambda: S.op("dve", lambda e: e.tensor_copy(z, PS[k][0:rows, :]), reads=[("ps", k)], writes=sc(0, 1)))
                    s1.append(lambda: S.op("dve", lambda e: e.tensor_tensor(t1, z, z, ALU.mult), reads=sc(0, 1), writes=sc(2, 3)))
                    s1.append(lambda: S.op("dve", lambda e: e.tensor_scalar(t1, t1, 0.044715, 1.0, ALU.mult, ALU.add), reads=sc(2, 3), writes=sc(2, 3)))
                    s1.append(lambda: S.op("dve", lambda e: e.tensor_tensor(t1, t1, z, ALU.mult), reads=sc(0, 1, 2, 3), writes=sc(2, 3)))
                    s1.append(lambda: S.op("act", lambda e: e.activation(t1, t1, AF.Exp, scale=-1.5957691216057308), reads=sc(2, 3), writes=sc(2, 3)))
                    s1.append(lambda: S.op("act", lambda e: e.activation(t1, t1, AF.Ln, bias=1.0, scale=1.0), reads=sc(2, 3), writes=sc(2, 3)))
                    s1.append(lambda: S.op("act", lambda e: e.activation(t1, t1, AF.Exp, scale=-1.0), reads=sc(2, 3), writes=sc(2, 3)))
                    s1.append(lambda: S.op("dve", lambda e: e.tensor_tensor(z, z, t1, ALU.mult), reads=sc(0, 1, 2, 3), writes=sc(0, 1)))
                    s1.append(lambda: S.op("act", lambda e: e.activation(t1[:, 0:256], z[:, 256:512], AF.Square, accum_out=ssv),
                                           reads=sc(0, 1), writes=sc(2) + [ssvt]))
                    s1.append(lambda: S.op("act", lambda e: e.activation(ssv, ssv, AF.Ln, bias=EPS, scale=1.0 / 256), reads=[ssvt], writes=[ssvt]))
                    s1.append(lambda: S.op("act", lambda e: e.activation(ssv, ssv, AF.Exp, scale=-0.5), reads=[ssvt], writes=[ssvt]))
                    s1.append(lambda: S.op("dve", lambda e: e.scalar_tensor_tensor(vvf, z[:, 256:512], ssv, GSl[0:rows, :], ALU.mult, ALU.mult),
                                           reads=sc(0, 1) + [("sguc", 1)] + [ssvt], writes=sc(3)))
                    s1.append(lambda: S.op("dve", lambda e: e.tensor_copy(vvb, vvf), reads=sc(3), writes=sc(2)))
                    if i == 16:
                        s1.append(lambda: dma("sp", ogv[l], vvf, reads=sc(3), key="ogv"))
                    k2 = psnext("s")
                    if i < 16:
                        items = [(PS[k2][0:128, 64 * g:64 * g + 64], WSTl[:, 128 * g:128 * g + 128], vvb[:, 64 * g:64 * g + 64], True, True) for g in range(4)]
                        bsap = prm(l, "bs", 4)
                        wtok = [("sguc", 0)]
                    else:
                        items = [(PS[k2][0:32, 64 * g:64 * g + 64], WSTS(l, g), vvb[:, 64 * g:64 * g + 64], True, True) for g in range(4)]
                        bsap = prm(l, "bss", 4, 0, 32)
                        wtok = ["wsts"] + [("wsts", l, b) for b in range(2)]
                    s1.append(lambda: mm_group(None, items, reads=sc(2) + wtok, writes=[("ps", k2)]))
                    s2.append(lambda: S.op("dve", lambda e: e.tensor_tensor(
                        sg.rearrange("p (g d) -> p g d", g=4), PS[k2][0:rows, 0:256].rearrange("p (g d) -> p g d", g=4),
                        bsap.unsqueeze(2).to_broadcast([rows, 4, 64]), ALU.add),
                        reads=[("ps", k2), "prm"], writes=sc(4)))
                    s2.append(lambda: S.op("dve", lambda e: e.tensor_tensor(sg, sg, z[:, 0:256], ALU.mult), reads=sc(0, 1, 4), writes=sc(4)))
                    s2.append(lambda: S.op("act", lambda e: e.activation(sgb, sg, AF.Square, accum_out=sss),
                                           reads=sc(4), writes=sc(5) + [ssst]))
                    s2.append(lambda: S.op("act", lambda e: e.activation(sss, sss, AF.Ln, bias=EPS, scale=1.0 / 256), reads=[ssst], writes=[ssst]))
                    s2.append(lambda: S.op("act", lambda e: e.activation(RSGU(i, rows), sss, AF.Exp, scale=-0.5),
                                           reads=[ssst], writes=[("rsgu", i)]))
                    s2.append(lambda: S.op("dve", lambda e: e.tensor_copy(sgb, sg), reads=sc(4, 5), writes=sc(5)))

                    def trs():
                        kt = psnext("o")

                        def tr(e):
                            ins = None
                            for cc in range(2):
                                ins = e.transpose(PSB[kt][:, cc * 128:cc * 128 + rows], sgb[:, cc * 128:(cc + 1) * 128], IDENT[0:rows, 0:rows])
                            return ins
                        S.op("pe", tr, reads=sc(5) + ["cst"], writes=[("ps", kt)])
                        for cc in range(2):
                            S.op("dve", lambda e, cc=cc, l=l: e.tensor_scalar(
                                MX(4 + cc, c0, n), PSB[kt][:, cc * 128:cc * 128 + rows], prm(l, "ggo", 1, 0, 128, 4 + cc), None, ALU.mult),
                                reads=[("ps", kt), "prm"], writes=[("mx", 4 + cc, i)])
                    s2.append(trs)
                    return s1, s2

                def interleave(a, b):
                    for q in range(max(len(a), len(b))):
                        if q < len(a):
                            a[q]()
                        if q < len(b):
                            b[q]()

                def vk_tile(i):
                    rows = tile_rows(i)
                    c0, n = tcols(i)
                    for which, slot, oprompt, osample, sg in (("v", 0, ofv, osv, 4), ("k", 1, ofk, osk, 6)):
                        k = psnext("g")
                        mm_group(None, [(PS[k][0:rows, :], HT(c, c0, n), WS(slot, c, 0, 512), c == 0, c == 7)
                                        for c in range(8)],
                                 reads=[("ht", i), ("ws", slot)], writes=[("ps", k)])
                        stg = SCRf(sg, 512, 0, rows)
                        S.op(("act" if which == "v" else "dve"), (lambda e, stg=stg, k=k, rows=rows: e.copy(stg, PS[k][0:rows, :])) if which == "v"
                             else (lambda e, stg=stg, k=k, rows=rows: e.tensor_copy(stg, PS[k][0:rows, :])),
                             reads=[("ps", k)], writes=scr(sg, sg + 1))
                        if which == "v":
                            va3 = AR[0:rows, VA_O // 2 + i * 520: VA_O // 2 + (i + 1) * 520].rearrange(
                                "p (h k) -> p h k", k=65)[:, :, 0:64]
                            S.op("dve", lambda e, va3=va3, k=k, rows=rows: e.tensor_copy(
                                va3, PS[k][0:rows, :].rearrange("p (h k) -> p h k", k=64)),
                                reads=[("ps", k), "va_ones"], writes=[("va", i)])
                        dst = oprompt[l, 128 * i:128 * i + 128, :] if i < 16 else osample[l, :, :]
                        dma("sp", dst, stg, reads=scr(sg, sg + 1), key=("stg", sg))

                sgu_pipe = {"B": [], "C": []}

                def sgu_step(t):
                    if t is not None:
                        s1_, s2_ = sgu_ops(t)
                        sA, sB, sC = s1_[:-1], [s1_[-1]] + s2_[:-1], [s2_[-1]]
                    else:
                        sA, sB, sC = [], [], []
                    interleave(sA, sgu_pipe["B"])
                    for f_ in sgu_pipe["C"]:
                        f_()
                    sgu_pipe["C"] = sgu_pipe["B_c"] if "B_c" in sgu_pipe else []
                    sgu_pipe["B"] = sB
                    sgu_pipe["B_c"] = sC

                def pn_a(i):
                    rows = tile_rows(i)
                    c0, n = tcols(i)
                    return prenorm_tile(X[0:rows, i, :], rows, ("x", i), prm(l, "gpm", 8), HT3(c0, n), ("ht", i), i % 2, defer=True)
                pn_b = {0: pn_a(0)}
                for i in range(17):
                    if i + 1 < 17:
                        pn_b[i + 1] = pn_a(i + 1)
                    pn_b.pop(i)()
                    if i >= 1:
                        vk_tile(i - 1)
                        sgu_step(i - 1)
                vk_tile(16)
                sgu_step(16)
                sgu_step(None)
                sgu_step(None)
                S.op("dve", lambda e: e.memset(bf(MX_O + 15 * 1024, 2), 0.0),
                     writes=[("sguA", g) for g in range(6)] + [("sguB", g) for g in range(6)] + [("sguc", 0), ("sguc", 1)]
                     + [("mx", c, i, p0) for c in (0, 1, 2, 3) for i in range(17) for p0 in (0, 64)])

                _chk("vk%d" % l)
                NB = SM[0:8, 210:211]
                S.op("dve", lambda e, l=l: e.tensor_scalar(NB, prm(l, "bf", 1, 0, 8), -1.0, None, ALU.mult),
                     reads=["prm"], writes=["cw", "nb"])
                for bi, (c0, n) in enumerate(BLKS):
                    k = psnext("g")
                    mm_group(None, [(PS[k][0:8, 0:n], WS(2, c, 0, 8), HT(c, c0, n), c == 0, c == 7) for c in range(8)],
                             reads=ALLHT + [("ws", 2)], writes=["cw", ("ps", k)])
                    tmp = SCRf(0, 512, 96, 104)
                    S.op("act", lambda e, k=k, n=n, tmp=tmp: e.activation(tmp[:, 0:n], PS[k][0:8, 0:n], AF.Exp, bias=NB, scale=-1.0),
                         reads=[("ps", k), "nb"], writes=scr(0, 1))
                    S.op("act", lambda e, n=n, tmp=tmp: e.activation(tmp[:, 0:n], tmp[:, 0:n], AF.Ln, bias=1.0, scale=1.0),
                         reads=scr(0, 1), writes=scr(0, 1))
                    S.op("dve", lambda e, c0=c0, n=n, tmp=tmp: e.tensor_scalar(CT(c0, n), tmp[:, 0:n], -1.0, None, ALU.mult),
                         reads=scr(0, 1), writes=["cw", ("ct", bi)])
                dma("sp", olf[l], CT(0, 2048), reads=[("ct", b) for b in range(4)], writes=["cw"], key="olf")
                dma("sp", oslf[l], CT(2048, 32), reads=[("ct", 4)], writes=["cw"], key="olf2")
                SLF = SM[96:104, 220:252]
                S.op("dve", lambda e: e.tensor_copy(SLF, CT(2048, 32)), reads=[("ct", 4)], writes=["cw", "slf"])
                S.op("dve", lambda e: e.tensor_tensor_scan(CT(0, 2048), ONE8.to_broadcast([8, 2048]), CT(0, 2048), 0.0, ALU.mult, ALU.add),
                     reads=[("ct", b) for b in range(4)] + ["one"], writes=["cw", "ctp"])
                S.op("dve", lambda e: e.tensor_copy(HI(0, 2048), CT(0, 2048)), reads=["ctp"], writes=["cw", "hi"])
                S.op("dve", lambda e: e.tensor_tensor(CT(0, 2048), CT(0, 2048), HI(0, 2048), ALU.subtract),
                     reads=["ctp", "hi"], writes=["cw", "ctp"])
                S.op("dve", lambda e: e.tensor_copy(LO(0, 2048), CT(0, 2048)), reads=["ctp"], writes=["cw", "lo"])
                dma("sp", cscr_p[:, 0, :], HI(0, 2048), reads=["hi"], writes=["cw", "cscr_p"], key="cscr0")
                dma("sp", cscr_p[:, 1, :], LO(0, 2048), reads=["lo"], writes=["cw", "cscr_p2"], key="cscr1")
                CS = lambda b, c0, n: CT(b * 1040 + c0, n)
                for b in range(2):
                    dma("sp", CS(b, 0, 1024), clfT[l, b], reads=["ctp", "lo", ("ct", 4), "slf"], writes=["cw", ("cs", b)], key=("csl", b))
                    S.op("dve", lambda e, b=b: e.tensor_copy(CS(b, 1024, 16), SM[96:104, 220 + 16 * b:236 + 16 * b]),
                         reads=["slf", "ctp", "lo", ("ct", 4)], writes=["cw", ("cs2", b)])
                    S.op("dve", lambda e, b=b: e.tensor_tensor_scan(CS(b, 0, 1040), ONE8.to_broadcast([8, 1040]), CS(b, 0, 1040), 0.0, ALU.mult, ALU.add),
                         reads=[("cs", b), ("cs2", b), "one"], writes=["cw", ("csc", b)])
                S.op("dve", lambda e: e.tensor_copy(HI(0, 2080), CT(0, 2080)),
                     reads=[("csc", 0), ("csc", 1), "cscr_p", "cscr_p2"], writes=["cw", "hi"])
                S.op("dve", lambda e: e.tensor_tensor(CT(0, 2080), CT(0, 2080), HI(0, 2080), ALU.subtract),
                     reads=["hi"], writes=["cw", ("csc", 0), ("csc", 1)])
                S.op("dve", lambda e: e.tensor_copy(LO(0, 2080), CT(0, 2080)), reads=[("csc", 0), ("csc", 1), "cscr_p2"], writes=["cw", "lo"])
                for b in range(2):
                    dma("sp", cscr_s[:, 0, b, :], HI(b * 1040, 1040), reads=["hi"], writes=["cw", ("cscr_s", b, 0)], key=("cscrs", b, 0))
                    dma("sp", cscr_s[:, 1, b, :], LO(b * 1040, 1040), reads=["lo"], writes=["cw", ("cscr_s", b, 1)], key=("cscrs", b, 1))

                _chk("fg%d" % l)
                _chk("sgu%d" % l)
                S.op("dve", lambda e: e.memset(bf(QK_O, 4 * NT, 64, 128), 0.0),
                     writes=["cw", "hi", "lo"] + [("qkc", s_) for s_ in range(4)])
                for s_ in (0, 2):
                    S.op("dve", lambda e, s_=s_: e.memset(QK(s_, 0, NT, 64, 67), -1.0), writes=[("qkc", s_)])
                for s_ in (1, 3):
                    S.op("dve", lambda e, s_=s_: e.memset(QK(s_, 0, NT, 64, 65), 1.0), writes=[("qkc", s_)])
                wload(0, w_mkv[l], 512)
                MEMT3 = AR[:, (SCR_O + 4096) // 2:(SCR_O + 4096) // 2 + 2048].rearrange("p (c t) -> p c t", c=8)
                MEMT = lambda c, c0, n: bf(SCR_O + 4096 + (c * 256 + c0) * 2, n)
                memx = f32(WS_O[2], 1024)
                for mt in range(2):
                    dma("sp", memx, mem[128 * mt:128 * mt + 128, :], writes=[("ws", 2)], key="memx")
                    prenorm_tile(memx, 128, ("ws", 2), prm(l, "gmem", 8), MEMT3[:, :, 128 * mt:128 * mt + 128],
                                 ("memt", mt), 0, extra_writes=scr(4, 5, 6, 7))
                memt_toks = [("memt", 0), ("memt", 1)] + scr(4, 5, 6, 7)
                stgk = f32(WS_O[2], 512)
                for mt in range(2):
                    k = psnext("g")
                    mm_group(None, [(PS[k][:, :], MEMT(c, 128 * mt, 128), WS(0, c, 0, 512), c == 0, c == 7) for c in range(8)],
                             reads=memt_toks + [("ws", 0)], writes=[("ps", k)])
                    S.op("act", lambda e, k=k: e.copy(stgk, PS[k][:, :]), reads=[("ps", k)], writes=[("ws", 2)])
                    S.op("dve", lambda e, k=k, mt=mt: e.tensor_copy(
                        MVAt[:, mt * 260:(mt + 1) * 260].rearrange("p (h k) -> p h k", k=65)[:, :, 0:64],
                        PS[k][:, 256:512].rearrange("p (h k) -> p h k", k=64)),
                        reads=[("ps", k), "mva_ones"], writes=[("mva", mt)])
                    dma("sp", omk[l, 128 * mt:128 * mt + 128, :], stgk[:, 0:256], reads=[("ws", 2)], key="omk")
                    dma("sp", omv[l, 128 * mt:128 * mt + 128, :], stgk[:, 256:512], reads=[("ws", 2)], key="omv")
                for pr in range(2):
                    k = psnext("g")
                    mm_group(None, [(PS[k][:, 0:256], WS(0, c, 128 * pr, 128), MEMT(c, 0, 256), c == 0, c == 7) for c in range(8)],
                             reads=memt_toks + [("ws", 0)], writes=[("ps", k)])
                    for hh in range(2):
                        h = 2 * pr + hh
                        S.op("act", lambda e, k=k, hh=hh, h=h: e.copy(MKTt[0:64, h * 256:(h + 1) * 256], PS[k][64 * hh:64 * hh + 64, 0:256]),
                             reads=[("ps", k)], writes=[("mkt", h)])
                wload(2, w_in[l][:, 2056:2312], 256)
                for b in range(2):
                    S.op("dve", lambda e, b=b: e.memset(KAS(b, 0, 1040, 64, 65), 1.0), reads=[("ws", 0)], writes=[("ws", 0), ("sc", b, "one")])
                    S.op("dve", lambda e, b=b: e.memset(VAS3(b)[:, :, 64:65], 1.0), reads=[("ws", 0)], writes=[("ws", 0), ("sc", b, "one")])

                S.op("dve", lambda e: e.memset(SM[:, 160:200], 0.0),
                     writes=[("ssq", g, i) for g in (0, 2) for i in range(17)])

                _chk("memkv%d" % l)
                for pr in range(2):
                    for bi, (c0, n) in enumerate(BLKS):
                        k = psnext("g")
                        mm_group(None, [(PS[k][:, 0:n], WS(2, c, 128 * pr, 128), HT(c, c0, n), c == 0, c == 7) for c in range(8)],
                                 reads=ALLHT + [("ws", 2)], writes=[("ps", k)])
                        for hh in range(2):
                            S.op("act", lambda e, k=k, hh=hh, c0=c0, n=n: e.activation(
                                QK(2 * hh, c0, n, 0, 64), PS[k][64 * hh:64 * hh + 64, 0:n], AF.Copy, scale=0.125),
                                reads=[("ps", k)], writes=[("qa", hh, bi)])
                    for hh in range(2):
                        h = 2 * pr + hh
                        for I in range(4):
                            ob = psnext("o")
                            kts = [dict(ka=MKTt[0:128, h * 256 + 128 * j:h * 256 + 128 * j + 128], nk=128,
                                        va=bf(MVA_O + ((j * 4 + h) * 65) * 2, 128), lo=0, hi=512, pv_lo=0,
                                        mask=None, pt="rot", reads=[("mkt", h), ("qa", hh, I), "mkt_zero", ("qkc", 0), ("qkc", 2)],
                                        vreads=[("mva", j), "mva_ones"])
                                   for j in range(2)]
                            osubs = [(PS[ob][:, 65 * s:65 * s + 65], 128 * s, 128 * s + 128, 0, 1) for s in range(4)]
                            attend(lambda c0, n, hh=hh, I=I: QK(2 * hh, 512 * I + c0, n, 0, 128), 128, kts, "fm", ob, "mem")
                            nsq.step(lambda ob=ob, I=I, pr=pr, hh=hh: norm_store_fm(ob, I, 2, 6 + pr, 64 * hh, SM[:, 180 + 4 * I:184 + 4 * I], l))
                        ob = psnext("g")
                        kts = []
                        for b in range(2):
                            dma("pool", KAS(b, 0, 256, 0, 64), cmkT[l, b, h], reads=[("ws", 0)], writes=[("sc", b, "k")], key=("sck", b))
                            dma("pool", VAS3(b)[:, 0:2, 0:64],
                                cmv[l, b].rearrange("(j p) (h d) -> p j h d", p=128, h=4)[:, :, h, :],
                                reads=[("ws", 0)], writes=[("sc", b, "v")], key=("scv", b), slow=True)
                            for j in range(2):
                                kts.append(dict(ka=KAS(b, 128 * j, 128, 0, 64), nk=128, va=VAS(b, j),
                                                lo=16 * b, hi=16 * b + 16, pv_lo=0, mask=None, pt=3 * b + (j % 3),
                                                reads=[("sc", b, "k"), ("qa", hh, 4)], vreads=[("sc", b, "v"), ("sc", b, "one")]))
                        osubs = [(PS[ob][0:32, 0:65], 0, 32, 0, len(kts) - 1)]
                        attend(lambda c0, n, hh=hh: QK(2 * hh, 2048 + c0, n, 0, 64), 64, kts, osubs, ob, "mems")
                        flush_pv()
                        norm_and_store(ob, 0, 32, 16, 2, 6 + pr, 64 * hh, SSMEM(16, 32), l)

                flush_pv()
                nsq.flush()
                _chk("memattn%d" % l)
                wload(2, w_in[l][:, 0:512], 512)
                for pr in range(4):
                    for bi, (c0, n) in enumerate(BLKS):
                        for qk, slot in ((0, 2), (1, 1)):
                            k = psnext("g")
                            mm_group(None, [(PS[k][:, 0:n], WS(slot, c, 128 * pr, 128), HT(c, c0, n), c == 0, c == 7) for c in range(8)],
                                     reads=ALLHT + [("ws", slot)], writes=[("ps", k)])
                            for hh in range(2):
                                if qk == 0:
                                    S.op("act", lambda e, k=k, hh=hh, c0=c0, n=n: e.activation(
                                        QK(2 * hh, c0, n, 0, 64), PS[k][64 * hh:64 * hh + 64, 0:n], AF.Copy, scale=0.125),
                                        reads=[("ps", k)], writes=[("qa", hh, bi)])
                                else:
                                    S.op("dve", lambda e, k=k, hh=hh, c0=c0, n=n: e.tensor_copy(
                                        QK(2 * hh + 1, c0, n, 0, 64), PS[k][64 * hh:64 * hh + 64, 0:n]),
                                        reads=[("ps", k)], writes=[("ka", hh, bi)])
                    if pr == 3:
                        wload(2, w_out[l][:, 0:512], 512)
                        wload(1, w_out[l][:, 512:1024], 512)
                    for hh in range(2):
                        h = 2 * pr + hh
                        dma("sp", QK(2 * hh, 0, 2048, 64, 65), cscr_p[h:h + 1, 0, :], reads=["cscr_p", ("qkc", 2 * hh)],
                            writes=[("qar", hh, 0)], key=("qar", hh))
                        dma("sp", QK(2 * hh + 1, 0, 2048, 65, 66), cscr_p[h:h + 1, 0, :], reads=["cscr_p", ("qkc", 2 * hh + 1)],
                            writes=[("kar", hh, 0)], key=("kar", hh))
                        dma("sp", QK(2 * hh + 1, 0, 2048, 66, 67), cscr_p[h:h + 1, 1, :], reads=["cscr_p2"],
                            writes=[("kar", hh, 1)], key=("kar", hh))
                        for b in range(2):
                            dma("sp", QK(2 * hh, 2048 + 16 * b, 16, 64, 65), cscr_s[h:h + 1, 0, b, 1024:1040],
                                reads=[("cscr_s", b, 0)], writes=[("qar", hh, 1 + b)], key=("qar", hh))
                            dma("sp", QK(2 * hh + 1, 2048 + 16 * b, 16, 65, 66), cscr_s[h:h + 1, 0, b, 1024:1040],
                                reads=[("cscr_s", b, 0)], writes=[("kar", hh, 2 + b)], key=("kar", hh))
                            dma("sp", QK(2 * hh + 1, 2048 + 16 * b, 16, 66, 67), cscr_s[h:h + 1, 1, b, 1024:1040],
                                reads=[("cscr_s", b, 1)], writes=[("kar", hh, 4 + b)], key=("kar", hh))
                        qar = [("qar", hh, x) for x in range(3)] + [("qkc", 2 * hh)]
                        kar = [("kar", hh, x) for x in range(6)] + [("qkc", 2 * hh + 1)]
                        for I in range(4):
                            ob = psnext("o")
                            kts = []
                            for j in range(4 * I + 4):
                                a = j - 4 * I
                                kts.append(dict(ka=QK(2 * hh + 1, 128 * j, 128, 0, 128), nk=128, va=VA128(j, h),
                                                lo=(128 * a if a >= 0 else 0), hi=512, pv_lo=(128 * a if a >= 0 else 0),
                                                mask=((MASKNEG, 128) if a >= 0 else None), pt="rot",
                                                reads=[("ka", hh, j // 4), ("qa", hh, I)] + qar + kar,
                                                vreads=[("va", j), "va_ones"]))
                            osubs = [(PS[ob][:, 65 * s:65 * s + 65], 128 * s, 128 * s + 128, 0, 4 * I + s) for s in range(4)]
                            attend(lambda c0, n, hh=hh, I=I: QK(2 * hh, 512 * I + c0, n, 0, 128), 128, kts, "fm", ob, "fox")
                            nsq.step(lambda ob=ob, I=I, pr=pr, hh=hh: norm_store_fm(ob, I, 0, pr, 64 * hh, SM[:, 160 + 4 * I:164 + 4 * I], l))
                        ob = psnext("g")
                        kts = []
                        for b in range(2):
                            dma("pool", KAS(b, 0, 1024, 0, 64), ckT[l, b, h], reads=[("ws", 0)], writes=[("sc", b, "k")], key=("sck", b))
                            dma("pool", VAS3(b)[:, :, 0:64],
                                cv[l, b].rearrange("(j p) (h d) -> p j h d", p=128, h=8)[:, :, h, :],
                                reads=[("ws", 0)], writes=[("sc", b, "v")], key=("scv", b), slow=True)
                            dma("sp", KAS(b, 0, 1024, 65, 66), cscr_s[h:h + 1, 0, b, 0:1024], reads=[("cscr_s", b, 0), ("ws", 0)],
                                writes=[("sc", b, "r")], key=("scr", b))
                            dma("sp", KAS(b, 0, 1024, 66, 67), cscr_s[h:h + 1, 1, b, 0:1024], reads=[("cscr_s", b, 1), ("ws", 0)],
                                writes=[("sc", b, "r2")], key=("scr", b))
                            for j in range(8):
                                kts.append(dict(ka=KAS(b, 128 * j, 128), nk=128, va=VAS(b, j),
                                                lo=16 * b, hi=16 * b + 16, pv_lo=0, mask=None, pt=3 * b + (j % 3),
                                                reads=[("sc", b, "k"), ("sc", b, "r"), ("sc", b, "r2"), ("sc", b, "one"), ("qa", hh, 4)] + qar,
                                                vreads=[("sc", b, "v"), ("sc", b, "one")]))
                        kts.append(dict(ka=QK(2 * hh + 1, 2048, 32, 0, 67), nk=32, va=VA(16, h, 32), lo=0, hi=32, pv_lo=0,
                                        mask=(MASKS, 32), pt=6, reads=[("ka", hh, 4), ("qa", hh, 4)] + qar + kar,
                                        vreads=[("va", 16), "va_ones"]))
                        osubs = [(PS[ob][0:32, 0:65], 0, 32, 0, len(kts) - 1)]
                        attend(lambda c0, n, hh=hh: QK(2 * hh, 2048 + c0, n, 0, 67), 67, kts, osubs, ob, "foxs")
                        flush_pv()
                        norm_and_store(ob, 0, 32, 16, 0, pr, 64 * hh, SSFOX(16, 32), l)

                flush_pv()
                nsq.flush()
                _chk("fox%d" % l)
                S.op("act", lambda e: e.activation(SM[:, 100:117], SM[:, 160:177], AF.Ln, bias=EPS, scale=1.0 / 512),
                     reads=[("ssq", 0, i) for i in range(17)], writes=["rfox"])
                S.op("act", lambda e: e.activation(SM[:, 100:117], SM[:, 100:117], AF.Exp, scale=-0.5), reads=["rfox"], writes=["rfox"])
                S.op("act", lambda e: e.activation(SM[:, 140:157], SM[:, 180:197], AF.Ln, bias=EPS, scale=1.0 / 256),
                     reads=[("ssq", 2, i) for i in range(17)], writes=["rmem"])
                S.op("act", lambda e: e.activation(SM[:, 140:157], SM[:, 140:157], AF.Exp, scale=-0.5), reads=["rmem"], writes=["rmem"])

                GP = SCRf(4, 1024)
                dma("sp", GP, g_post_mix[l:l + 1, :].broadcast_to([128, D]), writes=scr(4, 5, 6, 7), key="gp")
                for i in range(17):
                    rows = tile_rows(i)
                    c0, n = tcols(i)
                    ostg = SCRf(0, 1024, 0, rows)
                    mxr = [("mx", c, i, p0) for c in (0, 1, 2, 3, 6, 7) for p0 in (0, 64)] + [("mx", 4, i), ("mx", 5, i)]
                    last_b = None
                    for hf in range(2):
                        banks = [psnext("w") for _ in range(3)]
                        for gi, (cs, b) in enumerate(zip(((0, 1, 2, 3), (4, 5), (6, 7)), banks)):
                            wsl = 2 if hf == 0 else 1
                            mm_group(None, [(PS[b][0:rows, :], MX(c, c0, n), WS(wsl, c, 0, 512), c == cs[0], c == cs[-1]) for c in cs],
                                     reads=mxr + [("ws", wsl)], writes=[("ps", b)])
                        o_h = ostg[:, 512 * hf:512 * hf + 512]
                        S.op("act", lambda e, o_h=o_h, b=banks[0], rows=rows, i=i: e.activation(o_h, PS[b][0:rows, :], AF.Copy, scale=RFOX(i, rows)),
                             reads=[("ps", banks[0]), "rfox"], writes=scr(2 * hf, 2 * hf + 1))
                        S.op("dve", lambda e, o_h=o_h, b=banks[1], rows=rows, i=i: e.scalar_tensor_tensor(o_h, PS[b][0:rows, :], RSGU(i, rows), o_h, ALU.mult, ALU.add),
                             reads=[("ps", banks[1]), ("rsgu", i)] + scr(2 * hf, 2 * hf + 1), writes=scr(2 * hf, 2 * hf + 1))
                        S.op("dve", lambda e, o_h=o_h, b=banks[2], rows=rows, i=i: e.scalar_tensor_tensor(o_h, PS[b][0:rows, :], RMEM(i, rows), o_h, ALU.mult, ALU.add),
                             reads=[("ps", banks[2]), "rmem"] + scr(2 * hf, 2 * hf + 1), writes=scr(2 * hf, 2 * hf + 1))
                        last_b = banks
                    post_norm_residual(ostg, rows, i, bf(WS_O[0], 1024, 0, rows), [("ws", 0)] + SC_TOKS, GP, scr(4, 5, 6, 7), scr(0, 1, 2, 3))

                _chk("wout%d" % l)
                S.barrier()
                ffn_phase(l)
                if l < L - 1:
                    S.barrier()

        except _Stop:
            pass
        S.emit(st)
    return nc


_NC_CACHE = {}


def _prep_inputs(inp):
    f = lambda a: np.ascontiguousarray(np.asarray(a, dtype=np.float32))
    x_prompt = f(inp["x_prompt"]); x_sample = f(inp["x_sample"]); mem_prompt = f(inp["mem_prompt"])
    cfk = f(inp["cache_fox_k"]); cfv = f(inp["cache_fox_v"]); clf = f(inp["cache_fox_logf"])
    cmk = f(inp["cache_mem_k"]); cmvv = f(inp["cache_mem_v"]); cconv = f(inp["cache_ffn_conv"])
    ckT_all = np.ascontiguousarray(cfk.transpose(0, 1, 3, 4, 2))
    cv_all = cfv.reshape(L, 16, 1024, 512)
    clfT_all = np.ascontiguousarray(clf.transpose(0, 1, 3, 2))
    cmkT_all = np.ascontiguousarray(cmk.transpose(0, 1, 3, 4, 2))
    cmv_all = cmvv.reshape(L, 16, 256, 256)
    wsT = np.ascontiguousarray(f(inp["w_spatial"]).transpose(0, 1, 3, 2))
    fm8 = lambda g: f(g).reshape(L, 8, 128).transpose(0, 2, 1)
    b_sp = f(inp["b_spatial"])
    w_dw = f(inp["w_dwconv"]); b_dw = f(inp["b_dwconv"]); b_f = f(inp["b_forget"])
    cst = np.zeros((128, 288), dtype=ml_dtypes.bfloat16)
    cst[:, 0:128] = np.eye(128, dtype=np.float32).astype(ml_dtypes.bfloat16)
    kk, qq = np.meshgrid(np.arange(128), np.arange(128), indexing="ij")
    cst[:, 128:256] = np.where(kk <= qq, 0.0, NEG).astype(ml_dtypes.bfloat16)
    k2, q2 = np.meshgrid(np.arange(32), np.arange(32), indexing="ij")
    ok = (k2 // 16 == q2 // 16) & (k2 % 16 <= q2 % 16)
    cst[0:32, 256:288] = np.where(ok, 0.0, NEG).astype(ml_dtypes.bfloat16)
    wu = f(inp["w_up"])
    wu_a = wu[:, :, 0:FF].reshape(L, 8, 128, NJ, 128)
    wu_l = wu[:, :, FF:2 * FF].reshape(L, 8, 128, NJ, 128)
    w_up_r = np.ascontiguousarray(np.concatenate([wu_a, wu_l], axis=4).transpose(0, 3, 2, 1, 4))
    shared = dict(
        w_in=f(inp["w_in"]), w_mkv=f(inp["w_mem_kv"]), w_out=f(inp["w_out"]), w_up=w_up_r,
        w_dn=f(inp["w_down"]), g_post_mix=f(inp["g_post_mix"]), g_post_ffn=f(inp["g_post_ffn"]),
        g_sgu=f(inp["g_sgu"]), wsT=wsT, cst=cst)
    gpm = fm8(inp["g_pre_mix"]); ggo = fm8(inp["g_group_out"]); gmem = fm8(inp["g_mem"]); gpf = fm8(inp["g_pre_ffn"])
    in_maps = []
    for c in range(NCORES):
        prm = np.zeros((128, NPRM), dtype=np.float32)
        for l in range(L):
            o = l * PPL
            prm[:, o + PO["gpm"]:o + PO["gpm"] + 8] = gpm[l]
            prm[:, o + PO["ggo"]:o + PO["ggo"] + 8] = ggo[l]
            prm[:, o + PO["gmem"]:o + PO["gmem"] + 8] = gmem[l]
            prm[:, o + PO["gpf"]:o + PO["gpf"] + 8] = gpf[l]
            prm[:, o + PO["bs"]:o + PO["bs"] + 4] = b_sp[l].T
            prm[0:16, o + PO["bss"]:o + PO["bss"] + 4] = b_sp[l][:, 0:16].T
            prm[16:32, o + PO["bss"]:o + PO["bss"] + 4] = b_sp[l][:, 0:16].T
            prm[:, o + PO["wdw"]:o + PO["wdw"] + 66] = w_dw[l].reshape(3, NJ, 128).transpose(2, 1, 0).reshape(128, 66)
            prm[:, o + PO["bdw"]:o + PO["bdw"] + 22] = b_dw[l].reshape(NJ, 128).T
            prm[0:8, o + PO["bf"]] = b_f[l]
            cc = cconv[l, 2 * c:2 * c + 2]
            prm[:, o + PO["cconv"]:o + PO["cconv"] + 88] = cc.reshape(2, 2, NJ, 128).transpose(3, 2, 0, 1).reshape(128, 88)
        m = dict(shared)
        m.update(
            x_p=x_prompt[c], x_s=x_sample[2 * c:2 * c + 2].reshape(32, D), mem=mem_prompt[c],
            ckT=np.ascontiguousarray(ckT_all[:, 2 * c:2 * c + 2]), cv=np.ascontiguousarray(cv_all[:, 2 * c:2 * c + 2]),
            clfT=np.ascontiguousarray(clfT_all[:, 2 * c:2 * c + 2]), cmkT=np.ascontiguousarray(cmkT_all[:, 2 * c:2 * c + 2]),
            cmv=np.ascontiguousarray(cmv_all[:, 2 * c:2 * c + 2]), prm=prm)
        in_maps.append(m)
    return in_maps


def kernel(**inp):
    in_maps = _prep_inputs(inp)
    if "nc" not in _NC_CACHE:
        _NC_CACHE["nc"] = build()
    nc = _NC_CACHE["nc"]
    res = run_bass_kernel_spmd(nc, in_maps, core_ids=list(range(NCORES)))
    R = res.results
    cat = lambda name: np.stack([np.asarray(r[name], dtype=np.float32) for r in R], axis=0)
    y_p = cat("y_p")
    y_s = cat("y_s").reshape(16, 16, D)
    fk = cat("ofk").transpose(1, 0, 2, 3).reshape(L, 8, 2048, 8, 64)
    fv = cat("ofv").transpose(1, 0, 2, 3).reshape(L, 8, 2048, 8, 64)
    lf = cat("olf").transpose(1, 0, 3, 2)
    mk = cat("omk").transpose(1, 0, 2, 3).reshape(L, 8, 256, 4, 64)
    mv = cat("omv").transpose(1, 0, 2, 3).reshape(L, 8, 256, 4, 64)
    cvp = cat("oconv").transpose(1, 0, 4, 3, 2).reshape(L, 8, 2, FF)
    sk = cat("osk").transpose(1, 0, 2, 3).reshape(L, 16, 16, 8, 64)
    sv = cat("osv").transpose(1, 0, 2, 3).reshape(L, 16, 16, 8, 64)
    slf = cat("oslf").transpose(1, 0, 3, 2).reshape(L, 16, 16, 8)
    gv = cat("ogv").transpose(1, 0, 2, 3).reshape(L, 16, 16, 256)
    cvs = cat("osconv").transpose(1, 0, 4, 5, 3, 2).reshape(L, 16, 2, FF)
    outs = (y_p, y_s, fk, fv, lf, mk, mv, cvp, sk, sv, slf, gv, cvs)
    return tuple(np.ascontiguousarray(o, dtype=np.float32) for o in outs)
```

```python
import numpy as np
import ml_dtypes
from contextlib import ExitStack
import concourse.bass as bass
import concourse.mybir as mybir
from concourse.bass_utils import run_bass_kernel_spmd

F32 = mybir.dt.float32
BF16 = mybir.dt.bfloat16
ALU = mybir.AluOpType
AF = mybir.ActivationFunctionType

ENGS = ("pe", "act", "dve", "pool", "sp")
STOP = None
DBG = set()


class _Stop(Exception):
    pass


def _chk(name):
    if STOP == name:
        raise _Stop()
NCORES = 8
L = 2
D = 1024
NT = 2080
FF = 2816
NJ = 22
EPS = 1e-6
NEG = -30000.0


class Op:
    __slots__ = ("eng", "fn", "idx", "deps", "dma_key", "marked", "seq", "cum")

    def __init__(self, eng, fn, idx, dma_key):
        self.eng = eng
        self.fn = fn
        self.idx = idx
        self.deps = ()
        self.dma_key = dma_key
        self.marked = False
        self.seq = 0
        self.cum = 0


class Sched:
    def __init__(self, nc):
        self.nc = nc
        self.streams = {e: [] for e in ENGS}
        self.last_writer = {}
        self.readers = {}
        self.dma_counts = {}
        self.dma_last = {}

    def op(self, eng, fn, reads=(), writes=(), dma_key=None):
        o = Op(eng, fn, len(self.streams[eng]), dma_key)
        deps = set()
        lw = self.last_writer
        rd = self.readers
        for t in reads:
            w = lw.get(t)
            if w is not None:
                deps.add(w)
            if type(t) is tuple and t[0] == "ps":
                for r in rd.get(t, ()):
                    if r.eng != eng:
                        deps.add(r)
        for t in writes:
            w = lw.get(t)
            if w is not None:
                deps.add(w)
            r = rd.get(t)
            if r:
                deps.update(r)
        for t in reads:
            rd.setdefault(t, []).append(o)
        for t in writes:
            lw[t] = o
            rd[t] = []
        deps.discard(o)
        if eng == "pe":
            deps = {d for d in deps if not (d.eng == "pe" and d.dma_key is None)}
        o.deps = deps
        if dma_key is not None:
            c = self.dma_counts.get(dma_key, 0) + 1
            self.dma_counts[dma_key] = c
            o.cum = 16 * c
            self.dma_last[dma_key] = o
        self.streams[eng].append(o)
        return o

    def barrier(self):
        lasts = [s[-1] for s in self.streams.values() if s]
        lasts += list(self.dma_last.values())
        for e in ENGS:
            o = Op(e, lambda eng: eng.nop(), len(self.streams[e]), None)
            o.deps = {d for d in lasts if not (d.eng == e and d.dma_key is None)}
            self.streams[e].append(o)

    def emit(self, stack):
        nc = self.nc
        for e in ENGS:
            for o in self.streams[e]:
                for d in o.deps:
                    if d.dma_key is None:
                        d.marked = True
        sems = {}
        for e in ENGS:
            n = 0
            for o in self.streams[e]:
                if o.dma_key is None and o.marked:
                    n += 1
                    o.seq = n
            if n:
                sems[e] = stack.enter_context(nc.semaphore("s_" + e))
        dsems = {}
        for k in self.dma_counts:
            dsems[k] = stack.enter_context(nc.semaphore("d%d" % len(dsems)))
        self.n_sems = len(sems) + len(dsems)
        final = {k: o.cum for k, o in self.dma_last.items()}

        def run(eng_name, eng):
            waited = {}
            for o in self.streams[eng_name]:
                need = {}
                for d in o.deps:
                    if d.dma_key is not None:
                        key = ("d", d.dma_key)
                        val = d.cum
                    else:
                        key = ("c", d.eng)
                        val = d.seq
                    if val > need.get(key, 0):
                        need[key] = val
                for key, val in need.items():
                    if waited.get(key, 0) >= val:
                        continue
                    waited[key] = val
                    s = dsems[key[1]] if key[0] == "d" else sems[key[1]]
                    eng.wait_ge(s, val)
                ins = o.fn(eng)
                if o.dma_key is not None:
                    ins.then_inc(dsems[o.dma_key], 16)
                elif o.marked:
                    ins.then_inc(sems[o.eng], 1)
            if eng_name == "sp":
                for k, v in final.items():
                    if waited.get(("d", k), 0) < v:
                        eng.wait_ge(dsems[k], v)

        with nc.Block() as block:
            @block.tensor
            def _(t):
                run("pe", t)

            @block.scalar
            def _(t):
                run("act", t)

            @block.vector
            def _(t):
                run("dve", t)

            @block.gpsimd
            def _(t):
                run("pool", t)

            @block.sync
            def _(t):
                run("sp", t)


def tile_rows(i):
    return 128 if i < 16 else 32


def tcols(i):
    return (128 * i, 128) if i < 16 else (2048, 32)


BLKS = [(0, 512), (512, 512), (1024, 512), (1536, 512), (2048, 32)]

PO = {}
_o = 0
for _n, _w in (("gpm", 8), ("ggo", 8), ("gmem", 8), ("gpf", 8), ("bs", 4), ("bss", 4),
               ("wdw", 66), ("bdw", 22), ("bf", 1), ("cconv", 88)):
    PO[_n] = _o
    _o += _w
PPL = _o
NPRM = PPL * L


def build():
    nc = bass.Bass("TRN2", target_bir_lowering=False)

    def din(name, shape, dt=F32):
        return nc.dram_tensor(name, list(shape), dt, kind="ExternalInput").ap()

    def dout(name, shape):
        return nc.dram_tensor(name, list(shape), F32, kind="ExternalOutput").ap()

    x_p = din("x_p", [2048, D])
    x_s = din("x_s", [32, D])
    mem = din("mem", [256, D])
    ckT = din("ckT", [L, 2, 8, 64, 1024])
    cv = din("cv", [L, 2, 1024, 512])
    clfT = din("clfT", [L, 2, 8, 1024])
    cmkT = din("cmkT", [L, 2, 4, 64, 256])
    cmv = din("cmv", [L, 2, 256, 256])
    w_in = din("w_in", [L, D, 2312])
    w_mkv = din("w_mkv", [L, D, 512])
    w_out = din("w_out", [L, D, D])
    w_up = din("w_up", [L, NJ, 128, 8, 256])
    w_dn = din("w_dn", [L, FF, D])
    g_post_mix = din("g_post_mix", [L, D])
    g_post_ffn = din("g_post_ffn", [L, D])
    g_sgu = din("g_sgu", [L, 256])
    wsT = din("wsT", [L, 4, 128, 128])
    prm_h = din("prm", [128, NPRM])
    cst_h = din("cst", [128, 288], BF16)

    y_p = dout("y_p", [2048, D])
    y_s = dout("y_s", [32, D])
    ofk = dout("ofk", [L, 2048, 512])
    ofv = dout("ofv", [L, 2048, 512])
    olf = dout("olf", [L, 8, 2048])
    omk = dout("omk", [L, 256, 256])
    omv = dout("omv", [L, 256, 256])
    oconv = dout("oconv", [L, 128, NJ, 2])
    osk = dout("osk", [L, 32, 512])
    osv = dout("osv", [L, 32, 512])
    oslf = dout("oslf", [L, 8, 32])
    ogv = dout("ogv", [L, 32, 256])
    osconv = dout("osconv", [L, 128, NJ, 2, 2])

    cscr_p = nc.dram_tensor("cscr_p", [8, 2, 2048], BF16).ap()
    cscr_s = nc.dram_tensor("cscr_s", [8, 2, 2, 1040], BF16).ap()

    with ExitStack() as st:
        S = Sched(nc)
        X = st.enter_context(nc.sbuf_tensor("X", [128, 17, D], F32))
        PRM = st.enter_context(nc.sbuf_tensor("PRM", [128, NPRM], F32))
        CST = st.enter_context(nc.sbuf_tensor("CST", [128, 288], BF16))
        WSTSt = st.enter_context(nc.sbuf_tensor("WSTS", [32, L * 4 * 32], BF16))
        PTSt = st.enter_context(nc.sbuf_tensor("PTS", [128, 7 * 32], BF16))
        SM = st.enter_context(nc.sbuf_tensor("SM", [128, 256], F32))
        ONEROW = st.enter_context(nc.sbuf_tensor("ONEROW", [128, 64], F32))
        remaining = nc.sbuf_bytes_remaining
        remaining = remaining() if callable(remaining) else remaining
        ARB = (remaining - 512) // 64 * 64
        AR = st.enter_context(nc.sbuf_tensor("AR", [128, ARB // 2], BF16))
        AR32 = AR.bitcast(F32)
        PS = [st.enter_context(nc.psum_tensor("ps%d" % k, [128, 512], F32)) for k in range(8)]
        PSB = [p.bitcast(BF16) for p in PS]

        IDENT = CST[:, 0:128]
        MASKNEG = CST[:, 128:256]
        MASKS = CST[0:32, 256:288]


        def WSTS(l, g):
            o = (l * 4 + g) * 32
            return WSTSt[0:32, o:o + 32]

        def prm(l, name, w, p0=0, p1=128, off=0):
            o = l * PPL + PO[name] + off
            return PRM[p0:p1, o:o + w]

        SM_SS, SM_SD, SM_RSTD = 0, 1, 2
        SM_R = 16
        SM_SSF = 70
        SM_T = 90

        def bf(off, n, p0=0, p1=128):
            return AR[p0:p1, off // 2: off // 2 + n]

        def f32(off, n, p0=0, p1=128):
            return AR32[p0:p1, off // 4: off // 4 + n]

        cur = [0]

        def carve(nbytes):
            o = cur[0]
            cur[0] = o + (nbytes + 63) // 64 * 64
            return o

        HT_O = carve(8 * NT * 2)
        MX_O = carve(8 * NT * 2)
        VA_O = carve(17 * 8 * 65 * 2)
        QK_O = carve(4 * NT * 2)
        WS_O = [carve(8320) for _ in range(3)]
        SCR_O = carve(8192)
        MKT_O = carve(2048)
        MVA_O = carve(1040)
        MKTt = AR[:, MKT_O // 2:MKT_O // 2 + 1024]
        MVAt = AR[:, MVA_O // 2:MVA_O // 2 + 520]
        MIX_END = cur[0]
        assert MIX_END <= ARB, (MIX_END, ARB)

        def HT(c, c0, n):
            return bf(HT_O + (c * NT + c0) * 2, n)

        def HT3(c0, n):
            return AR[:, HT_O // 2: HT_O // 2 + 8 * NT].rearrange("p (c t) -> p c t", c=8)[:, :, c0:c0 + n]

        def MX(c, c0, n, p0=0, p1=128):
            return bf(MX_O + (c * NT + c0) * 2, n, p0, p1)

        def VA(j, h, nk=128):
            return bf(VA_O + ((j * 8 + h) * 65) * 2, 65, 0, nk)

        def VA128(j, h):
            return bf(VA_O + ((j * 8 + h) * 65) * 2, 128, 0, 128)

        def QK(slot, c0, n, p0, p1):
            return bf(QK_O + (slot * NT + c0) * 2, n, p0, p1)

        CT = lambda c0, n: f32(QK_O + c0 * 4, n, 96, 104)
        HI = lambda c0, n: bf(QK_O + 8320 + c0 * 2, n, 96, 104)
        LO = lambda c0, n: bf(QK_O + 8320 + 4160 + c0 * 2, n, 96, 104)

        def WS(s, c, c0, n):
            return bf(WS_O[s] + (c * 520 + c0) * 2, n)

        def WS3(s, ncol):
            return AR[:, WS_O[s] // 2: WS_O[s] // 2 + 8 * 520].rearrange("p (c n) -> p c n", c=8)[:, :, 0:ncol]

        def SCRb(g, n, p0=0, p1=128, off=0):
            return bf(SCR_O + g * 1024 + off * 2, n, p0, p1)

        def SCRf(g, n, p0=0, p1=128, off=0):
            return f32(SCR_O + g * 1024 + off * 4, n, p0, p1)

        def scr(*gs):
            return [("scr", g) for g in gs]

        def dma(eng, out, in_, reads=(), writes=(), key=None, slow=False):
            if slow:
                fn = lambda e: e.dma_start(out=out, in_=in_, allow_slow_non_contiguous=True)
            else:
                fn = lambda e: e.dma_start(out=out, in_=in_)
            return S.op(eng, fn, reads=reads, writes=writes, dma_key=key)

        psr = {"g": [0, 1], "s": [2, 3, 4], "o": [5, 6], "t": [7]}
        psi = {k: 0 for k in psr}

        def psnext(pool):
            lst = psr[pool]
            k = lst[psi[pool] % len(lst)]
            psi[pool] += 1
            return k

        def mm_group(out_fn, items, reads, writes):
            def fn(e):
                ins = None
                for (o, a, b, s0, s1) in items:
                    ins = e.matmul(o, a, b, start=s0, stop=s1, skip_group_check=True)
                return ins
            return S.op("pe", fn, reads=reads, writes=writes)

        dma("sp", PRM[:, :], prm_h, writes=["prm"], key="prm")
        dma("sp", CST[:, :], cst_h, writes=["cst"], key="cst")
        S.op("dve", lambda e: e.memset(WSTSt[:, :], 0.0), writes=["wsts"])
        for l in range(L):
            for b in range(2):
                dma("pool",
                    WSTSt[16 * b:16 * b + 16, l * 128:(l + 1) * 128].rearrange("p (g i) -> p g i", g=4)[:, :, 16 * b:16 * b + 16],
                    wsT[l, :, 0:16, 0:16].rearrange("g j i -> j g i"),
                    reads=["wsts"], writes=[("wsts", l, b)], key=("wsts", l), slow=True)
        S.op("dve", lambda e: e.memset(PTSt[:, :], 0.0), writes=["pts%d" % q for q in range(7)])
        S.op("dve", lambda e: e.memset(SM[:, 200:201], 1.0), writes=["one"])
        S.op("dve", lambda e: e.memset(ONEROW[:, :], 1.0), writes=["onerow"])
        ONE8 = SM[96:104, 200:201]

        for i in range(16):
            dma("sp", X[:, i, :], x_p[128 * i:128 * i + 128, :], writes=[("x", i)], key=("x", i))
        dma("sp", X[0:32, 16, :], x_s, writes=[("x", 16)], key=("x", 16))

        def prenorm_tile(src, rows, xtok, gcol, dst3, dst_tok, slot, extra_writes=(), hb=None, tpool="t", hb_toks=None, defer=False):
            if hb is None:
                hb = SCRb(2 * slot, 1024, 0, rows)
            hbt = list(hb_toks) if hb_toks is not None else scr(2 * slot, 2 * slot + 1)
            c0 = 3 * slot
            ss = SM[0:rows, c0:c0 + 1]
            sd = SM[0:rows, c0 + 1:c0 + 2]
            rs = SM[0:rows, c0 + 2:c0 + 3]
            S.op("act", lambda e: e.activation(hb, src, AF.Square, accum_out=ss),
                 reads=[xtok], writes=hbt + [("sm", c0)])
            S.op("act", lambda e: e.activation(sd, ss, AF.Ln, bias=EPS, scale=1.0 / D),
                 reads=[("sm", c0)], writes=[("sm", c0 + 1)])
            S.op("act", lambda e: e.activation(rs, sd, AF.Exp, scale=-0.5), reads=[("sm", c0 + 1)], writes=[("sm", c0 + 2)])
            S.op("act", lambda e: e.activation(hb, src, AF.Copy, scale=rs),
                 reads=[xtok, ("sm", c0 + 2)], writes=hbt)
            def part_b():
                k = psnext(tpool)

                def tr(e):
                    ins = None
                    for c in range(8):
                        ins = e.transpose(PSB[k][:, c * 128:c * 128 + rows], hb[:, c * 128:(c + 1) * 128],
                                          IDENT[0:rows, 0:rows])
                    return ins
                S.op("pe", tr, reads=hbt + ["cst"], writes=[("ps", k)])
                src3 = PSB[k][:, 0:1024].rearrange("p (c t) -> p c t", c=8)[:, :, 0:rows]
                S.op("dve", lambda e: e.tensor_tensor(dst3, src3, gcol.unsqueeze(2).to_broadcast([128, 8, rows]), ALU.mult),
                     reads=[("ps", k), "prm"], writes=[dst_tok] + list(extra_writes))
            if defer:
                return part_b
            part_b()

        PVQ = []

        def flush_pv():
            while PVQ:
                PVQ.pop(0)()

        def attend(qa_fn, K, ktiles, osubs, obank, tag):
            pendq = []
            first_done = [False]

            def pv(kt_i, kt, pt_ap, pt_tok):
                items = []
                if osubs == "fm":
                    lo_, hi_ = kt["lo"], kt["hi"]
                    mrows = kt["va"].shape[1]
                    items.append((PS[obank][0:mrows, lo_:hi_], kt["va"], pt_ap[:, lo_:hi_], kt_i == 0, kt_i == len(ktiles) - 1))
                    mm_group(None, items, reads=[pt_tok] + kt["vreads"], writes=[("ps", obank)])
                    return
                for (o_ap, c0, c1, fk, lk) in osubs:
                    if fk <= kt_i <= lk and kt["pv_lo"] <= c0:
                        items.append((o_ap, pt_ap[:, c0:c1], kt["va"], not first_done[0], kt_i == lk))
                        first_done[0] = True
                if items:
                    mm_group(None, items, reads=[pt_tok] + kt["vreads"], writes=[("ps", obank)])

            for kt_i, kt in enumerate(ktiles):
                k = psnext("s")
                nk, lo, hi = kt["nk"], kt["lo"], kt["hi"]
                items = []
                if kt["mask"] is not None:
                    mask_ap, mc = kt["mask"]
                    items.append((PS[k][0:nk, lo:lo + mc], kt["ka"], qa_fn(lo, mc), True, False))
                    items.append((PS[k][0:nk, lo:lo + mc], IDENT[0:nk, 0:nk], mask_ap, False, True))
                    if hi > lo + mc:
                        items.append((PS[k][0:nk, lo + mc:hi], kt["ka"], qa_fn(lo + mc, hi - lo - mc), True, True))
                else:
                    items.append((PS[k][0:nk, lo:hi], kt["ka"], qa_fn(lo, hi - lo), True, True))
                mm_group(None, items, reads=kt["reads"] + ["cst"], writes=[("ps", k)])
                if kt["pt"] == "rot":
                    g = attend.ptc % 4
                    attend.ptc += 1
                    pt_ap = SCRb(g, 512, 0, nk)
                    pt_tok = ("scr", g)
                else:
                    pt_ap = PTSt[0:nk, 32 * kt["pt"]:32 * kt["pt"] + 32]
                    pt_tok = "pts%d" % kt["pt"]
                S.op("act", lambda e, pt_ap=pt_ap, k=k, nk=nk, lo=lo, hi=hi:
                     e.activation(pt_ap[:, lo:hi], PS[k][0:nk, lo:hi], AF.Exp),
                     reads=[("ps", k)], writes=[pt_tok])
                while len(PVQ) >= 2:
                    PVQ.pop(0)()
                PVQ.append(lambda kt_i=kt_i, kt=kt, pt_ap=pt_ap, pt_tok=pt_tok: pv(kt_i, kt, pt_ap, pt_tok))
        attend.ptc = 0

        def cs_all():
            return ["cw"]

        def post_norm_residual(ostg, rows, i, junk, junk_toks, GPap, gp_toks, o_toks):
            ssc = SM[0:rows, SM_T + 8:SM_T + 9]
            S.op("act", lambda e: e.activation(junk, ostg, AF.Square, accum_out=ssc),
                 reads=o_toks, writes=list(junk_toks) + ["pn"])
            S.op("act", lambda e: e.activation(ssc, ssc, AF.Ln, bias=EPS, scale=1.0 / D), reads=["pn"], writes=["pn"])
            S.op("act", lambda e: e.activation(ssc, ssc, AF.Exp, scale=-0.5), reads=["pn"], writes=["pn"])
            S.op("dve", lambda e: e.scalar_tensor_tensor(ostg, ostg, ssc, GPap[0:rows, :], ALU.mult, ALU.mult),
                 reads=list(o_toks) + list(gp_toks) + ["pn"], writes=o_toks)
            S.op("pool", lambda e: e.tensor_tensor(X[0:rows, i, :], X[0:rows, i, :], ostg, ALU.add),
                 reads=list(o_toks) + [("x", i)], writes=[("x", i)])

        def ffn_phase(l):
            psr.clear()
            psr.update({"u": [0, 1, 2, 4, 5, 6], "t": [3], "d": [4, 5, 6, 7]})
            for kk in psr:
                psi[kk] = 0
            HC = 1056
            o = [0]

            def cv_(nb):
                r = o[0]
                o[0] = r + (nb + 63) // 64 * 64
                return r
            H2_O = cv_(8 * HC * 2)
            ACT_O = cv_(NJ * HC * 2)
            WDN_O = cv_(NJ * 1024 * 2)
            WU_O = [cv_(4096) for _ in range(2)]
            AE_O = [cv_(2064) for _ in range(2)]
            LIN_O = [cv_(1024) for _ in range(2)]
            CONV_O = [cv_(2048) for _ in range(2)]
            OSTG_O = cv_(4096)
            GPF_O = cv_(4096)
            HBF_O = cv_(2048)
            SAVE_O = cv_(NJ * 2 * 4)
            SAVES_O = cv_(NJ * 4 * 4)
            assert o[0] <= ARB, (o[0], ARB)

            H2 = lambda c, c0, n: bf(H2_O + (c * HC + c0) * 2, n)
            H23 = lambda c0, n: AR[:, H2_O // 2:H2_O // 2 + 8 * HC].rearrange("p (c t) -> p c t", c=8)[:, :, c0:c0 + n]
            ACTB = lambda j, c0, n: bf(ACT_O + (j * HC + c0) * 2, n)
            WDN = lambda j, c0, n: bf(WDN_O + (j * 1024 + c0) * 2, n)
            WDN3 = AR[:, WDN_O // 2:WDN_O // 2 + NJ * 1024].rearrange("p (j n) -> p j n", j=NJ)
            WUB = lambda s: (WU_O[s] if s < 2 else OSTG_O)
            WU = lambda s, c, c0, n: bf(WUB(s) + (c * 256 + c0) * 2, n)
            WU3 = lambda s: AR[:, WUB(s) // 2:WUB(s) // 2 + 2048].rearrange("p (c n) -> p c n", c=8)
            wutok = lambda s, x: (("wu", s, x) if s < 2 else ("ostg%d" % x))
            OSTG = f32(OSTG_O, 1024)
            GPF = f32(GPF_O, 1024)
            HBF = bf(HBF_O, 1024)
            SAVE = f32(SAVE_O, NJ * 2)
            SAVES = f32(SAVES_O, NJ * 4)

            dma("sp", GPF, g_post_ffn[l:l + 1, :].broadcast_to([128, D]), writes=["gpf"], key="gp")
            wdn_src = w_dn[l].rearrange("(j p) n -> p j n", p=128)
            for part in range(2):
                dma("pool", WDN3[:, 11 * part:11 * part + 11, :], wdn_src[:, 11 * part:11 * part + 11, :],
                    writes=[("wdn", part)], key=("wdn", part))
            aecnt = 0

            def wu_load(j, s_):
                dma("pool", WU3(s_), w_up[l, j], writes=[wutok(s_, 0), wutok(s_, 1)], key=("wu", s_, 0))

            for half in range(2):
                tiles = list(range(0, 8)) if half == 0 else list(range(8, 17))
                base = 0 if half == 0 else 1024
                blocks = [(0, 512), (512, 512)] + ([(1024, 32)] if half == 1 else [])
                h2toks = [("h2", (i if i < 8 else i - 8)) for i in tiles]
                if half == 0:
                    HB2 = bf(AE_O[0], 1024)

                    def pa(i):
                        c0, n = tcols(i)
                        if i % 2 == 0:
                            return prenorm_tile(X[:, i, :], 128, ("x", i), prm(l, "gpf", 8), H23(c0, n), ("h2", i), 0,
                                                hb=HBF[:, :], tpool="t", hb_toks=["hbf"], defer=True)
                        return prenorm_tile(X[:, i, :], 128, ("x", i), prm(l, "gpf", 8), H23(c0, n), ("h2", i), 1,
                                            hb=HB2, tpool="t", hb_toks=[("ae", 0), ("aeh", 0), ("aet", 0)], defer=True)
                    pbq = {0: pa(0)}
                    for i in tiles:
                        if i + 1 < 8:
                            pbq[i + 1] = pa(i + 1)
                        pbq.pop(i)()
                wu_load(0, 0)
                wu_load(1, 1)
                pend_tail = None
                for j in range(NJ):
                    s = j % 3
                    if j + 2 < NJ:
                        wu_load(j + 2, (j + 2) % 3)
                    w0 = prm(l, "wdw", 1, 0, 128, 3 * j)
                    w1 = prm(l, "wdw", 1, 0, 128, 3 * j + 1)
                    w2 = prm(l, "wdw", 1, 0, 128, 3 * j + 2)
                    bd = prm(l, "bdw", 1, 0, 128, j)
                    for bidx, (lc0, n) in enumerate(blocks):
                        sample = (n == 32)
                        ka = psnext("u")
                        mm_group(None, [(PS[ka][:, 0:n], WU(s, c, 0, 128), H2(c, lc0, n), c == 0, c == 7) for c in range(8)],
                                 reads=h2toks + [wutok(s, 0)], writes=[("ps", ka)])
                        kl = psnext("u")
                        mm_group(None, [(PS[kl][:, 0:n], WU(s, c, 128, 128), H2(c, lc0, n), c == 0, c == 7) for c in range(8)],
                                 reads=h2toks + [wutok(s, 1)], writes=[("ps", kl)])
                        p = aecnt % 2
                        aecnt += 1
                        lin = bf(LIN_O[p], 512)
                        conv = f32(CONV_O[p], 512)
                        if sample:
                            AEs = f32(AE_O[p], 36).rearrange("p (b t) -> p b t", b=2)
                            S.op("dve", lambda e, AEs=AEs, j=j: e.tensor_copy(
                                AEs[:, :, 0:2], prm(l, "cconv", 4, 0, 128, 4 * j).rearrange("p (b r) -> p b r", b=2)),
                                reads=["prm"], writes=[("aeh", p)])
                            S.op("act", lambda e, AEs=AEs, ka=ka: e.copy(AEs[:, :, 2:18], PS[ka][:, 0:32].rearrange("p (b t) -> p b t", b=2)),
                                 reads=[("ps", ka)], writes=[("ae", p)])
                            taps = [AEs[:, :, k:k + 16] for k in range(3)]
                            cv3 = conv[:, 0:32].rearrange("p (b t) -> p b t", b=2)
                            psa = PS[ka][:, 0:32].rearrange("p (b t) -> p b t", b=2)
                            sil_out = f32(AE_O[p] + 256, 32)
                            sil_in = conv[:, 0:32]
                        else:
                            AEp = f32(AE_O[p], 514)
                            if bidx == 0 and half == 0:
                                S.op("dve", lambda e, AEp=AEp: e.memset(AEp[:, 0:2], 0.0), writes=[("aeh", p)])
                            elif bidx == 0:
                                S.op("dve", lambda e, AEp=AEp, j=j: e.tensor_copy(AEp[:, 0:2], SAVE[:, 2 * j:2 * j + 2]),
                                     reads=[("save", j)], writes=[("aeh", p)])
                            else:
                                prev = f32(AE_O[1 - p], 514)
                                S.op("dve", lambda e, AEp=AEp, prev=prev: e.tensor_copy(AEp[:, 0:2], prev[:, 512:514]),
                                     reads=[("aet", 1 - p)], writes=[("aeh", p)])
                            S.op("act", lambda e, AEp=AEp, ka=ka: e.copy(AEp[:, 2:514], PS[ka][:, 0:512]),
                                 reads=[("ps", ka)], writes=[("ae", p), ("aet", p)])
                            taps = [AEp[:, k:k + 512] for k in range(3)]
                            cv3 = conv
                            psa = PS[ka][:, 0:512]
                            sil_out = AEp[:, 0:512]
                            sil_in = conv
                        S.op("act", lambda e, cv3=cv3, psa=psa, w2=w2, bd=bd: e.activation(cv3, psa, AF.Identity, bias=bd, scale=w2),
                             reads=[("ps", ka), "prm"], writes=[("conv", p)])
                        S.op("act", lambda e, lin=lin, kl=kl, n=n: e.copy(lin[:, 0:n], PS[kl][:, 0:n]),
                             reads=[("ps", kl)], writes=[("lin", p)])
                        S.op("dve", lambda e, cv3=cv3, taps=taps, w1=w1: e.scalar_tensor_tensor(cv3, taps[1], w1, cv3, ALU.mult, ALU.add),
                             reads=[("ae", p), ("aeh", p), ("conv", p)], writes=[("conv", p)])
                        S.op("dve", lambda e, cv3=cv3, taps=taps, w0=w0: e.scalar_tensor_tensor(cv3, taps[0], w0, cv3, ALU.mult, ALU.add),
                             reads=[("ae", p), ("aeh", p), ("conv", p)], writes=[("conv", p)])
                        if not sample and bidx == 1:
                            S.op("dve", lambda e, AEp=AEp, j=j: e.tensor_copy(SAVE[:, 2 * j:2 * j + 2], AEp[:, 512:514]),
                                 reads=[("aet", p)], writes=[("save", j)])
                        if sample:
                            S.op("dve", lambda e, AEs=AEs, j=j: e.tensor_copy(
                                SAVES[:, 4 * j:4 * j + 4].rearrange("p (b r) -> p b r", b=2), AEs[:, :, 16:18]),
                                reads=[("ae", p)], writes=[("saves", j)])

                        def tail(p=p, sil_out=sil_out, sil_in=sil_in, j=j, lc0=lc0, n=n, lin=lin, half=half):
                            S.op("act", lambda e: e.activation(sil_out, sil_in, AF.Silu),
                                 reads=[("conv", p), ("ae", p), ("aeh", p)], writes=[("ae", p), ("aeh", p)])
                            S.op("pool", lambda e: e.tensor_tensor(ACTB(j, lc0, n), sil_out, lin[:, 0:n], ALU.mult),
                                 reads=[("ae", p), ("aeh", p), ("lin", p)], writes=[("actb", j, half)])
                        if pend_tail is not None:
                            pend_tail()
                        pend_tail = tail
                if pend_tail is not None:
                    pend_tail()
                    pend_tail = None
                if half == 1:
                    dma("sp", oconv[l], SAVE.rearrange("p (j r) -> p j r", r=2), reads=[("save", j) for j in range(NJ)], key="oconv")
                    dma("sp", osconv[l], SAVES.rearrange("p (j b r) -> p j b r", b=2, r=2), reads=[("saves", j) for j in range(NJ)], key="osconv")
                atoks = [("actb", j, half) for j in range(NJ)]
                nxt = list(range(8, 17)) if half == 0 else []
                for ti_, i in enumerate(tiles):
                    rows = tile_rows(i)
                    c0, n = tcols(i)
                    lc0 = c0 - base
                    defer_b = []
                    for i2 in (nxt[ti_:ti_ + 1] if ti_ < 7 else nxt[7:8]):
                        rows2 = tile_rows(i2)
                        c02, n2 = tcols(i2)
                        defer_b.append(prenorm_tile(X[0:rows2, i2, :], rows2, ("x", i2), prm(l, "gpf", 8), H23(c02 - 1024, n2),
                                                    ("h2", i2 - 8), 0, hb=HBF[0:rows2, :], tpool="t", hb_toks=["hbf"], defer=True))
                        if "nodefer" in DBG:
                            defer_b.pop()()
                    for hf in range(2):
                        kd = psnext("d")
                        mm_group(None, [(PS[kd][0:rows, :], ACTB(j, lc0, n), WDN(j, 512 * hf, 512), j == 0, j == NJ - 1) for j in range(NJ)],
                                 reads=atoks + [("wdn", 0), ("wdn", 1)], writes=[("ps", kd)])
                        if hf == 0:
                            S.op("act", lambda e, kd=kd, rows=rows: e.copy(OSTG[0:rows, 0:512], PS[kd][0:rows, :]),
                                 reads=[("ps", kd)], writes=["ostg0"])
                        else:
                            S.op("dve", lambda e, kd=kd, rows=rows: e.tensor_copy(OSTG[0:rows, 512:1024], PS[kd][0:rows, :]),
                                 reads=[("ps", kd)], writes=["ostg1"])
                    for fb in defer_b:
                        fb()
                    post_norm_residual(OSTG[0:rows, :], rows, i, HBF[0:rows, :], ["hbf"], GPF, ["gpf"], ["ostg0", "ostg1"])
                    if l == L - 1:
                        if i < 16:
                            dma("sp", y_p[128 * i:128 * i + 128, :], X[:, i, :], reads=[("x", i)], key=("x", i))
                        else:
                            dma("sp", y_s, X[0:32, 16, :], reads=[("x", 16)], key=("x", 16))
                if half == 0:
                    prenorm_tile(X[0:32, 16, :], 32, ("x", 16), prm(l, "gpf", 8), H23(1024, 32), ("h2", 8), 0,
                                 hb=HBF[0:32, :], tpool="t", hb_toks=["hbf"])
        RFOX = lambda i, rows=128: SM[0:rows, 100 + i:101 + i]
        RSGU = lambda i, rows=128: SM[0:rows, 120 + i:121 + i]
        RMEM = lambda i, rows=128: SM[0:rows, 140 + i:141 + i]
        SSFOX = lambda i, rows=128: SM[0:rows, 160 + i:161 + i]
        SSMEM = lambda i, rows=128: SM[0:rows, 180 + i:181 + i]
        ALLHT = [("ht", i) for i in range(17)]

        class NSQ:
            def __init__(self):
                self.p1 = None
                self.p2 = None
                self.p3 = None

            def step(self, new_p1):
                if self.p3 is not None:
                    self.p3()
                    self.p3 = None
                if self.p2 is not None:
                    self.p3 = self.p2()
                    self.p2 = None
                if self.p1 is not None:
                    self.p2 = self.p1()
                self.p1 = new_p1

            def flush(self):
                self.step(None)
                self.step(None)
                self.step(None)
        nsq = NSQ()

        def wload(slot, src2d, ncol, extra_writes=()):
            dma("pool", WS3(slot, ncol), src2d.rearrange("(c p) n -> p c n", p=128),
                writes=[("ws", slot)] + list(extra_writes), key=("ws", slot))

        def KAS(b, c0, n, p0=0, p1=67):
            return bf(WS_O[0] + b * 3200 + c0 * 2, n, p0, p1)

        def VAS(b, j):
            return bf(WS_O[0] + b * 3200 + 2080 + j * 130, 65)

        def VAS3(b):
            return AR[:, (WS_O[0] + b * 3200 + 2080) // 2:(WS_O[0] + b * 3200 + 2080) // 2 + 520].rearrange("p (j k) -> p j k", k=65)

        SC_TOKS = [("sc", b, x) for b in range(2) for x in ("k", "v", "r", "one")]

        def norm_store_fm(obank, I, grp, chunk, p0, ssq4, l):
            rlrow = f32(SCR_O + 6 * 1024, 512, 64, 65)
            hfT = f32(SCR_O + 6 * 1024, 512, 0, 64)
            S.op("act", lambda e: e.activation(rlrow, PS[obank][64:65, 0:512], AF.Ln), reads=[("ps", obank)], writes=[("scr", 6, "r")])
            S.op("act", lambda e: e.activation(rlrow, rlrow, AF.Exp, scale=-1.0), reads=[("scr", 6, "r")], writes=[("scr", 6, "r")])
            kb = psnext("g")
            S.op("pe", lambda e: e.matmul(PS[kb][0:64, 0:512], ONEROW[64:65, 0:64], rlrow, start=True, stop=True, skip_group_check=True),
                 reads=[("scr", 6, "r"), "onerow"], writes=[("ps", kb)])
            S.op("dve", lambda e: e.tensor_copy(hfT, PS[kb][0:64, 0:512]), reads=[("ps", kb)], writes=scr(6, 7))
            S.op("dve", lambda e: e.tensor_tensor(hfT, PS[obank][0:64, 0:512], hfT, ALU.mult), reads=[("ps", obank)] + scr(6, 7), writes=scr(6, 7))
            return lambda: norm_store_fm2(I, grp, chunk, p0, ssq4, l)

        def norm_store_fm2(I, grp, chunk, p0, ssq4, l):
            hfT = f32(SCR_O + 6 * 1024, 512, 0, 64)
            sqb = bf(SCR_O + 4 * 1024, 512, 0, 64)
            S.op("act", lambda e: e.activation(MX(chunk, 512 * I, 512, p0, p0 + 64), hfT, AF.Copy,
                                               scale=prm(l, "ggo", 1, p0, p0 + 64, chunk)),
                 reads=scr(6, 7) + ["prm"], writes=[("mx", chunk, 4 * I + s_, p0) for s_ in range(4)])
            S.op("pool", lambda e: e.tensor_tensor(sqb, hfT, hfT, ALU.mult), reads=scr(6, 7), writes=scr(4))
            return lambda: norm_store_fm3(I, grp, ssq4)

        def norm_store_fm3(I, grp, ssq4):
            sqb = bf(SCR_O + 4 * 1024, 512, 0, 64)
            kt = psnext("t")
            ones_col = bf(VA_O + 64 * 2, 1, 0, 64)

            def ssmm(e):
                ins = None
                for s_ in range(4):
                    ins = e.matmul(PS[kt][:, s_:s_ + 1], sqb[:, 128 * s_:128 * s_ + 128], ones_col, start=True, stop=True,
                                   skip_group_check=True)
                return ins
            S.op("pe", ssmm, reads=scr(4) + ["va_ones"], writes=[("ps", kt)])
            toks4 = [("ssq", grp, 4 * I + s_) for s_ in range(4)]
            S.op("dve", lambda e: e.tensor_tensor(ssq4, ssq4, PS[kt][:, 0:4], ALU.add), reads=[("ps", kt)] + toks4, writes=toks4)

        def norm_and_store_block(obank, I, grp, chunk, p0, ssq4, l):
            O4 = PS[obank][:, 0:260].rearrange("p (s k) -> p s k", k=65)
            rl4 = SM[:, SM_T + 10:SM_T + 14]
            sq4 = SM[:, SM_T + 14:SM_T + 18]
            S.op("dve", lambda e: e.reciprocal(rl4.unsqueeze(2), O4[:, :, 64:65]), reads=[("ps", obank)], writes=["rl4"])
            hf4 = SCRf(6, 256)
            hb4 = SCRb(7, 256)
            S.op("dve", lambda e: e.tensor_tensor(hf4.rearrange("p (s k) -> p s k", k=64), O4[:, :, 0:64],
                                                  rl4.unsqueeze(2).to_broadcast([128, 4, 64]), ALU.mult),
                 reads=[("ps", obank), "rl4"], writes=scr(6))

            def sqs(e):
                ins = None
                for s_ in range(4):
                    ins = e.activation(hb4[:, 64 * s_:64 * s_ + 64], hf4[:, 64 * s_:64 * s_ + 64], AF.Square,
                                       accum_out=sq4[:, s_:s_ + 1])
                return ins
            S.op("act", sqs, reads=scr(6), writes=scr(7) + ["sq4"])
            toks4 = [("ssq", grp, 4 * I + s_) for s_ in range(4)]
            S.op("dve", lambda e: e.tensor_tensor(ssq4, ssq4, sq4, ALU.add), reads=["sq4"] + toks4, writes=toks4)
            S.op("act", lambda e: e.copy(hb4, hf4), reads=scr(6, 7), writes=scr(7))
            kt = psnext("t")

            def tr(e):
                ins = None
                for s_ in range(4):
                    ins = e.transpose(PSB[kt][p0:p0 + 64, 128 * s_:128 * s_ + 128], hb4[:, 64 * s_:64 * s_ + 64], IDENT[:, :])
                return ins
            S.op("pe", tr, reads=scr(7) + ["cst"], writes=[("ps", kt)])
            S.op("dve", lambda e: e.tensor_scalar(MX(chunk, 512 * I, 512, p0, p0 + 64), PSB[kt][p0:p0 + 64, 0:512],
                                                  prm(l, "ggo", 1, p0, p0 + 64, chunk), None, ALU.mult),
                 reads=[("ps", kt), "prm"], writes=[("mx", chunk, 4 * I + s_, p0) for s_ in range(4)])

        def norm_and_store(obank, ocol, rows, tile_i, grp, chunk, p0, ssq_acc, l):
            rl = SM[0:rows, SM_T + 4:SM_T + 5]
            S.op("dve", lambda e: e.reciprocal(rl, PS[obank][0:rows, ocol + 64:ocol + 65]),
                 reads=[("ps", obank)], writes=["rl"])
            hf = SCRf(5, 64, 0, rows)
            hb = SCRb(5, 64, 0, rows, off=256)
            S.op("dve", lambda e: e.tensor_scalar(hf, PS[obank][0:rows, ocol:ocol + 64], rl, None, ALU.mult),
                 reads=[("ps", obank), "rl"], writes=scr(5))
            sq = SM[0:rows, SM_T + 5:SM_T + 6]
            S.op("act", lambda e: e.activation(hb, hf, AF.Square, accum_out=sq), reads=scr(5), writes=scr(5) + ["sq"])
            S.op("dve", lambda e: e.tensor_tensor(ssq_acc, ssq_acc, sq, ALU.add),
                 reads=["sq", ("ssq", grp, tile_i)], writes=[("ssq", grp, tile_i)])
            S.op("act", lambda e: e.copy(hb, hf), reads=scr(5), writes=scr(5))
            kt = psnext("t")
            c0, n = tcols(tile_i)
            S.op("pe", lambda e: e.transpose(PSB[kt][p0:p0 + 64, 0:rows], hb, IDENT[0:rows, 0:rows]),
                 reads=scr(5) + ["cst"], writes=[("ps", kt)])
            S.op("dve", lambda e: e.tensor_scalar(MX(chunk, c0, n, p0, p0 + 64), PSB[kt][p0:p0 + 64, 0:rows],
                                                  prm(l, "ggo", 1, p0, p0 + 64, chunk), None, ALU.mult),
                 reads=[("ps", kt), "prm"], writes=[("mx", chunk, tile_i, p0)])

        try:
            for l in range(L):
                psr.clear()
                psr.update({"g": [0, 1], "s": [2, 3, 4], "o": [5, 6], "t": [7], "w": [0, 1, 2, 3, 4, 5]})
                for kk in psr:
                    psi[kk] = 0
                S.op("dve", lambda e: e.memset(
                    AR[:, VA_O // 2: VA_O // 2 + 17 * 8 * 65].rearrange("p (a k) -> p a k", k=65)[:, :, 64:65], 1.0),
                    writes=["va_ones"])
                S.op("dve", lambda e: e.memset(MVAt[:, :].rearrange("p (a k) -> p a k", k=65)[:, :, 64:65], 1.0),
                     writes=["mva_ones"])
                S.op("dve", lambda e: e.memset(MKTt[64:128, :], 0.0), writes=["mkt_zero"])
                wload(0, w_in[l][:, 1024:1536], 512, extra_writes=SC_TOKS if l > 0 else ())
                wload(1, w_in[l][:, 512:1024], 512)
                wload(2, w_in[l][:, 1536:2056], 520)

                SGC = MX_O + 12288
                WSTl = bf(SGC, 512)
                GSl = f32(SGC + 1024, 256)
                dma("pool", WSTl.rearrange("p (g i) -> p g i", g=4), wsT[l].rearrange("g j i -> j g i"),
                    writes=[("sguc", 0)], key="wst")
                S.op("dve", lambda e: e.memset(bf(SGC, 512, 64, 128).rearrange("p (g i) -> p g i", g=4)[:, :, 0:64], 0.0),
                     reads=[("sguc", 0)], writes=[("sguc", 0)])
                dma("sp", GSl, g_sgu[l:l + 1, :].broadcast_to([128, 256]), writes=[("sguc", 1)], key="gs")

                def sgu_ops(i):
                    st_ = 0 if "setA" in DBG else i % 2
                    base = MX_O if st_ == 0 else MX_O + 6144
                    tk = "sguA" if st_ == 0 else "sguB"
                    Sb = lambda g, n, p0=0, p1=128: bf(base + g * 1024, n, p0, p1)
                    Sf = lambda g, n, p0=0, p1=128: f32(base + g * 1024, n, p0, p1)
                    sc = lambda *gs: [(tk, g) for g in gs]
                    rows = tile_rows(i)
                    c0, n = tcols(i)
                    s1, s2 = [], []
                    k = psnext("s")
                    z = Sf(0, 512, 0, rows)
                    t1 = Sf(2, 512, 0, rows)
                    ssv = SM[0:rows, SM_T + 20 + 2 * st_:SM_T + 21 + 2 * st_]
                    sss = SM[0:rows, SM_T + 21 + 2 * st_:SM_T + 22 + 2 * st_]
                    ssvt, ssst = ("ssv", st_), ("sss", st_)
                    vvf = Sf(3, 256, 0, rows)
                    vvb = Sb(2, 256, 0, rows)
                    sg = Sf(4, 256, 0, rows)
                    sgb = Sb(5, 256, 0, rows)
                    s1.append(lambda: mm_group(None, [(PS[k][0:rows, :], HT(c, c0, n), WS(2, c, 8, 512), c == 0, c == 7) for c in range(8)],
                                               reads=[("ht", i), ("ws", 2)], writes=[("ps", k)]))
                    s1.append(lambda: S.op("dve", lambda e: e.tensor_copy(z, PS[k][0:rows, :]), reads=[("ps", k)], writes=sc(0, 1)))
                    s1.append(lambda: S.op("dve", lambda e: e.tensor_tensor(t1, z, z, ALU.mult), reads=sc(0, 1), writes=sc(2, 3)))
                    s1.append(lambda: S.op("dve", lambda e: e.tensor_scalar(t1, t1, 0.044715, 1.0, ALU.mult, ALU.add), reads=sc(2, 3), writes=sc(2, 3)))
                    s1.append(lambda: S.op("dve", lambda e: e.tensor_tensor(t1, t1, z, ALU.mult), reads=sc(0, 1, 2, 3), writes=sc(2, 3)))
                    s1.append(lambda: S.op("act", lambda e: e.activation(t1, t1, AF.Exp, scale=-1.5957691216057308), reads=sc(2, 3), writes=sc(2, 3)))
                    s1.append(lambda: S.op("act", lambda e: e.activation(t1, t1, AF.Ln, bias=1.0, scale=1.0), reads=sc(2, 3), writes=sc(2, 3)))
                    s1.append(lambda: S.op("act", lambda e: e.activation(t1, t1, AF.Exp, scale=-1.0), reads=sc(2, 3), writes=sc(2, 3)))
                    s1.append(lambda: S.op("dve", lambda e: e.tensor_tensor(z, z, t1, ALU.mult), reads=sc(0, 1, 2, 3), writes=sc(0, 1)))
                    s1.append(lambda: S.op("act", lambda e: e.activation(t1[:, 0:256], z[:, 256:512], AF.Square, accum_out=ssv),
                                           reads=sc(0, 1), writes=sc(2) + [ssvt]))
                    s1.append(lambda: S.op("act", lambda e: e.activation(ssv, ssv, AF.Ln, bias=EPS, scale=1.0 / 256), reads=[ssvt], writes=[ssvt]))
                    s1.append(lambda: S.op("act", lambda e: e.activation(ssv, ssv, AF.Exp, scale=-0.5), reads=[ssvt], writes=[ssvt]))
                    s1.append(lambda: S.op("dve", lambda e: e.scalar_tensor_tensor(vvf, z[:, 256:512], ssv, GSl[0:rows, :], ALU.mult, ALU.mult),
                                           reads=sc(0, 1) + [("sguc", 1)] + [ssvt], writes=sc(3)))
                    s1.append(lambda: S.op("dve", lambda e: e.tensor_copy(vvb, vvf), reads=sc(3), writes=sc(2)))
                    if i == 16:
                        s1.append(lambda: dma("sp", ogv[l], vvf, reads=sc(3), key="ogv"))
                    k2 = psnext("s")
                    if i < 16:
                        items = [(PS[k2][0:128, 64 * g:64 * g + 64], WSTl[:, 128 * g:128 * g + 128], vvb[:, 64 * g:64 * g + 64], True, True) for g in range(4)]
                        bsap = prm(l, "bs", 4)
                        wtok = [("sguc", 0)]
                    else:
                        items = [(PS[k2][0:32, 64 * g:64 * g + 64], WSTS(l, g), vvb[:, 64 * g:64 * g + 64], True, True) for g in range(4)]
                        bsap = prm(l, "bss", 4, 0, 32)
                        wtok = ["wsts"] + [("wsts", l, b) for b in range(2)]
                    s1.append(lambda: mm_group(None, items, reads=sc(2) + wtok, writes=[("ps", k2)]))
                    s2.append(lambda: S.op("dve", lambda e: e.tensor_tensor(
                        sg.rearrange("p (g d) -> p g d", g=4), PS[k2][0:rows, 0:256].rearrange("p (g d) -> p g d", g=4),
                        bsap.unsqueeze(2).to_broadcast([rows, 4, 64]), ALU.add),
                        reads=[("ps", k2), "prm"], writes=sc(4)))
                    s2.append(lambda: S.op("dve", lambda e: e.tensor_tensor(sg, sg, z[:, 0:256], ALU.mult), reads=sc(0, 1, 4), writes=sc(4)))
                    s2.append(lambda: S.op("act", lambda e: e.activation(sgb, sg, AF.Square, accum_out=sss),
                                           reads=sc(4), writes=sc(5) + [ssst]))
                    s2.append(lambda: S.op("act", lambda e: e.activation(sss, sss, AF.Ln, bias=EPS, scale=1.0 / 256), reads=[ssst], writes=[ssst]))
                    s2.append(lambda: S.op("act", lambda e: e.activation(RSGU(i, rows), sss, AF.Exp, scale=-0.5),
                                           reads=[ssst], writes=[("rsgu", i)]))
                    s2.append(lambda: S.op("dve", lambda e: e.tensor_copy(sgb, sg), reads=sc(4, 5), writes=sc(5)))

                    def trs():
                        kt = psnext("o")

                        def tr(e):
                            ins = None
                            for cc in range(2):
                                ins = e.transpose(PSB[kt][:, cc * 128:cc * 128 + rows], sgb[:, cc * 128:(cc + 1) * 128], IDENT[0:rows, 0:rows])
                            return ins
                        S.op("pe", tr, reads=sc(5) + ["cst"], writes=[("ps", kt)])
                        for cc in range(2):
                            S.op("dve", lambda e, cc=cc, l=l: e.tensor_scalar(
                                MX(4 + cc, c0, n), PSB[kt][:, cc * 128:cc * 128 + rows], prm(l, "ggo", 1, 0, 128, 4 + cc), None, ALU.mult),
                                reads=[("ps", kt), "prm"], writes=[("mx", 4 + cc, i)])
                    s2.append(trs)
                    return s1, s2

                def interleave(a, b):
                    for q in range(max(len(a), len(b))):
                        if q < len(a):
                            a[q]()
                        if q < len(b):
                            b[q]()

                def vk_tile(i):
                    rows = tile_rows(i)
                    c0, n = tcols(i)
                    for which, slot, oprompt, osample, sg in (("v", 0, ofv, osv, 4), ("k", 1, ofk, osk, 6)):
                        k = psnext("g")
                        mm_group(None, [(PS[k][0:rows, :], HT(c, c0, n), WS(slot, c, 0, 512), c == 0, c == 7)
                                        for c in range(8)],
                                 reads=[("ht", i), ("ws", slot)], writes=[("ps", k)])
                        stg = SCRf(sg, 512, 0, rows)
                        S.op(("act" if which == "v" else "dve"), (lambda e, stg=stg, k=k, rows=rows: e.copy(stg, PS[k][0:rows, :])) if which == "v"
                             else (lambda e, stg=stg, k=k, rows=rows: e.tensor_copy(stg, PS[k][0:rows, :])),
                             reads=[("ps", k)], writes=scr(sg, sg + 1))
                        if which == "v":
                            va3 = AR[0:rows, VA_O // 2 + i * 520: VA_O // 2 + (i + 1) * 520].rearrange(
                                "p (h k) -> p h k", k=65)[:, :, 0:64]
                            S.op("dve", lambda e, va3=va3, k=k, rows=rows: e.tensor_copy(
                                va3, PS[k][0:rows, :].rearrange("p (h k) -> p h k", k=64)),
                                reads=[("ps", k), "va_ones"], writes=[("va", i)])
                        dst = oprompt[l, 128 * i:128 * i + 128, :] if i < 16 else osample[l, :, :]
                        dma("sp", dst, stg, reads=scr(sg, sg + 1), key=("stg", sg))

                sgu_pipe = {"B": [], "C": []}

                def sgu_step(t):
                    if t is not None:
                        s1_, s2_ = sgu_ops(t)
                        sA, sB, sC = s1_[:-1], [s1_[-1]] + s2_[:-1], [s2_[-1]]
                    else:
                        sA, sB, sC = [], [], []
                    interleave(sA, sgu_pipe["B"])
                    for f_ in sgu_pipe["C"]:
                        f_()
                    sgu_pipe["C"] = sgu_pipe["B_c"] if "B_c" in sgu_pipe else []
                    sgu_pipe["B"] = sB
                    sgu_pipe["B_c"] = sC

                def pn_a(i):
                    rows = tile_rows(i)
                    c0, n = tcols(i)
                    return prenorm_tile(X[0:rows, i, :], rows, ("x", i), prm(l, "gpm", 8), HT3(c0, n), ("ht", i), i % 2, defer=True)
                pn_b = {0: pn_a(0)}
                for i in range(17):
                    if i + 1 < 17:
                        pn_b[i + 1] = pn_a(i + 1)
                    pn_b.pop(i)()
                    if i >= 1:
                        vk_tile(i - 1)
                        sgu_step(i - 1)
                vk_tile(16)
                sgu_step(16)
                sgu_step(None)
                sgu_step(None)
                S.op("dve", lambda e: e.memset(bf(MX_O + 15 * 1024, 2), 0.0),
                     writes=[("sguA", g) for g in range(6)] + [("sguB", g) for g in range(6)] + [("sguc", 0), ("sguc", 1)]
                     + [("mx", c, i, p0) for c in (0, 1, 2, 3) for i in range(17) for p0 in (0, 64)])

                _chk("vk%d" % l)
                NB = SM[0:8, 210:211]
                S.op("dve", lambda e, l=l: e.tensor_scalar(NB, prm(l, "bf", 1, 0, 8), -1.0, None, ALU.mult),
                     reads=["prm"], writes=["cw", "nb"])
                for bi, (c0, n) in enumerate(BLKS):
                    k = psnext("g")
                    mm_group(None, [(PS[k][0:8, 0:n], WS(2, c, 0, 8), HT(c, c0, n), c == 0, c == 7) for c in range(8)],
                             reads=ALLHT + [("ws", 2)], writes=["cw", ("ps", k)])
                    tmp = SCRf(0, 512, 96, 104)
                    S.op("act", lambda e, k=k, n=n, tmp=tmp: e.activation(tmp[:, 0:n], PS[k][0:8, 0:n], AF.Exp, bias=NB, scale=-1.0),
                         reads=[("ps", k), "nb"], writes=scr(0, 1))
                    S.op("act", lambda e, n=n, tmp=tmp: e.activation(tmp[:, 0:n], tmp[:, 0:n], AF.Ln, bias=1.0, scale=1.0),
                         reads=scr(0, 1), writes=scr(0, 1))
                    S.op("dve", lambda e, c0=c0, n=n, tmp=tmp: e.tensor_scalar(CT(c0, n), tmp[:, 0:n], -1.0, None, ALU.mult),
                         reads=scr(0, 1), writes=["cw", ("ct", bi)])
                dma("sp", olf[l], CT(0, 2048), reads=[("ct", b) for b in range(4)], writes=["cw"], key="olf")
                dma("sp", oslf[l], CT(2048, 32), reads=[("ct", 4)], writes=["cw"], key="olf2")
                SLF = SM[96:104, 220:252]
                S.op("dve", lambda e: e.tensor_copy(SLF, CT(2048, 32)), reads=[("ct", 4)], writes=["cw", "slf"])
                S.op("dve", lambda e: e.tensor_tensor_scan(CT(0, 2048), ONE8.to_broadcast([8, 2048]), CT(0, 2048), 0.0, ALU.mult, ALU.add),
                     reads=[("ct", b) for b in range(4)] + ["one"], writes=["cw", "ctp"])
                S.op("dve", lambda e: e.tensor_copy(HI(0, 2048), CT(0, 2048)), reads=["ctp"], writes=["cw", "hi"])
                S.op("dve", lambda e: e.tensor_tensor(CT(0, 2048), CT(0, 2048), HI(0, 2048), ALU.subtract),
                     reads=["ctp", "hi"], writes=["cw", "ctp"])
                S.op("dve", lambda e: e.tensor_copy(LO(0, 2048), CT(0, 2048)), reads=["ctp"], writes=["cw", "lo"])
                dma("sp", cscr_p[:, 0, :], HI(0, 2048), reads=["hi"], writes=["cw", "cscr_p"], key="cscr0")
                dma("sp", cscr_p[:, 1, :], LO(0, 2048), reads=["lo"], writes=["cw", "cscr_p2"], key="cscr1")
                CS = lambda b, c0, n: CT(b * 1040 + c0, n)
                for b in range(2):
                    dma("sp", CS(b, 0, 1024), clfT[l, b], reads=["ctp", "lo", ("ct", 4), "slf"], writes=["cw", ("cs", b)], key=("csl", b))
                    S.op("dve", lambda e, b=b: e.tensor_copy(CS(b, 1024, 16), SM[96:104, 220 + 16 * b:236 + 16 * b]),
                         reads=["slf", "ctp", "lo", ("ct", 4)], writes=["cw", ("cs2", b)])
                    S.op("dve", lambda e, b=b: e.tensor_tensor_scan(CS(b, 0, 1040), ONE8.to_broadcast([8, 1040]), CS(b, 0, 1040), 0.0, ALU.mult, ALU.add),
                         reads=[("cs", b), ("cs2", b), "one"], writes=["cw", ("csc", b)])
                S.op("dve", lambda e: e.tensor_copy(HI(0, 2080), CT(0, 2080)),
                     reads=[("csc", 0), ("csc", 1), "cscr_p", "cscr_p2"], writes=["cw", "hi"])
                S.op("dve", lambda e: e.tensor_tensor(CT(0, 2080), CT(0, 2080), HI(0, 2080), ALU.subtract),
                     reads=["hi"], writes=["cw", ("csc", 0), ("csc", 1)])
                S.op("dve", lambda e: e.tensor_copy(LO(0, 2080), CT(0, 2080)), reads=[("csc", 0), ("csc", 1), "cscr_p2"], writes=["cw", "lo"])
                for b in range(2):
                    dma("sp", cscr_s[:, 0, b, :], HI(b * 1040, 1040), reads=["hi"], writes=["cw", ("cscr_s", b, 0)], key=("cscrs", b, 0))
                    dma("sp", cscr_s[:, 1, b, :], LO(b * 1040, 1040), reads=["lo"], writes=["cw", ("cscr_s", b, 1)], key=("cscrs", b, 1))

                _chk("fg%d" % l)
                _chk("sgu%d" % l)
                S.op("dve", lambda e: e.memset(bf(QK_O, 4 * NT, 64, 128), 0.0),
                     writes=["cw", "hi", "lo"] + [("qkc", s_) for s_ in range(4)])
                for s_ in (0, 2):
                    S.op("dve", lambda e, s_=s_: e.memset(QK(s_, 0, NT, 64, 67), -1.0), writes=[("qkc", s_)])
                for s_ in (1, 3):
                    S.op("dve", lambda e, s_=s_: e.memset(QK(s_, 0, NT, 64, 65), 1.0), writes=[("qkc", s_)])
                wload(0, w_mkv[l], 512)
                MEMT3 = AR[:, (SCR_O + 4096) // 2:(SCR_O + 4096) // 2 + 2048].rearrange("p (c t) -> p c t", c=8)
                MEMT = lambda c, c0, n: bf(SCR_O + 4096 + (c * 256 + c0) * 2, n)
                memx = f32(WS_O[2], 1024)
                for mt in range(2):
                    dma("sp", memx, mem[128 * mt:128 * mt + 128, :], writes=[("ws", 2)], key="memx")
                    prenorm_tile(memx, 128, ("ws", 2), prm(l, "gmem", 8), MEMT3[:, :, 128 * mt:128 * mt + 128],
                                 ("memt", mt), 0, extra_writes=scr(4, 5, 6, 7))
                memt_toks = [("memt", 0), ("memt", 1)] + scr(4, 5, 6, 7)
                stgk = f32(WS_O[2], 512)
                for mt in range(2):
                    k = psnext("g")
                    mm_group(None, [(PS[k][:, :], MEMT(c, 128 * mt, 128), WS(0, c, 0, 512), c == 0, c == 7) for c in range(8)],
                             reads=memt_toks + [("ws", 0)], writes=[("ps", k)])
                    S.op("act", lambda e, k=k: e.copy(stgk, PS[k][:, :]), reads=[("ps", k)], writes=[("ws", 2)])
                    S.op("dve", lambda e, k=k, mt=mt: e.tensor_copy(
                        MVAt[:, mt * 260:(mt + 1) * 260].rearrange("p (h k) -> p h k", k=65)[:, :, 0:64],
                        PS[k][:, 256:512].rearrange("p (h k) -> p h k", k=64)),
                        reads=[("ps", k), "mva_ones"], writes=[("mva", mt)])
                    dma("sp", omk[l, 128 * mt:128 * mt + 128, :], stgk[:, 0:256], reads=[("ws", 2)], key="omk")
                    dma("sp", omv[l, 128 * mt:128 * mt + 128, :], stgk[:, 256:512], reads=[("ws", 2)], key="omv")
                for pr in range(2):
                    k = psnext("g")
                    mm_group(None, [(PS[k][:, 0:256], WS(0, c, 128 * pr, 128), MEMT(c, 0, 256), c == 0, c == 7) for c in range(8)],
                             reads=memt_toks + [("ws", 0)], writes=[("ps", k)])
                    for hh in range(2):
                        h = 2 * pr + hh
                        S.op("act", lambda e, k=k, hh=hh, h=h: e.copy(MKTt[0:64, h * 256:(h + 1) * 256], PS[k][64 * hh:64 * hh + 64, 0:256]),
                             reads=[("ps", k)], writes=[("mkt", h)])
                wload(2, w_in[l][:, 2056:2312], 256)
                for b in range(2):
                    S.op("dve", lambda e, b=b: e.memset(KAS(b, 0, 1040, 64, 65), 1.0), reads=[("ws", 0)], writes=[("ws", 0), ("sc", b, "one")])
                    S.op("dve", lambda e, b=b: e.memset(VAS3(b)[:, :, 64:65], 1.0), reads=[("ws", 0)], writes=[("ws", 0), ("sc", b, "one")])

                S.op("dve", lambda e: e.memset(SM[:, 160:200], 0.0),
                     writes=[("ssq", g, i) for g in (0, 2) for i in range(17)])

                _chk("memkv%d" % l)
                for pr in range(2):
                    for bi, (c0, n) in enumerate(BLKS):
                        k = psnext("g")
                        mm_group(None, [(PS[k][:, 0:n], WS(2, c, 128 * pr, 128), HT(c, c0, n), c == 0, c == 7) for c in range(8)],
                                 reads=ALLHT + [("ws", 2)], writes=[("ps", k)])
                        for hh in range(2):
                            S.op("act", lambda e, k=k, hh=hh, c0=c0, n=n: e.activation(
                                QK(2 * hh, c0, n, 0, 64), PS[k][64 * hh:64 * hh + 64, 0:n], AF.Copy, scale=0.125),
                                reads=[("ps", k)], writes=[("qa", hh, bi)])
                    for hh in range(2):
                        h = 2 * pr + hh
                        for I in range(4):
                            ob = psnext("o")
                            kts = [dict(ka=MKTt[0:128, h * 256 + 128 * j:h * 256 + 128 * j + 128], nk=128,
                                        va=bf(MVA_O + ((j * 4 + h) * 65) * 2, 128), lo=0, hi=512, pv_lo=0,
                                        mask=None, pt="rot", reads=[("mkt", h), ("qa", hh, I), "mkt_zero", ("qkc", 0), ("qkc", 2)] + [("qar", hh, x_) for x_ in range(3)],
                                        vreads=[("mva", j), "mva_ones"])
                                   for j in range(2)]
                            osubs = [(PS[ob][:, 65 * s:65 * s + 65], 128 * s, 128 * s + 128, 0, 1) for s in range(4)]
                            attend(lambda c0, n, hh=hh, I=I: QK(2 * hh, 512 * I + c0, n, 0, 128), 128, kts, "fm", ob, "mem")
                            nsq.step(lambda ob=ob, I=I, pr=pr, hh=hh: norm_store_fm(ob, I, 2, 6 + pr, 64 * hh, SM[:, 180 + 4 * I:184 + 4 * I], l))
                        ob = psnext("g")
                        kts = []
                        for b in range(2):
                            dma("pool", KAS(b, 0, 256, 0, 64), cmkT[l, b, h], reads=[("ws", 0)], writes=[("sc", b, "k")], key=("sck", b))
                            dma("pool", VAS3(b)[:, 0:2, 0:64],
                                cmv[l, b].rearrange("(j p) (h d) -> p j h d", p=128, h=4)[:, :, h, :],
                                reads=[("ws", 0)], writes=[("sc", b, "v")], key=("scv", b), slow=True)
                            for j in range(2):
                                kts.append(dict(ka=KAS(b, 128 * j, 128, 0, 64), nk=128, va=VAS(b, j),
                                                lo=16 * b, hi=16 * b + 16, pv_lo=0, mask=None, pt=3 * b + (j % 3),
                                                reads=[("sc", b, "k"), ("qa", hh, 4)], vreads=[("sc", b, "v"), ("sc", b, "one")]))
                        osubs = [(PS[ob][0:32, 0:65], 0, 32, 0, len(kts) - 1)]
                        attend(lambda c0, n, hh=hh: QK(2 * hh, 2048 + c0, n, 0, 64), 64, kts, osubs, ob, "mems")
                        flush_pv()
                        norm_and_store(ob, 0, 32, 16, 2, 6 + pr, 64 * hh, SSMEM(16, 32), l)

                flush_pv()
                nsq.flush()
                _chk("memattn%d" % l)
                wload(2, w_in[l][:, 0:512], 512)
                for pr in range(4):
                    for bi, (c0, n) in enumerate(BLKS):
                        for qk, slot in ((0, 2), (1, 1)):
                            k = psnext("g")
                            mm_group(None, [(PS[k][:, 0:n], WS(slot, c, 128 * pr, 128), HT(c, c0, n), c == 0, c == 7) for c in range(8)],
                                     reads=ALLHT + [("ws", slot)], writes=[("ps", k)])
                            for hh in range(2):
                                if qk == 0:
                                    S.op("act", lambda e, k=k, hh=hh, c0=c0, n=n: e.activation(
                                        QK(2 * hh, c0, n, 0, 64), PS[k][64 * hh:64 * hh + 64, 0:n], AF.Copy, scale=0.125),
                                        reads=[("ps", k)], writes=[("qa", hh, bi)])
                                else:
                                    S.op("dve", lambda e, k=k, hh=hh, c0=c0, n=n: e.tensor_copy(
                                        QK(2 * hh + 1, c0, n, 0, 64), PS[k][64 * hh:64 * hh + 64, 0:n]),
                                        reads=[("ps", k)], writes=[("ka", hh, bi)])
                    if pr == 3:
                        wload(2, w_out[l][:, 0:512], 512)
                        wload(1, w_out[l][:, 512:1024], 512)
                    for hh in range(2):
                        h = 2 * pr + hh
                        dma("sp", QK(2 * hh, 0, 2048, 64, 65), cscr_p[h:h + 1, 0, :], reads=["cscr_p", ("qkc", 2 * hh)],
                            writes=[("qar", hh, 0)], key=("qar", hh))
                        dma("sp", QK(2 * hh + 1, 0, 2048, 65, 66), cscr_p[h:h + 1, 0, :], reads=["cscr_p", ("qkc", 2 * hh + 1)],
                            writes=[("kar", hh, 0)], key=("kar", hh))
                        dma("sp", QK(2 * hh + 1, 0, 2048, 66, 67), cscr_p[h:h + 1, 1, :], reads=["cscr_p2"],
                            writes=[("kar", hh, 1)], key=("kar", hh))
                        for b in range(2):
                            dma("sp", QK(2 * hh, 2048 + 16 * b, 16, 64, 65), cscr_s[h:h + 1, 0, b, 1024:1040],
                                reads=[("cscr_s", b, 0)], writes=[("qar", hh, 1 + b)], key=("qar", hh))
                            dma("sp", QK(2 * hh + 1, 2048 + 16 * b, 16, 65, 66), cscr_s[h:h + 1, 0, b, 1024:1040],
                                reads=[("cscr_s", b, 0)], writes=[("kar", hh, 2 + b)], key=("kar", hh))
                            dma("sp", QK(2 * hh + 1, 2048 + 16 * b, 16, 66, 67), cscr_s[h:h + 1, 1, b, 1024:1040],
                                reads=[("cscr_s", b, 1)], writes=[("kar", hh, 4 + b)], key=("kar", hh))
                        qar = [("qar", hh, x) for x in range(3)] + [("qkc", 2 * hh)]
                        kar = [("kar", hh, x) for x in range(6)] + [("qkc", 2 * hh + 1)]
                        for I in range(4):
                            ob = psnext("o")
                            kts = []
                            for j in range(4 * I + 4):
                                a = j - 4 * I
                                kts.append(dict(ka=QK(2 * hh + 1, 128 * j, 128, 0, 128), nk=128, va=VA128(j, h),
                                                lo=(128 * a if a >= 0 else 0), hi=512, pv_lo=(128 * a if a >= 0 else 0),
                                                mask=((MASKNEG, 128) if a >= 0 else None), pt="rot",
                                                reads=[("ka", hh, j // 4), ("qa", hh, I)] + qar + kar,
                                                vreads=[("va", j), "va_ones"]))
                            osubs = [(PS[ob][:, 65 * s:65 * s + 65], 128 * s, 128 * s + 128, 0, 4 * I + s) for s in range(4)]
                            attend(lambda c0, n, hh=hh, I=I: QK(2 * hh, 512 * I + c0, n, 0, 128), 128, kts, "fm", ob, "fox")
                            nsq.step(lambda ob=ob, I=I, pr=pr, hh=hh: norm_store_fm(ob, I, 0, pr, 64 * hh, SM[:, 160 + 4 * I:164 + 4 * I], l))
                        ob = psnext("g")
                        kts = []
                        for b in range(2):
                            dma("pool", KAS(b, 0, 1024, 0, 64), ckT[l, b, h], reads=[("ws", 0)], writes=[("sc", b, "k")], key=("sck", b))
                            dma("pool", VAS3(b)[:, :, 0:64],
                                cv[l, b].rearrange("(j p) (h d) -> p j h d", p=128, h=8)[:, :, h, :],
                                reads=[("ws", 0)], writes=[("sc", b, "v")], key=("scv", b), slow=True)
                            dma("sp", KAS(b, 0, 1024, 65, 66), cscr_s[h:h + 1, 0, b, 0:1024], reads=[("cscr_s", b, 0), ("ws", 0)],
                                writes=[("sc", b, "r")], key=("scr", b))
                            dma("sp", KAS(b, 0, 1024, 66, 67), cscr_s[h:h + 1, 1, b, 0:1024], reads=[("cscr_s", b, 1), ("ws", 0)],
                                writes=[("sc", b, "r2")], key=("scr", b))
                            for j in range(8):
                                kts.append(dict(ka=KAS(b, 128 * j, 128), nk=128, va=VAS(b, j),
                                                lo=16 * b, hi=16 * b + 16, pv_lo=0, mask=None, pt=3 * b + (j % 3),
                                                reads=[("sc", b, "k"), ("sc", b, "r"), ("sc", b, "r2"), ("sc", b, "one"), ("qa", hh, 4)] + qar,
                                                vreads=[("sc", b, "v"), ("sc", b, "one")]))
                        kts.append(dict(ka=QK(2 * hh + 1, 2048, 32, 0, 67), nk=32, va=VA(16, h, 32), lo=0, hi=32, pv_lo=0,
                                        mask=(MASKS, 32), pt=6, reads=[("ka", hh, 4), ("qa", hh, 4)] + qar + kar,
                                        vreads=[("va", 16), "va_ones"]))
                        osubs = [(PS[ob][0:32, 0:65], 0, 32, 0, len(kts) - 1)]
                        attend(lambda c0, n, hh=hh: QK(2 * hh, 2048 + c0, n, 0, 67), 67, kts, osubs, ob, "foxs")
                        flush_pv()
                        norm_and_store(ob, 0, 32, 16, 0, pr, 64 * hh, SSFOX(16, 32), l)

                flush_pv()
                nsq.flush()
                _chk("fox%d" % l)
                S.op("act", lambda e: e.activation(SM[:, 100:117], SM[:, 160:177], AF.Ln, bias=EPS, scale=1.0 / 512),
                     reads=[("ssq", 0, i) for i in range(17)], writes=["rfox"])
                S.op("act", lambda e: e.activation(SM[:, 100:117], SM[:, 100:117], AF.Exp, scale=-0.5), reads=["rfox"], writes=["rfox"])
                S.op("act", lambda e: e.activation(SM[:, 140:157], SM[:, 180:197], AF.Ln, bias=EPS, scale=1.0 / 256),
                     reads=[("ssq", 2, i) for i in range(17)], writes=["rmem"])
                S.op("act", lambda e: e.activation(SM[:, 140:157], SM[:, 140:157], AF.Exp, scale=-0.5), reads=["rmem"], writes=["rmem"])

                GP = SCRf(4, 1024)
                dma("sp", GP, g_post_mix[l:l + 1, :].broadcast_to([128, D]), writes=scr(4, 5, 6, 7), key="gp")
                for i in range(17):
                    rows = tile_rows(i)
                    c0, n = tcols(i)
                    ostg = SCRf(0, 1024, 0, rows)
                    mxr = [("mx", c, i, p0) for c in (0, 1, 2, 3, 6, 7) for p0 in (0, 64)] + [("mx", 4, i), ("mx", 5, i)]
                    last_b = None
                    for hf in range(2):
                        banks = [psnext("w") for _ in range(3)]
                        for gi, (cs, b) in enumerate(zip(((0, 1, 2, 3), (4, 5), (6, 7)), banks)):
                            wsl = 2 if hf == 0 else 1
                            mm_group(None, [(PS[b][0:rows, :], MX(c, c0, n), WS(wsl, c, 0, 512), c == cs[0], c == cs[-1]) for c in cs],
                                     reads=mxr + [("ws", wsl)], writes=[("ps", b)])
                        o_h = ostg[:, 512 * hf:512 * hf + 512]
                        S.op("act", lambda e, o_h=o_h, b=banks[0], rows=rows, i=i: e.activation(o_h, PS[b][0:rows, :], AF.Copy, scale=RFOX(i, rows)),
                             reads=[("ps", banks[0]), "rfox"], writes=scr(2 * hf, 2 * hf + 1))
                        S.op("dve", lambda e, o_h=o_h, b=banks[1], rows=rows, i=i: e.scalar_tensor_tensor(o_h, PS[b][0:rows, :], RSGU(i, rows), o_h, ALU.mult, ALU.add),
                             reads=[("ps", banks[1]), ("rsgu", i)] + scr(2 * hf, 2 * hf + 1), writes=scr(2 * hf, 2 * hf + 1))
                        S.op("dve", lambda e, o_h=o_h, b=banks[2], rows=rows, i=i: e.scalar_tensor_tensor(o_h, PS[b][0:rows, :], RMEM(i, rows), o_h, ALU.mult, ALU.add),
                             reads=[("ps", banks[2]), "rmem"] + scr(2 * hf, 2 * hf + 1), writes=scr(2 * hf, 2 * hf + 1))
                        last_b = banks
                    post_norm_residual(ostg, rows, i, bf(WS_O[0], 1024, 0, rows), [("ws", 0)] + SC_TOKS, GP, scr(4, 5, 6, 7), scr(0, 1, 2, 3))

                _chk("wout%d" % l)
                S.barrier()
                ffn_phase(l)
                if l < L - 1:
                    S.barrier()

        except _Stop:
            pass
        S.emit(st)
    return nc


_NC_CACHE = {}


def _prep_inputs(inp):
    f = lambda a: np.ascontiguousarray(np.asarray(a, dtype=np.float32))
    x_prompt = f(inp["x_prompt"]); x_sample = f(inp["x_sample"]); mem_prompt = f(inp["mem_prompt"])
    cfk = f(inp["cache_fox_k"]); cfv = f(inp["cache_fox_v"]); clf = f(inp["cache_fox_logf"])
    cmk = f(inp["cache_mem_k"]); cmvv = f(inp["cache_mem_v"]); cconv = f(inp["cache_ffn_conv"])
    ckT_all = np.ascontiguousarray(cfk.transpose(0, 1, 3, 4, 2))
    cv_all = cfv.reshape(L, 16, 1024, 512)
    clfT_all = np.ascontiguousarray(clf.transpose(0, 1, 3, 2))
    cmkT_all = np.ascontiguousarray(cmk.transpose(0, 1, 3, 4, 2))
    cmv_all = cmvv.reshape(L, 16, 256, 256)
    wsT = np.ascontiguousarray(f(inp["w_spatial"]).transpose(0, 1, 3, 2))
    fm8 = lambda g: f(g).reshape(L, 8, 128).transpose(0, 2, 1)
    b_sp = f(inp["b_spatial"])
    w_dw = f(inp["w_dwconv"]); b_dw = f(inp["b_dwconv"]); b_f = f(inp["b_forget"])
    cst = np.zeros((128, 288), dtype=ml_dtypes.bfloat16)
    cst[:, 0:128] = np.eye(128, dtype=np.float32).astype(ml_dtypes.bfloat16)
    kk, qq = np.meshgrid(np.arange(128), np.arange(128), indexing="ij")
    cst[:, 128:256] = np.where(kk <= qq, 0.0, NEG).astype(ml_dtypes.bfloat16)
    k2, q2 = np.meshgrid(np.arange(32), np.arange(32), indexing="ij")
    ok = (k2 // 16 == q2 // 16) & (k2 % 16 <= q2 % 16)
    cst[0:32, 256:288] = np.where(ok, 0.0, NEG).astype(ml_dtypes.bfloat16)
    wu = f(inp["w_up"])
    wu_a = wu[:, :, 0:FF].reshape(L, 8, 128, NJ, 128)
    wu_l = wu[:, :, FF:2 * FF].reshape(L, 8, 128, NJ, 128)
    w_up_r = np.ascontiguousarray(np.concatenate([wu_a, wu_l], axis=4).transpose(0, 3, 2, 1, 4))
    shared = dict(
        w_in=f(inp["w_in"]), w_mkv=f(inp["w_mem_kv"]), w_out=f(inp["w_out"]), w_up=w_up_r,
        w_dn=f(inp["w_down"]), g_post_mix=f(inp["g_post_mix"]), g_post_ffn=f(inp["g_post_ffn"]),
        g_sgu=f(inp["g_sgu"]), wsT=wsT, cst=cst)
    gpm = fm8(inp["g_pre_mix"]); ggo = fm8(inp["g_group_out"]); gmem = fm8(inp["g_mem"]); gpf = fm8(inp["g_pre_ffn"])
    in_maps = []
    for c in range(NCORES):
        prm = np.zeros((128, NPRM), dtype=np.float32)
        for l in range(L):
            o = l * PPL
            prm[:, o + PO["gpm"]:o + PO["gpm"] + 8] = gpm[l]
            prm[:, o + PO["ggo"]:o + PO["ggo"] + 8] = ggo[l]
            prm[:, o + PO["gmem"]:o + PO["gmem"] + 8] = gmem[l]
            prm[:, o + PO["gpf"]:o + PO["gpf"] + 8] = gpf[l]
            prm[:, o + PO["bs"]:o + PO["bs"] + 4] = b_sp[l].T
            prm[0:16, o + PO["bss"]:o + PO["bss"] + 4] = b_sp[l][:, 0:16].T
            prm[16:32, o + PO["bss"]:o + PO["bss"] + 4] = b_sp[l][:, 0:16].T
            prm[:, o + PO["wdw"]:o + PO["wdw"] + 66] = w_dw[l].reshape(3, NJ, 128).transpose(2, 1, 0).reshape(128, 66)
            prm[:, o + PO["bdw"]:o + PO["bdw"] + 22] = b_dw[l].reshape(NJ, 128).T
            prm[0:8, o + PO["bf"]] = b_f[l]
            cc = cconv[l, 2 * c:2 * c + 2]
            prm[:, o + PO["cconv"]:o + PO["cconv"] + 88] = cc.reshape(2, 2, NJ, 128).transpose(3, 2, 0, 1).reshape(128, 88)
        m = dict(shared)
        m.update(
            x_p=x_prompt[c], x_s=x_sample[2 * c:2 * c + 2].reshape(32, D), mem=mem_prompt[c],
            ckT=np.ascontiguousarray(ckT_all[:, 2 * c:2 * c + 2]), cv=np.ascontiguousarray(cv_all[:, 2 * c:2 * c + 2]),
            clfT=np.ascontiguousarray(clfT_all[:, 2 * c:2 * c + 2]), cmkT=np.ascontiguousarray(cmkT_all[:, 2 * c:2 * c + 2]),
            cmv=np.ascontiguousarray(cmv_all[:, 2 * c:2 * c + 2]), prm=prm)
        in_maps.append(m)
    return in_maps


def kernel(**inp):
    in_maps = _prep_inputs(inp)
    if "nc" not in _NC_CACHE:
        _NC_CACHE["nc"] = build()
    nc = _NC_CACHE["nc"]
    res = run_bass_kernel_spmd(nc, in_maps, core_ids=list(range(NCORES)))
    R = res.results
    cat = lambda name: np.stack([np.asarray(r[name], dtype=np.float32) for r in R], axis=0)
    y_p = cat("y_p")
    y_s = cat("y_s").reshape(16, 16, D)
    fk = cat("ofk").transpose(1, 0, 2, 3).reshape(L, 8, 2048, 8, 64)
    fv = cat("ofv").transpose(1, 0, 2, 3).reshape(L, 8, 2048, 8, 64)
    lf = cat("olf").transpose(1, 0, 3, 2)
    mk = cat("omk").transpose(1, 0, 2, 3).reshape(L, 8, 256, 4, 64)
    mv = cat("omv").transpose(1, 0, 2, 3).reshape(L, 8, 256, 4, 64)
    cvp = cat("oconv").transpose(1, 0, 4, 3, 2).reshape(L, 8, 2, FF)
    sk = cat("osk").transpose(1, 0, 2, 3).reshape(L, 16, 16, 8, 64)
    sv = cat("osv").transpose(1, 0, 2, 3).reshape(L, 16, 16, 8, 64)
    slf = cat("oslf").transpose(1, 0, 3, 2).reshape(L, 16, 16, 8)
    gv = cat("ogv").transpose(1, 0, 2, 3).reshape(L, 16, 16, 256)
    cvs = cat("osconv").transpose(1, 0, 4, 5, 3, 2).reshape(L, 16, 2, FF)
    outs = (y_p, y_s, fk, fv, lf, mk, mv, cvp, sk, sv, slf, gv, cvs)
    return tuple(np.ascontiguousarray(o, dtype=np.float32) for o in outs)
```

```python
import numpy as np
import ml_dtypes
from contextlib import ExitStack
import concourse.bass as bass
import concourse.mybir as mybir
from concourse.bass_utils import run_bass_kernel_spmd

F32 = mybir.dt.float32
BF16 = mybir.dt.bfloat16
ALU = mybir.AluOpType
AF = mybir.ActivationFunctionType

ENGS = ("pe", "act", "dve", "pool", "sp")
STOP = None
DBG = set()


class _Stop(Exception):
    pass


def _chk(name):
    if STOP == name:
        raise _Stop()
NCORES = 8
L = 2
D = 1024
NT = 2080
FF = 2816
NJ = 22
EPS = 1e-6
NEG = -30000.0


class Op:
    __slots__ = ("eng", "fn", "idx", "deps", "dma_key", "marked", "seq", "cum")

    def __init__(self, eng, fn, idx, dma_key):
        self.eng = eng
        self.fn = fn
        self.idx = idx
        self.deps = ()
        self.dma_key = dma_key
        self.marked = False
        self.seq = 0
        self.cum = 0


class Sched:
    def __init__(self, nc):
        self.nc = nc
        self.streams = {e: [] for e in ENGS}
        self.last_writer = {}
        self.readers = {}
        self.dma_counts = {}
        self.dma_last = {}

    def op(self, eng, fn, reads=(), writes=(), dma_key=None):
        o = Op(eng, fn, len(self.streams[eng]), dma_key)
        deps = set()
        lw = self.last_writer
        rd = self.readers
        for t in reads:
            w = lw.get(t)
            if w is not None:
                deps.add(w)
            if type(t) is tuple and t[0] == "ps":
                for r in rd.get(t, ()):
                    if r.eng != eng:
                        deps.add(r)
        for t in writes:
            w = lw.get(t)
            if w is not None:
                deps.add(w)
            r = rd.get(t)
            if r:
                deps.update(r)
        for t in reads:
            rd.setdefault(t, []).append(o)
        for t in writes:
            lw[t] = o
            rd[t] = []
        deps.discard(o)
        if eng == "pe":
            deps = {d for d in deps if not (d.eng == "pe" and d.dma_key is None)}
        o.deps = deps
        if dma_key is not None:
            c = self.dma_counts.get(dma_key, 0) + 1
            self.dma_counts[dma_key] = c
            o.cum = 16 * c
            self.dma_last[dma_key] = o
        self.streams[eng].append(o)
        return o

    def barrier(self):
        lasts = [s[-1] for s in self.streams.values() if s]
        lasts += list(self.dma_last.values())
        for e in ENGS:
            o = Op(e, lambda eng: eng.nop(), len(self.streams[e]), None)
            o.deps = {d for d in lasts if not (d.eng == e and d.dma_key is None)}
            self.streams[e].append(o)

    def emit(self, stack):
        nc = self.nc
        for e in ENGS:
            for o in self.streams[e]:
                for d in o.deps:
                    if d.dma_key is None:
                        d.marked = True
        sems = {}
        for e in ENGS:
            n = 0
            for o in self.streams[e]:
                if o.dma_key is None and o.marked:
                    n += 1
                    o.seq = n
            if n:
                sems[e] = stack.enter_context(nc.semaphore("s_" + e))
        dsems = {}
        for k in self.dma_counts:
            dsems[k] = stack.enter_context(nc.semaphore("d%d" % len(dsems)))
        self.n_sems = len(sems) + len(dsems)
        final = {k: o.cum for k, o in self.dma_last.items()}

        def run(eng_name, eng):
            waited = {}
            for o in self.streams[eng_name]:
                need = {}
                for d in o.deps:
                    if d.dma_key is not None:
                        key = ("d", d.dma_key)
                        val = d.cum
                    else:
                        key = ("c", d.eng)
                        val = d.seq
                    if val > need.get(key, 0):
                        need[key] = val
                for key, val in need.items():
                    if waited.get(key, 0) >= val:
                        continue
                    waited[key] = val
                    s = dsems[key[1]] if key[0] == "d" else sems[key[1]]
                    eng.wait_ge(s, val)
                ins = o.fn(eng)
                if o.dma_key is not None:
                    ins.then_inc(dsems[o.dma_key], 16)
                elif o.marked:
                    ins.then_inc(sems[o.eng], 1)
            if eng_name == "sp":
                for k, v in final.items():
                    if waited.get(("d", k), 0) < v:
                        eng.wait_ge(dsems[k], v)

        with nc.Block() as block:
            @block.tensor
            def _(t):
                run("pe", t)

            @block.scalar
            def _(t):
                run("act", t)

            @block.vector
            def _(t):
                run("dve", t)

            @block.gpsimd
            def _(t):
                run("pool", t)

            @block.sync
            def _(t):
                run("sp", t)


def tile_rows(i):
    return 128 if i < 16 else 32


def tcols(i):
    return (128 * i, 128) if i < 16 else (2048, 32)


BLKS = [(0, 512), (512, 512), (1024, 512), (1536, 512), (2048, 32)]

PO = {}
_o = 0
for _n, _w in (("gpm", 8), ("ggo", 8), ("gmem", 8), ("gpf", 8), ("bs", 4), ("bss", 4),
               ("wdw", 66), ("bdw", 22), ("bf", 1), ("cconv", 88)):
    PO[_n] = _o
    _o += _w
PPL = _o
NPRM = PPL * L


def build():
    nc = bass.Bass("TRN2", target_bir_lowering=False)

    def din(name, shape, dt=F32):
        return nc.dram_tensor(name, list(shape), dt, kind="ExternalInput").ap()

    def dout(name, shape):
        return nc.dram_tensor(name, list(shape), F32, kind="ExternalOutput").ap()

    x_p = din("x_p", [2048, D])
    x_s = din("x_s", [32, D])
    mem = din("mem", [256, D])
    ckT = din("ckT", [L, 2, 8, 64, 1024])
    cv = din("cv", [L, 2, 1024, 512])
    clfT = din("clfT", [L, 2, 8, 1024])
    cmkT = din("cmkT", [L, 2, 4, 64, 256])
    cmv = din("cmv", [L, 2, 256, 256])
    w_in = din("w_in", [L, D, 2312])
    w_mkv = din("w_mkv", [L, D, 512])
    w_out = din("w_out", [L, D, D])
    w_up = din("w_up", [L, NJ, 128, 8, 256])
    w_dn = din("w_dn", [L, FF, D])
    g_post_mix = din("g_post_mix", [L, D])
    g_post_ffn = din("g_post_ffn", [L, D])
    g_sgu = din("g_sgu", [L, 256])
    wsT = din("wsT", [L, 4, 128, 128])
    prm_h = din("prm", [128, NPRM])
    cst_h = din("cst", [128, 288], BF16)

    y_p = dout("y_p", [2048, D])
    y_s = dout("y_s", [32, D])
    ofk = dout("ofk", [L, 2048, 512])
    ofv = dout("ofv", [L, 2048, 512])
    olf = dout("olf", [L, 8, 2048])
    omk = dout("omk", [L, 256, 256])
    omv = dout("omv", [L, 256, 256])
    oconv = dout("oconv", [L, 128, NJ, 2])
    osk = dout("osk", [L, 32, 512])
    osv = dout("osv", [L, 32, 512])
    oslf = dout("oslf", [L, 8, 32])
    ogv = dout("ogv", [L, 32, 256])
    osconv = dout("osconv", [L, 128, NJ, 2, 2])

    cscr_p = nc.dram_tensor("cscr_p", [8, 2, 2048], BF16).ap()
    cscr_s = nc.dram_tensor("cscr_s", [8, 2, 2, 1040], BF16).ap()

    with ExitStack() as st:
        S = Sched(nc)
        X = st.enter_context(nc.sbuf_tensor("X", [128, 17, D], F32))
        PRM = st.enter_context(nc.sbuf_tensor("PRM", [128, NPRM], F32))
        CST = st.enter_context(nc.sbuf_tensor("CST", [128, 288], BF16))
        WSTSt = st.enter_context(nc.sbuf_tensor("WSTS", [32, L * 4 * 32], BF16))
        PTSt = st.enter_context(nc.sbuf_tensor("PTS", [128, 7 * 32], BF16))
        SM = st.enter_context(nc.sbuf_tensor("SM", [128, 256], F32))
        ONEROW = st.enter_context(nc.sbuf_tensor("ONEROW", [128, 64], F32))
        remaining = nc.sbuf_bytes_remaining
        remaining = remaining() if callable(remaining) else remaining
        ARB = (remaining - 512) // 64 * 64
        AR = st.enter_context(nc.sbuf_tensor("AR", [128, ARB // 2], BF16))
        AR32 = AR.bitcast(F32)
        PS = [st.enter_context(nc.psum_tensor("ps%d" % k, [128, 512], F32)) for k in range(8)]
        PSB = [p.bitcast(BF16) for p in PS]

        IDENT = CST[:, 0:128]
        MASKNEG = CST[:, 128:256]
        MASKS = CST[0:32, 256:288]


        def WSTS(l, g):
            o = (l * 4 + g) * 32
            return WSTSt[0:32, o:o + 32]

        def prm(l, name, w, p0=0, p1=128, off=0):
            o = l * PPL + PO[name] + off
            return PRM[p0:p1, o:o + w]

        SM_SS, SM_SD, SM_RSTD = 0, 1, 2
        SM_R = 16
        SM_SSF = 70
        SM_T = 90

        def bf(off, n, p0=0, p1=128):
            return AR[p0:p1, off // 2: off // 2 + n]

        def f32(off, n, p0=0, p1=128):
            return AR32[p0:p1, off // 4: off // 4 + n]

        cur = [0]

        def carve(nbytes):
            o = cur[0]
            cur[0] = o + (nbytes + 63) // 64 * 64
            return o

        HT_O = carve(8 * NT * 2)
        MX_O = carve(8 * NT * 2)
        VA_O = carve(17 * 8 * 65 * 2)
        QK_O = carve(4 * NT * 2)
        WS_O = [carve(8320) for _ in range(3)]
        SCR_O = carve(8192)
        MKT_O = carve(2048)
        MVA_O = carve(1040 + 128)
        MKTt = AR[:, MKT_O // 2:MKT_O // 2 + 1024]
        MVAt = AR[:, MVA_O // 2:MVA_O // 2 + 520]
        MIX_END = cur[0]
        assert MIX_END <= ARB, (MIX_END, ARB)

        def HT(c, c0, n):
            return bf(HT_O + (c * NT + c0) * 2, n)

        def HT3(c0, n):
            return AR[:, HT_O // 2: HT_O // 2 + 8 * NT].rearrange("p (c t) -> p c t", c=8)[:, :, c0:c0 + n]

        def MX(c, c0, n, p0=0, p1=128):
            return bf(MX_O + (c * NT + c0) * 2, n, p0, p1)

        def VA(j, h, nk=128):
            return bf(VA_O + ((j * 8 + h) * 65) * 2, 65, 0, nk)

        def VA128(j, h):
            return bf(VA_O + ((j * 8 + h) * 65) * 2, 128, 0, 128)

        def QK(slot, c0, n, p0, p1):
            return bf(QK_O + (slot * NT + c0) * 2, n, p0, p1)

        CT = lambda c0, n: f32(QK_O + c0 * 4, n, 96, 104)
        HI = lambda c0, n: bf(QK_O + 8320 + c0 * 2, n, 96, 104)
        LO = lambda c0, n: bf(QK_O + 8320 + 4160 + c0 * 2, n, 96, 104)

        def WS(s, c, c0, n):
            return bf(WS_O[s] + (c * 520 + c0) * 2, n)

        def WS3(s, ncol):
            return AR[:, WS_O[s] // 2: WS_O[s] // 2 + 8 * 520].rearrange("p (c n) -> p c n", c=8)[:, :, 0:ncol]

        def SCRb(g, n, p0=0, p1=128, off=0):
            return bf(SCR_O + g * 1024 + off * 2, n, p0, p1)

        def SCRf(g, n, p0=0, p1=128, off=0):
            return f32(SCR_O + g * 1024 + off * 4, n, p0, p1)

        def scr(*gs):
            return [("scr", g) for g in gs]

        def dma(eng, out, in_, reads=(), writes=(), key=None, slow=False):
            if slow:
                fn = lambda e: e.dma_start(out=out, in_=in_, allow_slow_non_contiguous=True)
            else:
                fn = lambda e: e.dma_start(out=out, in_=in_)
            return S.op(eng, fn, reads=reads, writes=writes, dma_key=key)

        psr = {"g": [0, 1], "s": [2, 3, 4], "o": [5, 6], "t": [7]}
        psi = {k: 0 for k in psr}

        def psnext(pool):
            lst = psr[pool]
            k = lst[psi[pool] % len(lst)]
            psi[pool] += 1
            return k

        def mm_group(out_fn, items, reads, writes):
            def fn(e):
                ins = None
                for (o, a, b, s0, s1) in items:
                    ins = e.matmul(o, a, b, start=s0, stop=s1, skip_group_check=True)
                return ins
            return S.op("pe", fn, reads=reads, writes=writes)

        dma("sp", PRM[:, :], prm_h, writes=["prm"], key="prm")
        dma("sp", CST[:, :], cst_h, writes=["cst"], key="cst")
        S.op("dve", lambda e: e.memset(WSTSt[:, :], 0.0), writes=["wsts"])
        for l in range(L):
            for b in range(2):
                dma("pool",
                    WSTSt[16 * b:16 * b + 16, l * 128:(l + 1) * 128].rearrange("p (g i) -> p g i", g=4)[:, :, 16 * b:16 * b + 16],
                    wsT[l, :, 0:16, 0:16].rearrange("g j i -> j g i"),
                    reads=["wsts"], writes=[("wsts", l, b)], key=("wsts", l), slow=True)
        S.op("dve", lambda e: e.memset(PTSt[:, :], 0.0), writes=["pts%d" % q for q in range(7)])
        S.op("dve", lambda e: e.memset(SM[:, 200:201], 1.0), writes=["one"])
        S.op("dve", lambda e: e.memset(ONEROW[:, :], 1.0), writes=["onerow"])
        ONE8 = SM[96:104, 200:201]

        for i in range(16):
            dma("sp", X[:, i, :], x_p[128 * i:128 * i + 128, :], writes=[("x", i)], key=("x", i))
        dma("sp", X[0:32, 16, :], x_s, writes=[("x", 16)], key=("x", 16))

        def prenorm_tile(src, rows, xtok, gcol, dst3, dst_tok, slot, extra_writes=(), hb=None, tpool="t", hb_toks=None, defer=False):
            if hb is None:
                hb = SCRb(2 * slot, 1024, 0, rows)
            hbt = list(hb_toks) if hb_toks is not None else scr(2 * slot, 2 * slot + 1)
            c0 = 3 * slot
            ss = SM[0:rows, c0:c0 + 1]
            sd = SM[0:rows, c0 + 1:c0 + 2]
            rs = SM[0:rows, c0 + 2:c0 + 3]
            S.op("act", lambda e: e.activation(hb, src, AF.Square, accum_out=ss),
                 reads=[xtok], writes=hbt + [("sm", c0)])
            S.op("act", lambda e: e.activation(sd, ss, AF.Ln, bias=EPS, scale=1.0 / D),
                 reads=[("sm", c0)], writes=[("sm", c0 + 1)])
            S.op("act", lambda e: e.activation(rs, sd, AF.Exp, scale=-0.5), reads=[("sm", c0 + 1)], writes=[("sm", c0 + 2)])
            S.op("act", lambda e: e.activation(hb, src, AF.Copy, scale=rs),
                 reads=[xtok, ("sm", c0 + 2)], writes=hbt)
            def part_b():
                k = psnext(tpool)

                def tr(e):
                    ins = None
                    for c in range(8):
                        ins = e.transpose(PSB[k][:, c * 128:c * 128 + rows], hb[:, c * 128:(c + 1) * 128],
                                          IDENT[0:rows, 0:rows])
                    return ins
                S.op("pe", tr, reads=hbt + ["cst"], writes=[("ps", k)])
                src3 = PSB[k][:, 0:1024].rearrange("p (c t) -> p c t", c=8)[:, :, 0:rows]
                S.op("dve", lambda e: e.tensor_tensor(dst3, src3, gcol.unsqueeze(2).to_broadcast([128, 8, rows]), ALU.mult),
                     reads=[("ps", k), "prm"], writes=[dst_tok] + list(extra_writes))
            if defer:
                return part_b
            part_b()

        PVQ = []

        def flush_pv():
            while PVQ:
                PVQ.pop(0)()

        def attend(qa_fn, K, ktiles, osubs, obank, tag):
            pendq = []
            first_done = [False]

            def pv(kt_i, kt, pt_ap, pt_tok):
                items = []
                if osubs == "fm":
                    lo_, hi_ = kt["lo"], kt["hi"]
                    mrows = kt["va"].shape[1]
                    items.append((PS[obank][0:mrows, lo_:hi_], kt["va"], pt_ap[:, lo_:hi_], kt_i == 0, kt_i == len(ktiles) - 1))
                    mm_group(None, items, reads=[pt_tok] + kt["vreads"], writes=[("ps", obank)])
                    return
                for (o_ap, c0, c1, fk, lk) in osubs:
                    if fk <= kt_i <= lk and kt["pv_lo"] <= c0:
                        items.append((o_ap, pt_ap[:, c0:c1], kt["va"], not first_done[0], kt_i == lk))
                        first_done[0] = True
                if items:
                    mm_group(None, items, reads=[pt_tok] + kt["vreads"], writes=[("ps", obank)])

            for kt_i, kt in enumerate(ktiles):
                k = psnext("s")
                nk, lo, hi = kt["nk"], kt["lo"], kt["hi"]
                items = []
                if kt["mask"] is not None:
                    mask_ap, mc = kt["mask"]
                    items.append((PS[k][0:nk, lo:lo + mc], kt["ka"], qa_fn(lo, mc), True, False))
                    items.append((PS[k][0:nk, lo:lo + mc], IDENT[0:nk, 0:nk], mask_ap, False, True))
                    if hi > lo + mc:
                        items.append((PS[k][0:nk, lo + mc:hi], kt["ka"], qa_fn(lo + mc, hi - lo - mc), True, True))
                else:
                    items.append((PS[k][0:nk, lo:hi], kt["ka"], qa_fn(lo, hi - lo), True, True))
                mm_group(None, items, reads=kt["reads"] + ["cst"], writes=[("ps", k)])
                if kt["pt"] == "rot":
                    g = attend.ptc % 4
                    attend.ptc += 1
                    pt_ap = SCRb(g, 512, 0, nk)
                    pt_tok = ("scr", g)
                else:
                    pt_ap = PTSt[0:nk, 32 * kt["pt"]:32 * kt["pt"] + 32]
                    pt_tok = "pts%d" % kt["pt"]
                S.op("act", lambda e, pt_ap=pt_ap, k=k, nk=nk, lo=lo, hi=hi:
                     e.activation(pt_ap[:, lo:hi], PS[k][0:nk, lo:hi], AF.Exp),
                     reads=[("ps", k)], writes=[pt_tok])
                while len(PVQ) >= 2:
                    PVQ.pop(0)()
                PVQ.append(lambda kt_i=kt_i, kt=kt, pt_ap=pt_ap, pt_tok=pt_tok: pv(kt_i, kt, pt_ap, pt_tok))
        attend.ptc = 0

        def cs_all():
            return ["cw"]

        def post_norm_residual(ostg, rows, i, junk, junk_toks, GPap, gp_toks, o_toks):
            ssc = SM[0:rows, SM_T + 8:SM_T + 9]
            S.op("act", lambda e: e.activation(junk, ostg, AF.Square, accum_out=ssc),
                 reads=o_toks, writes=list(junk_toks) + ["pn"])
            S.op("act", lambda e: e.activation(ssc, ssc, AF.Ln, bias=EPS, scale=1.0 / D), reads=["pn"], writes=["pn"])
            S.op("act", lambda e: e.activation(ssc, ssc, AF.Exp, scale=-0.5), reads=["pn"], writes=["pn"])
            S.op("dve", lambda e: e.scalar_tensor_tensor(ostg, ostg, ssc, GPap[0:rows, :], ALU.mult, ALU.mult),
                 reads=list(o_toks) + list(gp_toks) + ["pn"], writes=o_toks)
            S.op("pool", lambda e: e.tensor_tensor(X[0:rows, i, :], X[0:rows, i, :], ostg, ALU.add),
                 reads=list(o_toks) + [("x", i)], writes=[("x", i)])

        def ffn_phase(l):
            psr.clear()
            psr.update({"u": [0, 1, 2, 4, 5, 6], "t": [3], "d": [4, 5, 6, 7]})
            for kk in psr:
                psi[kk] = 0
            HC = 1056
            o = [0]

            def cv_(nb):
                r = o[0]
                o[0] = r + (nb + 63) // 64 * 64
                return r
            H2_O = cv_(8 * HC * 2)
            ACT_O = cv_(NJ * HC * 2)
            WDN_O = cv_(NJ * 1024 * 2)
            WU_O = [cv_(4096) for _ in range(2)]
            AE_O = [cv_(2064) for _ in range(2)]
            LIN_O = [cv_(1024) for _ in range(2)]
            CONV_O = [cv_(2048) for _ in range(2)]
            OSTG_O = cv_(4096)
            GPF_O = cv_(4096)
            HBF_O = cv_(2048)
            SAVE_O = cv_(NJ * 2 * 4)
            SAVES_O = cv_(NJ * 4 * 4)
            assert o[0] <= ARB, (o[0], ARB)

            H2 = lambda c, c0, n: bf(H2_O + (c * HC + c0) * 2, n)
            H23 = lambda c0, n: AR[:, H2_O // 2:H2_O // 2 + 8 * HC].rearrange("p (c t) -> p c t", c=8)[:, :, c0:c0 + n]
            ACTB = lambda j, c0, n: bf(ACT_O + (j * HC + c0) * 2, n)
            WDN = lambda j, c0, n: bf(WDN_O + (j * 1024 + c0) * 2, n)
            WDN3 = AR[:, WDN_O // 2:WDN_O // 2 + NJ * 1024].rearrange("p (j n) -> p j n", j=NJ)
            WUB = lambda s: (WU_O[s] if s < 2 else OSTG_O)
            WU = lambda s, c, c0, n: bf(WUB(s) + (c * 256 + c0) * 2, n)
            WU3 = lambda s: AR[:, WUB(s) // 2:WUB(s) // 2 + 2048].rearrange("p (c n) -> p c n", c=8)
            wutok = lambda s, x: (("wu", s, x) if s < 2 else ("ostg%d" % x))
            OSTG = f32(OSTG_O, 1024)
            GPF = f32(GPF_O, 1024)
            HBF = bf(HBF_O, 1024)
            SAVE = f32(SAVE_O, NJ * 2)
            SAVES = f32(SAVES_O, NJ * 4)

            dma("sp", GPF, g_post_ffn[l:l + 1, :].broadcast_to([128, D]), writes=["gpf"], key="gp")
            wdn_src = w_dn[l].rearrange("(j p) n -> p j n", p=128)
            for part in range(2):
                dma("pool", WDN3[:, 11 * part:11 * part + 11, :], wdn_src[:, 11 * part:11 * part + 11, :],
                    writes=[("wdn", part)], key=("wdn", part))
            aecnt = 0

            def wu_load(j, s_):
                dma("pool", WU3(s_), w_up[l, j], writes=[wutok(s_, 0), wutok(s_, 1)], key=("wu", s_, 0))

            for half in range(2):
                tiles = list(range(0, 8)) if half == 0 else list(range(8, 17))
                base = 0 if half == 0 else 1024
                blocks = [(0, 512), (512, 512)] + ([(1024, 32)] if half == 1 else [])
                h2toks = [("h2", (i if i < 8 else i - 8)) for i in tiles]
                if half == 0:
                    HB2 = bf(AE_O[0], 1024)

                    def pa(i):
                        c0, n = tcols(i)
                        if i % 2 == 0:
                            return prenorm_tile(X[:, i, :], 128, ("x", i), prm(l, "gpf", 8), H23(c0, n), ("h2", i), 0,
                                                hb=HBF[:, :], tpool="t", hb_toks=["hbf"], defer=True)
                        return prenorm_tile(X[:, i, :], 128, ("x", i), prm(l, "gpf", 8), H23(c0, n), ("h2", i), 1,
                                            hb=HB2, tpool="t", hb_toks=[("ae", 0), ("aeh", 0), ("aet", 0)], defer=True)
                    pbq = {0: pa(0)}
                    for i in tiles:
                        if i + 1 < 8:
                            pbq[i + 1] = pa(i + 1)
                        pbq.pop(i)()
                wu_load(0, 0)
                wu_load(1, 1)
                pend_tail = None
                for j in range(NJ):
                    s = j % 3
                    if j + 2 < NJ:
                        wu_load(j + 2, (j + 2) % 3)
                    w0 = prm(l, "wdw", 1, 0, 128, 3 * j)
                    w1 = prm(l, "wdw", 1, 0, 128, 3 * j + 1)
                    w2 = prm(l, "wdw", 1, 0, 128, 3 * j + 2)
                    bd = prm(l, "bdw", 1, 0, 128, j)
                    for bidx, (lc0, n) in enumerate(blocks):
                        sample = (n == 32)
                        ka = psnext("u")
                        mm_group(None, [(PS[ka][:, 0:n], WU(s, c, 0, 128), H2(c, lc0, n), c == 0, c == 7) for c in range(8)],
                                 reads=h2toks + [wutok(s, 0)], writes=[("ps", ka)])
                        kl = psnext("u")
                        mm_group(None, [(PS[kl][:, 0:n], WU(s, c, 128, 128), H2(c, lc0, n), c == 0, c == 7) for c in range(8)],
                                 reads=h2toks + [wutok(s, 1)], writes=[("ps", kl)])
                        p = aecnt % 2
                        aecnt += 1
                        lin = bf(LIN_O[p], 512)
                        conv = f32(CONV_O[p], 512)
                        if sample:
                            AEs = f32(AE_O[p], 36).rearrange("p (b t) -> p b t", b=2)
                            S.op("dve", lambda e, AEs=AEs, j=j: e.tensor_copy(
                                AEs[:, :, 0:2], prm(l, "cconv", 4, 0, 128, 4 * j).rearrange("p (b r) -> p b r", b=2)),
                                reads=["prm"], writes=[("aeh", p)])
                            S.op("act", lambda e, AEs=AEs, ka=ka: e.copy(AEs[:, :, 2:18], PS[ka][:, 0:32].rearrange("p (b t) -> p b t", b=2)),
                                 reads=[("ps", ka)], writes=[("ae", p)])
                            taps = [AEs[:, :, k:k + 16] for k in range(3)]
                            cv3 = conv[:, 0:32].rearrange("p (b t) -> p b t", b=2)
                            psa = PS[ka][:, 0:32].rearrange("p (b t) -> p b t", b=2)
                            sil_out = f32(AE_O[p] + 256, 32)
                            sil_in = conv[:, 0:32]
                        else:
                            AEp = f32(AE_O[p], 514)
                            if bidx == 0 and half == 0:
                                S.op("dve", lambda e, AEp=AEp: e.memset(AEp[:, 0:2], 0.0), writes=[("aeh", p)])
                            elif bidx == 0:
                                S.op("dve", lambda e, AEp=AEp, j=j: e.tensor_copy(AEp[:, 0:2], SAVE[:, 2 * j:2 * j + 2]),
                                     reads=[("save", j)], writes=[("aeh", p)])
                            else:
                                prev = f32(AE_O[1 - p], 514)
                                S.op("dve", lambda e, AEp=AEp, prev=prev: e.tensor_copy(AEp[:, 0:2], prev[:, 512:514]),
                                     reads=[("aet", 1 - p)], writes=[("aeh", p)])
                            S.op("act", lambda e, AEp=AEp, ka=ka: e.copy(AEp[:, 2:514], PS[ka][:, 0:512]),
                                 reads=[("ps", ka)], writes=[("ae", p), ("aet", p)])
                            taps = [AEp[:, k:k + 512] for k in range(3)]
                            cv3 = conv
                            psa = PS[ka][:, 0:512]
                            sil_out = AEp[:, 0:512]
                            sil_in = conv
                        S.op("act", lambda e, cv3=cv3, psa=psa, w2=w2, bd=bd: e.activation(cv3, psa, AF.Identity, bias=bd, scale=w2),
                             reads=[("ps", ka), "prm"], writes=[("conv", p)])
                        S.op("act", lambda e, lin=lin, kl=kl, n=n: e.copy(lin[:, 0:n], PS[kl][:, 0:n]),
                             reads=[("ps", kl)], writes=[("lin", p)])
                        S.op("dve", lambda e, cv3=cv3, taps=taps, w1=w1: e.scalar_tensor_tensor(cv3, taps[1], w1, cv3, ALU.mult, ALU.add),
                             reads=[("ae", p), ("aeh", p), ("conv", p)], writes=[("conv", p)])
                        S.op("dve", lambda e, cv3=cv3, taps=taps, w0=w0: e.scalar_tensor_tensor(cv3, taps[0], w0, cv3, ALU.mult, ALU.add),
                             reads=[("ae", p), ("aeh", p), ("conv", p)], writes=[("conv", p)])
                        if not sample and bidx == 1:
                            S.op("dve", lambda e, AEp=AEp, j=j: e.tensor_copy(SAVE[:, 2 * j:2 * j + 2], AEp[:, 512:514]),
                                 reads=[("aet", p)], writes=[("save", j)])
                        if sample:
                            S.op("dve", lambda e, AEs=AEs, j=j: e.tensor_copy(
                                SAVES[:, 4 * j:4 * j + 4].rearrange("p (b r) -> p b r", b=2), AEs[:, :, 16:18]),
                                reads=[("ae", p)], writes=[("saves", j)])

                        def tail(p=p, sil_out=sil_out, sil_in=sil_in, j=j, lc0=lc0, n=n, lin=lin, half=half):
                            S.op("act", lambda e: e.activation(sil_out, sil_in, AF.Silu),
                                 reads=[("conv", p), ("ae", p), ("aeh", p)], writes=[("ae", p), ("aeh", p)])
                            S.op("pool", lambda e: e.tensor_tensor(ACTB(j, lc0, n), sil_out, lin[:, 0:n], ALU.mult),
                                 reads=[("ae", p), ("aeh", p), ("lin", p)], writes=[("actb", j, half)])
                        if pend_tail is not None:
                            pend_tail()
                        pend_tail = tail
                if pend_tail is not None:
                    pend_tail()
                    pend_tail = None
                if half == 1:
                    dma("sp", oconv[l], SAVE.rearrange("p (j r) -> p j r", r=2), reads=[("save", j) for j in range(NJ)], key="oconv")
                    dma("sp", osconv[l], SAVES.rearrange("p (j b r) -> p j b r", b=2, r=2), reads=[("saves", j) for j in range(NJ)], key="osconv")
                atoks = [("actb", j, half) for j in range(NJ)]
                nxt = list(range(8, 17)) if half == 0 else []
                for ti_, i in enumerate(tiles):
                    rows = tile_rows(i)
                    c0, n = tcols(i)
                    lc0 = c0 - base
                    defer_b = []
                    for i2 in (nxt[ti_:ti_ + 1] if ti_ < 7 else nxt[7:8]):
                        rows2 = tile_rows(i2)
                        c02, n2 = tcols(i2)
                        defer_b.append(prenorm_tile(X[0:rows2, i2, :], rows2, ("x", i2), prm(l, "gpf", 8), H23(c02 - 1024, n2),
                                                    ("h2", i2 - 8), 0, hb=HBF[0:rows2, :], tpool="t", hb_toks=["hbf"], defer=True))
                        if "nodefer" in DBG:
                            defer_b.pop()()
                    for hf in range(2):
                        kd = psnext("d")
                        mm_group(None, [(PS[kd][0:rows, :], ACTB(j, lc0, n), WDN(j, 512 * hf, 512), j == 0, j == NJ - 1) for j in range(NJ)],
                                 reads=atoks + [("wdn", 0), ("wdn", 1)], writes=[("ps", kd)])
                        if hf == 0:
                            S.op("act", lambda e, kd=kd, rows=rows: e.copy(OSTG[0:rows, 0:512], PS[kd][0:rows, :]),
                                 reads=[("ps", kd)], writes=["ostg0"])
                        else:
                            S.op("dve", lambda e, kd=kd, rows=rows: e.tensor_copy(OSTG[0:rows, 512:1024], PS[kd][0:rows, :]),
                                 reads=[("ps", kd)], writes=["ostg1"])
                    for fb in defer_b:
                        fb()
                    post_norm_residual(OSTG[0:rows, :], rows, i, HBF[0:rows, :], ["hbf"], GPF, ["gpf"], ["ostg0", "ostg1"])
                    if l == L - 1:
                        if i < 16:
                            dma("sp", y_p[128 * i:128 * i + 128, :], X[:, i, :], reads=[("x", i)], key=("x", i))
                        else:
                            dma("sp", y_s, X[0:32, 16, :], reads=[("x", 16)], key=("x", 16))
                if half == 0:
                    prenorm_tile(X[0:32, 16, :], 32, ("x", 16), prm(l, "gpf", 8), H23(1024, 32), ("h2", 8), 0,
                                 hb=HBF[0:32, :], tpool="t", hb_toks=["hbf"])
        RFOX = lambda i, rows=128: SM[0:rows, 100 + i:101 + i]
        RSGU = lambda i, rows=128: SM[0:rows, 120 + i:121 + i]
        RMEM = lambda i, rows=128: SM[0:rows, 140 + i:141 + i]
        SSFOX = lambda i, rows=128: SM[0:rows, 160 + i:161 + i]
        SSMEM = lambda i, rows=128: SM[0:rows, 180 + i:181 + i]
        ALLHT = [("ht", i) for i in range(17)]

        class NSQ:
            def __init__(self):
                self.p1 = None
                self.p2 = None
                self.p3 = None

            def step(self, new_p1):
                if self.p3 is not None:
                    self.p3()
                    self.p3 = None
                if self.p2 is not None:
                    self.p3 = self.p2()
                    self.p2 = None
                if self.p1 is not None:
                    self.p2 = self.p1()
                self.p1 = new_p1

            def flush(self):
                self.step(None)
                self.step(None)
                self.step(None)
        nsq = NSQ()

        def wload(slot, src2d, ncol, extra_writes=()):
            dma("pool", WS3(slot, ncol), src2d.rearrange("(c p) n -> p c n", p=128),
                writes=[("ws", slot)] + list(extra_writes), key=("ws", slot))

        def KAS(b, c0, n, p0=0, p1=67):
            return bf(WS_O[0] + b * 3200 + c0 * 2, n, p0, p1)

        def VAS(b, j):
            return bf(WS_O[0] + b * 3200 + 2080 + j * 130, 65)

        def VAS3(b):
            return AR[:, (WS_O[0] + b * 3200 + 2080) // 2:(WS_O[0] + b * 3200 + 2080) // 2 + 520].rearrange("p (j k) -> p j k", k=65)

        SC_TOKS = [("sc", b, x) for b in range(2) for x in ("k", "v", "r", "one")]

        def norm_store_fm(obank, I, grp, chunk, p0, ssq4, l):
            rlrow = f32(SCR_O + 6 * 1024, 512, 64, 65)
            hfT = f32(SCR_O + 6 * 1024, 512, 0, 64)
            S.op("act", lambda e: e.activation(rlrow, PS[obank][64:65, 0:512], AF.Ln), reads=[("ps", obank)], writes=[("scr", 6, "r")])
            S.op("act", lambda e: e.activation(rlrow, rlrow, AF.Exp, scale=-1.0), reads=[("scr", 6, "r")], writes=[("scr", 6, "r")])
            kb = psnext("g")
            S.op("pe", lambda e: e.matmul(PS[kb][0:64, 0:512], ONEROW[64:65, 0:64], rlrow, start=True, stop=True, skip_group_check=True),
                 reads=[("scr", 6, "r"), "onerow"], writes=[("ps", kb)])
            S.op("dve", lambda e: e.tensor_copy(hfT, PS[kb][0:64, 0:512]), reads=[("ps", kb)], writes=scr(6, 7))
            S.op("dve", lambda e: e.tensor_tensor(hfT, PS[obank][0:64, 0:512], hfT, ALU.mult), reads=[("ps", obank)] + scr(6, 7), writes=scr(6, 7))
            return lambda: norm_store_fm2(I, grp, chunk, p0, ssq4, l)

        def norm_store_fm2(I, grp, chunk, p0, ssq4, l):
            hfT = f32(SCR_O + 6 * 1024, 512, 0, 64)
            sqb = bf(SCR_O + 4 * 1024, 512, 0, 64)
            S.op("act", lambda e: e.activation(MX(chunk, 512 * I, 512, p0, p0 + 64), hfT, AF.Copy,
                                               scale=prm(l, "ggo", 1, p0, p0 + 64, chunk)),
                 reads=scr(6, 7) + ["prm"], writes=[("mx", chunk, 4 * I + s_, p0) for s_ in range(4)])
            S.op("pool", lambda e: e.tensor_tensor(sqb, hfT, hfT, ALU.mult), reads=scr(6, 7), writes=scr(4))
            return lambda: norm_store_fm3(I, grp, ssq4)

        def norm_store_fm3(I, grp, ssq4):
            sqb = bf(SCR_O + 4 * 1024, 512, 0, 64)
            kt = psnext("t")
            ones_col = bf(VA_O + 64 * 2, 1, 0, 64)

            def ssmm(e):
                ins = None
                for s_ in range(4):
                    ins = e.matmul(PS[kt][:, s_:s_ + 1], sqb[:, 128 * s_:128 * s_ + 128], ones_col, start=True, stop=True,
                                   skip_group_check=True)
                return ins
            S.op("pe", ssmm, reads=scr(4) + ["va_ones"], writes=[("ps", kt)])
            toks4 = [("ssq", grp, 4 * I + s_) for s_ in range(4)]
            S.op("dve", lambda e: e.tensor_tensor(ssq4, ssq4, PS[kt][:, 0:4], ALU.add), reads=[("ps", kt)] + toks4, writes=toks4)

        def norm_and_store_block(obank, I, grp, chunk, p0, ssq4, l):
            O4 = PS[obank][:, 0:260].rearrange("p (s k) -> p s k", k=65)
            rl4 = SM[:, SM_T + 10:SM_T + 14]
            sq4 = SM[:, SM_T + 14:SM_T + 18]
            S.op("dve", lambda e: e.reciprocal(rl4.unsqueeze(2), O4[:, :, 64:65]), reads=[("ps", obank)], writes=["rl4"])
            hf4 = SCRf(6, 256)
            hb4 = SCRb(7, 256)
            S.op("dve", lambda e: e.tensor_tensor(hf4.rearrange("p (s k) -> p s k", k=64), O4[:, :, 0:64],
                                                  rl4.unsqueeze(2).to_broadcast([128, 4, 64]), ALU.mult),
                 reads=[("ps", obank), "rl4"], writes=scr(6))

            def sqs(e):
                ins = None
                for s_ in range(4):
                    ins = e.activation(hb4[:, 64 * s_:64 * s_ + 64], hf4[:, 64 * s_:64 * s_ + 64], AF.Square,
                                       accum_out=sq4[:, s_:s_ + 1])
                return ins
            S.op("act", sqs, reads=scr(6), writes=scr(7) + ["sq4"])
            toks4 = [("ssq", grp, 4 * I + s_) for s_ in range(4)]
            S.op("dve", lambda e: e.tensor_tensor(ssq4, ssq4, sq4, ALU.add), reads=["sq4"] + toks4, writes=toks4)
            S.op("act", lambda e: e.copy(hb4, hf4), reads=scr(6, 7), writes=scr(7))
            kt = psnext("t")

            def tr(e):
                ins = None
                for s_ in range(4):
                    ins = e.transpose(PSB[kt][p0:p0 + 64, 128 * s_:128 * s_ + 128], hb4[:, 64 * s_:64 * s_ + 64], IDENT[:, :])
                return ins
            S.op("pe", tr, reads=scr(7) + ["cst"], writes=[("ps", kt)])
            S.op("dve", lambda e: e.tensor_scalar(MX(chunk, 512 * I, 512, p0, p0 + 64), PSB[kt][p0:p0 + 64, 0:512],
                                                  prm(l, "ggo", 1, p0, p0 + 64, chunk), None, ALU.mult),
                 reads=[("ps", kt), "prm"], writes=[("mx", chunk, 4 * I + s_, p0) for s_ in range(4)])

        def norm_and_store(obank, ocol, rows, tile_i, grp, chunk, p0, ssq_acc, l):
            rl = SM[0:rows, SM_T + 4:SM_T + 5]
            S.op("dve", lambda e: e.reciprocal(rl, PS[obank][0:rows, ocol + 64:ocol + 65]),
                 reads=[("ps", obank)], writes=["rl"])
            hf = SCRf(5, 64, 0, rows)
            hb = SCRb(5, 64, 0, rows, off=256)
            S.op("dve", lambda e: e.tensor_scalar(hf, PS[obank][0:rows, ocol:ocol + 64], rl, None, ALU.mult),
                 reads=[("ps", obank), "rl"], writes=scr(5))
            sq = SM[0:rows, SM_T + 5:SM_T + 6]
            S.op("act", lambda e: e.activation(hb, hf, AF.Square, accum_out=sq), reads=scr(5), writes=scr(5) + ["sq"])
            S.op("dve", lambda e: e.tensor_tensor(ssq_acc, ssq_acc, sq, ALU.add),
                 reads=["sq", ("ssq", grp, tile_i)], writes=[("ssq", grp, tile_i)])
            S.op("act", lambda e: e.copy(hb, hf), reads=scr(5), writes=scr(5))
            kt = psnext("t")
            c0, n = tcols(tile_i)
            S.op("pe", lambda e: e.transpose(PSB[kt][p0:p0 + 64, 0:rows], hb, IDENT[0:rows, 0:rows]),
                 reads=scr(5) + ["cst"], writes=[("ps", kt)])
            S.op("dve", lambda e: e.tensor_scalar(MX(chunk, c0, n, p0, p0 + 64), PSB[kt][p0:p0 + 64, 0:rows],
                                                  prm(l, "ggo", 1, p0, p0 + 64, chunk), None, ALU.mult),
                 reads=[("ps", kt), "prm"], writes=[("mx", chunk, tile_i, p0)])

        try:
            for l in range(L):
                psr.clear()
                psr.update({"g": [0, 1], "s": [2, 3, 4], "o": [5, 6], "t": [7], "w": [0, 1, 2, 3, 4, 5]})
                for kk in psr:
                    psi[kk] = 0
                S.op("dve", lambda e: e.memset(
                    AR[:, VA_O // 2: VA_O // 2 + 17 * 8 * 65].rearrange("p (a k) -> p a k", k=65)[:, :, 64:65], 1.0),
                    writes=["va_ones"])
                S.op("dve", lambda e: e.memset(MVAt[:, :].rearrange("p (a k) -> p a k", k=65)[:, :, 64:65], 1.0),
                     writes=["mva_ones"])
                S.op("dve", lambda e: e.memset(MKTt[64:128, :], 0.0), writes=["mkt_zero"])
                S.op("dve", lambda e: e.memset(bf(VA_O + 16 * 520 * 2, 520, 32, 64), 0.0), writes=["va_pad"])
                S.op("dve", lambda e: e.memset(bf(VA_O + 16 * 520 * 2, 520, 64, 128), 0.0), writes=["va_pad2"])
                S.op("dve", lambda e: e.memset(bf(MVA_O + 1040, 64), 0.0), writes=["mva_pad"])
                wload(0, w_in[l][:, 1024:1536], 512, extra_writes=SC_TOKS if l > 0 else ())
                wload(1, w_in[l][:, 512:1024], 512)
                wload(2, w_in[l][:, 1536:2056], 520)

                SGC = MX_O + 12288
                WSTl = bf(SGC, 512)
                GSl = f32(SGC + 1024, 256)
                dma("pool", WSTl.rearrange("p (g i) -> p g i", g=4), wsT[l].rearrange("g j i -> j g i"),
                    writes=[("sguc", 0)], key="wst")
                S.op("dve", lambda e: e.memset(bf(SGC, 512, 64, 128).rearrange("p (g i) -> p g i", g=4)[:, :, 0:64], 0.0),
                     reads=[("sguc", 0)], writes=[("sguc", 0)])
                dma("sp", GSl, g_sgu[l:l + 1, :].broadcast_to([128, 256]), writes=[("sguc", 1)], key="gs")

                def sgu_ops(i):
                    st_ = 0 if "setA" in DBG else i % 2
                    base = MX_O if st_ == 0 else MX_O + 6144
                    tk = "sguA" if st_ == 0 else "sguB"
                    Sb = lambda g, n, p0=0, p1=128: bf(base + g * 1024, n, p0, p1)
                    Sf = lambda g, n, p0=0, p1=128: f32(base + g * 1024, n, p0, p1)
                    sc = lambda *gs: [(tk, g) for g in gs]
                    rows = tile_rows(i)
                    c0, n = tcols(i)
                    s1, s2 = [], []
                    k = psnext("s")
                    z = Sf(0, 512, 0, rows)
                    t1 = Sf(2, 512, 0, rows)
                    ssv = SM[0:rows, SM_T + 20 + 2 * st_:SM_T + 21 + 2 * st_]
                    sss = SM[0:rows, SM_T + 21 + 2 * st_:SM_T + 22 + 2 * st_]
                    ssvt, ssst = ("ssv", st_), ("sss", st_)
                    vvf = Sf(3, 256, 0, rows)
                    vvb = Sb(2, 256, 0, rows)
                    sg = Sf(4, 256, 0, rows)
                    sgb = Sb(5, 256, 0, rows)
                    s1.append(lambda: mm_group(None, [(PS[k][0:rows, :], HT(c, c0, n), WS(2, c, 8, 512), c == 0, c == 7) for c in range(8)],
                                               reads=[("ht", i), ("ws", 2)], writes=[("ps", k)]))
                    s1.append(lambda: S.op("dve", lambda e: e.tensor_copy(z, PS[k][0:rows, :]), reads=[("ps", k)], writes=sc(0, 1)))
                    s1.append(lambda: S.op("dve", lambda e: e.tensor_tensor(t1, z, z, ALU.mult), reads=sc(0, 1), writes=sc(2, 3)))
                    s1.append(lambda: S.op("dve", lambda e: e.tensor_scalar(t1, t1, 0.044715, 1.0, ALU.mult, ALU.add), reads=sc(2, 3), writes=sc(2, 3)))
                    s1.append(lambda: S.op("dve", lambda e: e.tensor_tensor(t1, t1, z, ALU.mult), reads=sc(0, 1, 2, 3), writes=sc(2, 3)))
                    s1.append(lambda: S.op("act", lambda e: e.activation(t1, t1, AF.Exp, scale=-1.5957691216057308), reads=sc(2, 3), writes=sc(2, 3)))
                    s1.append(lambda: S.op("act", lambda e: e.activation(t1, t1, AF.Ln, bias=1.0, scale=1.0), reads=sc(2, 3), writes=sc(2, 3)))
                    s1.append(lambda: S.op("act", lambda e: e.activation(t1, t1, AF.Exp, scale=-1.0), reads=sc(2, 3), writes=sc(2, 3)))
                    s1.append(lambda: S.op("dve", lambda e: e.tensor_tensor(z, z, t1, ALU.mult), reads=sc(0, 1, 2, 3), writes=sc(0, 1)))
                    s1.append(lambda: S.op("act", lambda e: e.activation(t1[:, 0:256], z[:, 256:512], AF.Square, accum_out=ssv),
                                           reads=sc(0, 1), writes=sc(2) + [ssvt]))
                    s1.append(lambda: S.op("act", lambda e: e.activation(ssv, ssv, AF.Ln, bias=EPS, scale=1.0 / 256), reads=[ssvt], writes=[ssvt]))
                    s1.append(lambda: S.op("act", lambda e: e.activation(ssv, ssv, AF.Exp, scale=-0.5), reads=[ssvt], writes=[ssvt]))
                    s1.append(lambda: S.op("dve", lambda e: e.scalar_tensor_tensor(vvf, z[:, 256:512], ssv, GSl[0:rows, :], ALU.mult, ALU.mult),
                                           reads=sc(0, 1) + [("sguc", 1)] + [ssvt], writes=sc(3)))
                    s1.append(lambda: S.op("dve", lambda e: e.tensor_copy(vvb, vvf), reads=sc(3), writes=sc(2)))
                    if i == 16:
                        s1.append(lambda: dma("sp", ogv[l], vvf, reads=sc(3), key="ogv"))
                    k2 = psnext("s")
                    if i < 16:
                        items = [(PS[k2][0:128, 64 * g:64 * g + 64], WSTl[:, 128 * g:128 * g + 128], vvb[:, 64 * g:64 * g + 64], True, True) for g in range(4)]
                        bsap = prm(l, "bs", 4)
                        wtok = [("sguc", 0)]
                    else:
                        items = [(PS[k2][0:32, 64 * g:64 * g + 64], WSTS(l, g), vvb[:, 64 * g:64 * g + 64], True, True) for g in range(4)]
                        bsap = prm(l, "bss", 4, 0, 32)
                        wtok = ["wsts"] + [("wsts", l, b) for b in range(2)]
                    s1.append(lambda: mm_group(None, items, reads=sc(2) + wtok, writes=[("ps", k2)]))
                    s2.append(lambda: S.op("dve", lambda e: e.tensor_tensor(
                        sg.rearrange("p (g d) -> p g d", g=4), PS[k2][0:rows, 0:256].rearrange("p (g d) -> p g d", g=4),
                        bsap.unsqueeze(2).to_broadcast([rows, 4, 64]), ALU.add),
                        reads=[("ps", k2), "prm"], writes=sc(4)))
                    s2.append(lambda: S.op("dve", lambda e: e.tensor_tensor(sg, sg, z[:, 0:256], ALU.mult), reads=sc(0, 1, 4), writes=sc(4)))
                    s2.append(lambda: S.op("act", lambda e: e.activation(sgb, sg, AF.Square, accum_out=sss),
                                           reads=sc(4), writes=sc(5) + [ssst]))
                    s2.append(lambda: S.op("act", lambda e: e.activation(sss, sss, AF.Ln, bias=EPS, scale=1.0 / 256), reads=[ssst], writes=[ssst]))
                    s2.append(lambda: S.op("act", lambda e: e.activation(RSGU(i, rows), sss, AF.Exp, scale=-0.5),
                                           reads=[ssst], writes=[("rsgu", i)]))
                    s2.append(lambda: S.op("dve", lambda e: e.tensor_copy(sgb, sg), reads=sc(4, 5), writes=sc(5)))

                    def trs():
                        kt = psnext("o")

                        def tr(e):
                            ins = None
                            for cc in range(2):
                                ins = e.transpose(PSB[kt][:, cc * 128:cc * 128 + rows], sgb[:, cc * 128:(cc + 1) * 128], IDENT[0:rows, 0:rows])
                            return ins
                        S.op("pe", tr, reads=sc(5) + ["cst"], writes=[("ps", kt)])
                        for cc in range(2):
                            S.op("dve", lambda e, cc=cc, l=l: e.tensor_scalar(
                                MX(4 + cc, c0, n), PSB[kt][:, cc * 128:cc * 128 + rows], prm(l, "ggo", 1, 0, 128, 4 + cc), None, ALU.mult),
                                reads=[("ps", kt), "prm"], writes=[("mx", 4 + cc, i)])
                    s2.append(trs)
                    return s1, s2

                def interleave(a, b):
                    for q in range(max(len(a), len(b))):
                        if q < len(a):
                            a[q]()
                        if q < len(b):
                            b[q]()

                def vk_tile(i):
                    rows = tile_rows(i)
                    c0, n = tcols(i)
                    for which, slot, oprompt, osample, sg in (("v", 0, ofv, osv, 4), ("k", 1, ofk, osk, 6)):
                        k = psnext("g")
                        mm_group(None, [(PS[k][0:rows, :], HT(c, c0, n), WS(slot, c, 0, 512), c == 0, c == 7)
                                        for c in range(8)],
                                 reads=[("ht", i), ("ws", slot)], writes=[("ps", k)])
                        stg = SCRf(sg, 512, 0, rows)
                        S.op(("act" if which == "v" else "dve"), (lambda e, stg=stg, k=k, rows=rows: e.copy(stg, PS[k][0:rows, :])) if which == "v"
                             else (lambda e, stg=stg, k=k, rows=rows: e.tensor_copy(stg, PS[k][0:rows, :])),
                             reads=[("ps", k)], writes=scr(sg, sg + 1))
                        if which == "v":
                            va3 = AR[0:rows, VA_O // 2 + i * 520: VA_O // 2 + (i + 1) * 520].rearrange(
                                "p (h k) -> p h k", k=65)[:, :, 0:64]
                            S.op("dve", lambda e, va3=va3, k=k, rows=rows: e.tensor_copy(
                                va3, PS[k][0:rows, :].rearrange("p (h k) -> p h k", k=64)),
                                reads=[("ps", k), "va_ones"], writes=[("va", i)])
                        dst = oprompt[l, 128 * i:128 * i + 128, :] if i < 16 else osample[l, :, :]
                        dma("sp", dst, stg, reads=scr(sg, sg + 1), key=("stg", sg))

                sgu_pipe = {"B": [], "C": []}

                def sgu_step(t):
                    if t is not None:
                        s1_, s2_ = sgu_ops(t)
                        sA, sB, sC = s1_[:-1], [s1_[-1]] + s2_[:-1], [s2_[-1]]
                    else:
                        sA, sB, sC = [], [], []
                    interleave(sA, sgu_pipe["B"])
                    for f_ in sgu_pipe["C"]:
                        f_()
                    sgu_pipe["C"] = sgu_pipe["B_c"] if "B_c" in sgu_pipe else []
                    sgu_pipe["B"] = sB
                    sgu_pipe["B_c"] = sC

                def pn_a(i):
                    rows = tile_rows(i)
                    c0, n = tcols(i)
                    return prenorm_tile(X[0:rows, i, :], rows, ("x", i), prm(l, "gpm", 8), HT3(c0, n), ("ht", i), i % 2, defer=True)
                pn_b = {0: pn_a(0)}
                for i in range(17):
                    if i + 1 < 17:
                        pn_b[i + 1] = pn_a(i + 1)
                    pn_b.pop(i)()
                    if i >= 1:
                        vk_tile(i - 1)
                        sgu_step(i - 1)
                vk_tile(16)
                sgu_step(16)
                sgu_step(None)
                sgu_step(None)
                S.op("dve", lambda e: e.memset(bf(MX_O + 15 * 1024, 2), 0.0),
                     writes=[("sguA", g) for g in range(6)] + [("sguB", g) for g in range(6)] + [("sguc", 0), ("sguc", 1)]
                     + [("mx", c, i, p0) for c in (0, 1, 2, 3) for i in range(17) for p0 in (0, 64)])

                _chk("vk%d" % l)
                NB = SM[0:8, 210:211]
                S.op("dve", lambda e, l=l: e.tensor_scalar(NB, prm(l, "bf", 1, 0, 8), -1.0, None, ALU.mult),
                     reads=["prm"], writes=["cw", "nb"])
                for bi, (c0, n) in enumerate(BLKS):
                    k = psnext("g")
                    mm_group(None, [(PS[k][0:8, 0:n], WS(2, c, 0, 8), HT(c, c0, n), c == 0, c == 7) for c in range(8)],
                             reads=ALLHT + [("ws", 2)], writes=["cw", ("ps", k)])
                    tmp = SCRf(0, 512, 96, 104)
                    S.op("act", lambda e, k=k, n=n, tmp=tmp: e.activation(tmp[:, 0:n], PS[k][0:8, 0:n], AF.Exp, bias=NB, scale=-1.0),
                         reads=[("ps", k), "nb"], writes=scr(0, 1))
                    S.op("act", lambda e, n=n, tmp=tmp: e.activation(tmp[:, 0:n], tmp[:, 0:n], AF.Ln, bias=1.0, scale=1.0),
                         reads=scr(0, 1), writes=scr(0, 1))
                    S.op("dve", lambda e, c0=c0, n=n, tmp=tmp: e.tensor_scalar(CT(c0, n), tmp[:, 0:n], -1.0, None, ALU.mult),
                         reads=scr(0, 1), writes=["cw", ("ct", bi)])
                dma("sp", olf[l], CT(0, 2048), reads=[("ct", b) for b in range(4)], writes=["cw"], key="olf")
                dma("sp", oslf[l], CT(2048, 32), reads=[("ct", 4)], writes=["cw"], key="olf2")
                SLF = SM[96:104, 220:252]
                S.op("dve", lambda e: e.tensor_copy(SLF, CT(2048, 32)), reads=[("ct", 4)], writes=["cw", "slf"])
                S.op("dve", lambda e: e.tensor_tensor_scan(CT(0, 2048), ONE8.to_broadcast([8, 2048]), CT(0, 2048), 0.0, ALU.mult, ALU.add),
                     reads=[("ct", b) for b in range(4)] + ["one"], writes=["cw", "ctp"])
                S.op("dve", lambda e: e.tensor_copy(HI(0, 2048), CT(0, 2048)), reads=["ctp"], writes=["cw", "hi"])
                S.op("dve", lambda e: e.tensor_tensor(CT(0, 2048), CT(0, 2048), HI(0, 2048), ALU.subtract),
                     reads=["ctp", "hi"], writes=["cw", "ctp"])
                S.op("dve", lambda e: e.tensor_copy(LO(0, 2048), CT(0, 2048)), reads=["ctp"], writes=["cw", "lo"])
                dma("sp", cscr_p[:, 0, :], HI(0, 2048), reads=["hi"], writes=["cw", "cscr_p"], key="cscr0")
                dma("sp", cscr_p[:, 1, :], LO(0, 2048), reads=["lo"], writes=["cw", "cscr_p2"], key="cscr1")
                CS = lambda b, c0, n: CT(b * 1040 + c0, n)
                for b in range(2):
                    dma("sp", CS(b, 0, 1024), clfT[l, b], reads=["ctp", "lo", ("ct", 4), "slf"], writes=["cw", ("cs", b)], key=("csl", b))
                    S.op("dve", lambda e, b=b: e.tensor_copy(CS(b, 1024, 16), SM[96:104, 220 + 16 * b:236 + 16 * b]),
                         reads=["slf", "ctp", "lo", ("ct", 4)], writes=["cw", ("cs2", b)])
                    S.op("dve", lambda e, b=b: e.tensor_tensor_scan(CS(b, 0, 1040), ONE8.to_broadcast([8, 1040]), CS(b, 0, 1040), 0.0, ALU.mult, ALU.add),
                         reads=[("cs", b), ("cs2", b), "one"], writes=["cw", ("csc", b)])
                S.op("dve", lambda e: e.tensor_copy(HI(0, 2080), CT(0, 2080)),
                     reads=[("csc", 0), ("csc", 1), "cscr_p", "cscr_p2"], writes=["cw", "hi"])
                S.op("dve", lambda e: e.tensor_tensor(CT(0, 2080), CT(0, 2080), HI(0, 2080), ALU.subtract),
                     reads=["hi"], writes=["cw", ("csc", 0), ("csc", 1)])
                S.op("dve", lambda e: e.tensor_copy(LO(0, 2080), CT(0, 2080)), reads=[("csc", 0), ("csc", 1), "cscr_p2"], writes=["cw", "lo"])
                for b in range(2):
                    dma("sp", cscr_s[:, 0, b, :], HI(b * 1040, 1040), reads=["hi"], writes=["cw", ("cscr_s", b, 0)], key=("cscrs", b, 0))
                    dma("sp", cscr_s[:, 1, b, :], LO(b * 1040, 1040), reads=["lo"], writes=["cw", ("cscr_s", b, 1)], key=("cscrs", b, 1))

                _chk("fg%d" % l)
                _chk("sgu%d" % l)
                S.op("dve", lambda e: e.memset(bf(QK_O, 4 * NT, 64, 128), 0.0),
                     writes=["cw", "hi", "lo"] + [("qkc", s_) for s_ in range(4)])
                for s_ in (0, 2):
                    S.op("dve", lambda e, s_=s_: e.memset(QK(s_, 0, NT, 64, 67), -1.0), writes=[("qkc", s_)])
                for s_ in (1, 3):
                    S.op("dve", lambda e, s_=s_: e.memset(QK(s_, 0, NT, 64, 65), 1.0), writes=[("qkc", s_)])
                wload(0, w_mkv[l], 512)
                MEMT3 = AR[:, (SCR_O + 4096) // 2:(SCR_O + 4096) // 2 + 2048].rearrange("p (c t) -> p c t", c=8)
                MEMT = lambda c, c0, n: bf(SCR_O + 4096 + (c * 256 + c0) * 2, n)
                memx = f32(WS_O[2], 1024)
                for mt in range(2):
                    dma("sp", memx, mem[128 * mt:128 * mt + 128, :], writes=[("ws", 2)], key="memx")
                    prenorm_tile(memx, 128, ("ws", 2), prm(l, "gmem", 8), MEMT3[:, :, 128 * mt:128 * mt + 128],
                                 ("memt", mt), 0, extra_writes=scr(4, 5, 6, 7))
                memt_toks = [("memt", 0), ("memt", 1)] + scr(4, 5, 6, 7)
                stgk = f32(WS_O[2], 512)
                for mt in range(2):
                    k = psnext("g")
                    mm_group(None, [(PS[k][:, :], MEMT(c, 128 * mt, 128), WS(0, c, 0, 512), c == 0, c == 7) for c in range(8)],
                             reads=memt_toks + [("ws", 0)], writes=[("ps", k)])
                    S.op("act", lambda e, k=k: e.copy(stgk, PS[k][:, :]), reads=[("ps", k)], writes=[("ws", 2)])
                    S.op("dve", lambda e, k=k, mt=mt: e.tensor_copy(
                        MVAt[:, mt * 260:(mt + 1) * 260].rearrange("p (h k) -> p h k", k=65)[:, :, 0:64],
                        PS[k][:, 256:512].rearrange("p (h k) -> p h k", k=64)),
                        reads=[("ps", k), "mva_ones"], writes=[("mva", mt)])
                    dma("sp", omk[l, 128 * mt:128 * mt + 128, :], stgk[:, 0:256], reads=[("ws", 2)], key="omk")
                    dma("sp", omv[l, 128 * mt:128 * mt + 128, :], stgk[:, 256:512], reads=[("ws", 2)], key="omv")
                for pr in range(2):
                    k = psnext("g")
                    mm_group(None, [(PS[k][:, 0:256], WS(0, c, 128 * pr, 128), MEMT(c, 0, 256), c == 0, c == 7) for c in range(8)],
                             reads=memt_toks + [("ws", 0)], writes=[("ps", k)])
                    for hh in range(2):
                        h = 2 * pr + hh
                        S.op("act", lambda e, k=k, hh=hh, h=h: e.copy(MKTt[0:64, h * 256:(h + 1) * 256], PS[k][64 * hh:64 * hh + 64, 0:256]),
                             reads=[("ps", k)], writes=[("mkt", h)])
                wload(2, w_in[l][:, 2056:2312], 256)
                for b in range(2):
                    S.op("dve", lambda e, b=b: e.memset(KAS(b, 0, 1040, 64, 65), 1.0), reads=[("ws", 0)], writes=[("ws", 0), ("sc", b, "one")])
                    S.op("dve", lambda e, b=b: e.memset(VAS3(b)[:, :, 64:65], 1.0), reads=[("ws", 0)], writes=[("ws", 0), ("sc", b, "one")])

                S.op("dve", lambda e: e.memset(SM[:, 160:200], 0.0),
                     writes=[("ssq", g, i) for g in (0, 2) for i in range(17)])

                _chk("memkv%d" % l)
                for pr in range(2):
                    for bi, (c0, n) in enumerate(BLKS):
                        k = psnext("g")
                        mm_group(None, [(PS[k][:, 0:n], WS(2, c, 128 * pr, 128), HT(c, c0, n), c == 0, c == 7) for c in range(8)],
                                 reads=ALLHT + [("ws", 2)], writes=[("ps", k)])
                        for hh in range(2):
                            S.op("act", lambda e, k=k, hh=hh, c0=c0, n=n: e.activation(
                                QK(2 * hh, c0, n, 0, 64), PS[k][64 * hh:64 * hh + 64, 0:n], AF.Copy, scale=0.125),
                                reads=[("ps", k)], writes=[("qa", hh, bi)])
                    for hh in range(2):
                        h = 2 * pr + hh
                        for I in range(4):
                            ob = psnext("o")
                            kts = [dict(ka=MKTt[0:128, h * 256 + 128 * j:h * 256 + 128 * j + 128], nk=128,
                                        va=bf(MVA_O + ((j * 4 + h) * 65) * 2, 128), lo=0, hi=512, pv_lo=0,
                                        mask=None, pt="rot", reads=[("mkt", h), ("qa", hh, I), "mkt_zero", ("qkc", 0), ("qkc", 2)] + [("qar", hh, x_) for x_ in range(3)],
                                        vreads=[("mva", j), "mva_ones"])
                                   for j in range(2)]
                            osubs = [(PS[ob][:, 65 * s:65 * s + 65], 128 * s, 128 * s + 128, 0, 1) for s in range(4)]
                            attend(lambda c0, n, hh=hh, I=I: QK(2 * hh, 512 * I + c0, n, 0, 128), 128, kts, "fm", ob, "mem")
                            nsq.step(lambda ob=ob, I=I, pr=pr, hh=hh: norm_store_fm(ob, I, 2, 6 + pr, 64 * hh, SM[:, 180 + 4 * I:184 + 4 * I], l))
                        ob = psnext("g")
                        kts = []
                        for b in range(2):
                            dma("pool", KAS(b, 0, 256, 0, 64), cmkT[l, b, h], reads=[("ws", 0)], writes=[("sc", b, "k")], key=("sck", b))
                            dma("pool", VAS3(b)[:, 0:2, 0:64],
                                cmv[l, b].rearrange("(j p) (h d) -> p j h d", p=128, h=4)[:, :, h, :],
                                reads=[("ws", 0)], writes=[("sc", b, "v")], key=("scv", b), slow=True)
                            for j in range(2):
                                kts.append(dict(ka=KAS(b, 128 * j, 128, 0, 64), nk=128, va=VAS(b, j),
                                                lo=16 * b, hi=16 * b + 16, pv_lo=0, mask=None, pt=3 * b + (j % 3),
                                                reads=[("sc", b, "k"), ("qa", hh, 4)], vreads=[("sc", b, "v"), ("sc", b, "one")]))
                        osubs = [(PS[ob][0:32, 0:65], 0, 32, 0, len(kts) - 1)]
                        attend(lambda c0, n, hh=hh: QK(2 * hh, 2048 + c0, n, 0, 64), 64, kts, osubs, ob, "mems")
                        flush_pv()
                        norm_and_store(ob, 0, 32, 16, 2, 6 + pr, 64 * hh, SSMEM(16, 32), l)

                flush_pv()
                nsq.flush()
                _chk("memattn%d" % l)
                wload(2, w_in[l][:, 0:512], 512)
                for pr in range(4):
                    for bi, (c0, n) in enumerate(BLKS):
                        for qk, slot in ((0, 2), (1, 1)):
                            k = psnext("g")
                            mm_group(None, [(PS[k][:, 0:n], WS(slot, c, 128 * pr, 128), HT(c, c0, n), c == 0, c == 7) for c in range(8)],
                                     reads=ALLHT + [("ws", slot)], writes=[("ps", k)])
                            for hh in range(2):
                                if qk == 0:
                                    S.op("act", lambda e, k=k, hh=hh, c0=c0, n=n: e.activation(
                                        QK(2 * hh, c0, n, 0, 64), PS[k][64 * hh:64 * hh + 64, 0:n], AF.Copy, scale=0.125),
                                        reads=[("ps", k)], writes=[("qa", hh, bi)])
                                else:
                                    S.op("dve", lambda e, k=k, hh=hh, c0=c0, n=n: e.tensor_copy(
                                        QK(2 * hh + 1, c0, n, 0, 64), PS[k][64 * hh:64 * hh + 64, 0:n]),
                                        reads=[("ps", k)], writes=[("ka", hh, bi)])
                    if pr == 3:
                        wload(2, w_out[l][:, 0:512], 512)
                        wload(1, w_out[l][:, 512:1024], 512)
                    for hh in range(2):
                        h = 2 * pr + hh
                        dma("sp", QK(2 * hh, 0, 2048, 64, 65), cscr_p[h:h + 1, 0, :], reads=["cscr_p", ("qkc", 2 * hh)],
                            writes=[("qar", hh, 0)], key=("qar", hh))
                        dma("sp", QK(2 * hh + 1, 0, 2048, 65, 66), cscr_p[h:h + 1, 0, :], reads=["cscr_p", ("qkc", 2 * hh + 1)],
                            writes=[("kar", hh, 0)], key=("kar", hh))
                        dma("sp", QK(2 * hh + 1, 0, 2048, 66, 67), cscr_p[h:h + 1, 1, :], reads=["cscr_p2"],
                            writes=[("kar", hh, 1)], key=("kar", hh))
                        for b in range(2):
                            dma("sp", QK(2 * hh, 2048 + 16 * b, 16, 64, 65), cscr_s[h:h + 1, 0, b, 1024:1040],
                                reads=[("cscr_s", b, 0)], writes=[("qar", hh, 1 + b)], key=("qar", hh))
                            dma("sp", QK(2 * hh + 1, 2048 + 16 * b, 16, 65, 66), cscr_s[h:h + 1, 0, b, 1024:1040],
                                reads=[("cscr_s", b, 0)], writes=[("kar", hh, 2 + b)], key=("kar", hh))
                            dma("sp", QK(2 * hh + 1, 2048 + 16 * b, 16, 66, 67), cscr_s[h:h + 1, 1, b, 1024:1040],
                                reads=[("cscr_s", b, 1)], writes=[("kar", hh, 4 + b)], key=("kar", hh))
                        qar = [("qar", hh, x) for x in range(3)] + [("qkc", 2 * hh)]
                        kar = [("kar", hh, x) for x in range(6)] + [("qkc", 2 * hh + 1)]
                        for I in range(4):
                            ob = psnext("o")
                            kts = []
                            for j in range(4 * I + 4):
                                a = j - 4 * I
                                kts.append(dict(ka=QK(2 * hh + 1, 128 * j, 128, 0, 128), nk=128, va=VA128(j, h),
                                                lo=(128 * a if a >= 0 else 0), hi=512, pv_lo=(128 * a if a >= 0 else 0),
                                                mask=((MASKNEG, 128) if a >= 0 else None), pt="rot",
                                                reads=[("ka", hh, j // 4), ("qa", hh, I)] + qar + kar,
                                                vreads=[("va", j), "va_ones"]))
                            osubs = [(PS[ob][:, 65 * s:65 * s + 65], 128 * s, 128 * s + 128, 0, 4 * I + s) for s in range(4)]
                            attend(lambda c0, n, hh=hh, I=I: QK(2 * hh, 512 * I + c0, n, 0, 128), 128, kts, "fm", ob, "fox")
                            nsq.step(lambda ob=ob, I=I, pr=pr, hh=hh: norm_store_fm(ob, I, 0, pr, 64 * hh, SM[:, 160 + 4 * I:164 + 4 * I], l))
                        ob = psnext("g")
                        kts = []
                        for b in range(2):
                            dma("pool", KAS(b, 0, 1024, 0, 64), ckT[l, b, h], reads=[("ws", 0)], writes=[("sc", b, "k")], key=("sck", b))
                            dma("pool", VAS3(b)[:, :, 0:64],
                                cv[l, b].rearrange("(j p) (h d) -> p j h d", p=128, h=8)[:, :, h, :],
                                reads=[("ws", 0)], writes=[("sc", b, "v")], key=("scv", b), slow=True)
                            dma("sp", KAS(b, 0, 1024, 65, 66), cscr_s[h:h + 1, 0, b, 0:1024], reads=[("cscr_s", b, 0), ("ws", 0)],
                                writes=[("sc", b, "r")], key=("scr", b))
                            dma("sp", KAS(b, 0, 1024, 66, 67), cscr_s[h:h + 1, 1, b, 0:1024], reads=[("cscr_s", b, 1), ("ws", 0)],
                                writes=[("sc", b, "r2")], key=("scr", b))
                            for j in range(8):
                                kts.append(dict(ka=KAS(b, 128 * j, 128), nk=128, va=VAS(b, j),
                                                lo=16 * b, hi=16 * b + 16, pv_lo=0, mask=None, pt=3 * b + (j % 3),
                                                reads=[("sc", b, "k"), ("sc", b, "r"), ("sc", b, "r2"), ("sc", b, "one"), ("qa", hh, 4)] + qar,
                                                vreads=[("sc", b, "v"), ("sc", b, "one")]))
                        kts.append(dict(ka=QK(2 * hh + 1, 2048, 32, 0, 67), nk=32, va=VA(16, h, 32), lo=0, hi=32, pv_lo=0,
                                        mask=(MASKS, 32), pt=6, reads=[("ka", hh, 4), ("qa", hh, 4)] + qar + kar,
                                        vreads=[("va", 16), "va_ones"]))
                        osubs = [(PS[ob][0:32, 0:65], 0, 32, 0, len(kts) - 1)]
                        attend(lambda c0, n, hh=hh: QK(2 * hh, 2048 + c0, n, 0, 67), 67, kts, osubs, ob, "foxs")
                        flush_pv()
                        norm_and_store(ob, 0, 32, 16, 0, pr, 64 * hh, SSFOX(16, 32), l)

                flush_pv()
                nsq.flush()
                _chk("fox%d" % l)
                S.op("act", lambda e: e.activation(SM[:, 100:117], SM[:, 160:177], AF.Ln, bias=EPS, scale=1.0 / 512),
                     reads=[("ssq", 0, i) for i in range(17)], writes=["rfox"])
                S.op("act", lambda e: e.activation(SM[:, 100:117], SM[:, 100:117], AF.Exp, scale=-0.5), reads=["rfox"], writes=["rfox"])
                S.op("act", lambda e: e.activation(SM[:, 140:157], SM[:, 180:197], AF.Ln, bias=EPS, scale=1.0 / 256),
                     reads=[("ssq", 2, i) for i in range(17)], writes=["rmem"])
                S.op("act", lambda e: e.activation(SM[:, 140:157], SM[:, 140:157], AF.Exp, scale=-0.5), reads=["rmem"], writes=["rmem"])

                GP = SCRf(4, 1024)
                dma("sp", GP, g_post_mix[l:l + 1, :].broadcast_to([128, D]), writes=scr(4, 5, 6, 7), key="gp")
                for i in range(17):
                    rows = tile_rows(i)
                    c0, n = tcols(i)
                    ostg = SCRf(0, 1024, 0, rows)
                    mxr = [("mx", c, i, p0) for c in (0, 1, 2, 3, 6, 7) for p0 in (0, 64)] + [("mx", 4, i), ("mx", 5, i)]
                    last_b = None
                    for hf in range(2):
                        banks = [psnext("w") for _ in range(3)]
                        for gi, (cs, b) in enumerate(zip(((0, 1, 2, 3), (4, 5), (6, 7)), banks)):
                            wsl = 2 if hf == 0 else 1
                            mm_group(None, [(PS[b][0:rows, :], MX(c, c0, n), WS(wsl, c, 0, 512), c == cs[0], c == cs[-1]) for c in cs],
                                     reads=mxr + [("ws", wsl)], writes=[("ps", b)])
                        o_h = ostg[:, 512 * hf:512 * hf + 512]
                        S.op("act", lambda e, o_h=o_h, b=banks[0], rows=rows, i=i: e.activation(o_h, PS[b][0:rows, :], AF.Copy, scale=RFOX(i, rows)),
                             reads=[("ps", banks[0]), "rfox"], writes=scr(2 * hf, 2 * hf + 1))
                        S.op("dve", lambda e, o_h=o_h, b=banks[1], rows=rows, i=i: e.scalar_tensor_tensor(o_h, PS[b][0:rows, :], RSGU(i, rows), o_h, ALU.mult, ALU.add),
                             reads=[("ps", banks[1]), ("rsgu", i)] + scr(2 * hf, 2 * hf + 1), writes=scr(2 * hf, 2 * hf + 1))
                        S.op("dve", lambda e, o_h=o_h, b=banks[2], rows=rows, i=i: e.scalar_tensor_tensor(o_h, PS[b][0:rows, :], RMEM(i, rows), o_h, ALU.mult, ALU.add),
                             reads=[("ps", banks[2]), "rmem"] + scr(2 * hf, 2 * hf + 1), writes=scr(2 * hf, 2 * hf + 1))
                        last_b = banks
                    post_norm_residual(ostg, rows, i, bf(WS_O[0], 1024, 0, rows), [("ws", 0)] + SC_TOKS, GP, scr(4, 5, 6, 7), scr(0, 1, 2, 3))

                _chk("wout%d" % l)
                S.barrier()
                ffn_phase(l)
                if l < L - 1:
                    S.barrier()

        except _Stop:
            pass
        S.emit(st)
    return nc


_NC_CACHE = {}


def _prep_inputs(inp):
    f = lambda a: np.ascontiguousarray(np.asarray(a, dtype=np.float32))
    x_prompt = f(inp["x_prompt"]); x_sample = f(inp["x_sample"]); mem_prompt = f(inp["mem_prompt"])
    cfk = f(inp["cache_fox_k"]); cfv = f(inp["cache_fox_v"]); clf = f(inp["cache_fox_logf"])
    cmk = f(inp["cache_mem_k"]); cmvv = f(inp["cache_mem_v"]); cconv = f(inp["cache_ffn_conv"])
    ckT_all = np.ascontiguousarray(cfk.transpose(0, 1, 3, 4, 2))
    cv_all = cfv.reshape(L, 16, 1024, 512)
    clfT_all = np.ascontiguousarray(clf.transpose(0, 1, 3, 2))
    cmkT_all = np.ascontiguousarray(cmk.transpose(0, 1, 3, 4, 2))
    cmv_all = cmvv.reshape(L, 16, 256, 256)
    wsT = np.ascontiguousarray(f(inp["w_spatial"]).transpose(0, 1, 3, 2))
    fm8 = lambda g: f(g).reshape(L, 8, 128).transpose(0, 2, 1)
    b_sp = f(inp["b_spatial"])
    w_dw = f(inp["w_dwconv"]); b_dw = f(inp["b_dwconv"]); b_f = f(inp["b_forget"])
    cst = np.zeros((128, 288), dtype=ml_dtypes.bfloat16)
    cst[:, 0:128] = np.eye(128, dtype=np.float32).astype(ml_dtypes.bfloat16)
    kk, qq = np.meshgrid(np.arange(128), np.arange(128), indexing="ij")
    cst[:, 128:256] = np.where(kk <= qq, 0.0, NEG).astype(ml_dtypes.bfloat16)
    k2, q2 = np.meshgrid(np.arange(32), np.arange(32), indexing="ij")
    ok = (k2 // 16 == q2 // 16) & (k2 % 16 <= q2 % 16)
    cst[0:32, 256:288] = np.where(ok, 0.0, NEG).astype(ml_dtypes.bfloat16)
    wu = f(inp["w_up"])
    wu_a = wu[:, :, 0:FF].reshape(L, 8, 128, NJ, 128)
    wu_l = wu[:, :, FF:2 * FF].reshape(L, 8, 128, NJ, 128)
    w_up_r = np.ascontiguousarray(np.concatenate([wu_a, wu_l], axis=4).transpose(0, 3, 2, 1, 4))
    shared = dict(
        w_in=f(inp["w_in"]), w_mkv=f(inp["w_mem_kv"]), w_out=f(inp["w_out"]), w_up=w_up_r,
        w_dn=f(inp["w_down"]), g_post_mix=f(inp["g_post_mix"]), g_post_ffn=f(inp["g_post_ffn"]),
        g_sgu=f(inp["g_sgu"]), wsT=wsT, cst=cst)
    gpm = fm8(inp["g_pre_mix"]); ggo = fm8(inp["g_group_out"]); gmem = fm8(inp["g_mem"]); gpf = fm8(inp["g_pre_ffn"])
    in_maps = []
    for c in range(NCORES):
        prm = np.zeros((128, NPRM), dtype=np.float32)
        for l in range(L):
            o = l * PPL
            prm[:, o + PO["gpm"]:o + PO["gpm"] + 8] = gpm[l]
            prm[:, o + PO["ggo"]:o + PO["ggo"] + 8] = ggo[l]
            prm[:, o + PO["gmem"]:o + PO["gmem"] + 8] = gmem[l]
            prm[:, o + PO["gpf"]:o + PO["gpf"] + 8] = gpf[l]
            prm[:, o + PO["bs"]:o + PO["bs"] + 4] = b_sp[l].T
            prm[0:16, o + PO["bss"]:o + PO["bss"] + 4] = b_sp[l][:, 0:16].T
            prm[16:32, o + PO["bss"]:o + PO["bss"] + 4] = b_sp[l][:, 0:16].T
            prm[:, o + PO["wdw"]:o + PO["wdw"] + 66] = w_dw[l].reshape(3, NJ, 128).transpose(2, 1, 0).reshape(128, 66)
            prm[:, o + PO["bdw"]:o + PO["bdw"] + 22] = b_dw[l].reshape(NJ, 128).T
            prm[0:8, o + PO["bf"]] = b_f[l]
            cc = cconv[l, 2 * c:2 * c + 2]
            prm[:, o + PO["cconv"]:o + PO["cconv"] + 88] = cc.reshape(2, 2, NJ, 128).transpose(3, 2, 0, 1).reshape(128, 88)
        m = dict(shared)
        m.update(
            x_p=x_prompt[c], x_s=x_sample[2 * c:2 * c + 2].reshape(32, D), mem=mem_prompt[c],
            ckT=np.ascontiguousarray(ckT_all[:, 2 * c:2 * c + 2]), cv=np.ascontiguousarray(cv_all[:, 2 * c:2 * c + 2]),
            clfT=np.ascontiguousarray(clfT_all[:, 2 * c:2 * c + 2]), cmkT=np.ascontiguousarray(cmkT_all[:, 2 * c:2 * c + 2]),
            cmv=np.ascontiguousarray(cmv_all[:, 2 * c:2 * c + 2]), prm=prm)
        in_maps.append(m)
    return in_maps


def kernel(**inp):
    in_maps = _prep_inputs(inp)
    if "nc" not in _NC_CACHE:
        _NC_CACHE["nc"] = build()
    nc = _NC_CACHE["nc"]
    res = run_bass_kernel_spmd(nc, in_maps, core_ids=list(range(NCORES)))
    R = res.results
    cat = lambda name: np.stack([np.asarray(r[name], dtype=np.float32) for r in R], axis=0)
    y_p = cat("y_p")
    y_s = cat("y_s").reshape(16, 16, D)
    fk = cat("ofk").transpose(1, 0, 2, 3).reshape(L, 8, 2048, 8, 64)
    fv = cat("ofv").transpose(1, 0, 2, 3).reshape(L, 8, 2048, 8, 64)
    lf = cat("olf").transpose(1, 0, 3, 2)
    mk = cat("omk").transpose(1, 0, 2, 3).reshape(L, 8, 256, 4, 64)
    mv = cat("omv").transpose(1, 0, 2, 3).reshape(L, 8, 256, 4, 64)
    cvp = cat("oconv").transpose(1, 0, 4, 3, 2).reshape(L, 8, 2, FF)
    sk = cat("osk").transpose(1, 0, 2, 3).reshape(L, 16, 16, 8, 64)
    sv = cat("osv").transpose(1, 0, 2, 3).reshape(L, 16, 16, 8, 64)
    slf = cat("oslf").transpose(1, 0, 3, 2).reshape(L, 16, 16, 8)
    gv = cat("ogv").transpose(1, 0, 2, 3).reshape(L, 16, 16, 256)
    cvs = cat("osconv").transpose(1, 0, 4, 5, 3, 2).reshape(L, 16, 2, FF)
    outs = (y_p, y_s, fk, fv, lf, mk, mv, cvp, sk, sv, slf, gv, cvs)
    return tuple(np.ascontiguousarray(o, dtype=np.float32) for o in outs)
```

```python
import numpy as np
import ml_dtypes
from contextlib import ExitStack
import concourse.bass as bass
import concourse.mybir as mybir
from concourse.bass_utils import run_bass_kernel_spmd

F32 = mybir.dt.float32
BF16 = mybir.dt.bfloat16
ALU = mybir.AluOpType
AF = mybir.ActivationFunctionType

ENGS = ("pe", "act", "dve", "pool", "sp")
STOP = None
DBG = set()


class _Stop(Exception):
    pass


def _chk(name):
    if STOP == name:
        raise _Stop()
NCORES = 8
L = 2
D = 1024
NT = 2080
FF = 2816
NJ = 22
EPS = 1e-6
NEG = -30000.0


class Op:
    __slots__ = ("eng", "fn", "idx", "deps", "dma_key", "marked", "seq", "cum")

    def __init__(self, eng, fn, idx, dma_key):
        self.eng = eng
        self.fn = fn
        self.idx = idx
        self.deps = ()
        self.dma_key = dma_key
        self.marked = False
        self.seq = 0
        self.cum = 0


class Sched:
    def __init__(self, nc):
        self.nc = nc
        self.streams = {e: [] for e in ENGS}
        self.last_writer = {}
        self.readers = {}
        self.dma_counts = {}
        self.dma_last = {}

    def op(self, eng, fn, reads=(), writes=(), dma_key=None):
        o = Op(eng, fn, len(self.streams[eng]), dma_key)
        deps = set()
        lw = self.last_writer
        rd = self.readers
        for t in reads:
            w = lw.get(t)
            if w is not None:
                deps.add(w)
            if type(t) is tuple and t[0] == "ps":
                for r in rd.get(t, ()):
                    if r.eng != eng:
                        deps.add(r)
        for t in writes:
            w = lw.get(t)
            if w is not None:
                deps.add(w)
            r = rd.get(t)
            if r:
                deps.update(r)
        for t in reads:
            rd.setdefault(t, []).append(o)
        for t in writes:
            lw[t] = o
            rd[t] = []
        deps.discard(o)
        if eng == "pe":
            deps = {d for d in deps if not (d.eng == "pe" and d.dma_key is None)}
        o.deps = deps
        if dma_key is not None:
            c = self.dma_counts.get(dma_key, 0) + 1
            self.dma_counts[dma_key] = c
            o.cum = 16 * c
            self.dma_last[dma_key] = o
        self.streams[eng].append(o)
        return o

    def barrier(self):
        lasts = [s[-1] for s in self.streams.values() if s]
        lasts += list(self.dma_last.values())
        for e in ENGS:
            o = Op(e, lambda eng: eng.nop(), len(self.streams[e]), None)
            o.deps = {d for d in lasts if not (d.eng == e and d.dma_key is None)}
            self.streams[e].append(o)

    def emit(self, stack):
        nc = self.nc
        for e in ENGS:
            for o in self.streams[e]:
                for d in o.deps:
                    if d.dma_key is None:
                        d.marked = True
        sems = {}
        for e in ENGS:
            n = 0
            for o in self.streams[e]:
                if o.dma_key is None and o.marked:
                    n += 1
                    o.seq = n
            if n:
                sems[e] = stack.enter_context(nc.semaphore("s_" + e))
        dsems = {}
        for k in self.dma_counts:
            dsems[k] = stack.enter_context(nc.semaphore("d%d" % len(dsems)))
        self.n_sems = len(sems) + len(dsems)
        final = {k: o.cum for k, o in self.dma_last.items()}

        def run(eng_name, eng):
            waited = {}
            for o in self.streams[eng_name]:
                need = {}
                for d in o.deps:
                    if d.dma_key is not None:
                        key = ("d", d.dma_key)
                        val = d.cum
                    else:
                        key = ("c", d.eng)
                        val = d.seq
                    if val > need.get(key, 0):
                        need[key] = val
                for key, val in need.items():
                    if waited.get(key, 0) >= val:
                        continue
                    waited[key] = val
                    s = dsems[key[1]] if key[0] == "d" else sems[key[1]]
                    eng.wait_ge(s, val)
                ins = o.fn(eng)
                if o.dma_key is not None:
                    ins.then_inc(dsems[o.dma_key], 16)
                elif o.marked:
                    ins.then_inc(sems[o.eng], 1)
            if eng_name == "sp":
                for k, v in final.items():
                    if waited.get(("d", k), 0) < v:
                        eng.wait_ge(dsems[k], v)

        with nc.Block() as block:
            @block.tensor
            def _(t):
                run("pe", t)

            @block.scalar
            def _(t):
                run("act", t)

            @block.vector
            def _(t):
                run("dve", t)

            @block.gpsimd
            def _(t):
                run("pool", t)

            @block.sync
            def _(t):
                run("sp", t)


def tile_rows(i):
    return 128 if i < 16 else 32


def tcols(i):
    return (128 * i, 128) if i < 16 else (2048, 32)


BLKS = [(0, 512), (512, 512), (1024, 512), (1536, 512), (2048, 32)]

PO = {}
_o = 0
for _n, _w in (("gpm", 8), ("ggo", 8), ("gmem", 8), ("gpf", 8), ("bs", 4), ("bss", 4),
               ("wdw", 66), ("bdw", 22), ("bf", 1), ("cconv", 88)):
    PO[_n] = _o
    _o += _w
PPL = _o
NPRM = PPL * L


def build():
    nc = bass.Bass("TRN2", target_bir_lowering=False)

    def din(name, shape, dt=F32):
        return nc.dram_tensor(name, list(shape), dt, kind="ExternalInput").ap()

    def dout(name, shape):
        return nc.dram_tensor(name, list(shape), F32, kind="ExternalOutput").ap()

    x_p = din("x_p", [2048, D])
    x_s = din("x_s", [32, D])
    mem = din("mem", [256, D])
    ckT = din("ckT", [L, 2, 8, 64, 1024])
    cv = din("cv", [L, 2, 1024, 512])
    clfT = din("clfT", [L, 2, 8, 1024])
    cmkT = din("cmkT", [L, 2, 4, 64, 256])
    cmv = din("cmv", [L, 2, 256, 256])
    w_in = din("w_in", [L, D, 2312])
    w_mkv = din("w_mkv", [L, D, 512])
    w_out = din("w_out", [L, D, D])
    w_up = din("w_up", [L, NJ, 128, 8, 256])
    w_dn = din("w_dn", [L, FF, D])
    g_post_mix = din("g_post_mix", [L, D])
    g_post_ffn = din("g_post_ffn", [L, D])
    g_sgu = din("g_sgu", [L, 256])
    wsT = din("wsT", [L, 4, 128, 128])
    prm_h = din("prm", [128, NPRM])
    cst_h = din("cst", [128, 288], BF16)

    y_p = dout("y_p", [2048, D])
    y_s = dout("y_s", [32, D])
    ofk = dout("ofk", [L, 2048, 512])
    ofv = dout("ofv", [L, 2048, 512])
    olf = dout("olf", [L, 8, 2048])
    omk = dout("omk", [L, 256, 256])
    omv = dout("omv", [L, 256, 256])
    oconv = dout("oconv", [L, 128, NJ, 2])
    osk = dout("osk", [L, 32, 512])
    osv = dout("osv", [L, 32, 512])
    oslf = dout("oslf", [L, 8, 32])
    ogv = dout("ogv", [L, 32, 256])
    osconv = dout("osconv", [L, 128, NJ, 2, 2])

    cscr_p = nc.dram_tensor("cscr_p", [8, 2, 2048], BF16).ap()
    cscr_s = nc.dram_tensor("cscr_s", [8, 2, 2, 1040], BF16).ap()

    with ExitStack() as st:
        S = Sched(nc)
        X = st.enter_context(nc.sbuf_tensor("X", [128, 17, D], F32))
        PRM = st.enter_context(nc.sbuf_tensor("PRM", [128, NPRM], F32))
        CST = st.enter_context(nc.sbuf_tensor("CST", [128, 288], BF16))
        WSTSt = st.enter_context(nc.sbuf_tensor("WSTS", [32, L * 4 * 32], BF16))
        PTSt = st.enter_context(nc.sbuf_tensor("PTS", [128, 7 * 32], BF16))
        SM = st.enter_context(nc.sbuf_tensor("SM", [128, 256], F32))
        ONEROW = st.enter_context(nc.sbuf_tensor("ONEROW", [128, 64], F32))
        remaining = nc.sbuf_bytes_remaining
        remaining = remaining() if callable(remaining) else remaining
        ARB = (remaining - 512) // 64 * 64
        AR = st.enter_context(nc.sbuf_tensor("AR", [128, ARB // 2], BF16))
        AR32 = AR.bitcast(F32)
        PS = [st.enter_context(nc.psum_tensor("ps%d" % k, [128, 512], F32)) for k in range(8)]
        PSB = [p.bitcast(BF16) for p in PS]

        IDENT = CST[:, 0:128]
        MASKNEG = CST[:, 128:256]
        MASKS = CST[0:32, 256:288]


        def WSTS(l, g):
            o = (l * 4 + g) * 32
            return WSTSt[0:32, o:o + 32]

        def prm(l, name, w, p0=0, p1=128, off=0):
            o = l * PPL + PO[name] + off
            return PRM[p0:p1, o:o + w]

        SM_SS, SM_SD, SM_RSTD = 0, 1, 2
        SM_R = 16
        SM_SSF = 70
        SM_T = 90

        def bf(off, n, p0=0, p1=128):
            return AR[p0:p1, off // 2: off // 2 + n]

        def f32(off, n, p0=0, p1=128):
            return AR32[p0:p1, off // 4: off // 4 + n]

        cur = [0]

        def carve(nbytes):
            o = cur[0]
            cur[0] = o + (nbytes + 63) // 64 * 64
            return o

        HT_O = carve(8 * NT * 2)
        MX_O = carve(8 * NT * 2)
        VA_O = carve(17 * 8 * 65 * 2)
        QK_O = carve(4 * NT * 2)
        WS_O = [carve(8320) for _ in range(3)]
        SCR_O = carve(8192)
        MKT_O = carve(2048)
        MVA_O = carve(1040 + 128)
        MKTt = AR[:, MKT_O // 2:MKT_O // 2 + 1024]
        MVAt = AR[:, MVA_O // 2:MVA_O // 2 + 520]
        MIX_END = cur[0]
        assert MIX_END <= ARB, (MIX_END, ARB)

        def HT(c, c0, n):
            return bf(HT_O + (c * NT + c0) * 2, n)

        def HT3(c0, n):
            return AR[:, HT_O // 2: HT_O // 2 + 8 * NT].rearrange("p (c t) -> p c t", c=8)[:, :, c0:c0 + n]

        def MX(c, c0, n, p0=0, p1=128):
            return bf(MX_O + (c * NT + c0) * 2, n, p0, p1)

        def VA(j, h, nk=128):
            return bf(VA_O + ((j * 8 + h) * 65) * 2, 65, 0, nk)

        def VA128(j, h):
            return bf(VA_O + ((j * 8 + h) * 65) * 2, 128, 0, 128)

        def QK(slot, c0, n, p0, p1):
            return bf(QK_O + (slot * NT + c0) * 2, n, p0, p1)

        CT = lambda c0, n: f32(QK_O + c0 * 4, n, 96, 104)
        HI = lambda c0, n: bf(QK_O + 8320 + c0 * 2, n, 96, 104)
        LO = lambda c0, n: bf(QK_O + 8320 + 4160 + c0 * 2, n, 96, 104)

        def WS(s, c, c0, n):
            return bf(WS_O[s] + (c * 520 + c0) * 2, n)

        def WS3(s, ncol):
            return AR[:, WS_O[s] // 2: WS_O[s] // 2 + 8 * 520].rearrange("p (c n) -> p c n", c=8)[:, :, 0:ncol]

        def SCRb(g, n, p0=0, p1=128, off=0):
            return bf(SCR_O + g * 1024 + off * 2, n, p0, p1)

        def SCRf(g, n, p0=0, p1=128, off=0):
            return f32(SCR_O + g * 1024 + off * 4, n, p0, p1)

        def scr(*gs):
            return [("scr", g) for g in gs]

        def dma(eng, out, in_, reads=(), writes=(), key=None, slow=False):
            if slow:
                fn = lambda e: e.dma_start(out=out, in_=in_, allow_slow_non_contiguous=True)
            else:
                fn = lambda e: e.dma_start(out=out, in_=in_)
            return S.op(eng, fn, reads=reads, writes=writes, dma_key=key)

        psr = {"g": [0, 1], "s": [2, 3, 4], "o": [5, 6], "t": [7]}
        psi = {k: 0 for k in psr}

        def psnext(pool):
            lst = psr[pool]
            k = lst[psi[pool] % len(lst)]
            psi[pool] += 1
            return k

        def mm_group(out_fn, items, reads, writes):
            def fn(e):
                ins = None
                for (o, a, b, s0, s1) in items:
                    ins = e.matmul(o, a, b, start=s0, stop=s1, skip_group_check=True)
                return ins
            return S.op("pe", fn, reads=reads, writes=writes)

        dma("sp", PRM[:, :], prm_h, writes=["prm"], key="prm")
        dma("sp", CST[:, :], cst_h, writes=["cst"], key="cst")
        S.op("dve", lambda e: e.memset(WSTSt[:, :], 0.0), writes=["wsts"])
        for l in range(L):
            for b in range(2):
                dma("pool",
                    WSTSt[16 * b:16 * b + 16, l * 128:(l + 1) * 128].rearrange("p (g i) -> p g i", g=4)[:, :, 16 * b:16 * b + 16],
                    wsT[l, :, 0:16, 0:16].rearrange("g j i -> j g i"),
                    reads=["wsts"], writes=[("wsts", l, b)], key=("wsts", l), slow=True)
        S.op("dve", lambda e: e.memset(PTSt[:, :], 0.0), writes=["pts%d" % q for q in range(7)])
        S.op("dve", lambda e: e.memset(SM[:, 200:201], 1.0), writes=["one"])
        S.op("dve", lambda e: e.memset(ONEROW[:, :], 1.0), writes=["onerow"])
        ONE8 = SM[96:104, 200:201]

        for i in range(16):
            dma("sp", X[:, i, :], x_p[128 * i:128 * i + 128, :], writes=[("x", i)], key=("x", i))
        dma("sp", X[0:32, 16, :], x_s, writes=[("x", 16)], key=("x", 16))

        def prenorm_tile(src, rows, xtok, gcol, dst3, dst_tok, slot, extra_writes=(), hb=None, tpool="t", hb_toks=None, defer=False):
            if hb is None:
                hb = SCRb(2 * slot, 1024, 0, rows)
            hbt = list(hb_toks) if hb_toks is not None else scr(2 * slot, 2 * slot + 1)
            c0 = 3 * slot
            ss = SM[0:rows, c0:c0 + 1]
            sd = SM[0:rows, c0 + 1:c0 + 2]
            rs = SM[0:rows, c0 + 2:c0 + 3]
            S.op("act", lambda e: e.activation(hb, src, AF.Square, accum_out=ss),
                 reads=[xtok], writes=hbt + [("sm", c0)])
            S.op("act", lambda e: e.activation(sd, ss, AF.Ln, bias=EPS, scale=1.0 / D),
                 reads=[("sm", c0)], writes=[("sm", c0 + 1)])
            S.op("act", lambda e: e.activation(rs, sd, AF.Exp, scale=-0.5), reads=[("sm", c0 + 1)], writes=[("sm", c0 + 2)])
            S.op("act", lambda e: e.activation(hb, src, AF.Copy, scale=rs),
                 reads=[xtok, ("sm", c0 + 2)], writes=hbt)
            def part_b():
                k = psnext(tpool)

                def tr(e):
                    ins = None
                    for c in range(8):
                        ins = e.transpose(PSB[k][:, c * 128:c * 128 + rows], hb[:, c * 128:(c + 1) * 128],
                                          IDENT[0:rows, 0:rows])
                    return ins
                S.op("pe", tr, reads=hbt + ["cst"], writes=[("ps", k)])
                src3 = PSB[k][:, 0:1024].rearrange("p (c t) -> p c t", c=8)[:, :, 0:rows]
                S.op("dve", lambda e: e.tensor_tensor(dst3, src3, gcol.unsqueeze(2).to_broadcast([128, 8, rows]), ALU.mult),
                     reads=[("ps", k), "prm"], writes=[dst_tok] + list(extra_writes))
            if defer:
                return part_b
            part_b()

        PVQ = []

        def flush_pv():
            while PVQ:
                PVQ.pop(0)()

        def attend(qa_fn, K, ktiles, osubs, obank, tag):
            pendq = []
            first_done = [False]

            def pv(kt_i, kt, pt_ap, pt_tok):
                items = []
                if osubs == "fm":
                    lo_, hi_ = kt["lo"], kt["hi"]
                    mrows = kt["va"].shape[1]
                    items.append((PS[obank][0:mrows, lo_:hi_], kt["va"], pt_ap[:, lo_:hi_], kt_i == 0, kt_i == len(ktiles) - 1))
                    mm_group(None, items, reads=[pt_tok] + kt["vreads"], writes=[("ps", obank)])
                    return
                for (o_ap, c0, c1, fk, lk) in osubs:
                    if fk <= kt_i <= lk and kt["pv_lo"] <= c0:
                        items.append((o_ap, pt_ap[:, c0:c1], kt["va"], not first_done[0], kt_i == lk))
                        first_done[0] = True
                if items:
                    mm_group(None, items, reads=[pt_tok] + kt["vreads"], writes=[("ps", obank)])

            for kt_i, kt in enumerate(ktiles):
                k = psnext("s")
                nk, lo, hi = kt["nk"], kt["lo"], kt["hi"]
                items = []
                if kt["mask"] is not None:
                    mask_ap, mc = kt["mask"]
                    items.append((PS[k][0:nk, lo:lo + mc], kt["ka"], qa_fn(lo, mc), True, False))
                    items.append((PS[k][0:nk, lo:lo + mc], IDENT[0:nk, 0:nk], mask_ap, False, True))
                    if hi > lo + mc:
                        items.append((PS[k][0:nk, lo + mc:hi], kt["ka"], qa_fn(lo + mc, hi - lo - mc), True, True))
                else:
                    items.append((PS[k][0:nk, lo:hi], kt["ka"], qa_fn(lo, hi - lo), True, True))
                mm_group(None, items, reads=kt["reads"] + ["cst"], writes=[("ps", k)])
                if kt["pt"] == "rot":
                    g = attend.ptc % 4
                    attend.ptc += 1
                    pt_ap = SCRb(g, 512, 0, nk)
                    pt_tok = ("scr", g)
                else:
                    pt_ap = PTSt[0:nk, 32 * kt["pt"]:32 * kt["pt"] + 32]
                    pt_tok = "pts%d" % kt["pt"]
                S.op("act", lambda e, pt_ap=pt_ap, k=k, nk=nk, lo=lo, hi=hi:
                     e.activation(pt_ap[:, lo:hi], PS[k][0:nk, lo:hi], AF.Exp),
                     reads=[("ps", k)], writes=[pt_tok])
                while len(PVQ) >= 2:
                    PVQ.pop(0)()
                PVQ.append(lambda kt_i=kt_i, kt=kt, pt_ap=pt_ap, pt_tok=pt_tok: pv(kt_i, kt, pt_ap, pt_tok))
        attend.ptc = 0

        def cs_all():
            return ["cw"]

        def post_norm_residual(ostg, rows, i, junk, junk_toks, GPap, gp_toks, o_toks):
            ssc = SM[0:rows, SM_T + 8:SM_T + 9]
            S.op("act", lambda e: e.activation(junk, ostg, AF.Square, accum_out=ssc),
                 reads=o_toks, writes=list(junk_toks) + ["pn"])
            S.op("act", lambda e: e.activation(ssc, ssc, AF.Ln, bias=EPS, scale=1.0 / D), reads=["pn"], writes=["pn"])
            S.op("act", lambda e: e.activation(ssc, ssc, AF.Exp, scale=-0.5), reads=["pn"], writes=["pn"])
            S.op("dve", lambda e: e.scalar_tensor_tensor(ostg, ostg, ssc, GPap[0:rows, :], ALU.mult, ALU.mult),
                 reads=list(o_toks) + list(gp_toks) + ["pn"], writes=o_toks)
            S.op("pool", lambda e: e.tensor_tensor(X[0:rows, i, :], X[0:rows, i, :], ostg, ALU.add),
                 reads=list(o_toks) + [("x", i)], writes=[("x", i)])

        def ffn_phase(l):
            psr.clear()
            psr.update({"u": [0, 1, 2, 4, 5, 6], "t": [3], "d": [4, 5, 6, 7]})
            for kk in psr:
                psi[kk] = 0
            HC = 1056
            o = [0]

            def cv_(nb):
                r = o[0]
                o[0] = r + (nb + 63) // 64 * 64
                return r
            H2_O = cv_(8 * HC * 2)
            ACT_O = cv_(NJ * HC * 2)
            WDN_O = cv_(NJ * 1024 * 2)
            WU_O = [cv_(4096) for _ in range(2)]
            AE_O = [cv_(2064) for _ in range(2)]
            LIN_O = [cv_(1024) for _ in range(2)]
            CONV_O = [cv_(2048) for _ in range(2)]
            OSTG_O = cv_(4096)
            GPF_O = cv_(4096)
            HBF_O = cv_(2048)
            SAVE_O = cv_(NJ * 2 * 4)
            SAVES_O = cv_(NJ * 4 * 4)
            assert o[0] <= ARB, (o[0], ARB)

            H2 = lambda c, c0, n: bf(H2_O + (c * HC + c0) * 2, n)
            H23 = lambda c0, n: AR[:, H2_O // 2:H2_O // 2 + 8 * HC].rearrange("p (c t) -> p c t", c=8)[:, :, c0:c0 + n]
            ACTB = lambda j, c0, n: bf(ACT_O + (j * HC + c0) * 2, n)
            WDN = lambda j, c0, n: bf(WDN_O + (j * 1024 + c0) * 2, n)
            WDN3 = AR[:, WDN_O // 2:WDN_O // 2 + NJ * 1024].rearrange("p (j n) -> p j n", j=NJ)
            WUB = lambda s: (WU_O[s] if s < 2 else OSTG_O)
            WU = lambda s, c, c0, n: bf(WUB(s) + (c * 256 + c0) * 2, n)
            WU3 = lambda s: AR[:, WUB(s) // 2:WUB(s) // 2 + 2048].rearrange("p (c n) -> p c n", c=8)
            wutok = lambda s, x: (("wu", s, x) if s < 2 else ("ostg%d" % x))
            OSTG = f32(OSTG_O, 1024)
            GPF = f32(GPF_O, 1024)
            HBF = bf(HBF_O, 1024)
            SAVE = f32(SAVE_O, NJ * 2)
            SAVES = f32(SAVES_O, NJ * 4)

            dma("sp", GPF, g_post_ffn[l:l + 1, :].broadcast_to([128, D]), writes=["gpf"], key="gp")
            wdn_src = w_dn[l].rearrange("(j p) n -> p j n", p=128)
            for part in range(2):
                dma("pool", WDN3[:, 11 * part:11 * part + 11, :], wdn_src[:, 11 * part:11 * part + 11, :],
                    writes=[("wdn", part)], key=("wdn", part))
            aecnt = 0

            def wu_load(j, s_):
                dma("pool", WU3(s_), w_up[l, j], writes=[wutok(s_, 0), wutok(s_, 1)], key=("wu", s_, 0))

            for half in range(2):
                tiles = list(range(0, 8)) if half == 0 else list(range(8, 17))
                base = 0 if half == 0 else 1024
                blocks = [(0, 512), (512, 512)] + ([(1024, 32)] if half == 1 else [])
                h2toks = [("h2", (i if i < 8 else i - 8)) for i in tiles]
                if half == 0:
                    HB2 = bf(AE_O[0], 1024)

                    def pa(i):
                        c0, n = tcols(i)
                        if i % 2 == 0:
                            return prenorm_tile(X[:, i, :], 128, ("x", i), prm(l, "gpf", 8), H23(c0, n), ("h2", i), 0,
                                                hb=HBF[:, :], tpool="t", hb_toks=["hbf"], defer=True)
                        return prenorm_tile(X[:, i, :], 128, ("x", i), prm(l, "gpf", 8), H23(c0, n), ("h2", i), 1,
                                            hb=HB2, tpool="t", hb_toks=[("ae", 0), ("aeh", 0), ("aet", 0)], defer=True)
                    pbq = {0: pa(0)}
                    for i in tiles:
                        if i + 1 < 8:
                            pbq[i + 1] = pa(i + 1)
                        pbq.pop(i)()
                wu_load(0, 0)
                wu_load(1, 1)
                pend_tail = None
                for j in range(NJ):
                    s = j % 3
                    if j + 2 < NJ:
                        wu_load(j + 2, (j + 2) % 3)
                    w0 = prm(l, "wdw", 1, 0, 128, 3 * j)
                    w1 = prm(l, "wdw", 1, 0, 128, 3 * j + 1)
                    w2 = prm(l, "wdw", 1, 0, 128, 3 * j + 2)
                    bd = prm(l, "bdw", 1, 0, 128, j)
                    for bidx, (lc0, n) in enumerate(blocks):
                        sample = (n == 32)
                        ka = psnext("u")
                        mm_group(None, [(PS[ka][:, 0:n], WU(s, c, 0, 128), H2(c, lc0, n), c == 0, c == 7) for c in range(8)],
                                 reads=h2toks + [wutok(s, 0)], writes=[("ps", ka)])
                        kl = psnext("u")
                        mm_group(None, [(PS[kl][:, 0:n], WU(s, c, 128, 128), H2(c, lc0, n), c == 0, c == 7) for c in range(8)],
                                 reads=h2toks + [wutok(s, 1)], writes=[("ps", kl)])
                        p = aecnt % 2
                        aecnt += 1
                        lin = bf(LIN_O[p], 512)
                        conv = f32(CONV_O[p], 512)
                        if sample:
                            AEs = f32(AE_O[p], 36).rearrange("p (b t) -> p b t", b=2)
                            S.op("dve", lambda e, AEs=AEs, j=j: e.tensor_copy(
                                AEs[:, :, 0:2], prm(l, "cconv", 4, 0, 128, 4 * j).rearrange("p (b r) -> p b r", b=2)),
                                reads=["prm"], writes=[("aeh", p)])
                            S.op("act", lambda e, AEs=AEs, ka=ka: e.copy(AEs[:, :, 2:18], PS[ka][:, 0:32].rearrange("p (b t) -> p b t", b=2)),
                                 reads=[("ps", ka)], writes=[("ae", p)])
                            taps = [AEs[:, :, k:k + 16] for k in range(3)]
                            cv3 = conv[:, 0:32].rearrange("p (b t) -> p b t", b=2)
                            psa = PS[ka][:, 0:32].rearrange("p (b t) -> p b t", b=2)
                            sil_out = f32(AE_O[p] + 256, 32)
                            sil_in = conv[:, 0:32]
                        else:
                            AEp = f32(AE_O[p], 514)
                            if bidx == 0 and half == 0:
                                S.op("dve", lambda e, AEp=AEp: e.memset(AEp[:, 0:2], 0.0), writes=[("aeh", p)])
                            elif bidx == 0:
                                S.op("dve", lambda e, AEp=AEp, j=j: e.tensor_copy(AEp[:, 0:2], SAVE[:, 2 * j:2 * j + 2]),
                                     reads=[("save", j)], writes=[("aeh", p)])
                            else:
                                prev = f32(AE_O[1 - p], 514)
                                S.op("dve", lambda e, AEp=AEp, prev=prev: e.tensor_copy(AEp[:, 0:2], prev[:, 512:514]),
                                     reads=[("aet", 1 - p)], writes=[("aeh", p)])
                            S.op("act", lambda e, AEp=AEp, ka=ka: e.copy(AEp[:, 2:514], PS[ka][:, 0:512]),
                                 reads=[("ps", ka)], writes=[("ae", p), ("aet", p)])
                            taps = [AEp[:, k:k + 512] for k in range(3)]
                            cv3 = conv
                            psa = PS[ka][:, 0:512]
                            sil_out = AEp[:, 0:512]
                            sil_in = conv
                        S.op("act", lambda e, cv3=cv3, psa=psa, w2=w2, bd=bd: e.activation(cv3, psa, AF.Identity, bias=bd, scale=w2),
                             reads=[("ps", ka), "prm"], writes=[("conv", p)])
                        S.op("act", lambda e, lin=lin, kl=kl, n=n: e.copy(lin[:, 0:n], PS[kl][:, 0:n]),
                             reads=[("ps", kl)], writes=[("lin", p)])
                        S.op("dve", lambda e, cv3=cv3, taps=taps, w1=w1: e.scalar_tensor_tensor(cv3, taps[1], w1, cv3, ALU.mult, ALU.add),
                             reads=[("ae", p), ("aeh", p), ("conv", p)], writes=[("conv", p)])
                        S.op("dve", lambda e, cv3=cv3, taps=taps, w0=w0: e.scalar_tensor_tensor(cv3, taps[0], w0, cv3, ALU.mult, ALU.add),
                             reads=[("ae", p), ("aeh", p), ("conv", p)], writes=[("conv", p)])
                        if not sample and bidx == 1:
                            S.op("dve", lambda e, AEp=AEp, j=j: e.tensor_copy(SAVE[:, 2 * j:2 * j + 2], AEp[:, 512:514]),
                                 reads=[("aet", p)], writes=[("save", j)])
                        if sample:
                            S.op("dve", lambda e, AEs=AEs, j=j: e.tensor_copy(
                                SAVES[:, 4 * j:4 * j + 4].rearrange("p (b r) -> p b r", b=2), AEs[:, :, 16:18]),
                                reads=[("ae", p)], writes=[("saves", j)])

                        def tail(p=p, sil_out=sil_out, sil_in=sil_in, j=j, lc0=lc0, n=n, lin=lin, half=half):
                            S.op("act", lambda e: e.activation(sil_out, sil_in, AF.Silu),
                                 reads=[("conv", p), ("ae", p), ("aeh", p)], writes=[("ae", p), ("aeh", p)])
                            S.op("pool", lambda e: e.tensor_tensor(ACTB(j, lc0, n), sil_out, lin[:, 0:n], ALU.mult),
                                 reads=[("ae", p), ("aeh", p), ("lin", p)], writes=[("actb", j, half)])
                        if pend_tail is not None:
                            pend_tail()
                        pend_tail = tail
                if pend_tail is not None:
                    pend_tail()
                    pend_tail = None
                if half == 1:
                    dma("sp", oconv[l], SAVE.rearrange("p (j r) -> p j r", r=2), reads=[("save", j) for j in range(NJ)], key="oconv")
                    dma("sp", osconv[l], SAVES.rearrange("p (j b r) -> p j b r", b=2, r=2), reads=[("saves", j) for j in range(NJ)], key="osconv")
                atoks = [("actb", j, half) for j in range(NJ)]
                nxt = list(range(8, 17)) if half == 0 else []
                for ti_, i in enumerate(tiles):
                    rows = tile_rows(i)
                    c0, n = tcols(i)
                    lc0 = c0 - base
                    defer_b = []
                    for i2 in (nxt[ti_:ti_ + 1] if ti_ < 7 else nxt[7:8]):
                        rows2 = tile_rows(i2)
                        c02, n2 = tcols(i2)
                        defer_b.append(prenorm_tile(X[0:rows2, i2, :], rows2, ("x", i2), prm(l, "gpf", 8), H23(c02 - 1024, n2),
                                                    ("h2", i2 - 8), 0, hb=HBF[0:rows2, :], tpool="t", hb_toks=["hbf"], defer=True))
                        if "nodefer" in DBG:
                            defer_b.pop()()
                    for hf in range(2):
                        kd = psnext("d")
                        mm_group(None, [(PS[kd][0:rows, :], ACTB(j, lc0, n), WDN(j, 512 * hf, 512), j == 0, j == NJ - 1) for j in range(NJ)],
                                 reads=atoks + [("wdn", 0), ("wdn", 1)], writes=[("ps", kd)])
                        if hf == 0:
                            S.op("act", lambda e, kd=kd, rows=rows: e.copy(OSTG[0:rows, 0:512], PS[kd][0:rows, :]),
                                 reads=[("ps", kd)], writes=["ostg0"])
                        else:
                            S.op("dve", lambda e, kd=kd, rows=rows: e.tensor_copy(OSTG[0:rows, 512:1024], PS[kd][0:rows, :]),
                                 reads=[("ps", kd)], writes=["ostg1"])
                    for fb in defer_b:
                        fb()
                    post_norm_residual(OSTG[0:rows, :], rows, i, HBF[0:rows, :], ["hbf"], GPF, ["gpf"], ["ostg0", "ostg1"])
                    if l == L - 1:
                        if i < 16:
                            dma("sp", y_p[128 * i:128 * i + 128, :], X[:, i, :], reads=[("x", i)], key=("x", i))
                        else:
                            dma("sp", y_s, X[0:32, 16, :], reads=[("x", 16)], key=("x", 16))
                if half == 0:
                    prenorm_tile(X[0:32, 16, :], 32, ("x", 16), prm(l, "gpf", 8), H23(1024, 32), ("h2", 8), 0,
                                 hb=HBF[0:32, :], tpool="t", hb_toks=["hbf"])
        RFOX = lambda i, rows=128: SM[0:rows, 100 + i:101 + i]
        RSGU = lambda i, rows=128: SM[0:rows, 120 + i:121 + i]
        RMEM = lambda i, rows=128: SM[0:rows, 140 + i:141 + i]
        SSFOX = lambda i, rows=128: SM[0:rows, 160 + i:161 + i]
        SSMEM = lambda i, rows=128: SM[0:rows, 180 + i:181 + i]
        ALLHT = [("ht", i) for i in range(17)]

        class NSQ:
            def __init__(self):
                self.p1 = None
                self.p2 = None
                self.p3 = None

            def step(self, new_p1):
                if self.p3 is not None:
                    self.p3()
                    self.p3 = None
                if self.p2 is not None:
                    self.p3 = self.p2()
                    self.p2 = None
                if self.p1 is not None:
                    self.p2 = self.p1()
                self.p1 = new_p1

            def flush(self):
                self.step(None)
                self.step(None)
                self.step(None)
        nsq = NSQ()

        def wload(slot, src2d, ncol, extra_writes=()):
            dma("pool", WS3(slot, ncol), src2d.rearrange("(c p) n -> p c n", p=128),
                writes=[("ws", slot)] + list(extra_writes), key=("ws", slot))

        def KAS(b, c0, n, p0=0, p1=67):
            return bf(WS_O[0] + b * 3200 + c0 * 2, n, p0, p1)

        def VAS(b, j):
            return bf(WS_O[0] + b * 3200 + 2080 + j * 130, 65)

        def VAS3(b):
            return AR[:, (WS_O[0] + b * 3200 + 2080) // 2:(WS_O[0] + b * 3200 + 2080) // 2 + 520].rearrange("p (j k) -> p j k", k=65)

        SC_TOKS = [("sc", b, x) for b in range(2) for x in ("k", "v", "r", "one")]

        def norm_store_fm(obank, I, grp, chunk, p0, ssq4, l):
            rlrow = f32(SCR_O + 6 * 1024, 512, 64, 65)
            hfT = f32(SCR_O + 6 * 1024, 512, 0, 64)
            S.op("act", lambda e: e.activation(rlrow, PS[obank][64:65, 0:512], AF.Ln), reads=[("ps", obank)], writes=[("scr", 6, "r")])
            S.op("act", lambda e: e.activation(rlrow, rlrow, AF.Exp, scale=-1.0), reads=[("scr", 6, "r")], writes=[("scr", 6, "r")])
            kb = psnext("g")
            S.op("pe", lambda e: e.matmul(PS[kb][0:64, 0:512], ONEROW[64:65, 0:64], rlrow, start=True, stop=True, skip_group_check=True),
                 reads=[("scr", 6, "r"), "onerow"], writes=[("ps", kb)])
            S.op("dve", lambda e: e.tensor_copy(hfT, PS[kb][0:64, 0:512]), reads=[("ps", kb)], writes=scr(6, 7))
            S.op("dve", lambda e: e.tensor_tensor(hfT, PS[obank][0:64, 0:512], hfT, ALU.mult), reads=[("ps", obank)] + scr(6, 7), writes=scr(6, 7))
            return lambda: norm_store_fm2(I, grp, chunk, p0, ssq4, l)

        def norm_store_fm2(I, grp, chunk, p0, ssq4, l):
            hfT = f32(SCR_O + 6 * 1024, 512, 0, 64)
            sqb = bf(SCR_O + 4 * 1024, 512, 0, 64)
            S.op("act", lambda e: e.activation(MX(chunk, 512 * I, 512, p0, p0 + 64), hfT, AF.Copy,
                                               scale=prm(l, "ggo", 1, p0, p0 + 64, chunk)),
                 reads=scr(6, 7) + ["prm"], writes=[("mx", chunk, 4 * I + s_, p0) for s_ in range(4)])
            S.op("pool", lambda e: e.tensor_tensor(sqb, hfT, hfT, ALU.mult), reads=scr(6, 7), writes=scr(4))
            return lambda: norm_store_fm3(I, grp, ssq4)

        def norm_store_fm3(I, grp, ssq4):
            sqb = bf(SCR_O + 4 * 1024, 512, 0, 64)
            kt = psnext("t")
            ones_col = bf(VA_O + 64 * 2, 1, 0, 64)

            def ssmm(e):
                ins = None
                for s_ in range(4):
                    ins = e.matmul(PS[kt][:, s_:s_ + 1], sqb[:, 128 * s_:128 * s_ + 128], ones_col, start=True, stop=True,
                                   skip_group_check=True)
                return ins
            S.op("pe", ssmm, reads=scr(4) + ["va_ones"], writes=[("ps", kt)])
            toks4 = [("ssq", grp, 4 * I + s_) for s_ in range(4)]
            S.op("dve", lambda e: e.tensor_tensor(ssq4, ssq4, PS[kt][:, 0:4], ALU.add), reads=[("ps", kt)] + toks4, writes=toks4)

        def norm_and_store_block(obank, I, grp, chunk, p0, ssq4, l):
            O4 = PS[obank][:, 0:260].rearrange("p (s k) -> p s k", k=65)
            rl4 = SM[:, SM_T + 10:SM_T + 14]
            sq4 = SM[:, SM_T + 14:SM_T + 18]
            S.op("dve", lambda e: e.reciprocal(rl4.unsqueeze(2), O4[:, :, 64:65]), reads=[("ps", obank)], writes=["rl4"])
            hf4 = SCRf(6, 256)
            hb4 = SCRb(7, 256)
            S.op("dve", lambda e: e.tensor_tensor(hf4.rearrange("p (s k) -> p s k", k=64), O4[:, :, 0:64],
                                                  rl4.unsqueeze(2).to_broadcast([128, 4, 64]), ALU.mult),
                 reads=[("ps", obank), "rl4"], writes=scr(6))

            def sqs(e):
                ins = None
                for s_ in range(4):
                    ins = e.activation(hb4[:, 64 * s_:64 * s_ + 64], hf4[:, 64 * s_:64 * s_ + 64], AF.Square,
                                       accum_out=sq4[:, s_:s_ + 1])
                return ins
            S.op("act", sqs, reads=scr(6), writes=scr(7) + ["sq4"])
            toks4 = [("ssq", grp, 4 * I + s_) for s_ in range(4)]
            S.op("dve", lambda e: e.tensor_tensor(ssq4, ssq4, sq4, ALU.add), reads=["sq4"] + toks4, writes=toks4)
            S.op("act", lambda e: e.copy(hb4, hf4), reads=scr(6, 7), writes=scr(7))
            kt = psnext("t")

            def tr(e):
                ins = None
                for s_ in range(4):
                    ins = e.transpose(PSB[kt][p0:p0 + 64, 128 * s_:128 * s_ + 128], hb4[:, 64 * s_:64 * s_ + 64], IDENT[:, :])
                return ins
            S.op("pe", tr, reads=scr(7) + ["cst"], writes=[("ps", kt)])
            S.op("dve", lambda e: e.tensor_scalar(MX(chunk, 512 * I, 512, p0, p0 + 64), PSB[kt][p0:p0 + 64, 0:512],
                                                  prm(l, "ggo", 1, p0, p0 + 64, chunk), None, ALU.mult),
                 reads=[("ps", kt), "prm"], writes=[("mx", chunk, 4 * I + s_, p0) for s_ in range(4)])

        def norm_and_store(obank, ocol, rows, tile_i, grp, chunk, p0, ssq_acc, l):
            rl = SM[0:rows, SM_T + 4:SM_T + 5]
            S.op("dve", lambda e: e.reciprocal(rl, PS[obank][0:rows, ocol + 64:ocol + 65]),
                 reads=[("ps", obank)], writes=["rl"])
            hf = SCRf(5, 64, 0, rows)
            hb = SCRb(5, 64, 0, rows, off=256)
            S.op("dve", lambda e: e.tensor_scalar(hf, PS[obank][0:rows, ocol:ocol + 64], rl, None, ALU.mult),
                 reads=[("ps", obank), "rl"], writes=scr(5))
            sq = SM[0:rows, SM_T + 5:SM_T + 6]
            S.op("act", lambda e: e.activation(hb, hf, AF.Square, accum_out=sq), reads=scr(5), writes=scr(5) + ["sq"])
            S.op("dve", lambda e: e.tensor_tensor(ssq_acc, ssq_acc, sq, ALU.add),
                 reads=["sq", ("ssq", grp, tile_i)], writes=[("ssq", grp, tile_i)])
            S.op("act", lambda e: e.copy(hb, hf), reads=scr(5), writes=scr(5))
            kt = psnext("t")
            c0, n = tcols(tile_i)
            S.op("pe", lambda e: e.transpose(PSB[kt][p0:p0 + 64, 0:rows], hb, IDENT[0:rows, 0:rows]),
                 reads=scr(5) + ["cst"], writes=[("ps", kt)])
            S.op("dve", lambda e: e.tensor_scalar(MX(chunk, c0, n, p0, p0 + 64), PSB[kt][p0:p0 + 64, 0:rows],
                                                  prm(l, "ggo", 1, p0, p0 + 64, chunk), None, ALU.mult),
                 reads=[("ps", kt), "prm"], writes=[("mx", chunk, tile_i, p0)])

        try:
            for l in range(L):
                psr.clear()
                psr.update({"g": [0, 1], "s": [2, 3, 4], "o": [5, 6], "t": [7], "w": [0, 1, 2, 3, 4, 5]})
                for kk in psr:
                    psi[kk] = 0
                S.op("dve", lambda e: e.memset(
                    AR[:, VA_O // 2: VA_O // 2 + 17 * 8 * 65].rearrange("p (a k) -> p a k", k=65)[:, :, 64:65], 1.0),
                    writes=["va_ones"])
                S.op("dve", lambda e: e.memset(MVAt[:, :].rearrange("p (a k) -> p a k", k=65)[:, :, 64:65], 1.0),
                     writes=["mva_ones"])
                S.op("dve", lambda e: e.memset(MKTt[64:128, :], 0.0), writes=["mkt_zero"])
                S.op("dve", lambda e: e.memset(bf(VA_O + 16 * 520 * 2, 520, 32, 64), 0.0), writes=["va_pad", "va_ones"])
                S.op("dve", lambda e: e.memset(bf(VA_O + 16 * 520 * 2, 520, 64, 128), 0.0), writes=["va_pad2", "va_ones"])
                S.op("dve", lambda e: e.memset(bf(MVA_O + 1040, 64), 0.0), writes=["mva_pad"])
                wload(0, w_in[l][:, 1024:1536], 512, extra_writes=SC_TOKS if l > 0 else ())
                wload(1, w_in[l][:, 512:1024], 512)
                wload(2, w_in[l][:, 1536:2056], 520)

                SGC = MX_O + 12288
                WSTl = bf(SGC, 512)
                GSl = f32(SGC + 1024, 256)
                dma("pool", WSTl.rearrange("p (g i) -> p g i", g=4), wsT[l].rearrange("g j i -> j g i"),
                    writes=[("sguc", 0)], key="wst")
                S.op("dve", lambda e: e.memset(bf(SGC, 512, 64, 128).rearrange("p (g i) -> p g i", g=4)[:, :, 0:64], 0.0),
                     reads=[("sguc", 0)], writes=[("sguc", 0)])
                dma("sp", GSl, g_sgu[l:l + 1, :].broadcast_to([128, 256]), writes=[("sguc", 1)], key="gs")

                def sgu_ops(i):
                    st_ = 0 if "setA" in DBG else i % 2
                    base = MX_O if st_ == 0 else MX_O + 6144
                    tk = "sguA" if st_ == 0 else "sguB"
                    Sb = lambda g, n, p0=0, p1=128: bf(base + g * 1024, n, p0, p1)
                    Sf = lambda g, n, p0=0, p1=128: f32(base + g * 1024, n, p0, p1)
                    sc = lambda *gs: [(tk, g) for g in gs]
                    rows = tile_rows(i)
                    c0, n = tcols(i)
                    s1, s2 = [], []
                    k = psnext("s")
                    z = Sf(0, 512, 0, rows)
                    t1 = Sf(2, 512, 0, rows)
                    ssv = SM[0:rows, SM_T + 20 + 2 * st_:SM_T + 21 + 2 * st_]
                    sss = SM[0:rows, SM_T + 21 + 2 * st_:SM_T + 22 + 2 * st_]
                    ssvt, ssst = ("ssv", st_), ("sss", st_)
                    vvf = Sf(3, 256, 0, rows)
                    vvb = Sb(2, 256, 0, rows)
                    sg = Sf(4, 256, 0, rows)
                    sgb = Sb(5, 256, 0, rows)
                    s1.append(lambda: mm_group(None, [(PS[k][0:rows, :], HT(c, c0, n), WS(2, c, 8, 512), c == 0, c == 7) for c in range(8)],
                                               reads=[("ht", i), ("ws", 2)], writes=[("ps", k)]))
                    s1.append(lambda: S.op("dve", lambda e: e.tensor_copy(z, PS[k][0:rows, :]), reads=[("ps", k)], writes=sc(0, 1)))
                    s1.append(lambda: S.op("dve", lambda e: e.tensor_tensor(t1, z, z, ALU.mult), reads=sc(0, 1), writes=sc(2, 3)))
                    s1.append(lambda: S.op("dve", lambda e: e.tensor_scalar(t1, t1, 0.044715, 1.0, ALU.mult, ALU.add), reads=sc(2, 3), writes=sc(2, 3)))
                    s1.append(lambda: S.op("dve", lambda e: e.tensor_tensor(t1, t1, z, ALU.mult), reads=sc(0, 1, 2, 3), writes=sc(2, 3)))
                    s1.append(lambda: S.op("act", lambda e: e.activation(t1, t1, AF.Exp, scale=-1.5957691216057308), reads=sc(2, 3), writes=sc(2, 3)))
                    s1.append(lambda: S.op("act", lambda e: e.activation(t1, t1, AF.Ln, bias=1.0, scale=1.0), reads=sc(2, 3), writes=sc(2, 3)))
                    s1.append(lambda: S.op("act", lambda e: e.activation(t1, t1, AF.Exp, scale=-1.0), reads=sc(2, 3), writes=sc(2, 3)))
                    s1.append(lambda: S.op("dve", lambda e: e.tensor_tensor(z, z, t1, ALU.mult), reads=sc(0, 1, 2, 3), writes=sc(0, 1)))
                    s1.append(lambda: S.op("act", lambda e: e.activation(t1[:, 0:256], z[:, 256:512], AF.Square, accum_out=ssv),
                                           reads=sc(0, 1), writes=sc(2) + [ssvt]))
                    s1.append(lambda: S.op("act", lambda e: e.activation(ssv, ssv, AF.Ln, bias=EPS, scale=1.0 / 256), reads=[ssvt], writes=[ssvt]))
                    s1.append(lambda: S.op("act", lambda e: e.activation(ssv, ssv, AF.Exp, scale=-0.5), reads=[ssvt], writes=[ssvt]))
                    s1.append(lambda: S.op("dve", lambda e: e.scalar_tensor_tensor(vvf, z[:, 256:512], ssv, GSl[0:rows, :], ALU.mult, ALU.mult),
                                           reads=sc(0, 1) + [("sguc", 1)] + [ssvt], writes=sc(3)))
                    s1.append(lambda: S.op("dve", lambda e: e.tensor_copy(vvb, vvf), reads=sc(3), writes=sc(2)))
                    if i == 16:
                        s1.append(lambda: dma("sp", ogv[l], vvf, reads=sc(3), key="ogv"))
                    k2 = psnext("s")
                    if i < 16:
                        items = [(PS[k2][0:128, 64 * g:64 * g + 64], WSTl[:, 128 * g:128 * g + 128], vvb[:, 64 * g:64 * g + 64], True, True) for g in range(4)]
                        bsap = prm(l, "bs", 4)
                        wtok = [("sguc", 0)]
                    else:
                        items = [(PS[k2][0:32, 64 * g:64 * g + 64], WSTS(l, g), vvb[:, 64 * g:64 * g + 64], True, True) for g in range(4)]
                        bsap = prm(l, "bss", 4, 0, 32)
                        wtok = ["wsts"] + [("wsts", l, b) for b in range(2)]
                    s1.append(lambda: mm_group(None, items, reads=sc(2) + wtok, writes=[("ps", k2)]))
                    s2.append(lambda: S.op("dve", lambda e: e.tensor_tensor(
                        sg.rearrange("p (g d) -> p g d", g=4), PS[k2][0:rows, 0:256].rearrange("p (g d) -> p g d", g=4),
                        bsap.unsqueeze(2).to_broadcast([rows, 4, 64]), ALU.add),
                        reads=[("ps", k2), "prm"], writes=sc(4)))
                    s2.append(lambda: S.op("dve", lambda e: e.tensor_tensor(sg, sg, z[:, 0:256], ALU.mult), reads=sc(0, 1, 4), writes=sc(4)))
                    s2.append(lambda: S.op("act", lambda e: e.activation(sgb, sg, AF.Square, accum_out=sss),
                                           reads=sc(4), writes=sc(5) + [ssst]))
                    s2.append(lambda: S.op("act", lambda e: e.activation(sss, sss, AF.Ln, bias=EPS, scale=1.0 / 256), reads=[ssst], writes=[ssst]))
                    s2.append(lambda: S.op("act", lambda e: e.activation(RSGU(i, rows), sss, AF.Exp, scale=-0.5),
                                           reads=[ssst], writes=[("rsgu", i)]))
                    s2.append(lambda: S.op("dve", lambda e: e.tensor_copy(sgb, sg), reads=sc(4, 5), writes=sc(5)))

                    def trs():
                        kt = psnext("o")

                        def tr(e):
                            ins = None
                            for cc in range(2):
                                ins = e.transpose(PSB[kt][:, cc * 128:cc * 128 + rows], sgb[:, cc * 128:(cc + 1) * 128], IDENT[0:rows, 0:rows])
                            return ins
                        S.op("pe", tr, reads=sc(5) + ["cst"], writes=[("ps", kt)])
                        for cc in range(2):
                            S.op("dve", lambda e, cc=cc, l=l: e.tensor_scalar(
                                MX(4 + cc, c0, n), PSB[kt][:, cc * 128:cc * 128 + rows], prm(l, "ggo", 1, 0, 128, 4 + cc), None, ALU.mult),
                                reads=[("ps", kt), "prm"], writes=[("mx", 4 + cc, i)])
                    s2.append(trs)
                    return s1, s2

                def interleave(a, b):
                    for q in range(max(len(a), len(b))):
                        if q < len(a):
                            a[q]()
                        if q < len(b):
                            b[q]()

                def vk_tile(i):
                    rows = tile_rows(i)
                    c0, n = tcols(i)
                    for which, slot, oprompt, osample, sg in (("v", 0, ofv, osv, 4), ("k", 1, ofk, osk, 6)):
                        k = psnext("g")
                        mm_group(None, [(PS[k][0:rows, :], HT(c, c0, n), WS(slot, c, 0, 512), c == 0, c == 7)
                                        for c in range(8)],
                                 reads=[("ht", i), ("ws", slot)], writes=[("ps", k)])
                        stg = SCRf(sg, 512, 0, rows)
                        S.op(("act" if which == "v" else "dve"), (lambda e, stg=stg, k=k, rows=rows: e.copy(stg, PS[k][0:rows, :])) if which == "v"
                             else (lambda e, stg=stg, k=k, rows=rows: e.tensor_copy(stg, PS[k][0:rows, :])),
                             reads=[("ps", k)], writes=scr(sg, sg + 1))
                        if which == "v":
                            va3 = AR[0:rows, VA_O // 2 + i * 520: VA_O // 2 + (i + 1) * 520].rearrange(
                                "p (h k) -> p h k", k=65)[:, :, 0:64]
                            S.op("dve", lambda e, va3=va3, k=k, rows=rows: e.tensor_copy(
                                va3, PS[k][0:rows, :].rearrange("p (h k) -> p h k", k=64)),
                                reads=[("ps", k), "va_ones"], writes=[("va", i)])
                        dst = oprompt[l, 128 * i:128 * i + 128, :] if i < 16 else osample[l, :, :]
                        dma("sp", dst, stg, reads=scr(sg, sg + 1), key=("stg", sg))

                sgu_pipe = {"B": [], "C": []}

                def sgu_step(t):
                    if t is not None:
                        s1_, s2_ = sgu_ops(t)
                        sA, sB, sC = s1_[:-1], [s1_[-1]] + s2_[:-1], [s2_[-1]]
                    else:
                        sA, sB, sC = [], [], []
                    interleave(sA, sgu_pipe["B"])
                    for f_ in sgu_pipe["C"]:
                        f_()
                    sgu_pipe["C"] = sgu_pipe["B_c"] if "B_c" in sgu_pipe else []
                    sgu_pipe["B"] = sB
                    sgu_pipe["B_c"] = sC

                def pn_a(i):
                    rows = tile_rows(i)
                    c0, n = tcols(i)
                    return prenorm_tile(X[0:rows, i, :], rows, ("x", i), prm(l, "gpm", 8), HT3(c0, n), ("ht", i), i % 2, defer=True)
                pn_b = {0: pn_a(0)}
                for i in range(17):
                    if i + 1 < 17:
                        pn_b[i + 1] = pn_a(i + 1)
                    pn_b.pop(i)()
                    if i >= 1:
                        vk_tile(i - 1)
                        sgu_step(i - 1)
                vk_tile(16)
                sgu_step(16)
                sgu_step(None)
                sgu_step(None)
                S.op("dve", lambda e: e.memset(bf(MX_O + 15 * 1024, 2), 0.0),
                     writes=[("sguA", g) for g in range(6)] + [("sguB", g) for g in range(6)] + [("sguc", 0), ("sguc", 1)]
                     + [("mx", c, i, p0) for c in (0, 1, 2, 3) for i in range(17) for p0 in (0, 64)])

                _chk("vk%d" % l)
                NB = SM[0:8, 210:211]
                S.op("dve", lambda e, l=l: e.tensor_scalar(NB, prm(l, "bf", 1, 0, 8), -1.0, None, ALU.mult),
                     reads=["prm"], writes=["cw", "nb"])
                for bi, (c0, n) in enumerate(BLKS):
                    k = psnext("g")
                    mm_group(None, [(PS[k][0:8, 0:n], WS(2, c, 0, 8), HT(c, c0, n), c == 0, c == 7) for c in range(8)],
                             reads=ALLHT + [("ws", 2)], writes=["cw", ("ps", k)])
                    tmp = SCRf(0, 512, 96, 104)
                    S.op("act", lambda e, k=k, n=n, tmp=tmp: e.activation(tmp[:, 0:n], PS[k][0:8, 0:n], AF.Exp, bias=NB, scale=-1.0),
                         reads=[("ps", k), "nb"], writes=scr(0, 1))
                    S.op("act", lambda e, n=n, tmp=tmp: e.activation(tmp[:, 0:n], tmp[:, 0:n], AF.Ln, bias=1.0, scale=1.0),
                         reads=scr(0, 1), writes=scr(0, 1))
                    S.op("dve", lambda e, c0=c0, n=n, tmp=tmp: e.tensor_scalar(CT(c0, n), tmp[:, 0:n], -1.0, None, ALU.mult),
                         reads=scr(0, 1), writes=["cw", ("ct", bi)])
                dma("sp", olf[l], CT(0, 2048), reads=[("ct", b) for b in range(4)], writes=["cw"], key="olf")
                dma("sp", oslf[l], CT(2048, 32), reads=[("ct", 4)], writes=["cw"], key="olf2")
                SLF = SM[96:104, 220:252]
                S.op("dve", lambda e: e.tensor_copy(SLF, CT(2048, 32)), reads=[("ct", 4)], writes=["cw", "slf"])
                S.op("dve", lambda e: e.tensor_tensor_scan(CT(0, 2048), ONE8.to_broadcast([8, 2048]), CT(0, 2048), 0.0, ALU.mult, ALU.add),
                     reads=[("ct", b) for b in range(4)] + ["one"], writes=["cw", "ctp"])
                S.op("dve", lambda e: e.tensor_copy(HI(0, 2048), CT(0, 2048)), reads=["ctp"], writes=["cw", "hi"])
                S.op("dve", lambda e: e.tensor_tensor(CT(0, 2048), CT(0, 2048), HI(0, 2048), ALU.subtract),
                     reads=["ctp", "hi"], writes=["cw", "ctp"])
                S.op("dve", lambda e: e.tensor_copy(LO(0, 2048), CT(0, 2048)), reads=["ctp"], writes=["cw", "lo"])
                dma("sp", cscr_p[:, 0, :], HI(0, 2048), reads=["hi"], writes=["cw", "cscr_p"], key="cscr0")
                dma("sp", cscr_p[:, 1, :], LO(0, 2048), reads=["lo"], writes=["cw", "cscr_p2"], key="cscr1")
                CS = lambda b, c0, n: CT(b * 1040 + c0, n)
                for b in range(2):
                    dma("sp", CS(b, 0, 1024), clfT[l, b], reads=["ctp", "lo", ("ct", 4), "slf"], writes=["cw", ("cs", b)], key=("csl", b))
                    S.op("dve", lambda e, b=b: e.tensor_copy(CS(b, 1024, 16), SM[96:104, 220 + 16 * b:236 + 16 * b]),
                         reads=["slf", "ctp", "lo", ("ct", 4)], writes=["cw", ("cs2", b)])
                    S.op("dve", lambda e, b=b: e.tensor_tensor_scan(CS(b, 0, 1040), ONE8.to_broadcast([8, 1040]), CS(b, 0, 1040), 0.0, ALU.mult, ALU.add),
                         reads=[("cs", b), ("cs2", b), "one"], writes=["cw", ("csc", b)])
                S.op("dve", lambda e: e.tensor_copy(HI(0, 2080), CT(0, 2080)),
                     reads=[("csc", 0), ("csc", 1), "cscr_p", "cscr_p2"], writes=["cw", "hi"])
                S.op("dve", lambda e: e.tensor_tensor(CT(0, 2080), CT(0, 2080), HI(0, 2080), ALU.subtract),
                     reads=["hi"], writes=["cw", ("csc", 0), ("csc", 1)])
                S.op("dve", lambda e: e.tensor_copy(LO(0, 2080), CT(0, 2080)), reads=[("csc", 0), ("csc", 1), "cscr_p2"], writes=["cw", "lo"])
                for b in range(2):
                    dma("sp", cscr_s[:, 0, b, :], HI(b * 1040, 1040), reads=["hi"], writes=["cw", ("cscr_s", b, 0)], key=("cscrs", b, 0))
                    dma("sp", cscr_s[:, 1, b, :], LO(b * 1040, 1040), reads=["lo"], writes=["cw", ("cscr_s", b, 1)], key=("cscrs", b, 1))

                _chk("fg%d" % l)
                _chk("sgu%d" % l)
                S.op("dve", lambda e: e.memset(bf(QK_O, 4 * NT, 64, 128), 0.0),
                     writes=["cw", "hi", "lo"] + [("qkc", s_) for s_ in range(4)])
                for s_ in (0, 2):
                    S.op("dve", lambda e, s_=s_: e.memset(QK(s_, 0, NT, 64, 67), -1.0), writes=[("qkc", s_)])
                for s_ in (1, 3):
                    S.op("dve", lambda e, s_=s_: e.memset(QK(s_, 0, NT, 64, 65), 1.0), writes=[("qkc", s_)])
                wload(0, w_mkv[l], 512)
                MEMT3 = AR[:, (SCR_O + 4096) // 2:(SCR_O + 4096) // 2 + 2048].rearrange("p (c t) -> p c t", c=8)
                MEMT = lambda c, c0, n: bf(SCR_O + 4096 + (c * 256 + c0) * 2, n)
                memx = f32(WS_O[2], 1024)
                for mt in range(2):
                    dma("sp", memx, mem[128 * mt:128 * mt + 128, :], writes=[("ws", 2)], key="memx")
                    prenorm_tile(memx, 128, ("ws", 2), prm(l, "gmem", 8), MEMT3[:, :, 128 * mt:128 * mt + 128],
                                 ("memt", mt), 0, extra_writes=scr(4, 5, 6, 7))
                memt_toks = [("memt", 0), ("memt", 1)] + scr(4, 5, 6, 7)
                stgk = f32(WS_O[2], 512)
                for mt in range(2):
                    k = psnext("g")
                    mm_group(None, [(PS[k][:, :], MEMT(c, 128 * mt, 128), WS(0, c, 0, 512), c == 0, c == 7) for c in range(8)],
                             reads=memt_toks + [("ws", 0)], writes=[("ps", k)])
                    S.op("act", lambda e, k=k: e.copy(stgk, PS[k][:, :]), reads=[("ps", k)], writes=[("ws", 2)])
                    S.op("dve", lambda e, k=k, mt=mt: e.tensor_copy(
                        MVAt[:, mt * 260:(mt + 1) * 260].rearrange("p (h k) -> p h k", k=65)[:, :, 0:64],
                        PS[k][:, 256:512].rearrange("p (h k) -> p h k", k=64)),
                        reads=[("ps", k), "mva_ones"], writes=[("mva", mt)])
                    dma("sp", omk[l, 128 * mt:128 * mt + 128, :], stgk[:, 0:256], reads=[("ws", 2)], key="omk")
                    dma("sp", omv[l, 128 * mt:128 * mt + 128, :], stgk[:, 256:512], reads=[("ws", 2)], key="omv")
                for pr in range(2):
                    k = psnext("g")
                    mm_group(None, [(PS[k][:, 0:256], WS(0, c, 128 * pr, 128), MEMT(c, 0, 256), c == 0, c == 7) for c in range(8)],
                             reads=memt_toks + [("ws", 0)], writes=[("ps", k)])
                    for hh in range(2):
                        h = 2 * pr + hh
                        S.op("act", lambda e, k=k, hh=hh, h=h: e.copy(MKTt[0:64, h * 256:(h + 1) * 256], PS[k][64 * hh:64 * hh + 64, 0:256]),
                             reads=[("ps", k)], writes=[("mkt", h)])
                wload(2, w_in[l][:, 2056:2312], 256)
                for b in range(2):
                    S.op("dve", lambda e, b=b: e.memset(KAS(b, 0, 1040, 64, 65), 1.0), reads=[("ws", 0)], writes=[("ws", 0), ("sc", b, "one")])
                    S.op("dve", lambda e, b=b: e.memset(VAS3(b)[:, :, 64:65], 1.0), reads=[("ws", 0)], writes=[("ws", 0), ("sc", b, "one")])

                S.op("dve", lambda e: e.memset(SM[:, 160:200], 0.0),
                     writes=[("ssq", g, i) for g in (0, 2) for i in range(17)])

                _chk("memkv%d" % l)
                for pr in range(2):
                    for bi, (c0, n) in enumerate(BLKS):
                        k = psnext("g")
                        mm_group(None, [(PS[k][:, 0:n], WS(2, c, 128 * pr, 128), HT(c, c0, n), c == 0, c == 7) for c in range(8)],
                                 reads=ALLHT + [("ws", 2)], writes=[("ps", k)])
                        for hh in range(2):
                            S.op("act", lambda e, k=k, hh=hh, c0=c0, n=n: e.activation(
                                QK(2 * hh, c0, n, 0, 64), PS[k][64 * hh:64 * hh + 64, 0:n], AF.Copy, scale=0.125),
                                reads=[("ps", k)], writes=[("qa", hh, bi)])
                    for hh in range(2):
                        h = 2 * pr + hh
                        for I in range(4):
                            ob = psnext("o")
                            kts = [dict(ka=MKTt[0:128, h * 256 + 128 * j:h * 256 + 128 * j + 128], nk=128,
                                        va=bf(MVA_O + ((j * 4 + h) * 65) * 2, 128), lo=0, hi=512, pv_lo=0,
                                        mask=None, pt="rot", reads=[("mkt", h), ("qa", hh, I), "mkt_zero", ("qkc", 0), ("qkc", 2)] + [("qar", hh, x_) for x_ in range(3)],
                                        vreads=[("mva", j), "mva_ones"])
                                   for j in range(2)]
                            osubs = [(PS[ob][:, 65 * s:65 * s + 65], 128 * s, 128 * s + 128, 0, 1) for s in range(4)]
                            attend(lambda c0, n, hh=hh, I=I: QK(2 * hh, 512 * I + c0, n, 0, 128), 128, kts, "fm", ob, "mem")
                            nsq.step(lambda ob=ob, I=I, pr=pr, hh=hh: norm_store_fm(ob, I, 2, 6 + pr, 64 * hh, SM[:, 180 + 4 * I:184 + 4 * I], l))
                        ob = psnext("g")
                        kts = []
                        for b in range(2):
                            dma("pool", KAS(b, 0, 256, 0, 64), cmkT[l, b, h], reads=[("ws", 0)], writes=[("sc", b, "k")], key=("sck", b))
                            dma("pool", VAS3(b)[:, 0:2, 0:64],
                                cmv[l, b].rearrange("(j p) (h d) -> p j h d", p=128, h=4)[:, :, h, :],
                                reads=[("ws", 0)], writes=[("sc", b, "v")], key=("scv", b), slow=True)
                            for j in range(2):
                                kts.append(dict(ka=KAS(b, 128 * j, 128, 0, 64), nk=128, va=VAS(b, j),
                                                lo=16 * b, hi=16 * b + 16, pv_lo=0, mask=None, pt=3 * b + (j % 3),
                                                reads=[("sc", b, "k"), ("qa", hh, 4)], vreads=[("sc", b, "v"), ("sc", b, "one")]))
                        osubs = [(PS[ob][0:32, 0:65], 0, 32, 0, len(kts) - 1)]
                        attend(lambda c0, n, hh=hh: QK(2 * hh, 2048 + c0, n, 0, 64), 64, kts, osubs, ob, "mems")
                        flush_pv()
                        norm_and_store(ob, 0, 32, 16, 2, 6 + pr, 64 * hh, SSMEM(16, 32), l)

                flush_pv()
                nsq.flush()
                _chk("memattn%d" % l)
                wload(2, w_in[l][:, 0:512], 512)
                for pr in range(4):
                    for bi, (c0, n) in enumerate(BLKS):
                        for qk, slot in ((0, 2), (1, 1)):
                            k = psnext("g")
                            mm_group(None, [(PS[k][:, 0:n], WS(slot, c, 128 * pr, 128), HT(c, c0, n), c == 0, c == 7) for c in range(8)],
                                     reads=ALLHT + [("ws", slot)], writes=[("ps", k)])
                            for hh in range(2):
                                if qk == 0:
                                    S.op("act", lambda e, k=k, hh=hh, c0=c0, n=n: e.activation(
                                        QK(2 * hh, c0, n, 0, 64), PS[k][64 * hh:64 * hh + 64, 0:n], AF.Copy, scale=0.125),
                                        reads=[("ps", k)], writes=[("qa", hh, bi)])
                                else:
                                    S.op("dve", lambda e, k=k, hh=hh, c0=c0, n=n: e.tensor_copy(
                                        QK(2 * hh + 1, c0, n, 0, 64), PS[k][64 * hh:64 * hh + 64, 0:n]),
                                        reads=[("ps", k)], writes=[("ka", hh, bi)])
                    if pr == 3:
                        wload(2, w_out[l][:, 0:512], 512)
                        wload(1, w_out[l][:, 512:1024], 512)
                    for hh in range(2):
                        h = 2 * pr + hh
                        dma("sp", QK(2 * hh, 0, 2048, 64, 65), cscr_p[h:h + 1, 0, :], reads=["cscr_p", ("qkc", 2 * hh)],
                            writes=[("qar", hh, 0)], key=("qar", hh))
                        dma("sp", QK(2 * hh + 1, 0, 2048, 65, 66), cscr_p[h:h + 1, 0, :], reads=["cscr_p", ("qkc", 2 * hh + 1)],
                            writes=[("kar", hh, 0)], key=("kar", hh))
                        dma("sp", QK(2 * hh + 1, 0, 2048, 66, 67), cscr_p[h:h + 1, 1, :], reads=["cscr_p2"],
                            writes=[("kar", hh, 1)], key=("kar", hh))
                        for b in range(2):
                            dma("sp", QK(2 * hh, 2048 + 16 * b, 16, 64, 65), cscr_s[h:h + 1, 0, b, 1024:1040],
                                reads=[("cscr_s", b, 0)], writes=[("qar", hh, 1 + b)], key=("qar", hh))
                            dma("sp", QK(2 * hh + 1, 2048 + 16 * b, 16, 65, 66), cscr_s[h:h + 1, 0, b, 1024:1040],
                                reads=[("cscr_s", b, 0)], writes=[("kar", hh, 2 + b)], key=("kar", hh))
                            dma("sp", QK(2 * hh + 1, 2048 + 16 * b, 16, 66, 67), cscr_s[h:h + 1, 1, b, 1024:1040],
                                reads=[("cscr_s", b, 1)], writes=[("kar", hh, 4 + b)], key=("kar", hh))
                        qar = [("qar", hh, x) for x in range(3)] + [("qkc", 2 * hh)]
                        kar = [("kar", hh, x) for x in range(6)] + [("qkc", 2 * hh + 1)]
                        for I in range(4):
                            ob = psnext("o")
                            kts = []
                            for j in range(4 * I + 4):
                                a = j - 4 * I
                                kts.append(dict(ka=QK(2 * hh + 1, 128 * j, 128, 0, 128), nk=128, va=VA128(j, h),
                                                lo=(128 * a if a >= 0 else 0), hi=512, pv_lo=(128 * a if a >= 0 else 0),
                                                mask=((MASKNEG, 128) if a >= 0 else None), pt="rot",
                                                reads=[("ka", hh, j // 4), ("qa", hh, I)] + qar + kar,
                                                vreads=[("va", j), "va_ones"]))
                            osubs = [(PS[ob][:, 65 * s:65 * s + 65], 128 * s, 128 * s + 128, 0, 4 * I + s) for s in range(4)]
                            attend(lambda c0, n, hh=hh, I=I: QK(2 * hh, 512 * I + c0, n, 0, 128), 128, kts, "fm", ob, "fox")
                            nsq.step(lambda ob=ob, I=I, pr=pr, hh=hh: norm_store_fm(ob, I, 0, pr, 64 * hh, SM[:, 160 + 4 * I:164 + 4 * I], l))
                        ob = psnext("g")
                        kts = []
                        for b in range(2):
                            dma("pool", KAS(b, 0, 1024, 0, 64), ckT[l, b, h], reads=[("ws", 0)], writes=[("sc", b, "k")], key=("sck", b))
                            dma("pool", VAS3(b)[:, :, 0:64],
                                cv[l, b].rearrange("(j p) (h d) -> p j h d", p=128, h=8)[:, :, h, :],
                                reads=[("ws", 0)], writes=[("sc", b, "v")], key=("scv", b), slow=True)
                            dma("sp", KAS(b, 0, 1024, 65, 66), cscr_s[h:h + 1, 0, b, 0:1024], reads=[("cscr_s", b, 0), ("ws", 0)],
                                writes=[("sc", b, "r")], key=("scr", b))
                            dma("sp", KAS(b, 0, 1024, 66, 67), cscr_s[h:h + 1, 1, b, 0:1024], reads=[("cscr_s", b, 1), ("ws", 0)],
                                writes=[("sc", b, "r2")], key=("scr", b))
                            for j in range(8):
                                kts.append(dict(ka=KAS(b, 128 * j, 128), nk=128, va=VAS(b, j),
                                                lo=16 * b, hi=16 * b + 16, pv_lo=0, mask=None, pt=3 * b + (j % 3),
                                                reads=[("sc", b, "k"), ("sc", b, "r"), ("sc", b, "r2"), ("sc", b, "one"), ("qa", hh, 4)] + qar,
                                                vreads=[("sc", b, "v"), ("sc", b, "one")]))
                        kts.append(dict(ka=QK(2 * hh + 1, 2048, 32, 0, 67), nk=32, va=VA(16, h, 32), lo=0, hi=32, pv_lo=0,
                                        mask=(MASKS, 32), pt=6, reads=[("ka", hh, 4), ("qa", hh, 4)] + qar + kar,
                                        vreads=[("va", 16), "va_ones"]))
                        osubs = [(PS[ob][0:32, 0:65], 0, 32, 0, len(kts) - 1)]
                        attend(lambda c0, n, hh=hh: QK(2 * hh, 2048 + c0, n, 0, 67), 67, kts, osubs, ob, "foxs")
                        flush_pv()
                        norm_and_store(ob, 0, 32, 16, 0, pr, 64 * hh, SSFOX(16, 32), l)

                flush_pv()
                nsq.flush()
                _chk("fox%d" % l)
                S.op("act", lambda e: e.activation(SM[:, 100:117], SM[:, 160:177], AF.Ln, bias=EPS, scale=1.0 / 512),
                     reads=[("ssq", 0, i) for i in range(17)], writes=["rfox"])
                S.op("act", lambda e: e.activation(SM[:, 100:117], SM[:, 100:117], AF.Exp, scale=-0.5), reads=["rfox"], writes=["rfox"])
                S.op("act", lambda e: e.activation(SM[:, 140:157], SM[:, 180:197], AF.Ln, bias=EPS, scale=1.0 / 256),
                     reads=[("ssq", 2, i) for i in range(17)], writes=["rmem"])
                S.op("act", lambda e: e.activation(SM[:, 140:157], SM[:, 140:157], AF.Exp, scale=-0.5), reads=["rmem"], writes=["rmem"])

                GP = SCRf(4, 1024)
                dma("sp", GP, g_post_mix[l:l + 1, :].broadcast_to([128, D]), writes=scr(4, 5, 6, 7), key="gp")
                for i in range(17):
                    rows = tile_rows(i)
                    c0, n = tcols(i)
                    ostg = SCRf(0, 1024, 0, rows)
                    mxr = [("mx", c, i, p0) for c in (0, 1, 2, 3, 6, 7) for p0 in (0, 64)] + [("mx", 4, i), ("mx", 5, i)]
                    last_b = None
                    for hf in range(2):
                        banks = [psnext("w") for _ in range(3)]
                        for gi, (cs, b) in enumerate(zip(((0, 1, 2, 3), (4, 5), (6, 7)), banks)):
                            wsl = 2 if hf == 0 else 1
                            mm_group(None, [(PS[b][0:rows, :], MX(c, c0, n), WS(wsl, c, 0, 512), c == cs[0], c == cs[-1]) for c in cs],
                                     reads=mxr + [("ws", wsl)], writes=[("ps", b)])
                        o_h = ostg[:, 512 * hf:512 * hf + 512]
                        S.op("act", lambda e, o_h=o_h, b=banks[0], rows=rows, i=i: e.activation(o_h, PS[b][0:rows, :], AF.Copy, scale=RFOX(i, rows)),
                             reads=[("ps", banks[0]), "rfox"], writes=scr(2 * hf, 2 * hf + 1))
                        S.op("dve", lambda e, o_h=o_h, b=banks[1], rows=rows, i=i: e.scalar_tensor_tensor(o_h, PS[b][0:rows, :], RSGU(i, rows), o_h, ALU.mult, ALU.add),
                             reads=[("ps", banks[1]), ("rsgu", i)] + scr(2 * hf, 2 * hf + 1), writes=scr(2 * hf, 2 * hf + 1))
                        S.op("dve", lambda e, o_h=o_h, b=banks[2], rows=rows, i=i: e.scalar_tensor_tensor(o_h, PS[b][0:rows, :], RMEM(i, rows), o_h, ALU.mult, ALU.add),
                             reads=[("ps", banks[2]), "rmem"] + scr(2 * hf, 2 * hf + 1), writes=scr(2 * hf, 2 * hf + 1))
                        last_b = banks
                    post_norm_residual(ostg, rows, i, bf(WS_O[0], 1024, 0, rows), [("ws", 0)] + SC_TOKS, GP, scr(4, 5, 6, 7), scr(0, 1, 2, 3))

                _chk("wout%d" % l)
                S.barrier()
                ffn_phase(l)
                if l < L - 1:
                    S.barrier()

        except _Stop:
            pass
        S.emit(st)
    return nc


_NC_CACHE = {}


def _prep_inputs(inp):
    f = lambda a: np.ascontiguousarray(np.asarray(a, dtype=np.float32))
    x_prompt = f(inp["x_prompt"]); x_sample = f(inp["x_sample"]); mem_prompt = f(inp["mem_prompt"])
    cfk = f(inp["cache_fox_k"]); cfv = f(inp["cache_fox_v"]); clf = f(inp["cache_fox_logf"])
    cmk = f(inp["cache_mem_k"]); cmvv = f(inp["cache_mem_v"]); cconv = f(inp["cache_ffn_conv"])
    ckT_all = np.ascontiguousarray(cfk.transpose(0, 1, 3, 4, 2))
    cv_all = cfv.reshape(L, 16, 1024, 512)
    clfT_all = np.ascontiguousarray(clf.transpose(0, 1, 3, 2))
    cmkT_all = np.ascontiguousarray(cmk.transpose(0, 1, 3, 4, 2))
    cmv_all = cmvv.reshape(L, 16, 256, 256)
    wsT = np.ascontiguousarray(f(inp["w_spatial"]).transpose(0, 1, 3, 2))
    fm8 = lambda g: f(g).reshape(L, 8, 128).transpose(0, 2, 1)
    b_sp = f(inp["b_spatial"])
    w_dw = f(inp["w_dwconv"]); b_dw = f(inp["b_dwconv"]); b_f = f(inp["b_forget"])
    cst = np.zeros((128, 288), dtype=ml_dtypes.bfloat16)
    cst[:, 0:128] = np.eye(128, dtype=np.float32).astype(ml_dtypes.bfloat16)
    kk, qq = np.meshgrid(np.arange(128), np.arange(128), indexing="ij")
    cst[:, 128:256] = np.where(kk <= qq, 0.0, NEG).astype(ml_dtypes.bfloat16)
    k2, q2 = np.meshgrid(np.arange(32), np.arange(32), indexing="ij")
    ok = (k2 // 16 == q2 // 16) & (k2 % 16 <= q2 % 16)
    cst[0:32, 256:288] = np.where(ok, 0.0, NEG).astype(ml_dtypes.bfloat16)
    wu = f(inp["w_up"])
    wu_a = wu[:, :, 0:FF].reshape(L, 8, 128, NJ, 128)
    wu_l = wu[:, :, FF:2 * FF].reshape(L, 8, 128, NJ, 128)
    w_up_r = np.ascontiguousarray(np.concatenate([wu_a, wu_l], axis=4).transpose(0, 3, 2, 1, 4))
    shared = dict(
        w_in=f(inp["w_in"]), w_mkv=f(inp["w_mem_kv"]), w_out=f(inp["w_out"]), w_up=w_up_r,
        w_dn=f(inp["w_down"]), g_post_mix=f(inp["g_post_mix"]), g_post_ffn=f(inp["g_post_ffn"]),
        g_sgu=f(inp["g_sgu"]), wsT=wsT, cst=cst)
    gpm = fm8(inp["g_pre_mix"]); ggo = fm8(inp["g_group_out"]); gmem = fm8(inp["g_mem"]); gpf = fm8(inp["g_pre_ffn"])
    in_maps = []
    for c in range(NCORES):
        prm = np.zeros((128, NPRM), dtype=np.float32)
        for l in range(L):
            o = l * PPL
            prm[:, o + PO["gpm"]:o + PO["gpm"] + 8] = gpm[l]
            prm[:, o + PO["ggo"]:o + PO["ggo"] + 8] = ggo[l]
            prm[:, o + PO["gmem"]:o + PO["gmem"] + 8] = gmem[l]
            prm[:, o + PO["gpf"]:o + PO["gpf"] + 8] = gpf[l]
            prm[:, o + PO["bs"]:o + PO["bs"] + 4] = b_sp[l].T
            prm[0:16, o + PO["bss"]:o + PO["bss"] + 4] = b_sp[l][:, 0:16].T
            prm[16:32, o + PO["bss"]:o + PO["bss"] + 4] = b_sp[l][:, 0:16].T
            prm[:, o + PO["wdw"]:o + PO["wdw"] + 66] = w_dw[l].reshape(3, NJ, 128).transpose(2, 1, 0).reshape(128, 66)
            prm[:, o + PO["bdw"]:o + PO["bdw"] + 22] = b_dw[l].reshape(NJ, 128).T
            prm[0:8, o + PO["bf"]] = b_f[l]
            cc = cconv[l, 2 * c:2 * c + 2]
            prm[:, o + PO["cconv"]:o + PO["cconv"] + 88] = cc.reshape(2, 2, NJ, 128).transpose(3, 2, 0, 1).reshape(128, 88)
        m = dict(shared)
        m.update(
            x_p=x_prompt[c], x_s=x_sample[2 * c:2 * c + 2].reshape(32, D), mem=mem_prompt[c],
            ckT=np.ascontiguousarray(ckT_all[:, 2 * c:2 * c + 2]), cv=np.ascontiguousarray(cv_all[:, 2 * c:2 * c + 2]),
            clfT=np.ascontiguousarray(clfT_all[:, 2 * c:2 * c + 2]), cmkT=np.ascontiguousarray(cmkT_all[:, 2 * c:2 * c + 2]),
            cmv=np.ascontiguousarray(cmv_all[:, 2 * c:2 * c + 2]), prm=prm)
        in_maps.append(m)
    return in_maps


def kernel(**inp):
    in_maps = _prep_inputs(inp)
    if "nc" not in _NC_CACHE:
        _NC_CACHE["nc"] = build()
    nc = _NC_CACHE["nc"]
    res = run_bass_kernel_spmd(nc, in_maps, core_ids=list(range(NCORES)))
    R = res.results
    cat = lambda name: np.stack([np.asarray(r[name], dtype=np.float32) for r in R], axis=0)
    y_p = cat("y_p")
    y_s = cat("y_s").reshape(16, 16, D)
    fk = cat("ofk").transpose(1, 0, 2, 3).reshape(L, 8, 2048, 8, 64)
    fv = cat("ofv").transpose(1, 0, 2, 3).reshape(L, 8, 2048, 8, 64)
    lf = cat("olf").transpose(1, 0, 3, 2)
    mk = cat("omk").transpose(1, 0, 2, 3).reshape(L, 8, 256, 4, 64)
    mv = cat("omv").transpose(1, 0, 2, 3).reshape(L, 8, 256, 4, 64)
    cvp = cat("oconv").transpose(1, 0, 4, 3, 2).reshape(L, 8, 2, FF)
    sk = cat("osk").transpose(1, 0, 2, 3).reshape(L, 16, 16, 8, 64)
    sv = cat("osv").transpose(1, 0, 2, 3).reshape(L, 16, 16, 8, 64)
    slf = cat("oslf").transpose(1, 0, 3, 2).reshape(L, 16, 16, 8)
    gv = cat("ogv").transpose(1, 0, 2, 3).reshape(L, 16, 16, 256)
    cvs = cat("osconv").transpose(1, 0, 4, 5, 3, 2).reshape(L, 16, 2, FF)
    outs = (y_p, y_s, fk, fv, lf, mk, mv, cvp, sk, sv, slf, gv, cvs)
    return tuple(np.ascontiguousarray(o, dtype=np.float32) for o in outs)
```
